# Optimizing a Trainium2 kernel written in Bass

```python
import math
import jax, jax.numpy as jnp
from jax import lax
import numpy as np

D_MODEL = 1024
BATCH = 2
SEQ = 8192
DEPTH = 4
DEC_BATCH = 4
DEC_SEQ = 4096
PAST_LEN = 128

N_META = 16
N_MIXERS = 3
D_FF = 2816
EPS = 1e-6
CHUNK = 64
PAD = CHUNK - N_META
Q_BLOCK = 128
CONV_K = 5
ROPE_THETA = 500000.0

A_HEADS = 8
A_DK = 128
A_DV = 128
A_KW = A_HEADS * A_DK
A_VW = A_HEADS * A_DV
B_HEADS = 8
B_HD = 64
B_ROT = B_HD // 4
B_QKW = B_HEADS * B_HD
B_VW = B_HEADS * 2 * B_HD
C_HEADS = 8
C_DK = 128
C_DV = 128
C_KW = C_HEADS * C_DK
C_VW = C_HEADS * C_DV

N_A = (DEPTH + 2) // 3
N_B = (DEPTH + 1) // 3
N_C = DEPTH // 3

kernel_name = "hybrid_bidir_deltanet_diffattn_hgrn2_encoder"

F32 = jnp.float32


def rmsnorm(x, w):
    xf = x.astype(F32)
    y = xf * lax.rsqrt(jnp.mean(xf * xf, axis=-1, keepdims=True) + EPS)
    return (y * w.astype(F32)).astype(x.dtype)


def l2norm(x):
    xf = x.astype(F32)
    return xf * lax.rsqrt(jnp.sum(xf * xf, axis=-1, keepdims=True) + EPS)


def swiglu(x, w_up, w_down):
    gate, up = jnp.split(x @ w_up, 2, axis=-1)
    return (jax.nn.silu(gate) * up) @ w_down


def centred_conv(x, w):
    half = (CONV_K - 1) // 2
    return lax.conv_general_dilated(
        x, w[:, None, :].astype(x.dtype), window_strides=(1,), padding=[(half, half)],
        dimension_numbers=("NWC", "WIO", "NWC"), feature_group_count=x.shape[-1])


def rope_partial(x, pos):
    half = B_ROT // 2
    inv_freq = jnp.exp(-math.log(ROPE_THETA) * jnp.arange(half, dtype=F32) / half)
    ang = pos.astype(F32)[:, None] * inv_freq
    shape = (ang.shape[0],) + (1,) * (x.ndim - 3) + (half,)
    cos, sin = jnp.cos(ang).reshape(shape), jnp.sin(ang).reshape(shape)
    xr = x[..., :B_ROT].astype(F32)
    x1, x2 = xr[..., :half], xr[..., half:]
    rot = jnp.concatenate([x1 * cos - x2 * sin, x2 * cos + x1 * sin], axis=-1).astype(x.dtype)
    return jnp.concatenate([rot, x[..., B_ROT:]], axis=-1)


def pad_front(t):
    return jnp.pad(t, [(0, 0), (PAD, 0)] + [(0, 0)] * (t.ndim - 2))


def to_bidir(t):
    t = pad_front(t)
    return jnp.concatenate([t, jnp.flip(t, axis=1)], axis=0)


def to_bidir_dir(t):
    t = pad_front(t)
    return jnp.concatenate([t[:, :, 0], jnp.flip(t[:, :, 1], axis=1)], axis=0)


def from_bidir(o, bsz):
    return (o[:bsz] + jnp.flip(o[bsz:], axis=1))[:, PAD:]


def to_chunks(t, n_chunks):
    t = t.reshape((t.shape[0], n_chunks, CHUNK, t.shape[2]) + t.shape[3:])
    return jnp.moveaxis(t, 3, 1)


def delta_rule_chunked(q, k, v, beta, log_a):
    bsz, seq_len, n_heads, dk = q.shape
    dv = v.shape[-1]
    n_chunks = seq_len // CHUNK
    q, k, v, beta, log_a = (to_chunks(t, n_chunks) for t in (q, k, v, beta, log_a))
    gc = jnp.cumsum(log_a, axis=-1)
    incl = jnp.tril(jnp.ones((CHUNK, CHUNK), dtype=bool))
    strict = jnp.tril(jnp.ones((CHUNK, CHUNK), dtype=bool), -1)
    decay = jnp.exp(jnp.where(incl, gc[..., :, None] - gc[..., None, :], -jnp.inf))
    kb = k * beta[..., None]
    a_mat = jnp.where(strict, jnp.einsum("bhnid,bhnjd->bhnij", kb, k) * decay, 0.0)
    rhs = jnp.concatenate([v * beta[..., None], kb * jnp.exp(gc)[..., None]], axis=-1)
    sol = lax.linalg.triangular_solve(a_mat + jnp.eye(CHUNK, dtype=F32), rhs,
                                      left_side=True, lower=True, unit_diagonal=True)
    u, w = sol[..., :dv], sol[..., dv:]
    qk = jnp.einsum("bhnid,bhnjd->bhnij", q, k) * decay
    q_dec = q * jnp.exp(gc)[..., None]
    k_dec = k * jnp.exp(gc[..., -1:] - gc)[..., None]
    a_last = jnp.exp(gc[..., -1])

    def step(state, xs):
        u_c, w_c, qk_c, qd_c, kd_c, al_c = xs
        v_new = u_c - jnp.einsum("bhcd,bhde->bhce", w_c, state)
        out = jnp.einsum("bhcd,bhde->bhce", qd_c, state) + jnp.einsum("bhij,bhje->bhie", qk_c, v_new)
        state = state * al_c[..., None, None] + jnp.einsum("bhcd,bhce->bhde", kd_c, v_new)
        return state, out

    xs = tuple(jnp.moveaxis(t, 2, 0) for t in (u, w, qk, q_dec, k_dec, a_last))
    state0 = jnp.zeros((bsz, n_heads, dk, dv), F32)
    _, o = lax.scan(step, state0, xs)
    return jnp.transpose(o, (1, 0, 3, 2, 4)).reshape(bsz, seq_len, n_heads, dv)


def gla_chunked(q, k, v, log_f):
    bsz, seq_len, n_heads, dk = q.shape
    dv = v.shape[-1]
    n_chunks = seq_len // CHUNK
    q, k, v, log_f = (to_chunks(t, n_chunks) for t in (q, k, v, log_f))
    bc = jnp.cumsum(log_f, axis=-2)
    q_dec = q * jnp.exp(bc)
    k_dec = k * jnp.exp(bc[..., -1:, :] - bc)
    f_last = jnp.exp(bc[..., -1, :])
    incl = jnp.tril(jnp.ones((CHUNK, CHUNK), dtype=bool))[:, :, None]

    def step(state, xs):
        qc, kc, vc, bcc, qd, kd, fl = xs
        dec = jnp.exp(jnp.where(incl, bcc[:, :, :, None, :] - bcc[:, :, None, :, :], -jnp.inf))
        attn = jnp.einsum("bhid,bhjd,bhijd->bhij", qc, kc, dec)
        out = jnp.einsum("bhcd,bhde->bhce", qd, state) + jnp.einsum("bhij,bhje->bhie", attn, vc)
        state = state * fl[..., None] + jnp.einsum("bhcd,bhce->bhde", kd, vc)
        return state, out

    xs = tuple(jnp.moveaxis(t, 2, 0) for t in (q, k, v, bc, q_dec, k_dec, f_last))
    state0 = jnp.zeros((bsz, n_heads, dk, dv), F32)
    _, o = lax.scan(step, state0, xs)
    return jnp.transpose(o, (1, 0, 3, 2, 4)).reshape(bsz, seq_len, n_heads, dv)


def gated_deltanet(h, w_in, conv_w, a_log, dt_bias, o_norm, w_out):
    bsz, seq_len, _ = h.shape
    proj = h @ w_in
    qkv = proj[..., :2 * A_KW + A_VW]
    gate = proj[..., 2 * A_KW + A_VW:2 * A_KW + 2 * A_VW].reshape(bsz, seq_len, A_HEADS, A_DV)
    ba = proj[..., 2 * A_KW + 2 * A_VW:].astype(F32).reshape(bsz, seq_len, 2, 2, A_HEADS)
    qkv = jax.nn.silu(centred_conv(qkv, conv_w))
    q = l2norm(qkv[..., :A_KW].reshape(bsz, seq_len, A_HEADS, A_DK)) * (A_DK ** -0.5)
    k = l2norm(qkv[..., A_KW:2 * A_KW].reshape(bsz, seq_len, A_HEADS, A_DK))
    v = qkv[..., 2 * A_KW:].astype(F32).reshape(bsz, seq_len, A_HEADS, A_DV)
    beta = jax.nn.sigmoid(ba[:, :, 0])
    log_a = -jnp.exp(a_log.astype(F32)) * jax.nn.softplus(ba[:, :, 1] + dt_bias.astype(F32))
    o = delta_rule_chunked(to_bidir(q), to_bidir(k), to_bidir(v), to_bidir_dir(beta), to_bidir_dir(log_a))
    o = from_bidir(o, bsz)
    o = rmsnorm(o, o_norm) * jax.nn.silu(gate.astype(F32))
    return o.reshape(bsz, seq_len, A_VW).astype(h.dtype) @ w_out


def diff_attention(h, w_in, lam_vec, sub_norm, w_out, layer_idx):
    bsz, seq_len, _ = h.shape
    proj = h @ w_in
    qk = proj[..., :4 * B_QKW].reshape(bsz, seq_len, 4, B_HEADS, B_HD)
    v = proj[..., 4 * B_QKW:].reshape(bsz, seq_len, B_HEADS, 2 * B_HD)
    qk = rope_partial(qk, jnp.arange(seq_len))
    q = qk[:, :, :2] * (B_HD ** -0.5)
    k = qk[:, :, 2:]
    lam_init = 0.8 - 0.6 * math.exp(-0.3 * layer_idx)
    lv = lam_vec.astype(F32)
    lam = jnp.exp(jnp.sum(lv[0] * lv[1])) - jnp.exp(jnp.sum(lv[2] * lv[3])) + lam_init
    n_blk = -(-seq_len // Q_BLOCK)
    q = jnp.pad(q, ((0, 0), (0, n_blk * Q_BLOCK - seq_len), (0, 0), (0, 0), (0, 0)))
    q = jnp.moveaxis(q.reshape(bsz, n_blk, Q_BLOCK, 2, B_HEADS, B_HD), 1, 0)

    def attend(qb):
        s = jnp.einsum("bqmhd,bkmhd->bmhqk", qb, k, preferred_element_type=F32)
        p = jax.nn.softmax(s, axis=-1)
        pd = p[:, 0] - lam * p[:, 1]
        return jnp.einsum("bhqk,bkhe->bqhe", pd.astype(v.dtype), v)

    o = lax.map(attend, q)
    o = jnp.moveaxis(o, 0, 1).reshape(bsz, n_blk * Q_BLOCK, B_HEADS, 2 * B_HD)[:, :seq_len]
    o = rmsnorm(o, sub_norm) * (1.0 - lam_init)
    return o.reshape(bsz, seq_len, B_VW) @ w_out


def hgrn2(h, w_in, lb_logits, layer_idx, o_norm, w_out):
    bsz, seq_len, _ = h.shape
    proj = h @ w_in
    q = jax.nn.silu(proj[..., :C_KW].astype(F32)).reshape(bsz, seq_len, C_HEADS, C_DK) * (C_DK ** -0.5)
    i_in = proj[..., C_KW:C_KW + C_VW].astype(F32).reshape(bsz, seq_len, C_HEADS, C_DV)
    gate = proj[..., C_KW + C_VW:C_KW + 2 * C_VW].reshape(bsz, seq_len, C_HEADS, C_DV)
    f_logit = proj[..., C_KW + 2 * C_VW:].astype(F32).reshape(bsz, seq_len, 2, C_HEADS, C_DK)
    lb_w = jax.nn.softmax(lb_logits.astype(F32), axis=0)
    lb = (jnp.cumsum(lb_w, axis=0) - lb_w[0])[layer_idx].reshape(C_HEADS, C_DK)
    f = lb + (1.0 - lb) * jax.nn.sigmoid(f_logit)
    o = gla_chunked(to_bidir(q), to_bidir_dir(1.0 - f), to_bidir(i_in), to_bidir_dir(jnp.log(f)))
    o = from_bidir(o, bsz)
    o = rmsnorm(o, o_norm) * jax.nn.silu(gate.astype(F32))
    return o.reshape(bsz, seq_len, C_VW).astype(h.dtype) @ w_out


def encoder_trunk(x, meta_tokens, norm_w, ffn_w_up, ffn_w_down,
                  a_w_in, a_conv_w, a_log, a_dt_bias, a_o_norm, a_w_out,
                  b_w_in, b_lambda, b_sub_norm, b_w_out,
                  c_w_in, c_lb_logits, c_o_norm, c_w_out, final_norm):
    bsz = x.shape[0]
    meta = jnp.broadcast_to(meta_tokens.astype(x.dtype)[None], (bsz, N_META, x.shape[-1]))
    h = jnp.concatenate([meta, x], axis=1)
    for i in range(DEPTH):
        kind, j = i % N_MIXERS, i // N_MIXERS
        h = h + 0.5 * swiglu(rmsnorm(h, norm_w[i, 0]), ffn_w_up[i, 0], ffn_w_down[i, 0])
        hn = rmsnorm(h, norm_w[i, 1])
        if kind == 0:
            mix = gated_deltanet(hn, a_w_in[j], a_conv_w[j], a_log[j], a_dt_bias[j], a_o_norm[j], a_w_out[j])
        elif kind == 1:
            mix = diff_attention(hn, b_w_in[j], b_lambda[j], b_sub_norm[j], b_w_out[j], i)
        else:
            mix = hgrn2(hn, c_w_in[j], c_lb_logits, i, c_o_norm[j], c_w_out[j])
        h = h + mix.astype(h.dtype)
        h = h + 0.5 * swiglu(rmsnorm(h, norm_w[i, 2]), ffn_w_up[i, 1], ffn_w_down[i, 1])
    return rmsnorm(h, final_norm)[:, N_META:]


def setup_inputs(seed: int = 0) -> dict:
    key = jax.random.key(seed)
    ks = jax.random.split(key, 21)

    def normal(k, shape, scale):
        return jax.random.normal(k, shape, F32) * scale

    def gain(k, shape):
        return 1.0 + 0.05 * jax.random.normal(k, shape, F32)

    dt = jnp.exp(jax.random.uniform(ks[8], (N_A, 2, A_HEADS), F32, math.log(1e-3), math.log(1e-1)))
    return {
        "x_prompt": normal(ks[0], (BATCH, SEQ, D_MODEL), 1.0),
        "x_sample": normal(ks[1], (DEC_BATCH, DEC_SEQ, D_MODEL), 1.0),
        "meta_tokens": normal(ks[2], (N_META, D_MODEL), 1.0),
        "norm_w": gain(ks[3], (DEPTH, 3, D_MODEL)),
        "ffn_w_up": normal(ks[4], (DEPTH, 2, D_MODEL, 2 * D_FF), D_MODEL ** -0.5),
        "ffn_w_down": normal(ks[5], (DEPTH, 2, D_FF, D_MODEL), D_FF ** -0.5),
        "a_w_in": normal(ks[6], (N_A, D_MODEL, 2 * A_KW + 2 * A_VW + 4 * A_HEADS), D_MODEL ** -0.5),
        "a_conv_w": normal(ks[7], (N_A, CONV_K, 2 * A_KW + A_VW), CONV_K ** -0.5),
        "a_log": jnp.log(jax.random.uniform(ks[9], (N_A, 2, A_HEADS), F32, 1.0, 16.0)),
        "a_dt_bias": dt + jnp.log(-jnp.expm1(-dt)),
        "a_o_norm": gain(ks[10], (N_A, A_DV)),
        "a_w_out": normal(ks[11], (N_A, A_VW, D_MODEL), A_VW ** -0.5),
        "b_w_in": normal(ks[12], (N_B, D_MODEL, 4 * B_QKW + B_VW), D_MODEL ** -0.5),
        "b_lambda": normal(ks[13], (N_B, 4, B_HD), 0.1),
        "b_sub_norm": gain(ks[14], (N_B, 2 * B_HD)),
        "b_w_out": normal(ks[15], (N_B, B_VW, D_MODEL), B_VW ** -0.5),
        "c_w_in": normal(ks[16], (N_C, D_MODEL, 3 * C_KW + 2 * C_VW), D_MODEL ** -0.5),
        "c_lb_logits": normal(ks[17], (DEPTH, C_KW), 0.5),
        "c_o_norm": gain(ks[18], (N_C, C_DV)),
        "c_w_out": normal(ks[19], (N_C, C_VW, D_MODEL), C_VW ** -0.5),
        "final_norm": gain(ks[20], (D_MODEL,)),
    }


def reference(x_prompt, x_sample, meta_tokens, norm_w, ffn_w_up, ffn_w_down,
              a_w_in, a_conv_w, a_log, a_dt_bias, a_o_norm, a_w_out,
              b_w_in, b_lambda, b_sub_norm, b_w_out,
              c_w_in, c_lb_logits, c_o_norm, c_w_out, final_norm):
    y_prompt = encoder_trunk(x_prompt, meta_tokens, norm_w, ffn_w_up, ffn_w_down,
                             a_w_in, a_conv_w, a_log, a_dt_bias, a_o_norm, a_w_out,
                             b_w_in, b_lambda, b_sub_norm, b_w_out,
                             c_w_in, c_lb_logits, c_o_norm, c_w_out, final_norm)
    y_sample = encoder_trunk(x_sample, meta_tokens, norm_w, ffn_w_up, ffn_w_down,
                             a_w_in, a_conv_w, a_log, a_dt_bias, a_o_norm, a_w_out,
                             b_w_in, b_lambda, b_sub_norm, b_w_out,
                             c_w_in, c_lb_logits, c_o_norm, c_w_out, final_norm)
    return (y_prompt, y_sample)
```

```python
import numpy as np
from contextlib import ExitStack
import concourse.bass as bass
import concourse.mybir as mybir
from concourse.bass_utils import run_bass_kernel_spmd

F32 = mybir.dt.float32
BF16 = mybir.dt.bfloat16
ALU = mybir.AluOpType
AF = mybir.ActivationFunctionType
AX = mybir.AxisListType

D_MODEL = 1024
D_FF = 2816
N_META = 16
EPS = 1e-6


class Eng:
    def __init__(self, name, sem):
        self.name = name
        self.sem = sem
        self.cnt = 0
        self.seen = {}
        self.ops = []


class Tl:
    def __init__(self, t, name):
        self.t = t
        self.name = name
        self.w = None
        self.r = {}
        self.dsem = None

    def __getitem__(self, idx):
        return self.t[idx]

    def view(self):
        return Tl(self.t, self.name)


class KB:
    SAME_ENGINE_SYNC = True

    def __init__(self, nc, n_dma_sems=84):
        self.nc = nc
        self.es = ExitStack()
        self.engs = {}
        for name in ("pe", "act", "dve", "pool", "sp"):
            sem = self.es.enter_context(nc.semaphore("s_" + name))
            self.engs[name] = Eng(name, sem)
        self.pe, self.act, self.dve, self.pool, self.sp = (self.engs[n] for n in ("pe", "act", "dve", "pool", "sp"))
        self.dma_pool = []
        for i in range(n_dma_sems):
            sem = self.es.enter_context(nc.semaphore("s_d%d" % i))
            self.dma_pool.append([sem, 0])
        self.dma_free = list(range(n_dma_sems))
        self.phase_tiles = []
        self.dma_tiles = []
        self.pes = None
        self.uid = 0
        self.nops = 0

    def begin(self):
        self.pes = ExitStack()
        self.phase_tiles = []

    def sb(self, shape, dt, name=None):
        self.uid += 1
        name = "%s_%d" % (name or "t", self.uid)
        t = self.pes.enter_context(self.nc.sbuf_tensor(name, list(shape), dt))
        tl = Tl(t, name)
        self.phase_tiles.append(tl)
        return tl

    def views(self, tl, n):
        vs = [tl.view() for _ in range(n)]
        self.phase_tiles.extend(vs)
        return vs

    def ps(self, shape, dt=F32, name=None):
        self.uid += 1
        name = "%s_%d" % (name or "p", self.uid)
        t = self.pes.enter_context(self.nc.psum_tensor(name, list(shape), dt))
        tl = Tl(t, name)
        self.phase_tiles.append(tl)
        return tl

    def gsb(self, shape, dt, name):
        t = self.es.enter_context(self.nc.sbuf_tensor(name, list(shape), dt))
        return Tl(t, name)

    def _deps(self, eng, reads, writes):
        deps = []
        for tl in reads:
            if tl.w is not None:
                deps.append(tl.w)
        for tl in writes:
            if tl.w is not None:
                deps.append(tl.w)
            deps.extend(tl.r.values())
        waits = []
        for key, sem, cnt in deps:
            if key == eng.name and (eng.name in ("pe", "sp") or not self.SAME_ENGINE_SYNC):
                continue
            if eng.seen.get(key, 0) < cnt:
                eng.seen[key] = cnt
                waits.append((sem, cnt))
        return waits

    def op(self, eng, fn, reads=(), writes=()):
        waits = self._deps(eng, reads, writes)
        eng.cnt += 1
        me = (eng.name, eng.sem, eng.cnt)
        eng.ops.append((waits, fn, (eng.sem, 1)))
        for tl in writes:
            tl.w = me
            tl.r = {}
        for tl in reads:
            if tl not in writes:
                tl.r[eng.name] = me
        self.nops += 1

    def dma(self, q, out, in_, reads=(), writes=(), **kw):
        tl = (list(writes) + list(reads))[0]
        if tl.dsem is None:
            tl.dsem = self.dma_free.pop()
            self.dma_tiles.append(tl)
        slot = self.dma_pool[tl.dsem]
        waits = self._deps(q, reads, writes)
        slot[1] += 16
        key = "d%d" % tl.dsem
        me = (key, slot[0], slot[1])
        q.ops.append((waits, (lambda e, o=out, i=in_, k=kw: e.dma_start(out=o, in_=i, **k)), (slot[0], 16)))
        for t in writes:
            t.w = me
            t.r = {}
        for t in reads:
            t.r[key] = me
        self.nops += 1

    def end(self):
        used = set()
        for tl in self.dma_tiles:
            if tl.dsem is not None:
                used.add(tl.dsem)
        for d in sorted(used):
            sem, cnt = self.dma_pool[d]
            key = "d%d" % d
            if self.sp.seen.get(key, 0) < cnt:
                self.sp.seen[key] = cnt
                self.sp.ops.append(([(sem, cnt)], None, None))
        for e in (self.pe, self.act, self.dve, self.pool):
            if self.sp.seen.get(e.name, 0) < e.cnt:
                self.sp.seen[e.name] = e.cnt
                self.sp.ops.append(([(e.sem, e.cnt)], None, None))
        self.sp.cnt += 1
        self.sp.ops.append(([], (lambda e: e.nop()), (self.sp.sem, 1)))
        for e in (self.pe, self.act, self.dve, self.pool):
            e.ops.append(([(self.sp.sem, self.sp.cnt)], None, None))
        with self.nc.Block() as block:
            for name, deco in (("pe", block.tensor), ("act", block.scalar), ("dve", block.vector),
                               ("pool", block.gpsimd), ("sp", block.sync)):
                eng = self.engs[name]
                ops = eng.ops
                eng.ops = []

                def body(e, ops=ops):
                    for waits, fn, inc in ops:
                        for sem, cnt in waits:
                            e.wait_ge(sem, cnt)
                        if fn is not None:
                            ins = fn(e)
                            if inc is not None:
                                ins.then_inc(inc[0], inc[1])

                deco(body)
        for d in used:
            self.dma_free.append(d)
        for tl in self.dma_tiles:
            tl.dsem = None
        self.dma_tiles = []
        for e in self.engs.values():
            for o in self.engs.values():
                e.seen[o.name] = o.cnt
            for d in range(len(self.dma_pool)):
                e.seen["d%d" % d] = self.dma_pool[d][1]
        self.pes.close()
        self.pes = None

    def close(self):
        self.es.close()


def tiles_of(n, step=512):
    out = []
    s = 0
    while s < n:
        out.append((s, min(step, n - s)))
        s += step
    return out


class Prog:
    def __init__(self, T, depth, n_groups, mixers=True, ffn=True, only=None):
        self.only = only
        self.mixers = mixers
        self.ffn = ffn
        self.T = T
        self.depth = depth
        self.n_groups = n_groups
        base = (T // n_groups) // 512 * 512 if n_groups > 1 else T
        self.groups = [(g * base, base) for g in range(n_groups - 1)]
        self.groups.append(((n_groups - 1) * base, T - (n_groups - 1) * base))
        nc = bass.Bass("TRN2", target_bir_lowering=False)
        self.nc = nc
        d = nc.dram_tensor
        self.xin = d("xin", [T, D_MODEL], F32, kind="ExternalInput").ap()
        self.yout = d("yout", [T, D_MODEL], F32, kind="ExternalOutput").ap()
        self.norm_w = d("norm_w", [128, (depth * 3 + 1) * 8], F32, kind="ExternalInput").ap()
        self.w_up = d("ffn_w_up", [depth, 2, D_MODEL, 2 * D_FF], F32, kind="ExternalInput").ap()
        self.w_dn = d("ffn_w_down", [depth, 2, D_FF, D_MODEL], F32, kind="ExternalInput").ap()
        self.ident_in = d("ident", [128, 128], F32, kind="ExternalInput").ap()
        self.hT = d("hT", [D_MODEL, T], F32).ap()
        self.b_w = d("b_w_in", [D_MODEL, 3072], F32, kind="ExternalInput").ap()
        self.b_wsw = d("b_w_sw", [D_MODEL, 2048], F32, kind="ExternalInput").ap()
        self.b_wo = d("b_w_out", [D_MODEL, D_MODEL], F32, kind="ExternalInput").ap()
        self.b_small = d("b_small", [128, 256 + 4], F32, kind="ExternalInput").ap()
        self.kbias_in = d("kbias", [128, (T + 127) // 128], F32, kind="ExternalInput").ap()
        self.qkT = d("qkT", [2048, T], BF16).ap()
        self.v_tm = d("v_tm", [T, 1024], BF16).ap()
        self.oT = d("oT", [D_MODEL, T], BF16).ap()
        self.tri_in = d("tri", [64, 4 * 64], F32, kind="ExternalInput").ap()
        self.tmask_rep = d("tmask_rep", [128, T], F32, kind="ExternalInput").ap()
        self.tmask_col = d("tmask_col", [128, (T + 127) // 128], F32, kind="ExternalInput").ap()
        self.c_w = d("c_w_in", [D_MODEL, 5120], F32, kind="ExternalInput").ap()
        self.c_wo = d("c_w_out", [D_MODEL, D_MODEL], F32, kind="ExternalInput").ap()
        self.c_small = d("c_small", [128, 32 + 1], F32, kind="ExternalInput").ap()
        self.fmA = d("fmA", [D_MODEL, T], BF16).ap()
        self.fmB = [d("fmB%d" % i, [D_MODEL, T], BF16).ap() for i in range(2)]
        self.fmG = d("fmG", [D_MODEL, T], BF16).ap()
        self.tmK = [d("tmK%d" % i, [T, D_MODEL], BF16).ap() for i in range(2)]
        self.tmL = [d("tmL%d" % i, [T, D_MODEL], F32).ap() for i in range(2)]
        self.obT = d("obT", [D_MODEL, T], F32).ap()
        self.lbrow_t = d("lbrow", [2, D_MODEL], F32)
        self.n_a = (depth + 2) // 3
        self.a_w = d("a_w_in", [self.n_a, D_MODEL, 4128], F32, kind="ExternalInput").ap()
        self.a_wo = d("a_w_out", [self.n_a, D_MODEL, D_MODEL], F32, kind="ExternalInput").ap()
        self.a_small = d("a_small", [self.n_a, 128, 32 + 1 + 120], F32, kind="ExternalInput").ap()
        self.pre = d("pre", [3 * D_MODEL, T], BF16).ap()
        self.ba_tm = d("ba_tm", [T, 32], F32).ap()
        self.identb_in = d("identb", [128, 128], BF16, kind="ExternalInput").ap()
        self.k = KB(nc)
        k = self.k
        self.ident = k.gsb([128, 128], F32, "identc")
        self.ones_bf = k.gsb([128, 128], BF16, "onesbf")
        self.nw = k.gsb([128, (depth * 3 + 1) * 8], F32, "nwcol")
        self.build()
        k.close()

    def phase_in(self):
        k, nc, T = self.k, self.nc, self.T
        k.begin()
        k.dma(k.sp, self.ident[:], self.ident_in, writes=[self.ident])
        k.op(k.dve, lambda e: e.memset(self.ones_bf[:], 1.0), writes=[self.ones_bf])
        nrows = self.depth * 3 + 1
        k.dma(k.sp, self.nw[:], self.norm_w, writes=[self.nw])
        xt = [k.sb([128, 4, D_MODEL], F32, "xt") for _ in range(2)]
        st = [k.sb([128, 8, 512], F32, "st") for _ in range(2)]
        pst = [k.ps([128, 512], F32, "pst") for _ in range(4)]
        pi = 0
        for gi, (t0, n) in enumerate(tiles_of(T, 512)):
            x = xt[gi % 2]
            s = st[gi % 2]
            nb = (n + 127) // 128
            blocks = [(b * 128, min(128, n - b * 128)) for b in range(nb)]
            if n % 128 == 0:
                k.dma(k.sp, x[:, 0:nb, :], self.xin[t0:t0 + n, :].rearrange("(j p) f -> p j f", p=128), writes=[x])
            else:
                for b, (o, r) in enumerate(blocks):
                    k.dma(k.sp, x[0:r, b, :], self.xin[t0 + o:t0 + o + r, :], writes=[x])
            for c in range(8):
                p = pst[pi % 4]
                pi += 1
                for b, (o, r) in enumerate(blocks):
                    k.op(k.pe, (lambda e, p=p, x=x, b=b, o=o, r=r, c=c: e.transpose(
                        p[:, o:o + r], x[0:r, b, c * 128:(c + 1) * 128], self.ident[0:r, 0:r])),
                        reads=[x, self.ident], writes=[p])
                eng = k.dve if c % 2 == 0 else k.act
                if eng is k.dve:
                    k.op(eng, (lambda e, p=p, s=s, c=c, n=n: e.tensor_copy(s[:, c, 0:n], p[:, 0:n])), reads=[p], writes=[s])
                else:
                    k.op(eng, (lambda e, p=p, s=s, c=c, n=n: e.copy(s[:, c, 0:n], p[:, 0:n])), reads=[p], writes=[s])
            k.dma(k.sp, self.hT.rearrange("(c p) t -> p c t", p=128)[:, :, t0:t0 + n], s[:, :, 0:n], reads=[s])
        k.end()


    def emit_norm(self, y, yv, hn, hnv, tl, nrow, pd, sq, rstd):
        k = self.k
        for ti, (o, n) in enumerate(tl):
            ss = pd[ti % 4]
            for c in range(8):
                s_ = sq[c % 2]
                k.op(k.act, (lambda e, s_=s_, c=c, o=o, n=n: e.activation(s_[:, 0:n], y[:, c, o:o + n], AF.Square)),
                     reads=[yv[c][ti]], writes=[s_])
                k.op(k.pe, (lambda e, ss=ss, s_=s_, c=c, n=n: e.matmul(ss[:, 0:n], lhsT=self.ones_bf[:], rhs=s_[:, 0:n],
                                                                        start=(c == 0), stop=(c == 7))),
                     reads=[s_, self.ones_bf], writes=[ss])
            r = rstd[ti % 2]
            k.op(k.act, (lambda e, r=r, ss=ss, n=n: e.activation(r[:, 0:n], ss[:, 0:n], AF.Ln, bias=self.eps_col[:, 0:1],
                                                                  scale=1.0 / D_MODEL)),
                 reads=[ss, self.eps_col], writes=[r])
            k.op(k.act, (lambda e, r=r, n=n: e.activation(r[:, 0:n], r[:, 0:n], AF.Exp, scale=-0.5)), reads=[r], writes=[r])
            for c in range(8):
                col = nrow * 8 + c
                k.op(k.dve, (lambda e, r=r, c=c, o=o, n=n, col=col: e.scalar_tensor_tensor(
                    out=hn[:, c, o:o + n], in0=y[:, c, o:o + n], scalar=self.nw[:, col:col + 1], in1=r[:, 0:n],
                    op0=ALU.mult, op1=ALU.mult)),
                    reads=[yv[c][ti], r, self.nw], writes=[hnv[c][ti]])

    def phase_ffn(self, li, fj, nrow):
        k, nc = self.k, self.nc
        HG = 256
        NG = D_FF // HG
        hTv = self.hT.rearrange("(c p) t -> p c t", p=128)
        for (T0, GS) in self.groups:
            tl = tiles_of(GS, 512)
            NT = len(tl)
            k.begin()
            y = k.sb([128, 8, GS], F32, "y")
            hn = k.sb([128, 8, GS], BF16, "hn")
            yv = [k.views(y, NT) for _ in range(8)]
            hnv = [k.views(hn, NT) for _ in range(8)]
            sq = [k.sb([128, 512], BF16, "sq") for _ in range(2)]
            rstd = [k.sb([128, 512], F32, "rstd") for _ in range(2)]
            wg = [k.sb([128, 8, HG], BF16, "wg") for _ in range(2)]
            wu = [k.sb([128, 8, HG], BF16, "wu") for _ in range(2)]
            wd = [k.sb([128, HG // 128, D_MODEL], BF16, "wd") for _ in range(2)]
            sg = [k.sb([128, 2, 512], F32, "sg") for _ in range(2)]
            act = [k.sb([128, 2, 512], BF16, "act") for _ in range(2)]
            pg = [k.ps([128, 512], F32, "pg") for _ in range(2)]
            pu = [k.ps([128, 512], F32, "pu") for _ in range(2)]
            pd = [k.ps([128, 512], F32, "pd") for _ in range(4)]
            allv = [v for c in range(8) for v in yv[c]]
            k.dma(k.sp, y[:], hTv[:, :, T0:T0 + GS], writes=allv)
            self.emit_norm(y, yv, hn, hnv, tl, nrow, pd, sq, rstd)
            wup = self.w_up[li, fj]
            wdn = self.w_dn[li, fj]
            it = 0
            for g in range(NG):
                b = g % 2
                k.dma(k.pool, wg[b][:], wup[:, g * HG:(g + 1) * HG].rearrange("(kk p) c -> p kk c", p=128), writes=[wg[b]])
                k.dma(k.pool, wu[b][:], wup[:, D_FF + g * HG:D_FF + (g + 1) * HG].rearrange("(kk p) c -> p kk c", p=128),
                      writes=[wu[b]])
                k.dma(k.pool, wd[b][:], wdn[g * HG:(g + 1) * HG, :].rearrange("(kk p) c -> p kk c", p=128), writes=[wd[b]])
                for ti, (o, n) in enumerate(tl):
                    ab = it % 2
                    it += 1
                    for j in range(2):
                        for (pt, wt) in ((pg[j], wg[b]), (pu[j], wu[b])):
                            for c in range(8):
                                k.op(k.pe, (lambda e, pt=pt, wt=wt, j=j, c=c, o=o, n=n: e.matmul(
                                    pt[:, 0:n], lhsT=wt[:, c, j * 128:(j + 1) * 128], rhs=hn[:, c, o:o + n],
                                    start=(c == 0), stop=(c == 7))),
                                    reads=[wt, hnv[c][ti]], writes=[pt])
                    for j in range(2):
                        k.op(k.act, (lambda e, j=j, ab=ab, n=n: e.activation(sg[ab][:, j, 0:n], pg[j][:, 0:n], AF.Silu)),
                             reads=[pg[j]], writes=[sg[ab]])
                    for j in range(2):
                        k.op(k.dve, (lambda e, j=j, ab=ab, n=n: e.scalar_tensor_tensor(
                            out=act[ab][:, j, 0:n], in0=pu[j][:, 0:n], scalar=0.5, in1=sg[ab][:, j, 0:n],
                            op0=ALU.mult, op1=ALU.mult)),
                            reads=[pu[j], sg[ab]], writes=[act[ab]])
                    for m in range(8):
                        pdt = pd[m % 4]
                        for j in range(2):
                            k.op(k.pe, (lambda e, pdt=pdt, b=b, j=j, m=m, ab=ab, n=n: e.matmul(
                                pdt[:, 0:n], lhsT=wd[b][:, j, m * 128:(m + 1) * 128], rhs=act[ab][:, j, 0:n],
                                start=(j == 0), stop=(j == 1))),
                                reads=[wd[b], act[ab]], writes=[pdt])
                        k.op(k.dve, (lambda e, pdt=pdt, m=m, o=o, n=n: e.tensor_tensor(
                            out=y[:, m, o:o + n], in0=y[:, m, o:o + n], in1=pdt[:, 0:n], op=ALU.add)),
                            reads=[pdt, yv[m][ti]], writes=[yv[m][ti]])
            k.dma(k.sp, hTv[:, :, T0:T0 + GS], y[:], reads=allv)
            k.end()

    def phase_out(self, nrow):
        k, nc, T = self.k, self.nc, self.T
        hTv = self.hT.rearrange("(c p) t -> p c t", p=128)
        k.begin()
        hb = [k.sb([128, 8, 512], F32, "hb") for _ in range(2)]
        sq = [k.sb([128, 512], BF16, "sq") for _ in range(2)]
        rstd = [k.sb([128, 512], F32, "rstd") for _ in range(2)]
        yn = [k.sb([128, 8, 512], F32, "yn") for _ in range(2)]
        ot = [k.sb([128, 4, D_MODEL], F32, "ot") for _ in range(2)]
        pss = k.ps([128, 512], F32, "pss")
        pt = [k.ps([128, 512], F32, "pt") for _ in range(4)]
        pi = 0
        for gi, (t0, n) in enumerate(tiles_of(T, 512)):
            h = hb[gi % 2]
            r = rstd[gi % 2]
            yy = yn[gi % 2]
            o_ = ot[gi % 2]
            k.dma(k.sp, h[:, :, 0:n], hTv[:, :, t0:t0 + n], writes=[h])
            for c in range(8):
                s_ = sq[c % 2]
                k.op(k.act, (lambda e, s_=s_, h=h, c=c, n=n: e.activation(s_[:, 0:n], h[:, c, 0:n], AF.Square)),
                     reads=[h], writes=[s_])
                k.op(k.pe, (lambda e, s_=s_, c=c, n=n: e.matmul(pss[:, 0:n], lhsT=self.ones_bf[:], rhs=s_[:, 0:n],
                                                                 start=(c == 0), stop=(c == 7))),
                     reads=[s_, self.ones_bf], writes=[pss])
            k.op(k.act, (lambda e, r=r, n=n: e.activation(r[:, 0:n], pss[:, 0:n], AF.Ln, bias=self.eps_col[:, 0:1],
                                                           scale=1.0 / D_MODEL)),
                 reads=[pss, self.eps_col], writes=[r])
            k.op(k.act, (lambda e, r=r, n=n: e.activation(r[:, 0:n], r[:, 0:n], AF.Exp, scale=-0.5)), reads=[r], writes=[r])
            for c in range(8):
                col = nrow * 8 + c
                k.op(k.dve, (lambda e, r=r, h=h, yy=yy, c=c, n=n, col=col: e.scalar_tensor_tensor(
                    out=yy[:, c, 0:n], in0=h[:, c, 0:n], scalar=self.nw[:, col:col + 1], in1=r[:, 0:n],
                    op0=ALU.mult, op1=ALU.mult)),
                    reads=[h, r, self.nw], writes=[yy])
            nb = (n + 127) // 128
            blocks = [(b * 128, min(128, n - b * 128)) for b in range(nb)]
            for b, (o, rr) in enumerate(blocks):
                for half in range(2):
                    p = pt[pi % 4]
                    pi += 1
                    for cc in range(4):
                        c = half * 4 + cc
                        k.op(k.pe, (lambda e, p=p, yy=yy, c=c, cc=cc, o=o, rr=rr: e.transpose(
                            p[0:rr, cc * 128:(cc + 1) * 128], yy[:, c, o:o + rr], self.ident[:])),
                            reads=[yy, self.ident], writes=[p])
                    if half == 0:
                        k.op(k.dve, (lambda e, p=p, o_=o_, b=b, rr=rr: e.tensor_copy(o_[0:rr, b, 0:512], p[0:rr, :])),
                             reads=[p], writes=[o_])
                    else:
                        k.op(k.act, (lambda e, p=p, o_=o_, b=b, rr=rr: e.copy(o_[0:rr, b, 512:1024], p[0:rr, :])),
                             reads=[p], writes=[o_])
            if n % 128 == 0:
                k.dma(k.sp, self.yout[t0:t0 + n, :].rearrange("(j p) f -> p j f", p=128), o_[:, 0:nb, :], reads=[o_])
            else:
                for b, (o, rr) in enumerate(blocks):
                    k.dma(k.sp, self.yout[t0 + o:t0 + o + rr, :], o_[0:rr, b, :], reads=[o_])
        k.end()


    def emit_rope_tables(self, T0, tl, cosF, sinF, tmp):
        k = self.k
        PI = float(np.pi)
        invf = self.bsm[:, 257:258]
        sign = self.bsm[:, 258:259]
        for ti, (o, n) in enumerate(tl):
            pos, ang, ki, kf, tf = tmp
            k.op(k.pool, (lambda e, pos=pos, n=n, b=T0 + o: e.iota(pos[:, 0:n], [[1, n]], base=b, channel_multiplier=0,
                                                                   allow_small_or_imprecise_dtypes=True)), writes=[pos])
            k.op(k.dve, (lambda e, n=n: e.tensor_scalar(ang[:, 0:n], pos[:, 0:n], invf, None, op0=ALU.mult)),
                 reads=[pos, self.bsm], writes=[ang])
            def reduce(src, dst, shift, n=n):
                k.op(k.dve, (lambda e: e.tensor_scalar(dst[:, 0:n], src[:, 0:n], shift, None, op0=ALU.add)),
                     reads=[src], writes=[dst])
                k.op(k.dve, (lambda e: e.tensor_scalar(ki[:, 0:n], dst[:, 0:n], 1.0 / (2 * PI), None, op0=ALU.mult)),
                     reads=[dst], writes=[ki])
                k.op(k.dve, (lambda e: e.tensor_copy(tf[:, 0:n], ki[:, 0:n])), reads=[ki], writes=[tf])
                k.op(k.dve, (lambda e: e.scalar_tensor_tensor(out=dst[:, 0:n], in0=tf[:, 0:n], scalar=-2 * PI, in1=dst[:, 0:n],
                                                              op0=ALU.mult, op1=ALU.add)), reads=[tf, dst], writes=[dst])
                k.op(k.dve, (lambda e: e.tensor_scalar(tf[:, 0:n], dst[:, 0:n], PI, -2 * PI, op0=ALU.is_gt, op1=ALU.mult)),
                     reads=[dst], writes=[tf])
                k.op(k.dve, (lambda e: e.tensor_tensor(out=dst[:, 0:n], in0=dst[:, 0:n], in1=tf[:, 0:n], op=ALU.add)),
                     reads=[dst, tf], writes=[dst])
                k.op(k.dve, (lambda e: e.tensor_scalar(tf[:, 0:n], dst[:, 0:n], -PI, 2 * PI, op0=ALU.is_lt, op1=ALU.mult)),
                     reads=[dst], writes=[tf])
                k.op(k.dve, (lambda e: e.tensor_tensor(out=dst[:, 0:n], in0=dst[:, 0:n], in1=tf[:, 0:n], op=ALU.add)),
                     reads=[dst, tf], writes=[dst])
                k.op(k.dve, (lambda e: e.tensor_scalar(dst[:, 0:n], dst[:, 0:n], -PI, PI, op0=ALU.max, op1=ALU.min)),
                     reads=[dst], writes=[dst])
            reduce(ang, kf, 0.0)
            reduce(ang, pos, PI / 2)
            s_, c_ = sinF[ti], cosF[ti]
            k.op(k.act, (lambda e, s_=s_, n=n: e.activation(s_[:, 0:n], kf[:, 0:n], AF.Sin)), reads=[kf], writes=[s_])
            k.op(k.act, (lambda e, c_=c_, n=n: e.activation(c_[:, 0:n], pos[:, 0:n], AF.Sin)), reads=[pos], writes=[c_])
            k.op(k.dve, (lambda e, s_=s_, n=n: e.tensor_scalar(s_[:, 0:n], s_[:, 0:n], sign, None, op0=ALU.mult)),
                 reads=[s_, self.bsm], writes=[s_])

    def phase_attn_proj(self, nrow):
        k, nc = self.k, self.nc
        hTv = self.hT.rearrange("(c p) t -> p c t", p=128)
        for (T0, GS) in self.groups:
            tl = tiles_of(GS, 512)
            NT = len(tl)
            k.begin()
            y = k.sb([128, 8, GS], F32, "y")
            hn = k.sb([128, 8, GS], BF16, "hn")
            yv = [k.views(y, NT) for _ in range(8)]
            hnv = [k.views(hn, NT) for _ in range(8)]
            sq = [k.sb([128, 512], BF16, "sq") for _ in range(2)]
            rstd = [k.sb([128, 512], F32, "rstd") for _ in range(2)]
            pd = [k.ps([128, 512], F32, "pd") for _ in range(4)]
            pa = [k.ps([128, 512], F32, "pa") for _ in range(2)]
            pb = [k.ps([128, 512], F32, "pb") for _ in range(2)]
            allv = [v for c in range(8) for v in yv[c]]
            k.dma(k.sp, y[:], hTv[:, :, T0:T0 + GS], writes=allv)
            self.emit_norm(y, yv, hn, hnv, tl, nrow, pd, sq, rstd)
            cosF = [k.sb([128, 512], F32, "cosF") for _ in range(NT)]
            sinF = [k.sb([128, 512], F32, "sinF") for _ in range(NT)]
            tmp = (k.sb([128, 512], F32, "pos"), k.sb([128, 512], F32, "ang"),
                   k.sb([128, 512], mybir.dt.int32, "ki"), k.sb([128, 512], F32, "kf"), k.sb([128, 512], F32, "tf"))
            self.emit_rope_tables(T0, tl, cosF, sinF, tmp)
            wa = [k.sb([128, 8, 128], BF16, "wa") for _ in range(2)]
            wb = [k.sb([128, 8, 128], BF16, "wb") for _ in range(2)]
            t1 = [k.sb([128, 512], F32, "t1") for _ in range(2)]
            t2 = [k.sb([128, 512], F32, "t2") for _ in range(2)]
            stg = [k.sb([128, 512], BF16, "stg") for _ in range(3)]
            it = 0
            for m in range(16):
                b = m % 2
                k.dma(k.pool, wa[b][:], self.b_w[:, m * 128:(m + 1) * 128].rearrange("(kk p) c -> p kk c", p=128), writes=[wa[b]])
                k.dma(k.pool, wb[b][:], self.b_wsw[:, m * 128:(m + 1) * 128].rearrange("(kk p) c -> p kk c", p=128), writes=[wb[b]])
                for ti, (o, n) in enumerate(tl):
                    ab = it % 2
                    sb_ = stg[it % 3]
                    it += 1
                    for (pt, wt) in ((pa[ab], wa[b]), (pb[ab], wb[b])):
                        for c in range(8):
                            k.op(k.pe, (lambda e, pt=pt, wt=wt, c=c, o=o, n=n: e.matmul(
                                pt[:, 0:n], lhsT=wt[:, c, :], rhs=hn[:, c, o:o + n], start=(c == 0), stop=(c == 7))),
                                reads=[wt, hnv[c][ti]], writes=[pt])
                    k.op(k.dve, (lambda e, ab=ab, ti=ti, n=n: e.tensor_tensor(out=t1[ab][:, 0:n], in0=pa[ab][:, 0:n],
                                                                              in1=cosF[ti][:, 0:n], op=ALU.mult)),
                         reads=[pa[ab], cosF[ti]], writes=[t1[ab]])
                    k.op(k.dve, (lambda e, ab=ab, ti=ti, n=n: e.tensor_tensor(out=t2[ab][:, 0:n], in0=pb[ab][:, 0:n],
                                                                              in1=sinF[ti][:, 0:n], op=ALU.mult)),
                         reads=[pb[ab], sinF[ti]], writes=[t2[ab]])
                    k.op(k.pool, (lambda e, ab=ab, sb_=sb_, n=n: e.tensor_tensor(out=sb_[:, 0:n], in0=t1[ab][:, 0:n],
                                                                                 in1=t2[ab][:, 0:n], op=ALU.add)),
                         reads=[t1[ab], t2[ab]], writes=[sb_])
                    k.dma(k.sp, self.qkT[m * 128:(m + 1) * 128, T0 + o:T0 + o + n], sb_[:, 0:n], reads=[sb_])
            wv = [k.sb([128, 8, 512], BF16, "wv") for _ in range(2)]
            vst = [k.sb([128, 512], BF16, "vst") for _ in range(3)]
            it = 0
            for vb in range(2):
                k.dma(k.pool, wv[vb][:], self.b_w[:, 2048 + vb * 512:2048 + (vb + 1) * 512].rearrange("(kk p) c -> p kk c", p=128),
                      writes=[wv[vb]])
                for ti, (o, n) in enumerate(tl):
                    for bo in range(0, n, 128):
                        r = min(128, n - bo)
                        pt = pd[it % 4]
                        vs = vst[it % 3]
                        it += 1
                        for c in range(8):
                            k.op(k.pe, (lambda e, pt=pt, vb=vb, c=c, o=o, bo=bo, r=r: e.matmul(
                                pt[0:r, :], lhsT=hn[:, c, o + bo:o + bo + r], rhs=wv[vb][:, c, :], start=(c == 0), stop=(c == 7))),
                                reads=[wv[vb], hnv[c][ti]], writes=[pt])
                        k.op(k.act, (lambda e, pt=pt, vs=vs, r=r: e.copy(vs[0:r, :], pt[0:r, :])), reads=[pt], writes=[vs])
                        k.dma(k.sp, self.v_tm[T0 + o + bo:T0 + o + bo + r, vb * 512:(vb + 1) * 512], vs[0:r, :], reads=[vs])
            k.end()

    def phase_attn_core(self):
        k, nc, T = self.k, self.nc, self.T
        NKT = (T + 127) // 128
        qtl = tiles_of(T, 512)
        k.begin()
        lt = k.sb([128, 256], F32, "lt")
        l2 = k.sb([128, 2], F32, "l2")
        neglam = k.sb([128, 1], F32, "neglam")
        subw = k.sb([128, 1], F32, "subw")
        kb = k.sb([128, NKT], F32, "kb")
        k.dma(k.sp, kb[:], self.kbias_in, writes=[kb])
        k.op(k.dve, (lambda e: e.tensor_tensor(out=lt[:, 0:64], in0=self.bsm[:, 0:64], in1=self.bsm[:, 64:128], op=ALU.mult)),
             reads=[self.bsm], writes=[lt])
        k.op(k.dve, (lambda e: e.tensor_tensor(out=lt[:, 64:128], in0=self.bsm[:, 128:192], in1=self.bsm[:, 192:256], op=ALU.mult)),
             reads=[self.bsm], writes=[lt])
        k.op(k.dve, (lambda e: e.reduce_sum(l2[:, 0:2], lt[:, 0:128].rearrange("p (a b) -> p a b", a=2), axis=AX.X)),
             reads=[lt], writes=[l2])
        k.op(k.act, (lambda e: e.activation(l2[:, 0:2], l2[:, 0:2], AF.Exp)), reads=[l2], writes=[l2])
        k.op(k.dve, (lambda e: e.tensor_tensor(out=neglam[:], in0=l2[:, 1:2], in1=l2[:, 0:1], op=ALU.subtract)),
             reads=[l2], writes=[neglam])
        k.op(k.dve, (lambda e: e.tensor_tensor(out=neglam[:], in0=neglam[:], in1=self.bsm[:, 259:260], op=ALU.subtract)),
             reads=[neglam, self.bsm], writes=[neglam])
        k.op(k.dve, (lambda e: e.tensor_scalar(subw[:], self.bsm[:, 259:260], -1.0, 1.0, op0=ALU.mult, op1=ALU.add)),
             reads=[self.bsm], writes=[subw])
        k.op(k.dve, (lambda e: e.tensor_tensor(out=subw[:], in0=subw[:], in1=self.bsm[:, 256:257], op=ALU.mult)),
             reads=[subw, self.bsm], writes=[subw])

        kk1 = [k.sb([64, T], BF16, "kk1") for _ in range(2)]
        kk2 = [k.sb([64, T], BF16, "kk2") for _ in range(2)]
        vh = [k.sb([128, NKT, 128], BF16, "vh") for _ in range(2)]
        q1 = [k.sb([64, 512], BF16, "q1") for _ in range(2)]
        q2 = [k.sb([64, 512], BF16, "q2") for _ in range(2)]
        p1 = [k.sb([128, 512], BF16, "p1") for _ in range(3)]
        p2 = [k.sb([128, 512], BF16, "p2") for _ in range(3)]
        ps1 = [k.ps([128, 512], F32, "ps1") for _ in range(2)]
        ps2 = [k.ps([128, 512], F32, "ps2") for _ in range(2)]
        num1, num2, z1, z2 = (k.ps([128, 512], F32, nm) for nm in ("num1", "num2", "z1", "z2"))
        r1 = k.sb([128, 512], F32, "r1")
        r2 = k.sb([128, 512], F32, "r2")
        o1 = k.sb([128, 512], F32, "o1")
        o2 = k.sb([128, 512], F32, "o2")
        oo = k.sb([128, 512], F32, "oo")
        sqo = k.sb([128, 512], BF16, "sqo")
        rs = k.sb([128, 512], F32, "rs")
        ost = [k.sb([128, 512], BF16, "ost") for _ in range(2)]
        nfull = T // 128
        rem = T - nfull * 128
        it = 0
        fi = 0
        for h in range(8):
            hb = h % 2
            k.dma(k.sp, kk1[hb][:], self.qkT[1024 + h * 64:1024 + (h + 1) * 64, :], writes=[kk1[hb]])
            k.dma(k.sp, kk2[hb][:], self.qkT[1536 + h * 64:1536 + (h + 1) * 64, :], writes=[kk2[hb]])
            k.dma(k.sp, vh[hb][:, 0:nfull, :],
                  self.v_tm[0:nfull * 128, h * 128:(h + 1) * 128].rearrange("(kt p) e -> p kt e", p=128), writes=[vh[hb]])
            if rem:
                k.dma(k.sp, vh[hb][0:rem, nfull, :], self.v_tm[nfull * 128:T, h * 128:(h + 1) * 128], writes=[vh[hb]])
            for qi, (t0, n) in enumerate(qtl):
                qb = fi % 2
                k.dma(k.sp, q1[qb][:, 0:n], self.qkT[h * 64:(h + 1) * 64, t0:t0 + n], writes=[q1[qb]])
                k.dma(k.sp, q2[qb][:, 0:n], self.qkT[512 + h * 64:512 + (h + 1) * 64, t0:t0 + n], writes=[q2[qb]])
                for kt in range(NKT):
                    kn = 128 if kt < nfull else rem
                    sb2 = it % 2
                    pb3 = it % 3
                    it += 1
                    first, last = (kt == 0), (kt == NKT - 1)
                    for (ps, kk, qq, pp, nm, zz) in ((ps1[sb2], kk1[hb], q1[qb], p1[pb3], num1, z1),
                                                     (ps2[sb2], kk2[hb], q2[qb], p2[pb3], num2, z2)):
                        k.op(k.pe, (lambda e, ps=ps, kk=kk, qq=qq, kt=kt, kn=kn, n=n: e.matmul(
                            ps[0:kn, 0:n], lhsT=kk[:, kt * 128:kt * 128 + kn], rhs=qq[:, 0:n], start=True, stop=True)),
                            reads=[kk, qq], writes=[ps])
                        k.op(k.act, (lambda e, ps=ps, pp=pp, kt=kt, kn=kn, n=n: e.activation(
                            pp[0:kn, 0:n], ps[0:kn, 0:n], AF.Exp, bias=kb[0:kn, kt:kt + 1], scale=0.125)),
                            reads=[ps, kb], writes=[pp])
                        k.op(k.pe, (lambda e, nm=nm, pp=pp, hb=hb, kt=kt, kn=kn, n=n, first=first, last=last: e.matmul(
                            nm[:, 0:n], lhsT=vh[hb][0:kn, kt, :], rhs=pp[0:kn, 0:n], start=first, stop=last)),
                            reads=[vh[hb], pp], writes=[nm])
                        k.op(k.pe, (lambda e, zz=zz, pp=pp, kn=kn, n=n, first=first, last=last: e.matmul(
                            zz[:, 0:n], lhsT=self.ones_bf[0:kn, :], rhs=pp[0:kn, 0:n], start=first, stop=last)),
                            reads=[self.ones_bf, pp], writes=[zz])
                k.op(k.dve, (lambda e, n=n: e.reciprocal(r1[:, 0:n], z1[:, 0:n])), reads=[z1], writes=[r1])
                k.op(k.dve, (lambda e, n=n: e.reciprocal(r2[:, 0:n], z2[:, 0:n])), reads=[z2], writes=[r2])
                k.op(k.dve, (lambda e, n=n: e.tensor_tensor(out=o1[:, 0:n], in0=num1[:, 0:n], in1=r1[:, 0:n], op=ALU.mult)),
                     reads=[num1, r1], writes=[o1])
                k.op(k.dve, (lambda e, n=n: e.scalar_tensor_tensor(out=o2[:, 0:n], in0=num2[:, 0:n], scalar=neglam[:, 0:1],
                                                                   in1=r2[:, 0:n], op0=ALU.mult, op1=ALU.mult)),
                     reads=[num2, r2, neglam], writes=[o2])
                k.op(k.pool, (lambda e, n=n: e.tensor_tensor(out=oo[:, 0:n], in0=o1[:, 0:n], in1=o2[:, 0:n], op=ALU.add)),
                     reads=[o1, o2], writes=[oo])
                k.op(k.act, (lambda e, n=n: e.activation(sqo[:, 0:n], oo[:, 0:n], AF.Square)), reads=[oo], writes=[sqo])
                pss = ps1[it % 2]
                k.op(k.pe, (lambda e, pss=pss, n=n: e.matmul(pss[:, 0:n], lhsT=self.ones_bf[:], rhs=sqo[:, 0:n], start=True, stop=True)),
                     reads=[self.ones_bf, sqo], writes=[pss])
                k.op(k.act, (lambda e, pss=pss, n=n: e.activation(rs[:, 0:n], pss[:, 0:n], AF.Ln, bias=self.eps_col[:, 0:1],
                                                                   scale=1.0 / 128)), reads=[pss, self.eps_col], writes=[rs])
                k.op(k.act, (lambda e, n=n: e.activation(rs[:, 0:n], rs[:, 0:n], AF.Exp, scale=-0.5)), reads=[rs], writes=[rs])
                os_ = ost[fi % 2]
                fi += 1
                k.op(k.dve, (lambda e, os_=os_, n=n: e.scalar_tensor_tensor(out=os_[:, 0:n], in0=oo[:, 0:n], scalar=subw[:, 0:1],
                                                                            in1=rs[:, 0:n], op0=ALU.mult, op1=ALU.mult)),
                     reads=[oo, rs, subw], writes=[os_])
                k.dma(k.sp, self.oT[h * 128:(h + 1) * 128, t0:t0 + n], os_[:, 0:n], reads=[os_])
        k.end()

    def phase_out_proj(self, wo_ap):
        k, nc, T = self.k, self.nc, self.T
        hTv = self.hT.rearrange("(c p) t -> p c t", p=128)
        oTv = self.oT.rearrange("(c p) t -> p c t", p=128)
        k.begin()
        wo = k.sb([128, 8, D_MODEL], BF16, "wo")
        for c in range(8):
            for hf in range(2):
                k.dma(k.pool, wo[:, c, hf * 512:(hf + 1) * 512], wo_ap[c * 128:(c + 1) * 128, hf * 512:(hf + 1) * 512], writes=[wo])
        ob = [k.sb([128, 8, 512], BF16, "ob") for _ in range(2)]
        hb = [k.sb([128, 8, 512], F32, "hb") for _ in range(2)]
        pp = [k.ps([128, 512], F32, "pp") for _ in range(4)]
        pi = 0
        for gi, (t0, n) in enumerate(tiles_of(T, 512)):
            o_ = ob[gi % 2]
            h_ = hb[gi % 2]
            k.dma(k.sp, o_[:, :, 0:n], oTv[:, :, t0:t0 + n], writes=[o_])
            k.dma(k.sp, h_[:, :, 0:n], hTv[:, :, t0:t0 + n], writes=[h_])
            for m in range(8):
                p = pp[pi % 4]
                pi += 1
                for c in range(8):
                    k.op(k.pe, (lambda e, p=p, o_=o_, c=c, m=m, n=n: e.matmul(
                        p[:, 0:n], lhsT=wo[:, c, m * 128:(m + 1) * 128], rhs=o_[:, c, 0:n], start=(c == 0), stop=(c == 7))),
                        reads=[wo, o_], writes=[p])
                k.op(k.dve, (lambda e, p=p, h_=h_, m=m, n=n: e.tensor_tensor(out=h_[:, m, 0:n], in0=h_[:, m, 0:n],
                                                                             in1=p[:, 0:n], op=ALU.add)),
                     reads=[p, h_], writes=[h_])
            k.dma(k.sp, hTv[:, :, t0:t0 + n], h_[:, :, 0:n], reads=[h_])
        k.end()

    def mixer_attn(self, nrow):
        import os
        st = int(os.environ.get("ATT_STAGE", "3"))
        self.phase_attn_proj(nrow)
        if st >= 2:
            self.phase_attn_core()
        if st >= 3:
            self.phase_out_proj(self.b_wo)


    def phase_proj(self, nrow, setup, fm_jobs, tm_jobs):
        k, nc = self.k, self.nc
        hTv = self.hT.rearrange("(c p) t -> p c t", p=128)
        for (T0, GS) in self.groups:
            tl = tiles_of(GS, 512)
            NT = len(tl)
            k.begin()
            y = k.sb([128, 8, GS], F32, "y")
            hn = k.sb([128, 8, GS], BF16, "hn")
            yv = [k.views(y, NT) for _ in range(8)]
            hnv = [k.views(hn, NT) for _ in range(8)]
            sq = [k.sb([128, 512], BF16, "sq") for _ in range(2)]
            rstd = [k.sb([128, 512], F32, "rstd") for _ in range(2)]
            pd = [k.ps([128, 512], F32, "pd") for _ in range(4)]
            allv = [v for c in range(8) for v in yv[c]]
            k.dma(k.sp, y[:], hTv[:, :, T0:T0 + GS], writes=allv)
            self.emit_norm(y, yv, hn, hnv, tl, nrow, pd, sq, rstd)
            ctx = setup(T0, tl)
            wa = [k.sb([128, 8, 128], BF16, "wa") for _ in range(2)]
            it = 0
            for ji, (wap, post) in enumerate(fm_jobs):
                b = ji % 2
                k.dma(k.pool, wa[b][:], wap.rearrange("(kk p) c -> p kk c", p=128), writes=[wa[b]])
                for ti, (o, n) in enumerate(tl):
                    pt = pd[it % 4]
                    it += 1
                    for c in range(8):
                        k.op(k.pe, (lambda e, pt=pt, b=b, c=c, o=o, n=n: e.matmul(
                            pt[:, 0:n], lhsT=wa[b][:, c, :], rhs=hn[:, c, o:o + n], start=(c == 0), stop=(c == 7))),
                            reads=[wa[b], hnv[c][ti]], writes=[pt])
                    post(ctx, pt, T0, ti, o, n)
            wv = [k.sb([128, 8, 512], BF16, "wv") for _ in range(2)]
            for ji, job in enumerate(tm_jobs):
                wap, post = job[0], job[1]
                ncl = job[2] if len(job) > 2 else 512
                b = ji % 2
                k.dma(k.pool, wv[b][:, :, 0:ncl], wap.rearrange("(kk p) c -> p kk c", p=128), writes=[wv[b]])
                for ti, (o, n) in enumerate(tl):
                    for bo in range(0, n, 128):
                        r = min(128, n - bo)
                        pt = pd[it % 4]
                        it += 1
                        for c in range(8):
                            k.op(k.pe, (lambda e, pt=pt, b=b, c=c, o=o, bo=bo, r=r, ncl=ncl: e.matmul(
                                pt[0:r, 0:ncl], lhsT=hn[:, c, o + bo:o + bo + r], rhs=wv[b][:, c, 0:ncl], start=(c == 0), stop=(c == 7))),
                                reads=[wv[b], hnv[c][ti]], writes=[pt])
                        post(ctx, pt, T0 + o + bo, r)
            k.end()

    def hg_prep(self, li):
        k, nc = self.k, self.nc
        k.begin()
        e = k.sb([128, 32], F32, "lbe")
        ssum = k.sb([128, 8], F32, "lbs")
        k.op(k.act, (lambda en: en.activation(e[:], self.csm[:, 0:32], AF.Exp)), reads=[self.csm], writes=[e])
        k.op(k.dve, (lambda en: en.tensor_tensor(out=ssum[:], in0=e[:, 0:8], in1=e[:, 8:16], op=ALU.add)), reads=[e], writes=[ssum])
        k.op(k.dve, (lambda en: en.tensor_tensor(out=ssum[:], in0=ssum[:], in1=e[:, 16:24], op=ALU.add)), reads=[e, ssum], writes=[ssum])
        k.op(k.dve, (lambda en: en.tensor_tensor(out=ssum[:], in0=ssum[:], in1=e[:, 24:32], op=ALU.add)), reads=[e, ssum], writes=[ssum])
        k.op(k.dve, (lambda en: en.reciprocal(ssum[:], ssum[:])), reads=[ssum], writes=[ssum])
        lbc = self.lb_col
        k.op(k.dve, (lambda en: en.memset(lbc[:, 0:8], 0.0)), writes=[lbc])
        for r in range(1, li + 1):
            k.op(k.dve, (lambda en, r=r: en.tensor_tensor(out=lbc[:, 0:8], in0=lbc[:, 0:8], in1=e[:, r * 8:(r + 1) * 8], op=ALU.add)),
                 reads=[e, lbc], writes=[lbc])
        k.op(k.dve, (lambda en: en.tensor_tensor(out=lbc[:, 0:8], in0=lbc[:, 0:8], in1=ssum[:], op=ALU.mult)), reads=[lbc, ssum], writes=[lbc])
        k.op(k.dve, (lambda en: en.tensor_scalar(lbc[:, 8:16], lbc[:, 0:8], -1.0, 1.0, op0=ALU.mult, op1=ALU.add)), reads=[lbc], writes=[lbc])
        lbr = self.lbrow_t.ap()
        k.dma(k.sp, lbr.rearrange("r (c p) -> p r c", p=128), lbc[:].rearrange("p (r c) -> p r c", c=8), reads=[lbc],
              allow_slow_non_contiguous=True)
        k.end()
        k.begin()
        k.dma(k.sp, self.oml_row[:], bass.AP(self.lbrow_t, D_MODEL, [[0, 128], [1, D_MODEL]]), writes=[self.oml_row])
        k.end()

    def phase_hg_proj(self, nrow):
        k = self.k
        QS = 128 ** -0.5

        def setup(T0, tl):
            ctx = {}
            ctx["tm"] = [k.sb([128, 512], F32, "tmk") for _ in tl]
            for ti, (o, n) in enumerate(tl):
                k.dma(k.sp, ctx["tm"][ti][:, 0:n], self.tmask_rep[:, T0 + o:T0 + o + n], writes=[ctx["tm"][ti]])
            ctx["sg"] = [k.sb([128, 512], F32, "sg") for _ in range(2)]
            ctx["st"] = [k.sb([128, 512], BF16, "st") for _ in range(3)]
            ctx["k32"] = [k.sb([128, 512], F32, "k32") for _ in range(2)]
            ctx["lf"] = [k.sb([128, 512], F32, "lf") for _ in range(2)]
            ctx["kb"] = [k.sb([128, 512], BF16, "kb") for _ in range(2)]
            ctx["i"] = 0
            return ctx

        def post_silu(dst, scale):
            def post(ctx, pt, T0, ti, o, n):
                i = ctx["i"]
                ctx["i"] += 1
                sg, st = ctx["sg"][i % 2], ctx["st"][i % 3]
                k.op(k.act, (lambda e: e.activation(sg[:, 0:n], pt[:, 0:n], AF.Silu)), reads=[pt], writes=[sg])
                k.op(k.pool, (lambda e: e.tensor_scalar(st[:, 0:n], sg[:, 0:n], scale, None, op0=ALU.mult)), reads=[sg], writes=[st])
                return st
            return post

        def fm_q(c):
            base = post_silu(None, QS)

            def post(ctx, pt, T0, ti, o, n):
                st = base(ctx, pt, T0, ti, o, n)
                k.dma(k.sp, self.fmA[c * 128:(c + 1) * 128, T0 + o:T0 + o + n], st[:, 0:n], reads=[st])
            return post

        def fm_g(c):
            base = post_silu(None, 1.0)

            def post(ctx, pt, T0, ti, o, n):
                st = base(ctx, pt, T0, ti, o, n)
                k.dma(k.sp, self.fmG[c * 128:(c + 1) * 128, T0 + o:T0 + o + n], st[:, 0:n], reads=[st])
            return post

        def fm_k(d, c):
            def post(ctx, pt, T0, ti, o, n):
                i = ctx["i"]
                ctx["i"] += 1
                sg, st = ctx["sg"][i % 2], ctx["st"][i % 3]
                k.op(k.act, (lambda e: e.activation(sg[:, 0:n], pt[:, 0:n], AF.Sigmoid, scale=-1.0)), reads=[pt], writes=[sg])
                k.op(k.dve, (lambda e: e.scalar_tensor_tensor(out=st[:, 0:n], in0=sg[:, 0:n], scalar=self.lb_col[:, 8 + c:9 + c],
                                                              in1=ctx["tm"][ti][:, 0:n], op0=ALU.mult, op1=ALU.mult)),
                     reads=[sg, self.lb_col, ctx["tm"][ti]], writes=[st])
                k.dma(k.sp, self.fmB[d][c * 128:(c + 1) * 128, T0 + o:T0 + o + n], st[:, 0:n], reads=[st])
            return post

        def tm_v(cb):
            def post(ctx, pt, tok0, r):
                i = ctx["i"]
                ctx["i"] += 1
                st = ctx["st"][i % 3]
                k.op(k.act, (lambda e: e.copy(st[0:r, :], pt[0:r, :])), reads=[pt], writes=[st])
                k.dma(k.sp, self.v_tm[tok0:tok0 + r, cb * 512:(cb + 1) * 512], st[0:r, :], reads=[st])
            return post

        def tm_f(d, cb):
            def post(ctx, pt, tok0, r):
                i = ctx["i"]
                ctx["i"] += 1
                sg, k32, lf, kb = ctx["sg"][i % 2], ctx["k32"][i % 2], ctx["lf"][i % 2], ctx["kb"][i % 2]
                blk = tok0 // 128
                assert tok0 % 128 == 0
                k.op(k.act, (lambda e: e.activation(sg[0:r, :], pt[0:r, :], AF.Sigmoid, scale=-1.0)), reads=[pt], writes=[sg])
                k.op(k.dve, (lambda e: e.scalar_tensor_tensor(out=k32[0:r, :], in0=sg[0:r, :], scalar=self.tmc[0:r, blk:blk + 1],
                                                              in1=self.oml_row[0:r, cb * 512:(cb + 1) * 512], op0=ALU.mult, op1=ALU.mult)),
                     reads=[sg, self.tmc, self.oml_row], writes=[k32])
                k.op(k.act, (lambda e: e.activation(lf[0:r, :], k32[0:r, :], AF.Ln, bias=self.one_col[0:r, 0:1], scale=-1.0)),
                     reads=[k32, self.one_col], writes=[lf])
                k.op(k.pool, (lambda e: e.tensor_copy(kb[0:r, :], k32[0:r, :])), reads=[k32], writes=[kb])
                k.dma(k.sp, self.tmK[d][tok0:tok0 + r, cb * 512:(cb + 1) * 512], kb[0:r, :], reads=[kb])
                k.dma(k.sp, self.tmL[d][tok0:tok0 + r, cb * 512:(cb + 1) * 512], lf[0:r, :], reads=[lf])
            return post

        W = self.c_w
        fm = []
        for c in range(8):
            fm.append((W[:, c * 128:(c + 1) * 128], fm_q(c)))
        for c in range(8):
            fm.append((W[:, 2048 + c * 128:2048 + (c + 1) * 128], fm_g(c)))
        for d in range(2):
            for c in range(8):
                fm.append((W[:, 3072 + d * 1024 + c * 128:3072 + d * 1024 + (c + 1) * 128], fm_k(d, c)))
        tm = []
        for cb in range(2):
            tm.append((W[:, 1024 + cb * 512:1024 + (cb + 1) * 512], tm_v(cb)))
        for d in range(2):
            for cb in range(2):
                tm.append((W[:, 3072 + d * 1024 + cb * 512:3072 + d * 1024 + (cb + 1) * 512], tm_f(d, cb)))
        self.phase_proj(nrow, setup, fm, tm)

    def phase_hg_core(self, onorm_col):
        k, nc, T = self.k, self.nc, self.T
        tiles = tiles_of(T, 512)
        for sweep in (1, 0):
            d = sweep
            k.begin()
            if d == 0:
                Uin, Uex, Mk, lastcol = self.tri[:, 0:64], self.tri[:, 64:128], 0, 63
            else:
                Uin, Uex, Mk, lastcol = self.tri[:, 128:192], self.tri[:, 192:256], 2, 0
            mrep = k.sb([64, 512], F32, "mrep")
            for c in range(8):
                k.op(k.dve, (lambda e, c=c, Mk=Mk: e.tensor_copy(mrep[:, c * 64:(c + 1) * 64], self.tri[:, Mk * 64:(Mk + 1) * 64])),
                     reads=[self.tri], writes=[mrep])
            S = [k.sb([128, 128], F32, "S") for _ in range(8)]
            Sb = [k.sb([128, 128], BF16, "Sb") for _ in range(8)]
            for h in range(8):
                k.op(k.dve, (lambda e, h=h: e.memset(S[h][:], 0.0)), writes=[S[h]])
                k.op(k.pool, (lambda e, h=h: e.memset(Sb[h][:], 0.0)), writes=[Sb[h]])
            NB = 2
            qT = [k.sb([128, 512], BF16, "qT") for _ in range(NB)]
            kT = [k.sb([128, 512], BF16, "kT") for _ in range(NB)]
            ktm = [k.sb([64, 8, 128], BF16, "ktm") for _ in range(NB)]
            lf = [k.sb([64, 8, 128], F32, "lf") for _ in range(NB)]
            vt = [k.sb([64, 8, 128], BF16, "vt") for _ in range(NB)]
            eb = [k.sb([128, 512], F32, "eb") for _ in range(NB)]
            enb = [k.sb([128, 512], F32, "enb") for _ in range(NB)]
            qd = [k.sb([128, 512], BF16, "qd") for _ in range(NB)]
            kd = [k.sb([128, 512], BF16, "kd") for _ in range(NB)]
            ekd = [k.sb([64, 8, 128], F32, "ekd") for _ in range(NB)]
            kdec = [k.sb([64, 8, 128], BF16, "kdec") for _ in range(NB)]
            atm = [k.sb([64, 512], BF16, "atm") for _ in range(NB)]
            pbc = k.ps([128, 512], F32, "pbc")
            psuf = [k.ps([64, 4, 128], F32, "psuf") for _ in range(2)]
            pat = k.ps([64, 512], F32, "pat")
            pout = k.ps([128, 512], F32, "pout")
            pst = [k.ps([128, 128], F32, "pst") for _ in range(2)]
            ost = [k.sb([128, 512], F32, "ost") for _ in range(2)]
            if d == 0:
                obt = [k.sb([128, 512], F32, "obt") for _ in range(2)]
                gt = [k.sb([128, 512], BF16, "gt") for _ in range(2)]
                sqo = k.sb([128, 512], BF16, "sqo")
                rs = k.sb([128, 512], F32, "rs")
                fin = [k.sb([128, 512], BF16, "fin") for _ in range(2)]
            it = 0
            order = list(range(len(tiles)))
            if d == 1:
                order = order[::-1]
            for ti in order:
                t0, n = tiles[ti]
                ncn = n // 64
                corder = list(range(ncn)) if d == 0 else list(range(ncn))[::-1]
                for h in range(8):
                    b = it % NB
                    it += 1
                    rows = slice(h * 128, (h + 1) * 128)
                    k.dma(k.sp, qT[b][:, 0:n], self.fmA[rows, t0:t0 + n], writes=[qT[b]])
                    k.dma(k.sp, kT[b][:, 0:n], self.fmB[d][rows, t0:t0 + n], writes=[kT[b]])
                    k.dma(k.sp, ktm[b][:, 0:ncn, :], self.tmK[d][t0:t0 + n, rows].rearrange("(c p) e -> p c e", p=64), writes=[ktm[b]])
                    k.dma(k.sp, lf[b][:, 0:ncn, :], self.tmL[d][t0:t0 + n, rows].rearrange("(c p) e -> p c e", p=64), writes=[lf[b]])
                    k.dma(k.sp, vt[b][:, 0:ncn, :], self.v_tm[t0:t0 + n, rows].rearrange("(c p) e -> p c e", p=64), writes=[vt[b]])
                    if d == 0:
                        k.dma(k.sp, obt[b][:, 0:n], self.obT[rows, t0:t0 + n], writes=[obt[b]])
                        k.dma(k.sp, gt[b][:, 0:n], self.fmG[rows, t0:t0 + n], writes=[gt[b]])
                    for c in range(ncn):
                        k.op(k.pe, (lambda e, b=b, c=c: e.matmul(pbc[:, c * 64:(c + 1) * 64], lhsT=lf[b][:, c, :], rhs=Uin,
                                                                 start=True, stop=True)), reads=[lf[b], self.tri], writes=[pbc])
                    k.op(k.act, (lambda e, b=b, n=n: e.activation(eb[b][:, 0:n], pbc[:, 0:n], AF.Exp)), reads=[pbc], writes=[eb[b]])
                    k.op(k.act, (lambda e, b=b, n=n: e.activation(enb[b][:, 0:n], pbc[:, 0:n], AF.Exp, scale=-1.0)), reads=[pbc], writes=[enb[b]])
                    k.op(k.dve, (lambda e, b=b, n=n: e.tensor_tensor(out=qd[b][:, 0:n], in0=qT[b][:, 0:n], in1=eb[b][:, 0:n], op=ALU.mult)),
                         reads=[qT[b], eb[b]], writes=[qd[b]])
                    k.op(k.dve, (lambda e, b=b, n=n: e.tensor_tensor(out=kd[b][:, 0:n], in0=kT[b][:, 0:n], in1=enb[b][:, 0:n], op=ALU.mult)),
                         reads=[kT[b], enb[b]], writes=[kd[b]])
                    for c in range(ncn):
                        ps_ = psuf[c // 4]
                        k.op(k.pe, (lambda e, b=b, c=c, ps_=ps_: e.matmul(ps_[:, c % 4, :], lhsT=Uex, rhs=lf[b][:, c, :],
                                                                          start=True, stop=True)), reads=[lf[b], self.tri], writes=[ps_])
                    for half in range((ncn + 3) // 4):
                        nn = min(4, ncn - half * 4)
                        k.op(k.act, (lambda e, b=b, half=half, nn=nn: e.activation(ekd[b][:, half * 4:half * 4 + nn, :],
                                                                                  psuf[half][:, 0:nn, :], AF.Exp)),
                             reads=[psuf[half]], writes=[ekd[b]])
                    k.op(k.dve, (lambda e, b=b, ncn=ncn: e.tensor_tensor(out=kdec[b][:, 0:ncn, :], in0=ktm[b][:, 0:ncn, :],
                                                                        in1=ekd[b][:, 0:ncn, :], op=ALU.mult)),
                         reads=[ktm[b], ekd[b]], writes=[kdec[b]])
                    for c in range(ncn):
                        cs = slice(c * 64, (c + 1) * 64)
                        k.op(k.pe, (lambda e, b=b, cs=cs: e.matmul(pat[:, cs], lhsT=kd[b][:, cs], rhs=qd[b][:, cs], start=True, stop=True)),
                             reads=[kd[b], qd[b]], writes=[pat])
                    k.op(k.dve, (lambda e, b=b, n=n: e.tensor_tensor(out=atm[b][:, 0:n], in0=pat[:, 0:n], in1=mrep[:, 0:n], op=ALU.mult)),
                         reads=[pat, mrep], writes=[atm[b]])
                    for c in corder:
                        cs = slice(c * 64, (c + 1) * 64)
                        k.op(k.pe, (lambda e, b=b, cs=cs, h=h: e.matmul(pout[:, cs], lhsT=Sb[h][:], rhs=qd[b][:, cs], start=True, stop=False)),
                             reads=[Sb[h], qd[b]], writes=[pout])
                        k.op(k.pe, (lambda e, b=b, cs=cs, c=c: e.matmul(pout[:, cs], lhsT=vt[b][:, c, :], rhs=atm[b][:, cs], start=False, stop=True)),
                             reads=[vt[b], atm[b]], writes=[pout])
                        pp = pst[c % 2]
                        k.op(k.pe, (lambda e, b=b, c=c, pp=pp: e.matmul(pp[:], lhsT=kdec[b][:, c, :], rhs=vt[b][:, c, :], start=True, stop=True)),
                             reads=[kdec[b], vt[b]], writes=[pp])
                        fc = c * 64 + lastcol
                        k.op(k.dve, (lambda e, b=b, h=h, pp=pp, fc=fc: e.scalar_tensor_tensor(
                            out=S[h][:], in0=S[h][:], scalar=eb[b][:, fc:fc + 1], in1=pp[:], op0=ALU.mult, op1=ALU.add)),
                            reads=[S[h], eb[b], pp], writes=[S[h]])
                        k.op(k.act, (lambda e, h=h: e.copy(Sb[h][:], S[h][:])), reads=[S[h]], writes=[Sb[h]])
                    if d == 1:
                        os_ = ost[it % 2]
                        k.op(k.act, (lambda e, os_=os_, n=n: e.copy(os_[:, 0:n], pout[:, 0:n])), reads=[pout], writes=[os_])
                        k.dma(k.sp, self.obT[rows, t0:t0 + n], os_[:, 0:n], reads=[os_])
                    else:
                        os_ = ost[it % 2]
                        k.op(k.dve, (lambda e, os_=os_, b=b, n=n: e.tensor_tensor(out=os_[:, 0:n], in0=pout[:, 0:n], in1=obt[b][:, 0:n], op=ALU.add)),
                             reads=[pout, obt[b]], writes=[os_])
                        k.op(k.act, (lambda e, os_=os_, n=n: e.activation(sqo[:, 0:n], os_[:, 0:n], AF.Square)), reads=[os_], writes=[sqo])
                        k.op(k.pe, (lambda e, n=n: e.matmul(pbc[:, 0:n], lhsT=self.ones_bf[:], rhs=sqo[:, 0:n], start=True, stop=True)),
                             reads=[self.ones_bf, sqo], writes=[pbc])
                        k.op(k.act, (lambda e, n=n: e.activation(rs[:, 0:n], pbc[:, 0:n], AF.Ln, bias=self.eps_col[:, 0:1], scale=1.0 / 128)),
                             reads=[pbc, self.eps_col], writes=[rs])
                        k.op(k.act, (lambda e, n=n: e.activation(rs[:, 0:n], rs[:, 0:n], AF.Exp, scale=-0.5)), reads=[rs], writes=[rs])
                        k.op(k.dve, (lambda e, os_=os_, n=n: e.scalar_tensor_tensor(out=os_[:, 0:n], in0=os_[:, 0:n], scalar=onorm_col,
                                                                                   in1=rs[:, 0:n], op0=ALU.mult, op1=ALU.mult)),
                             reads=[os_, rs, self.csm], writes=[os_])
                        fo = fin[it % 2]
                        k.op(k.pool, (lambda e, os_=os_, fo=fo, b=b, n=n: e.tensor_tensor(out=fo[:, 0:n], in0=os_[:, 0:n], in1=gt[b][:, 0:n], op=ALU.mult)),
                             reads=[os_, gt[b]], writes=[fo])
                        k.dma(k.sp, self.oT[rows, t0:t0 + n], fo[:, 0:n], reads=[fo])
            k.end()

    def mixer_hgrn(self, li, nrow):
        self.hg_prep(li)
        self.phase_hg_proj(nrow)
        self.phase_hg_core(self.csm[:, 32:33])
        self.phase_out_proj(self.c_wo)


    def phase_gd_proj(self, j, nrow):
        k = self.k
        W = self.a_w[j]

        def setup(T0, tl):
            ctx = {}
            ctx["tm"] = [k.sb([128, 512], F32, "tmk") for _ in tl]
            for ti, (o, n) in enumerate(tl):
                k.dma(k.sp, ctx["tm"][ti][:, 0:n], self.tmask_rep[:, T0 + o:T0 + o + n], writes=[ctx["tm"][ti]])
            ctx["sg"] = [k.sb([128, 512], F32, "sg") for _ in range(2)]
            ctx["st"] = [k.sb([128, 512], BF16, "st") for _ in range(3)]
            ctx["z"] = [k.sb([128, 16], F32, "z") for _ in range(2)]
            ctx["bat"] = [k.sb([128, 32], F32, "bat") for _ in range(2)]
            na = k.sb([128, 16], F32, "negA")
            k.op(k.act, (lambda e: e.activation(na[:], self.asm[:, 0:16], AF.Exp)), reads=[self.asm], writes=[na])
            k.op(k.dve, (lambda e: e.tensor_scalar(na[:], na[:], -1.0, None, op0=ALU.mult)), reads=[na], writes=[na])
            ctx["negA"] = na
            ctx["i"] = 0
            return ctx

        def fm_pre(c):
            def post(ctx, pt, T0, ti, o, n):
                i = ctx["i"]
                ctx["i"] += 1
                st = ctx["st"][i % 3]
                k.op(k.dve, (lambda e: e.tensor_tensor(out=st[:, 0:n], in0=pt[:, 0:n], in1=ctx["tm"][ti][:, 0:n], op=ALU.mult)),
                     reads=[pt, ctx["tm"][ti]], writes=[st])
                k.dma(k.sp, self.pre[c * 128:(c + 1) * 128, T0 + o:T0 + o + n], st[:, 0:n], reads=[st])
            return post

        def fm_g(c):
            def post(ctx, pt, T0, ti, o, n):
                i = ctx["i"]
                ctx["i"] += 1
                st = ctx["st"][i % 3]
                k.op(k.act, (lambda e: e.activation(st[:, 0:n], pt[:, 0:n], AF.Silu)), reads=[pt], writes=[st])
                k.dma(k.sp, self.fmG[c * 128:(c + 1) * 128, T0 + o:T0 + o + n], st[:, 0:n], reads=[st])
            return post

        def tm_ba(ctx, pt, tok0, r):
            i = ctx["i"]
            ctx["i"] += 1
            z, bat = ctx["z"][i % 2], ctx["bat"][i % 2]
            blk = tok0 // 128
            assert tok0 % 128 == 0
            k.op(k.dve, (lambda e: e.tensor_tensor(out=z[0:r, :], in0=pt[0:r, 16:32], in1=self.asm[0:r, 16:32], op=ALU.add)),
                 reads=[pt, self.asm], writes=[z])
            k.op(k.act, (lambda e: e.activation(z[0:r, :], z[0:r, :], AF.Exp)), reads=[z], writes=[z])
            k.op(k.act, (lambda e: e.activation(z[0:r, :], z[0:r, :], AF.Ln, bias=self.one_col[0:r, 0:1])), reads=[z, self.one_col], writes=[z])
            k.op(k.dve, (lambda e: e.tensor_tensor(out=bat[0:r, 16:32], in0=z[0:r, :], in1=ctx["negA"][0:r, :], op=ALU.mult)),
                 reads=[z, ctx["negA"]], writes=[bat])
            k.op(k.act, (lambda e: e.activation(bat[0:r, 0:16], pt[0:r, 0:16], AF.Sigmoid)), reads=[pt], writes=[bat])
            k.op(k.dve, (lambda e: e.tensor_scalar(bat[0:r, 0:16], bat[0:r, 0:16], self.tmc[0:r, blk:blk + 1], None, op0=ALU.mult)),
                 reads=[bat, self.tmc], writes=[bat])
            k.dma(k.sp, self.ba_tm[tok0:tok0 + r, :], bat[0:r, :], reads=[bat])

        fm = []
        for c in range(24):
            fm.append((W[:, c * 128:(c + 1) * 128], fm_pre(c)))
        for c in range(8):
            fm.append((W[:, 3072 + c * 128:3072 + (c + 1) * 128], fm_g(c)))
        tm = [(W[:, 4096:4128], tm_ba, 32)]
        self.phase_proj(nrow, setup, fm, tm)

    def phase_gd_conv(self):
        k, T = self.k, self.T
        tiles = tiles_of(T, 512)
        QS = 128 ** -0.5
        k.begin()
        xin = [k.sb([128, 516], BF16, "cx") for _ in range(3)]
        acc = [k.sb([128, 512], F32, "cacc") for _ in range(2)]
        sv = [k.sb([128, 512], F32, "csv") for _ in range(2)]
        sq = [k.sb([128, 512], BF16, "csq") for _ in range(2)]
        rs = [k.sb([128, 512], F32, "crs") for _ in range(2)]
        ob = [k.sb([128, 512], BF16, "cob") for _ in range(3)]
        tmt = [k.sb([128, 512], F32, "ctm") for _ in range(2)]
        tb = [k.sb([128, 4, 128], BF16, "ctb") for _ in range(2)]
        pss = [k.ps([128, 512], F32, "cps") for _ in range(2)]
        ptb = [k.ps([128, 512], BF16, "cpt") for _ in range(2)]
        it = 0
        for ti, (t0, n) in enumerate(tiles):
            tmk = tmt[ti % 2]
            k.dma(k.sp, tmk[:, 0:n], self.tmask_rep[:, t0:t0 + n], writes=[tmk])
            for cc in range(24):
                kind = cc // 8
                x = xin[it % 3]
                a_, s_, q_, r_, o_ = acc[it % 2], sv[it % 2], sq[it % 2], rs[it % 2], ob[it % 3]
                ps = pss[it % 2]
                it += 1
                lo = max(t0 - 2, 0)
                hi = min(t0 + n + 2, T)
                if lo > t0 - 2 or hi < t0 + n + 2:
                    k.op(k.pool, (lambda e, x=x: e.memset(x[:], 0.0)), writes=[x])
                k.dma(k.sp, x[:, lo - (t0 - 2):hi - (t0 - 2)], self.pre[cc * 128:(cc + 1) * 128, lo:hi], writes=[x])
                wcol = lambda jj, cc=cc: self.asm[:, 33 + cc * 5 + jj:34 + cc * 5 + jj]
                k.op(k.dve, (lambda e, a_=a_, x=x, n=n, w=wcol(0): e.tensor_scalar(a_[:, 0:n], x[:, 0:n], w, None, op0=ALU.mult)),
                     reads=[x, self.asm], writes=[a_])
                for jj in range(1, 5):
                    k.op(k.dve, (lambda e, a_=a_, x=x, n=n, jj=jj, w=wcol(jj): e.scalar_tensor_tensor(
                        out=a_[:, 0:n], in0=x[:, jj:jj + n], scalar=w, in1=a_[:, 0:n], op0=ALU.mult, op1=ALU.add)),
                        reads=[x, a_, self.asm], writes=[a_])
                k.op(k.act, (lambda e, a_=a_, s_=s_, n=n: e.activation(s_[:, 0:n], a_[:, 0:n], AF.Silu)), reads=[a_], writes=[s_])
                if kind < 2:
                    k.op(k.act, (lambda e, s_=s_, q_=q_, n=n: e.activation(q_[:, 0:n], s_[:, 0:n], AF.Square)), reads=[s_], writes=[q_])
                    k.op(k.pe, (lambda e, ps=ps, q_=q_, n=n: e.matmul(ps[:, 0:n], lhsT=self.ones_bf[:], rhs=q_[:, 0:n], start=True, stop=True)),
                         reads=[self.ones_bf, q_], writes=[ps])
                    k.op(k.act, (lambda e, ps=ps, r_=r_, n=n: e.activation(r_[:, 0:n], ps[:, 0:n], AF.Ln, bias=self.eps_col[:, 0:1])),
                         reads=[ps, self.eps_col], writes=[r_])
                    k.op(k.act, (lambda e, r_=r_, n=n: e.activation(r_[:, 0:n], r_[:, 0:n], AF.Exp, scale=-0.5)), reads=[r_], writes=[r_])
                if kind == 0:
                    k.op(k.dve, (lambda e, o_=o_, s_=s_, r_=r_, n=n: e.scalar_tensor_tensor(
                        out=o_[:, 0:n], in0=s_[:, 0:n], scalar=QS, in1=r_[:, 0:n], op0=ALU.mult, op1=ALU.mult)),
                        reads=[s_, r_], writes=[o_])
                    k.dma(k.sp, self.fmA[cc * 128:(cc + 1) * 128, t0:t0 + n], o_[:, 0:n], reads=[o_])
                    continue
                if kind == 1:
                    k.op(k.dve, (lambda e, s_=s_, r_=r_, n=n: e.tensor_tensor(out=s_[:, 0:n], in0=s_[:, 0:n], in1=r_[:, 0:n], op=ALU.mult)),
                         reads=[s_, r_], writes=[s_])
                    k.op(k.pool, (lambda e, o_=o_, s_=s_, tmk=tmk, n=n: e.tensor_tensor(out=o_[:, 0:n], in0=s_[:, 0:n], in1=tmk[:, 0:n], op=ALU.mult)),
                         reads=[s_, tmk], writes=[o_])
                    k.dma(k.sp, self.fmB[0][(cc - 8) * 128:(cc - 7) * 128, t0:t0 + n], o_[:, 0:n], reads=[o_])
                    dst = self.tmK[0]
                else:
                    k.op(k.pool, (lambda e, o_=o_, s_=s_, n=n: e.tensor_copy(o_[:, 0:n], s_[:, 0:n])), reads=[s_], writes=[o_])
                    dst = self.v_tm
                cl = (cc % 8)
                pt = ptb[it % 2]
                tt = tb[it % 2]
                nb = (n + 127) // 128
                for b in range(nb):
                    r = min(128, n - b * 128)
                    k.op(k.pe, (lambda e, pt=pt, o_=o_, b=b, r=r: e.transpose(pt[0:r, b * 128:(b + 1) * 128], o_[:, b * 128:b * 128 + r],
                                                                             self.identb[:])),
                         reads=[o_, self.identb], writes=[pt])
                if n % 128 == 0:
                    k.op(k.act, (lambda e, pt=pt, tt=tt, nb=nb: e.copy(tt[:, 0:nb, :], pt[:, 0:nb * 128].rearrange("p (b d) -> p b d", d=128))),
                         reads=[pt], writes=[tt])
                    k.dma(k.sp, dst[t0:t0 + n, cl * 128:(cl + 1) * 128].rearrange("(b p) d -> p b d", p=128), tt[:, 0:nb, :], reads=[tt])
                else:
                    for b in range(nb):
                        r = min(128, n - b * 128)
                        k.op(k.act, (lambda e, pt=pt, tt=tt, b=b, r=r: e.copy(tt[0:r, b, :], pt[0:r, b * 128:(b + 1) * 128])),
                             reads=[pt], writes=[tt])
                        k.dma(k.sp, dst[t0 + b * 128:t0 + b * 128 + r, cl * 128:(cl + 1) * 128], tt[0:r, b, :], reads=[tt])
        k.end()

    def phase_gd_core(self):
        import os
        GDCUT = int(os.environ.get("GD_CUT", "99"))
        GDSUB = int(os.environ.get("GD_SUB", "99"))
        k, nc, T = self.k, self.nc, self.T
        tiles = tiles_of(T, 512)
        onorm_col = self.asm[:, 32:33]
        for sweep in (1, 0):
            d = sweep
            k.begin()
            tri = self.tri
            if d == 0:
                Uin, MS, MI, MN = tri[:, 0:64], 1, 0, 3
            else:
                Uin, MS, MI, MN = tri[:, 128:192], 3, 2, 1
            mI = k.sb([64, 512], F32, "mI")
            mS = k.sb([64, 512], F32, "mS")
            mN = k.sb([64, 512], F32, "mN")
            idr = k.sb([64, 512], F32, "idr")
            for c in range(8):
                cs = slice(c * 64, (c + 1) * 64)
                k.op(k.dve, (lambda e, cs=cs: e.tensor_copy(mI[:, cs], tri[:, MI * 64:(MI + 1) * 64])), reads=[tri], writes=[mI])
                k.op(k.dve, (lambda e, cs=cs: e.tensor_copy(mS[:, cs], tri[:, MS * 64:(MS + 1) * 64])), reads=[tri], writes=[mS])
                k.op(k.dve, (lambda e, cs=cs: e.tensor_copy(mN[:, cs], tri[:, MN * 64:(MN + 1) * 64])), reads=[tri], writes=[mN])
                k.op(k.dve, (lambda e, cs=cs: e.tensor_copy(idr[:, cs], self.ident[0:64, 0:64])), reads=[self.ident], writes=[idr])
            ones_f = k.sb([64, 128], F32, "onesf")
            k.op(k.dve, (lambda e: e.memset(ones_f[:], 1.0)), writes=[ones_f])
            S = [k.sb([128, 128], F32, "S") for _ in range(8)]
            Sb = [k.sb([128, 128], BF16, "Sb") for _ in range(8)]
            for h in range(8):
                k.op(k.dve, (lambda e, h=h: e.memset(S[h][:], 0.0)), writes=[S[h]])
                k.op(k.pool, (lambda e, h=h: e.memset(Sb[h][:], 0.0)), writes=[Sb[h]])
            NB = 2
            mk = lambda shp, dt, nm: [k.sb(shp, dt, nm) for _ in range(NB)]
            qT, kT = mk([128, 512], BF16, "qT"), mk([128, 512], BF16, "kT")
            ktm, vtm = mk([64, 8, 128], BF16, "ktm"), mk([64, 8, 128], BF16, "vtm")
            bat = mk([64, 8, 32], F32, "bat")
            lab = mk([64, 2, 8], F32, "lab")
            gc, egc, bek, dcol = mk([64, 8], F32, "gc"), mk([64, 8], F32, "egc"), mk([64, 8], F32, "bek"), mk([64, 8], F32, "dcol")
            alast = mk([128, 8], F32, "alast")
            dg = mk([64, 512], F32, "dg")
            egr = mk([128, 512], F32, "egr")
            qd = mk([128, 512], BF16, "qd")
            fab = mk([64, 512], F32, "fab")
            fm_ = mk([64, 512], F32, "fm")
            fmi = mk([64, 512], F32, "fmi")
            gf = mk([64, 512], F32, "gf")
            a32 = mk([64, 512], F32, "a32")
            Rb = [mk([64, 512], F32, "Rb%d" % i) for i in range(2)]
            Pb = [mk([64, 512], F32, "Pb%d" % i) for i in range(2)]
            PTb = [mk([64, 512], F32, "PTb%d" % i) for i in range(2)]
            rhu, rhw, kdec = mk([64, 8, 128], F32, "rhu"), mk([64, 8, 128], F32, "rhw"), mk([64, 8, 128], BF16, "kdec")
            gfu = mk([64, 512], F32, "gfu")
            dgb = mk([64, 512], F32, "dgb")
            u_sb = mk([64, 8, 128], F32, "u")
            nwT = mk([128, 512], BF16, "nwT")
            qkm = mk([64, 512], BF16, "qkm")
            vn = [k.sb([64, 128], BF16, "vn") for _ in range(2)]
            ost = [k.sb([128, 512], F32, "ost") for _ in range(2)]
            B = [k.ps([128, 512], F32, "B%d" % i) for i in range(7)]
            Bs, Brow, BG, BD, BP, BT, Bo = B
            Bv = Bs
            if d == 0:
                obt = mk([128, 512], F32, "obt")
                gt = mk([128, 512], BF16, "gt")
                sqo = k.sb([128, 512], BF16, "sqo")
                rs = k.sb([128, 512], F32, "rs")
                fin = [k.sb([128, 512], BF16, "fin") for _ in range(2)]
            it = 0
            order = list(range(len(tiles)))
            if d == 1:
                order = order[::-1]
            for ti in order:
                t0, n = tiles[ti]
                ncn = n // 64
                corder = list(range(ncn)) if d == 0 else list(range(ncn))[::-1]
                for h in (range(7, -1, -1) if os.environ.get('GD_HREV') else range(8)):
                    b = it % NB
                    it += 1
                    rows = slice(h * 128, (h + 1) * 128)
                    k.dma(k.sp, qT[b][:, 0:n], self.fmA[rows, t0:t0 + n], writes=[qT[b]])
                    k.dma(k.sp, kT[b][:, 0:n], self.fmB[0][rows, t0:t0 + n], writes=[kT[b]])
                    k.dma(k.sp, ktm[b][:, 0:ncn, :], self.tmK[0][t0:t0 + n, rows].rearrange("(c p) e -> p c e", p=64), writes=[ktm[b]])
                    k.dma(k.sp, vtm[b][:, 0:ncn, :], self.v_tm[t0:t0 + n, rows].rearrange("(c p) e -> p c e", p=64), writes=[vtm[b]])
                    k.dma(k.sp, bat[b][:, 0:ncn, :], self.ba_tm[t0:t0 + n, :].rearrange("(c p) e -> p c e", p=64), writes=[bat[b]])
                    if d == 0:
                        k.dma(k.sp, obt[b][:, 0:n], self.obT[rows, t0:t0 + n], writes=[obt[b]])
                        k.dma(k.sp, gt[b][:, 0:n], self.fmG[rows, t0:t0 + n], writes=[gt[b]])
                    col = d * 8 + h
                    k.op(k.dve, (lambda e, b=b, ncn=ncn, col=col: e.tensor_copy(lab[b][:, 0, 0:ncn], bat[b][:, 0:ncn, col])),
                         reads=[bat[b]], writes=[lab[b]])
                    k.op(k.dve, (lambda e, b=b, ncn=ncn, col=col: e.tensor_copy(lab[b][:, 1, 0:ncn], bat[b][:, 0:ncn, 16 + col])),
                         reads=[bat[b]], writes=[lab[b]])
                    beta = lambda c, b=b: lab[b][:, 0, c:c + 1]
                    k.op(k.pe, (lambda e, b=b, ncn=ncn: e.matmul(Bs[0:64, 0:ncn], lhsT=Uin, rhs=lab[b][:, 1, 0:ncn], start=True, stop=True)),
                         reads=[tri, lab[b]], writes=[Bs])
                    k.op(k.pe, (lambda e, b=b, ncn=ncn: e.matmul(Bs[:, 16:16 + ncn], lhsT=ones_f[:], rhs=lab[b][:, 1, 0:ncn], start=True, stop=True)),
                         reads=[ones_f, lab[b]], writes=[Bs])
                    k.op(k.dve, (lambda e, b=b, ncn=ncn: e.tensor_copy(gc[b][:, 0:ncn], Bs[0:64, 0:ncn])), reads=[Bs], writes=[gc[b]])
                    k.op(k.act, (lambda e, b=b, ncn=ncn: e.activation(alast[b][:, 0:ncn], Bs[:, 16:16 + ncn], AF.Exp)), reads=[Bs], writes=[alast[b]])
                    k.op(k.dve, (lambda e, b=b, ncn=ncn: e.tensor_tensor(out=dcol[b][:, 0:ncn], in0=Bs[0:64, 16:16 + ncn], in1=gc[b][:, 0:ncn],
                                                                        op=ALU.subtract)), reads=[Bs, gc[b]], writes=[dcol[b]])
                    k.op(k.act, (lambda e, b=b, ncn=ncn: e.activation(dcol[b][:, 0:ncn], dcol[b][:, 0:ncn], AF.Exp)), reads=[dcol[b]], writes=[dcol[b]])
                    k.op(k.act, (lambda e, b=b, ncn=ncn: e.activation(egc[b][:, 0:ncn], gc[b][:, 0:ncn], AF.Exp)), reads=[gc[b]], writes=[egc[b]])
                    k.op(k.dve, (lambda e, b=b, ncn=ncn: e.tensor_tensor(out=bek[b][:, 0:ncn], in0=egc[b][:, 0:ncn], in1=lab[b][:, 0, 0:ncn],
                                                                        op=ALU.mult)), reads=[egc[b], lab[b]], writes=[bek[b]])
                    if GDCUT < 1:
                        continue
                    for c in range(ncn):
                        cs = slice(c * 64, (c + 1) * 64)
                        k.op(k.dve, (lambda e, b=b, c=c, cs=cs: e.tensor_scalar(dg[b][:, cs], self.ident[0:64, 0:64], gc[b][:, c:c + 1], None,
                                                                               op0=ALU.mult)), reads=[self.ident, gc[b]], writes=[dg[b]])
                    for c in range(ncn):
                        cs = slice(c * 64, (c + 1) * 64)
                        k.op(k.pe, (lambda e, b=b, cs=cs: e.matmul(Brow[:, cs], lhsT=ones_f[:], rhs=dg[b][:, cs], start=True, stop=True)),
                             reads=[ones_f, dg[b]], writes=[Brow])
                    k.op(k.act, (lambda e, b=b, n=n: e.activation(egr[b][:, 0:n], Brow[:, 0:n], AF.Exp)), reads=[Brow], writes=[egr[b]])
                    k.op(k.dve, (lambda e, b=b, n=n: e.tensor_tensor(out=qd[b][:, 0:n], in0=qT[b][:, 0:n], in1=egr[b][:, 0:n], op=ALU.mult)),
                         reads=[qT[b], egr[b]], writes=[qd[b]])
                    for c in range(ncn):
                        cs = slice(c * 64, (c + 1) * 64)
                        k.op(k.dve, (lambda e, b=b, c=c, cs=cs: e.tensor_scalar(fab[b][:, cs], Brow[0:64, cs], gc[b][:, c:c + 1], None,
                                                                               op0=ALU.subtract)),
                             reads=[Brow, gc[b]], writes=[fab[b]])
                    k.op(k.act, (lambda e, b=b, n=n: e.activation(fab[b][:, 0:n], fab[b][:, 0:n], AF.Abs)), reads=[fab[b]], writes=[fab[b]])
                    k.op(k.act, (lambda e, b=b, n=n: e.activation(fm_[b][:, 0:n], fab[b][:, 0:n], AF.Exp, scale=-1.0)), reads=[fab[b]], writes=[fm_[b]])
                    k.op(k.pool, (lambda e, b=b, n=n: e.tensor_tensor(out=fmi[b][:, 0:n], in0=fm_[b][:, 0:n], in1=mI[:, 0:n], op=ALU.mult)),
                         reads=[fm_[b], mI], writes=[fmi[b]])
                    if GDCUT < 2:
                        continue
                    for c in range(ncn):
                        cs = slice(c * 64, (c + 1) * 64)
                        k.op(k.pe, (lambda e, b=b, cs=cs: e.matmul(BG[0:64, cs], lhsT=kT[b][:, cs], rhs=kT[b][:, cs], start=True, stop=True)),
                             reads=[kT[b]], writes=[BG])
                        k.op(k.dve, (lambda e, b=b, c=c, cs=cs: e.tensor_scalar(dgb[b][:, cs], self.ident[0:64, 0:64], lab[b][:, 0, c:c + 1], None,
                                                                               op0=ALU.mult)), reads=[self.ident, lab[b]], writes=[dgb[b]])
                    for c in range(ncn):
                        cs = slice(c * 64, (c + 1) * 64)
                        k.op(k.pe, (lambda e, b=b, cs=cs: e.matmul(BD[:, cs], lhsT=ones_f[:], rhs=dgb[b][:, cs], start=True, stop=True)),
                             reads=[ones_f, dgb[b]], writes=[BD])
                    k.op(k.dve, (lambda e, b=b, n=n: e.tensor_tensor(out=gf[b][:, 0:n], in0=BG[0:64, 0:n], in1=fm_[b][:, 0:n], op=ALU.mult)),
                         reads=[BG, fm_[b]], writes=[gf[b]])
                    k.op(k.pool, (lambda e, b=b, n=n: e.tensor_tensor(out=gfu[b][:, 0:n], in0=gf[b][:, 0:n], in1=mN[:, 0:n], op=ALU.mult)),
                         reads=[gf[b], mN], writes=[gfu[b]])
                    k.op(k.pool, (lambda e, b=b, n=n: e.tensor_tensor(out=gf[b][:, 0:n], in0=gf[b][:, 0:n], in1=mS[:, 0:n], op=ALU.mult)),
                         reads=[gf[b], mS], writes=[gf[b]])
                    R0, P0, PT0 = Rb[0][b], Pb[0][b], PTb[0][b]
                    for c in range(ncn):
                        cs = slice(c * 64, (c + 1) * 64)
                        k.op(k.dve, (lambda e, b=b, c=c, cs=cs, PT0=PT0: e.tensor_scalar(PT0[:, cs], gf[b][:, cs], lab[b][:, 0, c:c + 1], None, op0=ALU.mult)),
                             reads=[gf[b], lab[b]], writes=[PT0])
                    k.op(k.dve, (lambda e, b=b, n=n, P0=P0: e.tensor_tensor(out=P0[:, 0:n], in0=BD[0:64, 0:n], in1=gfu[b][:, 0:n], op=ALU.mult)),
                         reads=[BD, gfu[b]], writes=[P0])
                    k.op(k.pool, (lambda e, n=n, R0=R0, P0=P0: e.tensor_tensor(out=R0[:, 0:n], in0=idr[:, 0:n], in1=P0[:, 0:n], op=ALU.subtract)),
                         reads=[P0, idr], writes=[R0])
                    if GDCUT < 3:
                        continue
                    for lv in range(6):
                        cur, nxt = lv % 2, (lv + 1) % 2
                        Rc, Pc, PTc = Rb[cur][b], Pb[cur][b], PTb[cur][b]
                        Rn, Pn, PTn = Rb[nxt][b], Pb[nxt][b], PTb[nxt][b]
                        for c in range(ncn):
                            cs = slice(c * 64, (c + 1) * 64)
                            if lv >= 1:
                                k.op(k.pe, (lambda e, cs=cs, PTc=PTc, Rc=Rc: e.matmul(BD[0:64, cs], lhsT=PTc[:, cs], rhs=Rc[:, cs], start=True, stop=True)),
                                     reads=[PTc, Rc], writes=[BD])
                            if lv <= 4:
                                k.op(k.pe, (lambda e, cs=cs, PTc=PTc, Pc=Pc: e.matmul(BP[0:64, cs], lhsT=PTc[:, cs], rhs=Pc[:, cs], start=True, stop=True)),
                                     reads=[PTc, Pc], writes=[BP])
                                k.op(k.pe, (lambda e, cs=cs, PTc=PTc, Pc=Pc: e.matmul(BT[0:64, cs], lhsT=Pc[:, cs], rhs=PTc[:, cs], start=True, stop=True)),
                                     reads=[PTc, Pc], writes=[BT])
                        if lv >= 1:
                            k.op(k.dve, (lambda e, n=n, Rn=Rn, Rc=Rc: e.tensor_tensor(out=Rn[:, 0:n], in0=BD[0:64, 0:n], in1=Rc[:, 0:n], op=ALU.add)),
                                 reads=[BD, Rc], writes=[Rn])
                        else:
                            k.op(k.pool, (lambda e, n=n, Rn=Rn, Rc=Rc: e.tensor_copy(Rn[:, 0:n], Rc[:, 0:n])), reads=[Rc], writes=[Rn])
                        if lv <= 4:
                            k.op(k.act, (lambda e, n=n, Pn=Pn: e.copy(Pn[:, 0:n], BP[0:64, 0:n])), reads=[BP], writes=[Pn])
                            k.op(k.act, (lambda e, n=n, PTn=PTn: e.copy(PTn[:, 0:n], BT[0:64, 0:n])), reads=[BT], writes=[PTn])
                    if GDCUT < 4:
                        continue
                    Rf = Rb[0][b]
                    for c in range(ncn):
                        k.op(k.pool, (lambda e, b=b, c=c: e.tensor_scalar(rhu[b][:, c, :], vtm[b][:, c, :], lab[b][:, 0, c:c + 1], None, op0=ALU.mult)),
                             reads=[vtm[b], lab[b]], writes=[rhu[b]])
                        k.op(k.pool, (lambda e, b=b, c=c: e.tensor_scalar(rhw[b][:, c, :], ktm[b][:, c, :], bek[b][:, c:c + 1], None, op0=ALU.mult)),
                             reads=[ktm[b], bek[b]], writes=[rhw[b]])
                        k.op(k.pool, (lambda e, b=b, c=c: e.tensor_scalar(kdec[b][:, c, :], ktm[b][:, c, :], dcol[b][:, c:c + 1], None, op0=ALU.mult)),
                             reads=[ktm[b], dcol[b]], writes=[kdec[b]])
                    for c in range(ncn):
                        cs = slice(c * 64, (c + 1) * 64)
                        pu = BD if c < 4 else BP
                        us = slice((c % 4) * 128, (c % 4 + 1) * 128)
                        k.op(k.pe, (lambda e, b=b, c=c, cs=cs, pu=pu, us=us, Rf=Rf: e.matmul(pu[0:64, us], lhsT=Rf[:, cs], rhs=rhu[b][:, c, :], start=True, stop=True)),
                             reads=[Rf, rhu[b]], writes=[pu])
                        k.op(k.pe, (lambda e, b=b, c=c, cs=cs, Rf=Rf: e.matmul(BT[:, cs], lhsT=rhw[b][:, c, :], rhs=Rf[:, cs], start=True, stop=True)),
                             reads=[Rf, rhw[b]], writes=[BT])
                        k.op(k.pe, (lambda e, b=b, cs=cs: e.matmul(BG[0:64, cs], lhsT=kT[b][:, cs], rhs=qT[b][:, cs], start=True, stop=True)),
                             reads=[kT[b], qT[b]], writes=[BG])
                    n4 = min(ncn, 4)
                    k.op(k.act, (lambda e, b=b, n4=n4: e.copy(u_sb[b][:, 0:n4, :], BD[0:64, 0:n4 * 128].rearrange("p (c d) -> p c d", d=128))),
                         reads=[BD], writes=[u_sb[b]])
                    if ncn > 4:
                        k.op(k.act, (lambda e, b=b, ncn=ncn: e.copy(u_sb[b][:, 4:ncn, :], BP[0:64, 0:(ncn - 4) * 128].rearrange("p (c d) -> p c d", d=128))),
                             reads=[BP], writes=[u_sb[b]])
                    k.op(k.dve, (lambda e, b=b, n=n: e.tensor_scalar(nwT[b][:, 0:n], BT[:, 0:n], -1.0, None, op0=ALU.mult)), reads=[BT], writes=[nwT[b]])
                    k.op(k.dve, (lambda e, b=b, n=n: e.tensor_tensor(out=qkm[b][:, 0:n], in0=BG[0:64, 0:n], in1=fmi[b][:, 0:n], op=ALU.mult)),
                         reads=[BG, fmi[b]], writes=[qkm[b]])
                    if GDCUT < 5:
                        continue
                    for c in corder:
                        cs = slice(c * 64, (c + 1) * 64)
                        v_ = vn[c % 2]
                        k.op(k.pe, (lambda e, b=b, cs=cs, h=h: e.matmul(Bv[0:64, 256:384], lhsT=nwT[b][:, cs], rhs=Sb[h][:], start=True, stop=True)),
                             reads=[nwT[b], Sb[h]], writes=[Bv])
                        k.op(k.dve, (lambda e, b=b, c=c, v_=v_: e.tensor_tensor(out=v_[:], in0=Bv[0:64, 256:384], in1=u_sb[b][:, c, :], op=ALU.add)),
                             reads=[Bv, u_sb[b]], writes=[v_])
                        k.op(k.pe, (lambda e, b=b, cs=cs, h=h: e.matmul(Bo[:, cs], lhsT=Sb[h][:], rhs=qd[b][:, cs], start=True, stop=False)),
                             reads=[Sb[h], qd[b]], writes=[Bo])
                        k.op(k.pe, (lambda e, b=b, cs=cs, v_=v_: e.matmul(Bo[:, cs], lhsT=v_[:], rhs=qkm[b][:, cs], start=False, stop=True)),
                             reads=[v_, qkm[b]], writes=[Bo])
                        k.op(k.pe, (lambda e, b=b, c=c, v_=v_: e.matmul(Bs[:, 128:256], lhsT=kdec[b][:, c, :], rhs=v_[:], start=True, stop=True)),
                             reads=[kdec[b], v_], writes=[Bs])
                        k.op(k.dve, (lambda e, b=b, h=h, c=c: e.scalar_tensor_tensor(
                            out=S[h][:], in0=S[h][:], scalar=alast[b][:, c:c + 1], in1=Bs[:, 128:256], op0=ALU.mult, op1=ALU.add)),
                            reads=[S[h], alast[b], Bs], writes=[S[h]])
                        k.op(k.act, (lambda e, h=h: e.copy(Sb[h][:], S[h][:])), reads=[S[h]], writes=[Sb[h]])
                    if GDCUT < 6:
                        continue
                    os_ = ost[it % 2]
                    if d == 1:
                        k.op(k.act, (lambda e, os_=os_, n=n: e.copy(os_[:, 0:n], Bo[:, 0:n])), reads=[Bo], writes=[os_])
                        k.dma(k.sp, self.obT[rows, t0:t0 + n], os_[:, 0:n], reads=[os_])
                    else:
                        k.op(k.dve, (lambda e, os_=os_, b=b, n=n: e.tensor_tensor(out=os_[:, 0:n], in0=Bo[:, 0:n], in1=obt[b][:, 0:n], op=ALU.add)),
                             reads=[Bo, obt[b]], writes=[os_])
                        k.op(k.act, (lambda e, os_=os_, n=n: e.activation(sqo[:, 0:n], os_[:, 0:n], AF.Square)), reads=[os_], writes=[sqo])
                        k.op(k.pe, (lambda e, n=n: e.matmul(Brow[:, 0:n], lhsT=self.ones_bf[:], rhs=sqo[:, 0:n], start=True, stop=True)),
                             reads=[self.ones_bf, sqo], writes=[Brow])
                        k.op(k.act, (lambda e, n=n: e.activation(rs[:, 0:n], Brow[:, 0:n], AF.Ln, bias=self.eps_col[:, 0:1], scale=1.0 / 128)),
                             reads=[Brow, self.eps_col], writes=[rs])
                        k.op(k.act, (lambda e, n=n: e.activation(rs[:, 0:n], rs[:, 0:n], AF.Exp, scale=-0.5)), reads=[rs], writes=[rs])
                        k.op(k.dve, (lambda e, os_=os_, n=n: e.scalar_tensor_tensor(out=os_[:, 0:n], in0=os_[:, 0:n], scalar=onorm_col,
                                                                                   in1=rs[:, 0:n], op0=ALU.mult, op1=ALU.mult)),
                             reads=[os_, rs, self.asm], writes=[os_])
                        fo = fin[it % 2]
                        k.op(k.pool, (lambda e, os_=os_, fo=fo, b=b, n=n: e.tensor_tensor(out=fo[:, 0:n], in0=os_[:, 0:n], in1=gt[b][:, 0:n], op=ALU.mult)),
                             reads=[os_, gt[b]], writes=[fo])
                        k.dma(k.sp, self.oT[rows, t0:t0 + n], fo[:, 0:n], reads=[fo])
            if d == 1 and os.environ.get("DEBUG_DUMP"):
                o = self.nc.dram_tensor("dbg_S0", [128, 128], F32, kind="ExternalOutput").ap()
                k.dma(k.sp, o, S[0][:], reads=[S[0]])
                o = self.nc.dram_tensor("dbg_u", [64, 8 * 128], F32, kind="ExternalOutput").ap()
                k.dma(k.sp, o, u_sb[0][:].rearrange("p c d -> p (c d)"), reads=[u_sb[0]])
                o = self.nc.dram_tensor("dbg_rhu", [64, 8 * 128], BF16, kind="ExternalOutput").ap()
                k.dma(k.sp, o, rhu[0][:].rearrange("p c d -> p (c d)"), reads=[rhu[0]])
                o = self.nc.dram_tensor("dbg_R", [64, 512], BF16, kind="ExternalOutput").ap()
                k.dma(k.sp, o, Rb[0][0][:], reads=[Rb[0][0]])
                o = self.nc.dram_tensor("dbg_alast", [128, 8], F32, kind="ExternalOutput").ap()
                k.dma(k.sp, o, alast[0][:], reads=[alast[0]])
            k.end()

    def mixer_gdn(self, j, nrow):
        k = self.k
        k.begin()
        k.dma(k.sp, self.asm[:], self.a_small[j], writes=[self.asm])
        k.end()
        import os
        st = int(os.environ.get("GD_STAGE", "9"))
        self.phase_gd_proj(j, nrow)
        if st >= 2:
            self.phase_gd_conv()
        if st >= 3:
            self.phase_gd_core()
        if st >= 4:
            self.phase_out_proj(self.a_wo[j])

    def build(self):
        k = self.k
        self.eps_col = k.gsb([128, 1], F32, "epscol")
        k.begin()
        k.op(k.dve, lambda e: e.memset(self.eps_col[:], EPS), writes=[self.eps_col])
        self.bsm = k.gsb([128, 260], F32, "bsm")
        k.dma(k.sp, self.bsm[:], self.b_small, writes=[self.bsm])
        self.csm = k.gsb([128, 33], F32, "csm")
        k.dma(k.sp, self.csm[:], self.c_small, writes=[self.csm])
        self.tri = k.gsb([64, 256], F32, "tric")
        k.dma(k.sp, self.tri[:], self.tri_in, writes=[self.tri])
        self.tmc = k.gsb([128, (self.T + 127) // 128], F32, "tmc")
        k.dma(k.sp, self.tmc[:], self.tmask_col, writes=[self.tmc])
        self.one_col = k.gsb([128, 1], F32, "onecol")
        k.op(k.dve, lambda e: e.memset(self.one_col[:], 1.0), writes=[self.one_col])
        self.lb_col = k.gsb([128, 16], F32, "lbcol")
        self.asm = k.gsb([128, 153], F32, "asm")
        self.identb = k.gsb([128, 128], BF16, "identbc")
        k.dma(k.sp, self.identb[:], self.identb_in, writes=[self.identb])
        self.oml_row = k.gsb([128, D_MODEL], F32, "omlrow")
        k.end()
        self.phase_in()
        if self.only == "gdn":
            self.mixer_gdn(0, 0 * 3 + 1)
            self.phase_out(self.depth * 3)
            import os
            if os.environ.get("DEBUG_DUMP"):
                k.begin()
                dummy = k.sb([128, 4], F32, "dummy")
                for nm, ap, shp, dt in (("fmA", self.fmA, [D_MODEL, self.T], BF16), ("fmB0", self.fmB[0], [D_MODEL, self.T], BF16),
                                        ("tmK0", self.tmK[0], [self.T, D_MODEL], BF16), ("v_tm", self.v_tm, [self.T, D_MODEL], BF16),
                                        ("ba_tm", self.ba_tm, [self.T, 32], F32), ("obT", self.obT, [D_MODEL, self.T], F32),
                                        ("oT", self.oT, [D_MODEL, self.T], BF16), ("fmG", self.fmG, [D_MODEL, self.T], BF16)):
                    o = self.nc.dram_tensor("dbg_" + nm, shp, dt, kind="ExternalOutput").ap()
                    k.dma(k.sp, o, ap, reads=[dummy])
                k.end()
            return
        if self.only == "hgrn":
            self.mixer_hgrn(2, 2 * 3 + 1)
            self.phase_out(self.depth * 3)
            return
        if self.only == "attn":
            self.mixer_attn(1 * 3 + 1)
            self.phase_out(self.depth * 3)
            return
        for li in range(self.depth):
            if self.ffn:
                self.phase_ffn(li, 0, li * 3 + 0)
            if self.mixers and li % 3 == 1:
                self.mixer_attn(li * 3 + 1)
            if self.mixers and li % 3 == 2:
                self.mixer_hgrn(li, li * 3 + 1)
            if self.mixers and li % 3 == 0:
                self.mixer_gdn(li // 3, li * 3 + 1)
            if self.ffn:
                self.phase_ffn(li, 1, li * 3 + 2)
        self.phase_out(self.depth * 3)


def col_layout(a):
    a = np.asarray(a, np.float32)
    R = a.shape[0]
    C = a.shape[1] // 128
    return np.ascontiguousarray(a.reshape(R, C, 128).transpose(2, 0, 1).reshape(128, R * C))


def attn_host_inputs(b_w_in, b_lambda, b_sub_norm, layer_idx, T, n_valid):
    f32 = np.float32
    w = np.asarray(b_w_in, f32)
    perm = np.arange(2048)
    d = perm % 64
    perm = np.where(d < 8, perm + 8, np.where(d < 16, perm - 8, perm))
    w_sw = np.ascontiguousarray(w[:, perm])
    sm = np.zeros((128, 260), f32)
    sm[:, 0:256] = np.asarray(b_lambda, f32).reshape(1, 256)
    sm[:, 256] = np.asarray(b_sub_norm, f32).reshape(128)
    dd = np.arange(128) % 64
    ii = np.where(dd < 8, dd, dd - 8).astype(np.float64)
    invf = np.exp(-np.log(500000.0) * ii / 8.0)
    sm[:, 257] = np.where(dd < 16, invf, 0.0)
    sm[:, 258] = np.where(dd < 8, -1.0, np.where(dd < 16, 1.0, 0.0))
    sm[:, 259] = 0.8 - 0.6 * np.exp(-0.3 * layer_idx)
    nkt = (T + 127) // 128
    tok = np.arange(nkt * 128)
    kb = np.where(tok < n_valid, 0.0, -30000.0).astype(f32).reshape(nkt, 128).T
    return w_sw, sm, np.ascontiguousarray(kb)


def common_host_inputs(T, n_valid):
    f32 = np.float32
    j = np.arange(64)[:, None]
    i = np.arange(64)[None, :]
    tri = np.concatenate([(j <= i), (j > i), (j >= i), (j < i)], axis=1).astype(f32)
    tok = np.arange(T)
    tm = (tok < n_valid).astype(f32)
    tmask_rep = np.ascontiguousarray(np.broadcast_to(tm[None, :], (128, T)))
    nb = (T + 127) // 128
    tmp = np.zeros(nb * 128, f32)
    tmp[:T] = tm
    tmask_col = np.ascontiguousarray(tmp.reshape(nb, 128).T)
    return {"tri": tri, "tmask_rep": tmask_rep, "tmask_col": tmask_col, "ident": np.eye(128, dtype=f32)}


def hgrn_host_inputs(c_lb_logits, c_o_norm):
    sm = np.zeros((128, 33), np.float32)
    sm[:, 0:32] = col_layout(np.asarray(c_lb_logits, np.float32))
    sm[:, 32] = np.asarray(c_o_norm, np.float32).reshape(128)
    return sm


def gdn_host_inputs(a_log, a_dt_bias, a_o_norm, a_conv_w):
    f32 = np.float32
    n_a = np.asarray(a_log).shape[0]
    out = np.zeros((n_a, 128, 153), f32)
    for j in range(n_a):
        out[j, :, 0:16] = np.asarray(a_log[j], f32).reshape(1, 16)
        out[j, :, 16:32] = np.asarray(a_dt_bias[j], f32).reshape(1, 16)
        out[j, :, 32] = np.asarray(a_o_norm[j], f32).reshape(128)
        cw = np.asarray(a_conv_w[j], f32)
        out[j, :, 33:153] = cw.reshape(5, 24, 128).transpose(2, 1, 0).reshape(128, 120)
    return out


_PROG_CACHE = {}


def get_prog(T, depth, n_groups):
    key = (T, depth, n_groups)
    if key not in _PROG_CACHE:
        _PROG_CACHE[key] = Prog(T, depth, n_groups)
    return _PROG_CACHE[key]


T_FULL = 8256
SEQ_P = 8192
SEQ_S = 4096


def kernel(x_prompt, x_sample, meta_tokens, norm_w, ffn_w_up, ffn_w_down,
           a_w_in, a_conv_w, a_log, a_dt_bias, a_o_norm, a_w_out,
           b_w_in, b_lambda, b_sub_norm, b_w_out,
           c_w_in, c_lb_logits, c_o_norm, c_w_out, final_norm):
    import ml_dtypes
    depth = norm_w.shape[0]
    T = T_FULL
    prog = get_prog(T, depth, 4)
    f32 = np.float32
    seqs = [x_prompt[0], x_prompt[1], x_sample[0], x_sample[1], x_sample[2], x_sample[3], x_sample[0], x_sample[1]]
    nw = np.concatenate([np.asarray(norm_w, f32).reshape(depth * 3, D_MODEL), np.asarray(final_norm, f32)[None]], 0)
    nw = col_layout(nw)
    shared = {
        "norm_w": nw,
        "ffn_w_up": np.asarray(ffn_w_up, f32), "ffn_w_down": np.asarray(ffn_w_down, f32),
        "b_w_in": np.asarray(b_w_in[0], f32), "b_w_out": np.asarray(b_w_out[0], f32),
        "c_w_in": np.asarray(c_w_in[0], f32), "c_w_out": np.asarray(c_w_out[0], f32),
        "c_small": hgrn_host_inputs(c_lb_logits, c_o_norm[0]),
        "a_w_in": np.asarray(a_w_in, f32), "a_w_out": np.asarray(a_w_out, f32),
        "a_small": gdn_host_inputs(a_log, a_dt_bias, a_o_norm, a_conv_w),
        "identb": np.eye(128, dtype=f32).astype(ml_dtypes.bfloat16),
    }
    in_maps = []
    per_len = {}
    for s in seqs:
        L = s.shape[0]
        nv = N_META + L
        if L not in per_len:
            w_sw, sm, kb = attn_host_inputs(b_w_in[0], b_lambda[0], b_sub_norm[0], 1, T, nv)
            d = {"b_w_sw": w_sw, "b_small": sm, "kbias": kb}
            d.update(common_host_inputs(T, nv))
            per_len[L] = d
        xin = np.zeros((T, D_MODEL), f32)
        xin[:N_META] = meta_tokens
        xin[N_META:nv] = s
        m = {"xin": xin}
        m.update(shared)
        m.update(per_len[L])
        in_maps.append(m)
    res = run_bass_kernel_spmd(prog.nc, in_maps, core_ids=list(range(8)))
    outs = [r["yout"] for r in res.results]
    y_prompt = np.stack([outs[0][N_META:N_META + SEQ_P], outs[1][N_META:N_META + SEQ_P]], 0).astype(f32)
    y_sample = np.stack([outs[i][N_META:N_META + SEQ_S] for i in range(2, 6)], 0).astype(f32)
    return (y_prompt, y_sample)
```

```python
import numpy as np
from contextlib import ExitStack
import concourse.bass as bass
import concourse.mybir as mybir
from concourse.bass_utils import run_bass_kernel_spmd

F32 = mybir.dt.float32
BF16 = mybir.dt.bfloat16
ALU = mybir.AluOpType
AF = mybir.ActivationFunctionType
AX = mybir.AxisListType

D_MODEL = 1024
D_FF = 2816
N_META = 16
EPS = 1e-6


class Eng:
    def __init__(self, name, sem):
        self.name = name
        self.sem = sem
        self.cnt = 0
        self.seen = {}
        self.ops = []


class Tl:
    def __init__(self, t, name):
        self.t = t
        self.name = name
        self.w = None
        self.r = {}
        self.dsem = None

    def __getitem__(self, idx):
        return self.t[idx]

    def view(self):
        return Tl(self.t, self.name)


class KB:
    SAME_ENGINE_SYNC = True

    def __init__(self, nc, n_dma_sems=84):
        self.nc = nc
        self.es = ExitStack()
        self.engs = {}
        for name in ("pe", "act", "dve", "pool", "sp"):
            sem = self.es.enter_context(nc.semaphore("s_" + name))
            self.engs[name] = Eng(name, sem)
        self.pe, self.act, self.dve, self.pool, self.sp = (self.engs[n] for n in ("pe", "act", "dve", "pool", "sp"))
        self.dma_pool = []
        for i in range(n_dma_sems):
            sem = self.es.enter_context(nc.semaphore("s_d%d" % i))
            self.dma_pool.append([sem, 0])
        self.dma_free = list(range(n_dma_sems))
        self.phase_tiles = []
        self.dma_tiles = []
        self.lanes = {}
        self.cur_lane = None
        self.pes = None
        self.uid = 0
        self.nops = 0

    def begin(self):
        self.pes = ExitStack()
        self.phase_tiles = []

    def sb(self, shape, dt, name=None):
        self.uid += 1
        name = "%s_%d" % (name or "t", self.uid)
        t = self.pes.enter_context(self.nc.sbuf_tensor(name, list(shape), dt))
        tl = Tl(t, name)
        self.phase_tiles.append(tl)
        return tl

    def views(self, tl, n):
        vs = [tl.view() for _ in range(n)]
        self.phase_tiles.extend(vs)
        return vs

    def ps(self, shape, dt=F32, name=None):
        self.uid += 1
        name = "%s_%d" % (name or "p", self.uid)
        t = self.pes.enter_context(self.nc.psum_tensor(name, list(shape), dt))
        tl = Tl(t, name)
        self.phase_tiles.append(tl)
        return tl

    def gsb(self, shape, dt, name):
        t = self.es.enter_context(self.nc.sbuf_tensor(name, list(shape), dt))
        return Tl(t, name)

    def _deps(self, eng, reads, writes):
        deps = []
        for tl in reads:
            if tl.w is not None:
                deps.append(tl.w)
        for tl in writes:
            if tl.w is not None and tl.w[0] != eng.name:
                deps.append(tl.w)
            deps.extend(v for v in tl.r.values() if v[0] != eng.name)
        waits = []
        for key, sem, cnt in deps:
            if key == eng.name and (eng.name in ("pe", "sp") or not self.SAME_ENGINE_SYNC):
                continue
            if eng.seen.get(key, 0) < cnt:
                eng.seen[key] = cnt
                waits.append((sem, cnt))
        return waits

    def set_lane(self, lane):
        self.cur_lane = lane

    def merge(self):
        lanes = [v for _, v in sorted(self.lanes.items()) if v]
        self.lanes = {}
        save, self.cur_lane = self.cur_lane, None
        idx = [0] * len(lanes)
        left = sum(len(l) for l in lanes)
        while left:
            for li, l in enumerate(lanes):
                if idx[li] < len(l):
                    rec = l[idx[li]]
                    idx[li] += 1
                    left -= 1
                    if rec[0] == "op":
                        self.op(*rec[1:])
                    else:
                        self.dma(rec[1], rec[2], rec[3], rec[4], rec[5], **rec[6])
        self.cur_lane = save

    def op(self, eng, fn, reads=(), writes=()):
        if self.cur_lane is not None:
            self.lanes.setdefault(self.cur_lane, []).append(("op", eng, fn, tuple(reads), tuple(writes)))
            return
        waits = self._deps(eng, reads, writes)
        eng.cnt += 1
        me = (eng.name, eng.sem, eng.cnt)
        eng.ops.append((waits, fn, (eng.sem, 1)))
        for tl in writes:
            tl.w = me
            tl.r = {}
        for tl in reads:
            if tl not in writes:
                tl.r[eng.name] = me
        self.nops += 1

    def dma(self, q, out, in_, reads=(), writes=(), **kw):
        if self.cur_lane is not None:
            self.lanes.setdefault(self.cur_lane, []).append(("dma", q, out, in_, tuple(reads), tuple(writes), kw))
            return
        tl = (list(writes) + list(reads))[0]
        if tl.dsem is None:
            tl.dsem = self.dma_free.pop()
            self.dma_tiles.append(tl)
        slot = self.dma_pool[tl.dsem]
        waits = self._deps(q, reads, writes)
        slot[1] += 16
        key = "d%d" % tl.dsem
        me = (key, slot[0], slot[1])
        q.ops.append((waits, (lambda e, o=out, i=in_, k=kw: e.dma_start(out=o, in_=i, **k)), (slot[0], 16)))
        for t in writes:
            t.w = me
            t.r = {}
        for t in reads:
            t.r[key] = me
        self.nops += 1

    def end(self):
        used = set()
        for tl in self.dma_tiles:
            if tl.dsem is not None:
                used.add(tl.dsem)
        for d in sorted(used):
            sem, cnt = self.dma_pool[d]
            key = "d%d" % d
            if self.sp.seen.get(key, 0) < cnt:
                self.sp.seen[key] = cnt
                self.sp.ops.append(([(sem, cnt)], None, None))
        for e in (self.pe, self.act, self.dve, self.pool):
            if self.sp.seen.get(e.name, 0) < e.cnt:
                self.sp.seen[e.name] = e.cnt
                self.sp.ops.append(([(e.sem, e.cnt)], None, None))
        self.sp.cnt += 1
        self.sp.ops.append(([], (lambda e: e.nop()), (self.sp.sem, 1)))
        for e in (self.pe, self.act, self.dve, self.pool):
            e.ops.append(([(self.sp.sem, self.sp.cnt)], None, None))
        with self.nc.Block() as block:
            for name, deco in (("pe", block.tensor), ("act", block.scalar), ("dve", block.vector),
                               ("pool", block.gpsimd), ("sp", block.sync)):
                eng = self.engs[name]
                ops = eng.ops
                eng.ops = []

                def body(e, ops=ops):
                    for waits, fn, inc in ops:
                        for sem, cnt in waits:
                            e.wait_ge(sem, cnt)
                        if fn is not None:
                            ins = fn(e)
                            if inc is not None:
                                ins.then_inc(inc[0], inc[1])

                deco(body)
        for d in used:
            self.dma_free.append(d)
        for tl in self.dma_tiles:
            tl.dsem = None
        self.dma_tiles = []
        for e in self.engs.values():
            for o in self.engs.values():
                e.seen[o.name] = o.cnt
            for d in range(len(self.dma_pool)):
                e.seen["d%d" % d] = self.dma_pool[d][1]
        self.pes.close()
        self.pes = None

    def close(self):
        self.es.close()


def tiles_of(n, step=512):
    out = []
    s = 0
    while s < n:
        out.append((s, min(step, n - s)))
        s += step
    return out


class Prog:
    def __init__(self, T, depth, n_groups, mixers=True, ffn=True, only=None):
        self.only = only
        self.mixers = mixers
        self.ffn = ffn
        self.T = T
        self.depth = depth
        self.n_groups = n_groups
        base = (T // n_groups) // 512 * 512 if n_groups > 1 else T
        self.groups = [(g * base, base) for g in range(n_groups - 1)]
        self.groups.append(((n_groups - 1) * base, T - (n_groups - 1) * base))
        nc = bass.Bass("TRN2", target_bir_lowering=False)
        self.nc = nc
        d = nc.dram_tensor
        self.xin = d("xin", [T, D_MODEL], F32, kind="ExternalInput").ap()
        self.yout = d("yout", [T, D_MODEL], F32, kind="ExternalOutput").ap()
        self.norm_w = d("norm_w", [128, (depth * 3 + 1) * 8], F32, kind="ExternalInput").ap()
        self.w_up = d("ffn_w_up", [depth, 2, D_MODEL, 2 * D_FF], F32, kind="ExternalInput").ap()
        self.w_dn = d("ffn_w_down", [depth, 2, D_FF, D_MODEL], F32, kind="ExternalInput").ap()
        self.ident_in = d("ident", [128, 128], F32, kind="ExternalInput").ap()
        self.hT = d("hT", [D_MODEL, T], F32).ap()
        self.b_w = d("b_w_in", [D_MODEL, 3072], F32, kind="ExternalInput").ap()
        self.b_wsw = d("b_w_sw", [D_MODEL, 2048], F32, kind="ExternalInput").ap()
        self.b_wo = d("b_w_out", [D_MODEL, D_MODEL], F32, kind="ExternalInput").ap()
        self.b_small = d("b_small", [128, 256 + 4], F32, kind="ExternalInput").ap()
        self.kbias_in = d("kbias", [128, (T + 127) // 128], F32, kind="ExternalInput").ap()
        self.qkT = d("qkT", [2048, T], BF16).ap()
        self.v_tm = d("v_tm", [T, 1024], BF16).ap()
        self.oT = d("oT", [D_MODEL, T], BF16).ap()
        self.tri_in = d("tri", [64, 4 * 64], F32, kind="ExternalInput").ap()
        self.tmask_rep = d("tmask_rep", [128, T], F32, kind="ExternalInput").ap()
        self.tmask_col = d("tmask_col", [128, (T + 127) // 128], F32, kind="ExternalInput").ap()
        self.c_w = d("c_w_in", [D_MODEL, 5120], F32, kind="ExternalInput").ap()
        self.c_wo = d("c_w_out", [D_MODEL, D_MODEL], F32, kind="ExternalInput").ap()
        self.c_small = d("c_small", [128, 32 + 1], F32, kind="ExternalInput").ap()
        self.fmA = d("fmA", [D_MODEL, T], BF16).ap()
        self.fmB = [d("fmB%d" % i, [D_MODEL, T], BF16).ap() for i in range(2)]
        self.fmG = d("fmG", [D_MODEL, T], BF16).ap()
        self.tmK = [d("tmK%d" % i, [T, D_MODEL], BF16).ap() for i in range(2)]
        self.tmL = [d("tmL%d" % i, [T, D_MODEL], F32).ap() for i in range(2)]
        self.obT = d("obT", [D_MODEL, T], F32).ap()
        self.lbrow_t = d("lbrow", [2, D_MODEL], F32)
        self.n_a = (depth + 2) // 3
        self.a_w = d("a_w_in", [self.n_a, D_MODEL, 4128], F32, kind="ExternalInput").ap()
        self.a_wo = d("a_w_out", [self.n_a, D_MODEL, D_MODEL], F32, kind="ExternalInput").ap()
        self.a_small = d("a_small", [self.n_a, 128, 32 + 1 + 120], F32, kind="ExternalInput").ap()
        self.pre = d("pre", [3 * D_MODEL, T], BF16).ap()
        self.ba_tm = d("ba_tm", [T, 32], F32).ap()
        self.identb_in = d("identb", [128, 128], BF16, kind="ExternalInput").ap()
        self.k = KB(nc)
        k = self.k
        self.ident = k.gsb([128, 128], F32, "identc")
        self.ones_bf = k.gsb([128, 128], BF16, "onesbf")
        self.nw = k.gsb([128, (depth * 3 + 1) * 8], F32, "nwcol")
        self.build()
        k.close()

    def phase_in(self):
        k, nc, T = self.k, self.nc, self.T
        k.begin()
        k.dma(k.sp, self.ident[:], self.ident_in, writes=[self.ident])
        k.op(k.dve, lambda e: e.memset(self.ones_bf[:], 1.0), writes=[self.ones_bf])
        nrows = self.depth * 3 + 1
        k.dma(k.sp, self.nw[:], self.norm_w, writes=[self.nw])
        xt = [k.sb([128, 4, D_MODEL], F32, "xt") for _ in range(2)]
        st = [k.sb([128, 8, 512], F32, "st") for _ in range(2)]
        pst = [k.ps([128, 512], F32, "pst") for _ in range(4)]
        pi = 0
        for gi, (t0, n) in enumerate(tiles_of(T, 512)):
            x = xt[gi % 2]
            s = st[gi % 2]
            nb = (n + 127) // 128
            blocks = [(b * 128, min(128, n - b * 128)) for b in range(nb)]
            if n % 128 == 0:
                k.dma(k.sp, x[:, 0:nb, :], self.xin[t0:t0 + n, :].rearrange("(j p) f -> p j f", p=128), writes=[x])
            else:
                for b, (o, r) in enumerate(blocks):
                    k.dma(k.sp, x[0:r, b, :], self.xin[t0 + o:t0 + o + r, :], writes=[x])
            for c in range(8):
                p = pst[pi % 4]
                pi += 1
                for b, (o, r) in enumerate(blocks):
                    k.op(k.pe, (lambda e, p=p, x=x, b=b, o=o, r=r, c=c: e.transpose(
                        p[:, o:o + r], x[0:r, b, c * 128:(c + 1) * 128], self.ident[0:r, 0:r])),
                        reads=[x, self.ident], writes=[p])
                eng = k.dve if c % 2 == 0 else k.act
                if eng is k.dve:
                    k.op(eng, (lambda e, p=p, s=s, c=c, n=n: e.tensor_copy(s[:, c, 0:n], p[:, 0:n])), reads=[p], writes=[s])
                else:
                    k.op(eng, (lambda e, p=p, s=s, c=c, n=n: e.copy(s[:, c, 0:n], p[:, 0:n])), reads=[p], writes=[s])
            k.dma(k.sp, self.hT.rearrange("(c p) t -> p c t", p=128)[:, :, t0:t0 + n], s[:, :, 0:n], reads=[s])
        k.end()


    def emit_norm(self, y, yv, hn, hnv, tl, nrow, pd, sq, rstd):
        k = self.k
        for ti, (o, n) in enumerate(tl):
            ss = pd[ti % 4]
            for c in range(8):
                s_ = sq[c % 2]
                k.op(k.act, (lambda e, s_=s_, c=c, o=o, n=n: e.activation(s_[:, 0:n], y[:, c, o:o + n], AF.Square)),
                     reads=[yv[c][ti]], writes=[s_])
                k.op(k.pe, (lambda e, ss=ss, s_=s_, c=c, n=n: e.matmul(ss[:, 0:n], lhsT=self.ones_bf[:], rhs=s_[:, 0:n],
                                                                        start=(c == 0), stop=(c == 7))),
                     reads=[s_, self.ones_bf], writes=[ss])
            r = rstd[ti % 2]
            k.op(k.act, (lambda e, r=r, ss=ss, n=n: e.activation(r[:, 0:n], ss[:, 0:n], AF.Ln, bias=self.eps_col[:, 0:1],
                                                                  scale=1.0 / D_MODEL)),
                 reads=[ss, self.eps_col], writes=[r])
            k.op(k.act, (lambda e, r=r, n=n: e.activation(r[:, 0:n], r[:, 0:n], AF.Exp, scale=-0.5)), reads=[r], writes=[r])
            for c in range(8):
                col = nrow * 8 + c
                k.op(k.dve, (lambda e, r=r, c=c, o=o, n=n, col=col: e.scalar_tensor_tensor(
                    out=hn[:, c, o:o + n], in0=y[:, c, o:o + n], scalar=self.nw[:, col:col + 1], in1=r[:, 0:n],
                    op0=ALU.mult, op1=ALU.mult)),
                    reads=[yv[c][ti], r, self.nw], writes=[hnv[c][ti]])

    def phase_ffn(self, li, fj, nrow):
        k, nc = self.k, self.nc
        HG = 256
        NG = D_FF // HG
        hTv = self.hT.rearrange("(c p) t -> p c t", p=128)
        for (T0, GS) in self.groups:
            tl = tiles_of(GS, 512)
            NT = len(tl)
            k.begin()
            y = k.sb([128, 8, GS], F32, "y")
            hn = k.sb([128, 8, GS], BF16, "hn")
            yv = [k.views(y, NT) for _ in range(8)]
            hnv = [k.views(hn, NT) for _ in range(8)]
            sq = [k.sb([128, 512], BF16, "sq") for _ in range(2)]
            rstd = [k.sb([128, 512], F32, "rstd") for _ in range(2)]
            wg = [k.sb([128, 8, HG], BF16, "wg") for _ in range(2)]
            wu = [k.sb([128, 8, HG], BF16, "wu") for _ in range(2)]
            wd = [k.sb([128, HG // 128, D_MODEL], BF16, "wd") for _ in range(2)]
            sg = [k.sb([128, 2, 512], F32, "sg") for _ in range(2)]
            act = [k.sb([128, 2, 512], BF16, "act") for _ in range(2)]
            pg = [k.ps([128, 512], F32, "pg") for _ in range(2)]
            pu = [k.ps([128, 512], F32, "pu") for _ in range(2)]
            pd = [k.ps([128, 512], F32, "pd") for _ in range(4)]
            allv = [v for c in range(8) for v in yv[c]]
            k.dma(k.sp, y[:], hTv[:, :, T0:T0 + GS], writes=allv)
            self.emit_norm(y, yv, hn, hnv, tl, nrow, pd, sq, rstd)
            wup = self.w_up[li, fj]
            wdn = self.w_dn[li, fj]
            it = 0
            for g in range(NG):
                b = g % 2
                k.dma(k.pool, wg[b][:], wup[:, g * HG:(g + 1) * HG].rearrange("(kk p) c -> p kk c", p=128), writes=[wg[b]])
                k.dma(k.pool, wu[b][:], wup[:, D_FF + g * HG:D_FF + (g + 1) * HG].rearrange("(kk p) c -> p kk c", p=128),
                      writes=[wu[b]])
                k.dma(k.pool, wd[b][:], wdn[g * HG:(g + 1) * HG, :].rearrange("(kk p) c -> p kk c", p=128), writes=[wd[b]])
                for ti, (o, n) in enumerate(tl):
                    ab = it % 2
                    it += 1
                    for j in range(2):
                        for (pt, wt) in ((pg[j], wg[b]), (pu[j], wu[b])):
                            for c in range(8):
                                k.op(k.pe, (lambda e, pt=pt, wt=wt, j=j, c=c, o=o, n=n: e.matmul(
                                    pt[:, 0:n], lhsT=wt[:, c, j * 128:(j + 1) * 128], rhs=hn[:, c, o:o + n],
                                    start=(c == 0), stop=(c == 7))),
                                    reads=[wt, hnv[c][ti]], writes=[pt])
                    for j in range(2):
                        k.op(k.act, (lambda e, j=j, ab=ab, n=n: e.activation(sg[ab][:, j, 0:n], pg[j][:, 0:n], AF.Silu)),
                             reads=[pg[j]], writes=[sg[ab]])
                    for j in range(2):
                        k.op(k.dve, (lambda e, j=j, ab=ab, n=n: e.scalar_tensor_tensor(
                            out=act[ab][:, j, 0:n], in0=pu[j][:, 0:n], scalar=0.5, in1=sg[ab][:, j, 0:n],
                            op0=ALU.mult, op1=ALU.mult)),
                            reads=[pu[j], sg[ab]], writes=[act[ab]])
                    for m in range(8):
                        pdt = pd[m % 4]
                        for j in range(2):
                            k.op(k.pe, (lambda e, pdt=pdt, b=b, j=j, m=m, ab=ab, n=n: e.matmul(
                                pdt[:, 0:n], lhsT=wd[b][:, j, m * 128:(m + 1) * 128], rhs=act[ab][:, j, 0:n],
                                start=(j == 0), stop=(j == 1))),
                                reads=[wd[b], act[ab]], writes=[pdt])
                        k.op(k.dve, (lambda e, pdt=pdt, m=m, o=o, n=n: e.tensor_tensor(
                            out=y[:, m, o:o + n], in0=y[:, m, o:o + n], in1=pdt[:, 0:n], op=ALU.add)),
                            reads=[pdt, yv[m][ti]], writes=[yv[m][ti]])
            k.dma(k.sp, hTv[:, :, T0:T0 + GS], y[:], reads=allv)
            k.end()

    def phase_out(self, nrow):
        k, nc, T = self.k, self.nc, self.T
        hTv = self.hT.rearrange("(c p) t -> p c t", p=128)
        k.begin()
        hb = [k.sb([128, 8, 512], F32, "hb") for _ in range(2)]
        sq = [k.sb([128, 512], BF16, "sq") for _ in range(2)]
        rstd = [k.sb([128, 512], F32, "rstd") for _ in range(2)]
        yn = [k.sb([128, 8, 512], F32, "yn") for _ in range(2)]
        ot = [k.sb([128, 4, D_MODEL], F32, "ot") for _ in range(2)]
        pss = k.ps([128, 512], F32, "pss")
        pt = [k.ps([128, 512], F32, "pt") for _ in range(4)]
        pi = 0
        for gi, (t0, n) in enumerate(tiles_of(T, 512)):
            h = hb[gi % 2]
            r = rstd[gi % 2]
            yy = yn[gi % 2]
            o_ = ot[gi % 2]
            k.dma(k.sp, h[:, :, 0:n], hTv[:, :, t0:t0 + n], writes=[h])
            for c in range(8):
                s_ = sq[c % 2]
                k.op(k.act, (lambda e, s_=s_, h=h, c=c, n=n: e.activation(s_[:, 0:n], h[:, c, 0:n], AF.Square)),
                     reads=[h], writes=[s_])
                k.op(k.pe, (lambda e, s_=s_, c=c, n=n: e.matmul(pss[:, 0:n], lhsT=self.ones_bf[:], rhs=s_[:, 0:n],
                                                                 start=(c == 0), stop=(c == 7))),
                     reads=[s_, self.ones_bf], writes=[pss])
            k.op(k.act, (lambda e, r=r, n=n: e.activation(r[:, 0:n], pss[:, 0:n], AF.Ln, bias=self.eps_col[:, 0:1],
                                                           scale=1.0 / D_MODEL)),
                 reads=[pss, self.eps_col], writes=[r])
            k.op(k.act, (lambda e, r=r, n=n: e.activation(r[:, 0:n], r[:, 0:n], AF.Exp, scale=-0.5)), reads=[r], writes=[r])
            for c in range(8):
                col = nrow * 8 + c
                k.op(k.dve, (lambda e, r=r, h=h, yy=yy, c=c, n=n, col=col: e.scalar_tensor_tensor(
                    out=yy[:, c, 0:n], in0=h[:, c, 0:n], scalar=self.nw[:, col:col + 1], in1=r[:, 0:n],
                    op0=ALU.mult, op1=ALU.mult)),
                    reads=[h, r, self.nw], writes=[yy])
            nb = (n + 127) // 128
            blocks = [(b * 128, min(128, n - b * 128)) for b in range(nb)]
            for b, (o, rr) in enumerate(blocks):
                for half in range(2):
                    p = pt[pi % 4]
                    pi += 1
                    for cc in range(4):
                        c = half * 4 + cc
                        k.op(k.pe, (lambda e, p=p, yy=yy, c=c, cc=cc, o=o, rr=rr: e.transpose(
                            p[0:rr, cc * 128:(cc + 1) * 128], yy[:, c, o:o + rr], self.ident[:])),
                            reads=[yy, self.ident], writes=[p])
                    if half == 0:
                        k.op(k.dve, (lambda e, p=p, o_=o_, b=b, rr=rr: e.tensor_copy(o_[0:rr, b, 0:512], p[0:rr, :])),
                             reads=[p], writes=[o_])
                    else:
                        k.op(k.act, (lambda e, p=p, o_=o_, b=b, rr=rr: e.copy(o_[0:rr, b, 512:1024], p[0:rr, :])),
                             reads=[p], writes=[o_])
            if n % 128 == 0:
                k.dma(k.sp, self.yout[t0:t0 + n, :].rearrange("(j p) f -> p j f", p=128), o_[:, 0:nb, :], reads=[o_])
            else:
                for b, (o, rr) in enumerate(blocks):
                    k.dma(k.sp, self.yout[t0 + o:t0 + o + rr, :], o_[0:rr, b, :], reads=[o_])
        k.end()


    def emit_rope_tables(self, T0, tl, cosF, sinF, tmp):
        k = self.k
        PI = float(np.pi)
        invf = self.bsm[:, 257:258]
        sign = self.bsm[:, 258:259]
        for ti, (o, n) in enumerate(tl):
            pos, ang, ki, kf, tf = tmp
            k.op(k.pool, (lambda e, pos=pos, n=n, b=T0 + o: e.iota(pos[:, 0:n], [[1, n]], base=b, channel_multiplier=0,
                                                                   allow_small_or_imprecise_dtypes=True)), writes=[pos])
            k.op(k.dve, (lambda e, n=n: e.tensor_scalar(ang[:, 0:n], pos[:, 0:n], invf, None, op0=ALU.mult)),
                 reads=[pos, self.bsm], writes=[ang])
            def reduce(src, dst, shift, n=n):
                k.op(k.dve, (lambda e: e.tensor_scalar(dst[:, 0:n], src[:, 0:n], shift, None, op0=ALU.add)),
                     reads=[src], writes=[dst])
                k.op(k.dve, (lambda e: e.tensor_scalar(ki[:, 0:n], dst[:, 0:n], 1.0 / (2 * PI), None, op0=ALU.mult)),
                     reads=[dst], writes=[ki])
                k.op(k.dve, (lambda e: e.tensor_copy(tf[:, 0:n], ki[:, 0:n])), reads=[ki], writes=[tf])
                k.op(k.dve, (lambda e: e.scalar_tensor_tensor(out=dst[:, 0:n], in0=tf[:, 0:n], scalar=-2 * PI, in1=dst[:, 0:n],
                                                              op0=ALU.mult, op1=ALU.add)), reads=[tf, dst], writes=[dst])
                k.op(k.dve, (lambda e: e.tensor_scalar(tf[:, 0:n], dst[:, 0:n], PI, -2 * PI, op0=ALU.is_gt, op1=ALU.mult)),
                     reads=[dst], writes=[tf])
                k.op(k.dve, (lambda e: e.tensor_tensor(out=dst[:, 0:n], in0=dst[:, 0:n], in1=tf[:, 0:n], op=ALU.add)),
                     reads=[dst, tf], writes=[dst])
                k.op(k.dve, (lambda e: e.tensor_scalar(tf[:, 0:n], dst[:, 0:n], -PI, 2 * PI, op0=ALU.is_lt, op1=ALU.mult)),
                     reads=[dst], writes=[tf])
                k.op(k.dve, (lambda e: e.tensor_tensor(out=dst[:, 0:n], in0=dst[:, 0:n], in1=tf[:, 0:n], op=ALU.add)),
                     reads=[dst, tf], writes=[dst])
                k.op(k.dve, (lambda e: e.tensor_scalar(dst[:, 0:n], dst[:, 0:n], -PI, PI, op0=ALU.max, op1=ALU.min)),
                     reads=[dst], writes=[dst])
            reduce(ang, kf, 0.0)
            reduce(ang, pos, PI / 2)
            s_, c_ = sinF[ti], cosF[ti]
            k.op(k.act, (lambda e, s_=s_, n=n: e.activation(s_[:, 0:n], kf[:, 0:n], AF.Sin)), reads=[kf], writes=[s_])
            k.op(k.act, (lambda e, c_=c_, n=n: e.activation(c_[:, 0:n], pos[:, 0:n], AF.Sin)), reads=[pos], writes=[c_])
            k.op(k.dve, (lambda e, s_=s_, n=n: e.tensor_scalar(s_[:, 0:n], s_[:, 0:n], sign, None, op0=ALU.mult)),
                 reads=[s_, self.bsm], writes=[s_])

    def phase_attn_proj(self, nrow):
        k, nc = self.k, self.nc
        hTv = self.hT.rearrange("(c p) t -> p c t", p=128)
        for (T0, GS) in self.groups:
            tl = tiles_of(GS, 512)
            NT = len(tl)
            k.begin()
            y = k.sb([128, 8, GS], F32, "y")
            hn = k.sb([128, 8, GS], BF16, "hn")
            yv = [k.views(y, NT) for _ in range(8)]
            hnv = [k.views(hn, NT) for _ in range(8)]
            sq = [k.sb([128, 512], BF16, "sq") for _ in range(2)]
            rstd = [k.sb([128, 512], F32, "rstd") for _ in range(2)]
            pd = [k.ps([128, 512], F32, "pd") for _ in range(4)]
            pa = [k.ps([128, 512], F32, "pa") for _ in range(2)]
            pb = [k.ps([128, 512], F32, "pb") for _ in range(2)]
            allv = [v for c in range(8) for v in yv[c]]
            k.dma(k.sp, y[:], hTv[:, :, T0:T0 + GS], writes=allv)
            self.emit_norm(y, yv, hn, hnv, tl, nrow, pd, sq, rstd)
            cosF = [k.sb([128, 512], F32, "cosF") for _ in range(NT)]
            sinF = [k.sb([128, 512], F32, "sinF") for _ in range(NT)]
            tmp = (k.sb([128, 512], F32, "pos"), k.sb([128, 512], F32, "ang"),
                   k.sb([128, 512], mybir.dt.int32, "ki"), k.sb([128, 512], F32, "kf"), k.sb([128, 512], F32, "tf"))
            self.emit_rope_tables(T0, tl, cosF, sinF, tmp)
            wa = [k.sb([128, 8, 128], BF16, "wa") for _ in range(2)]
            wb = [k.sb([128, 8, 128], BF16, "wb") for _ in range(2)]
            t1 = [k.sb([128, 512], F32, "t1") for _ in range(2)]
            t2 = [k.sb([128, 512], F32, "t2") for _ in range(2)]
            stg = [k.sb([128, 512], BF16, "stg") for _ in range(3)]
            it = 0
            for m in range(16):
                b = m % 2
                k.dma(k.pool, wa[b][:], self.b_w[:, m * 128:(m + 1) * 128].rearrange("(kk p) c -> p kk c", p=128), writes=[wa[b]])
                k.dma(k.pool, wb[b][:], self.b_wsw[:, m * 128:(m + 1) * 128].rearrange("(kk p) c -> p kk c", p=128), writes=[wb[b]])
                for ti, (o, n) in enumerate(tl):
                    ab = it % 2
                    sb_ = stg[it % 3]
                    it += 1
                    for (pt, wt) in ((pa[ab], wa[b]), (pb[ab], wb[b])):
                        for c in range(8):
                            k.op(k.pe, (lambda e, pt=pt, wt=wt, c=c, o=o, n=n: e.matmul(
                                pt[:, 0:n], lhsT=wt[:, c, :], rhs=hn[:, c, o:o + n], start=(c == 0), stop=(c == 7))),
                                reads=[wt, hnv[c][ti]], writes=[pt])
                    k.op(k.dve, (lambda e, ab=ab, ti=ti, n=n: e.tensor_tensor(out=t1[ab][:, 0:n], in0=pa[ab][:, 0:n],
                                                                              in1=cosF[ti][:, 0:n], op=ALU.mult)),
                         reads=[pa[ab], cosF[ti]], writes=[t1[ab]])
                    k.op(k.dve, (lambda e, ab=ab, ti=ti, n=n: e.tensor_tensor(out=t2[ab][:, 0:n], in0=pb[ab][:, 0:n],
                                                                              in1=sinF[ti][:, 0:n], op=ALU.mult)),
                         reads=[pb[ab], sinF[ti]], writes=[t2[ab]])
                    k.op(k.pool, (lambda e, ab=ab, sb_=sb_, n=n: e.tensor_tensor(out=sb_[:, 0:n], in0=t1[ab][:, 0:n],
                                                                                 in1=t2[ab][:, 0:n], op=ALU.add)),
                         reads=[t1[ab], t2[ab]], writes=[sb_])
                    k.dma(k.sp, self.qkT[m * 128:(m + 1) * 128, T0 + o:T0 + o + n], sb_[:, 0:n], reads=[sb_])
            wv = [k.sb([128, 8, 512], BF16, "wv") for _ in range(2)]
            vst = [k.sb([128, 512], BF16, "vst") for _ in range(3)]
            it = 0
            for vb in range(2):
                k.dma(k.pool, wv[vb][:], self.b_w[:, 2048 + vb * 512:2048 + (vb + 1) * 512].rearrange("(kk p) c -> p kk c", p=128),
                      writes=[wv[vb]])
                for ti, (o, n) in enumerate(tl):
                    for bo in range(0, n, 128):
                        r = min(128, n - bo)
                        pt = pd[it % 4]
                        vs = vst[it % 3]
                        it += 1
                        for c in range(8):
                            k.op(k.pe, (lambda e, pt=pt, vb=vb, c=c, o=o, bo=bo, r=r: e.matmul(
                                pt[0:r, :], lhsT=hn[:, c, o + bo:o + bo + r], rhs=wv[vb][:, c, :], start=(c == 0), stop=(c == 7))),
                                reads=[wv[vb], hnv[c][ti]], writes=[pt])
                        k.op(k.act, (lambda e, pt=pt, vs=vs, r=r: e.copy(vs[0:r, :], pt[0:r, :])), reads=[pt], writes=[vs])
                        k.dma(k.sp, self.v_tm[T0 + o + bo:T0 + o + bo + r, vb * 512:(vb + 1) * 512], vs[0:r, :], reads=[vs])
            k.end()

    def phase_attn_core(self):
        k, nc, T = self.k, self.nc, self.T
        NKT = (T + 127) // 128
        qtl = tiles_of(T, 512)
        k.begin()
        lt = k.sb([128, 256], F32, "lt")
        l2 = k.sb([128, 2], F32, "l2")
        neglam = k.sb([128, 1], F32, "neglam")
        subw = k.sb([128, 1], F32, "subw")
        kb = k.sb([128, NKT], F32, "kb")
        k.dma(k.sp, kb[:], self.kbias_in, writes=[kb])
        k.op(k.dve, (lambda e: e.tensor_tensor(out=lt[:, 0:64], in0=self.bsm[:, 0:64], in1=self.bsm[:, 64:128], op=ALU.mult)),
             reads=[self.bsm], writes=[lt])
        k.op(k.dve, (lambda e: e.tensor_tensor(out=lt[:, 64:128], in0=self.bsm[:, 128:192], in1=self.bsm[:, 192:256], op=ALU.mult)),
             reads=[self.bsm], writes=[lt])
        k.op(k.dve, (lambda e: e.reduce_sum(l2[:, 0:2], lt[:, 0:128].rearrange("p (a b) -> p a b", a=2), axis=AX.X)),
             reads=[lt], writes=[l2])
        k.op(k.act, (lambda e: e.activation(l2[:, 0:2], l2[:, 0:2], AF.Exp)), reads=[l2], writes=[l2])
        k.op(k.dve, (lambda e: e.tensor_tensor(out=neglam[:], in0=l2[:, 1:2], in1=l2[:, 0:1], op=ALU.subtract)),
             reads=[l2], writes=[neglam])
        k.op(k.dve, (lambda e: e.tensor_tensor(out=neglam[:], in0=neglam[:], in1=self.bsm[:, 259:260], op=ALU.subtract)),
             reads=[neglam, self.bsm], writes=[neglam])
        k.op(k.dve, (lambda e: e.tensor_scalar(subw[:], self.bsm[:, 259:260], -1.0, 1.0, op0=ALU.mult, op1=ALU.add)),
             reads=[self.bsm], writes=[subw])
        k.op(k.dve, (lambda e: e.tensor_tensor(out=subw[:], in0=subw[:], in1=self.bsm[:, 256:257], op=ALU.mult)),
             reads=[subw, self.bsm], writes=[subw])

        kk1 = [k.sb([64, T], BF16, "kk1") for _ in range(2)]
        kk2 = [k.sb([64, T], BF16, "kk2") for _ in range(2)]
        vh = [k.sb([128, NKT, 128], BF16, "vh") for _ in range(2)]
        q1 = [k.sb([64, 512], BF16, "q1") for _ in range(2)]
        q2 = [k.sb([64, 512], BF16, "q2") for _ in range(2)]
        p1 = [k.sb([128, 512], BF16, "p1") for _ in range(3)]
        p2 = [k.sb([128, 512], BF16, "p2") for _ in range(3)]
        ps1 = [k.ps([128, 512], F32, "ps1") for _ in range(2)]
        ps2 = [k.ps([128, 512], F32, "ps2") for _ in range(2)]
        num1, num2, z1, z2 = (k.ps([128, 512], F32, nm) for nm in ("num1", "num2", "z1", "z2"))
        r1 = k.sb([128, 512], F32, "r1")
        r2 = k.sb([128, 512], F32, "r2")
        o1 = k.sb([128, 512], F32, "o1")
        o2 = k.sb([128, 512], F32, "o2")
        oo = k.sb([128, 512], F32, "oo")
        sqo = k.sb([128, 512], BF16, "sqo")
        rs = k.sb([128, 512], F32, "rs")
        ost = [k.sb([128, 512], BF16, "ost") for _ in range(2)]
        nfull = T // 128
        rem = T - nfull * 128
        it = 0
        fi = 0
        for h in range(8):
            hb = h % 2
            k.dma(k.sp, kk1[hb][:], self.qkT[1024 + h * 64:1024 + (h + 1) * 64, :], writes=[kk1[hb]])
            k.dma(k.sp, kk2[hb][:], self.qkT[1536 + h * 64:1536 + (h + 1) * 64, :], writes=[kk2[hb]])
            k.dma(k.sp, vh[hb][:, 0:nfull, :],
                  self.v_tm[0:nfull * 128, h * 128:(h + 1) * 128].rearrange("(kt p) e -> p kt e", p=128), writes=[vh[hb]])
            if rem:
                k.dma(k.sp, vh[hb][0:rem, nfull, :], self.v_tm[nfull * 128:T, h * 128:(h + 1) * 128], writes=[vh[hb]])
            for qi, (t0, n) in enumerate(qtl):
                qb = fi % 2
                k.dma(k.sp, q1[qb][:, 0:n], self.qkT[h * 64:(h + 1) * 64, t0:t0 + n], writes=[q1[qb]])
                k.dma(k.sp, q2[qb][:, 0:n], self.qkT[512 + h * 64:512 + (h + 1) * 64, t0:t0 + n], writes=[q2[qb]])
                for kt in range(NKT):
                    kn = 128 if kt < nfull else rem
                    sb2 = it % 2
                    pb3 = it % 3
                    it += 1
                    first, last = (kt == 0), (kt == NKT - 1)
                    for (ps, kk, qq, pp, nm, zz) in ((ps1[sb2], kk1[hb], q1[qb], p1[pb3], num1, z1),
                                                     (ps2[sb2], kk2[hb], q2[qb], p2[pb3], num2, z2)):
                        k.op(k.pe, (lambda e, ps=ps, kk=kk, qq=qq, kt=kt, kn=kn, n=n: e.matmul(
                            ps[0:kn, 0:n], lhsT=kk[:, kt * 128:kt * 128 + kn], rhs=qq[:, 0:n], start=True, stop=True)),
                            reads=[kk, qq], writes=[ps])
                        k.op(k.act, (lambda e, ps=ps, pp=pp, kt=kt, kn=kn, n=n: e.activation(
                            pp[0:kn, 0:n], ps[0:kn, 0:n], AF.Exp, bias=kb[0:kn, kt:kt + 1], scale=0.125)),
                            reads=[ps, kb], writes=[pp])
                        k.op(k.pe, (lambda e, nm=nm, pp=pp, hb=hb, kt=kt, kn=kn, n=n, first=first, last=last: e.matmul(
                            nm[:, 0:n], lhsT=vh[hb][0:kn, kt, :], rhs=pp[0:kn, 0:n], start=first, stop=last)),
                            reads=[vh[hb], pp], writes=[nm])
                        k.op(k.pe, (lambda e, zz=zz, pp=pp, kn=kn, n=n, first=first, last=last: e.matmul(
                            zz[:, 0:n], lhsT=self.ones_bf[0:kn, :], rhs=pp[0:kn, 0:n], start=first, stop=last)),
                            reads=[self.ones_bf, pp], writes=[zz])
                k.op(k.dve, (lambda e, n=n: e.reciprocal(r1[:, 0:n], z1[:, 0:n])), reads=[z1], writes=[r1])
                k.op(k.dve, (lambda e, n=n: e.reciprocal(r2[:, 0:n], z2[:, 0:n])), reads=[z2], writes=[r2])
                k.op(k.dve, (lambda e, n=n: e.tensor_tensor(out=o1[:, 0:n], in0=num1[:, 0:n], in1=r1[:, 0:n], op=ALU.mult)),
                     reads=[num1, r1], writes=[o1])
                k.op(k.dve, (lambda e, n=n: e.scalar_tensor_tensor(out=o2[:, 0:n], in0=num2[:, 0:n], scalar=neglam[:, 0:1],
                                                                   in1=r2[:, 0:n], op0=ALU.mult, op1=ALU.mult)),
                     reads=[num2, r2, neglam], writes=[o2])
                k.op(k.pool, (lambda e, n=n: e.tensor_tensor(out=oo[:, 0:n], in0=o1[:, 0:n], in1=o2[:, 0:n], op=ALU.add)),
                     reads=[o1, o2], writes=[oo])
                k.op(k.act, (lambda e, n=n: e.activation(sqo[:, 0:n], oo[:, 0:n], AF.Square)), reads=[oo], writes=[sqo])
                pss = ps1[it % 2]
                k.op(k.pe, (lambda e, pss=pss, n=n: e.matmul(pss[:, 0:n], lhsT=self.ones_bf[:], rhs=sqo[:, 0:n], start=True, stop=True)),
                     reads=[self.ones_bf, sqo], writes=[pss])
                k.op(k.act, (lambda e, pss=pss, n=n: e.activation(rs[:, 0:n], pss[:, 0:n], AF.Ln, bias=self.eps_col[:, 0:1],
                                                                   scale=1.0 / 128)), reads=[pss, self.eps_col], writes=[rs])
                k.op(k.act, (lambda e, n=n: e.activation(rs[:, 0:n], rs[:, 0:n], AF.Exp, scale=-0.5)), reads=[rs], writes=[rs])
                os_ = ost[fi % 2]
                fi += 1
                k.op(k.dve, (lambda e, os_=os_, n=n: e.scalar_tensor_tensor(out=os_[:, 0:n], in0=oo[:, 0:n], scalar=subw[:, 0:1],
                                                                            in1=rs[:, 0:n], op0=ALU.mult, op1=ALU.mult)),
                     reads=[oo, rs, subw], writes=[os_])
                k.dma(k.sp, self.oT[h * 128:(h + 1) * 128, t0:t0 + n], os_[:, 0:n], reads=[os_])
        k.end()

    def phase_out_proj(self, wo_ap):
        k, nc, T = self.k, self.nc, self.T
        hTv = self.hT.rearrange("(c p) t -> p c t", p=128)
        oTv = self.oT.rearrange("(c p) t -> p c t", p=128)
        k.begin()
        wo = k.sb([128, 8, D_MODEL], BF16, "wo")
        for c in range(8):
            for hf in range(2):
                k.dma(k.pool, wo[:, c, hf * 512:(hf + 1) * 512], wo_ap[c * 128:(c + 1) * 128, hf * 512:(hf + 1) * 512], writes=[wo])
        ob = [k.sb([128, 8, 512], BF16, "ob") for _ in range(2)]
        hb = [k.sb([128, 8, 512], F32, "hb") for _ in range(2)]
        pp = [k.ps([128, 512], F32, "pp") for _ in range(4)]
        pi = 0
        for gi, (t0, n) in enumerate(tiles_of(T, 512)):
            o_ = ob[gi % 2]
            h_ = hb[gi % 2]
            k.dma(k.sp, o_[:, :, 0:n], oTv[:, :, t0:t0 + n], writes=[o_])
            k.dma(k.sp, h_[:, :, 0:n], hTv[:, :, t0:t0 + n], writes=[h_])
            for m in range(8):
                p = pp[pi % 4]
                pi += 1
                for c in range(8):
                    k.op(k.pe, (lambda e, p=p, o_=o_, c=c, m=m, n=n: e.matmul(
                        p[:, 0:n], lhsT=wo[:, c, m * 128:(m + 1) * 128], rhs=o_[:, c, 0:n], start=(c == 0), stop=(c == 7))),
                        reads=[wo, o_], writes=[p])
                k.op(k.dve, (lambda e, p=p, h_=h_, m=m, n=n: e.tensor_tensor(out=h_[:, m, 0:n], in0=h_[:, m, 0:n],
                                                                             in1=p[:, 0:n], op=ALU.add)),
                     reads=[p, h_], writes=[h_])
            k.dma(k.sp, hTv[:, :, t0:t0 + n], h_[:, :, 0:n], reads=[h_])
        k.end()

    def mixer_attn(self, nrow):
        import os
        st = int(os.environ.get("ATT_STAGE", "3"))
        self.phase_attn_proj(nrow)
        if st >= 2:
            self.phase_attn_core()
        if st >= 3:
            self.phase_out_proj(self.b_wo)


    def phase_proj(self, nrow, setup, fm_jobs, tm_jobs):
        k, nc = self.k, self.nc
        hTv = self.hT.rearrange("(c p) t -> p c t", p=128)
        for (T0, GS) in self.groups:
            tl = tiles_of(GS, 512)
            NT = len(tl)
            k.begin()
            y = k.sb([128, 8, GS], F32, "y")
            hn = k.sb([128, 8, GS], BF16, "hn")
            yv = [k.views(y, NT) for _ in range(8)]
            hnv = [k.views(hn, NT) for _ in range(8)]
            sq = [k.sb([128, 512], BF16, "sq") for _ in range(2)]
            rstd = [k.sb([128, 512], F32, "rstd") for _ in range(2)]
            pd = [k.ps([128, 512], F32, "pd") for _ in range(4)]
            allv = [v for c in range(8) for v in yv[c]]
            k.dma(k.sp, y[:], hTv[:, :, T0:T0 + GS], writes=allv)
            self.emit_norm(y, yv, hn, hnv, tl, nrow, pd, sq, rstd)
            ctx = setup(T0, tl)
            wa = [k.sb([128, 8, 128], BF16, "wa") for _ in range(2)]
            it = 0
            for ji, (wap, post) in enumerate(fm_jobs):
                b = ji % 2
                k.dma(k.pool, wa[b][:], wap.rearrange("(kk p) c -> p kk c", p=128), writes=[wa[b]])
                for ti, (o, n) in enumerate(tl):
                    pt = pd[it % 4]
                    it += 1
                    for c in range(8):
                        k.op(k.pe, (lambda e, pt=pt, b=b, c=c, o=o, n=n: e.matmul(
                            pt[:, 0:n], lhsT=wa[b][:, c, :], rhs=hn[:, c, o:o + n], start=(c == 0), stop=(c == 7))),
                            reads=[wa[b], hnv[c][ti]], writes=[pt])
                    post(ctx, pt, T0, ti, o, n)
            wv = [k.sb([128, 8, 512], BF16, "wv") for _ in range(2)]
            for ji, job in enumerate(tm_jobs):
                wap, post = job[0], job[1]
                ncl = job[2] if len(job) > 2 else 512
                b = ji % 2
                k.dma(k.pool, wv[b][:, :, 0:ncl], wap.rearrange("(kk p) c -> p kk c", p=128), writes=[wv[b]])
                for ti, (o, n) in enumerate(tl):
                    for bo in range(0, n, 128):
                        r = min(128, n - bo)
                        pt = pd[it % 4]
                        it += 1
                        for c in range(8):
                            k.op(k.pe, (lambda e, pt=pt, b=b, c=c, o=o, bo=bo, r=r, ncl=ncl: e.matmul(
                                pt[0:r, 0:ncl], lhsT=hn[:, c, o + bo:o + bo + r], rhs=wv[b][:, c, 0:ncl], start=(c == 0), stop=(c == 7))),
                                reads=[wv[b], hnv[c][ti]], writes=[pt])
                        post(ctx, pt, T0 + o + bo, r)
            k.end()

    def hg_prep(self, li):
        k, nc = self.k, self.nc
        k.begin()
        e = k.sb([128, 32], F32, "lbe")
        ssum = k.sb([128, 8], F32, "lbs")
        k.op(k.act, (lambda en: en.activation(e[:], self.csm[:, 0:32], AF.Exp)), reads=[self.csm], writes=[e])
        k.op(k.dve, (lambda en: en.tensor_tensor(out=ssum[:], in0=e[:, 0:8], in1=e[:, 8:16], op=ALU.add)), reads=[e], writes=[ssum])
        k.op(k.dve, (lambda en: en.tensor_tensor(out=ssum[:], in0=ssum[:], in1=e[:, 16:24], op=ALU.add)), reads=[e, ssum], writes=[ssum])
        k.op(k.dve, (lambda en: en.tensor_tensor(out=ssum[:], in0=ssum[:], in1=e[:, 24:32], op=ALU.add)), reads=[e, ssum], writes=[ssum])
        k.op(k.dve, (lambda en: en.reciprocal(ssum[:], ssum[:])), reads=[ssum], writes=[ssum])
        lbc = self.lb_col
        k.op(k.dve, (lambda en: en.memset(lbc[:, 0:8], 0.0)), writes=[lbc])
        for r in range(1, li + 1):
            k.op(k.dve, (lambda en, r=r: en.tensor_tensor(out=lbc[:, 0:8], in0=lbc[:, 0:8], in1=e[:, r * 8:(r + 1) * 8], op=ALU.add)),
                 reads=[e, lbc], writes=[lbc])
        k.op(k.dve, (lambda en: en.tensor_tensor(out=lbc[:, 0:8], in0=lbc[:, 0:8], in1=ssum[:], op=ALU.mult)), reads=[lbc, ssum], writes=[lbc])
        k.op(k.dve, (lambda en: en.tensor_scalar(lbc[:, 8:16], lbc[:, 0:8], -1.0, 1.0, op0=ALU.mult, op1=ALU.add)), reads=[lbc], writes=[lbc])
        lbr = self.lbrow_t.ap()
        k.dma(k.sp, lbr.rearrange("r (c p) -> p r c", p=128), lbc[:].rearrange("p (r c) -> p r c", c=8), reads=[lbc],
              allow_slow_non_contiguous=True)
        k.end()
        k.begin()
        k.dma(k.sp, self.oml_row[:], bass.AP(self.lbrow_t, D_MODEL, [[0, 128], [1, D_MODEL]]), writes=[self.oml_row])
        k.end()

    def phase_hg_proj(self, nrow):
        k = self.k
        QS = 128 ** -0.5

        def setup(T0, tl):
            ctx = {}
            ctx["tm"] = [k.sb([128, 512], F32, "tmk") for _ in tl]
            for ti, (o, n) in enumerate(tl):
                k.dma(k.sp, ctx["tm"][ti][:, 0:n], self.tmask_rep[:, T0 + o:T0 + o + n], writes=[ctx["tm"][ti]])
            ctx["sg"] = [k.sb([128, 512], F32, "sg") for _ in range(2)]
            ctx["st"] = [k.sb([128, 512], BF16, "st") for _ in range(3)]
            ctx["k32"] = [k.sb([128, 512], F32, "k32") for _ in range(2)]
            ctx["lf"] = [k.sb([128, 512], F32, "lf") for _ in range(2)]
            ctx["kb"] = [k.sb([128, 512], BF16, "kb") for _ in range(2)]
            ctx["i"] = 0
            return ctx

        def post_silu(dst, scale):
            def post(ctx, pt, T0, ti, o, n):
                i = ctx["i"]
                ctx["i"] += 1
                sg, st = ctx["sg"][i % 2], ctx["st"][i % 3]
                k.op(k.act, (lambda e: e.activation(sg[:, 0:n], pt[:, 0:n], AF.Silu)), reads=[pt], writes=[sg])
                k.op(k.pool, (lambda e: e.tensor_scalar(st[:, 0:n], sg[:, 0:n], scale, None, op0=ALU.mult)), reads=[sg], writes=[st])
                return st
            return post

        def fm_q(c):
            base = post_silu(None, QS)

            def post(ctx, pt, T0, ti, o, n):
                st = base(ctx, pt, T0, ti, o, n)
                k.dma(k.sp, self.fmA[c * 128:(c + 1) * 128, T0 + o:T0 + o + n], st[:, 0:n], reads=[st])
            return post

        def fm_g(c):
            base = post_silu(None, 1.0)

            def post(ctx, pt, T0, ti, o, n):
                st = base(ctx, pt, T0, ti, o, n)
                k.dma(k.sp, self.fmG[c * 128:(c + 1) * 128, T0 + o:T0 + o + n], st[:, 0:n], reads=[st])
            return post

        def fm_k(d, c):
            def post(ctx, pt, T0, ti, o, n):
                i = ctx["i"]
                ctx["i"] += 1
                sg, st = ctx["sg"][i % 2], ctx["st"][i % 3]
                k.op(k.act, (lambda e: e.activation(sg[:, 0:n], pt[:, 0:n], AF.Sigmoid, scale=-1.0)), reads=[pt], writes=[sg])
                k.op(k.dve, (lambda e: e.scalar_tensor_tensor(out=st[:, 0:n], in0=sg[:, 0:n], scalar=self.lb_col[:, 8 + c:9 + c],
                                                              in1=ctx["tm"][ti][:, 0:n], op0=ALU.mult, op1=ALU.mult)),
                     reads=[sg, self.lb_col, ctx["tm"][ti]], writes=[st])
                k.dma(k.sp, self.fmB[d][c * 128:(c + 1) * 128, T0 + o:T0 + o + n], st[:, 0:n], reads=[st])
            return post

        def tm_v(cb):
            def post(ctx, pt, tok0, r):
                i = ctx["i"]
                ctx["i"] += 1
                st = ctx["st"][i % 3]
                k.op(k.act, (lambda e: e.copy(st[0:r, :], pt[0:r, :])), reads=[pt], writes=[st])
                k.dma(k.sp, self.v_tm[tok0:tok0 + r, cb * 512:(cb + 1) * 512], st[0:r, :], reads=[st])
            return post

        def tm_f(d, cb):
            def post(ctx, pt, tok0, r):
                i = ctx["i"]
                ctx["i"] += 1
                sg, k32, lf, kb = ctx["sg"][i % 2], ctx["k32"][i % 2], ctx["lf"][i % 2], ctx["kb"][i % 2]
                blk = tok0 // 128
                assert tok0 % 128 == 0
                k.op(k.act, (lambda e: e.activation(sg[0:r, :], pt[0:r, :], AF.Sigmoid, scale=-1.0)), reads=[pt], writes=[sg])
                k.op(k.dve, (lambda e: e.scalar_tensor_tensor(out=k32[0:r, :], in0=sg[0:r, :], scalar=self.tmc[0:r, blk:blk + 1],
                                                              in1=self.oml_row[0:r, cb * 512:(cb + 1) * 512], op0=ALU.mult, op1=ALU.mult)),
                     reads=[sg, self.tmc, self.oml_row], writes=[k32])
                k.op(k.act, (lambda e: e.activation(lf[0:r, :], k32[0:r, :], AF.Ln, bias=self.one_col[0:r, 0:1], scale=-1.0)),
                     reads=[k32, self.one_col], writes=[lf])
                k.op(k.pool, (lambda e: e.tensor_copy(kb[0:r, :], k32[0:r, :])), reads=[k32], writes=[kb])
                k.dma(k.sp, self.tmK[d][tok0:tok0 + r, cb * 512:(cb + 1) * 512], kb[0:r, :], reads=[kb])
                k.dma(k.sp, self.tmL[d][tok0:tok0 + r, cb * 512:(cb + 1) * 512], lf[0:r, :], reads=[lf])
            return post

        W = self.c_w
        fm = []
        for c in range(8):
            fm.append((W[:, c * 128:(c + 1) * 128], fm_q(c)))
        for c in range(8):
            fm.append((W[:, 2048 + c * 128:2048 + (c + 1) * 128], fm_g(c)))
        for d in range(2):
            for c in range(8):
                fm.append((W[:, 3072 + d * 1024 + c * 128:3072 + d * 1024 + (c + 1) * 128], fm_k(d, c)))
        tm = []
        for cb in range(2):
            tm.append((W[:, 1024 + cb * 512:1024 + (cb + 1) * 512], tm_v(cb)))
        for d in range(2):
            for cb in range(2):
                tm.append((W[:, 3072 + d * 1024 + cb * 512:3072 + d * 1024 + (cb + 1) * 512], tm_f(d, cb)))
        self.phase_proj(nrow, setup, fm, tm)

    def phase_hg_core(self, onorm_col):
        k, nc, T = self.k, self.nc, self.T
        tiles = tiles_of(T, 512)
        for sweep in (1, 0):
            d = sweep
            k.begin()
            if d == 0:
                Uin, Uex, Mk, lastcol = self.tri[:, 0:64], self.tri[:, 64:128], 0, 63
            else:
                Uin, Uex, Mk, lastcol = self.tri[:, 128:192], self.tri[:, 192:256], 2, 0
            mrep = k.sb([64, 512], F32, "mrep")
            for c in range(8):
                k.op(k.dve, (lambda e, c=c, Mk=Mk: e.tensor_copy(mrep[:, c * 64:(c + 1) * 64], self.tri[:, Mk * 64:(Mk + 1) * 64])),
                     reads=[self.tri], writes=[mrep])
            S = [k.sb([128, 128], F32, "S") for _ in range(8)]
            Sb = [k.sb([128, 128], BF16, "Sb") for _ in range(8)]
            for h in range(8):
                k.op(k.dve, (lambda e, h=h: e.memset(S[h][:], 0.0)), writes=[S[h]])
                k.op(k.pool, (lambda e, h=h: e.memset(Sb[h][:], 0.0)), writes=[Sb[h]])
            NB = 2
            qT = [k.sb([128, 512], BF16, "qT") for _ in range(NB)]
            kT = [k.sb([128, 512], BF16, "kT") for _ in range(NB)]
            ktm = [k.sb([64, 8, 128], BF16, "ktm") for _ in range(NB)]
            lf = [k.sb([64, 8, 128], F32, "lf") for _ in range(NB)]
            vt = [k.sb([64, 8, 128], BF16, "vt") for _ in range(NB)]
            eb = [k.sb([128, 512], F32, "eb") for _ in range(NB)]
            enb = [k.sb([128, 512], F32, "enb") for _ in range(NB)]
            qd = [k.sb([128, 512], BF16, "qd") for _ in range(NB)]
            kd = [k.sb([128, 512], BF16, "kd") for _ in range(NB)]
            ekd = [k.sb([64, 8, 128], F32, "ekd") for _ in range(NB)]
            kdec = [k.sb([64, 8, 128], BF16, "kdec") for _ in range(NB)]
            atm = [k.sb([64, 512], BF16, "atm") for _ in range(NB)]
            B = [k.ps([128, 512], F32, "Y%d" % i) for i in range(8)]
            ost = [k.sb([128, 512], F32, "ost") for _ in range(2)]
            if d == 0:
                obt = [k.sb([128, 512], F32, "obt") for _ in range(2)]
                gt = [k.sb([128, 512], BF16, "gt") for _ in range(2)]
                sqo_l = [k.sb([128, 512], BF16, "sqo") for _ in range(2)]
                rs_l = [k.sb([128, 512], F32, "rs") for _ in range(2)]
                fin = [k.sb([128, 512], BF16, "fin") for _ in range(2)]
            order = list(range(len(tiles)))
            if d == 1:
                order = order[::-1]

            def unit(h, b, it, t0, n, ncn, corder):
                if True:
                    rows = slice(h * 128, (h + 1) * 128)
                    Y = B[4 * b:4 * b + 4]
                    pbc, pat = Y[0], Y[3]
                    psuf = [Y[1], Y[2]]
                    pout = Y[1]
                    pst = [Y[2], Y[2]]
                    if d == 0:
                        sqo, rs = sqo_l[b], rs_l[b]
                    k.dma(k.sp, qT[b][:, 0:n], self.fmA[rows, t0:t0 + n], writes=[qT[b]])
                    k.dma(k.sp, kT[b][:, 0:n], self.fmB[d][rows, t0:t0 + n], writes=[kT[b]])
                    k.dma(k.sp, ktm[b][:, 0:ncn, :], self.tmK[d][t0:t0 + n, rows].rearrange("(c p) e -> p c e", p=64), writes=[ktm[b]])
                    k.dma(k.sp, lf[b][:, 0:ncn, :], self.tmL[d][t0:t0 + n, rows].rearrange("(c p) e -> p c e", p=64), writes=[lf[b]])
                    k.dma(k.sp, vt[b][:, 0:ncn, :], self.v_tm[t0:t0 + n, rows].rearrange("(c p) e -> p c e", p=64), writes=[vt[b]])
                    if d == 0:
                        k.dma(k.sp, obt[b][:, 0:n], self.obT[rows, t0:t0 + n], writes=[obt[b]])
                        k.dma(k.sp, gt[b][:, 0:n], self.fmG[rows, t0:t0 + n], writes=[gt[b]])
                    for c in range(ncn):
                        k.op(k.pe, (lambda e, b=b, c=c: e.matmul(pbc[:, c * 64:(c + 1) * 64], lhsT=lf[b][:, c, :], rhs=Uin,
                                                                 start=True, stop=True)), reads=[lf[b], self.tri], writes=[pbc])
                    k.op(k.act, (lambda e, b=b, n=n: e.activation(eb[b][:, 0:n], pbc[:, 0:n], AF.Exp)), reads=[pbc], writes=[eb[b]])
                    k.op(k.act, (lambda e, b=b, n=n: e.activation(enb[b][:, 0:n], pbc[:, 0:n], AF.Exp, scale=-1.0)), reads=[pbc], writes=[enb[b]])
                    k.op(k.dve, (lambda e, b=b, n=n: e.tensor_tensor(out=qd[b][:, 0:n], in0=qT[b][:, 0:n], in1=eb[b][:, 0:n], op=ALU.mult)),
                         reads=[qT[b], eb[b]], writes=[qd[b]])
                    k.op(k.dve, (lambda e, b=b, n=n: e.tensor_tensor(out=kd[b][:, 0:n], in0=kT[b][:, 0:n], in1=enb[b][:, 0:n], op=ALU.mult)),
                         reads=[kT[b], enb[b]], writes=[kd[b]])
                    for c in range(ncn):
                        ps_ = psuf[c // 4]
                        k.op(k.pe, (lambda e, b=b, c=c, ps_=ps_: e.matmul(ps_[0:64, (c % 4) * 128:(c % 4 + 1) * 128], lhsT=Uex, rhs=lf[b][:, c, :],
                                                                          start=True, stop=True)), reads=[lf[b], self.tri], writes=[ps_])
                    for half in range((ncn + 3) // 4):
                        nn = min(4, ncn - half * 4)
                        k.op(k.act, (lambda e, b=b, half=half, nn=nn, pq=psuf[half]: e.activation(
                            ekd[b][:, half * 4:half * 4 + nn, :], pq[0:64, 0:nn * 128].rearrange("p (c d) -> p c d", d=128), AF.Exp)),
                             reads=[psuf[half]], writes=[ekd[b]])
                    k.op(k.dve, (lambda e, b=b, ncn=ncn: e.tensor_tensor(out=kdec[b][:, 0:ncn, :], in0=ktm[b][:, 0:ncn, :],
                                                                        in1=ekd[b][:, 0:ncn, :], op=ALU.mult)),
                         reads=[ktm[b], ekd[b]], writes=[kdec[b]])
                    for c in range(ncn):
                        cs = slice(c * 64, (c + 1) * 64)
                        k.op(k.pe, (lambda e, b=b, cs=cs: e.matmul(pat[0:64, cs], lhsT=kd[b][:, cs], rhs=qd[b][:, cs], start=True, stop=True)),
                             reads=[kd[b], qd[b]], writes=[pat])
                    k.op(k.dve, (lambda e, b=b, n=n: e.tensor_tensor(out=atm[b][:, 0:n], in0=pat[0:64, 0:n], in1=mrep[:, 0:n], op=ALU.mult)),
                         reads=[pat, mrep], writes=[atm[b]])
                    for c in corder:
                        cs = slice(c * 64, (c + 1) * 64)
                        k.op(k.pe, (lambda e, b=b, cs=cs, h=h: e.matmul(pout[:, cs], lhsT=Sb[h][:], rhs=qd[b][:, cs], start=True, stop=False)),
                             reads=[Sb[h], qd[b]], writes=[pout])
                        k.op(k.pe, (lambda e, b=b, cs=cs, c=c: e.matmul(pout[:, cs], lhsT=vt[b][:, c, :], rhs=atm[b][:, cs], start=False, stop=True)),
                             reads=[vt[b], atm[b]], writes=[pout])
                        pp = pst[c % 2]
                        pc = slice((c % 2) * 128, (c % 2 + 1) * 128)
                        k.op(k.pe, (lambda e, b=b, c=c, pp=pp, pc=pc: e.matmul(pp[:, pc], lhsT=kdec[b][:, c, :], rhs=vt[b][:, c, :], start=True, stop=True)),
                             reads=[kdec[b], vt[b]], writes=[pp])
                        fc = c * 64 + lastcol
                        k.op(k.dve, (lambda e, b=b, h=h, pp=pp, fc=fc, pc=pc: e.scalar_tensor_tensor(
                            out=S[h][:], in0=S[h][:], scalar=eb[b][:, fc:fc + 1], in1=pp[:, pc], op0=ALU.mult, op1=ALU.add)),
                            reads=[S[h], eb[b], pp], writes=[S[h]])
                        k.op(k.act, (lambda e, h=h: e.copy(Sb[h][:], S[h][:])), reads=[S[h]], writes=[Sb[h]])
                    if d == 1:
                        os_ = ost[it % 2]
                        k.op(k.act, (lambda e, os_=os_, n=n: e.copy(os_[:, 0:n], pout[:, 0:n])), reads=[pout], writes=[os_])
                        k.dma(k.sp, self.obT[rows, t0:t0 + n], os_[:, 0:n], reads=[os_])
                    else:
                        os_ = ost[it % 2]
                        k.op(k.dve, (lambda e, os_=os_, b=b, n=n: e.tensor_tensor(out=os_[:, 0:n], in0=pout[:, 0:n], in1=obt[b][:, 0:n], op=ALU.add)),
                             reads=[pout, obt[b]], writes=[os_])
                        k.op(k.act, (lambda e, os_=os_, n=n: e.activation(sqo[:, 0:n], os_[:, 0:n], AF.Square)), reads=[os_], writes=[sqo])
                        k.op(k.pe, (lambda e, n=n: e.matmul(pbc[:, 0:n], lhsT=self.ones_bf[:], rhs=sqo[:, 0:n], start=True, stop=True)),
                             reads=[self.ones_bf, sqo], writes=[pbc])
                        k.op(k.act, (lambda e, n=n: e.activation(rs[:, 0:n], pbc[:, 0:n], AF.Ln, bias=self.eps_col[:, 0:1], scale=1.0 / 128)),
                             reads=[pbc, self.eps_col], writes=[rs])
                        k.op(k.act, (lambda e, n=n: e.activation(rs[:, 0:n], rs[:, 0:n], AF.Exp, scale=-0.5)), reads=[rs], writes=[rs])
                        k.op(k.dve, (lambda e, os_=os_, n=n: e.scalar_tensor_tensor(out=os_[:, 0:n], in0=os_[:, 0:n], scalar=onorm_col,
                                                                                   in1=rs[:, 0:n], op0=ALU.mult, op1=ALU.mult)),
                             reads=[os_, rs, self.csm], writes=[os_])
                        fo = fin[it % 2]
                        k.op(k.pool, (lambda e, os_=os_, fo=fo, b=b, n=n: e.tensor_tensor(out=fo[:, 0:n], in0=os_[:, 0:n], in1=gt[b][:, 0:n], op=ALU.mult)),
                             reads=[os_, gt[b]], writes=[fo])
                        k.dma(k.sp, self.oT[rows, t0:t0 + n], fo[:, 0:n], reads=[fo])
            uid = 0
            for ti in order:
                t0, n = tiles[ti]
                ncn = n // 64
                corder = list(range(ncn)) if d == 0 else list(range(ncn))[::-1]
                for h in range(8):
                    b = uid % NB
                    uid += 1
                    k.set_lane(b)
                    unit(h, b, uid, t0, n, ncn, corder)
                    k.set_lane(None)
                    if b == NB - 1:
                        k.merge()
            k.merge()
            k.end()

    def mixer_hgrn(self, li, nrow):
        self.hg_prep(li)
        self.phase_hg_proj(nrow)
        self.phase_hg_core(self.csm[:, 32:33])
        self.phase_out_proj(self.c_wo)


    def phase_gd_proj(self, j, nrow):
        k = self.k
        W = self.a_w[j]

        def setup(T0, tl):
            ctx = {}
            ctx["tm"] = [k.sb([128, 512], F32, "tmk") for _ in tl]
            for ti, (o, n) in enumerate(tl):
                k.dma(k.sp, ctx["tm"][ti][:, 0:n], self.tmask_rep[:, T0 + o:T0 + o + n], writes=[ctx["tm"][ti]])
            ctx["sg"] = [k.sb([128, 512], F32, "sg") for _ in range(2)]
            ctx["st"] = [k.sb([128, 512], BF16, "st") for _ in range(3)]
            ctx["z"] = [k.sb([128, 16], F32, "z") for _ in range(2)]
            ctx["bat"] = [k.sb([128, 32], F32, "bat") for _ in range(2)]
            na = k.sb([128, 16], F32, "negA")
            k.op(k.act, (lambda e: e.activation(na[:], self.asm[:, 0:16], AF.Exp)), reads=[self.asm], writes=[na])
            k.op(k.dve, (lambda e: e.tensor_scalar(na[:], na[:], -1.0, None, op0=ALU.mult)), reads=[na], writes=[na])
            ctx["negA"] = na
            ctx["i"] = 0
            return ctx

        def fm_pre(c):
            def post(ctx, pt, T0, ti, o, n):
                i = ctx["i"]
                ctx["i"] += 1
                st = ctx["st"][i % 3]
                k.op(k.dve, (lambda e: e.tensor_tensor(out=st[:, 0:n], in0=pt[:, 0:n], in1=ctx["tm"][ti][:, 0:n], op=ALU.mult)),
                     reads=[pt, ctx["tm"][ti]], writes=[st])
                k.dma(k.sp, self.pre[c * 128:(c + 1) * 128, T0 + o:T0 + o + n], st[:, 0:n], reads=[st])
            return post

        def fm_g(c):
            def post(ctx, pt, T0, ti, o, n):
                i = ctx["i"]
                ctx["i"] += 1
                st = ctx["st"][i % 3]
                k.op(k.act, (lambda e: e.activation(st[:, 0:n], pt[:, 0:n], AF.Silu)), reads=[pt], writes=[st])
                k.dma(k.sp, self.fmG[c * 128:(c + 1) * 128, T0 + o:T0 + o + n], st[:, 0:n], reads=[st])
            return post

        def tm_ba(ctx, pt, tok0, r):
            i = ctx["i"]
            ctx["i"] += 1
            z, bat = ctx["z"][i % 2], ctx["bat"][i % 2]
            blk = tok0 // 128
            assert tok0 % 128 == 0
            k.op(k.dve, (lambda e: e.tensor_tensor(out=z[0:r, :], in0=pt[0:r, 16:32], in1=self.asm[0:r, 16:32], op=ALU.add)),
                 reads=[pt, self.asm], writes=[z])
            k.op(k.act, (lambda e: e.activation(z[0:r, :], z[0:r, :], AF.Exp)), reads=[z], writes=[z])
            k.op(k.act, (lambda e: e.activation(z[0:r, :], z[0:r, :], AF.Ln, bias=self.one_col[0:r, 0:1])), reads=[z, self.one_col], writes=[z])
            k.op(k.dve, (lambda e: e.tensor_tensor(out=bat[0:r, 16:32], in0=z[0:r, :], in1=ctx["negA"][0:r, :], op=ALU.mult)),
                 reads=[z, ctx["negA"]], writes=[bat])
            k.op(k.act, (lambda e: e.activation(bat[0:r, 0:16], pt[0:r, 0:16], AF.Sigmoid)), reads=[pt], writes=[bat])
            k.op(k.dve, (lambda e: e.tensor_scalar(bat[0:r, 0:16], bat[0:r, 0:16], self.tmc[0:r, blk:blk + 1], None, op0=ALU.mult)),
                 reads=[bat, self.tmc], writes=[bat])
            k.dma(k.sp, self.ba_tm[tok0:tok0 + r, :], bat[0:r, :], reads=[bat])

        fm = []
        for c in range(24):
            fm.append((W[:, c * 128:(c + 1) * 128], fm_pre(c)))
        for c in range(8):
            fm.append((W[:, 3072 + c * 128:3072 + (c + 1) * 128], fm_g(c)))
        tm = [(W[:, 4096:4128], tm_ba, 32)]
        self.phase_proj(nrow, setup, fm, tm)

    def phase_gd_conv(self):
        k, T = self.k, self.T
        tiles = tiles_of(T, 512)
        QS = 128 ** -0.5
        k.begin()
        xin = [k.sb([128, 516], BF16, "cx") for _ in range(3)]
        acc = [k.sb([128, 512], F32, "cacc") for _ in range(2)]
        sv = [k.sb([128, 512], F32, "csv") for _ in range(2)]
        sq = [k.sb([128, 512], BF16, "csq") for _ in range(2)]
        rs = [k.sb([128, 512], F32, "crs") for _ in range(2)]
        ob = [k.sb([128, 512], BF16, "cob") for _ in range(3)]
        tmt = [k.sb([128, 512], F32, "ctm") for _ in range(2)]
        tb = [k.sb([128, 4, 128], BF16, "ctb") for _ in range(2)]
        pss = [k.ps([128, 512], F32, "cps") for _ in range(2)]
        ptb = [k.ps([128, 512], BF16, "cpt") for _ in range(2)]
        it = 0
        for ti, (t0, n) in enumerate(tiles):
            tmk = tmt[ti % 2]
            k.dma(k.sp, tmk[:, 0:n], self.tmask_rep[:, t0:t0 + n], writes=[tmk])
            for cc in range(24):
                kind = cc // 8
                x = xin[it % 3]
                a_, s_, q_, r_, o_ = acc[it % 2], sv[it % 2], sq[it % 2], rs[it % 2], ob[it % 3]
                ps = pss[it % 2]
                if it % 2 == 0:
                    k.merge()
                k.set_lane(it % 2)
                it += 1
                lo = max(t0 - 2, 0)
                hi = min(t0 + n + 2, T)
                if lo > t0 - 2 or hi < t0 + n + 2:
                    k.op(k.pool, (lambda e, x=x: e.memset(x[:], 0.0)), writes=[x])
                k.dma(k.sp, x[:, lo - (t0 - 2):hi - (t0 - 2)], self.pre[cc * 128:(cc + 1) * 128, lo:hi], writes=[x])
                wcol = lambda jj, cc=cc: self.asm[:, 33 + cc * 5 + jj:34 + cc * 5 + jj]
                k.op(k.dve, (lambda e, a_=a_, x=x, n=n, w=wcol(0): e.tensor_scalar(a_[:, 0:n], x[:, 0:n], w, None, op0=ALU.mult)),
                     reads=[x, self.asm], writes=[a_])
                for jj in range(1, 5):
                    k.op(k.dve, (lambda e, a_=a_, x=x, n=n, jj=jj, w=wcol(jj): e.scalar_tensor_tensor(
                        out=a_[:, 0:n], in0=x[:, jj:jj + n], scalar=w, in1=a_[:, 0:n], op0=ALU.mult, op1=ALU.add)),
                        reads=[x, a_, self.asm], writes=[a_])
                k.op(k.act, (lambda e, a_=a_, s_=s_, n=n: e.activation(s_[:, 0:n], a_[:, 0:n], AF.Silu)), reads=[a_], writes=[s_])
                if kind < 2:
                    k.op(k.act, (lambda e, s_=s_, q_=q_, n=n: e.activation(q_[:, 0:n], s_[:, 0:n], AF.Square)), reads=[s_], writes=[q_])
                    k.op(k.pe, (lambda e, ps=ps, q_=q_, n=n: e.matmul(ps[:, 0:n], lhsT=self.ones_bf[:], rhs=q_[:, 0:n], start=True, stop=True)),
                         reads=[self.ones_bf, q_], writes=[ps])
                    k.op(k.act, (lambda e, ps=ps, r_=r_, n=n: e.activation(r_[:, 0:n], ps[:, 0:n], AF.Ln, bias=self.eps_col[:, 0:1])),
                         reads=[ps, self.eps_col], writes=[r_])
                    k.op(k.act, (lambda e, r_=r_, n=n: e.activation(r_[:, 0:n], r_[:, 0:n], AF.Exp, scale=-0.5)), reads=[r_], writes=[r_])
                if kind == 0:
                    k.op(k.dve, (lambda e, o_=o_, s_=s_, r_=r_, n=n: e.scalar_tensor_tensor(
                        out=o_[:, 0:n], in0=s_[:, 0:n], scalar=QS, in1=r_[:, 0:n], op0=ALU.mult, op1=ALU.mult)),
                        reads=[s_, r_], writes=[o_])
                    k.dma(k.sp, self.fmA[cc * 128:(cc + 1) * 128, t0:t0 + n], o_[:, 0:n], reads=[o_])
                    k.set_lane(None)
                    continue
                if kind == 1:
                    k.op(k.dve, (lambda e, s_=s_, r_=r_, n=n: e.tensor_tensor(out=s_[:, 0:n], in0=s_[:, 0:n], in1=r_[:, 0:n], op=ALU.mult)),
                         reads=[s_, r_], writes=[s_])
                    k.op(k.pool, (lambda e, o_=o_, s_=s_, tmk=tmk, n=n: e.tensor_tensor(out=o_[:, 0:n], in0=s_[:, 0:n], in1=tmk[:, 0:n], op=ALU.mult)),
                         reads=[s_, tmk], writes=[o_])
                    k.dma(k.sp, self.fmB[0][(cc - 8) * 128:(cc - 7) * 128, t0:t0 + n], o_[:, 0:n], reads=[o_])
                    dst = self.tmK[0]
                else:
                    k.op(k.pool, (lambda e, o_=o_, s_=s_, n=n: e.tensor_copy(o_[:, 0:n], s_[:, 0:n])), reads=[s_], writes=[o_])
                    dst = self.v_tm
                cl = (cc % 8)
                pt = ptb[it % 2]
                tt = tb[it % 2]
                nb = (n + 127) // 128
                for b in range(nb):
                    r = min(128, n - b * 128)
                    k.op(k.pe, (lambda e, pt=pt, o_=o_, b=b, r=r: e.transpose(pt[0:r, b * 128:(b + 1) * 128], o_[:, b * 128:b * 128 + r],
                                                                             self.identb[:])),
                         reads=[o_, self.identb], writes=[pt])
                if n % 128 == 0:
                    k.op(k.act, (lambda e, pt=pt, tt=tt, nb=nb: e.copy(tt[:, 0:nb, :], pt[:, 0:nb * 128].rearrange("p (b d) -> p b d", d=128))),
                         reads=[pt], writes=[tt])
                    k.dma(k.sp, dst[t0:t0 + n, cl * 128:(cl + 1) * 128].rearrange("(b p) d -> p b d", p=128), tt[:, 0:nb, :], reads=[tt])
                else:
                    for b in range(nb):
                        r = min(128, n - b * 128)
                        k.op(k.act, (lambda e, pt=pt, tt=tt, b=b, r=r: e.copy(tt[0:r, b, :], pt[0:r, b * 128:(b + 1) * 128])),
                             reads=[pt], writes=[tt])
                        k.dma(k.sp, dst[t0 + b * 128:t0 + b * 128 + r, cl * 128:(cl + 1) * 128], tt[0:r, b, :], reads=[tt])
                k.set_lane(None)
        k.merge()
        k.end()

    def phase_gd_core(self):
        import os
        GDCUT = int(os.environ.get("GD_CUT", "99"))
        GDSUB = int(os.environ.get("GD_SUB", "99"))
        k, nc, T = self.k, self.nc, self.T
        tiles = tiles_of(T, 512)
        onorm_col = self.asm[:, 32:33]
        for sweep in (1, 0):
            d = sweep
            k.begin()
            tri = self.tri
            if d == 0:
                Uin, MS, MI, MN = tri[:, 0:64], 1, 0, 3
            else:
                Uin, MS, MI, MN = tri[:, 128:192], 3, 2, 1
            mI = k.sb([64, 512], F32, "mI")
            mS = k.sb([64, 512], F32, "mS")
            mN = k.sb([64, 512], F32, "mN")
            idr = k.sb([64, 512], F32, "idr")
            for c in range(8):
                cs = slice(c * 64, (c + 1) * 64)
                k.op(k.dve, (lambda e, cs=cs: e.tensor_copy(mI[:, cs], tri[:, MI * 64:(MI + 1) * 64])), reads=[tri], writes=[mI])
                k.op(k.dve, (lambda e, cs=cs: e.tensor_copy(mS[:, cs], tri[:, MS * 64:(MS + 1) * 64])), reads=[tri], writes=[mS])
                k.op(k.dve, (lambda e, cs=cs: e.tensor_copy(mN[:, cs], tri[:, MN * 64:(MN + 1) * 64])), reads=[tri], writes=[mN])
                k.op(k.dve, (lambda e, cs=cs: e.tensor_copy(idr[:, cs], self.ident[0:64, 0:64])), reads=[self.ident], writes=[idr])
            ones_f = k.sb([64, 128], F32, "onesf")
            k.op(k.dve, (lambda e: e.memset(ones_f[:], 1.0)), writes=[ones_f])
            S = [k.sb([128, 128], F32, "S") for _ in range(8)]
            Sb = [k.sb([128, 128], BF16, "Sb") for _ in range(8)]
            for h in range(8):
                k.op(k.dve, (lambda e, h=h: e.memset(S[h][:], 0.0)), writes=[S[h]])
                k.op(k.pool, (lambda e, h=h: e.memset(Sb[h][:], 0.0)), writes=[Sb[h]])
            NB = 2
            mk = lambda shp, dt, nm: [k.sb(shp, dt, nm) for _ in range(NB)]
            qT, kT = mk([128, 512], BF16, "qT"), mk([128, 512], BF16, "kT")
            ktm, vtm = mk([64, 8, 128], BF16, "ktm"), mk([64, 8, 128], BF16, "vtm")
            bat = mk([64, 8, 32], F32, "bat")
            lab = mk([64, 2, 8], F32, "lab")
            gc, egc, bek, dcol = mk([64, 8], F32, "gc"), mk([64, 8], F32, "egc"), mk([64, 8], F32, "bek"), mk([64, 8], F32, "dcol")
            alast = mk([128, 8], F32, "alast")
            dg = mk([64, 512], F32, "dg")
            egr = mk([128, 512], F32, "egr")
            qd = mk([128, 512], BF16, "qd")
            fab = mk([64, 512], F32, "fab")
            fm_ = mk([64, 512], F32, "fm")
            fmi = mk([64, 512], F32, "fmi")
            gf = mk([64, 512], F32, "gf")
            a32 = mk([64, 512], F32, "a32")
            Rb = [mk([64, 512], F32, "Rb%d" % i) for i in range(2)]
            Pb = [mk([64, 512], F32, "Pb%d" % i) for i in range(2)]
            PTb = [mk([64, 512], F32, "PTb%d" % i) for i in range(2)]
            rhu, rhw, kdec = mk([64, 8, 128], F32, "rhu"), mk([64, 8, 128], F32, "rhw"), mk([64, 8, 128], BF16, "kdec")
            gfu = mk([64, 512], F32, "gfu")
            dgb = mk([64, 512], F32, "dgb")
            u_sb = mk([64, 8, 128], F32, "u")
            nwT = mk([128, 512], BF16, "nwT")
            qkm = mk([64, 512], BF16, "qkm")
            vn = [[k.sb([64, 128], BF16, "vn") for _ in range(2)] for _ in range(NB)]
            ost = [k.sb([128, 512], F32, "ost") for _ in range(2)]
            B = [k.ps([128, 512], F32, "B%d" % i) for i in range(8)]
            if d == 0:
                obt = mk([128, 512], F32, "obt")
                gt = mk([128, 512], BF16, "gt")
                sqo_l = mk([128, 512], BF16, "sqo")
                rs_l = mk([128, 512], F32, "rs")
                fin = [k.sb([128, 512], BF16, "fin") for _ in range(2)]
            order = list(range(len(tiles)))
            if d == 1:
                order = order[::-1]

            def unit(h, b, it, t0, n, ncn, corder):
                if True:
                    rows = slice(h * 128, (h + 1) * 128)
                    X = B[4 * b:4 * b + 4]
                    Bs, Brow, BG, Bb = X
                    BD, BP, BT = X[1], X[2], X[3]
                    Bq, Bo, Bst, Bv = X[0], X[1], X[2], X[0]
                    k.dma(k.sp, qT[b][:, 0:n], self.fmA[rows, t0:t0 + n], writes=[qT[b]])
                    k.dma(k.sp, kT[b][:, 0:n], self.fmB[0][rows, t0:t0 + n], writes=[kT[b]])
                    k.dma(k.sp, ktm[b][:, 0:ncn, :], self.tmK[0][t0:t0 + n, rows].rearrange("(c p) e -> p c e", p=64), writes=[ktm[b]])
                    k.dma(k.sp, vtm[b][:, 0:ncn, :], self.v_tm[t0:t0 + n, rows].rearrange("(c p) e -> p c e", p=64), writes=[vtm[b]])
                    k.dma(k.sp, bat[b][:, 0:ncn, :], self.ba_tm[t0:t0 + n, :].rearrange("(c p) e -> p c e", p=64), writes=[bat[b]])
                    if d == 0:
                        k.dma(k.sp, obt[b][:, 0:n], self.obT[rows, t0:t0 + n], writes=[obt[b]])
                        k.dma(k.sp, gt[b][:, 0:n], self.fmG[rows, t0:t0 + n], writes=[gt[b]])
                    col = d * 8 + h
                    k.op(k.dve, (lambda e, b=b, ncn=ncn, col=col: e.tensor_copy(lab[b][:, 0, 0:ncn], bat[b][:, 0:ncn, col])),
                         reads=[bat[b]], writes=[lab[b]])
                    k.op(k.dve, (lambda e, b=b, ncn=ncn, col=col: e.tensor_copy(lab[b][:, 1, 0:ncn], bat[b][:, 0:ncn, 16 + col])),
                         reads=[bat[b]], writes=[lab[b]])
                    beta = lambda c, b=b: lab[b][:, 0, c:c + 1]
                    k.op(k.pe, (lambda e, b=b, ncn=ncn: e.matmul(Bs[0:64, 0:ncn], lhsT=Uin, rhs=lab[b][:, 1, 0:ncn], start=True, stop=True)),
                         reads=[tri, lab[b]], writes=[Bs])
                    k.op(k.pe, (lambda e, b=b, ncn=ncn: e.matmul(Bs[:, 16:16 + ncn], lhsT=ones_f[:], rhs=lab[b][:, 1, 0:ncn], start=True, stop=True)),
                         reads=[ones_f, lab[b]], writes=[Bs])
                    k.op(k.dve, (lambda e, b=b, ncn=ncn: e.tensor_copy(gc[b][:, 0:ncn], Bs[0:64, 0:ncn])), reads=[Bs], writes=[gc[b]])
                    k.op(k.act, (lambda e, b=b, ncn=ncn: e.activation(alast[b][:, 0:ncn], Bs[:, 16:16 + ncn], AF.Exp)), reads=[Bs], writes=[alast[b]])
                    k.op(k.dve, (lambda e, b=b, ncn=ncn: e.tensor_tensor(out=dcol[b][:, 0:ncn], in0=Bs[0:64, 16:16 + ncn], in1=gc[b][:, 0:ncn],
                                                                        op=ALU.subtract)), reads=[Bs, gc[b]], writes=[dcol[b]])
                    k.op(k.act, (lambda e, b=b, ncn=ncn: e.activation(dcol[b][:, 0:ncn], dcol[b][:, 0:ncn], AF.Exp)), reads=[dcol[b]], writes=[dcol[b]])
                    k.op(k.act, (lambda e, b=b, ncn=ncn: e.activation(egc[b][:, 0:ncn], gc[b][:, 0:ncn], AF.Exp)), reads=[gc[b]], writes=[egc[b]])
                    k.op(k.dve, (lambda e, b=b, ncn=ncn: e.tensor_tensor(out=bek[b][:, 0:ncn], in0=egc[b][:, 0:ncn], in1=lab[b][:, 0, 0:ncn],
                                                                        op=ALU.mult)), reads=[egc[b], lab[b]], writes=[bek[b]])
                    if GDCUT < 1:
                        return
                    for c in range(ncn):
                        cs = slice(c * 64, (c + 1) * 64)
                        k.op(k.dve, (lambda e, b=b, c=c, cs=cs: e.tensor_scalar(dg[b][:, cs], self.ident[0:64, 0:64], gc[b][:, c:c + 1], None,
                                                                               op0=ALU.mult)), reads=[self.ident, gc[b]], writes=[dg[b]])
                    for c in range(ncn):
                        cs = slice(c * 64, (c + 1) * 64)
                        k.op(k.pe, (lambda e, b=b, cs=cs: e.matmul(Brow[:, cs], lhsT=ones_f[:], rhs=dg[b][:, cs], start=True, stop=True)),
                             reads=[ones_f, dg[b]], writes=[Brow])
                    k.op(k.act, (lambda e, b=b, n=n: e.activation(egr[b][:, 0:n], Brow[:, 0:n], AF.Exp)), reads=[Brow], writes=[egr[b]])
                    k.op(k.dve, (lambda e, b=b, n=n: e.tensor_tensor(out=qd[b][:, 0:n], in0=qT[b][:, 0:n], in1=egr[b][:, 0:n], op=ALU.mult)),
                         reads=[qT[b], egr[b]], writes=[qd[b]])
                    for c in range(ncn):
                        cs = slice(c * 64, (c + 1) * 64)
                        k.op(k.dve, (lambda e, b=b, c=c, cs=cs: e.tensor_scalar(fab[b][:, cs], Brow[0:64, cs], gc[b][:, c:c + 1], None,
                                                                               op0=ALU.subtract)),
                             reads=[Brow, gc[b]], writes=[fab[b]])
                    k.op(k.act, (lambda e, b=b, n=n: e.activation(fab[b][:, 0:n], fab[b][:, 0:n], AF.Abs)), reads=[fab[b]], writes=[fab[b]])
                    k.op(k.act, (lambda e, b=b, n=n: e.activation(fm_[b][:, 0:n], fab[b][:, 0:n], AF.Exp, scale=-1.0)), reads=[fab[b]], writes=[fm_[b]])
                    k.op(k.pool, (lambda e, b=b, n=n: e.tensor_tensor(out=fmi[b][:, 0:n], in0=fm_[b][:, 0:n], in1=mI[:, 0:n], op=ALU.mult)),
                         reads=[fm_[b], mI], writes=[fmi[b]])
                    if GDCUT < 2:
                        return
                    for c in range(ncn):
                        cs = slice(c * 64, (c + 1) * 64)
                        k.op(k.pe, (lambda e, b=b, cs=cs: e.matmul(BG[0:64, cs], lhsT=kT[b][:, cs], rhs=kT[b][:, cs], start=True, stop=True)),
                             reads=[kT[b]], writes=[BG])
                        k.op(k.dve, (lambda e, b=b, c=c, cs=cs: e.tensor_scalar(dgb[b][:, cs], self.ident[0:64, 0:64], lab[b][:, 0, c:c + 1], None,
                                                                               op0=ALU.mult)), reads=[self.ident, lab[b]], writes=[dgb[b]])
                    for c in range(ncn):
                        cs = slice(c * 64, (c + 1) * 64)
                        k.op(k.pe, (lambda e, b=b, cs=cs: e.matmul(Bb[:, cs], lhsT=ones_f[:], rhs=dgb[b][:, cs], start=True, stop=True)),
                             reads=[ones_f, dgb[b]], writes=[Bb])
                    k.op(k.dve, (lambda e, b=b, n=n: e.tensor_tensor(out=gf[b][:, 0:n], in0=BG[0:64, 0:n], in1=fm_[b][:, 0:n], op=ALU.mult)),
                         reads=[BG, fm_[b]], writes=[gf[b]])
                    k.op(k.pool, (lambda e, b=b, n=n: e.tensor_tensor(out=gfu[b][:, 0:n], in0=gf[b][:, 0:n], in1=mN[:, 0:n], op=ALU.mult)),
                         reads=[gf[b], mN], writes=[gfu[b]])
                    k.op(k.pool, (lambda e, b=b, n=n: e.tensor_tensor(out=gf[b][:, 0:n], in0=gf[b][:, 0:n], in1=mS[:, 0:n], op=ALU.mult)),
                         reads=[gf[b], mS], writes=[gf[b]])
                    R0, P0, PT0 = Rb[0][b], Pb[0][b], PTb[0][b]
                    for c in range(ncn):
                        cs = slice(c * 64, (c + 1) * 64)
                        k.op(k.dve, (lambda e, b=b, c=c, cs=cs, PT0=PT0: e.tensor_scalar(PT0[:, cs], gf[b][:, cs], lab[b][:, 0, c:c + 1], None, op0=ALU.mult)),
                             reads=[gf[b], lab[b]], writes=[PT0])
                    k.op(k.dve, (lambda e, b=b, n=n, P0=P0: e.tensor_tensor(out=P0[:, 0:n], in0=Bb[0:64, 0:n], in1=gfu[b][:, 0:n], op=ALU.mult)),
                         reads=[Bb, gfu[b]], writes=[P0])
                    k.op(k.pool, (lambda e, n=n, R0=R0, P0=P0: e.tensor_tensor(out=R0[:, 0:n], in0=idr[:, 0:n], in1=P0[:, 0:n], op=ALU.subtract)),
                         reads=[P0, idr], writes=[R0])
                    if GDCUT < 3:
                        return
                    for lv in range(6):
                        cur, nxt = lv % 2, (lv + 1) % 2
                        Rc, Pc, PTc = Rb[cur][b], Pb[cur][b], PTb[cur][b]
                        Rn, Pn, PTn = Rb[nxt][b], Pb[nxt][b], PTb[nxt][b]
                        for c in range(ncn):
                            cs = slice(c * 64, (c + 1) * 64)
                            if lv >= 1:
                                k.op(k.pe, (lambda e, cs=cs, PTc=PTc, Rc=Rc: e.matmul(BD[0:64, cs], lhsT=PTc[:, cs], rhs=Rc[:, cs], start=True, stop=True)),
                                     reads=[PTc, Rc], writes=[BD])
                            if lv <= 4:
                                k.op(k.pe, (lambda e, cs=cs, PTc=PTc, Pc=Pc: e.matmul(BP[0:64, cs], lhsT=PTc[:, cs], rhs=Pc[:, cs], start=True, stop=True)),
                                     reads=[PTc, Pc], writes=[BP])
                                k.op(k.pe, (lambda e, cs=cs, PTc=PTc, Pc=Pc: e.matmul(BT[0:64, cs], lhsT=Pc[:, cs], rhs=PTc[:, cs], start=True, stop=True)),
                                     reads=[PTc, Pc], writes=[BT])
                        if lv >= 1:
                            k.op(k.dve, (lambda e, n=n, Rn=Rn, Rc=Rc: e.tensor_tensor(out=Rn[:, 0:n], in0=BD[0:64, 0:n], in1=Rc[:, 0:n], op=ALU.add)),
                                 reads=[BD, Rc], writes=[Rn])
                        else:
                            k.op(k.pool, (lambda e, n=n, Rn=Rn, Rc=Rc: e.tensor_copy(Rn[:, 0:n], Rc[:, 0:n])), reads=[Rc], writes=[Rn])
                        if lv <= 4:
                            k.op(k.act, (lambda e, n=n, Pn=Pn: e.copy(Pn[:, 0:n], BP[0:64, 0:n])), reads=[BP], writes=[Pn])
                            k.op(k.act, (lambda e, n=n, PTn=PTn: e.copy(PTn[:, 0:n], BT[0:64, 0:n])), reads=[BT], writes=[PTn])
                    if GDCUT < 4:
                        return
                    Rf = Rb[0][b]
                    for c in range(ncn):
                        k.op(k.pool, (lambda e, b=b, c=c: e.tensor_scalar(rhu[b][:, c, :], vtm[b][:, c, :], lab[b][:, 0, c:c + 1], None, op0=ALU.mult)),
                             reads=[vtm[b], lab[b]], writes=[rhu[b]])
                        k.op(k.pool, (lambda e, b=b, c=c: e.tensor_scalar(rhw[b][:, c, :], ktm[b][:, c, :], bek[b][:, c:c + 1], None, op0=ALU.mult)),
                             reads=[ktm[b], bek[b]], writes=[rhw[b]])
                        k.op(k.pool, (lambda e, b=b, c=c: e.tensor_scalar(kdec[b][:, c, :], ktm[b][:, c, :], dcol[b][:, c:c + 1], None, op0=ALU.mult)),
                             reads=[ktm[b], dcol[b]], writes=[kdec[b]])
                    for c in range(ncn):
                        cs = slice(c * 64, (c + 1) * 64)
                        pu = BD if c < 4 else BP
                        us = slice((c % 4) * 128, (c % 4 + 1) * 128)
                        k.op(k.pe, (lambda e, b=b, c=c, cs=cs, pu=pu, us=us, Rf=Rf: e.matmul(pu[0:64, us], lhsT=Rf[:, cs], rhs=rhu[b][:, c, :], start=True, stop=True)),
                             reads=[Rf, rhu[b]], writes=[pu])
                        k.op(k.pe, (lambda e, b=b, c=c, cs=cs, Rf=Rf: e.matmul(BT[:, cs], lhsT=rhw[b][:, c, :], rhs=Rf[:, cs], start=True, stop=True)),
                             reads=[Rf, rhw[b]], writes=[BT])
                        k.op(k.pe, (lambda e, b=b, cs=cs: e.matmul(Bq[0:64, cs], lhsT=kT[b][:, cs], rhs=qT[b][:, cs], start=True, stop=True)),
                             reads=[kT[b], qT[b]], writes=[Bq])
                    n4 = min(ncn, 4)
                    k.op(k.act, (lambda e, b=b, n4=n4: e.copy(u_sb[b][:, 0:n4, :], BD[0:64, 0:n4 * 128].rearrange("p (c d) -> p c d", d=128))),
                         reads=[BD], writes=[u_sb[b]])
                    if ncn > 4:
                        k.op(k.act, (lambda e, b=b, ncn=ncn: e.copy(u_sb[b][:, 4:ncn, :], BP[0:64, 0:(ncn - 4) * 128].rearrange("p (c d) -> p c d", d=128))),
                             reads=[BP], writes=[u_sb[b]])
                    k.op(k.dve, (lambda e, b=b, n=n: e.tensor_scalar(nwT[b][:, 0:n], BT[:, 0:n], -1.0, None, op0=ALU.mult)), reads=[BT], writes=[nwT[b]])
                    k.op(k.dve, (lambda e, b=b, n=n: e.tensor_tensor(out=qkm[b][:, 0:n], in0=Bq[0:64, 0:n], in1=fmi[b][:, 0:n], op=ALU.mult)),
                         reads=[Bq, fmi[b]], writes=[qkm[b]])
                    if GDCUT < 5:
                        return
                    for c in corder:
                        cs = slice(c * 64, (c + 1) * 64)
                        v_ = vn[b][c % 2]
                        k.op(k.pe, (lambda e, b=b, cs=cs, h=h: e.matmul(Bv[0:64, 256:384], lhsT=nwT[b][:, cs], rhs=Sb[h][:], start=True, stop=True)),
                             reads=[nwT[b], Sb[h]], writes=[Bv])
                        k.op(k.dve, (lambda e, b=b, c=c, v_=v_: e.tensor_tensor(out=v_[:], in0=Bv[0:64, 256:384], in1=u_sb[b][:, c, :], op=ALU.add)),
                             reads=[Bv, u_sb[b]], writes=[v_])
                        k.op(k.pe, (lambda e, b=b, cs=cs, h=h: e.matmul(Bo[:, cs], lhsT=Sb[h][:], rhs=qd[b][:, cs], start=True, stop=False)),
                             reads=[Sb[h], qd[b]], writes=[Bo])
                        k.op(k.pe, (lambda e, b=b, cs=cs, v_=v_: e.matmul(Bo[:, cs], lhsT=v_[:], rhs=qkm[b][:, cs], start=False, stop=True)),
                             reads=[v_, qkm[b]], writes=[Bo])
                        k.op(k.pe, (lambda e, b=b, c=c, v_=v_: e.matmul(Bs[:, 128:256], lhsT=kdec[b][:, c, :], rhs=v_[:], start=True, stop=True)),
                             reads=[kdec[b], v_], writes=[Bs])
                        k.op(k.dve, (lambda e, b=b, h=h, c=c: e.scalar_tensor_tensor(
                            out=S[h][:], in0=S[h][:], scalar=alast[b][:, c:c + 1], in1=Bs[:, 128:256], op0=ALU.mult, op1=ALU.add)),
                            reads=[S[h], alast[b], Bs], writes=[S[h]])
                        k.op(k.act, (lambda e, h=h: e.copy(Sb[h][:], S[h][:])), reads=[S[h]], writes=[Sb[h]])
                    if GDCUT < 6:
                        return
                    os_ = ost[it % 2]
                    if d == 1:
                        k.op(k.act, (lambda e, os_=os_, n=n: e.copy(os_[:, 0:n], Bo[:, 0:n])), reads=[Bo], writes=[os_])
                        k.dma(k.sp, self.obT[rows, t0:t0 + n], os_[:, 0:n], reads=[os_])
                    else:
                        sqo, rs = sqo_l[b], rs_l[b]
                        k.op(k.dve, (lambda e, os_=os_, b=b, n=n: e.tensor_tensor(out=os_[:, 0:n], in0=Bo[:, 0:n], in1=obt[b][:, 0:n], op=ALU.add)),
                             reads=[Bo, obt[b]], writes=[os_])
                        k.op(k.act, (lambda e, os_=os_, n=n: e.activation(sqo[:, 0:n], os_[:, 0:n], AF.Square)), reads=[os_], writes=[sqo])
                        k.op(k.pe, (lambda e, n=n: e.matmul(Bst[:, 0:n], lhsT=self.ones_bf[:], rhs=sqo[:, 0:n], start=True, stop=True)),
                             reads=[self.ones_bf, sqo], writes=[Bst])
                        k.op(k.act, (lambda e, n=n: e.activation(rs[:, 0:n], Bst[:, 0:n], AF.Ln, bias=self.eps_col[:, 0:1], scale=1.0 / 128)),
                             reads=[Bst, self.eps_col], writes=[rs])
                        k.op(k.act, (lambda e, n=n: e.activation(rs[:, 0:n], rs[:, 0:n], AF.Exp, scale=-0.5)), reads=[rs], writes=[rs])
                        k.op(k.dve, (lambda e, os_=os_, n=n: e.scalar_tensor_tensor(out=os_[:, 0:n], in0=os_[:, 0:n], scalar=onorm_col,
                                                                                   in1=rs[:, 0:n], op0=ALU.mult, op1=ALU.mult)),
                             reads=[os_, rs, self.asm], writes=[os_])
                        fo = fin[it % 2]
                        k.op(k.pool, (lambda e, os_=os_, fo=fo, b=b, n=n: e.tensor_tensor(out=fo[:, 0:n], in0=os_[:, 0:n], in1=gt[b][:, 0:n], op=ALU.mult)),
                             reads=[os_, gt[b]], writes=[fo])
                        k.dma(k.sp, self.oT[rows, t0:t0 + n], fo[:, 0:n], reads=[fo])
            uid = 0
            for ti in order:
                t0, n = tiles[ti]
                ncn = n // 64
                corder = list(range(ncn)) if d == 0 else list(range(ncn))[::-1]
                for h in range(8):
                    b = uid % NB
                    uid += 1
                    k.set_lane(b)
                    unit(h, b, uid, t0, n, ncn, corder)
                    k.set_lane(None)
                    if b == NB - 1:
                        k.merge()
            k.merge()
            if d == 1 and os.environ.get("DEBUG_DUMP"):
                o = self.nc.dram_tensor("dbg_S0", [128, 128], F32, kind="ExternalOutput").ap()
                k.dma(k.sp, o, S[0][:], reads=[S[0]])
                o = self.nc.dram_tensor("dbg_u", [64, 8 * 128], F32, kind="ExternalOutput").ap()
                k.dma(k.sp, o, u_sb[0][:].rearrange("p c d -> p (c d)"), reads=[u_sb[0]])
                o = self.nc.dram_tensor("dbg_rhu", [64, 8 * 128], BF16, kind="ExternalOutput").ap()
                k.dma(k.sp, o, rhu[0][:].rearrange("p c d -> p (c d)"), reads=[rhu[0]])
                o = self.nc.dram_tensor("dbg_R", [64, 512], BF16, kind="ExternalOutput").ap()
                k.dma(k.sp, o, Rb[0][0][:], reads=[Rb[0][0]])
                o = self.nc.dram_tensor("dbg_alast", [128, 8], F32, kind="ExternalOutput").ap()
                k.dma(k.sp, o, alast[0][:], reads=[alast[0]])
            k.end()

    def mixer_gdn(self, j, nrow):
        k = self.k
        k.begin()
        k.dma(k.sp, self.asm[:], self.a_small[j], writes=[self.asm])
        k.end()
        import os
        st = int(os.environ.get("GD_STAGE", "9"))
        self.phase_gd_proj(j, nrow)
        if st >= 2:
            self.phase_gd_conv()
        if st >= 3:
            self.phase_gd_core()
        if st >= 4:
            self.phase_out_proj(self.a_wo[j])

    def build(self):
        k = self.k
        self.eps_col = k.gsb([128, 1], F32, "epscol")
        k.begin()
        k.op(k.dve, lambda e: e.memset(self.eps_col[:], EPS), writes=[self.eps_col])
        self.bsm = k.gsb([128, 260], F32, "bsm")
        k.dma(k.sp, self.bsm[:], self.b_small, writes=[self.bsm])
        self.csm = k.gsb([128, 33], F32, "csm")
        k.dma(k.sp, self.csm[:], self.c_small, writes=[self.csm])
        self.tri = k.gsb([64, 256], F32, "tric")
        k.dma(k.sp, self.tri[:], self.tri_in, writes=[self.tri])
        self.tmc = k.gsb([128, (self.T + 127) // 128], F32, "tmc")
        k.dma(k.sp, self.tmc[:], self.tmask_col, writes=[self.tmc])
        self.one_col = k.gsb([128, 1], F32, "onecol")
        k.op(k.dve, lambda e: e.memset(self.one_col[:], 1.0), writes=[self.one_col])
        self.lb_col = k.gsb([128, 16], F32, "lbcol")
        self.asm = k.gsb([128, 153], F32, "asm")
        self.identb = k.gsb([128, 128], BF16, "identbc")
        k.dma(k.sp, self.identb[:], self.identb_in, writes=[self.identb])
        self.oml_row = k.gsb([128, D_MODEL], F32, "omlrow")
        k.end()
        self.phase_in()
        if self.only == "gdn":
            self.mixer_gdn(0, 0 * 3 + 1)
            self.phase_out(self.depth * 3)
            import os
            if os.environ.get("DEBUG_DUMP"):
                k.begin()
                dummy = k.sb([128, 4], F32, "dummy")
                for nm, ap, shp, dt in (("fmA", self.fmA, [D_MODEL, self.T], BF16), ("fmB0", self.fmB[0], [D_MODEL, self.T], BF16),
                                        ("tmK0", self.tmK[0], [self.T, D_MODEL], BF16), ("v_tm", self.v_tm, [self.T, D_MODEL], BF16),
                                        ("ba_tm", self.ba_tm, [self.T, 32], F32), ("obT", self.obT, [D_MODEL, self.T], F32),
                                        ("oT", self.oT, [D_MODEL, self.T], BF16), ("fmG", self.fmG, [D_MODEL, self.T], BF16)):
                    o = self.nc.dram_tensor("dbg_" + nm, shp, dt, kind="ExternalOutput").ap()
                    k.dma(k.sp, o, ap, reads=[dummy])
                k.end()
            return
        if self.only == "hgrn":
            self.mixer_hgrn(2, 2 * 3 + 1)
            self.phase_out(self.depth * 3)
            return
        if self.only == "attn":
            self.mixer_attn(1 * 3 + 1)
            self.phase_out(self.depth * 3)
            return
        for li in range(self.depth):
            if self.ffn:
                self.phase_ffn(li, 0, li * 3 + 0)
            if self.mixers and li % 3 == 1:
                self.mixer_attn(li * 3 + 1)
            if self.mixers and li % 3 == 2:
                self.mixer_hgrn(li, li * 3 + 1)
            if self.mixers and li % 3 == 0:
                self.mixer_gdn(li // 3, li * 3 + 1)
            if self.ffn:
                self.phase_ffn(li, 1, li * 3 + 2)
        self.phase_out(self.depth * 3)


def col_layout(a):
    a = np.asarray(a, np.float32)
    R = a.shape[0]
    C = a.shape[1] // 128
    return np.ascontiguousarray(a.reshape(R, C, 128).transpose(2, 0, 1).reshape(128, R * C))


def attn_host_inputs(b_w_in, b_lambda, b_sub_norm, layer_idx, T, n_valid):
    f32 = np.float32
    w = np.asarray(b_w_in, f32)
    perm = np.arange(2048)
    d = perm % 64
    perm = np.where(d < 8, perm + 8, np.where(d < 16, perm - 8, perm))
    w_sw = np.ascontiguousarray(w[:, perm])
    sm = np.zeros((128, 260), f32)
    sm[:, 0:256] = np.asarray(b_lambda, f32).reshape(1, 256)
    sm[:, 256] = np.asarray(b_sub_norm, f32).reshape(128)
    dd = np.arange(128) % 64
    ii = np.where(dd < 8, dd, dd - 8).astype(np.float64)
    invf = np.exp(-np.log(500000.0) * ii / 8.0)
    sm[:, 257] = np.where(dd < 16, invf, 0.0)
    sm[:, 258] = np.where(dd < 8, -1.0, np.where(dd < 16, 1.0, 0.0))
    sm[:, 259] = 0.8 - 0.6 * np.exp(-0.3 * layer_idx)
    nkt = (T + 127) // 128
    tok = np.arange(nkt * 128)
    kb = np.where(tok < n_valid, 0.0, -30000.0).astype(f32).reshape(nkt, 128).T
    return w_sw, sm, np.ascontiguousarray(kb)


def common_host_inputs(T, n_valid):
    f32 = np.float32
    j = np.arange(64)[:, None]
    i = np.arange(64)[None, :]
    tri = np.concatenate([(j <= i), (j > i), (j >= i), (j < i)], axis=1).astype(f32)
    tok = np.arange(T)
    tm = (tok < n_valid).astype(f32)
    tmask_rep = np.ascontiguousarray(np.broadcast_to(tm[None, :], (128, T)))
    nb = (T + 127) // 128
    tmp = np.zeros(nb * 128, f32)
    tmp[:T] = tm
    tmask_col = np.ascontiguousarray(tmp.reshape(nb, 128).T)
    return {"tri": tri, "tmask_rep": tmask_rep, "tmask_col": tmask_col, "ident": np.eye(128, dtype=f32)}


def hgrn_host_inputs(c_lb_logits, c_o_norm):
    sm = np.zeros((128, 33), np.float32)
    sm[:, 0:32] = col_layout(np.asarray(c_lb_logits, np.float32))
    sm[:, 32] = np.asarray(c_o_norm, np.float32).reshape(128)
    return sm


def gdn_host_inputs(a_log, a_dt_bias, a_o_norm, a_conv_w):
    f32 = np.float32
    n_a = np.asarray(a_log).shape[0]
    out = np.zeros((n_a, 128, 153), f32)
    for j in range(n_a):
        out[j, :, 0:16] = np.asarray(a_log[j], f32).reshape(1, 16)
        out[j, :, 16:32] = np.asarray(a_dt_bias[j], f32).reshape(1, 16)
        out[j, :, 32] = np.asarray(a_o_norm[j], f32).reshape(128)
        cw = np.asarray(a_conv_w[j], f32)
        out[j, :, 33:153] = cw.reshape(5, 24, 128).transpose(2, 1, 0).reshape(128, 120)
    return out


_PROG_CACHE = {}


def get_prog(T, depth, n_groups):
    key = (T, depth, n_groups)
    if key not in _PROG_CACHE:
        _PROG_CACHE[key] = Prog(T, depth, n_groups)
    return _PROG_CACHE[key]


T_FULL = 8256
SEQ_P = 8192
SEQ_S = 4096


def kernel(x_prompt, x_sample, meta_tokens, norm_w, ffn_w_up, ffn_w_down,
           a_w_in, a_conv_w, a_log, a_dt_bias, a_o_norm, a_w_out,
           b_w_in, b_lambda, b_sub_norm, b_w_out,
           c_w_in, c_lb_logits, c_o_norm, c_w_out, final_norm):
    import ml_dtypes
    depth = norm_w.shape[0]
    T = T_FULL
    prog = get_prog(T, depth, 4)
    f32 = np.float32
    seqs = [x_prompt[0], x_prompt[1], x_sample[0], x_sample[1], x_sample[2], x_sample[3], x_sample[0], x_sample[1]]
    nw = np.concatenate([np.asarray(norm_w, f32).reshape(depth * 3, D_MODEL), np.asarray(final_norm, f32)[None]], 0)
    nw = col_layout(nw)
    shared = {
        "norm_w": nw,
        "ffn_w_up": np.asarray(ffn_w_up, f32), "ffn_w_down": np.asarray(ffn_w_down, f32),
        "b_w_in": np.asarray(b_w_in[0], f32), "b_w_out": np.asarray(b_w_out[0], f32),
        "c_w_in": np.asarray(c_w_in[0], f32), "c_w_out": np.asarray(c_w_out[0], f32),
        "c_small": hgrn_host_inputs(c_lb_logits, c_o_norm[0]),
        "a_w_in": np.asarray(a_w_in, f32), "a_w_out": np.asarray(a_w_out, f32),
        "a_small": gdn_host_inputs(a_log, a_dt_bias, a_o_norm, a_conv_w),
        "identb": np.eye(128, dtype=f32).astype(ml_dtypes.bfloat16),
    }
    in_maps = []
    per_len = {}
    for s in seqs:
        L = s.shape[0]
        nv = N_META + L
        if L not in per_len:
            w_sw, sm, kb = attn_host_inputs(b_w_in[0], b_lambda[0], b_sub_norm[0], 1, T, nv)
            d = {"b_w_sw": w_sw, "b_small": sm, "kbias": kb}
            d.update(common_host_inputs(T, nv))
            per_len[L] = d
        xin = np.zeros((T, D_MODEL), f32)
        xin[:N_META] = meta_tokens
        xin[N_META:nv] = s
        m = {"xin": xin}
        m.update(shared)
        m.update(per_len[L])
        in_maps.append(m)
    res = run_bass_kernel_spmd(prog.nc, in_maps, core_ids=list(range(8)))
    outs = [r["yout"] for r in res.results]
    y_prompt = np.stack([outs[0][N_META:N_META + SEQ_P], outs[1][N_META:N_META + SEQ_P]], 0).astype(f32)
    y_sample = np.stack([outs[i][N_META:N_META + SEQ_S] for i in range(2, 6)], 0).astype(f32)
    return (y_prompt, y_sample)
```

```python
import numpy as np
from contextlib import ExitStack
import concourse.bass as bass
import concourse.mybir as mybir
from concourse.bass_utils import run_bass_kernel_spmd

F32 = mybir.dt.float32
BF16 = mybir.dt.bfloat16
ALU = mybir.AluOpType
AF = mybir.ActivationFunctionType
AX = mybir.AxisListType

D_MODEL = 1024
D_FF = 2816
N_META = 16
EPS = 1e-6


class Eng:
    def __init__(self, name, sem):
        self.name = name
        self.sem = sem
        self.cnt = 0
        self.seen = {}
        self.ops = []


class Tl:
    def __init__(self, t, name):
        self.t = t
        self.name = name
        self.w = None
        self.r = {}
        self.dsem = None

    def __getitem__(self, idx):
        return self.t[idx]

    def view(self):
        return Tl(self.t, self.name)


class KB:
    SAME_ENGINE_SYNC = True

    def __init__(self, nc, n_dma_sems=84):
        self.nc = nc
        self.es = ExitStack()
        self.engs = {}
        for name in ("pe", "act", "dve", "pool", "sp"):
            sem = self.es.enter_context(nc.semaphore("s_" + name))
            self.engs[name] = Eng(name, sem)
        self.pe, self.act, self.dve, self.pool, self.sp = (self.engs[n] for n in ("pe", "act", "dve", "pool", "sp"))
        self.dma_pool = []
        for i in range(n_dma_sems):
            sem = self.es.enter_context(nc.semaphore("s_d%d" % i))
            self.dma_pool.append([sem, 0])
        self.dma_free = list(range(n_dma_sems))
        self.phase_tiles = []
        self.dma_tiles = []
        self.lanes = {}
        self.marks = []
        self.cur_lane = None
        self.pes = None
        self.uid = 0
        self.nops = 0

    def begin(self):
        self.pes = ExitStack()
        self.phase_tiles = []

    def sb(self, shape, dt, name=None):
        self.uid += 1
        name = "%s_%d" % (name or "t", self.uid)
        t = self.pes.enter_context(self.nc.sbuf_tensor(name, list(shape), dt))
        tl = Tl(t, name)
        self.phase_tiles.append(tl)
        return tl

    def views(self, tl, n):
        vs = [tl.view() for _ in range(n)]
        self.phase_tiles.extend(vs)
        return vs

    def ps(self, shape, dt=F32, name=None):
        self.uid += 1
        name = "%s_%d" % (name or "p", self.uid)
        t = self.pes.enter_context(self.nc.psum_tensor(name, list(shape), dt))
        tl = Tl(t, name)
        self.phase_tiles.append(tl)
        return tl

    def gsb(self, shape, dt, name):
        t = self.es.enter_context(self.nc.sbuf_tensor(name, list(shape), dt))
        return Tl(t, name)

    def _deps(self, eng, reads, writes):
        deps = []
        for tl in reads:
            if tl.w is not None:
                deps.append(tl.w)
        for tl in writes:
            if tl.w is not None and tl.w[0] != eng.name:
                deps.append(tl.w)
            deps.extend(v for v in tl.r.values() if v[0] != eng.name)
        waits = []
        for key, sem, cnt in deps:
            if key == eng.name and (eng.name in ("pe", "sp") or not self.SAME_ENGINE_SYNC):
                continue
            if eng.seen.get(key, 0) < cnt:
                eng.seen[key] = cnt
                waits.append((sem, cnt))
        return waits

    def set_lane(self, lane):
        self.cur_lane = lane

    def merge(self):
        lanes = [v for _, v in sorted(self.lanes.items()) if v]
        self.lanes = {}
        save, self.cur_lane = self.cur_lane, None
        idx = [0] * len(lanes)
        left = sum(len(l) for l in lanes)
        while left:
            for li, l in enumerate(lanes):
                if idx[li] < len(l):
                    rec = l[idx[li]]
                    idx[li] += 1
                    left -= 1
                    if rec[0] == "op":
                        self.op(*rec[1:])
                    else:
                        self.dma(rec[1], rec[2], rec[3], rec[4], rec[5], **rec[6])
        self.cur_lane = save

    def mark(self):
        self.marks.append(len(self.lanes.get(self.cur_lane, [])))

    def emit_recs(self, recs):
        for rec in recs:
            if rec[0] == "op":
                self.op(*rec[1:])
            else:
                self.dma(rec[1], rec[2], rec[3], rec[4], rec[5], **rec[6])

    def emit_interleaved(self, a, b):
        save, self.cur_lane = self.cur_lane, None
        na, nb = len(a), len(b)
        ia = ib = 0
        while ia < na or ib < nb:
            if ib >= nb or (ia < na and ia * nb <= ib * na):
                self.emit_recs([a[ia]])
                ia += 1
            else:
                self.emit_recs([b[ib]])
                ib += 1
        self.cur_lane = save

    def op(self, eng, fn, reads=(), writes=()):
        if self.cur_lane is not None:
            self.lanes.setdefault(self.cur_lane, []).append(("op", eng, fn, tuple(reads), tuple(writes)))
            return
        waits = self._deps(eng, reads, writes)
        eng.cnt += 1
        me = (eng.name, eng.sem, eng.cnt)
        eng.ops.append((waits, fn, (eng.sem, 1)))
        for tl in writes:
            tl.w = me
            tl.r = {}
        for tl in reads:
            if tl not in writes:
                tl.r[eng.name] = me
        self.nops += 1

    def dma(self, q, out, in_, reads=(), writes=(), **kw):
        if self.cur_lane is not None:
            self.lanes.setdefault(self.cur_lane, []).append(("dma", q, out, in_, tuple(reads), tuple(writes), kw))
            return
        tl = (list(writes) + list(reads))[0]
        if tl.dsem is None:
            tl.dsem = self.dma_free.pop()
            self.dma_tiles.append(tl)
        slot = self.dma_pool[tl.dsem]
        waits = self._deps(q, reads, writes)
        slot[1] += 16
        key = "d%d" % tl.dsem
        me = (key, slot[0], slot[1])
        q.ops.append((waits, (lambda e, o=out, i=in_, k=kw: e.dma_start(out=o, in_=i, **k)), (slot[0], 16)))
        for t in writes:
            t.w = me
            t.r = {}
        for t in reads:
            t.r[key] = me
        self.nops += 1

    def end(self):
        used = set()
        for tl in self.dma_tiles:
            if tl.dsem is not None:
                used.add(tl.dsem)
        for d in sorted(used):
            sem, cnt = self.dma_pool[d]
            key = "d%d" % d
            if self.sp.seen.get(key, 0) < cnt:
                self.sp.seen[key] = cnt
                self.sp.ops.append(([(sem, cnt)], None, None))
        for e in (self.pe, self.act, self.dve, self.pool):
            if self.sp.seen.get(e.name, 0) < e.cnt:
                self.sp.seen[e.name] = e.cnt
                self.sp.ops.append(([(e.sem, e.cnt)], None, None))
        self.sp.cnt += 1
        self.sp.ops.append(([], (lambda e: e.nop()), (self.sp.sem, 1)))
        for e in (self.pe, self.act, self.dve, self.pool):
            e.ops.append(([(self.sp.sem, self.sp.cnt)], None, None))
        with self.nc.Block() as block:
            for name, deco in (("pe", block.tensor), ("act", block.scalar), ("dve", block.vector),
                               ("pool", block.gpsimd), ("sp", block.sync)):
                eng = self.engs[name]
                ops = eng.ops
                eng.ops = []

                def body(e, ops=ops):
                    for waits, fn, inc in ops:
                        for sem, cnt in waits:
                            e.wait_ge(sem, cnt)
                        if fn is not None:
                            ins = fn(e)
                            if inc is not None:
                                ins.then_inc(inc[0], inc[1])

                deco(body)
        for d in used:
            self.dma_free.append(d)
        for tl in self.dma_tiles:
            tl.dsem = None
        self.dma_tiles = []
        for e in self.engs.values():
            for o in self.engs.values():
                e.seen[o.name] = o.cnt
            for d in range(len(self.dma_pool)):
                e.seen["d%d" % d] = self.dma_pool[d][1]
        self.pes.close()
        self.pes = None

    def close(self):
        self.es.close()


def tiles_of(n, step=512):
    out = []
    s = 0
    while s < n:
        out.append((s, min(step, n - s)))
        s += step
    return out


class Prog:
    def __init__(self, T, depth, n_groups, mixers=True, ffn=True, only=None):
        self.only = only
        self.mixers = mixers
        self.ffn = ffn
        self.T = T
        self.depth = depth
        self.n_groups = n_groups
        base = (T // n_groups) // 512 * 512 if n_groups > 1 else T
        self.groups = [(g * base, base) for g in range(n_groups - 1)]
        self.groups.append(((n_groups - 1) * base, T - (n_groups - 1) * base))
        nc = bass.Bass("TRN2", target_bir_lowering=False)
        self.nc = nc
        d = nc.dram_tensor
        self.xin = d("xin", [T, D_MODEL], F32, kind="ExternalInput").ap()
        self.yout = d("yout", [T, D_MODEL], F32, kind="ExternalOutput").ap()
        self.norm_w = d("norm_w", [128, (depth * 3 + 1) * 8], F32, kind="ExternalInput").ap()
        self.w_up = d("ffn_w_up", [depth, 2, D_MODEL, 2 * D_FF], F32, kind="ExternalInput").ap()
        self.w_dn = d("ffn_w_down", [depth, 2, D_FF, D_MODEL], F32, kind="ExternalInput").ap()
        self.ident_in = d("ident", [128, 128], F32, kind="ExternalInput").ap()
        self.hT = d("hT", [D_MODEL, T], F32).ap()
        self.b_w = d("b_w_in", [D_MODEL, 3072], F32, kind="ExternalInput").ap()
        self.b_wsw = d("b_w_sw", [D_MODEL, 2048], F32, kind="ExternalInput").ap()
        self.b_wo = d("b_w_out", [D_MODEL, D_MODEL], F32, kind="ExternalInput").ap()
        self.b_small = d("b_small", [128, 256 + 4], F32, kind="ExternalInput").ap()
        self.kbias_in = d("kbias", [128, (T + 127) // 128], F32, kind="ExternalInput").ap()
        self.qkT = d("qkT", [2048, T], BF16).ap()
        self.v_tm = d("v_tm", [T, 1024], BF16).ap()
        self.oT = d("oT", [D_MODEL, T], BF16).ap()
        self.tri_in = d("tri", [64, 4 * 64], F32, kind="ExternalInput").ap()
        self.tmask_rep = d("tmask_rep", [128, T], F32, kind="ExternalInput").ap()
        self.tmask_col = d("tmask_col", [128, (T + 127) // 128], F32, kind="ExternalInput").ap()
        self.c_w = d("c_w_in", [D_MODEL, 5120], F32, kind="ExternalInput").ap()
        self.c_wo = d("c_w_out", [D_MODEL, D_MODEL], F32, kind="ExternalInput").ap()
        self.c_small = d("c_small", [128, 32 + 1], F32, kind="ExternalInput").ap()
        self.fmA = d("fmA", [D_MODEL, T], BF16).ap()
        self.fmB = [d("fmB%d" % i, [D_MODEL, T], BF16).ap() for i in range(2)]
        self.fmG = d("fmG", [D_MODEL, T], BF16).ap()
        self.tmK = [d("tmK%d" % i, [T, D_MODEL], BF16).ap() for i in range(2)]
        self.tmL = [d("tmL%d" % i, [T, D_MODEL], F32).ap() for i in range(2)]
        self.obT = d("obT", [D_MODEL, T], F32).ap()
        self.lbrow_t = d("lbrow", [2, D_MODEL], F32)
        self.n_a = (depth + 2) // 3
        self.a_w = d("a_w_in", [self.n_a, D_MODEL, 4128], F32, kind="ExternalInput").ap()
        self.a_wo = d("a_w_out", [self.n_a, D_MODEL, D_MODEL], F32, kind="ExternalInput").ap()
        self.a_small = d("a_small", [self.n_a, 128, 32 + 1 + 120], F32, kind="ExternalInput").ap()
        self.pre = d("pre", [3 * D_MODEL, T], BF16).ap()
        self.ba_tm = d("ba_tm", [T, 32], F32).ap()
        self.identb_in = d("identb", [128, 128], BF16, kind="ExternalInput").ap()
        self.k = KB(nc)
        k = self.k
        self.ident = k.gsb([128, 128], F32, "identc")
        self.ones_bf = k.gsb([128, 128], BF16, "onesbf")
        self.nw = k.gsb([128, (depth * 3 + 1) * 8], F32, "nwcol")
        self.build()
        k.close()

    def phase_in(self):
        k, nc, T = self.k, self.nc, self.T
        k.begin()
        k.dma(k.sp, self.ident[:], self.ident_in, writes=[self.ident])
        k.op(k.dve, lambda e: e.memset(self.ones_bf[:], 1.0), writes=[self.ones_bf])
        nrows = self.depth * 3 + 1
        k.dma(k.sp, self.nw[:], self.norm_w, writes=[self.nw])
        xt = [k.sb([128, 4, D_MODEL], F32, "xt") for _ in range(2)]
        st = [k.sb([128, 8, 512], F32, "st") for _ in range(2)]
        pst = [k.ps([128, 512], F32, "pst") for _ in range(4)]
        pi = 0
        for gi, (t0, n) in enumerate(tiles_of(T, 512)):
            x = xt[gi % 2]
            s = st[gi % 2]
            nb = (n + 127) // 128
            blocks = [(b * 128, min(128, n - b * 128)) for b in range(nb)]
            if n % 128 == 0:
                k.dma(k.sp, x[:, 0:nb, :], self.xin[t0:t0 + n, :].rearrange("(j p) f -> p j f", p=128), writes=[x])
            else:
                for b, (o, r) in enumerate(blocks):
                    k.dma(k.sp, x[0:r, b, :], self.xin[t0 + o:t0 + o + r, :], writes=[x])
            for c in range(8):
                p = pst[pi % 4]
                pi += 1
                for b, (o, r) in enumerate(blocks):
                    k.op(k.pe, (lambda e, p=p, x=x, b=b, o=o, r=r, c=c: e.transpose(
                        p[:, o:o + r], x[0:r, b, c * 128:(c + 1) * 128], self.ident[0:r, 0:r])),
                        reads=[x, self.ident], writes=[p])
                eng = k.dve if c % 2 == 0 else k.act
                if eng is k.dve:
                    k.op(eng, (lambda e, p=p, s=s, c=c, n=n: e.tensor_copy(s[:, c, 0:n], p[:, 0:n])), reads=[p], writes=[s])
                else:
                    k.op(eng, (lambda e, p=p, s=s, c=c, n=n: e.copy(s[:, c, 0:n], p[:, 0:n])), reads=[p], writes=[s])
            k.dma(k.sp, self.hT.rearrange("(c p) t -> p c t", p=128)[:, :, t0:t0 + n], s[:, :, 0:n], reads=[s])
        k.end()


    def emit_norm(self, y, yv, hn, hnv, tl, nrow, pd, sq, rstd):
        k = self.k
        for ti, (o, n) in enumerate(tl):
            ss = pd[ti % 4]
            for c in range(8):
                s_ = sq[c % 2]
                k.op(k.act, (lambda e, s_=s_, c=c, o=o, n=n: e.activation(s_[:, 0:n], y[:, c, o:o + n], AF.Square)),
                     reads=[yv[c][ti]], writes=[s_])
                k.op(k.pe, (lambda e, ss=ss, s_=s_, c=c, n=n: e.matmul(ss[:, 0:n], lhsT=self.ones_bf[:], rhs=s_[:, 0:n],
                                                                        start=(c == 0), stop=(c == 7))),
                     reads=[s_, self.ones_bf], writes=[ss])
            r = rstd[ti % 2]
            k.op(k.act, (lambda e, r=r, ss=ss, n=n: e.activation(r[:, 0:n], ss[:, 0:n], AF.Ln, bias=self.eps_col[:, 0:1],
                                                                  scale=1.0 / D_MODEL)),
                 reads=[ss, self.eps_col], writes=[r])
            k.op(k.act, (lambda e, r=r, n=n: e.activation(r[:, 0:n], r[:, 0:n], AF.Exp, scale=-0.5)), reads=[r], writes=[r])
            for c in range(8):
                col = nrow * 8 + c
                k.op(k.dve, (lambda e, r=r, c=c, o=o, n=n, col=col: e.scalar_tensor_tensor(
                    out=hn[:, c, o:o + n], in0=y[:, c, o:o + n], scalar=self.nw[:, col:col + 1], in1=r[:, 0:n],
                    op0=ALU.mult, op1=ALU.mult)),
                    reads=[yv[c][ti], r, self.nw], writes=[hnv[c][ti]])

    def phase_ffn(self, li, fj, nrow):
        k, nc = self.k, self.nc
        HG = 256
        NG = D_FF // HG
        hTv = self.hT.rearrange("(c p) t -> p c t", p=128)
        for (T0, GS) in self.groups:
            tl = tiles_of(GS, 512)
            NT = len(tl)
            k.begin()
            y = k.sb([128, 8, GS], F32, "y")
            hn = k.sb([128, 8, GS], BF16, "hn")
            yv = [k.views(y, NT) for _ in range(8)]
            hnv = [k.views(hn, NT) for _ in range(8)]
            sq = [k.sb([128, 512], BF16, "sq") for _ in range(2)]
            rstd = [k.sb([128, 512], F32, "rstd") for _ in range(2)]
            wg = [k.sb([128, 8, HG], BF16, "wg") for _ in range(2)]
            wu = [k.sb([128, 8, HG], BF16, "wu") for _ in range(2)]
            wd = [k.sb([128, HG // 128, D_MODEL], BF16, "wd") for _ in range(2)]
            sg = [k.sb([128, 2, 512], F32, "sg") for _ in range(2)]
            act = [k.sb([128, 2, 512], BF16, "act") for _ in range(2)]
            pg = [k.ps([128, 512], F32, "pg") for _ in range(2)]
            pu = [k.ps([128, 512], F32, "pu") for _ in range(2)]
            pd = [k.ps([128, 512], F32, "pd") for _ in range(4)]
            allv = [v for c in range(8) for v in yv[c]]
            k.dma(k.sp, y[:], hTv[:, :, T0:T0 + GS], writes=allv)
            self.emit_norm(y, yv, hn, hnv, tl, nrow, pd, sq, rstd)
            wup = self.w_up[li, fj]
            wdn = self.w_dn[li, fj]
            it = 0
            for g in range(NG):
                b = g % 2
                k.dma(k.pool, wg[b][:], wup[:, g * HG:(g + 1) * HG].rearrange("(kk p) c -> p kk c", p=128), writes=[wg[b]])
                k.dma(k.pool, wu[b][:], wup[:, D_FF + g * HG:D_FF + (g + 1) * HG].rearrange("(kk p) c -> p kk c", p=128),
                      writes=[wu[b]])
                k.dma(k.pool, wd[b][:], wdn[g * HG:(g + 1) * HG, :].rearrange("(kk p) c -> p kk c", p=128), writes=[wd[b]])
                for ti, (o, n) in enumerate(tl):
                    ab = it % 2
                    it += 1
                    for j in range(2):
                        for (pt, wt) in ((pg[j], wg[b]), (pu[j], wu[b])):
                            for c in range(8):
                                k.op(k.pe, (lambda e, pt=pt, wt=wt, j=j, c=c, o=o, n=n: e.matmul(
                                    pt[:, 0:n], lhsT=wt[:, c, j * 128:(j + 1) * 128], rhs=hn[:, c, o:o + n],
                                    start=(c == 0), stop=(c == 7))),
                                    reads=[wt, hnv[c][ti]], writes=[pt])
                    for j in range(2):
                        k.op(k.act, (lambda e, j=j, ab=ab, n=n: e.activation(sg[ab][:, j, 0:n], pg[j][:, 0:n], AF.Silu)),
                             reads=[pg[j]], writes=[sg[ab]])
                    for j in range(2):
                        k.op(k.dve, (lambda e, j=j, ab=ab, n=n: e.scalar_tensor_tensor(
                            out=act[ab][:, j, 0:n], in0=pu[j][:, 0:n], scalar=0.5, in1=sg[ab][:, j, 0:n],
                            op0=ALU.mult, op1=ALU.mult)),
                            reads=[pu[j], sg[ab]], writes=[act[ab]])
                    for m in range(8):
                        pdt = pd[m % 4]
                        for j in range(2):
                            k.op(k.pe, (lambda e, pdt=pdt, b=b, j=j, m=m, ab=ab, n=n: e.matmul(
                                pdt[:, 0:n], lhsT=wd[b][:, j, m * 128:(m + 1) * 128], rhs=act[ab][:, j, 0:n],
                                start=(j == 0), stop=(j == 1))),
                                reads=[wd[b], act[ab]], writes=[pdt])
                        k.op(k.dve, (lambda e, pdt=pdt, m=m, o=o, n=n: e.tensor_tensor(
                            out=y[:, m, o:o + n], in0=y[:, m, o:o + n], in1=pdt[:, 0:n], op=ALU.add)),
                            reads=[pdt, yv[m][ti]], writes=[yv[m][ti]])
            k.dma(k.sp, hTv[:, :, T0:T0 + GS], y[:], reads=allv)
            k.end()

    def phase_out(self, nrow):
        k, nc, T = self.k, self.nc, self.T
        hTv = self.hT.rearrange("(c p) t -> p c t", p=128)
        k.begin()
        hb = [k.sb([128, 8, 512], F32, "hb") for _ in range(2)]
        sq = [k.sb([128, 512], BF16, "sq") for _ in range(2)]
        rstd = [k.sb([128, 512], F32, "rstd") for _ in range(2)]
        yn = [k.sb([128, 8, 512], F32, "yn") for _ in range(2)]
        ot = [k.sb([128, 4, D_MODEL], F32, "ot") for _ in range(2)]
        pss = k.ps([128, 512], F32, "pss")
        pt = [k.ps([128, 512], F32, "pt") for _ in range(4)]
        pi = 0
        for gi, (t0, n) in enumerate(tiles_of(T, 512)):
            h = hb[gi % 2]
            r = rstd[gi % 2]
            yy = yn[gi % 2]
            o_ = ot[gi % 2]
            k.dma(k.sp, h[:, :, 0:n], hTv[:, :, t0:t0 + n], writes=[h])
            for c in range(8):
                s_ = sq[c % 2]
                k.op(k.act, (lambda e, s_=s_, h=h, c=c, n=n: e.activation(s_[:, 0:n], h[:, c, 0:n], AF.Square)),
                     reads=[h], writes=[s_])
                k.op(k.pe, (lambda e, s_=s_, c=c, n=n: e.matmul(pss[:, 0:n], lhsT=self.ones_bf[:], rhs=s_[:, 0:n],
                                                                 start=(c == 0), stop=(c == 7))),
                     reads=[s_, self.ones_bf], writes=[pss])
            k.op(k.act, (lambda e, r=r, n=n: e.activation(r[:, 0:n], pss[:, 0:n], AF.Ln, bias=self.eps_col[:, 0:1],
                                                           scale=1.0 / D_MODEL)),
                 reads=[pss, self.eps_col], writes=[r])
            k.op(k.act, (lambda e, r=r, n=n: e.activation(r[:, 0:n], r[:, 0:n], AF.Exp, scale=-0.5)), reads=[r], writes=[r])
            for c in range(8):
                col = nrow * 8 + c
                k.op(k.dve, (lambda e, r=r, h=h, yy=yy, c=c, n=n, col=col: e.scalar_tensor_tensor(
                    out=yy[:, c, 0:n], in0=h[:, c, 0:n], scalar=self.nw[:, col:col + 1], in1=r[:, 0:n],
                    op0=ALU.mult, op1=ALU.mult)),
                    reads=[h, r, self.nw], writes=[yy])
            nb = (n + 127) // 128
            blocks = [(b * 128, min(128, n - b * 128)) for b in range(nb)]
            for b, (o, rr) in enumerate(blocks):
                for half in range(2):
                    p = pt[pi % 4]
                    pi += 1
                    for cc in range(4):
                        c = half * 4 + cc
                        k.op(k.pe, (lambda e, p=p, yy=yy, c=c, cc=cc, o=o, rr=rr: e.transpose(
                            p[0:rr, cc * 128:(cc + 1) * 128], yy[:, c, o:o + rr], self.ident[:])),
                            reads=[yy, self.ident], writes=[p])
                    if half == 0:
                        k.op(k.dve, (lambda e, p=p, o_=o_, b=b, rr=rr: e.tensor_copy(o_[0:rr, b, 0:512], p[0:rr, :])),
                             reads=[p], writes=[o_])
                    else:
                        k.op(k.act, (lambda e, p=p, o_=o_, b=b, rr=rr: e.copy(o_[0:rr, b, 512:1024], p[0:rr, :])),
                             reads=[p], writes=[o_])
            if n % 128 == 0:
                k.dma(k.sp, self.yout[t0:t0 + n, :].rearrange("(j p) f -> p j f", p=128), o_[:, 0:nb, :], reads=[o_])
            else:
                for b, (o, rr) in enumerate(blocks):
                    k.dma(k.sp, self.yout[t0 + o:t0 + o + rr, :], o_[0:rr, b, :], reads=[o_])
        k.end()


    def emit_rope_tables(self, T0, tl, cosF, sinF, tmp):
        k = self.k
        PI = float(np.pi)
        invf = self.bsm[:, 257:258]
        sign = self.bsm[:, 258:259]
        for ti, (o, n) in enumerate(tl):
            pos, ang, ki, kf, tf = tmp
            k.op(k.pool, (lambda e, pos=pos, n=n, b=T0 + o: e.iota(pos[:, 0:n], [[1, n]], base=b, channel_multiplier=0,
                                                                   allow_small_or_imprecise_dtypes=True)), writes=[pos])
            k.op(k.dve, (lambda e, n=n: e.tensor_scalar(ang[:, 0:n], pos[:, 0:n], invf, None, op0=ALU.mult)),
                 reads=[pos, self.bsm], writes=[ang])
            def reduce(src, dst, shift, n=n):
                k.op(k.dve, (lambda e: e.tensor_scalar(dst[:, 0:n], src[:, 0:n], shift, None, op0=ALU.add)),
                     reads=[src], writes=[dst])
                k.op(k.dve, (lambda e: e.tensor_scalar(ki[:, 0:n], dst[:, 0:n], 1.0 / (2 * PI), None, op0=ALU.mult)),
                     reads=[dst], writes=[ki])
                k.op(k.dve, (lambda e: e.tensor_copy(tf[:, 0:n], ki[:, 0:n])), reads=[ki], writes=[tf])
                k.op(k.dve, (lambda e: e.scalar_tensor_tensor(out=dst[:, 0:n], in0=tf[:, 0:n], scalar=-2 * PI, in1=dst[:, 0:n],
                                                              op0=ALU.mult, op1=ALU.add)), reads=[tf, dst], writes=[dst])
                k.op(k.dve, (lambda e: e.tensor_scalar(tf[:, 0:n], dst[:, 0:n], PI, -2 * PI, op0=ALU.is_gt, op1=ALU.mult)),
                     reads=[dst], writes=[tf])
                k.op(k.dve, (lambda e: e.tensor_tensor(out=dst[:, 0:n], in0=dst[:, 0:n], in1=tf[:, 0:n], op=ALU.add)),
                     reads=[dst, tf], writes=[dst])
                k.op(k.dve, (lambda e: e.tensor_scalar(tf[:, 0:n], dst[:, 0:n], -PI, 2 * PI, op0=ALU.is_lt, op1=ALU.mult)),
                     reads=[dst], writes=[tf])
                k.op(k.dve, (lambda e: e.tensor_tensor(out=dst[:, 0:n], in0=dst[:, 0:n], in1=tf[:, 0:n], op=ALU.add)),
                     reads=[dst, tf], writes=[dst])
                k.op(k.dve, (lambda e: e.tensor_scalar(dst[:, 0:n], dst[:, 0:n], -PI, PI, op0=ALU.max, op1=ALU.min)),
                     reads=[dst], writes=[dst])
            reduce(ang, kf, 0.0)
            reduce(ang, pos, PI / 2)
            s_, c_ = sinF[ti], cosF[ti]
            k.op(k.act, (lambda e, s_=s_, n=n: e.activation(s_[:, 0:n], kf[:, 0:n], AF.Sin)), reads=[kf], writes=[s_])
            k.op(k.act, (lambda e, c_=c_, n=n: e.activation(c_[:, 0:n], pos[:, 0:n], AF.Sin)), reads=[pos], writes=[c_])
            k.op(k.dve, (lambda e, s_=s_, n=n: e.tensor_scalar(s_[:, 0:n], s_[:, 0:n], sign, None, op0=ALU.mult)),
                 reads=[s_, self.bsm], writes=[s_])

    def phase_attn_proj(self, nrow):
        k, nc = self.k, self.nc
        hTv = self.hT.rearrange("(c p) t -> p c t", p=128)
        for (T0, GS) in self.groups:
            tl = tiles_of(GS, 512)
            NT = len(tl)
            k.begin()
            y = k.sb([128, 8, GS], F32, "y")
            hn = k.sb([128, 8, GS], BF16, "hn")
            yv = [k.views(y, NT) for _ in range(8)]
            hnv = [k.views(hn, NT) for _ in range(8)]
            sq = [k.sb([128, 512], BF16, "sq") for _ in range(2)]
            rstd = [k.sb([128, 512], F32, "rstd") for _ in range(2)]
            pd = [k.ps([128, 512], F32, "pd") for _ in range(4)]
            pa = [k.ps([128, 512], F32, "pa") for _ in range(2)]
            pb = [k.ps([128, 512], F32, "pb") for _ in range(2)]
            allv = [v for c in range(8) for v in yv[c]]
            k.dma(k.sp, y[:], hTv[:, :, T0:T0 + GS], writes=allv)
            self.emit_norm(y, yv, hn, hnv, tl, nrow, pd, sq, rstd)
            cosF = [k.sb([128, 512], F32, "cosF") for _ in range(NT)]
            sinF = [k.sb([128, 512], F32, "sinF") for _ in range(NT)]
            tmp = (k.sb([128, 512], F32, "pos"), k.sb([128, 512], F32, "ang"),
                   k.sb([128, 512], mybir.dt.int32, "ki"), k.sb([128, 512], F32, "kf"), k.sb([128, 512], F32, "tf"))
            self.emit_rope_tables(T0, tl, cosF, sinF, tmp)
            wa = [k.sb([128, 8, 128], BF16, "wa") for _ in range(2)]
            wb = [k.sb([128, 8, 128], BF16, "wb") for _ in range(2)]
            t1 = [k.sb([128, 512], F32, "t1") for _ in range(2)]
            t2 = [k.sb([128, 512], F32, "t2") for _ in range(2)]
            stg = [k.sb([128, 512], BF16, "stg") for _ in range(3)]
            it = 0
            for m in range(16):
                b = m % 2
                k.dma(k.pool, wa[b][:], self.b_w[:, m * 128:(m + 1) * 128].rearrange("(kk p) c -> p kk c", p=128), writes=[wa[b]])
                k.dma(k.pool, wb[b][:], self.b_wsw[:, m * 128:(m + 1) * 128].rearrange("(kk p) c -> p kk c", p=128), writes=[wb[b]])
                for ti, (o, n) in enumerate(tl):
                    ab = it % 2
                    sb_ = stg[it % 3]
                    it += 1
                    for (pt, wt) in ((pa[ab], wa[b]), (pb[ab], wb[b])):
                        for c in range(8):
                            k.op(k.pe, (lambda e, pt=pt, wt=wt, c=c, o=o, n=n: e.matmul(
                                pt[:, 0:n], lhsT=wt[:, c, :], rhs=hn[:, c, o:o + n], start=(c == 0), stop=(c == 7))),
                                reads=[wt, hnv[c][ti]], writes=[pt])
                    k.op(k.dve, (lambda e, ab=ab, ti=ti, n=n: e.tensor_tensor(out=t1[ab][:, 0:n], in0=pa[ab][:, 0:n],
                                                                              in1=cosF[ti][:, 0:n], op=ALU.mult)),
                         reads=[pa[ab], cosF[ti]], writes=[t1[ab]])
                    k.op(k.dve, (lambda e, ab=ab, ti=ti, n=n: e.tensor_tensor(out=t2[ab][:, 0:n], in0=pb[ab][:, 0:n],
                                                                              in1=sinF[ti][:, 0:n], op=ALU.mult)),
                         reads=[pb[ab], sinF[ti]], writes=[t2[ab]])
                    k.op(k.pool, (lambda e, ab=ab, sb_=sb_, n=n: e.tensor_tensor(out=sb_[:, 0:n], in0=t1[ab][:, 0:n],
                                                                                 in1=t2[ab][:, 0:n], op=ALU.add)),
                         reads=[t1[ab], t2[ab]], writes=[sb_])
                    k.dma(k.sp, self.qkT[m * 128:(m + 1) * 128, T0 + o:T0 + o + n], sb_[:, 0:n], reads=[sb_])
            wv = [k.sb([128, 8, 512], BF16, "wv") for _ in range(2)]
            vst = [k.sb([128, 512], BF16, "vst") for _ in range(3)]
            it = 0
            for vb in range(2):
                k.dma(k.pool, wv[vb][:], self.b_w[:, 2048 + vb * 512:2048 + (vb + 1) * 512].rearrange("(kk p) c -> p kk c", p=128),
                      writes=[wv[vb]])
                for ti, (o, n) in enumerate(tl):
                    for bo in range(0, n, 128):
                        r = min(128, n - bo)
                        pt = pd[it % 4]
                        vs = vst[it % 3]
                        it += 1
                        for c in range(8):
                            k.op(k.pe, (lambda e, pt=pt, vb=vb, c=c, o=o, bo=bo, r=r: e.matmul(
                                pt[0:r, :], lhsT=hn[:, c, o + bo:o + bo + r], rhs=wv[vb][:, c, :], start=(c == 0), stop=(c == 7))),
                                reads=[wv[vb], hnv[c][ti]], writes=[pt])
                        k.op(k.act, (lambda e, pt=pt, vs=vs, r=r: e.copy(vs[0:r, :], pt[0:r, :])), reads=[pt], writes=[vs])
                        k.dma(k.sp, self.v_tm[T0 + o + bo:T0 + o + bo + r, vb * 512:(vb + 1) * 512], vs[0:r, :], reads=[vs])
            k.end()

    def phase_attn_core(self):
        k, nc, T = self.k, self.nc, self.T
        NKT = (T + 127) // 128
        qtl = tiles_of(T, 512)
        k.begin()
        lt = k.sb([128, 256], F32, "lt")
        l2 = k.sb([128, 2], F32, "l2")
        neglam = k.sb([128, 1], F32, "neglam")
        subw = k.sb([128, 1], F32, "subw")
        kb = k.sb([128, NKT], F32, "kb")
        k.dma(k.sp, kb[:], self.kbias_in, writes=[kb])
        k.op(k.dve, (lambda e: e.tensor_tensor(out=lt[:, 0:64], in0=self.bsm[:, 0:64], in1=self.bsm[:, 64:128], op=ALU.mult)),
             reads=[self.bsm], writes=[lt])
        k.op(k.dve, (lambda e: e.tensor_tensor(out=lt[:, 64:128], in0=self.bsm[:, 128:192], in1=self.bsm[:, 192:256], op=ALU.mult)),
             reads=[self.bsm], writes=[lt])
        k.op(k.dve, (lambda e: e.reduce_sum(l2[:, 0:2], lt[:, 0:128].rearrange("p (a b) -> p a b", a=2), axis=AX.X)),
             reads=[lt], writes=[l2])
        k.op(k.act, (lambda e: e.activation(l2[:, 0:2], l2[:, 0:2], AF.Exp)), reads=[l2], writes=[l2])
        k.op(k.dve, (lambda e: e.tensor_tensor(out=neglam[:], in0=l2[:, 1:2], in1=l2[:, 0:1], op=ALU.subtract)),
             reads=[l2], writes=[neglam])
        k.op(k.dve, (lambda e: e.tensor_tensor(out=neglam[:], in0=neglam[:], in1=self.bsm[:, 259:260], op=ALU.subtract)),
             reads=[neglam, self.bsm], writes=[neglam])
        k.op(k.dve, (lambda e: e.tensor_scalar(subw[:], self.bsm[:, 259:260], -1.0, 1.0, op0=ALU.mult, op1=ALU.add)),
             reads=[self.bsm], writes=[subw])
        k.op(k.dve, (lambda e: e.tensor_tensor(out=subw[:], in0=subw[:], in1=self.bsm[:, 256:257], op=ALU.mult)),
             reads=[subw, self.bsm], writes=[subw])

        kk1 = [k.sb([64, T], BF16, "kk1") for _ in range(2)]
        kk2 = [k.sb([64, T], BF16, "kk2") for _ in range(2)]
        vh = [k.sb([128, NKT, 128], BF16, "vh") for _ in range(2)]
        q1 = [k.sb([64, 512], BF16, "q1") for _ in range(2)]
        q2 = [k.sb([64, 512], BF16, "q2") for _ in range(2)]
        p1 = [k.sb([128, 512], BF16, "p1") for _ in range(3)]
        p2 = [k.sb([128, 512], BF16, "p2") for _ in range(3)]
        ps1 = [k.ps([128, 512], F32, "ps1") for _ in range(2)]
        ps2 = [k.ps([128, 512], F32, "ps2") for _ in range(2)]
        num1, num2, z1, z2 = (k.ps([128, 512], F32, nm) for nm in ("num1", "num2", "z1", "z2"))
        r1 = k.sb([128, 512], F32, "r1")
        r2 = k.sb([128, 512], F32, "r2")
        o1 = k.sb([128, 512], F32, "o1")
        o2 = k.sb([128, 512], F32, "o2")
        oo = k.sb([128, 512], F32, "oo")
        sqo = k.sb([128, 512], BF16, "sqo")
        rs = k.sb([128, 512], F32, "rs")
        ost = [k.sb([128, 512], BF16, "ost") for _ in range(2)]
        nfull = T // 128
        rem = T - nfull * 128
        it = 0
        fi = 0
        for h in range(8):
            hb = h % 2
            k.dma(k.sp, kk1[hb][:], self.qkT[1024 + h * 64:1024 + (h + 1) * 64, :], writes=[kk1[hb]])
            k.dma(k.sp, kk2[hb][:], self.qkT[1536 + h * 64:1536 + (h + 1) * 64, :], writes=[kk2[hb]])
            k.dma(k.sp, vh[hb][:, 0:nfull, :],
                  self.v_tm[0:nfull * 128, h * 128:(h + 1) * 128].rearrange("(kt p) e -> p kt e", p=128), writes=[vh[hb]])
            if rem:
                k.dma(k.sp, vh[hb][0:rem, nfull, :], self.v_tm[nfull * 128:T, h * 128:(h + 1) * 128], writes=[vh[hb]])
            for qi, (t0, n) in enumerate(qtl):
                qb = fi % 2
                k.dma(k.sp, q1[qb][:, 0:n], self.qkT[h * 64:(h + 1) * 64, t0:t0 + n], writes=[q1[qb]])
                k.dma(k.sp, q2[qb][:, 0:n], self.qkT[512 + h * 64:512 + (h + 1) * 64, t0:t0 + n], writes=[q2[qb]])
                for kt in range(NKT):
                    kn = 128 if kt < nfull else rem
                    sb2 = it % 2
                    pb3 = it % 3
                    it += 1
                    first, last = (kt == 0), (kt == NKT - 1)
                    for (ps, kk, qq, pp, nm, zz) in ((ps1[sb2], kk1[hb], q1[qb], p1[pb3], num1, z1),
                                                     (ps2[sb2], kk2[hb], q2[qb], p2[pb3], num2, z2)):
                        k.op(k.pe, (lambda e, ps=ps, kk=kk, qq=qq, kt=kt, kn=kn, n=n: e.matmul(
                            ps[0:kn, 0:n], lhsT=kk[:, kt * 128:kt * 128 + kn], rhs=qq[:, 0:n], start=True, stop=True)),
                            reads=[kk, qq], writes=[ps])
                        k.op(k.act, (lambda e, ps=ps, pp=pp, kt=kt, kn=kn, n=n: e.activation(
                            pp[0:kn, 0:n], ps[0:kn, 0:n], AF.Exp, bias=kb[0:kn, kt:kt + 1], scale=0.125)),
                            reads=[ps, kb], writes=[pp])
                        k.op(k.pe, (lambda e, nm=nm, pp=pp, hb=hb, kt=kt, kn=kn, n=n, first=first, last=last: e.matmul(
                            nm[:, 0:n], lhsT=vh[hb][0:kn, kt, :], rhs=pp[0:kn, 0:n], start=first, stop=last)),
                            reads=[vh[hb], pp], writes=[nm])
                        k.op(k.pe, (lambda e, zz=zz, pp=pp, kn=kn, n=n, first=first, last=last: e.matmul(
                            zz[:, 0:n], lhsT=self.ones_bf[0:kn, :], rhs=pp[0:kn, 0:n], start=first, stop=last)),
                            reads=[self.ones_bf, pp], writes=[zz])
                k.op(k.dve, (lambda e, n=n: e.reciprocal(r1[:, 0:n], z1[:, 0:n])), reads=[z1], writes=[r1])
                k.op(k.dve, (lambda e, n=n: e.reciprocal(r2[:, 0:n], z2[:, 0:n])), reads=[z2], writes=[r2])
                k.op(k.dve, (lambda e, n=n: e.tensor_tensor(out=o1[:, 0:n], in0=num1[:, 0:n], in1=r1[:, 0:n], op=ALU.mult)),
                     reads=[num1, r1], writes=[o1])
                k.op(k.dve, (lambda e, n=n: e.scalar_tensor_tensor(out=o2[:, 0:n], in0=num2[:, 0:n], scalar=neglam[:, 0:1],
                                                                   in1=r2[:, 0:n], op0=ALU.mult, op1=ALU.mult)),
                     reads=[num2, r2, neglam], writes=[o2])
                k.op(k.pool, (lambda e, n=n: e.tensor_tensor(out=oo[:, 0:n], in0=o1[:, 0:n], in1=o2[:, 0:n], op=ALU.add)),
                     reads=[o1, o2], writes=[oo])
                k.op(k.act, (lambda e, n=n: e.activation(sqo[:, 0:n], oo[:, 0:n], AF.Square)), reads=[oo], writes=[sqo])
                pss = ps1[it % 2]
                k.op(k.pe, (lambda e, pss=pss, n=n: e.matmul(pss[:, 0:n], lhsT=self.ones_bf[:], rhs=sqo[:, 0:n], start=True, stop=True)),
                     reads=[self.ones_bf, sqo], writes=[pss])
                k.op(k.act, (lambda e, pss=pss, n=n: e.activation(rs[:, 0:n], pss[:, 0:n], AF.Ln, bias=self.eps_col[:, 0:1],
                                                                   scale=1.0 / 128)), reads=[pss, self.eps_col], writes=[rs])
                k.op(k.act, (lambda e, n=n: e.activation(rs[:, 0:n], rs[:, 0:n], AF.Exp, scale=-0.5)), reads=[rs], writes=[rs])
                os_ = ost[fi % 2]
                fi += 1
                k.op(k.dve, (lambda e, os_=os_, n=n: e.scalar_tensor_tensor(out=os_[:, 0:n], in0=oo[:, 0:n], scalar=subw[:, 0:1],
                                                                            in1=rs[:, 0:n], op0=ALU.mult, op1=ALU.mult)),
                     reads=[oo, rs, subw], writes=[os_])
                k.dma(k.sp, self.oT[h * 128:(h + 1) * 128, t0:t0 + n], os_[:, 0:n], reads=[os_])
        k.end()

    def phase_out_proj(self, wo_ap):
        k, nc, T = self.k, self.nc, self.T
        hTv = self.hT.rearrange("(c p) t -> p c t", p=128)
        oTv = self.oT.rearrange("(c p) t -> p c t", p=128)
        k.begin()
        wo = k.sb([128, 8, D_MODEL], BF16, "wo")
        for c in range(8):
            for hf in range(2):
                k.dma(k.pool, wo[:, c, hf * 512:(hf + 1) * 512], wo_ap[c * 128:(c + 1) * 128, hf * 512:(hf + 1) * 512], writes=[wo])
        ob = [k.sb([128, 8, 512], BF16, "ob") for _ in range(2)]
        hb = [k.sb([128, 8, 512], F32, "hb") for _ in range(2)]
        pp = [k.ps([128, 512], F32, "pp") for _ in range(4)]
        pi = 0
        for gi, (t0, n) in enumerate(tiles_of(T, 512)):
            o_ = ob[gi % 2]
            h_ = hb[gi % 2]
            k.dma(k.sp, o_[:, :, 0:n], oTv[:, :, t0:t0 + n], writes=[o_])
            k.dma(k.sp, h_[:, :, 0:n], hTv[:, :, t0:t0 + n], writes=[h_])
            for m in range(8):
                p = pp[pi % 4]
                pi += 1
                for c in range(8):
                    k.op(k.pe, (lambda e, p=p, o_=o_, c=c, m=m, n=n: e.matmul(
                        p[:, 0:n], lhsT=wo[:, c, m * 128:(m + 1) * 128], rhs=o_[:, c, 0:n], start=(c == 0), stop=(c == 7))),
                        reads=[wo, o_], writes=[p])
                k.op(k.dve, (lambda e, p=p, h_=h_, m=m, n=n: e.tensor_tensor(out=h_[:, m, 0:n], in0=h_[:, m, 0:n],
                                                                             in1=p[:, 0:n], op=ALU.add)),
                     reads=[p, h_], writes=[h_])
            k.dma(k.sp, hTv[:, :, t0:t0 + n], h_[:, :, 0:n], reads=[h_])
        k.end()

    def mixer_attn(self, nrow):
        import os
        st = int(os.environ.get("ATT_STAGE", "3"))
        self.phase_attn_proj(nrow)
        if st >= 2:
            self.phase_attn_core()
        if st >= 3:
            self.phase_out_proj(self.b_wo)


    def phase_proj(self, nrow, setup, fm_jobs, tm_jobs):
        k, nc = self.k, self.nc
        hTv = self.hT.rearrange("(c p) t -> p c t", p=128)
        for (T0, GS) in self.groups:
            tl = tiles_of(GS, 512)
            NT = len(tl)
            k.begin()
            y = k.sb([128, 8, GS], F32, "y")
            hn = k.sb([128, 8, GS], BF16, "hn")
            yv = [k.views(y, NT) for _ in range(8)]
            hnv = [k.views(hn, NT) for _ in range(8)]
            sq = [k.sb([128, 512], BF16, "sq") for _ in range(2)]
            rstd = [k.sb([128, 512], F32, "rstd") for _ in range(2)]
            pd = [k.ps([128, 512], F32, "pd") for _ in range(4)]
            allv = [v for c in range(8) for v in yv[c]]
            k.dma(k.sp, y[:], hTv[:, :, T0:T0 + GS], writes=allv)
            self.emit_norm(y, yv, hn, hnv, tl, nrow, pd, sq, rstd)
            ctx = setup(T0, tl)
            wa = [k.sb([128, 8, 128], BF16, "wa") for _ in range(2)]
            it = 0
            for ji, (wap, post) in enumerate(fm_jobs):
                b = ji % 2
                k.dma(k.pool, wa[b][:], wap.rearrange("(kk p) c -> p kk c", p=128), writes=[wa[b]])
                for ti, (o, n) in enumerate(tl):
                    pt = pd[it % 4]
                    it += 1
                    for c in range(8):
                        k.op(k.pe, (lambda e, pt=pt, b=b, c=c, o=o, n=n: e.matmul(
                            pt[:, 0:n], lhsT=wa[b][:, c, :], rhs=hn[:, c, o:o + n], start=(c == 0), stop=(c == 7))),
                            reads=[wa[b], hnv[c][ti]], writes=[pt])
                    post(ctx, pt, T0, ti, o, n)
            wv = [k.sb([128, 8, 512], BF16, "wv") for _ in range(2)]
            for ji, job in enumerate(tm_jobs):
                wap, post = job[0], job[1]
                ncl = job[2] if len(job) > 2 else 512
                b = ji % 2
                k.dma(k.pool, wv[b][:, :, 0:ncl], wap.rearrange("(kk p) c -> p kk c", p=128), writes=[wv[b]])
                for ti, (o, n) in enumerate(tl):
                    for bo in range(0, n, 128):
                        r = min(128, n - bo)
                        pt = pd[it % 4]
                        it += 1
                        for c in range(8):
                            k.op(k.pe, (lambda e, pt=pt, b=b, c=c, o=o, bo=bo, r=r, ncl=ncl: e.matmul(
                                pt[0:r, 0:ncl], lhsT=hn[:, c, o + bo:o + bo + r], rhs=wv[b][:, c, 0:ncl], start=(c == 0), stop=(c == 7))),
                                reads=[wv[b], hnv[c][ti]], writes=[pt])
                        post(ctx, pt, T0 + o + bo, r)
            k.end()

    def hg_prep(self, li):
        k, nc = self.k, self.nc
        k.begin()
        e = k.sb([128, 32], F32, "lbe")
        ssum = k.sb([128, 8], F32, "lbs")
        k.op(k.act, (lambda en: en.activation(e[:], self.csm[:, 0:32], AF.Exp)), reads=[self.csm], writes=[e])
        k.op(k.dve, (lambda en: en.tensor_tensor(out=ssum[:], in0=e[:, 0:8], in1=e[:, 8:16], op=ALU.add)), reads=[e], writes=[ssum])
        k.op(k.dve, (lambda en: en.tensor_tensor(out=ssum[:], in0=ssum[:], in1=e[:, 16:24], op=ALU.add)), reads=[e, ssum], writes=[ssum])
        k.op(k.dve, (lambda en: en.tensor_tensor(out=ssum[:], in0=ssum[:], in1=e[:, 24:32], op=ALU.add)), reads=[e, ssum], writes=[ssum])
        k.op(k.dve, (lambda en: en.reciprocal(ssum[:], ssum[:])), reads=[ssum], writes=[ssum])
        lbc = self.lb_col
        k.op(k.dve, (lambda en: en.memset(lbc[:, 0:8], 0.0)), writes=[lbc])
        for r in range(1, li + 1):
            k.op(k.dve, (lambda en, r=r: en.tensor_tensor(out=lbc[:, 0:8], in0=lbc[:, 0:8], in1=e[:, r * 8:(r + 1) * 8], op=ALU.add)),
                 reads=[e, lbc], writes=[lbc])
        k.op(k.dve, (lambda en: en.tensor_tensor(out=lbc[:, 0:8], in0=lbc[:, 0:8], in1=ssum[:], op=ALU.mult)), reads=[lbc, ssum], writes=[lbc])
        k.op(k.dve, (lambda en: en.tensor_scalar(lbc[:, 8:16], lbc[:, 0:8], -1.0, 1.0, op0=ALU.mult, op1=ALU.add)), reads=[lbc], writes=[lbc])
        lbr = self.lbrow_t.ap()
        k.dma(k.sp, lbr.rearrange("r (c p) -> p r c", p=128), lbc[:].rearrange("p (r c) -> p r c", c=8), reads=[lbc],
              allow_slow_non_contiguous=True)
        k.end()
        k.begin()
        k.dma(k.sp, self.oml_row[:], bass.AP(self.lbrow_t, D_MODEL, [[0, 128], [1, D_MODEL]]), writes=[self.oml_row])
        k.end()

    def phase_hg_proj(self, nrow):
        k = self.k
        QS = 128 ** -0.5

        def setup(T0, tl):
            ctx = {}
            ctx["tm"] = [k.sb([128, 512], F32, "tmk") for _ in tl]
            for ti, (o, n) in enumerate(tl):
                k.dma(k.sp, ctx["tm"][ti][:, 0:n], self.tmask_rep[:, T0 + o:T0 + o + n], writes=[ctx["tm"][ti]])
            ctx["sg"] = [k.sb([128, 512], F32, "sg") for _ in range(2)]
            ctx["st"] = [k.sb([128, 512], BF16, "st") for _ in range(3)]
            ctx["k32"] = [k.sb([128, 512], F32, "k32") for _ in range(2)]
            ctx["lf"] = [k.sb([128, 512], F32, "lf") for _ in range(2)]
            ctx["kb"] = [k.sb([128, 512], BF16, "kb") for _ in range(2)]
            ctx["i"] = 0
            return ctx

        def post_silu(dst, scale):
            def post(ctx, pt, T0, ti, o, n):
                i = ctx["i"]
                ctx["i"] += 1
                sg, st = ctx["sg"][i % 2], ctx["st"][i % 3]
                k.op(k.act, (lambda e: e.activation(sg[:, 0:n], pt[:, 0:n], AF.Silu)), reads=[pt], writes=[sg])
                k.op(k.pool, (lambda e: e.tensor_scalar(st[:, 0:n], sg[:, 0:n], scale, None, op0=ALU.mult)), reads=[sg], writes=[st])
                return st
            return post

        def fm_q(c):
            base = post_silu(None, QS)

            def post(ctx, pt, T0, ti, o, n):
                st = base(ctx, pt, T0, ti, o, n)
                k.dma(k.sp, self.fmA[c * 128:(c + 1) * 128, T0 + o:T0 + o + n], st[:, 0:n], reads=[st])
            return post

        def fm_g(c):
            base = post_silu(None, 1.0)

            def post(ctx, pt, T0, ti, o, n):
                st = base(ctx, pt, T0, ti, o, n)
                k.dma(k.sp, self.fmG[c * 128:(c + 1) * 128, T0 + o:T0 + o + n], st[:, 0:n], reads=[st])
            return post

        def fm_k(d, c):
            def post(ctx, pt, T0, ti, o, n):
                i = ctx["i"]
                ctx["i"] += 1
                sg, st = ctx["sg"][i % 2], ctx["st"][i % 3]
                k.op(k.act, (lambda e: e.activation(sg[:, 0:n], pt[:, 0:n], AF.Sigmoid, scale=-1.0)), reads=[pt], writes=[sg])
                k.op(k.dve, (lambda e: e.scalar_tensor_tensor(out=st[:, 0:n], in0=sg[:, 0:n], scalar=self.lb_col[:, 8 + c:9 + c],
                                                              in1=ctx["tm"][ti][:, 0:n], op0=ALU.mult, op1=ALU.mult)),
                     reads=[sg, self.lb_col, ctx["tm"][ti]], writes=[st])
                k.dma(k.sp, self.fmB[d][c * 128:(c + 1) * 128, T0 + o:T0 + o + n], st[:, 0:n], reads=[st])
            return post

        def tm_v(cb):
            def post(ctx, pt, tok0, r):
                i = ctx["i"]
                ctx["i"] += 1
                st = ctx["st"][i % 3]
                k.op(k.act, (lambda e: e.copy(st[0:r, :], pt[0:r, :])), reads=[pt], writes=[st])
                k.dma(k.sp, self.v_tm[tok0:tok0 + r, cb * 512:(cb + 1) * 512], st[0:r, :], reads=[st])
            return post

        def tm_f(d, cb):
            def post(ctx, pt, tok0, r):
                i = ctx["i"]
                ctx["i"] += 1
                sg, k32, lf, kb = ctx["sg"][i % 2], ctx["k32"][i % 2], ctx["lf"][i % 2], ctx["kb"][i % 2]
                blk = tok0 // 128
                assert tok0 % 128 == 0
                k.op(k.act, (lambda e: e.activation(sg[0:r, :], pt[0:r, :], AF.Sigmoid, scale=-1.0)), reads=[pt], writes=[sg])
                k.op(k.dve, (lambda e: e.scalar_tensor_tensor(out=k32[0:r, :], in0=sg[0:r, :], scalar=self.tmc[0:r, blk:blk + 1],
                                                              in1=self.oml_row[0:r, cb * 512:(cb + 1) * 512], op0=ALU.mult, op1=ALU.mult)),
                     reads=[sg, self.tmc, self.oml_row], writes=[k32])
                k.op(k.act, (lambda e: e.activation(lf[0:r, :], k32[0:r, :], AF.Ln, bias=self.one_col[0:r, 0:1], scale=-1.0)),
                     reads=[k32, self.one_col], writes=[lf])
                k.op(k.pool, (lambda e: e.tensor_copy(kb[0:r, :], k32[0:r, :])), reads=[k32], writes=[kb])
                k.dma(k.sp, self.tmK[d][tok0:tok0 + r, cb * 512:(cb + 1) * 512], kb[0:r, :], reads=[kb])
                k.dma(k.sp, self.tmL[d][tok0:tok0 + r, cb * 512:(cb + 1) * 512], lf[0:r, :], reads=[lf])
            return post

        W = self.c_w
        fm = []
        for c in range(8):
            fm.append((W[:, c * 128:(c + 1) * 128], fm_q(c)))
        for c in range(8):
            fm.append((W[:, 2048 + c * 128:2048 + (c + 1) * 128], fm_g(c)))
        for d in range(2):
            for c in range(8):
                fm.append((W[:, 3072 + d * 1024 + c * 128:3072 + d * 1024 + (c + 1) * 128], fm_k(d, c)))
        tm = []
        for cb in range(2):
            tm.append((W[:, 1024 + cb * 512:1024 + (cb + 1) * 512], tm_v(cb)))
        for d in range(2):
            for cb in range(2):
                tm.append((W[:, 3072 + d * 1024 + cb * 512:3072 + d * 1024 + (cb + 1) * 512], tm_f(d, cb)))
        self.phase_proj(nrow, setup, fm, tm)

    def phase_hg_core(self, onorm_col):
        k, nc, T = self.k, self.nc, self.T
        tiles = tiles_of(T, 512)
        for sweep in (1, 0):
            d = sweep
            k.begin()
            if d == 0:
                Uin, Uex, Mk, lastcol = self.tri[:, 0:64], self.tri[:, 64:128], 0, 63
            else:
                Uin, Uex, Mk, lastcol = self.tri[:, 128:192], self.tri[:, 192:256], 2, 0
            mrep = k.sb([64, 512], F32, "mrep")
            for c in range(8):
                k.op(k.dve, (lambda e, c=c, Mk=Mk: e.tensor_copy(mrep[:, c * 64:(c + 1) * 64], self.tri[:, Mk * 64:(Mk + 1) * 64])),
                     reads=[self.tri], writes=[mrep])
            S = [k.sb([128, 128], F32, "S") for _ in range(8)]
            Sb = [k.sb([128, 128], BF16, "Sb") for _ in range(8)]
            for h in range(8):
                k.op(k.dve, (lambda e, h=h: e.memset(S[h][:], 0.0)), writes=[S[h]])
                k.op(k.pool, (lambda e, h=h: e.memset(Sb[h][:], 0.0)), writes=[Sb[h]])
            NB = 2
            qT = [k.sb([128, 512], BF16, "qT") for _ in range(NB)]
            kT = [k.sb([128, 512], BF16, "kT") for _ in range(NB)]
            ktm = [k.sb([64, 8, 128], BF16, "ktm") for _ in range(NB)]
            lf = [k.sb([64, 8, 128], F32, "lf") for _ in range(NB)]
            vt = [k.sb([64, 8, 128], BF16, "vt") for _ in range(NB)]
            eb = [k.sb([128, 512], F32, "eb") for _ in range(NB)]
            enb = [k.sb([128, 512], F32, "enb") for _ in range(NB)]
            qd = [k.sb([128, 512], BF16, "qd") for _ in range(NB)]
            kd = [k.sb([128, 512], BF16, "kd") for _ in range(NB)]
            ekd = [k.sb([64, 8, 128], F32, "ekd") for _ in range(NB)]
            kdec = [k.sb([64, 8, 128], BF16, "kdec") for _ in range(NB)]
            atm = [k.sb([64, 512], BF16, "atm") for _ in range(NB)]
            B = [k.ps([128, 512], F32, "Y%d" % i) for i in range(8)]
            ost = [k.sb([128, 512], F32, "ost") for _ in range(2)]
            if d == 0:
                obt = [k.sb([128, 512], F32, "obt") for _ in range(2)]
                gt = [k.sb([128, 512], BF16, "gt") for _ in range(2)]
                sqo_l = [k.sb([128, 512], BF16, "sqo") for _ in range(2)]
                rs_l = [k.sb([128, 512], F32, "rs") for _ in range(2)]
                fin = [k.sb([128, 512], BF16, "fin") for _ in range(2)]
            order = list(range(len(tiles)))
            if d == 1:
                order = order[::-1]

            def unit(h, b, it, t0, n, ncn, corder):
                if True:
                    rows = slice(h * 128, (h + 1) * 128)
                    Y = B[4 * b:4 * b + 4]
                    pbc, pat = Y[0], Y[3]
                    psuf = [Y[1], Y[2]]
                    pout = Y[1]
                    pst = [Y[2], Y[2]]
                    if d == 0:
                        sqo, rs = sqo_l[b], rs_l[b]
                    k.dma(k.sp, qT[b][:, 0:n], self.fmA[rows, t0:t0 + n], writes=[qT[b]])
                    k.dma(k.sp, kT[b][:, 0:n], self.fmB[d][rows, t0:t0 + n], writes=[kT[b]])
                    k.dma(k.sp, ktm[b][:, 0:ncn, :], self.tmK[d][t0:t0 + n, rows].rearrange("(c p) e -> p c e", p=64), writes=[ktm[b]])
                    k.dma(k.sp, lf[b][:, 0:ncn, :], self.tmL[d][t0:t0 + n, rows].rearrange("(c p) e -> p c e", p=64), writes=[lf[b]])
                    k.dma(k.sp, vt[b][:, 0:ncn, :], self.v_tm[t0:t0 + n, rows].rearrange("(c p) e -> p c e", p=64), writes=[vt[b]])
                    if d == 0:
                        k.dma(k.sp, obt[b][:, 0:n], self.obT[rows, t0:t0 + n], writes=[obt[b]])
                        k.dma(k.sp, gt[b][:, 0:n], self.fmG[rows, t0:t0 + n], writes=[gt[b]])
                    for c in range(ncn):
                        k.op(k.pe, (lambda e, b=b, c=c: e.matmul(pbc[:, c * 64:(c + 1) * 64], lhsT=lf[b][:, c, :], rhs=Uin,
                                                                 start=True, stop=True)), reads=[lf[b], self.tri], writes=[pbc])
                    k.op(k.act, (lambda e, b=b, n=n: e.activation(eb[b][:, 0:n], pbc[:, 0:n], AF.Exp)), reads=[pbc], writes=[eb[b]])
                    k.op(k.act, (lambda e, b=b, n=n: e.activation(enb[b][:, 0:n], pbc[:, 0:n], AF.Exp, scale=-1.0)), reads=[pbc], writes=[enb[b]])
                    k.op(k.dve, (lambda e, b=b, n=n: e.tensor_tensor(out=qd[b][:, 0:n], in0=qT[b][:, 0:n], in1=eb[b][:, 0:n], op=ALU.mult)),
                         reads=[qT[b], eb[b]], writes=[qd[b]])
                    k.op(k.dve, (lambda e, b=b, n=n: e.tensor_tensor(out=kd[b][:, 0:n], in0=kT[b][:, 0:n], in1=enb[b][:, 0:n], op=ALU.mult)),
                         reads=[kT[b], enb[b]], writes=[kd[b]])
                    for c in range(ncn):
                        ps_ = psuf[c // 4]
                        k.op(k.pe, (lambda e, b=b, c=c, ps_=ps_: e.matmul(ps_[0:64, (c % 4) * 128:(c % 4 + 1) * 128], lhsT=Uex, rhs=lf[b][:, c, :],
                                                                          start=True, stop=True)), reads=[lf[b], self.tri], writes=[ps_])
                    for half in range((ncn + 3) // 4):
                        nn = min(4, ncn - half * 4)
                        k.op(k.act, (lambda e, b=b, half=half, nn=nn, pq=psuf[half]: e.activation(
                            ekd[b][:, half * 4:half * 4 + nn, :], pq[0:64, 0:nn * 128].rearrange("p (c d) -> p c d", d=128), AF.Exp)),
                             reads=[psuf[half]], writes=[ekd[b]])
                    k.op(k.dve, (lambda e, b=b, ncn=ncn: e.tensor_tensor(out=kdec[b][:, 0:ncn, :], in0=ktm[b][:, 0:ncn, :],
                                                                        in1=ekd[b][:, 0:ncn, :], op=ALU.mult)),
                         reads=[ktm[b], ekd[b]], writes=[kdec[b]])
                    for c in range(ncn):
                        cs = slice(c * 64, (c + 1) * 64)
                        k.op(k.pe, (lambda e, b=b, cs=cs: e.matmul(pat[0:64, cs], lhsT=kd[b][:, cs], rhs=qd[b][:, cs], start=True, stop=True)),
                             reads=[kd[b], qd[b]], writes=[pat])
                    k.op(k.dve, (lambda e, b=b, n=n: e.tensor_tensor(out=atm[b][:, 0:n], in0=pat[0:64, 0:n], in1=mrep[:, 0:n], op=ALU.mult)),
                         reads=[pat, mrep], writes=[atm[b]])
                    k.mark()
                    for c in corder:
                        cs = slice(c * 64, (c + 1) * 64)
                        k.op(k.pe, (lambda e, b=b, cs=cs, h=h: e.matmul(pout[:, cs], lhsT=Sb[h][:], rhs=qd[b][:, cs], start=True, stop=False)),
                             reads=[Sb[h], qd[b]], writes=[pout])
                        k.op(k.pe, (lambda e, b=b, cs=cs, c=c: e.matmul(pout[:, cs], lhsT=vt[b][:, c, :], rhs=atm[b][:, cs], start=False, stop=True)),
                             reads=[vt[b], atm[b]], writes=[pout])
                        pp = pst[c % 2]
                        pc = slice((c % 2) * 128, (c % 2 + 1) * 128)
                        k.op(k.pe, (lambda e, b=b, c=c, pp=pp, pc=pc: e.matmul(pp[:, pc], lhsT=kdec[b][:, c, :], rhs=vt[b][:, c, :], start=True, stop=True)),
                             reads=[kdec[b], vt[b]], writes=[pp])
                        fc = c * 64 + lastcol
                        k.op(k.dve, (lambda e, b=b, h=h, pp=pp, fc=fc, pc=pc: e.scalar_tensor_tensor(
                            out=S[h][:], in0=S[h][:], scalar=eb[b][:, fc:fc + 1], in1=pp[:, pc], op0=ALU.mult, op1=ALU.add)),
                            reads=[S[h], eb[b], pp], writes=[S[h]])
                        k.op(k.act, (lambda e, h=h: e.copy(Sb[h][:], S[h][:])), reads=[S[h]], writes=[Sb[h]])
                    if d == 1:
                        os_ = ost[it % 2]
                        k.op(k.act, (lambda e, os_=os_, n=n: e.copy(os_[:, 0:n], pout[:, 0:n])), reads=[pout], writes=[os_])
                        k.dma(k.sp, self.obT[rows, t0:t0 + n], os_[:, 0:n], reads=[os_])
                    else:
                        os_ = ost[it % 2]
                        k.op(k.dve, (lambda e, os_=os_, b=b, n=n: e.tensor_tensor(out=os_[:, 0:n], in0=pout[:, 0:n], in1=obt[b][:, 0:n], op=ALU.add)),
                             reads=[pout, obt[b]], writes=[os_])
                        k.op(k.act, (lambda e, os_=os_, n=n: e.activation(sqo[:, 0:n], os_[:, 0:n], AF.Square)), reads=[os_], writes=[sqo])
                        k.op(k.pe, (lambda e, n=n: e.matmul(pbc[:, 0:n], lhsT=self.ones_bf[:], rhs=sqo[:, 0:n], start=True, stop=True)),
                             reads=[self.ones_bf, sqo], writes=[pbc])
                        k.op(k.act, (lambda e, n=n: e.activation(rs[:, 0:n], pbc[:, 0:n], AF.Ln, bias=self.eps_col[:, 0:1], scale=1.0 / 128)),
                             reads=[pbc, self.eps_col], writes=[rs])
                        k.op(k.act, (lambda e, n=n: e.activation(rs[:, 0:n], rs[:, 0:n], AF.Exp, scale=-0.5)), reads=[rs], writes=[rs])
                        k.op(k.dve, (lambda e, os_=os_, n=n: e.scalar_tensor_tensor(out=os_[:, 0:n], in0=os_[:, 0:n], scalar=onorm_col,
                                                                                   in1=rs[:, 0:n], op0=ALU.mult, op1=ALU.mult)),
                             reads=[os_, rs, self.csm], writes=[os_])
                        fo = fin[it % 2]
                        k.op(k.pool, (lambda e, os_=os_, fo=fo, b=b, n=n: e.tensor_tensor(out=fo[:, 0:n], in0=os_[:, 0:n], in1=gt[b][:, 0:n], op=ALU.mult)),
                             reads=[os_, gt[b]], writes=[fo])
                        k.dma(k.sp, self.oT[rows, t0:t0 + n], fo[:, 0:n], reads=[fo])
            uid = 0
            prev_scan = []
            for ti in order:
                t0, n = tiles[ti]
                ncn = n // 64
                corder = list(range(ncn)) if d == 0 else list(range(ncn))[::-1]
                for h in range(8):
                    b = uid % NB
                    uid += 1
                    k.marks = []
                    k.set_lane("u")
                    unit(h, b, uid, t0, n, ncn, corder)
                    k.set_lane(None)
                    recs = k.lanes.pop("u", [])
                    cut = k.marks[0] if k.marks else len(recs)
                    k.emit_interleaved(prev_scan, recs[:cut])
                    prev_scan = recs[cut:]
            k.emit_recs(prev_scan)
            k.end()

    def mixer_hgrn(self, li, nrow):
        self.hg_prep(li)
        self.phase_hg_proj(nrow)
        self.phase_hg_core(self.csm[:, 32:33])
        self.phase_out_proj(self.c_wo)


    def phase_gd_proj(self, j, nrow):
        k = self.k
        W = self.a_w[j]

        def setup(T0, tl):
            ctx = {}
            ctx["tm"] = [k.sb([128, 512], F32, "tmk") for _ in tl]
            for ti, (o, n) in enumerate(tl):
                k.dma(k.sp, ctx["tm"][ti][:, 0:n], self.tmask_rep[:, T0 + o:T0 + o + n], writes=[ctx["tm"][ti]])
            ctx["sg"] = [k.sb([128, 512], F32, "sg") for _ in range(2)]
            ctx["st"] = [k.sb([128, 512], BF16, "st") for _ in range(3)]
            ctx["z"] = [k.sb([128, 16], F32, "z") for _ in range(2)]
            ctx["bat"] = [k.sb([128, 32], F32, "bat") for _ in range(2)]
            na = k.sb([128, 16], F32, "negA")
            k.op(k.act, (lambda e: e.activation(na[:], self.asm[:, 0:16], AF.Exp)), reads=[self.asm], writes=[na])
            k.op(k.dve, (lambda e: e.tensor_scalar(na[:], na[:], -1.0, None, op0=ALU.mult)), reads=[na], writes=[na])
            ctx["negA"] = na
            ctx["i"] = 0
            return ctx

        def fm_pre(c):
            def post(ctx, pt, T0, ti, o, n):
                i = ctx["i"]
                ctx["i"] += 1
                st = ctx["st"][i % 3]
                k.op(k.dve, (lambda e: e.tensor_tensor(out=st[:, 0:n], in0=pt[:, 0:n], in1=ctx["tm"][ti][:, 0:n], op=ALU.mult)),
                     reads=[pt, ctx["tm"][ti]], writes=[st])
                k.dma(k.sp, self.pre[c * 128:(c + 1) * 128, T0 + o:T0 + o + n], st[:, 0:n], reads=[st])
            return post

        def fm_g(c):
            def post(ctx, pt, T0, ti, o, n):
                i = ctx["i"]
                ctx["i"] += 1
                st = ctx["st"][i % 3]
                k.op(k.act, (lambda e: e.activation(st[:, 0:n], pt[:, 0:n], AF.Silu)), reads=[pt], writes=[st])
                k.dma(k.sp, self.fmG[c * 128:(c + 1) * 128, T0 + o:T0 + o + n], st[:, 0:n], reads=[st])
            return post

        def tm_ba(ctx, pt, tok0, r):
            i = ctx["i"]
            ctx["i"] += 1
            z, bat = ctx["z"][i % 2], ctx["bat"][i % 2]
            blk = tok0 // 128
            assert tok0 % 128 == 0
            k.op(k.dve, (lambda e: e.tensor_tensor(out=z[0:r, :], in0=pt[0:r, 16:32], in1=self.asm[0:r, 16:32], op=ALU.add)),
                 reads=[pt, self.asm], writes=[z])
            k.op(k.act, (lambda e: e.activation(z[0:r, :], z[0:r, :], AF.Exp)), reads=[z], writes=[z])
            k.op(k.act, (lambda e: e.activation(z[0:r, :], z[0:r, :], AF.Ln, bias=self.one_col[0:r, 0:1])), reads=[z, self.one_col], writes=[z])
            k.op(k.dve, (lambda e: e.tensor_tensor(out=bat[0:r, 16:32], in0=z[0:r, :], in1=ctx["negA"][0:r, :], op=ALU.mult)),
                 reads=[z, ctx["negA"]], writes=[bat])
            k.op(k.act, (lambda e: e.activation(bat[0:r, 0:16], pt[0:r, 0:16], AF.Sigmoid)), reads=[pt], writes=[bat])
            k.op(k.dve, (lambda e: e.tensor_scalar(bat[0:r, 0:16], bat[0:r, 0:16], self.tmc[0:r, blk:blk + 1], None, op0=ALU.mult)),
                 reads=[bat, self.tmc], writes=[bat])
            k.dma(k.sp, self.ba_tm[tok0:tok0 + r, :], bat[0:r, :], reads=[bat])

        fm = []
        for c in range(24):
            fm.append((W[:, c * 128:(c + 1) * 128], fm_pre(c)))
        for c in range(8):
            fm.append((W[:, 3072 + c * 128:3072 + (c + 1) * 128], fm_g(c)))
        tm = [(W[:, 4096:4128], tm_ba, 32)]
        self.phase_proj(nrow, setup, fm, tm)

    def phase_gd_conv(self):
        k, T = self.k, self.T
        tiles = tiles_of(T, 512)
        QS = 128 ** -0.5
        k.begin()
        xin = [k.sb([128, 516], BF16, "cx") for _ in range(3)]
        acc = [k.sb([128, 512], F32, "cacc") for _ in range(2)]
        sv = [k.sb([128, 512], F32, "csv") for _ in range(2)]
        sq = [k.sb([128, 512], BF16, "csq") for _ in range(2)]
        rs = [k.sb([128, 512], F32, "crs") for _ in range(2)]
        ob = [k.sb([128, 512], BF16, "cob") for _ in range(3)]
        tmt = [k.sb([128, 512], F32, "ctm") for _ in range(2)]
        tb = [k.sb([128, 4, 128], BF16, "ctb") for _ in range(2)]
        pss = [k.ps([128, 512], F32, "cps") for _ in range(2)]
        ptb = [k.ps([128, 512], BF16, "cpt") for _ in range(2)]
        it = 0
        for ti, (t0, n) in enumerate(tiles):
            tmk = tmt[ti % 2]
            k.dma(k.sp, tmk[:, 0:n], self.tmask_rep[:, t0:t0 + n], writes=[tmk])
            for cc in range(24):
                kind = cc // 8
                x = xin[it % 3]
                a_, s_, q_, r_, o_ = acc[it % 2], sv[it % 2], sq[it % 2], rs[it % 2], ob[it % 3]
                ps = pss[it % 2]
                if it % 2 == 0:
                    k.merge()
                k.set_lane(it % 2)
                it += 1
                lo = max(t0 - 2, 0)
                hi = min(t0 + n + 2, T)
                if lo > t0 - 2 or hi < t0 + n + 2:
                    k.op(k.pool, (lambda e, x=x: e.memset(x[:], 0.0)), writes=[x])
                k.dma(k.sp, x[:, lo - (t0 - 2):hi - (t0 - 2)], self.pre[cc * 128:(cc + 1) * 128, lo:hi], writes=[x])
                wcol = lambda jj, cc=cc: self.asm[:, 33 + cc * 5 + jj:34 + cc * 5 + jj]
                k.op(k.dve, (lambda e, a_=a_, x=x, n=n, w=wcol(0): e.tensor_scalar(a_[:, 0:n], x[:, 0:n], w, None, op0=ALU.mult)),
                     reads=[x, self.asm], writes=[a_])
                for jj in range(1, 5):
                    k.op(k.dve, (lambda e, a_=a_, x=x, n=n, jj=jj, w=wcol(jj): e.scalar_tensor_tensor(
                        out=a_[:, 0:n], in0=x[:, jj:jj + n], scalar=w, in1=a_[:, 0:n], op0=ALU.mult, op1=ALU.add)),
                        reads=[x, a_, self.asm], writes=[a_])
                k.op(k.act, (lambda e, a_=a_, s_=s_, n=n: e.activation(s_[:, 0:n], a_[:, 0:n], AF.Silu)), reads=[a_], writes=[s_])
                if kind < 2:
                    k.op(k.act, (lambda e, s_=s_, q_=q_, n=n: e.activation(q_[:, 0:n], s_[:, 0:n], AF.Square)), reads=[s_], writes=[q_])
                    k.op(k.pe, (lambda e, ps=ps, q_=q_, n=n: e.matmul(ps[:, 0:n], lhsT=self.ones_bf[:], rhs=q_[:, 0:n], start=True, stop=True)),
                         reads=[self.ones_bf, q_], writes=[ps])
                    k.op(k.act, (lambda e, ps=ps, r_=r_, n=n: e.activation(r_[:, 0:n], ps[:, 0:n], AF.Ln, bias=self.eps_col[:, 0:1])),
                         reads=[ps, self.eps_col], writes=[r_])
                    k.op(k.act, (lambda e, r_=r_, n=n: e.activation(r_[:, 0:n], r_[:, 0:n], AF.Exp, scale=-0.5)), reads=[r_], writes=[r_])
                if kind == 0:
                    k.op(k.dve, (lambda e, o_=o_, s_=s_, r_=r_, n=n: e.scalar_tensor_tensor(
                        out=o_[:, 0:n], in0=s_[:, 0:n], scalar=QS, in1=r_[:, 0:n], op0=ALU.mult, op1=ALU.mult)),
                        reads=[s_, r_], writes=[o_])
                    k.dma(k.sp, self.fmA[cc * 128:(cc + 1) * 128, t0:t0 + n], o_[:, 0:n], reads=[o_])
                    k.set_lane(None)
                    continue
                if kind == 1:
                    k.op(k.dve, (lambda e, s_=s_, r_=r_, n=n: e.tensor_tensor(out=s_[:, 0:n], in0=s_[:, 0:n], in1=r_[:, 0:n], op=ALU.mult)),
                         reads=[s_, r_], writes=[s_])
                    k.op(k.pool, (lambda e, o_=o_, s_=s_, tmk=tmk, n=n: e.tensor_tensor(out=o_[:, 0:n], in0=s_[:, 0:n], in1=tmk[:, 0:n], op=ALU.mult)),
                         reads=[s_, tmk], writes=[o_])
                    k.dma(k.sp, self.fmB[0][(cc - 8) * 128:(cc - 7) * 128, t0:t0 + n], o_[:, 0:n], reads=[o_])
                    dst = self.tmK[0]
                else:
                    k.op(k.pool, (lambda e, o_=o_, s_=s_, n=n: e.tensor_copy(o_[:, 0:n], s_[:, 0:n])), reads=[s_], writes=[o_])
                    dst = self.v_tm
                cl = (cc % 8)
                pt = ptb[it % 2]
                tt = tb[it % 2]
                nb = (n + 127) // 128
                for b in range(nb):
                    r = min(128, n - b * 128)
                    k.op(k.pe, (lambda e, pt=pt, o_=o_, b=b, r=r: e.transpose(pt[0:r, b * 128:(b + 1) * 128], o_[:, b * 128:b * 128 + r],
                                                                             self.identb[:])),
                         reads=[o_, self.identb], writes=[pt])
                if n % 128 == 0:
                    k.op(k.act, (lambda e, pt=pt, tt=tt, nb=nb: e.copy(tt[:, 0:nb, :], pt[:, 0:nb * 128].rearrange("p (b d) -> p b d", d=128))),
                         reads=[pt], writes=[tt])
                    k.dma(k.sp, dst[t0:t0 + n, cl * 128:(cl + 1) * 128].rearrange("(b p) d -> p b d", p=128), tt[:, 0:nb, :], reads=[tt])
                else:
                    for b in range(nb):
                        r = min(128, n - b * 128)
                        k.op(k.act, (lambda e, pt=pt, tt=tt, b=b, r=r: e.copy(tt[0:r, b, :], pt[0:r, b * 128:(b + 1) * 128])),
                             reads=[pt], writes=[tt])
                        k.dma(k.sp, dst[t0 + b * 128:t0 + b * 128 + r, cl * 128:(cl + 1) * 128], tt[0:r, b, :], reads=[tt])
                k.set_lane(None)
        k.merge()
        k.end()

    def phase_gd_core(self):
        import os
        GDCUT = int(os.environ.get("GD_CUT", "99"))
        GDSUB = int(os.environ.get("GD_SUB", "99"))
        k, nc, T = self.k, self.nc, self.T
        tiles = tiles_of(T, 512)
        onorm_col = self.asm[:, 32:33]
        for sweep in (1, 0):
            d = sweep
            k.begin()
            tri = self.tri
            if d == 0:
                Uin, MS, MI, MN = tri[:, 0:64], 1, 0, 3
            else:
                Uin, MS, MI, MN = tri[:, 128:192], 3, 2, 1
            mI = k.sb([64, 512], F32, "mI")
            mS = k.sb([64, 512], F32, "mS")
            mN = k.sb([64, 512], F32, "mN")
            idr = k.sb([64, 512], F32, "idr")
            for c in range(8):
                cs = slice(c * 64, (c + 1) * 64)
                k.op(k.dve, (lambda e, cs=cs: e.tensor_copy(mI[:, cs], tri[:, MI * 64:(MI + 1) * 64])), reads=[tri], writes=[mI])
                k.op(k.dve, (lambda e, cs=cs: e.tensor_copy(mS[:, cs], tri[:, MS * 64:(MS + 1) * 64])), reads=[tri], writes=[mS])
                k.op(k.dve, (lambda e, cs=cs: e.tensor_copy(mN[:, cs], tri[:, MN * 64:(MN + 1) * 64])), reads=[tri], writes=[mN])
                k.op(k.dve, (lambda e, cs=cs: e.tensor_copy(idr[:, cs], self.ident[0:64, 0:64])), reads=[self.ident], writes=[idr])
            ones_f = k.sb([64, 128], F32, "onesf")
            k.op(k.dve, (lambda e: e.memset(ones_f[:], 1.0)), writes=[ones_f])
            S = [k.sb([128, 128], F32, "S") for _ in range(8)]
            Sb = [k.sb([128, 128], BF16, "Sb") for _ in range(8)]
            for h in range(8):
                k.op(k.dve, (lambda e, h=h: e.memset(S[h][:], 0.0)), writes=[S[h]])
                k.op(k.pool, (lambda e, h=h: e.memset(Sb[h][:], 0.0)), writes=[Sb[h]])
            NB = 2
            mk = lambda shp, dt, nm: [k.sb(shp, dt, nm) for _ in range(NB)]
            qT, kT = mk([128, 512], BF16, "qT"), mk([128, 512], BF16, "kT")
            ktm, vtm = mk([64, 8, 128], BF16, "ktm"), mk([64, 8, 128], BF16, "vtm")
            bat = mk([64, 8, 32], F32, "bat")
            lab = mk([64, 2, 8], F32, "lab")
            gc, egc, bek, dcol = mk([64, 8], F32, "gc"), mk([64, 8], F32, "egc"), mk([64, 8], F32, "bek"), mk([64, 8], F32, "dcol")
            alast = mk([128, 8], F32, "alast")
            dg = mk([64, 512], F32, "dg")
            egr = mk([128, 512], F32, "egr")
            qd = mk([128, 512], BF16, "qd")
            fab = mk([64, 512], F32, "fab")
            fm_ = mk([64, 512], F32, "fm")
            fmi = mk([64, 512], F32, "fmi")
            gf = mk([64, 512], F32, "gf")
            a32 = mk([64, 512], F32, "a32")
            Rb = [mk([64, 512], F32, "Rb%d" % i) for i in range(2)]
            Pb = [mk([64, 512], F32, "Pb%d" % i) for i in range(2)]
            PTb = [mk([64, 512], F32, "PTb%d" % i) for i in range(2)]
            rhu, rhw, kdec = mk([64, 8, 128], F32, "rhu"), mk([64, 8, 128], F32, "rhw"), mk([64, 8, 128], BF16, "kdec")
            gfu = mk([64, 512], F32, "gfu")
            dgb = mk([64, 512], F32, "dgb")
            u_sb = mk([64, 8, 128], F32, "u")
            nwT = mk([128, 512], BF16, "nwT")
            qkm = mk([64, 512], BF16, "qkm")
            vn = [[k.sb([64, 128], BF16, "vn") for _ in range(2)] for _ in range(NB)]
            ost = [k.sb([128, 512], F32, "ost") for _ in range(2)]
            B = [k.ps([128, 512], F32, "B%d" % i) for i in range(8)]
            if d == 0:
                obt = mk([128, 512], F32, "obt")
                gt = mk([128, 512], BF16, "gt")
                sqo_l = mk([128, 512], BF16, "sqo")
                rs_l = mk([128, 512], F32, "rs")
                fin = [k.sb([128, 512], BF16, "fin") for _ in range(2)]
            order = list(range(len(tiles)))
            if d == 1:
                order = order[::-1]

            def unit(h, b, it, t0, n, ncn, corder):
                if True:
                    rows = slice(h * 128, (h + 1) * 128)
                    X = B[4 * b:4 * b + 4]
                    Bs, Brow, BG, Bb = X
                    BD, BP, BT = X[1], X[2], X[3]
                    Bq, Bo, Bst, Bv = X[0], X[1], X[2], X[0]
                    k.dma(k.sp, qT[b][:, 0:n], self.fmA[rows, t0:t0 + n], writes=[qT[b]])
                    k.dma(k.sp, kT[b][:, 0:n], self.fmB[0][rows, t0:t0 + n], writes=[kT[b]])
                    k.dma(k.sp, ktm[b][:, 0:ncn, :], self.tmK[0][t0:t0 + n, rows].rearrange("(c p) e -> p c e", p=64), writes=[ktm[b]])
                    k.dma(k.sp, vtm[b][:, 0:ncn, :], self.v_tm[t0:t0 + n, rows].rearrange("(c p) e -> p c e", p=64), writes=[vtm[b]])
                    k.dma(k.sp, bat[b][:, 0:ncn, :], self.ba_tm[t0:t0 + n, :].rearrange("(c p) e -> p c e", p=64), writes=[bat[b]])
                    if d == 0:
                        k.dma(k.sp, obt[b][:, 0:n], self.obT[rows, t0:t0 + n], writes=[obt[b]])
                        k.dma(k.sp, gt[b][:, 0:n], self.fmG[rows, t0:t0 + n], writes=[gt[b]])
                    col = d * 8 + h
                    k.op(k.dve, (lambda e, b=b, ncn=ncn, col=col: e.tensor_copy(lab[b][:, 0, 0:ncn], bat[b][:, 0:ncn, col])),
                         reads=[bat[b]], writes=[lab[b]])
                    k.op(k.dve, (lambda e, b=b, ncn=ncn, col=col: e.tensor_copy(lab[b][:, 1, 0:ncn], bat[b][:, 0:ncn, 16 + col])),
                         reads=[bat[b]], writes=[lab[b]])
                    beta = lambda c, b=b: lab[b][:, 0, c:c + 1]
                    k.op(k.pe, (lambda e, b=b, ncn=ncn: e.matmul(Bs[0:64, 0:ncn], lhsT=Uin, rhs=lab[b][:, 1, 0:ncn], start=True, stop=True)),
                         reads=[tri, lab[b]], writes=[Bs])
                    k.op(k.pe, (lambda e, b=b, ncn=ncn: e.matmul(Bs[:, 16:16 + ncn], lhsT=ones_f[:], rhs=lab[b][:, 1, 0:ncn], start=True, stop=True)),
                         reads=[ones_f, lab[b]], writes=[Bs])
                    k.op(k.dve, (lambda e, b=b, ncn=ncn: e.tensor_copy(gc[b][:, 0:ncn], Bs[0:64, 0:ncn])), reads=[Bs], writes=[gc[b]])
                    k.op(k.act, (lambda e, b=b, ncn=ncn: e.activation(alast[b][:, 0:ncn], Bs[:, 16:16 + ncn], AF.Exp)), reads=[Bs], writes=[alast[b]])
                    k.op(k.dve, (lambda e, b=b, ncn=ncn: e.tensor_tensor(out=dcol[b][:, 0:ncn], in0=Bs[0:64, 16:16 + ncn], in1=gc[b][:, 0:ncn],
                                                                        op=ALU.subtract)), reads=[Bs, gc[b]], writes=[dcol[b]])
                    k.op(k.act, (lambda e, b=b, ncn=ncn: e.activation(dcol[b][:, 0:ncn], dcol[b][:, 0:ncn], AF.Exp)), reads=[dcol[b]], writes=[dcol[b]])
                    k.op(k.act, (lambda e, b=b, ncn=ncn: e.activation(egc[b][:, 0:ncn], gc[b][:, 0:ncn], AF.Exp)), reads=[gc[b]], writes=[egc[b]])
                    k.op(k.dve, (lambda e, b=b, ncn=ncn: e.tensor_tensor(out=bek[b][:, 0:ncn], in0=egc[b][:, 0:ncn], in1=lab[b][:, 0, 0:ncn],
                                                                        op=ALU.mult)), reads=[egc[b], lab[b]], writes=[bek[b]])
                    if GDCUT < 1:
                        return
                    k.op(k.dve, (lambda e, b=b, n=n, ncn=ncn: e.tensor_tensor(
                        out=dg[b][:, 0:n].rearrange("p (c i) -> p c i", i=64), in0=idr[:, 0:n].rearrange("p (c i) -> p c i", i=64),
                        in1=gc[b][:, 0:ncn].unsqueeze(2).to_broadcast([64, ncn, 64]), op=ALU.mult)), reads=[idr, gc[b]], writes=[dg[b]])
                    for c in range(ncn):
                        cs = slice(c * 64, (c + 1) * 64)
                        k.op(k.pe, (lambda e, b=b, cs=cs: e.matmul(Brow[:, cs], lhsT=ones_f[:], rhs=dg[b][:, cs], start=True, stop=True)),
                             reads=[ones_f, dg[b]], writes=[Brow])
                    k.op(k.act, (lambda e, b=b, n=n: e.activation(egr[b][:, 0:n], Brow[:, 0:n], AF.Exp)), reads=[Brow], writes=[egr[b]])
                    k.op(k.dve, (lambda e, b=b, n=n: e.tensor_tensor(out=qd[b][:, 0:n], in0=qT[b][:, 0:n], in1=egr[b][:, 0:n], op=ALU.mult)),
                         reads=[qT[b], egr[b]], writes=[qd[b]])
                    k.op(k.dve, (lambda e, b=b, n=n, ncn=ncn: e.tensor_tensor(
                        out=fab[b][:, 0:n].rearrange("p (c i) -> p c i", i=64), in0=Brow[0:64, 0:n].rearrange("p (c i) -> p c i", i=64),
                        in1=gc[b][:, 0:ncn].unsqueeze(2).to_broadcast([64, ncn, 64]), op=ALU.subtract)), reads=[Brow, gc[b]], writes=[fab[b]])
                    k.op(k.act, (lambda e, b=b, n=n: e.activation(fab[b][:, 0:n], fab[b][:, 0:n], AF.Abs)), reads=[fab[b]], writes=[fab[b]])
                    k.op(k.act, (lambda e, b=b, n=n: e.activation(fm_[b][:, 0:n], fab[b][:, 0:n], AF.Exp, scale=-1.0)), reads=[fab[b]], writes=[fm_[b]])
                    k.op(k.pool, (lambda e, b=b, n=n: e.tensor_tensor(out=fmi[b][:, 0:n], in0=fm_[b][:, 0:n], in1=mI[:, 0:n], op=ALU.mult)),
                         reads=[fm_[b], mI], writes=[fmi[b]])
                    if GDCUT < 2:
                        return
                    k.op(k.dve, (lambda e, b=b, n=n, ncn=ncn: e.tensor_tensor(
                        out=dgb[b][:, 0:n].rearrange("p (c i) -> p c i", i=64), in0=idr[:, 0:n].rearrange("p (c i) -> p c i", i=64),
                        in1=lab[b][:, 0, 0:ncn].unsqueeze(2).to_broadcast([64, ncn, 64]), op=ALU.mult)), reads=[idr, lab[b]], writes=[dgb[b]])
                    for c in range(ncn):
                        cs = slice(c * 64, (c + 1) * 64)
                        k.op(k.pe, (lambda e, b=b, cs=cs: e.matmul(BG[0:64, cs], lhsT=kT[b][:, cs], rhs=kT[b][:, cs], start=True, stop=True)),
                             reads=[kT[b]], writes=[BG])
                    for c in range(ncn):
                        cs = slice(c * 64, (c + 1) * 64)
                        k.op(k.pe, (lambda e, b=b, cs=cs: e.matmul(Bb[:, cs], lhsT=ones_f[:], rhs=dgb[b][:, cs], start=True, stop=True)),
                             reads=[ones_f, dgb[b]], writes=[Bb])
                    k.op(k.dve, (lambda e, b=b, n=n: e.tensor_tensor(out=gf[b][:, 0:n], in0=BG[0:64, 0:n], in1=fm_[b][:, 0:n], op=ALU.mult)),
                         reads=[BG, fm_[b]], writes=[gf[b]])
                    k.op(k.pool, (lambda e, b=b, n=n: e.tensor_tensor(out=gfu[b][:, 0:n], in0=gf[b][:, 0:n], in1=mN[:, 0:n], op=ALU.mult)),
                         reads=[gf[b], mN], writes=[gfu[b]])
                    k.op(k.pool, (lambda e, b=b, n=n: e.tensor_tensor(out=gf[b][:, 0:n], in0=gf[b][:, 0:n], in1=mS[:, 0:n], op=ALU.mult)),
                         reads=[gf[b], mS], writes=[gf[b]])
                    R0, P0, PT0 = Rb[0][b], Pb[0][b], PTb[0][b]
                    k.op(k.dve, (lambda e, b=b, n=n, ncn=ncn, PT0=PT0: e.tensor_tensor(
                        out=PT0[:, 0:n].rearrange("p (c i) -> p c i", i=64), in0=gf[b][:, 0:n].rearrange("p (c i) -> p c i", i=64),
                        in1=lab[b][:, 0, 0:ncn].unsqueeze(2).to_broadcast([64, ncn, 64]), op=ALU.mult)), reads=[gf[b], lab[b]], writes=[PT0])
                    k.op(k.dve, (lambda e, b=b, n=n, P0=P0: e.tensor_tensor(out=P0[:, 0:n], in0=Bb[0:64, 0:n], in1=gfu[b][:, 0:n], op=ALU.mult)),
                         reads=[Bb, gfu[b]], writes=[P0])
                    k.op(k.pool, (lambda e, n=n, R0=R0, P0=P0: e.tensor_tensor(out=R0[:, 0:n], in0=idr[:, 0:n], in1=P0[:, 0:n], op=ALU.subtract)),
                         reads=[P0, idr], writes=[R0])
                    if GDCUT < 3:
                        return
                    for lv in range(6):
                        cur, nxt = lv % 2, (lv + 1) % 2
                        Rc, Pc, PTc = Rb[cur][b], Pb[cur][b], PTb[cur][b]
                        Rn, Pn, PTn = Rb[nxt][b], Pb[nxt][b], PTb[nxt][b]
                        for c in range(ncn):
                            cs = slice(c * 64, (c + 1) * 64)
                            if lv >= 1:
                                k.op(k.pe, (lambda e, cs=cs, PTc=PTc, Rc=Rc: e.matmul(BD[0:64, cs], lhsT=PTc[:, cs], rhs=Rc[:, cs], start=True, stop=True)),
                                     reads=[PTc, Rc], writes=[BD])
                            if lv <= 4:
                                k.op(k.pe, (lambda e, cs=cs, PTc=PTc, Pc=Pc: e.matmul(BP[0:64, cs], lhsT=PTc[:, cs], rhs=Pc[:, cs], start=True, stop=True)),
                                     reads=[PTc, Pc], writes=[BP])
                                k.op(k.pe, (lambda e, cs=cs, PTc=PTc, Pc=Pc: e.matmul(BT[0:64, cs], lhsT=Pc[:, cs], rhs=PTc[:, cs], start=True, stop=True)),
                                     reads=[PTc, Pc], writes=[BT])
                        if lv >= 1:
                            k.op(k.dve, (lambda e, n=n, Rn=Rn, Rc=Rc: e.tensor_tensor(out=Rn[:, 0:n], in0=BD[0:64, 0:n], in1=Rc[:, 0:n], op=ALU.add)),
                                 reads=[BD, Rc], writes=[Rn])
                        else:
                            k.op(k.pool, (lambda e, n=n, Rn=Rn, Rc=Rc: e.tensor_copy(Rn[:, 0:n], Rc[:, 0:n])), reads=[Rc], writes=[Rn])
                        if lv <= 4:
                            k.op(k.act, (lambda e, n=n, Pn=Pn: e.copy(Pn[:, 0:n], BP[0:64, 0:n])), reads=[BP], writes=[Pn])
                            k.op(k.act, (lambda e, n=n, PTn=PTn: e.copy(PTn[:, 0:n], BT[0:64, 0:n])), reads=[BT], writes=[PTn])
                    if GDCUT < 4:
                        return
                    Rf = Rb[0][b]
                    k.op(k.dve, (lambda e, b=b, ncn=ncn: e.tensor_tensor(out=rhu[b][:, 0:ncn, :], in0=vtm[b][:, 0:ncn, :],
                                                                        in1=lab[b][:, 0, 0:ncn].unsqueeze(2).to_broadcast([64, ncn, 128]), op=ALU.mult)),
                         reads=[vtm[b], lab[b]], writes=[rhu[b]])
                    k.op(k.pool, (lambda e, b=b, ncn=ncn: e.tensor_tensor(out=rhw[b][:, 0:ncn, :], in0=ktm[b][:, 0:ncn, :],
                                                                         in1=bek[b][:, 0:ncn].unsqueeze(2).to_broadcast([64, ncn, 128]), op=ALU.mult)),
                         reads=[ktm[b], bek[b]], writes=[rhw[b]])
                    k.op(k.pool, (lambda e, b=b, ncn=ncn: e.tensor_tensor(out=kdec[b][:, 0:ncn, :], in0=ktm[b][:, 0:ncn, :],
                                                                         in1=dcol[b][:, 0:ncn].unsqueeze(2).to_broadcast([64, ncn, 128]), op=ALU.mult)),
                         reads=[ktm[b], dcol[b]], writes=[kdec[b]])
                    for c in range(ncn):
                        cs = slice(c * 64, (c + 1) * 64)
                        pu = BD if c < 4 else BP
                        us = slice((c % 4) * 128, (c % 4 + 1) * 128)
                        k.op(k.pe, (lambda e, b=b, c=c, cs=cs, pu=pu, us=us, Rf=Rf: e.matmul(pu[0:64, us], lhsT=Rf[:, cs], rhs=rhu[b][:, c, :], start=True, stop=True)),
                             reads=[Rf, rhu[b]], writes=[pu])
                        k.op(k.pe, (lambda e, b=b, c=c, cs=cs, Rf=Rf: e.matmul(BT[:, cs], lhsT=rhw[b][:, c, :], rhs=Rf[:, cs], start=True, stop=True)),
                             reads=[Rf, rhw[b]], writes=[BT])
                        k.op(k.pe, (lambda e, b=b, cs=cs: e.matmul(Bq[0:64, cs], lhsT=kT[b][:, cs], rhs=qT[b][:, cs], start=True, stop=True)),
                             reads=[kT[b], qT[b]], writes=[Bq])
                    n4 = min(ncn, 4)
                    k.op(k.act, (lambda e, b=b, n4=n4: e.copy(u_sb[b][:, 0:n4, :], BD[0:64, 0:n4 * 128].rearrange("p (c d) -> p c d", d=128))),
                         reads=[BD], writes=[u_sb[b]])
                    if ncn > 4:
                        k.op(k.act, (lambda e, b=b, ncn=ncn: e.copy(u_sb[b][:, 4:ncn, :], BP[0:64, 0:(ncn - 4) * 128].rearrange("p (c d) -> p c d", d=128))),
                             reads=[BP], writes=[u_sb[b]])
                    k.op(k.dve, (lambda e, b=b, n=n: e.tensor_scalar(nwT[b][:, 0:n], BT[:, 0:n], -1.0, None, op0=ALU.mult)), reads=[BT], writes=[nwT[b]])
                    k.op(k.dve, (lambda e, b=b, n=n: e.tensor_tensor(out=qkm[b][:, 0:n], in0=Bq[0:64, 0:n], in1=fmi[b][:, 0:n], op=ALU.mult)),
                         reads=[Bq, fmi[b]], writes=[qkm[b]])
                    if GDCUT < 5:
                        return
                    k.mark()
                    for c in corder:
                        cs = slice(c * 64, (c + 1) * 64)
                        v_ = vn[b][c % 2]
                        k.op(k.pe, (lambda e, b=b, cs=cs, h=h: e.matmul(Bv[0:64, 256:384], lhsT=nwT[b][:, cs], rhs=Sb[h][:], start=True, stop=True)),
                             reads=[nwT[b], Sb[h]], writes=[Bv])
                        k.op(k.dve, (lambda e, b=b, c=c, v_=v_: e.tensor_tensor(out=v_[:], in0=Bv[0:64, 256:384], in1=u_sb[b][:, c, :], op=ALU.add)),
                             reads=[Bv, u_sb[b]], writes=[v_])
                        k.op(k.pe, (lambda e, b=b, cs=cs, h=h: e.matmul(Bo[:, cs], lhsT=Sb[h][:], rhs=qd[b][:, cs], start=True, stop=False)),
                             reads=[Sb[h], qd[b]], writes=[Bo])
                        k.op(k.pe, (lambda e, b=b, cs=cs, v_=v_: e.matmul(Bo[:, cs], lhsT=v_[:], rhs=qkm[b][:, cs], start=False, stop=True)),
                             reads=[v_, qkm[b]], writes=[Bo])
                        k.op(k.pe, (lambda e, b=b, c=c, v_=v_: e.matmul(Bs[:, 128:256], lhsT=kdec[b][:, c, :], rhs=v_[:], start=True, stop=True)),
                             reads=[kdec[b], v_], writes=[Bs])
                        k.op(k.dve, (lambda e, b=b, h=h, c=c: e.scalar_tensor_tensor(
                            out=S[h][:], in0=S[h][:], scalar=alast[b][:, c:c + 1], in1=Bs[:, 128:256], op0=ALU.mult, op1=ALU.add)),
                            reads=[S[h], alast[b], Bs], writes=[S[h]])
                        k.op(k.act, (lambda e, h=h: e.copy(Sb[h][:], S[h][:])), reads=[S[h]], writes=[Sb[h]])
                    if GDCUT < 6:
                        return
                    os_ = ost[it % 2]
                    if d == 1:
                        k.op(k.act, (lambda e, os_=os_, n=n: e.copy(os_[:, 0:n], Bo[:, 0:n])), reads=[Bo], writes=[os_])
                        k.dma(k.sp, self.obT[rows, t0:t0 + n], os_[:, 0:n], reads=[os_])
                    else:
                        sqo, rs = sqo_l[b], rs_l[b]
                        k.op(k.dve, (lambda e, os_=os_, b=b, n=n: e.tensor_tensor(out=os_[:, 0:n], in0=Bo[:, 0:n], in1=obt[b][:, 0:n], op=ALU.add)),
                             reads=[Bo, obt[b]], writes=[os_])
                        k.op(k.act, (lambda e, os_=os_, n=n: e.activation(sqo[:, 0:n], os_[:, 0:n], AF.Square)), reads=[os_], writes=[sqo])
                        k.op(k.pe, (lambda e, n=n: e.matmul(Bst[:, 0:n], lhsT=self.ones_bf[:], rhs=sqo[:, 0:n], start=True, stop=True)),
                             reads=[self.ones_bf, sqo], writes=[Bst])
                        k.op(k.act, (lambda e, n=n: e.activation(rs[:, 0:n], Bst[:, 0:n], AF.Ln, bias=self.eps_col[:, 0:1], scale=1.0 / 128)),
                             reads=[Bst, self.eps_col], writes=[rs])
                        k.op(k.act, (lambda e, n=n: e.activation(rs[:, 0:n], rs[:, 0:n], AF.Exp, scale=-0.5)), reads=[rs], writes=[rs])
                        k.op(k.dve, (lambda e, os_=os_, n=n: e.scalar_tensor_tensor(out=os_[:, 0:n], in0=os_[:, 0:n], scalar=onorm_col,
                                                                                   in1=rs[:, 0:n], op0=ALU.mult, op1=ALU.mult)),
                             reads=[os_, rs, self.asm], writes=[os_])
                        fo = fin[it % 2]
                        k.op(k.pool, (lambda e, os_=os_, fo=fo, b=b, n=n: e.tensor_tensor(out=fo[:, 0:n], in0=os_[:, 0:n], in1=gt[b][:, 0:n], op=ALU.mult)),
                             reads=[os_, gt[b]], writes=[fo])
                        k.dma(k.sp, self.oT[rows, t0:t0 + n], fo[:, 0:n], reads=[fo])
            uid = 0
            prev_scan = []
            for ti in order:
                t0, n = tiles[ti]
                ncn = n // 64
                corder = list(range(ncn)) if d == 0 else list(range(ncn))[::-1]
                for h in range(8):
                    b = uid % NB
                    uid += 1
                    k.marks = []
                    k.set_lane("u")
                    unit(h, b, uid, t0, n, ncn, corder)
                    k.set_lane(None)
                    recs = k.lanes.pop("u", [])
                    cut = k.marks[0] if k.marks else len(recs)
                    k.emit_interleaved(prev_scan, recs[:cut])
                    prev_scan = recs[cut:]
            k.emit_recs(prev_scan)
            if d == 1 and os.environ.get("DEBUG_DUMP"):
                o = self.nc.dram_tensor("dbg_S0", [128, 128], F32, kind="ExternalOutput").ap()
                k.dma(k.sp, o, S[0][:], reads=[S[0]])
                o = self.nc.dram_tensor("dbg_u", [64, 8 * 128], F32, kind="ExternalOutput").ap()
                k.dma(k.sp, o, u_sb[0][:].rearrange("p c d -> p (c d)"), reads=[u_sb[0]])
                o = self.nc.dram_tensor("dbg_rhu", [64, 8 * 128], BF16, kind="ExternalOutput").ap()
                k.dma(k.sp, o, rhu[0][:].rearrange("p c d -> p (c d)"), reads=[rhu[0]])
                o = self.nc.dram_tensor("dbg_R", [64, 512], BF16, kind="ExternalOutput").ap()
                k.dma(k.sp, o, Rb[0][0][:], reads=[Rb[0][0]])
                o = self.nc.dram_tensor("dbg_alast", [128, 8], F32, kind="ExternalOutput").ap()
                k.dma(k.sp, o, alast[0][:], reads=[alast[0]])
            k.end()

    def mixer_gdn(self, j, nrow):
        k = self.k
        k.begin()
        k.dma(k.sp, self.asm[:], self.a_small[j], writes=[self.asm])
        k.end()
        import os
        st = int(os.environ.get("GD_STAGE", "9"))
        self.phase_gd_proj(j, nrow)
        if st >= 2:
            self.phase_gd_conv()
        if st >= 3:
            self.phase_gd_core()
        if st >= 4:
            self.phase_out_proj(self.a_wo[j])

    def build(self):
        k = self.k
        self.eps_col = k.gsb([128, 1], F32, "epscol")
        k.begin()
        k.op(k.dve, lambda e: e.memset(self.eps_col[:], EPS), writes=[self.eps_col])
        self.bsm = k.gsb([128, 260], F32, "bsm")
        k.dma(k.sp, self.bsm[:], self.b_small, writes=[self.bsm])
        self.csm = k.gsb([128, 33], F32, "csm")
        k.dma(k.sp, self.csm[:], self.c_small, writes=[self.csm])
        self.tri = k.gsb([64, 256], F32, "tric")
        k.dma(k.sp, self.tri[:], self.tri_in, writes=[self.tri])
        self.tmc = k.gsb([128, (self.T + 127) // 128], F32, "tmc")
        k.dma(k.sp, self.tmc[:], self.tmask_col, writes=[self.tmc])
        self.one_col = k.gsb([128, 1], F32, "onecol")
        k.op(k.dve, lambda e: e.memset(self.one_col[:], 1.0), writes=[self.one_col])
        self.lb_col = k.gsb([128, 16], F32, "lbcol")
        self.asm = k.gsb([128, 153], F32, "asm")
        self.identb = k.gsb([128, 128], BF16, "identbc")
        k.dma(k.sp, self.identb[:], self.identb_in, writes=[self.identb])
        self.oml_row = k.gsb([128, D_MODEL], F32, "omlrow")
        k.end()
        self.phase_in()
        if self.only == "gdn":
            self.mixer_gdn(0, 0 * 3 + 1)
            self.phase_out(self.depth * 3)
            import os
            if os.environ.get("DEBUG_DUMP"):
                k.begin()
                dummy = k.sb([128, 4], F32, "dummy")
                for nm, ap, shp, dt in (("fmA", self.fmA, [D_MODEL, self.T], BF16), ("fmB0", self.fmB[0], [D_MODEL, self.T], BF16),
                                        ("tmK0", self.tmK[0], [self.T, D_MODEL], BF16), ("v_tm", self.v_tm, [self.T, D_MODEL], BF16),
                                        ("ba_tm", self.ba_tm, [self.T, 32], F32), ("obT", self.obT, [D_MODEL, self.T], F32),
                                        ("oT", self.oT, [D_MODEL, self.T], BF16), ("fmG", self.fmG, [D_MODEL, self.T], BF16)):
                    o = self.nc.dram_tensor("dbg_" + nm, shp, dt, kind="ExternalOutput").ap()
                    k.dma(k.sp, o, ap, reads=[dummy])
                k.end()
            return
        if self.only == "hgrn":
            self.mixer_hgrn(2, 2 * 3 + 1)
            self.phase_out(self.depth * 3)
            return
        if self.only == "attn":
            self.mixer_attn(1 * 3 + 1)
            self.phase_out(self.depth * 3)
            return
        for li in range(self.depth):
            if self.ffn:
                self.phase_ffn(li, 0, li * 3 + 0)
            if self.mixers and li % 3 == 1:
                self.mixer_attn(li * 3 + 1)
            if self.mixers and li % 3 == 2:
                self.mixer_hgrn(li, li * 3 + 1)
            if self.mixers and li % 3 == 0:
                self.mixer_gdn(li // 3, li * 3 + 1)
            if self.ffn:
                self.phase_ffn(li, 1, li * 3 + 2)
        self.phase_out(self.depth * 3)


def col_layout(a):
    a = np.asarray(a, np.float32)
    R = a.shape[0]
    C = a.shape[1] // 128
    return np.ascontiguousarray(a.reshape(R, C, 128).transpose(2, 0, 1).reshape(128, R * C))


def attn_host_inputs(b_w_in, b_lambda, b_sub_norm, layer_idx, T, n_valid):
    f32 = np.float32
    w = np.asarray(b_w_in, f32)
    perm = np.arange(2048)
    d = perm % 64
    perm = np.where(d < 8, perm + 8, np.where(d < 16, perm - 8, perm))
    w_sw = np.ascontiguousarray(w[:, perm])
    sm = np.zeros((128, 260), f32)
    sm[:, 0:256] = np.asarray(b_lambda, f32).reshape(1, 256)
    sm[:, 256] = np.asarray(b_sub_norm, f32).reshape(128)
    dd = np.arange(128) % 64
    ii = np.where(dd < 8, dd, dd - 8).astype(np.float64)
    invf = np.exp(-np.log(500000.0) * ii / 8.0)
    sm[:, 257] = np.where(dd < 16, invf, 0.0)
    sm[:, 258] = np.where(dd < 8, -1.0, np.where(dd < 16, 1.0, 0.0))
    sm[:, 259] = 0.8 - 0.6 * np.exp(-0.3 * layer_idx)
    nkt = (T + 127) // 128
    tok = np.arange(nkt * 128)
    kb = np.where(tok < n_valid, 0.0, -30000.0).astype(f32).reshape(nkt, 128).T
    return w_sw, sm, np.ascontiguousarray(kb)


def common_host_inputs(T, n_valid):
    f32 = np.float32
    j = np.arange(64)[:, None]
    i = np.arange(64)[None, :]
    tri = np.concatenate([(j <= i), (j > i), (j >= i), (j < i)], axis=1).astype(f32)
    tok = np.arange(T)
    tm = (tok < n_valid).astype(f32)
    tmask_rep = np.ascontiguousarray(np.broadcast_to(tm[None, :], (128, T)))
    nb = (T + 127) // 128
    tmp = np.zeros(nb * 128, f32)
    tmp[:T] = tm
    tmask_col = np.ascontiguousarray(tmp.reshape(nb, 128).T)
    return {"tri": tri, "tmask_rep": tmask_rep, "tmask_col": tmask_col, "ident": np.eye(128, dtype=f32)}


def hgrn_host_inputs(c_lb_logits, c_o_norm):
    sm = np.zeros((128, 33), np.float32)
    sm[:, 0:32] = col_layout(np.asarray(c_lb_logits, np.float32))
    sm[:, 32] = np.asarray(c_o_norm, np.float32).reshape(128)
    return sm


def gdn_host_inputs(a_log, a_dt_bias, a_o_norm, a_conv_w):
    f32 = np.float32
    n_a = np.asarray(a_log).shape[0]
    out = np.zeros((n_a, 128, 153), f32)
    for j in range(n_a):
        out[j, :, 0:16] = np.asarray(a_log[j], f32).reshape(1, 16)
        out[j, :, 16:32] = np.asarray(a_dt_bias[j], f32).reshape(1, 16)
        out[j, :, 32] = np.asarray(a_o_norm[j], f32).reshape(128)
        cw = np.asarray(a_conv_w[j], f32)
        out[j, :, 33:153] = cw.reshape(5, 24, 128).transpose(2, 1, 0).reshape(128, 120)
    return out


_PROG_CACHE = {}


def get_prog(T, depth, n_groups):
    key = (T, depth, n_groups)
    if key not in _PROG_CACHE:
        _PROG_CACHE[key] = Prog(T, depth, n_groups)
    return _PROG_CACHE[key]


T_FULL = 8256
SEQ_P = 8192
SEQ_S = 4096


def kernel(x_prompt, x_sample, meta_tokens, norm_w, ffn_w_up, ffn_w_down,
           a_w_in, a_conv_w, a_log, a_dt_bias, a_o_norm, a_w_out,
           b_w_in, b_lambda, b_sub_norm, b_w_out,
           c_w_in, c_lb_logits, c_o_norm, c_w_out, final_norm):
    import ml_dtypes
    depth = norm_w.shape[0]
    T = T_FULL
    prog = get_prog(T, depth, 4)
    f32 = np.float32
    seqs = [x_prompt[0], x_prompt[1], x_sample[0], x_sample[1], x_sample[2], x_sample[3], x_sample[0], x_sample[1]]
    nw = np.concatenate([np.asarray(norm_w, f32).reshape(depth * 3, D_MODEL), np.asarray(final_norm, f32)[None]], 0)
    nw = col_layout(nw)
    shared = {
        "norm_w": nw,
        "ffn_w_up": np.asarray(ffn_w_up, f32), "ffn_w_down": np.asarray(ffn_w_down, f32),
        "b_w_in": np.asarray(b_w_in[0], f32), "b_w_out": np.asarray(b_w_out[0], f32),
        "c_w_in": np.asarray(c_w_in[0], f32), "c_w_out": np.asarray(c_w_out[0], f32),
        "c_small": hgrn_host_inputs(c_lb_logits, c_o_norm[0]),
        "a_w_in": np.asarray(a_w_in, f32), "a_w_out": np.asarray(a_w_out, f32),
        "a_small": gdn_host_inputs(a_log, a_dt_bias, a_o_norm, a_conv_w),
        "identb": np.eye(128, dtype=f32).astype(ml_dtypes.bfloat16),
    }
    in_maps = []
    per_len = {}
    for s in seqs:
        L = s.shape[0]
        nv = N_META + L
        if L not in per_len:
            w_sw, sm, kb = attn_host_inputs(b_w_in[0], b_lambda[0], b_sub_norm[0], 1, T, nv)
            d = {"b_w_sw": w_sw, "b_small": sm, "kbias": kb}
            d.update(common_host_inputs(T, nv))
            per_len[L] = d
        xin = np.zeros((T, D_MODEL), f32)
        xin[:N_META] = meta_tokens
        xin[N_META:nv] = s
        m = {"xin": xin}
        m.update(shared)
        m.update(per_len[L])
        in_maps.append(m)
    res = run_bass_kernel_spmd(prog.nc, in_maps, core_ids=list(range(8)))
    outs = [r["yout"] for r in res.results]
    y_prompt = np.stack([outs[0][N_META:N_META + SEQ_P], outs[1][N_META:N_META + SEQ_P]], 0).astype(f32)
    y_sample = np.stack([outs[i][N_META:N_META + SEQ_S] for i in range(2, 6)], 0).astype(f32)
    return (y_prompt, y_sample)
```

```python
import numpy as np
from contextlib import ExitStack
import concourse.bass as bass
import concourse.mybir as mybir
from concourse.bass_utils import run_bass_kernel_spmd

F32 = mybir.dt.float32
BF16 = mybir.dt.bfloat16
ALU = mybir.AluOpType
AF = mybir.ActivationFunctionType
AX = mybir.AxisListType

D_MODEL = 1024
D_FF = 2816
N_META = 16
EPS = 1e-6


class Eng:
    def __init__(self, name, sem):
        self.name = name
        self.sem = sem
        self.cnt = 0
        self.seen = {}
        self.ops = []


class Tl:
    def __init__(self, t, name):
        self.t = t
        self.name = name
        self.w = None
        self.r = {}
        self.dsem = None

    def __getitem__(self, idx):
        return self.t[idx]

    def view(self):
        return Tl(self.t, self.name)


class KB:
    SAME_ENGINE_SYNC = True

    def __init__(self, nc, n_dma_sems=84):
        self.nc = nc
        self.es = ExitStack()
        self.engs = {}
        for name in ("pe", "act", "dve", "pool", "sp"):
            sem = self.es.enter_context(nc.semaphore("s_" + name))
            self.engs[name] = Eng(name, sem)
        self.pe, self.act, self.dve, self.pool, self.sp = (self.engs[n] for n in ("pe", "act", "dve", "pool", "sp"))
        self.dma_pool = []
        for i in range(n_dma_sems):
            sem = self.es.enter_context(nc.semaphore("s_d%d" % i))
            self.dma_pool.append([sem, 0])
        self.dma_free = list(range(n_dma_sems))
        self.phase_tiles = []
        self.dma_tiles = []
        self.lanes = {}
        self.marks = []
        self.cur_lane = None
        self.pes = None
        self.uid = 0
        self.nops = 0

    def begin(self):
        self.pes = ExitStack()
        self.phase_tiles = []

    def sb(self, shape, dt, name=None):
        self.uid += 1
        name = "%s_%d" % (name or "t", self.uid)
        t = self.pes.enter_context(self.nc.sbuf_tensor(name, list(shape), dt))
        tl = Tl(t, name)
        self.phase_tiles.append(tl)
        return tl

    def views(self, tl, n):
        vs = [tl.view() for _ in range(n)]
        self.phase_tiles.extend(vs)
        return vs

    def ps(self, shape, dt=F32, name=None):
        self.uid += 1
        name = "%s_%d" % (name or "p", self.uid)
        t = self.pes.enter_context(self.nc.psum_tensor(name, list(shape), dt))
        tl = Tl(t, name)
        self.phase_tiles.append(tl)
        return tl

    def gsb(self, shape, dt, name):
        t = self.es.enter_context(self.nc.sbuf_tensor(name, list(shape), dt))
        return Tl(t, name)

    def _deps(self, eng, reads, writes):
        deps = []
        for tl in reads:
            if tl.w is not None:
                deps.append(tl.w)
        for tl in writes:
            if tl.w is not None and tl.w[0] != eng.name:
                deps.append(tl.w)
            deps.extend(v for v in tl.r.values() if v[0] != eng.name)
        waits = []
        for key, sem, cnt in deps:
            if key == eng.name and (eng.name in ("pe", "sp") or not self.SAME_ENGINE_SYNC):
                continue
            if eng.seen.get(key, 0) < cnt:
                eng.seen[key] = cnt
                waits.append((sem, cnt))
        return waits

    def set_lane(self, lane):
        self.cur_lane = lane

    def merge(self):
        lanes = [v for _, v in sorted(self.lanes.items()) if v]
        self.lanes = {}
        save, self.cur_lane = self.cur_lane, None
        idx = [0] * len(lanes)
        left = sum(len(l) for l in lanes)
        while left:
            for li, l in enumerate(lanes):
                if idx[li] < len(l):
                    rec = l[idx[li]]
                    idx[li] += 1
                    left -= 1
                    if rec[0] == "op":
                        self.op(*rec[1:])
                    else:
                        self.dma(rec[1], rec[2], rec[3], rec[4], rec[5], **rec[6])
        self.cur_lane = save

    def mark(self):
        self.marks.append(len(self.lanes.get(self.cur_lane, [])))

    def emit_recs(self, recs):
        for rec in recs:
            if rec[0] == "op":
                self.op(*rec[1:])
            else:
                self.dma(rec[1], rec[2], rec[3], rec[4], rec[5], **rec[6])

    def emit_interleaved(self, a, b):
        save, self.cur_lane = self.cur_lane, None
        na, nb = len(a), len(b)
        ia = ib = 0
        while ia < na or ib < nb:
            if ib >= nb or (ia < na and ia * nb <= ib * na):
                self.emit_recs([a[ia]])
                ia += 1
            else:
                self.emit_recs([b[ib]])
                ib += 1
        self.cur_lane = save

    def op(self, eng, fn, reads=(), writes=()):
        if self.cur_lane is not None:
            self.lanes.setdefault(self.cur_lane, []).append(("op", eng, fn, tuple(reads), tuple(writes)))
            return
        waits = self._deps(eng, reads, writes)
        eng.cnt += 1
        me = (eng.name, eng.sem, eng.cnt)
        eng.ops.append((waits, fn, (eng.sem, 1)))
        for tl in writes:
            tl.w = me
            tl.r = {}
        for tl in reads:
            if tl not in writes:
                tl.r[eng.name] = me
        self.nops += 1

    def dma(self, q, out, in_, reads=(), writes=(), **kw):
        if self.cur_lane is not None:
            self.lanes.setdefault(self.cur_lane, []).append(("dma", q, out, in_, tuple(reads), tuple(writes), kw))
            return
        tl = (list(writes) + list(reads))[0]
        if tl.dsem is None:
            tl.dsem = self.dma_free.pop()
            self.dma_tiles.append(tl)
        slot = self.dma_pool[tl.dsem]
        waits = self._deps(q, reads, writes)
        slot[1] += 16
        key = "d%d" % tl.dsem
        me = (key, slot[0], slot[1])
        q.ops.append((waits, (lambda e, o=out, i=in_, k=kw: e.dma_start(out=o, in_=i, **k)), (slot[0], 16)))
        for t in writes:
            t.w = me
            t.r = {}
        for t in reads:
            t.r[key] = me
        self.nops += 1

    def end(self):
        used = set()
        for tl in self.dma_tiles:
            if tl.dsem is not None:
                used.add(tl.dsem)
        for d in sorted(used):
            sem, cnt = self.dma_pool[d]
            key = "d%d" % d
            if self.sp.seen.get(key, 0) < cnt:
                self.sp.seen[key] = cnt
                self.sp.ops.append(([(sem, cnt)], None, None))
        for e in (self.pe, self.act, self.dve, self.pool):
            if self.sp.seen.get(e.name, 0) < e.cnt:
                self.sp.seen[e.name] = e.cnt
                self.sp.ops.append(([(e.sem, e.cnt)], None, None))
        self.sp.cnt += 1
        self.sp.ops.append(([], (lambda e: e.nop()), (self.sp.sem, 1)))
        for e in (self.pe, self.act, self.dve, self.pool):
            e.ops.append(([(self.sp.sem, self.sp.cnt)], None, None))
        with self.nc.Block() as block:
            for name, deco in (("pe", block.tensor), ("act", block.scalar), ("dve", block.vector),
                               ("pool", block.gpsimd), ("sp", block.sync)):
                eng = self.engs[name]
                ops = eng.ops
                eng.ops = []

                def body(e, ops=ops):
                    for waits, fn, inc in ops:
                        for sem, cnt in waits:
                            e.wait_ge(sem, cnt)
                        if fn is not None:
                            ins = fn(e)
                            if inc is not None:
                                ins.then_inc(inc[0], inc[1])

                deco(body)
        for d in used:
            self.dma_free.append(d)
        for tl in self.dma_tiles:
            tl.dsem = None
        self.dma_tiles = []
        for e in self.engs.values():
            for o in self.engs.values():
                e.seen[o.name] = o.cnt
            for d in range(len(self.dma_pool)):
                e.seen["d%d" % d] = self.dma_pool[d][1]
        self.pes.close()
        self.pes = None

    def close(self):
        self.es.close()


def tiles_of(n, step=512):
    out = []
    s = 0
    while s < n:
        out.append((s, min(step, n - s)))
        s += step
    return out


class Prog:
    def __init__(self, T, depth, n_groups, mixers=True, ffn=True, only=None):
        self.only = only
        self.mixers = mixers
        self.ffn = ffn
        self.T = T
        self.depth = depth
        self.n_groups = n_groups
        base = (T // n_groups) // 512 * 512 if n_groups > 1 else T
        self.groups = [(g * base, base) for g in range(n_groups - 1)]
        self.groups.append(((n_groups - 1) * base, T - (n_groups - 1) * base))
        nc = bass.Bass("TRN2", target_bir_lowering=False)
        self.nc = nc
        d = nc.dram_tensor
        self.xin = d("xin", [T, D_MODEL], F32, kind="ExternalInput").ap()
        self.yout = d("yout", [T, D_MODEL], F32, kind="ExternalOutput").ap()
        self.norm_w = d("norm_w", [128, (depth * 3 + 1) * 8], F32, kind="ExternalInput").ap()
        self.w_up = d("ffn_w_up", [depth, 2, D_MODEL, 2 * D_FF], F32, kind="ExternalInput").ap()
        self.w_dn = d("ffn_w_down", [depth, 2, D_FF, D_MODEL], F32, kind="ExternalInput").ap()
        self.ident_in = d("ident", [128, 128], F32, kind="ExternalInput").ap()
        self.hT = d("hT", [D_MODEL, T], F32).ap()
        self.b_w = d("b_w_in", [D_MODEL, 3072], F32, kind="ExternalInput").ap()
        self.b_wsw = d("b_w_sw", [D_MODEL, 2048], F32, kind="ExternalInput").ap()
        self.b_wo = d("b_w_out", [D_MODEL, D_MODEL], F32, kind="ExternalInput").ap()
        self.b_small = d("b_small", [128, 256 + 4], F32, kind="ExternalInput").ap()
        self.kbias_in = d("kbias", [128, (T + 127) // 128], F32, kind="ExternalInput").ap()
        self.qkT = d("qkT", [2048, T], BF16).ap()
        self.v_tm = d("v_tm", [T, 1024], BF16).ap()
        self.oT = d("oT", [D_MODEL, T], BF16).ap()
        self.tri_in = d("tri", [64, 4 * 64], F32, kind="ExternalInput").ap()
        self.tmask_rep = d("tmask_rep", [128, T], F32, kind="ExternalInput").ap()
        self.tmask_col = d("tmask_col", [128, (T + 127) // 128], F32, kind="ExternalInput").ap()
        self.c_w = d("c_w_in", [D_MODEL, 5120], F32, kind="ExternalInput").ap()
        self.c_wo = d("c_w_out", [D_MODEL, D_MODEL], F32, kind="ExternalInput").ap()
        self.c_small = d("c_small", [128, 32 + 1], F32, kind="ExternalInput").ap()
        self.fmA = d("fmA", [D_MODEL, T], BF16).ap()
        self.fmB = [d("fmB%d" % i, [D_MODEL, T], BF16).ap() for i in range(2)]
        self.fmG = d("fmG", [D_MODEL, T], BF16).ap()
        self.tmK = [d("tmK%d" % i, [T, D_MODEL], BF16).ap() for i in range(2)]
        self.tmL = [d("tmL%d" % i, [T, D_MODEL], F32).ap() for i in range(2)]
        self.obT = d("obT", [D_MODEL, T], F32).ap()
        self.lbrow_t = d("lbrow", [2, D_MODEL], F32)
        self.n_a = (depth + 2) // 3
        self.a_w = d("a_w_in", [self.n_a, D_MODEL, 4128], F32, kind="ExternalInput").ap()
        self.a_wo = d("a_w_out", [self.n_a, D_MODEL, D_MODEL], F32, kind="ExternalInput").ap()
        self.a_small = d("a_small", [self.n_a, 128, 32 + 1 + 120], F32, kind="ExternalInput").ap()
        self.pre = d("pre", [3 * D_MODEL, T], BF16).ap()
        self.ba_tm = d("ba_tm", [T, 32], F32).ap()
        self.identb_in = d("identb", [128, 128], BF16, kind="ExternalInput").ap()
        self.k = KB(nc)
        k = self.k
        self.ident = k.gsb([128, 128], F32, "identc")
        self.ones_bf = k.gsb([128, 128], BF16, "onesbf")
        self.nw = k.gsb([128, (depth * 3 + 1) * 8], F32, "nwcol")
        self.build()
        k.close()

    def phase_in(self):
        k, nc, T = self.k, self.nc, self.T
        k.begin()
        k.dma(k.sp, self.ident[:], self.ident_in, writes=[self.ident])
        k.op(k.dve, lambda e: e.memset(self.ones_bf[:], 1.0), writes=[self.ones_bf])
        nrows = self.depth * 3 + 1
        k.dma(k.sp, self.nw[:], self.norm_w, writes=[self.nw])
        xt = [k.sb([128, 4, D_MODEL], F32, "xt") for _ in range(2)]
        st = [k.sb([128, 8, 512], F32, "st") for _ in range(2)]
        pst = [k.ps([128, 512], F32, "pst") for _ in range(4)]
        pi = 0
        for gi, (t0, n) in enumerate(tiles_of(T, 512)):
            x = xt[gi % 2]
            s = st[gi % 2]
            nb = (n + 127) // 128
            blocks = [(b * 128, min(128, n - b * 128)) for b in range(nb)]
            if n % 128 == 0:
                k.dma(k.sp, x[:, 0:nb, :], self.xin[t0:t0 + n, :].rearrange("(j p) f -> p j f", p=128), writes=[x])
            else:
                for b, (o, r) in enumerate(blocks):
                    k.dma(k.sp, x[0:r, b, :], self.xin[t0 + o:t0 + o + r, :], writes=[x])
            for c in range(8):
                p = pst[pi % 4]
                pi += 1
                for b, (o, r) in enumerate(blocks):
                    k.op(k.pe, (lambda e, p=p, x=x, b=b, o=o, r=r, c=c: e.transpose(
                        p[:, o:o + r], x[0:r, b, c * 128:(c + 1) * 128], self.ident[0:r, 0:r])),
                        reads=[x, self.ident], writes=[p])
                eng = k.dve if c % 2 == 0 else k.act
                if eng is k.dve:
                    k.op(eng, (lambda e, p=p, s=s, c=c, n=n: e.tensor_copy(s[:, c, 0:n], p[:, 0:n])), reads=[p], writes=[s])
                else:
                    k.op(eng, (lambda e, p=p, s=s, c=c, n=n: e.copy(s[:, c, 0:n], p[:, 0:n])), reads=[p], writes=[s])
            k.dma(k.sp, self.hT.rearrange("(c p) t -> p c t", p=128)[:, :, t0:t0 + n], s[:, :, 0:n], reads=[s])
        k.end()


    def emit_norm(self, y, yv, hn, hnv, tl, nrow, pd, sq, rstd):
        k = self.k
        for ti, (o, n) in enumerate(tl):
            ss = pd[ti % 4]
            for c in range(8):
                s_ = sq[c % 2]
                k.op(k.act, (lambda e, s_=s_, c=c, o=o, n=n: e.activation(s_[:, 0:n], y[:, c, o:o + n], AF.Square)),
                     reads=[yv[c][ti]], writes=[s_])
                k.op(k.pe, (lambda e, ss=ss, s_=s_, c=c, n=n: e.matmul(ss[:, 0:n], lhsT=self.ones_bf[:], rhs=s_[:, 0:n],
                                                                        start=(c == 0), stop=(c == 7))),
                     reads=[s_, self.ones_bf], writes=[ss])
            r = rstd[ti % 2]
            k.op(k.act, (lambda e, r=r, ss=ss, n=n: e.activation(r[:, 0:n], ss[:, 0:n], AF.Ln, bias=self.eps_col[:, 0:1],
                                                                  scale=1.0 / D_MODEL)),
                 reads=[ss, self.eps_col], writes=[r])
            k.op(k.act, (lambda e, r=r, n=n: e.activation(r[:, 0:n], r[:, 0:n], AF.Exp, scale=-0.5)), reads=[r], writes=[r])
            for c in range(8):
                col = nrow * 8 + c
                k.op(k.dve, (lambda e, r=r, c=c, o=o, n=n, col=col: e.scalar_tensor_tensor(
                    out=hn[:, c, o:o + n], in0=y[:, c, o:o + n], scalar=self.nw[:, col:col + 1], in1=r[:, 0:n],
                    op0=ALU.mult, op1=ALU.mult)),
                    reads=[yv[c][ti], r, self.nw], writes=[hnv[c][ti]])

    def phase_ffn(self, li, fj, nrow):
        k, nc = self.k, self.nc
        HG = 256
        NG = D_FF // HG
        hTv = self.hT.rearrange("(c p) t -> p c t", p=128)
        for (T0, GS) in self.groups:
            tl = tiles_of(GS, 512)
            NT = len(tl)
            k.begin()
            y = k.sb([128, 8, GS], F32, "y")
            hn = k.sb([128, 8, GS], BF16, "hn")
            yv = [k.views(y, NT) for _ in range(8)]
            hnv = [k.views(hn, NT) for _ in range(8)]
            sq = [k.sb([128, 512], BF16, "sq") for _ in range(2)]
            rstd = [k.sb([128, 512], F32, "rstd") for _ in range(2)]
            wg = [k.sb([128, 8, HG], BF16, "wg") for _ in range(2)]
            wu = [k.sb([128, 8, HG], BF16, "wu") for _ in range(2)]
            wd = [k.sb([128, HG // 128, D_MODEL], BF16, "wd") for _ in range(2)]
            sg = [k.sb([128, 2, 512], F32, "sg") for _ in range(2)]
            act = [k.sb([128, 2, 512], BF16, "act") for _ in range(2)]
            pg = [k.ps([128, 512], F32, "pg") for _ in range(2)]
            pu = [k.ps([128, 512], F32, "pu") for _ in range(2)]
            pd = [k.ps([128, 512], F32, "pd") for _ in range(4)]
            allv = [v for c in range(8) for v in yv[c]]
            k.dma(k.sp, y[:], hTv[:, :, T0:T0 + GS], writes=allv)
            self.emit_norm(y, yv, hn, hnv, tl, nrow, pd, sq, rstd)
            wup = self.w_up[li, fj]
            wdn = self.w_dn[li, fj]
            it = 0
            for g in range(NG):
                b = g % 2
                k.dma(k.pool, wg[b][:], wup[:, g * HG:(g + 1) * HG].rearrange("(kk p) c -> p kk c", p=128), writes=[wg[b]])
                k.dma(k.pool, wu[b][:], wup[:, D_FF + g * HG:D_FF + (g + 1) * HG].rearrange("(kk p) c -> p kk c", p=128),
                      writes=[wu[b]])
                k.dma(k.pool, wd[b][:], wdn[g * HG:(g + 1) * HG, :].rearrange("(kk p) c -> p kk c", p=128), writes=[wd[b]])
                for ti, (o, n) in enumerate(tl):
                    ab = it % 2
                    it += 1
                    for j in range(2):
                        for (pt, wt) in ((pg[j], wg[b]), (pu[j], wu[b])):
                            for c in range(8):
                                k.op(k.pe, (lambda e, pt=pt, wt=wt, j=j, c=c, o=o, n=n: e.matmul(
                                    pt[:, 0:n], lhsT=wt[:, c, j * 128:(j + 1) * 128], rhs=hn[:, c, o:o + n],
                                    start=(c == 0), stop=(c == 7))),
                                    reads=[wt, hnv[c][ti]], writes=[pt])
                    for j in range(2):
                        k.op(k.act, (lambda e, j=j, ab=ab, n=n: e.activation(sg[ab][:, j, 0:n], pg[j][:, 0:n], AF.Silu)),
                             reads=[pg[j]], writes=[sg[ab]])
                    for j in range(2):
                        k.op(k.dve, (lambda e, j=j, ab=ab, n=n: e.scalar_tensor_tensor(
                            out=act[ab][:, j, 0:n], in0=pu[j][:, 0:n], scalar=0.5, in1=sg[ab][:, j, 0:n],
                            op0=ALU.mult, op1=ALU.mult)),
                            reads=[pu[j], sg[ab]], writes=[act[ab]])
                    for m in range(8):
                        pdt = pd[m % 4]
                        for j in range(2):
                            k.op(k.pe, (lambda e, pdt=pdt, b=b, j=j, m=m, ab=ab, n=n: e.matmul(
                                pdt[:, 0:n], lhsT=wd[b][:, j, m * 128:(m + 1) * 128], rhs=act[ab][:, j, 0:n],
                                start=(j == 0), stop=(j == 1))),
                                reads=[wd[b], act[ab]], writes=[pdt])
                        k.op(k.dve, (lambda e, pdt=pdt, m=m, o=o, n=n: e.tensor_tensor(
                            out=y[:, m, o:o + n], in0=y[:, m, o:o + n], in1=pdt[:, 0:n], op=ALU.add)),
                            reads=[pdt, yv[m][ti]], writes=[yv[m][ti]])
            k.dma(k.sp, hTv[:, :, T0:T0 + GS], y[:], reads=allv)
            k.end()

    def phase_out(self, nrow):
        k, nc, T = self.k, self.nc, self.T
        hTv = self.hT.rearrange("(c p) t -> p c t", p=128)
        k.begin()
        hb = [k.sb([128, 8, 512], F32, "hb") for _ in range(2)]
        sq = [k.sb([128, 512], BF16, "sq") for _ in range(2)]
        rstd = [k.sb([128, 512], F32, "rstd") for _ in range(2)]
        yn = [k.sb([128, 8, 512], F32, "yn") for _ in range(2)]
        ot = [k.sb([128, 4, D_MODEL], F32, "ot") for _ in range(2)]
        pss = k.ps([128, 512], F32, "pss")
        pt = [k.ps([128, 512], F32, "pt") for _ in range(4)]
        pi = 0
        for gi, (t0, n) in enumerate(tiles_of(T, 512)):
            h = hb[gi % 2]
            r = rstd[gi % 2]
            yy = yn[gi % 2]
            o_ = ot[gi % 2]
            k.dma(k.sp, h[:, :, 0:n], hTv[:, :, t0:t0 + n], writes=[h])
            for c in range(8):
                s_ = sq[c % 2]
                k.op(k.act, (lambda e, s_=s_, h=h, c=c, n=n: e.activation(s_[:, 0:n], h[:, c, 0:n], AF.Square)),
                     reads=[h], writes=[s_])
                k.op(k.pe, (lambda e, s_=s_, c=c, n=n: e.matmul(pss[:, 0:n], lhsT=self.ones_bf[:], rhs=s_[:, 0:n],
                                                                 start=(c == 0), stop=(c == 7))),
                     reads=[s_, self.ones_bf], writes=[pss])
            k.op(k.act, (lambda e, r=r, n=n: e.activation(r[:, 0:n], pss[:, 0:n], AF.Ln, bias=self.eps_col[:, 0:1],
                                                           scale=1.0 / D_MODEL)),
                 reads=[pss, self.eps_col], writes=[r])
            k.op(k.act, (lambda e, r=r, n=n: e.activation(r[:, 0:n], r[:, 0:n], AF.Exp, scale=-0.5)), reads=[r], writes=[r])
            for c in range(8):
                col = nrow * 8 + c
                k.op(k.dve, (lambda e, r=r, h=h, yy=yy, c=c, n=n, col=col: e.scalar_tensor_tensor(
                    out=yy[:, c, 0:n], in0=h[:, c, 0:n], scalar=self.nw[:, col:col + 1], in1=r[:, 0:n],
                    op0=ALU.mult, op1=ALU.mult)),
                    reads=[h, r, self.nw], writes=[yy])
            nb = (n + 127) // 128
            blocks = [(b * 128, min(128, n - b * 128)) for b in range(nb)]
            for b, (o, rr) in enumerate(blocks):
                for half in range(2):
                    p = pt[pi % 4]
                    pi += 1
                    for cc in range(4):
                        c = half * 4 + cc
                        k.op(k.pe, (lambda e, p=p, yy=yy, c=c, cc=cc, o=o, rr=rr: e.transpose(
                            p[0:rr, cc * 128:(cc + 1) * 128], yy[:, c, o:o + rr], self.ident[:])),
                            reads=[yy, self.ident], writes=[p])
                    if half == 0:
                        k.op(k.dve, (lambda e, p=p, o_=o_, b=b, rr=rr: e.tensor_copy(o_[0:rr, b, 0:512], p[0:rr, :])),
                             reads=[p], writes=[o_])
                    else:
                        k.op(k.act, (lambda e, p=p, o_=o_, b=b, rr=rr: e.copy(o_[0:rr, b, 512:1024], p[0:rr, :])),
                             reads=[p], writes=[o_])
            if n % 128 == 0:
                k.dma(k.sp, self.yout[t0:t0 + n, :].rearrange("(j p) f -> p j f", p=128), o_[:, 0:nb, :], reads=[o_])
            else:
                for b, (o, rr) in enumerate(blocks):
                    k.dma(k.sp, self.yout[t0 + o:t0 + o + rr, :], o_[0:rr, b, :], reads=[o_])
        k.end()


    def emit_rope_tables(self, T0, tl, cosF, sinF, tmp):
        k = self.k
        PI = float(np.pi)
        invf = self.bsm[:, 257:258]
        sign = self.bsm[:, 258:259]
        for ti, (o, n) in enumerate(tl):
            pos, ang, ki, kf, tf = tmp
            k.op(k.pool, (lambda e, pos=pos, n=n, b=T0 + o: e.iota(pos[:, 0:n], [[1, n]], base=b, channel_multiplier=0,
                                                                   allow_small_or_imprecise_dtypes=True)), writes=[pos])
            k.op(k.dve, (lambda e, n=n: e.tensor_scalar(ang[:, 0:n], pos[:, 0:n], invf, None, op0=ALU.mult)),
                 reads=[pos, self.bsm], writes=[ang])
            def reduce(src, dst, shift, n=n):
                k.op(k.dve, (lambda e: e.tensor_scalar(dst[:, 0:n], src[:, 0:n], shift, None, op0=ALU.add)),
                     reads=[src], writes=[dst])
                k.op(k.dve, (lambda e: e.tensor_scalar(ki[:, 0:n], dst[:, 0:n], 1.0 / (2 * PI), None, op0=ALU.mult)),
                     reads=[dst], writes=[ki])
                k.op(k.dve, (lambda e: e.tensor_copy(tf[:, 0:n], ki[:, 0:n])), reads=[ki], writes=[tf])
                k.op(k.dve, (lambda e: e.scalar_tensor_tensor(out=dst[:, 0:n], in0=tf[:, 0:n], scalar=-2 * PI, in1=dst[:, 0:n],
                                                              op0=ALU.mult, op1=ALU.add)), reads=[tf, dst], writes=[dst])
                k.op(k.dve, (lambda e: e.tensor_scalar(tf[:, 0:n], dst[:, 0:n], PI, -2 * PI, op0=ALU.is_gt, op1=ALU.mult)),
                     reads=[dst], writes=[tf])
                k.op(k.dve, (lambda e: e.tensor_tensor(out=dst[:, 0:n], in0=dst[:, 0:n], in1=tf[:, 0:n], op=ALU.add)),
                     reads=[dst, tf], writes=[dst])
                k.op(k.dve, (lambda e: e.tensor_scalar(tf[:, 0:n], dst[:, 0:n], -PI, 2 * PI, op0=ALU.is_lt, op1=ALU.mult)),
                     reads=[dst], writes=[tf])
                k.op(k.dve, (lambda e: e.tensor_tensor(out=dst[:, 0:n], in0=dst[:, 0:n], in1=tf[:, 0:n], op=ALU.add)),
                     reads=[dst, tf], writes=[dst])
                k.op(k.dve, (lambda e: e.tensor_scalar(dst[:, 0:n], dst[:, 0:n], -PI, PI, op0=ALU.max, op1=ALU.min)),
                     reads=[dst], writes=[dst])
            reduce(ang, kf, 0.0)
            reduce(ang, pos, PI / 2)
            s_, c_ = sinF[ti], cosF[ti]
            k.op(k.act, (lambda e, s_=s_, n=n: e.activation(s_[:, 0:n], kf[:, 0:n], AF.Sin)), reads=[kf], writes=[s_])
            k.op(k.act, (lambda e, c_=c_, n=n: e.activation(c_[:, 0:n], pos[:, 0:n], AF.Sin)), reads=[pos], writes=[c_])
            k.op(k.dve, (lambda e, s_=s_, n=n: e.tensor_scalar(s_[:, 0:n], s_[:, 0:n], sign, None, op0=ALU.mult)),
                 reads=[s_, self.bsm], writes=[s_])

    def phase_attn_proj(self, nrow):
        k, nc = self.k, self.nc
        hTv = self.hT.rearrange("(c p) t -> p c t", p=128)
        for (T0, GS) in self.groups:
            tl = tiles_of(GS, 512)
            NT = len(tl)
            k.begin()
            y = k.sb([128, 8, GS], F32, "y")
            hn = k.sb([128, 8, GS], BF16, "hn")
            yv = [k.views(y, NT) for _ in range(8)]
            hnv = [k.views(hn, NT) for _ in range(8)]
            sq = [k.sb([128, 512], BF16, "sq") for _ in range(2)]
            rstd = [k.sb([128, 512], F32, "rstd") for _ in range(2)]
            pd = [k.ps([128, 512], F32, "pd") for _ in range(4)]
            pa = [k.ps([128, 512], F32, "pa") for _ in range(2)]
            pb = [k.ps([128, 512], F32, "pb") for _ in range(2)]
            allv = [v for c in range(8) for v in yv[c]]
            k.dma(k.sp, y[:], hTv[:, :, T0:T0 + GS], writes=allv)
            self.emit_norm(y, yv, hn, hnv, tl, nrow, pd, sq, rstd)
            cosF = [k.sb([128, 512], F32, "cosF") for _ in range(NT)]
            sinF = [k.sb([128, 512], F32, "sinF") for _ in range(NT)]
            tmp = (k.sb([128, 512], F32, "pos"), k.sb([128, 512], F32, "ang"),
                   k.sb([128, 512], mybir.dt.int32, "ki"), k.sb([128, 512], F32, "kf"), k.sb([128, 512], F32, "tf"))
            self.emit_rope_tables(T0, tl, cosF, sinF, tmp)
            wa = [k.sb([128, 8, 128], BF16, "wa") for _ in range(2)]
            wb = [k.sb([128, 8, 128], BF16, "wb") for _ in range(2)]
            t1 = [k.sb([128, 512], F32, "t1") for _ in range(2)]
            t2 = [k.sb([128, 512], F32, "t2") for _ in range(2)]
            stg = [k.sb([128, 512], BF16, "stg") for _ in range(3)]
            it = 0
            for m in range(16):
                b = m % 2
                k.dma(k.pool, wa[b][:], self.b_w[:, m * 128:(m + 1) * 128].rearrange("(kk p) c -> p kk c", p=128), writes=[wa[b]])
                k.dma(k.pool, wb[b][:], self.b_wsw[:, m * 128:(m + 1) * 128].rearrange("(kk p) c -> p kk c", p=128), writes=[wb[b]])
                for ti, (o, n) in enumerate(tl):
                    ab = it % 2
                    sb_ = stg[it % 3]
                    it += 1
                    for (pt, wt) in ((pa[ab], wa[b]), (pb[ab], wb[b])):
                        for c in range(8):
                            k.op(k.pe, (lambda e, pt=pt, wt=wt, c=c, o=o, n=n: e.matmul(
                                pt[:, 0:n], lhsT=wt[:, c, :], rhs=hn[:, c, o:o + n], start=(c == 0), stop=(c == 7))),
                                reads=[wt, hnv[c][ti]], writes=[pt])
                    k.op(k.dve, (lambda e, ab=ab, ti=ti, n=n: e.tensor_tensor(out=t1[ab][:, 0:n], in0=pa[ab][:, 0:n],
                                                                              in1=cosF[ti][:, 0:n], op=ALU.mult)),
                         reads=[pa[ab], cosF[ti]], writes=[t1[ab]])
                    k.op(k.dve, (lambda e, ab=ab, ti=ti, n=n: e.tensor_tensor(out=t2[ab][:, 0:n], in0=pb[ab][:, 0:n],
                                                                              in1=sinF[ti][:, 0:n], op=ALU.mult)),
                         reads=[pb[ab], sinF[ti]], writes=[t2[ab]])
                    k.op(k.pool, (lambda e, ab=ab, sb_=sb_, n=n: e.tensor_tensor(out=sb_[:, 0:n], in0=t1[ab][:, 0:n],
                                                                                 in1=t2[ab][:, 0:n], op=ALU.add)),
                         reads=[t1[ab], t2[ab]], writes=[sb_])
                    k.dma(k.sp, self.qkT[m * 128:(m + 1) * 128, T0 + o:T0 + o + n], sb_[:, 0:n], reads=[sb_])
            wv = [k.sb([128, 8, 512], BF16, "wv") for _ in range(2)]
            vst = [k.sb([128, 512], BF16, "vst") for _ in range(3)]
            it = 0
            for vb in range(2):
                k.dma(k.pool, wv[vb][:], self.b_w[:, 2048 + vb * 512:2048 + (vb + 1) * 512].rearrange("(kk p) c -> p kk c", p=128),
                      writes=[wv[vb]])
                for ti, (o, n) in enumerate(tl):
                    for bo in range(0, n, 128):
                        r = min(128, n - bo)
                        pt = pd[it % 4]
                        vs = vst[it % 3]
                        it += 1
                        for c in range(8):
                            k.op(k.pe, (lambda e, pt=pt, vb=vb, c=c, o=o, bo=bo, r=r: e.matmul(
                                pt[0:r, :], lhsT=hn[:, c, o + bo:o + bo + r], rhs=wv[vb][:, c, :], start=(c == 0), stop=(c == 7))),
                                reads=[wv[vb], hnv[c][ti]], writes=[pt])
                        k.op(k.act, (lambda e, pt=pt, vs=vs, r=r: e.copy(vs[0:r, :], pt[0:r, :])), reads=[pt], writes=[vs])
                        k.dma(k.sp, self.v_tm[T0 + o + bo:T0 + o + bo + r, vb * 512:(vb + 1) * 512], vs[0:r, :], reads=[vs])
            k.end()

    def phase_attn_core(self):
        k, nc, T = self.k, self.nc, self.T
        NKT = (T + 127) // 128
        qtl = tiles_of(T, 512)
        k.begin()
        lt = k.sb([128, 256], F32, "lt")
        l2 = k.sb([128, 2], F32, "l2")
        neglam = k.sb([128, 1], F32, "neglam")
        subw = k.sb([128, 1], F32, "subw")
        kb = k.sb([128, NKT], F32, "kb")
        k.dma(k.sp, kb[:], self.kbias_in, writes=[kb])
        k.op(k.dve, (lambda e: e.tensor_tensor(out=lt[:, 0:64], in0=self.bsm[:, 0:64], in1=self.bsm[:, 64:128], op=ALU.mult)),
             reads=[self.bsm], writes=[lt])
        k.op(k.dve, (lambda e: e.tensor_tensor(out=lt[:, 64:128], in0=self.bsm[:, 128:192], in1=self.bsm[:, 192:256], op=ALU.mult)),
             reads=[self.bsm], writes=[lt])
        k.op(k.dve, (lambda e: e.reduce_sum(l2[:, 0:2], lt[:, 0:128].rearrange("p (a b) -> p a b", a=2), axis=AX.X)),
             reads=[lt], writes=[l2])
        k.op(k.act, (lambda e: e.activation(l2[:, 0:2], l2[:, 0:2], AF.Exp)), reads=[l2], writes=[l2])
        k.op(k.dve, (lambda e: e.tensor_tensor(out=neglam[:], in0=l2[:, 1:2], in1=l2[:, 0:1], op=ALU.subtract)),
             reads=[l2], writes=[neglam])
        k.op(k.dve, (lambda e: e.tensor_tensor(out=neglam[:], in0=neglam[:], in1=self.bsm[:, 259:260], op=ALU.subtract)),
             reads=[neglam, self.bsm], writes=[neglam])
        k.op(k.dve, (lambda e: e.tensor_scalar(subw[:], self.bsm[:, 259:260], -1.0, 1.0, op0=ALU.mult, op1=ALU.add)),
             reads=[self.bsm], writes=[subw])
        k.op(k.dve, (lambda e: e.tensor_tensor(out=subw[:], in0=subw[:], in1=self.bsm[:, 256:257], op=ALU.mult)),
             reads=[subw, self.bsm], writes=[subw])

        kk1 = [k.sb([64, T], BF16, "kk1") for _ in range(2)]
        kk2 = [k.sb([64, T], BF16, "kk2") for _ in range(2)]
        vh = [k.sb([128, NKT, 128], BF16, "vh") for _ in range(2)]
        q1 = [k.sb([64, 512], BF16, "q1") for _ in range(2)]
        q2 = [k.sb([64, 512], BF16, "q2") for _ in range(2)]
        p1 = [k.sb([128, 512], BF16, "p1") for _ in range(3)]
        p2 = [k.sb([128, 512], BF16, "p2") for _ in range(3)]
        ps1 = [k.ps([128, 512], F32, "ps1") for _ in range(2)]
        ps2 = [k.ps([128, 512], F32, "ps2") for _ in range(2)]
        num1, num2, z1, z2 = (k.ps([128, 512], F32, nm) for nm in ("num1", "num2", "z1", "z2"))
        za1 = k.sb([128, 512], F32, "za1")
        za2 = k.sb([128, 512], F32, "za2")
        ones_f = k.sb([128, 128], F32, "ones_f")
        k.op(k.dve, (lambda e: e.memset(ones_f[:], 1.0)), writes=[ones_f])
        r1 = k.sb([128, 512], F32, "r1")
        r2 = k.sb([128, 512], F32, "r2")
        o1 = k.sb([128, 512], F32, "o1")
        o2 = k.sb([128, 512], F32, "o2")
        oo = k.sb([128, 512], F32, "oo")
        sqo = k.sb([128, 512], BF16, "sqo")
        rs = k.sb([128, 512], F32, "rs")
        ost = [k.sb([128, 512], BF16, "ost") for _ in range(2)]
        nfull = T // 128
        rem = T - nfull * 128
        it = 0
        fi = 0
        for h in range(8):
            hb = h % 2
            k.dma(k.sp, kk1[hb][:], self.qkT[1024 + h * 64:1024 + (h + 1) * 64, :], writes=[kk1[hb]])
            k.dma(k.sp, kk2[hb][:], self.qkT[1536 + h * 64:1536 + (h + 1) * 64, :], writes=[kk2[hb]])
            k.dma(k.sp, vh[hb][:, 0:nfull, :],
                  self.v_tm[0:nfull * 128, h * 128:(h + 1) * 128].rearrange("(kt p) e -> p kt e", p=128), writes=[vh[hb]])
            if rem:
                k.dma(k.sp, vh[hb][0:rem, nfull, :], self.v_tm[nfull * 128:T, h * 128:(h + 1) * 128], writes=[vh[hb]])
            for qi, (t0, n) in enumerate(qtl):
                qb = fi % 2
                k.dma(k.sp, q1[qb][:, 0:n], self.qkT[h * 64:(h + 1) * 64, t0:t0 + n], writes=[q1[qb]])
                k.dma(k.sp, q2[qb][:, 0:n], self.qkT[512 + h * 64:512 + (h + 1) * 64, t0:t0 + n], writes=[q2[qb]])
                base_it = it
                it += NKT

                def emit_scores(kt):
                    kn = 128 if kt < nfull else rem
                    sb2 = (base_it + kt) % 2
                    for (ps, kk, qq) in ((ps1[sb2], kk1[hb], q1[qb]), (ps2[sb2], kk2[hb], q2[qb])):
                        k.op(k.pe, (lambda e, ps=ps, kk=kk, qq=qq, kt=kt, kn=kn, n=n: e.matmul(
                            ps[0:kn, 0:n], lhsT=kk[:, kt * 128:kt * 128 + kn], rhs=qq[:, 0:n], start=True, stop=True)),
                            reads=[kk, qq], writes=[ps])

                def emit_exp(kt):
                    kn = 128 if kt < nfull else rem
                    sb2 = (base_it + kt) % 2
                    pb3 = (base_it + kt) % 3
                    for (ps, pp) in ((ps1[sb2], p1[pb3]), (ps2[sb2], p2[pb3])):
                        k.op(k.act, (lambda e, ps=ps, pp=pp, kt=kt, kn=kn, n=n: e.activation(
                            pp[0:kn, 0:n], ps[0:kn, 0:n], AF.Exp, bias=kb[0:kn, kt:kt + 1], scale=0.125)),
                            reads=[ps, kb], writes=[pp])

                def emit_pv(kt):
                    kn = 128 if kt < nfull else rem
                    pb3 = (base_it + kt) % 3
                    first, last = (kt == 0), (kt == NKT - 1)
                    for (pp, nm, za, zeng) in ((p1[pb3], num1, za1, k.dve), (p2[pb3], num2, za2, k.pool)):
                        k.op(k.pe, (lambda e, nm=nm, pp=pp, kt=kt, kn=kn, n=n, first=first, last=last, vv=vh[hb]: e.matmul(
                            nm[:, 0:n], lhsT=vv[0:kn, kt, :], rhs=pp[0:kn, 0:n], start=first, stop=last)),
                            reads=[vh[hb], pp], writes=[nm])
                        if first:
                            k.op(zeng, (lambda e, za=za, pp=pp, n=n: e.tensor_copy(za[:, 0:n], pp[:, 0:n])), reads=[pp], writes=[za])
                        else:
                            k.op(zeng, (lambda e, za=za, pp=pp, kn=kn, n=n: e.tensor_tensor(out=za[0:kn, 0:n], in0=za[0:kn, 0:n],
                                                                                           in1=pp[0:kn, 0:n], op=ALU.add)),
                                 reads=[pp, za], writes=[za])

                emit_scores(0)
                for kt in range(NKT):
                    emit_exp(kt)
                    if kt + 1 < NKT:
                        emit_scores(kt + 1)
                    emit_pv(kt)
                for (zz, za) in ((z1, za1), (z2, za2)):
                    k.op(k.pe, (lambda e, zz=zz, za=za, n=n: e.matmul(zz[:, 0:n], lhsT=ones_f[:], rhs=za[:, 0:n], start=True, stop=True)),
                         reads=[ones_f, za], writes=[zz])
                k.op(k.dve, (lambda e, n=n: e.reciprocal(r1[:, 0:n], z1[:, 0:n])), reads=[z1], writes=[r1])
                k.op(k.dve, (lambda e, n=n: e.reciprocal(r2[:, 0:n], z2[:, 0:n])), reads=[z2], writes=[r2])
                k.op(k.dve, (lambda e, n=n: e.tensor_tensor(out=o1[:, 0:n], in0=num1[:, 0:n], in1=r1[:, 0:n], op=ALU.mult)),
                     reads=[num1, r1], writes=[o1])
                k.op(k.dve, (lambda e, n=n: e.scalar_tensor_tensor(out=o2[:, 0:n], in0=num2[:, 0:n], scalar=neglam[:, 0:1],
                                                                   in1=r2[:, 0:n], op0=ALU.mult, op1=ALU.mult)),
                     reads=[num2, r2, neglam], writes=[o2])
                k.op(k.pool, (lambda e, n=n: e.tensor_tensor(out=oo[:, 0:n], in0=o1[:, 0:n], in1=o2[:, 0:n], op=ALU.add)),
                     reads=[o1, o2], writes=[oo])
                k.op(k.act, (lambda e, n=n: e.activation(sqo[:, 0:n], oo[:, 0:n], AF.Square)), reads=[oo], writes=[sqo])
                pss = ps1[it % 2]
                k.op(k.pe, (lambda e, pss=pss, n=n: e.matmul(pss[:, 0:n], lhsT=self.ones_bf[:], rhs=sqo[:, 0:n], start=True, stop=True)),
                     reads=[self.ones_bf, sqo], writes=[pss])
                k.op(k.act, (lambda e, pss=pss, n=n: e.activation(rs[:, 0:n], pss[:, 0:n], AF.Ln, bias=self.eps_col[:, 0:1],
                                                                   scale=1.0 / 128)), reads=[pss, self.eps_col], writes=[rs])
                k.op(k.act, (lambda e, n=n: e.activation(rs[:, 0:n], rs[:, 0:n], AF.Exp, scale=-0.5)), reads=[rs], writes=[rs])
                os_ = ost[fi % 2]
                fi += 1
                k.op(k.dve, (lambda e, os_=os_, n=n: e.scalar_tensor_tensor(out=os_[:, 0:n], in0=oo[:, 0:n], scalar=subw[:, 0:1],
                                                                            in1=rs[:, 0:n], op0=ALU.mult, op1=ALU.mult)),
                     reads=[oo, rs, subw], writes=[os_])
                k.dma(k.sp, self.oT[h * 128:(h + 1) * 128, t0:t0 + n], os_[:, 0:n], reads=[os_])
        k.end()

    def phase_out_proj(self, wo_ap):
        k, nc, T = self.k, self.nc, self.T
        hTv = self.hT.rearrange("(c p) t -> p c t", p=128)
        oTv = self.oT.rearrange("(c p) t -> p c t", p=128)
        k.begin()
        wo = k.sb([128, 8, D_MODEL], BF16, "wo")
        for c in range(8):
            for hf in range(2):
                k.dma(k.pool, wo[:, c, hf * 512:(hf + 1) * 512], wo_ap[c * 128:(c + 1) * 128, hf * 512:(hf + 1) * 512], writes=[wo])
        ob = [k.sb([128, 8, 512], BF16, "ob") for _ in range(2)]
        hb = [k.sb([128, 8, 512], F32, "hb") for _ in range(2)]
        pp = [k.ps([128, 512], F32, "pp") for _ in range(4)]
        pi = 0
        for gi, (t0, n) in enumerate(tiles_of(T, 512)):
            o_ = ob[gi % 2]
            h_ = hb[gi % 2]
            k.dma(k.sp, o_[:, :, 0:n], oTv[:, :, t0:t0 + n], writes=[o_])
            k.dma(k.sp, h_[:, :, 0:n], hTv[:, :, t0:t0 + n], writes=[h_])
            for m in range(8):
                p = pp[pi % 4]
                pi += 1
                for c in range(8):
                    k.op(k.pe, (lambda e, p=p, o_=o_, c=c, m=m, n=n: e.matmul(
                        p[:, 0:n], lhsT=wo[:, c, m * 128:(m + 1) * 128], rhs=o_[:, c, 0:n], start=(c == 0), stop=(c == 7))),
                        reads=[wo, o_], writes=[p])
                k.op(k.dve, (lambda e, p=p, h_=h_, m=m, n=n: e.tensor_tensor(out=h_[:, m, 0:n], in0=h_[:, m, 0:n],
                                                                             in1=p[:, 0:n], op=ALU.add)),
                     reads=[p, h_], writes=[h_])
            k.dma(k.sp, hTv[:, :, t0:t0 + n], h_[:, :, 0:n], reads=[h_])
        k.end()

    def mixer_attn(self, nrow):
        import os
        st = int(os.environ.get("ATT_STAGE", "3"))
        self.phase_attn_proj(nrow)
        if st >= 2:
            self.phase_attn_core()
        if st >= 3:
            self.phase_out_proj(self.b_wo)


    def phase_proj(self, nrow, setup, fm_jobs, tm_jobs):
        k, nc = self.k, self.nc
        hTv = self.hT.rearrange("(c p) t -> p c t", p=128)
        for (T0, GS) in self.groups:
            tl = tiles_of(GS, 512)
            NT = len(tl)
            k.begin()
            y = k.sb([128, 8, GS], F32, "y")
            hn = k.sb([128, 8, GS], BF16, "hn")
            yv = [k.views(y, NT) for _ in range(8)]
            hnv = [k.views(hn, NT) for _ in range(8)]
            sq = [k.sb([128, 512], BF16, "sq") for _ in range(2)]
            rstd = [k.sb([128, 512], F32, "rstd") for _ in range(2)]
            pd = [k.ps([128, 512], F32, "pd") for _ in range(4)]
            allv = [v for c in range(8) for v in yv[c]]
            k.dma(k.sp, y[:], hTv[:, :, T0:T0 + GS], writes=allv)
            self.emit_norm(y, yv, hn, hnv, tl, nrow, pd, sq, rstd)
            ctx = setup(T0, tl)
            wa = [k.sb([128, 8, 128], BF16, "wa") for _ in range(2)]
            it = 0
            for ji, (wap, post) in enumerate(fm_jobs):
                b = ji % 2
                k.dma(k.pool, wa[b][:], wap.rearrange("(kk p) c -> p kk c", p=128), writes=[wa[b]])
                for ti, (o, n) in enumerate(tl):
                    pt = pd[it % 4]
                    it += 1
                    for c in range(8):
                        k.op(k.pe, (lambda e, pt=pt, b=b, c=c, o=o, n=n: e.matmul(
                            pt[:, 0:n], lhsT=wa[b][:, c, :], rhs=hn[:, c, o:o + n], start=(c == 0), stop=(c == 7))),
                            reads=[wa[b], hnv[c][ti]], writes=[pt])
                    post(ctx, pt, T0, ti, o, n)
            wv = [k.sb([128, 8, 512], BF16, "wv") for _ in range(2)]
            for ji, job in enumerate(tm_jobs):
                wap, post = job[0], job[1]
                ncl = job[2] if len(job) > 2 else 512
                b = ji % 2
                k.dma(k.pool, wv[b][:, :, 0:ncl], wap.rearrange("(kk p) c -> p kk c", p=128), writes=[wv[b]])
                for ti, (o, n) in enumerate(tl):
                    for bo in range(0, n, 128):
                        r = min(128, n - bo)
                        pt = pd[it % 4]
                        it += 1
                        for c in range(8):
                            k.op(k.pe, (lambda e, pt=pt, b=b, c=c, o=o, bo=bo, r=r, ncl=ncl: e.matmul(
                                pt[0:r, 0:ncl], lhsT=hn[:, c, o + bo:o + bo + r], rhs=wv[b][:, c, 0:ncl], start=(c == 0), stop=(c == 7))),
                                reads=[wv[b], hnv[c][ti]], writes=[pt])
                        post(ctx, pt, T0 + o + bo, r)
            k.end()

    def hg_prep(self, li):
        k, nc = self.k, self.nc
        k.begin()
        e = k.sb([128, 32], F32, "lbe")
        ssum = k.sb([128, 8], F32, "lbs")
        k.op(k.act, (lambda en: en.activation(e[:], self.csm[:, 0:32], AF.Exp)), reads=[self.csm], writes=[e])
        k.op(k.dve, (lambda en: en.tensor_tensor(out=ssum[:], in0=e[:, 0:8], in1=e[:, 8:16], op=ALU.add)), reads=[e], writes=[ssum])
        k.op(k.dve, (lambda en: en.tensor_tensor(out=ssum[:], in0=ssum[:], in1=e[:, 16:24], op=ALU.add)), reads=[e, ssum], writes=[ssum])
        k.op(k.dve, (lambda en: en.tensor_tensor(out=ssum[:], in0=ssum[:], in1=e[:, 24:32], op=ALU.add)), reads=[e, ssum], writes=[ssum])
        k.op(k.dve, (lambda en: en.reciprocal(ssum[:], ssum[:])), reads=[ssum], writes=[ssum])
        lbc = self.lb_col
        k.op(k.dve, (lambda en: en.memset(lbc[:, 0:8], 0.0)), writes=[lbc])
        for r in range(1, li + 1):
            k.op(k.dve, (lambda en, r=r: en.tensor_tensor(out=lbc[:, 0:8], in0=lbc[:, 0:8], in1=e[:, r * 8:(r + 1) * 8], op=ALU.add)),
                 reads=[e, lbc], writes=[lbc])
        k.op(k.dve, (lambda en: en.tensor_tensor(out=lbc[:, 0:8], in0=lbc[:, 0:8], in1=ssum[:], op=ALU.mult)), reads=[lbc, ssum], writes=[lbc])
        k.op(k.dve, (lambda en: en.tensor_scalar(lbc[:, 8:16], lbc[:, 0:8], -1.0, 1.0, op0=ALU.mult, op1=ALU.add)), reads=[lbc], writes=[lbc])
        lbr = self.lbrow_t.ap()
        k.dma(k.sp, lbr.rearrange("r (c p) -> p r c", p=128), lbc[:].rearrange("p (r c) -> p r c", c=8), reads=[lbc],
              allow_slow_non_contiguous=True)
        k.end()
        k.begin()
        k.dma(k.sp, self.oml_row[:], bass.AP(self.lbrow_t, D_MODEL, [[0, 128], [1, D_MODEL]]), writes=[self.oml_row])
        k.end()

    def phase_hg_proj(self, nrow):
        k = self.k
        QS = 128 ** -0.5

        def setup(T0, tl):
            ctx = {}
            ctx["tm"] = [k.sb([128, 512], F32, "tmk") for _ in tl]
            for ti, (o, n) in enumerate(tl):
                k.dma(k.sp, ctx["tm"][ti][:, 0:n], self.tmask_rep[:, T0 + o:T0 + o + n], writes=[ctx["tm"][ti]])
            ctx["sg"] = [k.sb([128, 512], F32, "sg") for _ in range(2)]
            ctx["st"] = [k.sb([128, 512], BF16, "st") for _ in range(3)]
            ctx["k32"] = [k.sb([128, 512], F32, "k32") for _ in range(2)]
            ctx["lf"] = [k.sb([128, 512], F32, "lf") for _ in range(2)]
            ctx["kb"] = [k.sb([128, 512], BF16, "kb") for _ in range(2)]
            ctx["i"] = 0
            return ctx

        def post_silu(dst, scale):
            def post(ctx, pt, T0, ti, o, n):
                i = ctx["i"]
                ctx["i"] += 1
                sg, st = ctx["sg"][i % 2], ctx["st"][i % 3]
                k.op(k.act, (lambda e: e.activation(sg[:, 0:n], pt[:, 0:n], AF.Silu)), reads=[pt], writes=[sg])
                k.op(k.pool, (lambda e: e.tensor_scalar(st[:, 0:n], sg[:, 0:n], scale, None, op0=ALU.mult)), reads=[sg], writes=[st])
                return st
            return post

        def fm_q(c):
            base = post_silu(None, QS)

            def post(ctx, pt, T0, ti, o, n):
                st = base(ctx, pt, T0, ti, o, n)
                k.dma(k.sp, self.fmA[c * 128:(c + 1) * 128, T0 + o:T0 + o + n], st[:, 0:n], reads=[st])
            return post

        def fm_g(c):
            base = post_silu(None, 1.0)

            def post(ctx, pt, T0, ti, o, n):
                st = base(ctx, pt, T0, ti, o, n)
                k.dma(k.sp, self.fmG[c * 128:(c + 1) * 128, T0 + o:T0 + o + n], st[:, 0:n], reads=[st])
            return post

        def fm_k(d, c):
            def post(ctx, pt, T0, ti, o, n):
                i = ctx["i"]
                ctx["i"] += 1
                sg, st = ctx["sg"][i % 2], ctx["st"][i % 3]
                k.op(k.act, (lambda e: e.activation(sg[:, 0:n], pt[:, 0:n], AF.Sigmoid, scale=-1.0)), reads=[pt], writes=[sg])
                k.op(k.dve, (lambda e: e.scalar_tensor_tensor(out=st[:, 0:n], in0=sg[:, 0:n], scalar=self.lb_col[:, 8 + c:9 + c],
                                                              in1=ctx["tm"][ti][:, 0:n], op0=ALU.mult, op1=ALU.mult)),
                     reads=[sg, self.lb_col, ctx["tm"][ti]], writes=[st])
                k.dma(k.sp, self.fmB[d][c * 128:(c + 1) * 128, T0 + o:T0 + o + n], st[:, 0:n], reads=[st])
            return post

        def tm_v(cb):
            def post(ctx, pt, tok0, r):
                i = ctx["i"]
                ctx["i"] += 1
                st = ctx["st"][i % 3]
                k.op(k.act, (lambda e: e.copy(st[0:r, :], pt[0:r, :])), reads=[pt], writes=[st])
                k.dma(k.sp, self.v_tm[tok0:tok0 + r, cb * 512:(cb + 1) * 512], st[0:r, :], reads=[st])
            return post

        def tm_f(d, cb):
            def post(ctx, pt, tok0, r):
                i = ctx["i"]
                ctx["i"] += 1
                sg, k32, lf, kb = ctx["sg"][i % 2], ctx["k32"][i % 2], ctx["lf"][i % 2], ctx["kb"][i % 2]
                blk = tok0 // 128
                assert tok0 % 128 == 0
                k.op(k.act, (lambda e: e.activation(sg[0:r, :], pt[0:r, :], AF.Sigmoid, scale=-1.0)), reads=[pt], writes=[sg])
                k.op(k.dve, (lambda e: e.scalar_tensor_tensor(out=k32[0:r, :], in0=sg[0:r, :], scalar=self.tmc[0:r, blk:blk + 1],
                                                              in1=self.oml_row[0:r, cb * 512:(cb + 1) * 512], op0=ALU.mult, op1=ALU.mult)),
                     reads=[sg, self.tmc, self.oml_row], writes=[k32])
                k.op(k.act, (lambda e: e.activation(lf[0:r, :], k32[0:r, :], AF.Ln, bias=self.one_col[0:r, 0:1], scale=-1.0)),
                     reads=[k32, self.one_col], writes=[lf])
                k.op(k.pool, (lambda e: e.tensor_copy(kb[0:r, :], k32[0:r, :])), reads=[k32], writes=[kb])
                k.dma(k.sp, self.tmK[d][tok0:tok0 + r, cb * 512:(cb + 1) * 512], kb[0:r, :], reads=[kb])
                k.dma(k.sp, self.tmL[d][tok0:tok0 + r, cb * 512:(cb + 1) * 512], lf[0:r, :], reads=[lf])
            return post

        W = self.c_w
        fm = []
        for c in range(8):
            fm.append((W[:, c * 128:(c + 1) * 128], fm_q(c)))
        for c in range(8):
            fm.append((W[:, 2048 + c * 128:2048 + (c + 1) * 128], fm_g(c)))
        for d in range(2):
            for c in range(8):
                fm.append((W[:, 3072 + d * 1024 + c * 128:3072 + d * 1024 + (c + 1) * 128], fm_k(d, c)))
        tm = []
        for cb in range(2):
            tm.append((W[:, 1024 + cb * 512:1024 + (cb + 1) * 512], tm_v(cb)))
        for d in range(2):
            for cb in range(2):
                tm.append((W[:, 3072 + d * 1024 + cb * 512:3072 + d * 1024 + (cb + 1) * 512], tm_f(d, cb)))
        self.phase_proj(nrow, setup, fm, tm)

    def phase_hg_core(self, onorm_col):
        k, nc, T = self.k, self.nc, self.T
        tiles = tiles_of(T, 512)
        for sweep in (1, 0):
            d = sweep
            k.begin()
            if d == 0:
                Uin, Uex, Mk, lastcol = self.tri[:, 0:64], self.tri[:, 64:128], 0, 63
            else:
                Uin, Uex, Mk, lastcol = self.tri[:, 128:192], self.tri[:, 192:256], 2, 0
            mrep = k.sb([64, 512], F32, "mrep")
            for c in range(8):
                k.op(k.dve, (lambda e, c=c, Mk=Mk: e.tensor_copy(mrep[:, c * 64:(c + 1) * 64], self.tri[:, Mk * 64:(Mk + 1) * 64])),
                     reads=[self.tri], writes=[mrep])
            S = [k.sb([128, 128], F32, "S") for _ in range(8)]
            Sb = [k.sb([128, 128], BF16, "Sb") for _ in range(8)]
            for h in range(8):
                k.op(k.dve, (lambda e, h=h: e.memset(S[h][:], 0.0)), writes=[S[h]])
                k.op(k.pool, (lambda e, h=h: e.memset(Sb[h][:], 0.0)), writes=[Sb[h]])
            NB = 2
            qT = [k.sb([128, 512], BF16, "qT") for _ in range(NB)]
            kT = [k.sb([128, 512], BF16, "kT") for _ in range(NB)]
            ktm = [k.sb([64, 8, 128], BF16, "ktm") for _ in range(NB)]
            lf = [k.sb([64, 8, 128], F32, "lf") for _ in range(NB)]
            vt = [k.sb([64, 8, 128], BF16, "vt") for _ in range(NB)]
            eb = [k.sb([128, 512], F32, "eb") for _ in range(NB)]
            enb = [k.sb([128, 512], F32, "enb") for _ in range(NB)]
            qd = [k.sb([128, 512], BF16, "qd") for _ in range(NB)]
            kd = [k.sb([128, 512], BF16, "kd") for _ in range(NB)]
            ekd = [k.sb([64, 8, 128], F32, "ekd") for _ in range(NB)]
            kdec = [k.sb([64, 8, 128], BF16, "kdec") for _ in range(NB)]
            atm = [k.sb([64, 512], BF16, "atm") for _ in range(NB)]
            B = [k.ps([128, 512], F32, "Y%d" % i) for i in range(8)]
            ost = [k.sb([128, 512], F32, "ost") for _ in range(2)]
            if d == 0:
                obt = [k.sb([128, 512], F32, "obt") for _ in range(2)]
                gt = [k.sb([128, 512], BF16, "gt") for _ in range(2)]
                sqo_l = [k.sb([128, 512], BF16, "sqo") for _ in range(2)]
                rs_l = [k.sb([128, 512], F32, "rs") for _ in range(2)]
                fin = [k.sb([128, 512], BF16, "fin") for _ in range(2)]
            order = list(range(len(tiles)))
            if d == 1:
                order = order[::-1]

            def unit(h, b, it, t0, n, ncn, corder):
                if True:
                    rows = slice(h * 128, (h + 1) * 128)
                    Y = B[4 * b:4 * b + 4]
                    pbc, pat = Y[0], Y[3]
                    psuf = [Y[1], Y[2]]
                    pout = Y[1]
                    pst = [Y[2], Y[2]]
                    if d == 0:
                        sqo, rs = sqo_l[b], rs_l[b]
                    k.dma(k.sp, qT[b][:, 0:n], self.fmA[rows, t0:t0 + n], writes=[qT[b]])
                    k.dma(k.sp, kT[b][:, 0:n], self.fmB[d][rows, t0:t0 + n], writes=[kT[b]])
                    k.dma(k.sp, ktm[b][:, 0:ncn, :], self.tmK[d][t0:t0 + n, rows].rearrange("(c p) e -> p c e", p=64), writes=[ktm[b]])
                    k.dma(k.sp, lf[b][:, 0:ncn, :], self.tmL[d][t0:t0 + n, rows].rearrange("(c p) e -> p c e", p=64), writes=[lf[b]])
                    k.dma(k.sp, vt[b][:, 0:ncn, :], self.v_tm[t0:t0 + n, rows].rearrange("(c p) e -> p c e", p=64), writes=[vt[b]])
                    if d == 0:
                        k.dma(k.sp, obt[b][:, 0:n], self.obT[rows, t0:t0 + n], writes=[obt[b]])
                        k.dma(k.sp, gt[b][:, 0:n], self.fmG[rows, t0:t0 + n], writes=[gt[b]])
                    for c in range(ncn):
                        k.op(k.pe, (lambda e, b=b, c=c: e.matmul(pbc[:, c * 64:(c + 1) * 64], lhsT=lf[b][:, c, :], rhs=Uin,
                                                                 start=True, stop=True)), reads=[lf[b], self.tri], writes=[pbc])
                    k.op(k.act, (lambda e, b=b, n=n: e.activation(eb[b][:, 0:n], pbc[:, 0:n], AF.Exp)), reads=[pbc], writes=[eb[b]])
                    k.op(k.act, (lambda e, b=b, n=n: e.activation(enb[b][:, 0:n], pbc[:, 0:n], AF.Exp, scale=-1.0)), reads=[pbc], writes=[enb[b]])
                    k.op(k.dve, (lambda e, b=b, n=n: e.tensor_tensor(out=qd[b][:, 0:n], in0=qT[b][:, 0:n], in1=eb[b][:, 0:n], op=ALU.mult)),
                         reads=[qT[b], eb[b]], writes=[qd[b]])
                    k.op(k.dve, (lambda e, b=b, n=n: e.tensor_tensor(out=kd[b][:, 0:n], in0=kT[b][:, 0:n], in1=enb[b][:, 0:n], op=ALU.mult)),
                         reads=[kT[b], enb[b]], writes=[kd[b]])
                    for c in range(ncn):
                        ps_ = psuf[c // 4]
                        k.op(k.pe, (lambda e, b=b, c=c, ps_=ps_: e.matmul(ps_[0:64, (c % 4) * 128:(c % 4 + 1) * 128], lhsT=Uex, rhs=lf[b][:, c, :],
                                                                          start=True, stop=True)), reads=[lf[b], self.tri], writes=[ps_])
                    for half in range((ncn + 3) // 4):
                        nn = min(4, ncn - half * 4)
                        k.op(k.act, (lambda e, b=b, half=half, nn=nn, pq=psuf[half]: e.activation(
                            ekd[b][:, half * 4:half * 4 + nn, :], pq[0:64, 0:nn * 128].rearrange("p (c d) -> p c d", d=128), AF.Exp)),
                             reads=[psuf[half]], writes=[ekd[b]])
                    k.op(k.dve, (lambda e, b=b, ncn=ncn: e.tensor_tensor(out=kdec[b][:, 0:ncn, :], in0=ktm[b][:, 0:ncn, :],
                                                                        in1=ekd[b][:, 0:ncn, :], op=ALU.mult)),
                         reads=[ktm[b], ekd[b]], writes=[kdec[b]])
                    for c in range(ncn):
                        cs = slice(c * 64, (c + 1) * 64)
                        k.op(k.pe, (lambda e, b=b, cs=cs: e.matmul(pat[0:64, cs], lhsT=kd[b][:, cs], rhs=qd[b][:, cs], start=True, stop=True)),
                             reads=[kd[b], qd[b]], writes=[pat])
                    k.op(k.dve, (lambda e, b=b, n=n: e.tensor_tensor(out=atm[b][:, 0:n], in0=pat[0:64, 0:n], in1=mrep[:, 0:n], op=ALU.mult)),
                         reads=[pat, mrep], writes=[atm[b]])
                    k.mark()
                    for c in corder:
                        cs = slice(c * 64, (c + 1) * 64)
                        k.op(k.pe, (lambda e, b=b, cs=cs, h=h: e.matmul(pout[:, cs], lhsT=Sb[h][:], rhs=qd[b][:, cs], start=True, stop=False)),
                             reads=[Sb[h], qd[b]], writes=[pout])
                        k.op(k.pe, (lambda e, b=b, cs=cs, c=c: e.matmul(pout[:, cs], lhsT=vt[b][:, c, :], rhs=atm[b][:, cs], start=False, stop=True)),
                             reads=[vt[b], atm[b]], writes=[pout])
                        pp = pst[c % 2]
                        pc = slice((c % 2) * 128, (c % 2 + 1) * 128)
                        k.op(k.pe, (lambda e, b=b, c=c, pp=pp, pc=pc: e.matmul(pp[:, pc], lhsT=kdec[b][:, c, :], rhs=vt[b][:, c, :], start=True, stop=True)),
                             reads=[kdec[b], vt[b]], writes=[pp])
                        fc = c * 64 + lastcol
                        k.op(k.dve, (lambda e, b=b, h=h, pp=pp, fc=fc, pc=pc: e.scalar_tensor_tensor(
                            out=S[h][:], in0=S[h][:], scalar=eb[b][:, fc:fc + 1], in1=pp[:, pc], op0=ALU.mult, op1=ALU.add)),
                            reads=[S[h], eb[b], pp], writes=[S[h]])
                        k.op(k.act, (lambda e, h=h: e.copy(Sb[h][:], S[h][:])), reads=[S[h]], writes=[Sb[h]])
                    if d == 1:
                        os_ = ost[it % 2]
                        k.op(k.act, (lambda e, os_=os_, n=n: e.copy(os_[:, 0:n], pout[:, 0:n])), reads=[pout], writes=[os_])
                        k.dma(k.sp, self.obT[rows, t0:t0 + n], os_[:, 0:n], reads=[os_])
                    else:
                        os_ = ost[it % 2]
                        k.op(k.dve, (lambda e, os_=os_, b=b, n=n: e.tensor_tensor(out=os_[:, 0:n], in0=pout[:, 0:n], in1=obt[b][:, 0:n], op=ALU.add)),
                             reads=[pout, obt[b]], writes=[os_])
                        k.op(k.act, (lambda e, os_=os_, n=n: e.activation(sqo[:, 0:n], os_[:, 0:n], AF.Square)), reads=[os_], writes=[sqo])
                        k.op(k.pe, (lambda e, n=n: e.matmul(pbc[:, 0:n], lhsT=self.ones_bf[:], rhs=sqo[:, 0:n], start=True, stop=True)),
                             reads=[self.ones_bf, sqo], writes=[pbc])
                        k.op(k.act, (lambda e, n=n: e.activation(rs[:, 0:n], pbc[:, 0:n], AF.Ln, bias=self.eps_col[:, 0:1], scale=1.0 / 128)),
                             reads=[pbc, self.eps_col], writes=[rs])
                        k.op(k.act, (lambda e, n=n: e.activation(rs[:, 0:n], rs[:, 0:n], AF.Exp, scale=-0.5)), reads=[rs], writes=[rs])
                        k.op(k.dve, (lambda e, os_=os_, n=n: e.scalar_tensor_tensor(out=os_[:, 0:n], in0=os_[:, 0:n], scalar=onorm_col,
                                                                                   in1=rs[:, 0:n], op0=ALU.mult, op1=ALU.mult)),
                             reads=[os_, rs, self.csm], writes=[os_])
                        fo = fin[it % 2]
                        k.op(k.pool, (lambda e, os_=os_, fo=fo, b=b, n=n: e.tensor_tensor(out=fo[:, 0:n], in0=os_[:, 0:n], in1=gt[b][:, 0:n], op=ALU.mult)),
                             reads=[os_, gt[b]], writes=[fo])
                        k.dma(k.sp, self.oT[rows, t0:t0 + n], fo[:, 0:n], reads=[fo])
            uid = 0
            prev_scan = []
            for ti in order:
                t0, n = tiles[ti]
                ncn = n // 64
                corder = list(range(ncn)) if d == 0 else list(range(ncn))[::-1]
                for h in range(8):
                    b = uid % NB
                    uid += 1
                    k.marks = []
                    k.set_lane("u")
                    unit(h, b, uid, t0, n, ncn, corder)
                    k.set_lane(None)
                    recs = k.lanes.pop("u", [])
                    cut = k.marks[0] if k.marks else len(recs)
                    k.emit_interleaved(prev_scan, recs[:cut])
                    prev_scan = recs[cut:]
            k.emit_recs(prev_scan)
            k.end()

    def mixer_hgrn(self, li, nrow):
        self.hg_prep(li)
        self.phase_hg_proj(nrow)
        self.phase_hg_core(self.csm[:, 32:33])
        self.phase_out_proj(self.c_wo)


    def phase_gd_proj(self, j, nrow):
        k = self.k
        W = self.a_w[j]

        def setup(T0, tl):
            ctx = {}
            ctx["tm"] = [k.sb([128, 512], F32, "tmk") for _ in tl]
            for ti, (o, n) in enumerate(tl):
                k.dma(k.sp, ctx["tm"][ti][:, 0:n], self.tmask_rep[:, T0 + o:T0 + o + n], writes=[ctx["tm"][ti]])
            ctx["sg"] = [k.sb([128, 512], F32, "sg") for _ in range(2)]
            ctx["st"] = [k.sb([128, 512], BF16, "st") for _ in range(3)]
            ctx["z"] = [k.sb([128, 16], F32, "z") for _ in range(2)]
            ctx["bat"] = [k.sb([128, 32], F32, "bat") for _ in range(2)]
            na = k.sb([128, 16], F32, "negA")
            k.op(k.act, (lambda e: e.activation(na[:], self.asm[:, 0:16], AF.Exp)), reads=[self.asm], writes=[na])
            k.op(k.dve, (lambda e: e.tensor_scalar(na[:], na[:], -1.0, None, op0=ALU.mult)), reads=[na], writes=[na])
            ctx["negA"] = na
            ctx["i"] = 0
            return ctx

        def fm_pre(c):
            def post(ctx, pt, T0, ti, o, n):
                i = ctx["i"]
                ctx["i"] += 1
                st = ctx["st"][i % 3]
                k.op(k.dve, (lambda e: e.tensor_tensor(out=st[:, 0:n], in0=pt[:, 0:n], in1=ctx["tm"][ti][:, 0:n], op=ALU.mult)),
                     reads=[pt, ctx["tm"][ti]], writes=[st])
                k.dma(k.sp, self.pre[c * 128:(c + 1) * 128, T0 + o:T0 + o + n], st[:, 0:n], reads=[st])
            return post

        def fm_g(c):
            def post(ctx, pt, T0, ti, o, n):
                i = ctx["i"]
                ctx["i"] += 1
                st = ctx["st"][i % 3]
                k.op(k.act, (lambda e: e.activation(st[:, 0:n], pt[:, 0:n], AF.Silu)), reads=[pt], writes=[st])
                k.dma(k.sp, self.fmG[c * 128:(c + 1) * 128, T0 + o:T0 + o + n], st[:, 0:n], reads=[st])
            return post

        def tm_ba(ctx, pt, tok0, r):
            i = ctx["i"]
            ctx["i"] += 1
            z, bat = ctx["z"][i % 2], ctx["bat"][i % 2]
            blk = tok0 // 128
            assert tok0 % 128 == 0
            k.op(k.dve, (lambda e: e.tensor_tensor(out=z[0:r, :], in0=pt[0:r, 16:32], in1=self.asm[0:r, 16:32], op=ALU.add)),
                 reads=[pt, self.asm], writes=[z])
            k.op(k.act, (lambda e: e.activation(z[0:r, :], z[0:r, :], AF.Exp)), reads=[z], writes=[z])
            k.op(k.act, (lambda e: e.activation(z[0:r, :], z[0:r, :], AF.Ln, bias=self.one_col[0:r, 0:1])), reads=[z, self.one_col], writes=[z])
            k.op(k.dve, (lambda e: e.tensor_tensor(out=bat[0:r, 16:32], in0=z[0:r, :], in1=ctx["negA"][0:r, :], op=ALU.mult)),
                 reads=[z, ctx["negA"]], writes=[bat])
            k.op(k.act, (lambda e: e.activation(bat[0:r, 0:16], pt[0:r, 0:16], AF.Sigmoid)), reads=[pt], writes=[bat])
            k.op(k.dve, (lambda e: e.tensor_scalar(bat[0:r, 0:16], bat[0:r, 0:16], self.tmc[0:r, blk:blk + 1], None, op0=ALU.mult)),
                 reads=[bat, self.tmc], writes=[bat])
            k.dma(k.sp, self.ba_tm[tok0:tok0 + r, :], bat[0:r, :], reads=[bat])

        fm = []
        for c in range(24):
            fm.append((W[:, c * 128:(c + 1) * 128], fm_pre(c)))
        for c in range(8):
            fm.append((W[:, 3072 + c * 128:3072 + (c + 1) * 128], fm_g(c)))
        tm = [(W[:, 4096:4128], tm_ba, 32)]
        self.phase_proj(nrow, setup, fm, tm)

    def phase_gd_conv(self):
        k, T = self.k, self.T
        tiles = tiles_of(T, 512)
        QS = 128 ** -0.5
        k.begin()
        xin = [k.sb([128, 516], BF16, "cx") for _ in range(3)]
        acc = [k.sb([128, 512], F32, "cacc") for _ in range(2)]
        sv = [k.sb([128, 512], F32, "csv") for _ in range(2)]
        sq = [k.sb([128, 512], BF16, "csq") for _ in range(2)]
        rs = [k.sb([128, 512], F32, "crs") for _ in range(2)]
        ob = [k.sb([128, 512], BF16, "cob") for _ in range(3)]
        tmt = [k.sb([128, 512], F32, "ctm") for _ in range(2)]
        tb = [k.sb([128, 4, 128], BF16, "ctb") for _ in range(2)]
        pss = [k.ps([128, 512], F32, "cps") for _ in range(2)]
        ptb = [k.ps([128, 512], BF16, "cpt") for _ in range(2)]
        it = 0
        for ti, (t0, n) in enumerate(tiles):
            tmk = tmt[ti % 2]
            k.dma(k.sp, tmk[:, 0:n], self.tmask_rep[:, t0:t0 + n], writes=[tmk])
            for cc in range(24):
                kind = cc // 8
                x = xin[it % 3]
                a_, s_, q_, r_, o_ = acc[it % 2], sv[it % 2], sq[it % 2], rs[it % 2], ob[it % 3]
                ps = pss[it % 2]
                if it % 2 == 0:
                    k.merge()
                k.set_lane(it % 2)
                it += 1
                lo = max(t0 - 2, 0)
                hi = min(t0 + n + 2, T)
                if lo > t0 - 2 or hi < t0 + n + 2:
                    k.op(k.pool, (lambda e, x=x: e.memset(x[:], 0.0)), writes=[x])
                k.dma(k.sp, x[:, lo - (t0 - 2):hi - (t0 - 2)], self.pre[cc * 128:(cc + 1) * 128, lo:hi], writes=[x])
                wcol = lambda jj, cc=cc: self.asm[:, 33 + cc * 5 + jj:34 + cc * 5 + jj]
                k.op(k.dve, (lambda e, a_=a_, x=x, n=n, w=wcol(0): e.tensor_scalar(a_[:, 0:n], x[:, 0:n], w, None, op0=ALU.mult)),
                     reads=[x, self.asm], writes=[a_])
                for jj in range(1, 5):
                    k.op(k.dve, (lambda e, a_=a_, x=x, n=n, jj=jj, w=wcol(jj): e.scalar_tensor_tensor(
                        out=a_[:, 0:n], in0=x[:, jj:jj + n], scalar=w, in1=a_[:, 0:n], op0=ALU.mult, op1=ALU.add)),
                        reads=[x, a_, self.asm], writes=[a_])
                k.op(k.act, (lambda e, a_=a_, s_=s_, n=n: e.activation(s_[:, 0:n], a_[:, 0:n], AF.Silu)), reads=[a_], writes=[s_])
                if kind < 2:
                    k.op(k.act, (lambda e, s_=s_, q_=q_, n=n: e.activation(q_[:, 0:n], s_[:, 0:n], AF.Square)), reads=[s_], writes=[q_])
                    k.op(k.pe, (lambda e, ps=ps, q_=q_, n=n: e.matmul(ps[:, 0:n], lhsT=self.ones_bf[:], rhs=q_[:, 0:n], start=True, stop=True)),
                         reads=[self.ones_bf, q_], writes=[ps])
                    k.op(k.act, (lambda e, ps=ps, r_=r_, n=n: e.activation(r_[:, 0:n], ps[:, 0:n], AF.Ln, bias=self.eps_col[:, 0:1])),
                         reads=[ps, self.eps_col], writes=[r_])
                    k.op(k.act, (lambda e, r_=r_, n=n: e.activation(r_[:, 0:n], r_[:, 0:n], AF.Exp, scale=-0.5)), reads=[r_], writes=[r_])
                if kind == 0:
                    k.op(k.dve, (lambda e, o_=o_, s_=s_, r_=r_, n=n: e.scalar_tensor_tensor(
                        out=o_[:, 0:n], in0=s_[:, 0:n], scalar=QS, in1=r_[:, 0:n], op0=ALU.mult, op1=ALU.mult)),
                        reads=[s_, r_], writes=[o_])
                    k.dma(k.sp, self.fmA[cc * 128:(cc + 1) * 128, t0:t0 + n], o_[:, 0:n], reads=[o_])
                    k.set_lane(None)
                    continue
                if kind == 1:
                    k.op(k.dve, (lambda e, s_=s_, r_=r_, n=n: e.tensor_tensor(out=s_[:, 0:n], in0=s_[:, 0:n], in1=r_[:, 0:n], op=ALU.mult)),
                         reads=[s_, r_], writes=[s_])
                    k.op(k.pool, (lambda e, o_=o_, s_=s_, tmk=tmk, n=n: e.tensor_tensor(out=o_[:, 0:n], in0=s_[:, 0:n], in1=tmk[:, 0:n], op=ALU.mult)),
                         reads=[s_, tmk], writes=[o_])
                    k.dma(k.sp, self.fmB[0][(cc - 8) * 128:(cc - 7) * 128, t0:t0 + n], o_[:, 0:n], reads=[o_])
                    dst = self.tmK[0]
                else:
                    k.op(k.pool, (lambda e, o_=o_, s_=s_, n=n: e.tensor_copy(o_[:, 0:n], s_[:, 0:n])), reads=[s_], writes=[o_])
                    dst = self.v_tm
                cl = (cc % 8)
                pt = ptb[it % 2]
                tt = tb[it % 2]
                nb = (n + 127) // 128
                for b in range(nb):
                    r = min(128, n - b * 128)
                    k.op(k.pe, (lambda e, pt=pt, o_=o_, b=b, r=r: e.transpose(pt[0:r, b * 128:(b + 1) * 128], o_[:, b * 128:b * 128 + r],
                                                                             self.identb[:])),
                         reads=[o_, self.identb], writes=[pt])
                if n % 128 == 0:
                    k.op(k.act, (lambda e, pt=pt, tt=tt, nb=nb: e.copy(tt[:, 0:nb, :], pt[:, 0:nb * 128].rearrange("p (b d) -> p b d", d=128))),
                         reads=[pt], writes=[tt])
                    k.dma(k.sp, dst[t0:t0 + n, cl * 128:(cl + 1) * 128].rearrange("(b p) d -> p b d", p=128), tt[:, 0:nb, :], reads=[tt])
                else:
                    for b in range(nb):
                        r = min(128, n - b * 128)
                        k.op(k.act, (lambda e, pt=pt, tt=tt, b=b, r=r: e.copy(tt[0:r, b, :], pt[0:r, b * 128:(b + 1) * 128])),
                             reads=[pt], writes=[tt])
                        k.dma(k.sp, dst[t0 + b * 128:t0 + b * 128 + r, cl * 128:(cl + 1) * 128], tt[0:r, b, :], reads=[tt])
                k.set_lane(None)
        k.merge()
        k.end()

    def phase_gd_core(self):
        import os
        GDCUT = int(os.environ.get("GD_CUT", "99"))
        GDSUB = int(os.environ.get("GD_SUB", "99"))
        k, nc, T = self.k, self.nc, self.T
        tiles = tiles_of(T, 512)
        onorm_col = self.asm[:, 32:33]
        for sweep in (1, 0):
            d = sweep
            k.begin()
            tri = self.tri
            if d == 0:
                Uin, MS, MI, MN = tri[:, 0:64], 1, 0, 3
            else:
                Uin, MS, MI, MN = tri[:, 128:192], 3, 2, 1
            mI = k.sb([64, 512], F32, "mI")
            mS = k.sb([64, 512], F32, "mS")
            mN = k.sb([64, 512], F32, "mN")
            idr = k.sb([64, 512], F32, "idr")
            for c in range(8):
                cs = slice(c * 64, (c + 1) * 64)
                k.op(k.dve, (lambda e, cs=cs: e.tensor_copy(mI[:, cs], tri[:, MI * 64:(MI + 1) * 64])), reads=[tri], writes=[mI])
                k.op(k.dve, (lambda e, cs=cs: e.tensor_copy(mS[:, cs], tri[:, MS * 64:(MS + 1) * 64])), reads=[tri], writes=[mS])
                k.op(k.dve, (lambda e, cs=cs: e.tensor_copy(mN[:, cs], tri[:, MN * 64:(MN + 1) * 64])), reads=[tri], writes=[mN])
                k.op(k.dve, (lambda e, cs=cs: e.tensor_copy(idr[:, cs], self.ident[0:64, 0:64])), reads=[self.ident], writes=[idr])
            ones_f = k.sb([64, 128], F32, "onesf")
            k.op(k.dve, (lambda e: e.memset(ones_f[:], 1.0)), writes=[ones_f])
            S = [k.sb([128, 128], F32, "S") for _ in range(8)]
            Sb = [k.sb([128, 128], BF16, "Sb") for _ in range(8)]
            for h in range(8):
                k.op(k.dve, (lambda e, h=h: e.memset(S[h][:], 0.0)), writes=[S[h]])
                k.op(k.pool, (lambda e, h=h: e.memset(Sb[h][:], 0.0)), writes=[Sb[h]])
            NB = 2
            mk = lambda shp, dt, nm: [k.sb(shp, dt, nm) for _ in range(NB)]
            qT, kT = mk([128, 512], BF16, "qT"), mk([128, 512], BF16, "kT")
            ktm, vtm = mk([64, 8, 128], BF16, "ktm"), mk([64, 8, 128], BF16, "vtm")
            bat = mk([64, 8, 32], F32, "bat")
            lab = mk([64, 2, 8], F32, "lab")
            gc, egc, bek, dcol = mk([64, 8], F32, "gc"), mk([64, 8], F32, "egc"), mk([64, 8], F32, "bek"), mk([64, 8], F32, "dcol")
            alast = mk([128, 8], F32, "alast")
            dg = mk([64, 512], F32, "dg")
            egr = mk([128, 512], F32, "egr")
            qd = mk([128, 512], BF16, "qd")
            fab = mk([64, 512], F32, "fab")
            fm_ = mk([64, 512], F32, "fm")
            fmi = mk([64, 512], F32, "fmi")
            gf = mk([64, 512], F32, "gf")
            a32 = mk([64, 512], F32, "a32")
            Rb = [mk([64, 512], F32, "Rb%d" % i) for i in range(2)]
            Pb = [mk([64, 512], F32, "Pb%d" % i) for i in range(2)]
            PTb = [mk([64, 512], F32, "PTb%d" % i) for i in range(2)]
            rhu, rhw, kdec = mk([64, 8, 128], F32, "rhu"), mk([64, 8, 128], F32, "rhw"), mk([64, 8, 128], BF16, "kdec")
            gfu = mk([64, 512], F32, "gfu")
            fmS = mk([64, 512], F32, "fmS")
            fmN = mk([64, 512], F32, "fmN")
            dgb = mk([64, 512], F32, "dgb")
            u_sb = mk([64, 8, 128], F32, "u")
            nwT = mk([128, 512], BF16, "nwT")
            qkm = mk([64, 512], BF16, "qkm")
            vn = [[k.sb([64, 128], BF16, "vn") for _ in range(2)] for _ in range(NB)]
            ost = [k.sb([128, 512], F32, "ost") for _ in range(2)]
            B = [k.ps([128, 512], F32, "B%d" % i) for i in range(8)]
            if d == 0:
                obt = mk([128, 512], F32, "obt")
                gt = mk([128, 512], BF16, "gt")
                sqo_l = mk([128, 512], BF16, "sqo")
                rs_l = mk([128, 512], F32, "rs")
                fin = [k.sb([128, 512], BF16, "fin") for _ in range(2)]
            order = list(range(len(tiles)))
            if d == 1:
                order = order[::-1]

            def unit(h, b, it, t0, n, ncn, corder):
                if True:
                    rows = slice(h * 128, (h + 1) * 128)
                    X = B[4 * b:4 * b + 4]
                    Bs, Brow, BG, Bb = X
                    BD, BP, BT = X[1], X[2], X[3]
                    Bq, Bo, Bst, Bv = X[0], X[1], X[2], X[0]
                    k.dma(k.sp, qT[b][:, 0:n], self.fmA[rows, t0:t0 + n], writes=[qT[b]])
                    k.dma(k.sp, kT[b][:, 0:n], self.fmB[0][rows, t0:t0 + n], writes=[kT[b]])
                    k.dma(k.sp, ktm[b][:, 0:ncn, :], self.tmK[0][t0:t0 + n, rows].rearrange("(c p) e -> p c e", p=64), writes=[ktm[b]])
                    k.dma(k.sp, vtm[b][:, 0:ncn, :], self.v_tm[t0:t0 + n, rows].rearrange("(c p) e -> p c e", p=64), writes=[vtm[b]])
                    k.dma(k.sp, bat[b][:, 0:ncn, :], self.ba_tm[t0:t0 + n, :].rearrange("(c p) e -> p c e", p=64), writes=[bat[b]])
                    if d == 0:
                        k.dma(k.sp, obt[b][:, 0:n], self.obT[rows, t0:t0 + n], writes=[obt[b]])
                        k.dma(k.sp, gt[b][:, 0:n], self.fmG[rows, t0:t0 + n], writes=[gt[b]])
                    col = d * 8 + h
                    k.op(k.dve, (lambda e, b=b, ncn=ncn, col=col: e.tensor_copy(lab[b][:, 0, 0:ncn], bat[b][:, 0:ncn, col])),
                         reads=[bat[b]], writes=[lab[b]])
                    k.op(k.dve, (lambda e, b=b, ncn=ncn, col=col: e.tensor_copy(lab[b][:, 1, 0:ncn], bat[b][:, 0:ncn, 16 + col])),
                         reads=[bat[b]], writes=[lab[b]])
                    beta = lambda c, b=b: lab[b][:, 0, c:c + 1]
                    k.op(k.pe, (lambda e, b=b, ncn=ncn: e.matmul(Bs[0:64, 0:ncn], lhsT=Uin, rhs=lab[b][:, 1, 0:ncn], start=True, stop=True)),
                         reads=[tri, lab[b]], writes=[Bs])
                    k.op(k.pe, (lambda e, b=b, ncn=ncn: e.matmul(Bs[:, 16:16 + ncn], lhsT=ones_f[:], rhs=lab[b][:, 1, 0:ncn], start=True, stop=True)),
                         reads=[ones_f, lab[b]], writes=[Bs])
                    k.op(k.dve, (lambda e, b=b, ncn=ncn: e.tensor_copy(gc[b][:, 0:ncn], Bs[0:64, 0:ncn])), reads=[Bs], writes=[gc[b]])
                    k.op(k.act, (lambda e, b=b, ncn=ncn: e.activation(alast[b][:, 0:ncn], Bs[:, 16:16 + ncn], AF.Exp)), reads=[Bs], writes=[alast[b]])
                    k.op(k.dve, (lambda e, b=b, ncn=ncn: e.tensor_tensor(out=dcol[b][:, 0:ncn], in0=Bs[0:64, 16:16 + ncn], in1=gc[b][:, 0:ncn],
                                                                        op=ALU.subtract)), reads=[Bs, gc[b]], writes=[dcol[b]])
                    k.op(k.act, (lambda e, b=b, ncn=ncn: e.activation(dcol[b][:, 0:ncn], dcol[b][:, 0:ncn], AF.Exp)), reads=[dcol[b]], writes=[dcol[b]])
                    k.op(k.act, (lambda e, b=b, ncn=ncn: e.activation(egc[b][:, 0:ncn], gc[b][:, 0:ncn], AF.Exp)), reads=[gc[b]], writes=[egc[b]])
                    k.op(k.dve, (lambda e, b=b, ncn=ncn: e.tensor_tensor(out=bek[b][:, 0:ncn], in0=egc[b][:, 0:ncn], in1=lab[b][:, 0, 0:ncn],
                                                                        op=ALU.mult)), reads=[egc[b], lab[b]], writes=[bek[b]])
                    if GDCUT < 1:
                        return
                    k.op(k.dve, (lambda e, b=b, n=n, ncn=ncn: e.tensor_tensor(
                        out=dg[b][:, 0:n].rearrange("p (c i) -> p c i", i=64), in0=idr[:, 0:n].rearrange("p (c i) -> p c i", i=64),
                        in1=gc[b][:, 0:ncn].unsqueeze(2).to_broadcast([64, ncn, 64]), op=ALU.mult)), reads=[idr, gc[b]], writes=[dg[b]])
                    for c in range(ncn):
                        cs = slice(c * 64, (c + 1) * 64)
                        k.op(k.pe, (lambda e, b=b, cs=cs: e.matmul(Brow[:, cs], lhsT=ones_f[:], rhs=dg[b][:, cs], start=True, stop=True)),
                             reads=[ones_f, dg[b]], writes=[Brow])
                    k.op(k.act, (lambda e, b=b, n=n: e.activation(egr[b][:, 0:n], Brow[:, 0:n], AF.Exp)), reads=[Brow], writes=[egr[b]])
                    k.op(k.dve, (lambda e, b=b, n=n: e.tensor_tensor(out=qd[b][:, 0:n], in0=qT[b][:, 0:n], in1=egr[b][:, 0:n], op=ALU.mult)),
                         reads=[qT[b], egr[b]], writes=[qd[b]])
                    k.op(k.dve, (lambda e, b=b, n=n, ncn=ncn: e.tensor_tensor(
                        out=fab[b][:, 0:n].rearrange("p (c i) -> p c i", i=64), in0=Brow[0:64, 0:n].rearrange("p (c i) -> p c i", i=64),
                        in1=gc[b][:, 0:ncn].unsqueeze(2).to_broadcast([64, ncn, 64]), op=ALU.subtract)), reads=[Brow, gc[b]], writes=[fab[b]])
                    k.op(k.act, (lambda e, b=b, n=n: e.activation(fab[b][:, 0:n], fab[b][:, 0:n], AF.Abs)), reads=[fab[b]], writes=[fab[b]])
                    k.op(k.act, (lambda e, b=b, n=n: e.activation(fm_[b][:, 0:n], fab[b][:, 0:n], AF.Exp, scale=-1.0)), reads=[fab[b]], writes=[fm_[b]])
                    k.op(k.pool, (lambda e, b=b, n=n: e.tensor_tensor(out=fmi[b][:, 0:n], in0=fm_[b][:, 0:n], in1=mI[:, 0:n], op=ALU.mult)),
                         reads=[fm_[b], mI], writes=[fmi[b]])
                    k.op(k.pool, (lambda e, b=b, n=n: e.tensor_tensor(out=fmS[b][:, 0:n], in0=fm_[b][:, 0:n], in1=mS[:, 0:n], op=ALU.mult)),
                         reads=[fm_[b], mS], writes=[fmS[b]])
                    k.op(k.pool, (lambda e, b=b, n=n: e.tensor_tensor(out=fmN[b][:, 0:n], in0=fm_[b][:, 0:n], in1=mN[:, 0:n], op=ALU.mult)),
                         reads=[fm_[b], mN], writes=[fmN[b]])
                    if GDCUT < 2:
                        return
                    k.op(k.dve, (lambda e, b=b, n=n, ncn=ncn: e.tensor_tensor(
                        out=dgb[b][:, 0:n].rearrange("p (c i) -> p c i", i=64), in0=idr[:, 0:n].rearrange("p (c i) -> p c i", i=64),
                        in1=lab[b][:, 0, 0:ncn].unsqueeze(2).to_broadcast([64, ncn, 64]), op=ALU.mult)), reads=[idr, lab[b]], writes=[dgb[b]])
                    for c in range(ncn):
                        cs = slice(c * 64, (c + 1) * 64)
                        k.op(k.pe, (lambda e, b=b, cs=cs: e.matmul(BG[0:64, cs], lhsT=kT[b][:, cs], rhs=kT[b][:, cs], start=True, stop=True)),
                             reads=[kT[b]], writes=[BG])
                    for c in range(ncn):
                        cs = slice(c * 64, (c + 1) * 64)
                        k.op(k.pe, (lambda e, b=b, cs=cs: e.matmul(Bb[:, cs], lhsT=ones_f[:], rhs=dgb[b][:, cs], start=True, stop=True)),
                             reads=[ones_f, dgb[b]], writes=[Bb])
                    k.op(k.dve, (lambda e, b=b, n=n: e.tensor_tensor(out=gf[b][:, 0:n], in0=BG[0:64, 0:n], in1=fmS[b][:, 0:n], op=ALU.mult)),
                         reads=[BG, fmS[b]], writes=[gf[b]])
                    k.op(k.dve, (lambda e, b=b, n=n: e.tensor_tensor(out=gfu[b][:, 0:n], in0=BG[0:64, 0:n], in1=fmN[b][:, 0:n], op=ALU.mult)),
                         reads=[BG, fmN[b]], writes=[gfu[b]])
                    R0, P0, PT0 = Rb[0][b], Pb[0][b], PTb[0][b]
                    k.op(k.dve, (lambda e, b=b, n=n, ncn=ncn, PT0=PT0: e.tensor_tensor(
                        out=PT0[:, 0:n].rearrange("p (c i) -> p c i", i=64), in0=gf[b][:, 0:n].rearrange("p (c i) -> p c i", i=64),
                        in1=lab[b][:, 0, 0:ncn].unsqueeze(2).to_broadcast([64, ncn, 64]), op=ALU.mult)), reads=[gf[b], lab[b]], writes=[PT0])
                    k.op(k.dve, (lambda e, b=b, n=n, P0=P0: e.tensor_tensor(out=P0[:, 0:n], in0=Bb[0:64, 0:n], in1=gfu[b][:, 0:n], op=ALU.mult)),
                         reads=[Bb, gfu[b]], writes=[P0])
                    k.op(k.pool, (lambda e, n=n, R0=R0, P0=P0: e.tensor_tensor(out=R0[:, 0:n], in0=idr[:, 0:n], in1=P0[:, 0:n], op=ALU.subtract)),
                         reads=[P0, idr], writes=[R0])
                    if GDCUT < 3:
                        return
                    for lv in range(6):
                        cur, nxt = lv % 2, (lv + 1) % 2
                        Pc, PTc = Pb[cur][b], PTb[cur][b]
                        Pn, PTn = Pb[nxt][b], PTb[nxt][b]
                        Rc, Rn = Rb[(lv + 1) % 2][b], Rb[lv % 2][b]
                        for c in range(ncn):
                            cs = slice(c * 64, (c + 1) * 64)
                            if lv >= 1:
                                k.op(k.pe, (lambda e, cs=cs, PTc=PTc, Rc=Rc: e.matmul(BD[0:64, cs], lhsT=PTc[:, cs], rhs=Rc[:, cs], start=True, stop=True)),
                                     reads=[PTc, Rc], writes=[BD])
                            if lv <= 4:
                                k.op(k.pe, (lambda e, cs=cs, PTc=PTc, Pc=Pc: e.matmul(BP[0:64, cs], lhsT=PTc[:, cs], rhs=Pc[:, cs], start=True, stop=True)),
                                     reads=[PTc, Pc], writes=[BP])
                                k.op(k.pe, (lambda e, cs=cs, PTc=PTc, Pc=Pc: e.matmul(BT[0:64, cs], lhsT=Pc[:, cs], rhs=PTc[:, cs], start=True, stop=True)),
                                     reads=[PTc, Pc], writes=[BT])
                        if lv >= 1:
                            k.op(k.dve, (lambda e, n=n, Rn=Rn, Rc=Rc: e.tensor_tensor(out=Rn[:, 0:n], in0=BD[0:64, 0:n], in1=Rc[:, 0:n], op=ALU.add)),
                                 reads=[BD, Rc], writes=[Rn])
                        if lv <= 4:
                            k.op(k.act, (lambda e, n=n, Pn=Pn: e.copy(Pn[:, 0:n], BP[0:64, 0:n])), reads=[BP], writes=[Pn])
                            k.op(k.act, (lambda e, n=n, PTn=PTn: e.copy(PTn[:, 0:n], BT[0:64, 0:n])), reads=[BT], writes=[PTn])
                    if GDCUT < 4:
                        return
                    Rf = Rb[1][b]
                    k.op(k.dve, (lambda e, b=b, ncn=ncn: e.tensor_tensor(out=rhu[b][:, 0:ncn, :], in0=vtm[b][:, 0:ncn, :],
                                                                        in1=lab[b][:, 0, 0:ncn].unsqueeze(2).to_broadcast([64, ncn, 128]), op=ALU.mult)),
                         reads=[vtm[b], lab[b]], writes=[rhu[b]])
                    k.op(k.pool, (lambda e, b=b, ncn=ncn: e.tensor_tensor(out=rhw[b][:, 0:ncn, :], in0=ktm[b][:, 0:ncn, :],
                                                                         in1=bek[b][:, 0:ncn].unsqueeze(2).to_broadcast([64, ncn, 128]), op=ALU.mult)),
                         reads=[ktm[b], bek[b]], writes=[rhw[b]])
                    k.op(k.pool, (lambda e, b=b, ncn=ncn: e.tensor_tensor(out=kdec[b][:, 0:ncn, :], in0=ktm[b][:, 0:ncn, :],
                                                                         in1=dcol[b][:, 0:ncn].unsqueeze(2).to_broadcast([64, ncn, 128]), op=ALU.mult)),
                         reads=[ktm[b], dcol[b]], writes=[kdec[b]])
                    for c in range(ncn):
                        cs = slice(c * 64, (c + 1) * 64)
                        pu = BD if c < 4 else BP
                        us = slice((c % 4) * 128, (c % 4 + 1) * 128)
                        k.op(k.pe, (lambda e, b=b, c=c, cs=cs, pu=pu, us=us, Rf=Rf: e.matmul(pu[0:64, us], lhsT=Rf[:, cs], rhs=rhu[b][:, c, :], start=True, stop=True)),
                             reads=[Rf, rhu[b]], writes=[pu])
                        k.op(k.pe, (lambda e, b=b, c=c, cs=cs, Rf=Rf: e.matmul(BT[:, cs], lhsT=rhw[b][:, c, :], rhs=Rf[:, cs], start=True, stop=True)),
                             reads=[Rf, rhw[b]], writes=[BT])
                        k.op(k.pe, (lambda e, b=b, cs=cs: e.matmul(Bq[0:64, cs], lhsT=kT[b][:, cs], rhs=qT[b][:, cs], start=True, stop=True)),
                             reads=[kT[b], qT[b]], writes=[Bq])
                    n4 = min(ncn, 4)
                    k.op(k.act, (lambda e, b=b, n4=n4: e.copy(u_sb[b][:, 0:n4, :], BD[0:64, 0:n4 * 128].rearrange("p (c d) -> p c d", d=128))),
                         reads=[BD], writes=[u_sb[b]])
                    if ncn > 4:
                        k.op(k.act, (lambda e, b=b, ncn=ncn: e.copy(u_sb[b][:, 4:ncn, :], BP[0:64, 0:(ncn - 4) * 128].rearrange("p (c d) -> p c d", d=128))),
                             reads=[BP], writes=[u_sb[b]])
                    k.op(k.dve, (lambda e, b=b, n=n: e.tensor_scalar(nwT[b][:, 0:n], BT[:, 0:n], -1.0, None, op0=ALU.mult)), reads=[BT], writes=[nwT[b]])
                    k.op(k.dve, (lambda e, b=b, n=n: e.tensor_tensor(out=qkm[b][:, 0:n], in0=Bq[0:64, 0:n], in1=fmi[b][:, 0:n], op=ALU.mult)),
                         reads=[Bq, fmi[b]], writes=[qkm[b]])
                    if GDCUT < 5:
                        return
                    k.mark()
                    for c in corder:
                        cs = slice(c * 64, (c + 1) * 64)
                        v_ = vn[b][c % 2]
                        k.op(k.pe, (lambda e, b=b, cs=cs, h=h: e.matmul(Bv[0:64, 256:384], lhsT=nwT[b][:, cs], rhs=Sb[h][:], start=True, stop=True)),
                             reads=[nwT[b], Sb[h]], writes=[Bv])
                        k.op(k.dve, (lambda e, b=b, c=c, v_=v_: e.tensor_tensor(out=v_[:], in0=Bv[0:64, 256:384], in1=u_sb[b][:, c, :], op=ALU.add)),
                             reads=[Bv, u_sb[b]], writes=[v_])
                        k.op(k.pe, (lambda e, b=b, cs=cs, h=h: e.matmul(Bo[:, cs], lhsT=Sb[h][:], rhs=qd[b][:, cs], start=True, stop=False)),
                             reads=[Sb[h], qd[b]], writes=[Bo])
                        k.op(k.pe, (lambda e, b=b, cs=cs, v_=v_: e.matmul(Bo[:, cs], lhsT=v_[:], rhs=qkm[b][:, cs], start=False, stop=True)),
                             reads=[v_, qkm[b]], writes=[Bo])
                        k.op(k.pe, (lambda e, b=b, c=c, v_=v_: e.matmul(Bs[:, 128:256], lhsT=kdec[b][:, c, :], rhs=v_[:], start=True, stop=True)),
                             reads=[kdec[b], v_], writes=[Bs])
                        k.op(k.dve, (lambda e, b=b, h=h, c=c: e.scalar_tensor_tensor(
                            out=S[h][:], in0=S[h][:], scalar=alast[b][:, c:c + 1], in1=Bs[:, 128:256], op0=ALU.mult, op1=ALU.add)),
                            reads=[S[h], alast[b], Bs], writes=[S[h]])
                        k.op(k.act, (lambda e, h=h: e.copy(Sb[h][:], S[h][:])), reads=[S[h]], writes=[Sb[h]])
                    if GDCUT < 6:
                        return
                    os_ = ost[it % 2]
                    if d == 1:
                        k.op(k.act, (lambda e, os_=os_, n=n: e.copy(os_[:, 0:n], Bo[:, 0:n])), reads=[Bo], writes=[os_])
                        k.dma(k.sp, self.obT[rows, t0:t0 + n], os_[:, 0:n], reads=[os_])
                    else:
                        sqo, rs = sqo_l[b], rs_l[b]
                        k.op(k.dve, (lambda e, os_=os_, b=b, n=n: e.tensor_tensor(out=os_[:, 0:n], in0=Bo[:, 0:n], in1=obt[b][:, 0:n], op=ALU.add)),
                             reads=[Bo, obt[b]], writes=[os_])
                        k.op(k.act, (lambda e, os_=os_, n=n: e.activation(sqo[:, 0:n], os_[:, 0:n], AF.Square)), reads=[os_], writes=[sqo])
                        k.op(k.pe, (lambda e, n=n: e.matmul(Bst[:, 0:n], lhsT=self.ones_bf[:], rhs=sqo[:, 0:n], start=True, stop=True)),
                             reads=[self.ones_bf, sqo], writes=[Bst])
                        k.op(k.act, (lambda e, n=n: e.activation(rs[:, 0:n], Bst[:, 0:n], AF.Ln, bias=self.eps_col[:, 0:1], scale=1.0 / 128)),
                             reads=[Bst, self.eps_col], writes=[rs])
                        k.op(k.act, (lambda e, n=n: e.activation(rs[:, 0:n], rs[:, 0:n], AF.Exp, scale=-0.5)), reads=[rs], writes=[rs])
                        k.op(k.dve, (lambda e, os_=os_, n=n: e.scalar_tensor_tensor(out=os_[:, 0:n], in0=os_[:, 0:n], scalar=onorm_col,
                                                                                   in1=rs[:, 0:n], op0=ALU.mult, op1=ALU.mult)),
                             reads=[os_, rs, self.asm], writes=[os_])
                        fo = fin[it % 2]
                        k.op(k.pool, (lambda e, os_=os_, fo=fo, b=b, n=n: e.tensor_tensor(out=fo[:, 0:n], in0=os_[:, 0:n], in1=gt[b][:, 0:n], op=ALU.mult)),
                             reads=[os_, gt[b]], writes=[fo])
                        k.dma(k.sp, self.oT[rows, t0:t0 + n], fo[:, 0:n], reads=[fo])
            uid = 0
            prev_scan = []
            for ti in order:
                t0, n = tiles[ti]
                ncn = n // 64
                corder = list(range(ncn)) if d == 0 else list(range(ncn))[::-1]
                for h in range(8):
                    b = uid % NB
                    uid += 1
                    k.marks = []
                    k.set_lane("u")
                    unit(h, b, uid, t0, n, ncn, corder)
                    k.set_lane(None)
                    recs = k.lanes.pop("u", [])
                    cut = k.marks[0] if k.marks else len(recs)
                    k.emit_interleaved(prev_scan, recs[:cut])
                    prev_scan = recs[cut:]
            k.emit_recs(prev_scan)
            if d == 1 and os.environ.get("DEBUG_DUMP"):
                o = self.nc.dram_tensor("dbg_S0", [128, 128], F32, kind="ExternalOutput").ap()
                k.dma(k.sp, o, S[0][:], reads=[S[0]])
                o = self.nc.dram_tensor("dbg_u", [64, 8 * 128], F32, kind="ExternalOutput").ap()
                k.dma(k.sp, o, u_sb[0][:].rearrange("p c d -> p (c d)"), reads=[u_sb[0]])
                o = self.nc.dram_tensor("dbg_rhu", [64, 8 * 128], BF16, kind="ExternalOutput").ap()
                k.dma(k.sp, o, rhu[0][:].rearrange("p c d -> p (c d)"), reads=[rhu[0]])
                o = self.nc.dram_tensor("dbg_R", [64, 512], BF16, kind="ExternalOutput").ap()
                k.dma(k.sp, o, Rb[0][0][:], reads=[Rb[0][0]])
                o = self.nc.dram_tensor("dbg_alast", [128, 8], F32, kind="ExternalOutput").ap()
                k.dma(k.sp, o, alast[0][:], reads=[alast[0]])
            k.end()

    def mixer_gdn(self, j, nrow):
        k = self.k
        k.begin()
        k.dma(k.sp, self.asm[:], self.a_small[j], writes=[self.asm])
        k.end()
        import os
        st = int(os.environ.get("GD_STAGE", "9"))
        self.phase_gd_proj(j, nrow)
        if st >= 2:
            self.phase_gd_conv()
        if st >= 3:
            self.phase_gd_core()
        if st >= 4:
            self.phase_out_proj(self.a_wo[j])

    def build(self):
        k = self.k
        self.eps_col = k.gsb([128, 1], F32, "epscol")
        k.begin()
        k.op(k.dve, lambda e: e.memset(self.eps_col[:], EPS), writes=[self.eps_col])
        self.bsm = k.gsb([128, 260], F32, "bsm")
        k.dma(k.sp, self.bsm[:], self.b_small, writes=[self.bsm])
        self.csm = k.gsb([128, 33], F32, "csm")
        k.dma(k.sp, self.csm[:], self.c_small, writes=[self.csm])
        self.tri = k.gsb([64, 256], F32, "tric")
        k.dma(k.sp, self.tri[:], self.tri_in, writes=[self.tri])
        self.tmc = k.gsb([128, (self.T + 127) // 128], F32, "tmc")
        k.dma(k.sp, self.tmc[:], self.tmask_col, writes=[self.tmc])
        self.one_col = k.gsb([128, 1], F32, "onecol")
        k.op(k.dve, lambda e: e.memset(self.one_col[:], 1.0), writes=[self.one_col])
        self.lb_col = k.gsb([128, 16], F32, "lbcol")
        self.asm = k.gsb([128, 153], F32, "asm")
        self.identb = k.gsb([128, 128], BF16, "identbc")
        k.dma(k.sp, self.identb[:], self.identb_in, writes=[self.identb])
        self.oml_row = k.gsb([128, D_MODEL], F32, "omlrow")
        k.end()
        self.phase_in()
        if self.only == "gdn":
            self.mixer_gdn(0, 0 * 3 + 1)
            self.phase_out(self.depth * 3)
            import os
            if os.environ.get("DEBUG_DUMP"):
                k.begin()
                dummy = k.sb([128, 4], F32, "dummy")
                for nm, ap, shp, dt in (("fmA", self.fmA, [D_MODEL, self.T], BF16), ("fmB0", self.fmB[0], [D_MODEL, self.T], BF16),
                                        ("tmK0", self.tmK[0], [self.T, D_MODEL], BF16), ("v_tm", self.v_tm, [self.T, D_MODEL], BF16),
                                        ("ba_tm", self.ba_tm, [self.T, 32], F32), ("obT", self.obT, [D_MODEL, self.T], F32),
                                        ("oT", self.oT, [D_MODEL, self.T], BF16), ("fmG", self.fmG, [D_MODEL, self.T], BF16)):
                    o = self.nc.dram_tensor("dbg_" + nm, shp, dt, kind="ExternalOutput").ap()
                    k.dma(k.sp, o, ap, reads=[dummy])
                k.end()
            return
        if self.only == "hgrn":
            self.mixer_hgrn(2, 2 * 3 + 1)
            self.phase_out(self.depth * 3)
            return
        if self.only == "attn":
            self.mixer_attn(1 * 3 + 1)
            self.phase_out(self.depth * 3)
            return
        for li in range(self.depth):
            if self.ffn:
                self.phase_ffn(li, 0, li * 3 + 0)
            if self.mixers and li % 3 == 1:
                self.mixer_attn(li * 3 + 1)
            if self.mixers and li % 3 == 2:
                self.mixer_hgrn(li, li * 3 + 1)
            if self.mixers and li % 3 == 0:
                self.mixer_gdn(li // 3, li * 3 + 1)
            if self.ffn:
                self.phase_ffn(li, 1, li * 3 + 2)
        self.phase_out(self.depth * 3)


def col_layout(a):
    a = np.asarray(a, np.float32)
    R = a.shape[0]
    C = a.shape[1] // 128
    return np.ascontiguousarray(a.reshape(R, C, 128).transpose(2, 0, 1).reshape(128, R * C))


def attn_host_inputs(b_w_in, b_lambda, b_sub_norm, layer_idx, T, n_valid):
    f32 = np.float32
    w = np.asarray(b_w_in, f32)
    perm = np.arange(2048)
    d = perm % 64
    perm = np.where(d < 8, perm + 8, np.where(d < 16, perm - 8, perm))
    w_sw = np.ascontiguousarray(w[:, perm])
    sm = np.zeros((128, 260), f32)
    sm[:, 0:256] = np.asarray(b_lambda, f32).reshape(1, 256)
    sm[:, 256] = np.asarray(b_sub_norm, f32).reshape(128)
    dd = np.arange(128) % 64
    ii = np.where(dd < 8, dd, dd - 8).astype(np.float64)
    invf = np.exp(-np.log(500000.0) * ii / 8.0)
    sm[:, 257] = np.where(dd < 16, invf, 0.0)
    sm[:, 258] = np.where(dd < 8, -1.0, np.where(dd < 16, 1.0, 0.0))
    sm[:, 259] = 0.8 - 0.6 * np.exp(-0.3 * layer_idx)
    nkt = (T + 127) // 128
    tok = np.arange(nkt * 128)
    kb = np.where(tok < n_valid, 0.0, -30000.0).astype(f32).reshape(nkt, 128).T
    return w_sw, sm, np.ascontiguousarray(kb)


def common_host_inputs(T, n_valid):
    f32 = np.float32
    j = np.arange(64)[:, None]
    i = np.arange(64)[None, :]
    tri = np.concatenate([(j <= i), (j > i), (j >= i), (j < i)], axis=1).astype(f32)
    tok = np.arange(T)
    tm = (tok < n_valid).astype(f32)
    tmask_rep = np.ascontiguousarray(np.broadcast_to(tm[None, :], (128, T)))
    nb = (T + 127) // 128
    tmp = np.zeros(nb * 128, f32)
    tmp[:T] = tm
    tmask_col = np.ascontiguousarray(tmp.reshape(nb, 128).T)
    return {"tri": tri, "tmask_rep": tmask_rep, "tmask_col": tmask_col, "ident": np.eye(128, dtype=f32)}


def hgrn_host_inputs(c_lb_logits, c_o_norm):
    sm = np.zeros((128, 33), np.float32)
    sm[:, 0:32] = col_layout(np.asarray(c_lb_logits, np.float32))
    sm[:, 32] = np.asarray(c_o_norm, np.float32).reshape(128)
    return sm


def gdn_host_inputs(a_log, a_dt_bias, a_o_norm, a_conv_w):
    f32 = np.float32
    n_a = np.asarray(a_log).shape[0]
    out = np.zeros((n_a, 128, 153), f32)
    for j in range(n_a):
        out[j, :, 0:16] = np.asarray(a_log[j], f32).reshape(1, 16)
        out[j, :, 16:32] = np.asarray(a_dt_bias[j], f32).reshape(1, 16)
        out[j, :, 32] = np.asarray(a_o_norm[j], f32).reshape(128)
        cw = np.asarray(a_conv_w[j], f32)
        out[j, :, 33:153] = cw.reshape(5, 24, 128).transpose(2, 1, 0).reshape(128, 120)
    return out


_PROG_CACHE = {}


def get_prog(T, depth, n_groups):
    key = (T, depth, n_groups)
    if key not in _PROG_CACHE:
        _PROG_CACHE[key] = Prog(T, depth, n_groups)
    return _PROG_CACHE[key]


T_FULL = 8256
SEQ_P = 8192
SEQ_S = 4096


def kernel(x_prompt, x_sample, meta_tokens, norm_w, ffn_w_up, ffn_w_down,
           a_w_in, a_conv_w, a_log, a_dt_bias, a_o_norm, a_w_out,
           b_w_in, b_lambda, b_sub_norm, b_w_out,
           c_w_in, c_lb_logits, c_o_norm, c_w_out, final_norm):
    import ml_dtypes
    depth = norm_w.shape[0]
    T = T_FULL
    prog = get_prog(T, depth, 4)
    f32 = np.float32
    seqs = [x_prompt[0], x_prompt[1], x_sample[0], x_sample[1], x_sample[2], x_sample[3], x_sample[0], x_sample[1]]
    nw = np.concatenate([np.asarray(norm_w, f32).reshape(depth * 3, D_MODEL), np.asarray(final_norm, f32)[None]], 0)
    nw = col_layout(nw)
    shared = {
        "norm_w": nw,
        "ffn_w_up": np.asarray(ffn_w_up, f32), "ffn_w_down": np.asarray(ffn_w_down, f32),
        "b_w_in": np.asarray(b_w_in[0], f32), "b_w_out": np.asarray(b_w_out[0], f32),
        "c_w_in": np.asarray(c_w_in[0], f32), "c_w_out": np.asarray(c_w_out[0], f32),
        "c_small": hgrn_host_inputs(c_lb_logits, c_o_norm[0]),
        "a_w_in": np.asarray(a_w_in, f32), "a_w_out": np.asarray(a_w_out, f32),
        "a_small": gdn_host_inputs(a_log, a_dt_bias, a_o_norm, a_conv_w),
        "identb": np.eye(128, dtype=f32).astype(ml_dtypes.bfloat16),
    }
    in_maps = []
    per_len = {}
    for s in seqs:
        L = s.shape[0]
        nv = N_META + L
        if L not in per_len:
            w_sw, sm, kb = attn_host_inputs(b_w_in[0], b_lambda[0], b_sub_norm[0], 1, T, nv)
            d = {"b_w_sw": w_sw, "b_small": sm, "kbias": kb}
            d.update(common_host_inputs(T, nv))
            per_len[L] = d
        xin = np.zeros((T, D_MODEL), f32)
        xin[:N_META] = meta_tokens
        xin[N_META:nv] = s
        m = {"xin": xin}
        m.update(shared)
        m.update(per_len[L])
        in_maps.append(m)
    res = run_bass_kernel_spmd(prog.nc, in_maps, core_ids=list(range(8)))
    outs = [r["yout"] for r in res.results]
    y_prompt = np.stack([outs[0][N_META:N_META + SEQ_P], outs[1][N_META:N_META + SEQ_P]], 0).astype(f32)
    y_sample = np.stack([outs[i][N_META:N_META + SEQ_S] for i in range(2, 6)], 0).astype(f32)
    return (y_prompt, y_sample)
```

```python
import numpy as np
from contextlib import ExitStack
import concourse.bass as bass
import concourse.mybir as mybir
from concourse.bass_utils import run_bass_kernel_spmd

F32 = mybir.dt.float32
BF16 = mybir.dt.bfloat16
ALU = mybir.AluOpType
AF = mybir.ActivationFunctionType
AX = mybir.AxisListType

D_MODEL = 1024
D_FF = 2816
N_META = 16
EPS = 1e-6


class Eng:
    def __init__(self, name, sem):
        self.name = name
        self.sem = sem
        self.cnt = 0
        self.seen = {}
        self.ops = []


class Tl:
    def __init__(self, t, name):
        self.t = t
        self.name = name
        self.w = None
        self.r = {}
        self.dsem = None

    def __getitem__(self, idx):
        return self.t[idx]

    def view(self):
        return Tl(self.t, self.name)


class KB:
    SAME_ENGINE_SYNC = True

    def __init__(self, nc, n_dma_sems=84):
        self.nc = nc
        self.es = ExitStack()
        self.engs = {}
        for name in ("pe", "act", "dve", "pool", "sp"):
            sem = self.es.enter_context(nc.semaphore("s_" + name))
            self.engs[name] = Eng(name, sem)
        self.pe, self.act, self.dve, self.pool, self.sp = (self.engs[n] for n in ("pe", "act", "dve", "pool", "sp"))
        self.dma_pool = []
        for i in range(n_dma_sems):
            sem = self.es.enter_context(nc.semaphore("s_d%d" % i))
            self.dma_pool.append([sem, 0])
        self.dma_free = list(range(n_dma_sems))
        self.phase_tiles = []
        self.dma_tiles = []
        self.lanes = {}
        self.marks = []
        self.cur_lane = None
        self.pes = None
        self.uid = 0
        self.nops = 0

    def begin(self):
        self.pes = ExitStack()
        self.phase_tiles = []

    def sb(self, shape, dt, name=None):
        self.uid += 1
        name = "%s_%d" % (name or "t", self.uid)
        t = self.pes.enter_context(self.nc.sbuf_tensor(name, list(shape), dt))
        tl = Tl(t, name)
        self.phase_tiles.append(tl)
        return tl

    def views(self, tl, n):
        vs = [tl.view() for _ in range(n)]
        self.phase_tiles.extend(vs)
        return vs

    def ps(self, shape, dt=F32, name=None):
        self.uid += 1
        name = "%s_%d" % (name or "p", self.uid)
        t = self.pes.enter_context(self.nc.psum_tensor(name, list(shape), dt))
        tl = Tl(t, name)
        self.phase_tiles.append(tl)
        return tl

    def gsb(self, shape, dt, name):
        t = self.es.enter_context(self.nc.sbuf_tensor(name, list(shape), dt))
        return Tl(t, name)

    def _deps(self, eng, reads, writes):
        deps = []
        for tl in reads:
            if tl.w is not None:
                deps.append(tl.w)
        for tl in writes:
            if tl.w is not None and tl.w[0] != eng.name:
                deps.append(tl.w)
            deps.extend(v for v in tl.r.values() if v[0] != eng.name)
        waits = []
        for key, sem, cnt in deps:
            if key == eng.name and (eng.name in ("pe", "sp") or not self.SAME_ENGINE_SYNC):
                continue
            if eng.seen.get(key, 0) < cnt:
                eng.seen[key] = cnt
                waits.append((sem, cnt))
        return waits

    def set_lane(self, lane):
        self.cur_lane = lane

    def merge(self):
        lanes = [v for _, v in sorted(self.lanes.items()) if v]
        self.lanes = {}
        save, self.cur_lane = self.cur_lane, None
        idx = [0] * len(lanes)
        left = sum(len(l) for l in lanes)
        while left:
            for li, l in enumerate(lanes):
                if idx[li] < len(l):
                    rec = l[idx[li]]
                    idx[li] += 1
                    left -= 1
                    if rec[0] == "op":
                        self.op(*rec[1:])
                    else:
                        self.dma(rec[1], rec[2], rec[3], rec[4], rec[5], **rec[6])
        self.cur_lane = save

    def mark(self):
        self.marks.append(len(self.lanes.get(self.cur_lane, [])))

    def emit_recs(self, recs):
        for rec in recs:
            if rec[0] == "op":
                self.op(*rec[1:])
            else:
                self.dma(rec[1], rec[2], rec[3], rec[4], rec[5], **rec[6])

    def emit_interleaved(self, a, b):
        save, self.cur_lane = self.cur_lane, None
        na, nb = len(a), len(b)
        ia = ib = 0
        while ia < na or ib < nb:
            if ib >= nb or (ia < na and ia * nb <= ib * na):
                self.emit_recs([a[ia]])
                ia += 1
            else:
                self.emit_recs([b[ib]])
                ib += 1
        self.cur_lane = save

    def op(self, eng, fn, reads=(), writes=()):
        if self.cur_lane is not None:
            self.lanes.setdefault(self.cur_lane, []).append(("op", eng, fn, tuple(reads), tuple(writes)))
            return
        waits = self._deps(eng, reads, writes)
        eng.cnt += 1
        me = (eng.name, eng.sem, eng.cnt)
        eng.ops.append((waits, fn, (eng.sem, 1)))
        for tl in writes:
            tl.w = me
            tl.r = {}
        for tl in reads:
            if tl not in writes:
                tl.r[eng.name] = me
        self.nops += 1

    def dma(self, q, out, in_, reads=(), writes=(), **kw):
        if self.cur_lane is not None:
            self.lanes.setdefault(self.cur_lane, []).append(("dma", q, out, in_, tuple(reads), tuple(writes), kw))
            return
        tl = (list(writes) + list(reads))[0]
        if tl.dsem is None:
            tl.dsem = self.dma_free.pop()
            self.dma_tiles.append(tl)
        slot = self.dma_pool[tl.dsem]
        waits = self._deps(q, reads, writes)
        slot[1] += 16
        key = "d%d" % tl.dsem
        me = (key, slot[0], slot[1])
        q.ops.append((waits, (lambda e, o=out, i=in_, k=kw: e.dma_start(out=o, in_=i, **k)), (slot[0], 16)))
        for t in writes:
            t.w = me
            t.r = {}
        for t in reads:
            t.r[key] = me
        self.nops += 1

    def end(self):
        used = set()
        for tl in self.dma_tiles:
            if tl.dsem is not None:
                used.add(tl.dsem)
        for d in sorted(used):
            sem, cnt = self.dma_pool[d]
            key = "d%d" % d
            if self.sp.seen.get(key, 0) < cnt:
                self.sp.seen[key] = cnt
                self.sp.ops.append(([(sem, cnt)], None, None))
        for e in (self.pe, self.act, self.dve, self.pool):
            if self.sp.seen.get(e.name, 0) < e.cnt:
                self.sp.seen[e.name] = e.cnt
                self.sp.ops.append(([(e.sem, e.cnt)], None, None))
        self.sp.cnt += 1
        self.sp.ops.append(([], (lambda e: e.nop()), (self.sp.sem, 1)))
        for e in (self.pe, self.act, self.dve, self.pool):
            e.ops.append(([(self.sp.sem, self.sp.cnt)], None, None))
        with self.nc.Block() as block:
            for name, deco in (("pe", block.tensor), ("act", block.scalar), ("dve", block.vector),
                               ("pool", block.gpsimd), ("sp", block.sync)):
                eng = self.engs[name]
                ops = eng.ops
                eng.ops = []

                def body(e, ops=ops):
                    for waits, fn, inc in ops:
                        for sem, cnt in waits:
                            e.wait_ge(sem, cnt)
                        if fn is not None:
                            ins = fn(e)
                            if inc is not None:
                                ins.then_inc(inc[0], inc[1])

                deco(body)
        for d in used:
            self.dma_free.append(d)
        for tl in self.dma_tiles:
            tl.dsem = None
        self.dma_tiles = []
        for e in self.engs.values():
            for o in self.engs.values():
                e.seen[o.name] = o.cnt
            for d in range(len(self.dma_pool)):
                e.seen["d%d" % d] = self.dma_pool[d][1]
        self.pes.close()
        self.pes = None

    def close(self):
        self.es.close()


def tiles_of(n, step=512):
    out = []
    s = 0
    while s < n:
        out.append((s, min(step, n - s)))
        s += step
    return out


class Prog:
    def __init__(self, T, depth, n_groups, mixers=True, ffn=True, only=None):
        self.only = only
        self.mixers = mixers
        self.ffn = ffn
        self.T = T
        self.depth = depth
        self.n_groups = n_groups
        base = (T // n_groups) // 512 * 512 if n_groups > 1 else T
        self.groups = [(g * base, base) for g in range(n_groups - 1)]
        self.groups.append(((n_groups - 1) * base, T - (n_groups - 1) * base))
        nc = bass.Bass("TRN2", target_bir_lowering=False)
        self.nc = nc
        d = nc.dram_tensor
        self.xin = d("xin", [T, D_MODEL], F32, kind="ExternalInput").ap()
        self.yout = d("yout", [T, D_MODEL], F32, kind="ExternalOutput").ap()
        self.norm_w = d("norm_w", [128, (depth * 3 + 1) * 8], F32, kind="ExternalInput").ap()
        self.w_up = d("ffn_w_up", [depth, 2, D_MODEL, 2 * D_FF], F32, kind="ExternalInput").ap()
        self.w_dn = d("ffn_w_down", [depth, 2, D_FF, D_MODEL], F32, kind="ExternalInput").ap()
        self.ident_in = d("ident", [128, 128], F32, kind="ExternalInput").ap()
        self.hT = d("hT", [D_MODEL, T], F32).ap()
        self.b_w = d("b_w_in", [D_MODEL, 3072], F32, kind="ExternalInput").ap()
        self.b_wsw = d("b_w_sw", [D_MODEL, 2048], F32, kind="ExternalInput").ap()
        self.b_wo = d("b_w_out", [D_MODEL, D_MODEL], F32, kind="ExternalInput").ap()
        self.b_small = d("b_small", [128, 256 + 4], F32, kind="ExternalInput").ap()
        self.kbias_in = d("kbias", [128, (T + 127) // 128], F32, kind="ExternalInput").ap()
        self.qkT = d("qkT", [2048, T], BF16).ap()
        self.v_tm = d("v_tm", [T, 1024], BF16).ap()
        self.oT = d("oT", [D_MODEL, T], BF16).ap()
        self.tri_in = d("tri", [64, 4 * 64], F32, kind="ExternalInput").ap()
        self.tmask_rep = d("tmask_rep", [128, T], F32, kind="ExternalInput").ap()
        self.tmask_col = d("tmask_col", [128, (T + 127) // 128], F32, kind="ExternalInput").ap()
        self.c_w = d("c_w_in", [D_MODEL, 5120], F32, kind="ExternalInput").ap()
        self.c_wo = d("c_w_out", [D_MODEL, D_MODEL], F32, kind="ExternalInput").ap()
        self.c_small = d("c_small", [128, 32 + 1], F32, kind="ExternalInput").ap()
        self.fmA = d("fmA", [D_MODEL, T], BF16).ap()
        self.fmB = [d("fmB%d" % i, [D_MODEL, T], BF16).ap() for i in range(2)]
        self.fmG = d("fmG", [D_MODEL, T], BF16).ap()
        self.tmK = [d("tmK%d" % i, [T, D_MODEL], BF16).ap() for i in range(2)]
        self.tmL = [d("tmL%d" % i, [T, D_MODEL], F32).ap() for i in range(2)]
        self.obT = d("obT", [D_MODEL, T], F32).ap()
        self.lbrow_t = d("lbrow", [2, D_MODEL], F32)
        self.n_a = (depth + 2) // 3
        self.a_w = d("a_w_in", [self.n_a, D_MODEL, 4128], F32, kind="ExternalInput").ap()
        self.a_wo = d("a_w_out", [self.n_a, D_MODEL, D_MODEL], F32, kind="ExternalInput").ap()
        self.a_small = d("a_small", [self.n_a, 128, 32 + 1 + 120], F32, kind="ExternalInput").ap()
        self.pre = d("pre", [3 * D_MODEL, T], BF16).ap()
        self.ba_tm = d("ba_tm", [T, 32], F32).ap()
        self.identb_in = d("identb", [128, 128], BF16, kind="ExternalInput").ap()
        self.k = KB(nc)
        k = self.k
        self.ident = k.gsb([128, 128], F32, "identc")
        self.ones_bf = k.gsb([128, 128], BF16, "onesbf")
        self.nw = k.gsb([128, (depth * 3 + 1) * 8], F32, "nwcol")
        self.build()
        k.close()

    def phase_in(self):
        k, nc, T = self.k, self.nc, self.T
        k.begin()
        k.dma(k.sp, self.ident[:], self.ident_in, writes=[self.ident])
        k.op(k.dve, lambda e: e.memset(self.ones_bf[:], 1.0), writes=[self.ones_bf])
        nrows = self.depth * 3 + 1
        k.dma(k.sp, self.nw[:], self.norm_w, writes=[self.nw])
        xt = [k.sb([128, 4, D_MODEL], F32, "xt") for _ in range(2)]
        st = [k.sb([128, 8, 512], F32, "st") for _ in range(2)]
        pst = [k.ps([128, 512], F32, "pst") for _ in range(4)]
        pi = 0
        for gi, (t0, n) in enumerate(tiles_of(T, 512)):
            x = xt[gi % 2]
            s = st[gi % 2]
            nb = (n + 127) // 128
            blocks = [(b * 128, min(128, n - b * 128)) for b in range(nb)]
            if n % 128 == 0:
                k.dma(k.sp, x[:, 0:nb, :], self.xin[t0:t0 + n, :].rearrange("(j p) f -> p j f", p=128), writes=[x])
            else:
                for b, (o, r) in enumerate(blocks):
                    k.dma(k.sp, x[0:r, b, :], self.xin[t0 + o:t0 + o + r, :], writes=[x])
            for c in range(8):
                p = pst[pi % 4]
                pi += 1
                for b, (o, r) in enumerate(blocks):
                    k.op(k.pe, (lambda e, p=p, x=x, b=b, o=o, r=r, c=c: e.transpose(
                        p[:, o:o + r], x[0:r, b, c * 128:(c + 1) * 128], self.ident[0:r, 0:r])),
                        reads=[x, self.ident], writes=[p])
                eng = k.dve if c % 2 == 0 else k.act
                if eng is k.dve:
                    k.op(eng, (lambda e, p=p, s=s, c=c, n=n: e.tensor_copy(s[:, c, 0:n], p[:, 0:n])), reads=[p], writes=[s])
                else:
                    k.op(eng, (lambda e, p=p, s=s, c=c, n=n: e.copy(s[:, c, 0:n], p[:, 0:n])), reads=[p], writes=[s])
            k.dma(k.sp, self.hT.rearrange("(c p) t -> p c t", p=128)[:, :, t0:t0 + n], s[:, :, 0:n], reads=[s])
        k.end()


    def emit_norm(self, y, yv, hn, hnv, tl, nrow, pd, sq, rstd):
        k = self.k
        for ti, (o, n) in enumerate(tl):
            ss = pd[ti % 4]
            for c in range(8):
                s_ = sq[c % 2]
                k.op(k.act, (lambda e, s_=s_, c=c, o=o, n=n: e.activation(s_[:, 0:n], y[:, c, o:o + n], AF.Square)),
                     reads=[yv[c][ti]], writes=[s_])
                k.op(k.pe, (lambda e, ss=ss, s_=s_, c=c, n=n: e.matmul(ss[:, 0:n], lhsT=self.ones_bf[:], rhs=s_[:, 0:n],
                                                                        start=(c == 0), stop=(c == 7))),
                     reads=[s_, self.ones_bf], writes=[ss])
            r = rstd[ti % 2]
            k.op(k.act, (lambda e, r=r, ss=ss, n=n: e.activation(r[:, 0:n], ss[:, 0:n], AF.Ln, bias=self.eps_col[:, 0:1],
                                                                  scale=1.0 / D_MODEL)),
                 reads=[ss, self.eps_col], writes=[r])
            k.op(k.act, (lambda e, r=r, n=n: e.activation(r[:, 0:n], r[:, 0:n], AF.Exp, scale=-0.5)), reads=[r], writes=[r])
            for c in range(8):
                col = nrow * 8 + c
                k.op(k.dve, (lambda e, r=r, c=c, o=o, n=n, col=col: e.scalar_tensor_tensor(
                    out=hn[:, c, o:o + n], in0=y[:, c, o:o + n], scalar=self.nw[:, col:col + 1], in1=r[:, 0:n],
                    op0=ALU.mult, op1=ALU.mult)),
                    reads=[yv[c][ti], r, self.nw], writes=[hnv[c][ti]])

    def phase_ffn(self, li, fj, nrow):
        k, nc = self.k, self.nc
        HG = 256
        NG = D_FF // HG
        hTv = self.hT.rearrange("(c p) t -> p c t", p=128)
        for (T0, GS) in self.groups:
            tl = tiles_of(GS, 512)
            NT = len(tl)
            k.begin()
            y = k.sb([128, 8, GS], F32, "y")
            hn = k.sb([128, 8, GS], BF16, "hn")
            yv = [k.views(y, NT) for _ in range(8)]
            hnv = [k.views(hn, NT) for _ in range(8)]
            sq = [k.sb([128, 512], BF16, "sq") for _ in range(2)]
            rstd = [k.sb([128, 512], F32, "rstd") for _ in range(2)]
            wg = [k.sb([128, 8, HG], BF16, "wg") for _ in range(2)]
            wu = [k.sb([128, 8, HG], BF16, "wu") for _ in range(2)]
            wd = [k.sb([128, HG // 128, D_MODEL], BF16, "wd") for _ in range(2)]
            sg = [k.sb([128, 2, 512], F32, "sg") for _ in range(2)]
            act = [k.sb([128, 2, 512], BF16, "act") for _ in range(2)]
            pg = [k.ps([128, 512], F32, "pg") for _ in range(2)]
            pu = [k.ps([128, 512], F32, "pu") for _ in range(2)]
            pd = [k.ps([128, 512], F32, "pd") for _ in range(4)]
            allv = [v for c in range(8) for v in yv[c]]
            k.dma(k.sp, y[:], hTv[:, :, T0:T0 + GS], writes=allv)
            self.emit_norm(y, yv, hn, hnv, tl, nrow, pd, sq, rstd)
            wup = self.w_up[li, fj]
            wdn = self.w_dn[li, fj]
            steps = []
            for g in range(NG):
                for ti, (o, n) in enumerate(tl):
                    steps.append((g, g % 2, ti, o, n, len(steps) % 2))
            loaded = set()

            def load_w(g, b):
                if g in loaded:
                    return
                loaded.add(g)
                k.dma(k.pool, wg[b][:], wup[:, g * HG:(g + 1) * HG].rearrange("(kk p) c -> p kk c", p=128), writes=[wg[b]])
                k.dma(k.pool, wu[b][:], wup[:, D_FF + g * HG:D_FF + (g + 1) * HG].rearrange("(kk p) c -> p kk c", p=128),
                      writes=[wu[b]])
                k.dma(k.pool, wd[b][:], wdn[g * HG:(g + 1) * HG, :].rearrange("(kk p) c -> p kk c", p=128), writes=[wd[b]])

            def emit_gu(step, j):
                g, b, ti, o, n, ab = step
                load_w(g, b)
                for (pt, wt) in ((pg[j], wg[b]), (pu[j], wu[b])):
                    for c in range(8):
                        k.op(k.pe, (lambda e, pt=pt, wt=wt, j=j, c=c, o=o, n=n: e.matmul(
                            pt[:, 0:n], lhsT=wt[:, c, j * 128:(j + 1) * 128], rhs=hn[:, c, o:o + n],
                            start=(c == 0), stop=(c == 7))),
                            reads=[wt, hnv[c][ti]], writes=[pt])
                k.op(k.act, (lambda e, j=j, ab=ab, n=n: e.activation(sg[ab][:, j, 0:n], pg[j][:, 0:n], AF.Silu)),
                     reads=[pg[j]], writes=[sgv[ab][j]])
                k.op(k.dve, (lambda e, j=j, ab=ab, n=n: e.scalar_tensor_tensor(
                    out=act[ab][:, j, 0:n], in0=pu[j][:, 0:n], scalar=0.5, in1=sg[ab][:, j, 0:n],
                    op0=ALU.mult, op1=ALU.mult)),
                    reads=[pu[j], sgv[ab][j]], writes=[actv[ab][j]])

            def emit_down(step):
                g, b, ti, o, n, ab = step
                for m in range(8):
                    pdt = pd[m % 4]
                    for j in range(2):
                        k.op(k.pe, (lambda e, pdt=pdt, b=b, j=j, m=m, ab=ab, n=n: e.matmul(
                            pdt[:, 0:n], lhsT=wd[b][:, j, m * 128:(m + 1) * 128], rhs=act[ab][:, j, 0:n],
                            start=(j == 0), stop=(j == 1))),
                            reads=[wd[b], actv[ab][j]], writes=[pdt])
                    k.op(k.dve, (lambda e, pdt=pdt, m=m, o=o, n=n: e.tensor_tensor(
                        out=y[:, m, o:o + n], in0=y[:, m, o:o + n], in1=pdt[:, 0:n], op=ALU.add)),
                        reads=[pdt, yv[m][ti]], writes=[yv[m][ti]])

            sgv = [k.views(sg[i], 2) for i in range(2)]
            actv = [k.views(act[i], 2) for i in range(2)]
            emit_gu(steps[0], 0)
            emit_gu(steps[0], 1)
            for i, st in enumerate(steps):
                if i + 1 < len(steps):
                    emit_gu(steps[i + 1], 0)
                emit_down(st)
                if i + 1 < len(steps):
                    emit_gu(steps[i + 1], 1)
            k.dma(k.sp, hTv[:, :, T0:T0 + GS], y[:], reads=allv)
            k.end()

    def phase_out(self, nrow):
        k, nc, T = self.k, self.nc, self.T
        hTv = self.hT.rearrange("(c p) t -> p c t", p=128)
        k.begin()
        hb = [k.sb([128, 8, 512], F32, "hb") for _ in range(2)]
        sq = [k.sb([128, 512], BF16, "sq") for _ in range(2)]
        rstd = [k.sb([128, 512], F32, "rstd") for _ in range(2)]
        yn = [k.sb([128, 8, 512], F32, "yn") for _ in range(2)]
        ot = [k.sb([128, 4, D_MODEL], F32, "ot") for _ in range(2)]
        pss = k.ps([128, 512], F32, "pss")
        pt = [k.ps([128, 512], F32, "pt") for _ in range(4)]
        pi = 0
        for gi, (t0, n) in enumerate(tiles_of(T, 512)):
            h = hb[gi % 2]
            r = rstd[gi % 2]
            yy = yn[gi % 2]
            o_ = ot[gi % 2]
            k.dma(k.sp, h[:, :, 0:n], hTv[:, :, t0:t0 + n], writes=[h])
            for c in range(8):
                s_ = sq[c % 2]
                k.op(k.act, (lambda e, s_=s_, h=h, c=c, n=n: e.activation(s_[:, 0:n], h[:, c, 0:n], AF.Square)),
                     reads=[h], writes=[s_])
                k.op(k.pe, (lambda e, s_=s_, c=c, n=n: e.matmul(pss[:, 0:n], lhsT=self.ones_bf[:], rhs=s_[:, 0:n],
                                                                 start=(c == 0), stop=(c == 7))),
                     reads=[s_, self.ones_bf], writes=[pss])
            k.op(k.act, (lambda e, r=r, n=n: e.activation(r[:, 0:n], pss[:, 0:n], AF.Ln, bias=self.eps_col[:, 0:1],
                                                           scale=1.0 / D_MODEL)),
                 reads=[pss, self.eps_col], writes=[r])
            k.op(k.act, (lambda e, r=r, n=n: e.activation(r[:, 0:n], r[:, 0:n], AF.Exp, scale=-0.5)), reads=[r], writes=[r])
            for c in range(8):
                col = nrow * 8 + c
                k.op(k.dve, (lambda e, r=r, h=h, yy=yy, c=c, n=n, col=col: e.scalar_tensor_tensor(
                    out=yy[:, c, 0:n], in0=h[:, c, 0:n], scalar=self.nw[:, col:col + 1], in1=r[:, 0:n],
                    op0=ALU.mult, op1=ALU.mult)),
                    reads=[h, r, self.nw], writes=[yy])
            nb = (n + 127) // 128
            blocks = [(b * 128, min(128, n - b * 128)) for b in range(nb)]
            for b, (o, rr) in enumerate(blocks):
                for half in range(2):
                    p = pt[pi % 4]
                    pi += 1
                    for cc in range(4):
                        c = half * 4 + cc
                        k.op(k.pe, (lambda e, p=p, yy=yy, c=c, cc=cc, o=o, rr=rr: e.transpose(
                            p[0:rr, cc * 128:(cc + 1) * 128], yy[:, c, o:o + rr], self.ident[:])),
                            reads=[yy, self.ident], writes=[p])
                    if half == 0:
                        k.op(k.dve, (lambda e, p=p, o_=o_, b=b, rr=rr: e.tensor_copy(o_[0:rr, b, 0:512], p[0:rr, :])),
                             reads=[p], writes=[o_])
                    else:
                        k.op(k.act, (lambda e, p=p, o_=o_, b=b, rr=rr: e.copy(o_[0:rr, b, 512:1024], p[0:rr, :])),
                             reads=[p], writes=[o_])
            if n % 128 == 0:
                k.dma(k.sp, self.yout[t0:t0 + n, :].rearrange("(j p) f -> p j f", p=128), o_[:, 0:nb, :], reads=[o_])
            else:
                for b, (o, rr) in enumerate(blocks):
                    k.dma(k.sp, self.yout[t0 + o:t0 + o + rr, :], o_[0:rr, b, :], reads=[o_])
        k.end()


    def emit_rope_tables(self, T0, tl, cosF, sinF, tmp):
        k = self.k
        PI = float(np.pi)
        invf = self.bsm[:, 257:258]
        sign = self.bsm[:, 258:259]
        for ti, (o, n) in enumerate(tl):
            pos, ang, ki, kf, tf = tmp
            k.op(k.pool, (lambda e, pos=pos, n=n, b=T0 + o: e.iota(pos[:, 0:n], [[1, n]], base=b, channel_multiplier=0,
                                                                   allow_small_or_imprecise_dtypes=True)), writes=[pos])
            k.op(k.dve, (lambda e, n=n: e.tensor_scalar(ang[:, 0:n], pos[:, 0:n], invf, None, op0=ALU.mult)),
                 reads=[pos, self.bsm], writes=[ang])
            def reduce(src, dst, shift, n=n):
                k.op(k.dve, (lambda e: e.tensor_scalar(dst[:, 0:n], src[:, 0:n], shift, None, op0=ALU.add)),
                     reads=[src], writes=[dst])
                k.op(k.dve, (lambda e: e.tensor_scalar(ki[:, 0:n], dst[:, 0:n], 1.0 / (2 * PI), None, op0=ALU.mult)),
                     reads=[dst], writes=[ki])
                k.op(k.dve, (lambda e: e.tensor_copy(tf[:, 0:n], ki[:, 0:n])), reads=[ki], writes=[tf])
                k.op(k.dve, (lambda e: e.scalar_tensor_tensor(out=dst[:, 0:n], in0=tf[:, 0:n], scalar=-2 * PI, in1=dst[:, 0:n],
                                                              op0=ALU.mult, op1=ALU.add)), reads=[tf, dst], writes=[dst])
                k.op(k.dve, (lambda e: e.tensor_scalar(tf[:, 0:n], dst[:, 0:n], PI, -2 * PI, op0=ALU.is_gt, op1=ALU.mult)),
                     reads=[dst], writes=[tf])
                k.op(k.dve, (lambda e: e.tensor_tensor(out=dst[:, 0:n], in0=dst[:, 0:n], in1=tf[:, 0:n], op=ALU.add)),
                     reads=[dst, tf], writes=[dst])
                k.op(k.dve, (lambda e: e.tensor_scalar(tf[:, 0:n], dst[:, 0:n], -PI, 2 * PI, op0=ALU.is_lt, op1=ALU.mult)),
                     reads=[dst], writes=[tf])
                k.op(k.dve, (lambda e: e.tensor_tensor(out=dst[:, 0:n], in0=dst[:, 0:n], in1=tf[:, 0:n], op=ALU.add)),
                     reads=[dst, tf], writes=[dst])
                k.op(k.dve, (lambda e: e.tensor_scalar(dst[:, 0:n], dst[:, 0:n], -PI, PI, op0=ALU.max, op1=ALU.min)),
                     reads=[dst], writes=[dst])
            reduce(ang, kf, 0.0)
            reduce(ang, pos, PI / 2)
            s_, c_ = sinF[ti], cosF[ti]
            k.op(k.act, (lambda e, s_=s_, n=n: e.activation(s_[:, 0:n], kf[:, 0:n], AF.Sin)), reads=[kf], writes=[s_])
            k.op(k.act, (lambda e, c_=c_, n=n: e.activation(c_[:, 0:n], pos[:, 0:n], AF.Sin)), reads=[pos], writes=[c_])
            k.op(k.dve, (lambda e, s_=s_, n=n: e.tensor_scalar(s_[:, 0:n], s_[:, 0:n], sign, None, op0=ALU.mult)),
                 reads=[s_, self.bsm], writes=[s_])

    def phase_attn_proj(self, nrow):
        k, nc = self.k, self.nc
        hTv = self.hT.rearrange("(c p) t -> p c t", p=128)
        for (T0, GS) in self.groups:
            tl = tiles_of(GS, 512)
            NT = len(tl)
            k.begin()
            y = k.sb([128, 8, GS], F32, "y")
            hn = k.sb([128, 8, GS], BF16, "hn")
            yv = [k.views(y, NT) for _ in range(8)]
            hnv = [k.views(hn, NT) for _ in range(8)]
            sq = [k.sb([128, 512], BF16, "sq") for _ in range(2)]
            rstd = [k.sb([128, 512], F32, "rstd") for _ in range(2)]
            pd = [k.ps([128, 512], F32, "pd") for _ in range(4)]
            pa = [k.ps([128, 512], F32, "pa") for _ in range(2)]
            pb = [k.ps([128, 512], F32, "pb") for _ in range(2)]
            allv = [v for c in range(8) for v in yv[c]]
            k.dma(k.sp, y[:], hTv[:, :, T0:T0 + GS], writes=allv)
            self.emit_norm(y, yv, hn, hnv, tl, nrow, pd, sq, rstd)
            cosF = [k.sb([128, 512], F32, "cosF") for _ in range(NT)]
            sinF = [k.sb([128, 512], F32, "sinF") for _ in range(NT)]
            tmp = (k.sb([128, 512], F32, "pos"), k.sb([128, 512], F32, "ang"),
                   k.sb([128, 512], mybir.dt.int32, "ki"), k.sb([128, 512], F32, "kf"), k.sb([128, 512], F32, "tf"))
            self.emit_rope_tables(T0, tl, cosF, sinF, tmp)
            wa = [k.sb([128, 8, 128], BF16, "wa") for _ in range(2)]
            wb = [k.sb([128, 8, 128], BF16, "wb") for _ in range(2)]
            t1 = [k.sb([128, 512], F32, "t1") for _ in range(2)]
            t2 = [k.sb([128, 512], F32, "t2") for _ in range(2)]
            stg = [k.sb([128, 512], BF16, "stg") for _ in range(3)]
            it = 0
            for m in range(16):
                b = m % 2
                k.dma(k.pool, wa[b][:], self.b_w[:, m * 128:(m + 1) * 128].rearrange("(kk p) c -> p kk c", p=128), writes=[wa[b]])
                k.dma(k.pool, wb[b][:], self.b_wsw[:, m * 128:(m + 1) * 128].rearrange("(kk p) c -> p kk c", p=128), writes=[wb[b]])
                for ti, (o, n) in enumerate(tl):
                    ab = it % 2
                    sb_ = stg[it % 3]
                    it += 1
                    for (pt, wt) in ((pa[ab], wa[b]), (pb[ab], wb[b])):
                        for c in range(8):
                            k.op(k.pe, (lambda e, pt=pt, wt=wt, c=c, o=o, n=n: e.matmul(
                                pt[:, 0:n], lhsT=wt[:, c, :], rhs=hn[:, c, o:o + n], start=(c == 0), stop=(c == 7))),
                                reads=[wt, hnv[c][ti]], writes=[pt])
                    k.op(k.dve, (lambda e, ab=ab, ti=ti, n=n: e.tensor_tensor(out=t1[ab][:, 0:n], in0=pa[ab][:, 0:n],
                                                                              in1=cosF[ti][:, 0:n], op=ALU.mult)),
                         reads=[pa[ab], cosF[ti]], writes=[t1[ab]])
                    k.op(k.dve, (lambda e, ab=ab, ti=ti, n=n: e.tensor_tensor(out=t2[ab][:, 0:n], in0=pb[ab][:, 0:n],
                                                                              in1=sinF[ti][:, 0:n], op=ALU.mult)),
                         reads=[pb[ab], sinF[ti]], writes=[t2[ab]])
                    k.op(k.pool, (lambda e, ab=ab, sb_=sb_, n=n: e.tensor_tensor(out=sb_[:, 0:n], in0=t1[ab][:, 0:n],
                                                                                 in1=t2[ab][:, 0:n], op=ALU.add)),
                         reads=[t1[ab], t2[ab]], writes=[sb_])
                    k.dma(k.sp, self.qkT[m * 128:(m + 1) * 128, T0 + o:T0 + o + n], sb_[:, 0:n], reads=[sb_])
            wv = [k.sb([128, 8, 512], BF16, "wv") for _ in range(2)]
            vst = [k.sb([128, 512], BF16, "vst") for _ in range(3)]
            it = 0
            for vb in range(2):
                k.dma(k.pool, wv[vb][:], self.b_w[:, 2048 + vb * 512:2048 + (vb + 1) * 512].rearrange("(kk p) c -> p kk c", p=128),
                      writes=[wv[vb]])
                for ti, (o, n) in enumerate(tl):
                    for bo in range(0, n, 128):
                        r = min(128, n - bo)
                        pt = pd[it % 4]
                        vs = vst[it % 3]
                        it += 1
                        for c in range(8):
                            k.op(k.pe, (lambda e, pt=pt, vb=vb, c=c, o=o, bo=bo, r=r: e.matmul(
                                pt[0:r, :], lhsT=hn[:, c, o + bo:o + bo + r], rhs=wv[vb][:, c, :], start=(c == 0), stop=(c == 7))),
                                reads=[wv[vb], hnv[c][ti]], writes=[pt])
                        k.op(k.act, (lambda e, pt=pt, vs=vs, r=r: e.copy(vs[0:r, :], pt[0:r, :])), reads=[pt], writes=[vs])
                        k.dma(k.sp, self.v_tm[T0 + o + bo:T0 + o + bo + r, vb * 512:(vb + 1) * 512], vs[0:r, :], reads=[vs])
            k.end()

    def phase_attn_core(self):
        k, nc, T = self.k, self.nc, self.T
        NKT = (T + 127) // 128
        qtl = tiles_of(T, 512)
        k.begin()
        lt = k.sb([128, 256], F32, "lt")
        l2 = k.sb([128, 2], F32, "l2")
        neglam = k.sb([128, 1], F32, "neglam")
        subw = k.sb([128, 1], F32, "subw")
        kb = k.sb([128, NKT], F32, "kb")
        k.dma(k.sp, kb[:], self.kbias_in, writes=[kb])
        k.op(k.dve, (lambda e: e.tensor_tensor(out=lt[:, 0:64], in0=self.bsm[:, 0:64], in1=self.bsm[:, 64:128], op=ALU.mult)),
             reads=[self.bsm], writes=[lt])
        k.op(k.dve, (lambda e: e.tensor_tensor(out=lt[:, 64:128], in0=self.bsm[:, 128:192], in1=self.bsm[:, 192:256], op=ALU.mult)),
             reads=[self.bsm], writes=[lt])
        k.op(k.dve, (lambda e: e.reduce_sum(l2[:, 0:2], lt[:, 0:128].rearrange("p (a b) -> p a b", a=2), axis=AX.X)),
             reads=[lt], writes=[l2])
        k.op(k.act, (lambda e: e.activation(l2[:, 0:2], l2[:, 0:2], AF.Exp)), reads=[l2], writes=[l2])
        k.op(k.dve, (lambda e: e.tensor_tensor(out=neglam[:], in0=l2[:, 1:2], in1=l2[:, 0:1], op=ALU.subtract)),
             reads=[l2], writes=[neglam])
        k.op(k.dve, (lambda e: e.tensor_tensor(out=neglam[:], in0=neglam[:], in1=self.bsm[:, 259:260], op=ALU.subtract)),
             reads=[neglam, self.bsm], writes=[neglam])
        k.op(k.dve, (lambda e: e.tensor_scalar(subw[:], self.bsm[:, 259:260], -1.0, 1.0, op0=ALU.mult, op1=ALU.add)),
             reads=[self.bsm], writes=[subw])
        k.op(k.dve, (lambda e: e.tensor_tensor(out=subw[:], in0=subw[:], in1=self.bsm[:, 256:257], op=ALU.mult)),
             reads=[subw, self.bsm], writes=[subw])

        kk1 = [k.sb([64, T], BF16, "kk1") for _ in range(2)]
        kk2 = [k.sb([64, T], BF16, "kk2") for _ in range(2)]
        vh = [k.sb([128, NKT, 128], BF16, "vh") for _ in range(2)]
        q1 = [k.sb([64, 512], BF16, "q1") for _ in range(2)]
        q2 = [k.sb([64, 512], BF16, "q2") for _ in range(2)]
        p1 = [k.sb([128, 512], BF16, "p1") for _ in range(3)]
        p2 = [k.sb([128, 512], BF16, "p2") for _ in range(3)]
        ps1 = [k.ps([128, 512], F32, "ps1") for _ in range(2)]
        ps2 = [k.ps([128, 512], F32, "ps2") for _ in range(2)]
        num1, num2, z1, z2 = (k.ps([128, 512], F32, nm) for nm in ("num1", "num2", "z1", "z2"))
        za1 = k.sb([128, 512], F32, "za1")
        za2 = k.sb([128, 512], F32, "za2")
        ones_f = k.sb([128, 128], F32, "ones_f")
        k.op(k.dve, (lambda e: e.memset(ones_f[:], 1.0)), writes=[ones_f])
        r1 = k.sb([128, 512], F32, "r1")
        r2 = k.sb([128, 512], F32, "r2")
        o1 = k.sb([128, 512], F32, "o1")
        o2 = k.sb([128, 512], F32, "o2")
        oo = k.sb([128, 512], F32, "oo")
        sqo = k.sb([128, 512], BF16, "sqo")
        rs = k.sb([128, 512], F32, "rs")
        ost = [k.sb([128, 512], BF16, "ost") for _ in range(2)]
        nfull = T // 128
        rem = T - nfull * 128
        it = 0
        fi = 0
        for h in range(8):
            hb = h % 2
            k.dma(k.sp, kk1[hb][:], self.qkT[1024 + h * 64:1024 + (h + 1) * 64, :], writes=[kk1[hb]])
            k.dma(k.sp, kk2[hb][:], self.qkT[1536 + h * 64:1536 + (h + 1) * 64, :], writes=[kk2[hb]])
            k.dma(k.sp, vh[hb][:, 0:nfull, :],
                  self.v_tm[0:nfull * 128, h * 128:(h + 1) * 128].rearrange("(kt p) e -> p kt e", p=128), writes=[vh[hb]])
            if rem:
                k.dma(k.sp, vh[hb][0:rem, nfull, :], self.v_tm[nfull * 128:T, h * 128:(h + 1) * 128], writes=[vh[hb]])
            for qi, (t0, n) in enumerate(qtl):
                qb = fi % 2
                k.dma(k.sp, q1[qb][:, 0:n], self.qkT[h * 64:(h + 1) * 64, t0:t0 + n], writes=[q1[qb]])
                k.dma(k.sp, q2[qb][:, 0:n], self.qkT[512 + h * 64:512 + (h + 1) * 64, t0:t0 + n], writes=[q2[qb]])
                base_it = it
                it += NKT

                def emit_scores(kt):
                    kn = 128 if kt < nfull else rem
                    sb2 = (base_it + kt) % 2
                    for (ps, kk, qq) in ((ps1[sb2], kk1[hb], q1[qb]), (ps2[sb2], kk2[hb], q2[qb])):
                        k.op(k.pe, (lambda e, ps=ps, kk=kk, qq=qq, kt=kt, kn=kn, n=n: e.matmul(
                            ps[0:kn, 0:n], lhsT=kk[:, kt * 128:kt * 128 + kn], rhs=qq[:, 0:n], start=True, stop=True)),
                            reads=[kk, qq], writes=[ps])

                def emit_exp(kt):
                    kn = 128 if kt < nfull else rem
                    sb2 = (base_it + kt) % 2
                    pb3 = (base_it + kt) % 3
                    for (ps, pp) in ((ps1[sb2], p1[pb3]), (ps2[sb2], p2[pb3])):
                        k.op(k.act, (lambda e, ps=ps, pp=pp, kt=kt, kn=kn, n=n: e.activation(
                            pp[0:kn, 0:n], ps[0:kn, 0:n], AF.Exp, bias=kb[0:kn, kt:kt + 1], scale=0.125)),
                            reads=[ps, kb], writes=[pp])

                def emit_pv(kt):
                    kn = 128 if kt < nfull else rem
                    pb3 = (base_it + kt) % 3
                    first, last = (kt == 0), (kt == NKT - 1)
                    for (pp, nm, za, zeng) in ((p1[pb3], num1, za1, k.dve), (p2[pb3], num2, za2, k.pool)):
                        k.op(k.pe, (lambda e, nm=nm, pp=pp, kt=kt, kn=kn, n=n, first=first, last=last, vv=vh[hb]: e.matmul(
                            nm[:, 0:n], lhsT=vv[0:kn, kt, :], rhs=pp[0:kn, 0:n], start=first, stop=last)),
                            reads=[vh[hb], pp], writes=[nm])
                        if first:
                            k.op(zeng, (lambda e, za=za, pp=pp, n=n: e.tensor_copy(za[:, 0:n], pp[:, 0:n])), reads=[pp], writes=[za])
                        else:
                            k.op(zeng, (lambda e, za=za, pp=pp, kn=kn, n=n: e.tensor_tensor(out=za[0:kn, 0:n], in0=za[0:kn, 0:n],
                                                                                           in1=pp[0:kn, 0:n], op=ALU.add)),
                                 reads=[pp, za], writes=[za])

                emit_scores(0)
                for kt in range(NKT):
                    emit_exp(kt)
                    if kt + 1 < NKT:
                        emit_scores(kt + 1)
                    emit_pv(kt)
                for (zz, za) in ((z1, za1), (z2, za2)):
                    k.op(k.pe, (lambda e, zz=zz, za=za, n=n: e.matmul(zz[:, 0:n], lhsT=ones_f[:], rhs=za[:, 0:n], start=True, stop=True)),
                         reads=[ones_f, za], writes=[zz])
                k.op(k.dve, (lambda e, n=n: e.reciprocal(r1[:, 0:n], z1[:, 0:n])), reads=[z1], writes=[r1])
                k.op(k.dve, (lambda e, n=n: e.reciprocal(r2[:, 0:n], z2[:, 0:n])), reads=[z2], writes=[r2])
                k.op(k.dve, (lambda e, n=n: e.tensor_tensor(out=o1[:, 0:n], in0=num1[:, 0:n], in1=r1[:, 0:n], op=ALU.mult)),
                     reads=[num1, r1], writes=[o1])
                k.op(k.dve, (lambda e, n=n: e.scalar_tensor_tensor(out=o2[:, 0:n], in0=num2[:, 0:n], scalar=neglam[:, 0:1],
                                                                   in1=r2[:, 0:n], op0=ALU.mult, op1=ALU.mult)),
                     reads=[num2, r2, neglam], writes=[o2])
                k.op(k.pool, (lambda e, n=n: e.tensor_tensor(out=oo[:, 0:n], in0=o1[:, 0:n], in1=o2[:, 0:n], op=ALU.add)),
                     reads=[o1, o2], writes=[oo])
                k.op(k.act, (lambda e, n=n: e.activation(sqo[:, 0:n], oo[:, 0:n], AF.Square)), reads=[oo], writes=[sqo])
                pss = ps1[it % 2]
                k.op(k.pe, (lambda e, pss=pss, n=n: e.matmul(pss[:, 0:n], lhsT=self.ones_bf[:], rhs=sqo[:, 0:n], start=True, stop=True)),
                     reads=[self.ones_bf, sqo], writes=[pss])
                k.op(k.act, (lambda e, pss=pss, n=n: e.activation(rs[:, 0:n], pss[:, 0:n], AF.Ln, bias=self.eps_col[:, 0:1],
                                                                   scale=1.0 / 128)), reads=[pss, self.eps_col], writes=[rs])
                k.op(k.act, (lambda e, n=n: e.activation(rs[:, 0:n], rs[:, 0:n], AF.Exp, scale=-0.5)), reads=[rs], writes=[rs])
                os_ = ost[fi % 2]
                fi += 1
                k.op(k.dve, (lambda e, os_=os_, n=n: e.scalar_tensor_tensor(out=os_[:, 0:n], in0=oo[:, 0:n], scalar=subw[:, 0:1],
                                                                            in1=rs[:, 0:n], op0=ALU.mult, op1=ALU.mult)),
                     reads=[oo, rs, subw], writes=[os_])
                k.dma(k.sp, self.oT[h * 128:(h + 1) * 128, t0:t0 + n], os_[:, 0:n], reads=[os_])
        k.end()

    def phase_out_proj(self, wo_ap):
        k, nc, T = self.k, self.nc, self.T
        hTv = self.hT.rearrange("(c p) t -> p c t", p=128)
        oTv = self.oT.rearrange("(c p) t -> p c t", p=128)
        k.begin()
        wo = k.sb([128, 8, D_MODEL], BF16, "wo")
        for c in range(8):
            for hf in range(2):
                k.dma(k.pool, wo[:, c, hf * 512:(hf + 1) * 512], wo_ap[c * 128:(c + 1) * 128, hf * 512:(hf + 1) * 512], writes=[wo])
        ob = [k.sb([128, 8, 512], BF16, "ob") for _ in range(2)]
        hb = [k.sb([128, 8, 512], F32, "hb") for _ in range(2)]
        pp = [k.ps([128, 512], F32, "pp") for _ in range(4)]
        pi = 0
        for gi, (t0, n) in enumerate(tiles_of(T, 512)):
            o_ = ob[gi % 2]
            h_ = hb[gi % 2]
            k.dma(k.sp, o_[:, :, 0:n], oTv[:, :, t0:t0 + n], writes=[o_])
            k.dma(k.sp, h_[:, :, 0:n], hTv[:, :, t0:t0 + n], writes=[h_])
            for m in range(8):
                p = pp[pi % 4]
                pi += 1
                for c in range(8):
                    k.op(k.pe, (lambda e, p=p, o_=o_, c=c, m=m, n=n: e.matmul(
                        p[:, 0:n], lhsT=wo[:, c, m * 128:(m + 1) * 128], rhs=o_[:, c, 0:n], start=(c == 0), stop=(c == 7))),
                        reads=[wo, o_], writes=[p])
                k.op(k.dve, (lambda e, p=p, h_=h_, m=m, n=n: e.tensor_tensor(out=h_[:, m, 0:n], in0=h_[:, m, 0:n],
                                                                             in1=p[:, 0:n], op=ALU.add)),
                     reads=[p, h_], writes=[h_])
            k.dma(k.sp, hTv[:, :, t0:t0 + n], h_[:, :, 0:n], reads=[h_])
        k.end()

    def mixer_attn(self, nrow):
        import os
        st = int(os.environ.get("ATT_STAGE", "3"))
        self.phase_attn_proj(nrow)
        if st >= 2:
            self.phase_attn_core()
        if st >= 3:
            self.phase_out_proj(self.b_wo)


    def phase_proj(self, nrow, setup, fm_jobs, tm_jobs):
        k, nc = self.k, self.nc
        hTv = self.hT.rearrange("(c p) t -> p c t", p=128)
        for (T0, GS) in self.groups:
            tl = tiles_of(GS, 512)
            NT = len(tl)
            k.begin()
            y = k.sb([128, 8, GS], F32, "y")
            hn = k.sb([128, 8, GS], BF16, "hn")
            yv = [k.views(y, NT) for _ in range(8)]
            hnv = [k.views(hn, NT) for _ in range(8)]
            sq = [k.sb([128, 512], BF16, "sq") for _ in range(2)]
            rstd = [k.sb([128, 512], F32, "rstd") for _ in range(2)]
            pd = [k.ps([128, 512], F32, "pd") for _ in range(4)]
            allv = [v for c in range(8) for v in yv[c]]
            k.dma(k.sp, y[:], hTv[:, :, T0:T0 + GS], writes=allv)
            self.emit_norm(y, yv, hn, hnv, tl, nrow, pd, sq, rstd)
            ctx = setup(T0, tl)
            wa = [k.sb([128, 8, 128], BF16, "wa") for _ in range(2)]
            it = 0
            for ji, (wap, post) in enumerate(fm_jobs):
                b = ji % 2
                k.dma(k.pool, wa[b][:], wap.rearrange("(kk p) c -> p kk c", p=128), writes=[wa[b]])
                for ti, (o, n) in enumerate(tl):
                    pt = pd[it % 4]
                    it += 1
                    for c in range(8):
                        k.op(k.pe, (lambda e, pt=pt, b=b, c=c, o=o, n=n: e.matmul(
                            pt[:, 0:n], lhsT=wa[b][:, c, :], rhs=hn[:, c, o:o + n], start=(c == 0), stop=(c == 7))),
                            reads=[wa[b], hnv[c][ti]], writes=[pt])
                    post(ctx, pt, T0, ti, o, n)
            wv = [k.sb([128, 8, 512], BF16, "wv") for _ in range(2)]
            for ji, job in enumerate(tm_jobs):
                wap, post = job[0], job[1]
                ncl = job[2] if len(job) > 2 else 512
                b = ji % 2
                k.dma(k.pool, wv[b][:, :, 0:ncl], wap.rearrange("(kk p) c -> p kk c", p=128), writes=[wv[b]])
                for ti, (o, n) in enumerate(tl):
                    for bo in range(0, n, 128):
                        r = min(128, n - bo)
                        pt = pd[it % 4]
                        it += 1
                        for c in range(8):
                            k.op(k.pe, (lambda e, pt=pt, b=b, c=c, o=o, bo=bo, r=r, ncl=ncl: e.matmul(
                                pt[0:r, 0:ncl], lhsT=hn[:, c, o + bo:o + bo + r], rhs=wv[b][:, c, 0:ncl], start=(c == 0), stop=(c == 7))),
                                reads=[wv[b], hnv[c][ti]], writes=[pt])
                        post(ctx, pt, T0 + o + bo, r)
            k.end()

    def hg_prep(self, li):
        k, nc = self.k, self.nc
        k.begin()
        e = k.sb([128, 32], F32, "lbe")
        ssum = k.sb([128, 8], F32, "lbs")
        k.op(k.act, (lambda en: en.activation(e[:], self.csm[:, 0:32], AF.Exp)), reads=[self.csm], writes=[e])
        k.op(k.dve, (lambda en: en.tensor_tensor(out=ssum[:], in0=e[:, 0:8], in1=e[:, 8:16], op=ALU.add)), reads=[e], writes=[ssum])
        k.op(k.dve, (lambda en: en.tensor_tensor(out=ssum[:], in0=ssum[:], in1=e[:, 16:24], op=ALU.add)), reads=[e, ssum], writes=[ssum])
        k.op(k.dve, (lambda en: en.tensor_tensor(out=ssum[:], in0=ssum[:], in1=e[:, 24:32], op=ALU.add)), reads=[e, ssum], writes=[ssum])
        k.op(k.dve, (lambda en: en.reciprocal(ssum[:], ssum[:])), reads=[ssum], writes=[ssum])
        lbc = self.lb_col
        k.op(k.dve, (lambda en: en.memset(lbc[:, 0:8], 0.0)), writes=[lbc])
        for r in range(1, li + 1):
            k.op(k.dve, (lambda en, r=r: en.tensor_tensor(out=lbc[:, 0:8], in0=lbc[:, 0:8], in1=e[:, r * 8:(r + 1) * 8], op=ALU.add)),
                 reads=[e, lbc], writes=[lbc])
        k.op(k.dve, (lambda en: en.tensor_tensor(out=lbc[:, 0:8], in0=lbc[:, 0:8], in1=ssum[:], op=ALU.mult)), reads=[lbc, ssum], writes=[lbc])
        k.op(k.dve, (lambda en: en.tensor_scalar(lbc[:, 8:16], lbc[:, 0:8], -1.0, 1.0, op0=ALU.mult, op1=ALU.add)), reads=[lbc], writes=[lbc])
        lbr = self.lbrow_t.ap()
        k.dma(k.sp, lbr.rearrange("r (c p) -> p r c", p=128), lbc[:].rearrange("p (r c) -> p r c", c=8), reads=[lbc],
              allow_slow_non_contiguous=True)
        k.end()
        k.begin()
        k.dma(k.sp, self.oml_row[:], bass.AP(self.lbrow_t, D_MODEL, [[0, 128], [1, D_MODEL]]), writes=[self.oml_row])
        k.end()

    def phase_hg_proj(self, nrow):
        k = self.k
        QS = 128 ** -0.5

        def setup(T0, tl):
            ctx = {}
            ctx["tm"] = [k.sb([128, 512], F32, "tmk") for _ in tl]
            for ti, (o, n) in enumerate(tl):
                k.dma(k.sp, ctx["tm"][ti][:, 0:n], self.tmask_rep[:, T0 + o:T0 + o + n], writes=[ctx["tm"][ti]])
            ctx["sg"] = [k.sb([128, 512], F32, "sg") for _ in range(2)]
            ctx["st"] = [k.sb([128, 512], BF16, "st") for _ in range(3)]
            ctx["k32"] = [k.sb([128, 512], F32, "k32") for _ in range(2)]
            ctx["lf"] = [k.sb([128, 512], F32, "lf") for _ in range(2)]
            ctx["kb"] = [k.sb([128, 512], BF16, "kb") for _ in range(2)]
            ctx["i"] = 0
            return ctx

        def post_silu(dst, scale):
            def post(ctx, pt, T0, ti, o, n):
                i = ctx["i"]
                ctx["i"] += 1
                sg, st = ctx["sg"][i % 2], ctx["st"][i % 3]
                k.op(k.act, (lambda e: e.activation(sg[:, 0:n], pt[:, 0:n], AF.Silu)), reads=[pt], writes=[sg])
                k.op(k.pool, (lambda e: e.tensor_scalar(st[:, 0:n], sg[:, 0:n], scale, None, op0=ALU.mult)), reads=[sg], writes=[st])
                return st
            return post

        def fm_q(c):
            base = post_silu(None, QS)

            def post(ctx, pt, T0, ti, o, n):
                st = base(ctx, pt, T0, ti, o, n)
                k.dma(k.sp, self.fmA[c * 128:(c + 1) * 128, T0 + o:T0 + o + n], st[:, 0:n], reads=[st])
            return post

        def fm_g(c):
            base = post_silu(None, 1.0)

            def post(ctx, pt, T0, ti, o, n):
                st = base(ctx, pt, T0, ti, o, n)
                k.dma(k.sp, self.fmG[c * 128:(c + 1) * 128, T0 + o:T0 + o + n], st[:, 0:n], reads=[st])
            return post

        def fm_k(d, c):
            def post(ctx, pt, T0, ti, o, n):
                i = ctx["i"]
                ctx["i"] += 1
                sg, st = ctx["sg"][i % 2], ctx["st"][i % 3]
                k.op(k.act, (lambda e: e.activation(sg[:, 0:n], pt[:, 0:n], AF.Sigmoid, scale=-1.0)), reads=[pt], writes=[sg])
                k.op(k.dve, (lambda e: e.scalar_tensor_tensor(out=st[:, 0:n], in0=sg[:, 0:n], scalar=self.lb_col[:, 8 + c:9 + c],
                                                              in1=ctx["tm"][ti][:, 0:n], op0=ALU.mult, op1=ALU.mult)),
                     reads=[sg, self.lb_col, ctx["tm"][ti]], writes=[st])
                k.dma(k.sp, self.fmB[d][c * 128:(c + 1) * 128, T0 + o:T0 + o + n], st[:, 0:n], reads=[st])
            return post

        def tm_v(cb):
            def post(ctx, pt, tok0, r):
                i = ctx["i"]
                ctx["i"] += 1
                st = ctx["st"][i % 3]
                k.op(k.act, (lambda e: e.copy(st[0:r, :], pt[0:r, :])), reads=[pt], writes=[st])
                k.dma(k.sp, self.v_tm[tok0:tok0 + r, cb * 512:(cb + 1) * 512], st[0:r, :], reads=[st])
            return post

        def tm_f(d, cb):
            def post(ctx, pt, tok0, r):
                i = ctx["i"]
                ctx["i"] += 1
                sg, k32, lf, kb = ctx["sg"][i % 2], ctx["k32"][i % 2], ctx["lf"][i % 2], ctx["kb"][i % 2]
                blk = tok0 // 128
                assert tok0 % 128 == 0
                k.op(k.act, (lambda e: e.activation(sg[0:r, :], pt[0:r, :], AF.Sigmoid, scale=-1.0)), reads=[pt], writes=[sg])
                k.op(k.dve, (lambda e: e.scalar_tensor_tensor(out=k32[0:r, :], in0=sg[0:r, :], scalar=self.tmc[0:r, blk:blk + 1],
                                                              in1=self.oml_row[0:r, cb * 512:(cb + 1) * 512], op0=ALU.mult, op1=ALU.mult)),
                     reads=[sg, self.tmc, self.oml_row], writes=[k32])
                k.op(k.act, (lambda e: e.activation(lf[0:r, :], k32[0:r, :], AF.Ln, bias=self.one_col[0:r, 0:1], scale=-1.0)),
                     reads=[k32, self.one_col], writes=[lf])
                k.op(k.pool, (lambda e: e.tensor_copy(kb[0:r, :], k32[0:r, :])), reads=[k32], writes=[kb])
                k.dma(k.sp, self.tmK[d][tok0:tok0 + r, cb * 512:(cb + 1) * 512], kb[0:r, :], reads=[kb])
                k.dma(k.sp, self.tmL[d][tok0:tok0 + r, cb * 512:(cb + 1) * 512], lf[0:r, :], reads=[lf])
            return post

        W = self.c_w
        fm = []
        for c in range(8):
            fm.append((W[:, c * 128:(c + 1) * 128], fm_q(c)))
        for c in range(8):
            fm.append((W[:, 2048 + c * 128:2048 + (c + 1) * 128], fm_g(c)))
        for d in range(2):
            for c in range(8):
                fm.append((W[:, 3072 + d * 1024 + c * 128:3072 + d * 1024 + (c + 1) * 128], fm_k(d, c)))
        tm = []
        for cb in range(2):
            tm.append((W[:, 1024 + cb * 512:1024 + (cb + 1) * 512], tm_v(cb)))
        for d in range(2):
            for cb in range(2):
                tm.append((W[:, 3072 + d * 1024 + cb * 512:3072 + d * 1024 + (cb + 1) * 512], tm_f(d, cb)))
        self.phase_proj(nrow, setup, fm, tm)

    def phase_hg_core(self, onorm_col):
        k, nc, T = self.k, self.nc, self.T
        tiles = tiles_of(T, 512)
        for sweep in (1, 0):
            d = sweep
            k.begin()
            if d == 0:
                Uin, Uex, Mk, lastcol = self.tri[:, 0:64], self.tri[:, 64:128], 0, 63
            else:
                Uin, Uex, Mk, lastcol = self.tri[:, 128:192], self.tri[:, 192:256], 2, 0
            mrep = k.sb([64, 512], F32, "mrep")
            for c in range(8):
                k.op(k.dve, (lambda e, c=c, Mk=Mk: e.tensor_copy(mrep[:, c * 64:(c + 1) * 64], self.tri[:, Mk * 64:(Mk + 1) * 64])),
                     reads=[self.tri], writes=[mrep])
            S = [k.sb([128, 128], F32, "S") for _ in range(8)]
            Sb = [k.sb([128, 128], BF16, "Sb") for _ in range(8)]
            for h in range(8):
                k.op(k.dve, (lambda e, h=h: e.memset(S[h][:], 0.0)), writes=[S[h]])
                k.op(k.pool, (lambda e, h=h: e.memset(Sb[h][:], 0.0)), writes=[Sb[h]])
            NB = 2
            qT = [k.sb([128, 512], BF16, "qT") for _ in range(NB)]
            kT = [k.sb([128, 512], BF16, "kT") for _ in range(NB)]
            ktm = [k.sb([64, 8, 128], BF16, "ktm") for _ in range(NB)]
            lf = [k.sb([64, 8, 128], F32, "lf") for _ in range(NB)]
            vt = [k.sb([64, 8, 128], BF16, "vt") for _ in range(NB)]
            eb = [k.sb([128, 512], F32, "eb") for _ in range(NB)]
            enb = [k.sb([128, 512], F32, "enb") for _ in range(NB)]
            qd = [k.sb([128, 512], BF16, "qd") for _ in range(NB)]
            kd = [k.sb([128, 512], BF16, "kd") for _ in range(NB)]
            ekd = [k.sb([64, 8, 128], F32, "ekd") for _ in range(NB)]
            kdec = [k.sb([64, 8, 128], BF16, "kdec") for _ in range(NB)]
            atm = [k.sb([64, 512], BF16, "atm") for _ in range(NB)]
            B = [k.ps([128, 512], F32, "Y%d" % i) for i in range(8)]
            ost = [k.sb([128, 512], F32, "ost") for _ in range(2)]
            if d == 0:
                obt = [k.sb([128, 512], F32, "obt") for _ in range(2)]
                gt = [k.sb([128, 512], BF16, "gt") for _ in range(2)]
                sqo_l = [k.sb([128, 512], BF16, "sqo") for _ in range(2)]
                rs_l = [k.sb([128, 512], F32, "rs") for _ in range(2)]
                fin = [k.sb([128, 512], BF16, "fin") for _ in range(2)]
            order = list(range(len(tiles)))
            if d == 1:
                order = order[::-1]

            def unit(h, b, it, t0, n, ncn, corder):
                if True:
                    rows = slice(h * 128, (h + 1) * 128)
                    Y = B[4 * b:4 * b + 4]
                    pbc, pat = Y[0], Y[3]
                    psuf = [Y[1], Y[2]]
                    pout = Y[1]
                    pst = [Y[2], Y[2]]
                    if d == 0:
                        sqo, rs = sqo_l[b], rs_l[b]
                    k.dma(k.sp, qT[b][:, 0:n], self.fmA[rows, t0:t0 + n], writes=[qT[b]])
                    k.dma(k.sp, kT[b][:, 0:n], self.fmB[d][rows, t0:t0 + n], writes=[kT[b]])
                    k.dma(k.sp, ktm[b][:, 0:ncn, :], self.tmK[d][t0:t0 + n, rows].rearrange("(c p) e -> p c e", p=64), writes=[ktm[b]])
                    k.dma(k.sp, lf[b][:, 0:ncn, :], self.tmL[d][t0:t0 + n, rows].rearrange("(c p) e -> p c e", p=64), writes=[lf[b]])
                    k.dma(k.sp, vt[b][:, 0:ncn, :], self.v_tm[t0:t0 + n, rows].rearrange("(c p) e -> p c e", p=64), writes=[vt[b]])
                    if d == 0:
                        k.dma(k.sp, obt[b][:, 0:n], self.obT[rows, t0:t0 + n], writes=[obt[b]])
                        k.dma(k.sp, gt[b][:, 0:n], self.fmG[rows, t0:t0 + n], writes=[gt[b]])
                    for c in range(ncn):
                        k.op(k.pe, (lambda e, b=b, c=c: e.matmul(pbc[:, c * 64:(c + 1) * 64], lhsT=lf[b][:, c, :], rhs=Uin,
                                                                 start=True, stop=True)), reads=[lf[b], self.tri], writes=[pbc])
                    k.op(k.act, (lambda e, b=b, n=n: e.activation(eb[b][:, 0:n], pbc[:, 0:n], AF.Exp)), reads=[pbc], writes=[eb[b]])
                    k.op(k.act, (lambda e, b=b, n=n: e.activation(enb[b][:, 0:n], pbc[:, 0:n], AF.Exp, scale=-1.0)), reads=[pbc], writes=[enb[b]])
                    k.op(k.dve, (lambda e, b=b, n=n: e.tensor_tensor(out=qd[b][:, 0:n], in0=qT[b][:, 0:n], in1=eb[b][:, 0:n], op=ALU.mult)),
                         reads=[qT[b], eb[b]], writes=[qd[b]])
                    k.op(k.dve, (lambda e, b=b, n=n: e.tensor_tensor(out=kd[b][:, 0:n], in0=kT[b][:, 0:n], in1=enb[b][:, 0:n], op=ALU.mult)),
                         reads=[kT[b], enb[b]], writes=[kd[b]])
                    for c in range(ncn):
                        ps_ = psuf[c // 4]
                        k.op(k.pe, (lambda e, b=b, c=c, ps_=ps_: e.matmul(ps_[0:64, (c % 4) * 128:(c % 4 + 1) * 128], lhsT=Uex, rhs=lf[b][:, c, :],
                                                                          start=True, stop=True)), reads=[lf[b], self.tri], writes=[ps_])
                    for half in range((ncn + 3) // 4):
                        nn = min(4, ncn - half * 4)
                        k.op(k.act, (lambda e, b=b, half=half, nn=nn, pq=psuf[half]: e.activation(
                            ekd[b][:, half * 4:half * 4 + nn, :], pq[0:64, 0:nn * 128].rearrange("p (c d) -> p c d", d=128), AF.Exp)),
                             reads=[psuf[half]], writes=[ekd[b]])
                    k.op(k.dve, (lambda e, b=b, ncn=ncn: e.tensor_tensor(out=kdec[b][:, 0:ncn, :], in0=ktm[b][:, 0:ncn, :],
                                                                        in1=ekd[b][:, 0:ncn, :], op=ALU.mult)),
                         reads=[ktm[b], ekd[b]], writes=[kdec[b]])
                    for c in range(ncn):
                        cs = slice(c * 64, (c + 1) * 64)
                        k.op(k.pe, (lambda e, b=b, cs=cs: e.matmul(pat[0:64, cs], lhsT=kd[b][:, cs], rhs=qd[b][:, cs], start=True, stop=True)),
                             reads=[kd[b], qd[b]], writes=[pat])
                    k.op(k.dve, (lambda e, b=b, n=n: e.tensor_tensor(out=atm[b][:, 0:n], in0=pat[0:64, 0:n], in1=mrep[:, 0:n], op=ALU.mult)),
                         reads=[pat, mrep], writes=[atm[b]])
                    k.mark()
                    for c in corder:
                        cs = slice(c * 64, (c + 1) * 64)
                        k.op(k.pe, (lambda e, b=b, cs=cs, h=h: e.matmul(pout[:, cs], lhsT=Sb[h][:], rhs=qd[b][:, cs], start=True, stop=False)),
                             reads=[Sb[h], qd[b]], writes=[pout])
                        k.op(k.pe, (lambda e, b=b, cs=cs, c=c: e.matmul(pout[:, cs], lhsT=vt[b][:, c, :], rhs=atm[b][:, cs], start=False, stop=True)),
                             reads=[vt[b], atm[b]], writes=[pout])
                        pp = pst[c % 2]
                        pc = slice((c % 2) * 128, (c % 2 + 1) * 128)
                        k.op(k.pe, (lambda e, b=b, c=c, pp=pp, pc=pc: e.matmul(pp[:, pc], lhsT=kdec[b][:, c, :], rhs=vt[b][:, c, :], start=True, stop=True)),
                             reads=[kdec[b], vt[b]], writes=[pp])
                        fc = c * 64 + lastcol
                        k.op(k.dve, (lambda e, b=b, h=h, pp=pp, fc=fc, pc=pc: e.scalar_tensor_tensor(
                            out=S[h][:], in0=S[h][:], scalar=eb[b][:, fc:fc + 1], in1=pp[:, pc], op0=ALU.mult, op1=ALU.add)),
                            reads=[S[h], eb[b], pp], writes=[S[h]])
                        k.op(k.act, (lambda e, h=h: e.copy(Sb[h][:], S[h][:])), reads=[S[h]], writes=[Sb[h]])
                    if d == 1:
                        os_ = ost[it % 2]
                        k.op(k.act, (lambda e, os_=os_, n=n: e.copy(os_[:, 0:n], pout[:, 0:n])), reads=[pout], writes=[os_])
                        k.dma(k.sp, self.obT[rows, t0:t0 + n], os_[:, 0:n], reads=[os_])
                    else:
                        os_ = ost[it % 2]
                        k.op(k.dve, (lambda e, os_=os_, b=b, n=n: e.tensor_tensor(out=os_[:, 0:n], in0=pout[:, 0:n], in1=obt[b][:, 0:n], op=ALU.add)),
                             reads=[pout, obt[b]], writes=[os_])
                        k.op(k.act, (lambda e, os_=os_, n=n: e.activation(sqo[:, 0:n], os_[:, 0:n], AF.Square)), reads=[os_], writes=[sqo])
                        k.op(k.pe, (lambda e, n=n: e.matmul(pbc[:, 0:n], lhsT=self.ones_bf[:], rhs=sqo[:, 0:n], start=True, stop=True)),
                             reads=[self.ones_bf, sqo], writes=[pbc])
                        k.op(k.act, (lambda e, n=n: e.activation(rs[:, 0:n], pbc[:, 0:n], AF.Ln, bias=self.eps_col[:, 0:1], scale=1.0 / 128)),
                             reads=[pbc, self.eps_col], writes=[rs])
                        k.op(k.act, (lambda e, n=n: e.activation(rs[:, 0:n], rs[:, 0:n], AF.Exp, scale=-0.5)), reads=[rs], writes=[rs])
                        k.op(k.dve, (lambda e, os_=os_, n=n: e.scalar_tensor_tensor(out=os_[:, 0:n], in0=os_[:, 0:n], scalar=onorm_col,
                                                                                   in1=rs[:, 0:n], op0=ALU.mult, op1=ALU.mult)),
                             reads=[os_, rs, self.csm], writes=[os_])
                        fo = fin[it % 2]
                        k.op(k.pool, (lambda e, os_=os_, fo=fo, b=b, n=n: e.tensor_tensor(out=fo[:, 0:n], in0=os_[:, 0:n], in1=gt[b][:, 0:n], op=ALU.mult)),
                             reads=[os_, gt[b]], writes=[fo])
                        k.dma(k.sp, self.oT[rows, t0:t0 + n], fo[:, 0:n], reads=[fo])
            uid = 0
            prev_scan = []
            for ti in order:
                t0, n = tiles[ti]
                ncn = n // 64
                corder = list(range(ncn)) if d == 0 else list(range(ncn))[::-1]
                for h in range(8):
                    b = uid % NB
                    uid += 1
                    k.marks = []
                    k.set_lane("u")
                    unit(h, b, uid, t0, n, ncn, corder)
                    k.set_lane(None)
                    recs = k.lanes.pop("u", [])
                    cut = k.marks[0] if k.marks else len(recs)
                    k.emit_interleaved(prev_scan, recs[:cut])
                    prev_scan = recs[cut:]
            k.emit_recs(prev_scan)
            k.end()

    def mixer_hgrn(self, li, nrow):
        self.hg_prep(li)
        self.phase_hg_proj(nrow)
        self.phase_hg_core(self.csm[:, 32:33])
        self.phase_out_proj(self.c_wo)


    def phase_gd_proj(self, j, nrow):
        k = self.k
        W = self.a_w[j]

        def setup(T0, tl):
            ctx = {}
            ctx["tm"] = [k.sb([128, 512], F32, "tmk") for _ in tl]
            for ti, (o, n) in enumerate(tl):
                k.dma(k.sp, ctx["tm"][ti][:, 0:n], self.tmask_rep[:, T0 + o:T0 + o + n], writes=[ctx["tm"][ti]])
            ctx["sg"] = [k.sb([128, 512], F32, "sg") for _ in range(2)]
            ctx["st"] = [k.sb([128, 512], BF16, "st") for _ in range(3)]
            ctx["z"] = [k.sb([128, 16], F32, "z") for _ in range(2)]
            ctx["bat"] = [k.sb([128, 32], F32, "bat") for _ in range(2)]
            na = k.sb([128, 16], F32, "negA")
            k.op(k.act, (lambda e: e.activation(na[:], self.asm[:, 0:16], AF.Exp)), reads=[self.asm], writes=[na])
            k.op(k.dve, (lambda e: e.tensor_scalar(na[:], na[:], -1.0, None, op0=ALU.mult)), reads=[na], writes=[na])
            ctx["negA"] = na
            ctx["i"] = 0
            return ctx

        def fm_pre(c):
            def post(ctx, pt, T0, ti, o, n):
                i = ctx["i"]
                ctx["i"] += 1
                st = ctx["st"][i % 3]
                k.op(k.dve, (lambda e: e.tensor_tensor(out=st[:, 0:n], in0=pt[:, 0:n], in1=ctx["tm"][ti][:, 0:n], op=ALU.mult)),
                     reads=[pt, ctx["tm"][ti]], writes=[st])
                k.dma(k.sp, self.pre[c * 128:(c + 1) * 128, T0 + o:T0 + o + n], st[:, 0:n], reads=[st])
            return post

        def fm_g(c):
            def post(ctx, pt, T0, ti, o, n):
                i = ctx["i"]
                ctx["i"] += 1
                st = ctx["st"][i % 3]
                k.op(k.act, (lambda e: e.activation(st[:, 0:n], pt[:, 0:n], AF.Silu)), reads=[pt], writes=[st])
                k.dma(k.sp, self.fmG[c * 128:(c + 1) * 128, T0 + o:T0 + o + n], st[:, 0:n], reads=[st])
            return post

        def tm_ba(ctx, pt, tok0, r):
            i = ctx["i"]
            ctx["i"] += 1
            z, bat = ctx["z"][i % 2], ctx["bat"][i % 2]
            blk = tok0 // 128
            assert tok0 % 128 == 0
            k.op(k.dve, (lambda e: e.tensor_tensor(out=z[0:r, :], in0=pt[0:r, 16:32], in1=self.asm[0:r, 16:32], op=ALU.add)),
                 reads=[pt, self.asm], writes=[z])
            k.op(k.act, (lambda e: e.activation(z[0:r, :], z[0:r, :], AF.Exp)), reads=[z], writes=[z])
            k.op(k.act, (lambda e: e.activation(z[0:r, :], z[0:r, :], AF.Ln, bias=self.one_col[0:r, 0:1])), reads=[z, self.one_col], writes=[z])
            k.op(k.dve, (lambda e: e.tensor_tensor(out=bat[0:r, 16:32], in0=z[0:r, :], in1=ctx["negA"][0:r, :], op=ALU.mult)),
                 reads=[z, ctx["negA"]], writes=[bat])
            k.op(k.act, (lambda e: e.activation(bat[0:r, 0:16], pt[0:r, 0:16], AF.Sigmoid)), reads=[pt], writes=[bat])
            k.op(k.dve, (lambda e: e.tensor_scalar(bat[0:r, 0:16], bat[0:r, 0:16], self.tmc[0:r, blk:blk + 1], None, op0=ALU.mult)),
                 reads=[bat, self.tmc], writes=[bat])
            k.dma(k.sp, self.ba_tm[tok0:tok0 + r, :], bat[0:r, :], reads=[bat])

        fm = []
        for c in range(24):
            fm.append((W[:, c * 128:(c + 1) * 128], fm_pre(c)))
        for c in range(8):
            fm.append((W[:, 3072 + c * 128:3072 + (c + 1) * 128], fm_g(c)))
        tm = [(W[:, 4096:4128], tm_ba, 32)]
        self.phase_proj(nrow, setup, fm, tm)

    def phase_gd_conv(self):
        k, T = self.k, self.T
        tiles = tiles_of(T, 512)
        QS = 128 ** -0.5
        k.begin()
        xin = [k.sb([128, 516], BF16, "cx") for _ in range(3)]
        acc = [k.sb([128, 512], F32, "cacc") for _ in range(2)]
        sv = [k.sb([128, 512], F32, "csv") for _ in range(2)]
        sq = [k.sb([128, 512], BF16, "csq") for _ in range(2)]
        rs = [k.sb([128, 512], F32, "crs") for _ in range(2)]
        ob = [k.sb([128, 512], BF16, "cob") for _ in range(3)]
        tmt = [k.sb([128, 512], F32, "ctm") for _ in range(2)]
        tb = [k.sb([128, 4, 128], BF16, "ctb") for _ in range(2)]
        pss = [k.ps([128, 512], F32, "cps") for _ in range(2)]
        ptb = [k.ps([128, 512], BF16, "cpt") for _ in range(2)]
        it = 0
        for ti, (t0, n) in enumerate(tiles):
            tmk = tmt[ti % 2]
            k.dma(k.sp, tmk[:, 0:n], self.tmask_rep[:, t0:t0 + n], writes=[tmk])
            for cc in range(24):
                kind = cc // 8
                x = xin[it % 3]
                a_, s_, q_, r_, o_ = acc[it % 2], sv[it % 2], sq[it % 2], rs[it % 2], ob[it % 3]
                ps = pss[it % 2]
                if it % 2 == 0:
                    k.merge()
                k.set_lane(it % 2)
                it += 1
                lo = max(t0 - 2, 0)
                hi = min(t0 + n + 2, T)
                if lo > t0 - 2 or hi < t0 + n + 2:
                    k.op(k.pool, (lambda e, x=x: e.memset(x[:], 0.0)), writes=[x])
                k.dma(k.sp, x[:, lo - (t0 - 2):hi - (t0 - 2)], self.pre[cc * 128:(cc + 1) * 128, lo:hi], writes=[x])
                wcol = lambda jj, cc=cc: self.asm[:, 33 + cc * 5 + jj:34 + cc * 5 + jj]
                k.op(k.dve, (lambda e, a_=a_, x=x, n=n, w=wcol(0): e.tensor_scalar(a_[:, 0:n], x[:, 0:n], w, None, op0=ALU.mult)),
                     reads=[x, self.asm], writes=[a_])
                for jj in range(1, 5):
                    k.op(k.dve, (lambda e, a_=a_, x=x, n=n, jj=jj, w=wcol(jj): e.scalar_tensor_tensor(
                        out=a_[:, 0:n], in0=x[:, jj:jj + n], scalar=w, in1=a_[:, 0:n], op0=ALU.mult, op1=ALU.add)),
                        reads=[x, a_, self.asm], writes=[a_])
                k.op(k.act, (lambda e, a_=a_, s_=s_, n=n: e.activation(s_[:, 0:n], a_[:, 0:n], AF.Silu)), reads=[a_], writes=[s_])
                if kind < 2:
                    k.op(k.act, (lambda e, s_=s_, q_=q_, n=n: e.activation(q_[:, 0:n], s_[:, 0:n], AF.Square)), reads=[s_], writes=[q_])
                    k.op(k.pe, (lambda e, ps=ps, q_=q_, n=n: e.matmul(ps[:, 0:n], lhsT=self.ones_bf[:], rhs=q_[:, 0:n], start=True, stop=True)),
                         reads=[self.ones_bf, q_], writes=[ps])
                    k.op(k.act, (lambda e, ps=ps, r_=r_, n=n: e.activation(r_[:, 0:n], ps[:, 0:n], AF.Ln, bias=self.eps_col[:, 0:1])),
                         reads=[ps, self.eps_col], writes=[r_])
                    k.op(k.act, (lambda e, r_=r_, n=n: e.activation(r_[:, 0:n], r_[:, 0:n], AF.Exp, scale=-0.5)), reads=[r_], writes=[r_])
                if kind == 0:
                    k.op(k.dve, (lambda e, o_=o_, s_=s_, r_=r_, n=n: e.scalar_tensor_tensor(
                        out=o_[:, 0:n], in0=s_[:, 0:n], scalar=QS, in1=r_[:, 0:n], op0=ALU.mult, op1=ALU.mult)),
                        reads=[s_, r_], writes=[o_])
                    k.dma(k.sp, self.fmA[cc * 128:(cc + 1) * 128, t0:t0 + n], o_[:, 0:n], reads=[o_])
                    k.set_lane(None)
                    continue
                if kind == 1:
                    k.op(k.dve, (lambda e, s_=s_, r_=r_, n=n: e.tensor_tensor(out=s_[:, 0:n], in0=s_[:, 0:n], in1=r_[:, 0:n], op=ALU.mult)),
                         reads=[s_, r_], writes=[s_])
                    k.op(k.pool, (lambda e, o_=o_, s_=s_, tmk=tmk, n=n: e.tensor_tensor(out=o_[:, 0:n], in0=s_[:, 0:n], in1=tmk[:, 0:n], op=ALU.mult)),
                         reads=[s_, tmk], writes=[o_])
                    k.dma(k.sp, self.fmB[0][(cc - 8) * 128:(cc - 7) * 128, t0:t0 + n], o_[:, 0:n], reads=[o_])
                    dst = self.tmK[0]
                else:
                    k.op(k.pool, (lambda e, o_=o_, s_=s_, n=n: e.tensor_copy(o_[:, 0:n], s_[:, 0:n])), reads=[s_], writes=[o_])
                    dst = self.v_tm
                cl = (cc % 8)
                pt = ptb[it % 2]
                tt = tb[it % 2]
                nb = (n + 127) // 128
                for b in range(nb):
                    r = min(128, n - b * 128)
                    k.op(k.pe, (lambda e, pt=pt, o_=o_, b=b, r=r: e.transpose(pt[0:r, b * 128:(b + 1) * 128], o_[:, b * 128:b * 128 + r],
                                                                             self.identb[:])),
                         reads=[o_, self.identb], writes=[pt])
                if n % 128 == 0:
                    k.op(k.act, (lambda e, pt=pt, tt=tt, nb=nb: e.copy(tt[:, 0:nb, :], pt[:, 0:nb * 128].rearrange("p (b d) -> p b d", d=128))),
                         reads=[pt], writes=[tt])
                    k.dma(k.sp, dst[t0:t0 + n, cl * 128:(cl + 1) * 128].rearrange("(b p) d -> p b d", p=128), tt[:, 0:nb, :], reads=[tt])
                else:
                    for b in range(nb):
                        r = min(128, n - b * 128)
                        k.op(k.act, (lambda e, pt=pt, tt=tt, b=b, r=r: e.copy(tt[0:r, b, :], pt[0:r, b * 128:(b + 1) * 128])),
                             reads=[pt], writes=[tt])
                        k.dma(k.sp, dst[t0 + b * 128:t0 + b * 128 + r, cl * 128:(cl + 1) * 128], tt[0:r, b, :], reads=[tt])
                k.set_lane(None)
        k.merge()
        k.end()

    def phase_gd_core(self):
        import os
        GDCUT = int(os.environ.get("GD_CUT", "99"))
        GDSUB = int(os.environ.get("GD_SUB", "99"))
        k, nc, T = self.k, self.nc, self.T
        tiles = tiles_of(T, 512)
        onorm_col = self.asm[:, 32:33]
        for sweep in (1, 0):
            d = sweep
            k.begin()
            tri = self.tri
            if d == 0:
                Uin, MS, MI, MN = tri[:, 0:64], 1, 0, 3
            else:
                Uin, MS, MI, MN = tri[:, 128:192], 3, 2, 1
            mI = k.sb([64, 512], F32, "mI")
            mS = k.sb([64, 512], F32, "mS")
            mN = k.sb([64, 512], F32, "mN")
            idr = k.sb([64, 512], F32, "idr")
            for c in range(8):
                cs = slice(c * 64, (c + 1) * 64)
                k.op(k.dve, (lambda e, cs=cs: e.tensor_copy(mI[:, cs], tri[:, MI * 64:(MI + 1) * 64])), reads=[tri], writes=[mI])
                k.op(k.dve, (lambda e, cs=cs: e.tensor_copy(mS[:, cs], tri[:, MS * 64:(MS + 1) * 64])), reads=[tri], writes=[mS])
                k.op(k.dve, (lambda e, cs=cs: e.tensor_copy(mN[:, cs], tri[:, MN * 64:(MN + 1) * 64])), reads=[tri], writes=[mN])
                k.op(k.dve, (lambda e, cs=cs: e.tensor_copy(idr[:, cs], self.ident[0:64, 0:64])), reads=[self.ident], writes=[idr])
            ones_f = k.sb([64, 128], F32, "onesf")
            k.op(k.dve, (lambda e: e.memset(ones_f[:], 1.0)), writes=[ones_f])
            S = [k.sb([128, 128], F32, "S") for _ in range(8)]
            Sb = [k.sb([128, 128], BF16, "Sb") for _ in range(8)]
            for h in range(8):
                k.op(k.dve, (lambda e, h=h: e.memset(S[h][:], 0.0)), writes=[S[h]])
                k.op(k.pool, (lambda e, h=h: e.memset(Sb[h][:], 0.0)), writes=[Sb[h]])
            NB = 2
            mk = lambda shp, dt, nm: [k.sb(shp, dt, nm) for _ in range(NB)]
            qT, kT = mk([128, 512], BF16, "qT"), mk([128, 512], BF16, "kT")
            ktm, vtm = mk([64, 8, 128], BF16, "ktm"), mk([64, 8, 128], BF16, "vtm")
            bat = mk([64, 8, 32], F32, "bat")
            lab = mk([64, 2, 8], F32, "lab")
            gc, egc, bek, dcol = mk([64, 8], F32, "gc"), mk([64, 8], F32, "egc"), mk([64, 8], F32, "bek"), mk([64, 8], F32, "dcol")
            alast = mk([128, 8], F32, "alast")
            dg = mk([64, 512], F32, "dg")
            egr = mk([128, 512], F32, "egr")
            qd = mk([128, 512], BF16, "qd")
            fab = mk([64, 512], F32, "fab")
            fm_ = mk([64, 512], F32, "fm")
            fmi = mk([64, 512], F32, "fmi")
            gf = mk([64, 512], F32, "gf")
            a32 = mk([64, 512], F32, "a32")
            Rb = [mk([64, 512], F32, "Rb%d" % i) for i in range(2)]
            Pb = [mk([64, 512], F32, "Pb%d" % i) for i in range(2)]
            PTb = [mk([64, 512], F32, "PTb%d" % i) for i in range(2)]
            rhu, rhw, kdec = mk([64, 8, 128], F32, "rhu"), mk([64, 8, 128], F32, "rhw"), mk([64, 8, 128], BF16, "kdec")
            gfu = mk([64, 512], F32, "gfu")
            fmS = mk([64, 512], F32, "fmS")
            fmN = mk([64, 512], F32, "fmN")
            dgb = mk([64, 512], F32, "dgb")
            u_sb = mk([64, 8, 128], F32, "u")
            nwT = mk([128, 512], BF16, "nwT")
            qkm = mk([64, 512], BF16, "qkm")
            vn = [[k.sb([64, 128], BF16, "vn") for _ in range(2)] for _ in range(NB)]
            ost = [k.sb([128, 512], F32, "ost") for _ in range(2)]
            B = [k.ps([128, 512], F32, "B%d" % i) for i in range(8)]
            if d == 0:
                obt = mk([128, 512], F32, "obt")
                gt = mk([128, 512], BF16, "gt")
                sqo_l = mk([128, 512], BF16, "sqo")
                rs_l = mk([128, 512], F32, "rs")
                fin = [k.sb([128, 512], BF16, "fin") for _ in range(2)]
            order = list(range(len(tiles)))
            if d == 1:
                order = order[::-1]

            def unit(h, b, it, t0, n, ncn, corder):
                if True:
                    rows = slice(h * 128, (h + 1) * 128)
                    X = B[4 * b:4 * b + 4]
                    Bs, Brow, BG, Bb = X
                    BD, BP, BT = X[1], X[2], X[3]
                    Bq, Bo, Bst, Bv = X[0], X[1], X[2], X[0]
                    k.dma(k.sp, qT[b][:, 0:n], self.fmA[rows, t0:t0 + n], writes=[qT[b]])
                    k.dma(k.sp, kT[b][:, 0:n], self.fmB[0][rows, t0:t0 + n], writes=[kT[b]])
                    k.dma(k.sp, ktm[b][:, 0:ncn, :], self.tmK[0][t0:t0 + n, rows].rearrange("(c p) e -> p c e", p=64), writes=[ktm[b]])
                    k.dma(k.sp, vtm[b][:, 0:ncn, :], self.v_tm[t0:t0 + n, rows].rearrange("(c p) e -> p c e", p=64), writes=[vtm[b]])
                    k.dma(k.sp, bat[b][:, 0:ncn, :], self.ba_tm[t0:t0 + n, :].rearrange("(c p) e -> p c e", p=64), writes=[bat[b]])
                    if d == 0:
                        k.dma(k.sp, obt[b][:, 0:n], self.obT[rows, t0:t0 + n], writes=[obt[b]])
                        k.dma(k.sp, gt[b][:, 0:n], self.fmG[rows, t0:t0 + n], writes=[gt[b]])
                    col = d * 8 + h
                    k.op(k.dve, (lambda e, b=b, ncn=ncn, col=col: e.tensor_copy(lab[b][:, 0, 0:ncn], bat[b][:, 0:ncn, col])),
                         reads=[bat[b]], writes=[lab[b]])
                    k.op(k.dve, (lambda e, b=b, ncn=ncn, col=col: e.tensor_copy(lab[b][:, 1, 0:ncn], bat[b][:, 0:ncn, 16 + col])),
                         reads=[bat[b]], writes=[lab[b]])
                    beta = lambda c, b=b: lab[b][:, 0, c:c + 1]
                    k.op(k.pe, (lambda e, b=b, ncn=ncn: e.matmul(Bs[0:64, 0:ncn], lhsT=Uin, rhs=lab[b][:, 1, 0:ncn], start=True, stop=True)),
                         reads=[tri, lab[b]], writes=[Bs])
                    k.op(k.pe, (lambda e, b=b, ncn=ncn: e.matmul(Bs[:, 16:16 + ncn], lhsT=ones_f[:], rhs=lab[b][:, 1, 0:ncn], start=True, stop=True)),
                         reads=[ones_f, lab[b]], writes=[Bs])
                    k.op(k.dve, (lambda e, b=b, ncn=ncn: e.tensor_copy(gc[b][:, 0:ncn], Bs[0:64, 0:ncn])), reads=[Bs], writes=[gc[b]])
                    k.op(k.act, (lambda e, b=b, ncn=ncn: e.activation(alast[b][:, 0:ncn], Bs[:, 16:16 + ncn], AF.Exp)), reads=[Bs], writes=[alast[b]])
                    k.op(k.dve, (lambda e, b=b, ncn=ncn: e.tensor_tensor(out=dcol[b][:, 0:ncn], in0=Bs[0:64, 16:16 + ncn], in1=gc[b][:, 0:ncn],
                                                                        op=ALU.subtract)), reads=[Bs, gc[b]], writes=[dcol[b]])
                    k.op(k.act, (lambda e, b=b, ncn=ncn: e.activation(dcol[b][:, 0:ncn], dcol[b][:, 0:ncn], AF.Exp)), reads=[dcol[b]], writes=[dcol[b]])
                    k.op(k.act, (lambda e, b=b, ncn=ncn: e.activation(egc[b][:, 0:ncn], gc[b][:, 0:ncn], AF.Exp)), reads=[gc[b]], writes=[egc[b]])
                    k.op(k.dve, (lambda e, b=b, ncn=ncn: e.tensor_tensor(out=bek[b][:, 0:ncn], in0=egc[b][:, 0:ncn], in1=lab[b][:, 0, 0:ncn],
                                                                        op=ALU.mult)), reads=[egc[b], lab[b]], writes=[bek[b]])
                    if GDCUT < 1:
                        return
                    k.op(k.dve, (lambda e, b=b, n=n, ncn=ncn: e.tensor_tensor(
                        out=dg[b][:, 0:n].rearrange("p (c i) -> p c i", i=64), in0=idr[:, 0:n].rearrange("p (c i) -> p c i", i=64),
                        in1=gc[b][:, 0:ncn].unsqueeze(2).to_broadcast([64, ncn, 64]), op=ALU.mult)), reads=[idr, gc[b]], writes=[dg[b]])
                    for c in range(ncn):
                        cs = slice(c * 64, (c + 1) * 64)
                        k.op(k.pe, (lambda e, b=b, cs=cs: e.matmul(Brow[:, cs], lhsT=ones_f[:], rhs=dg[b][:, cs], start=True, stop=True)),
                             reads=[ones_f, dg[b]], writes=[Brow])
                    k.op(k.act, (lambda e, b=b, n=n: e.activation(egr[b][:, 0:n], Brow[:, 0:n], AF.Exp)), reads=[Brow], writes=[egr[b]])
                    k.op(k.dve, (lambda e, b=b, n=n: e.tensor_tensor(out=qd[b][:, 0:n], in0=qT[b][:, 0:n], in1=egr[b][:, 0:n], op=ALU.mult)),
                         reads=[qT[b], egr[b]], writes=[qd[b]])
                    k.op(k.dve, (lambda e, b=b, n=n, ncn=ncn: e.tensor_tensor(
                        out=fab[b][:, 0:n].rearrange("p (c i) -> p c i", i=64), in0=Brow[0:64, 0:n].rearrange("p (c i) -> p c i", i=64),
                        in1=gc[b][:, 0:ncn].unsqueeze(2).to_broadcast([64, ncn, 64]), op=ALU.subtract)), reads=[Brow, gc[b]], writes=[fab[b]])
                    k.op(k.act, (lambda e, b=b, n=n: e.activation(fab[b][:, 0:n], fab[b][:, 0:n], AF.Abs)), reads=[fab[b]], writes=[fab[b]])
                    k.op(k.act, (lambda e, b=b, n=n: e.activation(fm_[b][:, 0:n], fab[b][:, 0:n], AF.Exp, scale=-1.0)), reads=[fab[b]], writes=[fm_[b]])
                    k.op(k.pool, (lambda e, b=b, n=n: e.tensor_tensor(out=fmi[b][:, 0:n], in0=fm_[b][:, 0:n], in1=mI[:, 0:n], op=ALU.mult)),
                         reads=[fm_[b], mI], writes=[fmi[b]])
                    k.op(k.pool, (lambda e, b=b, n=n: e.tensor_tensor(out=fmS[b][:, 0:n], in0=fm_[b][:, 0:n], in1=mS[:, 0:n], op=ALU.mult)),
                         reads=[fm_[b], mS], writes=[fmS[b]])
                    k.op(k.pool, (lambda e, b=b, n=n: e.tensor_tensor(out=fmN[b][:, 0:n], in0=fm_[b][:, 0:n], in1=mN[:, 0:n], op=ALU.mult)),
                         reads=[fm_[b], mN], writes=[fmN[b]])
                    if GDCUT < 2:
                        return
                    k.op(k.dve, (lambda e, b=b, n=n, ncn=ncn: e.tensor_tensor(
                        out=dgb[b][:, 0:n].rearrange("p (c i) -> p c i", i=64), in0=idr[:, 0:n].rearrange("p (c i) -> p c i", i=64),
                        in1=lab[b][:, 0, 0:ncn].unsqueeze(2).to_broadcast([64, ncn, 64]), op=ALU.mult)), reads=[idr, lab[b]], writes=[dgb[b]])
                    for c in range(ncn):
                        cs = slice(c * 64, (c + 1) * 64)
                        k.op(k.pe, (lambda e, b=b, cs=cs: e.matmul(BG[0:64, cs], lhsT=kT[b][:, cs], rhs=kT[b][:, cs], start=True, stop=True)),
                             reads=[kT[b]], writes=[BG])
                    for c in range(ncn):
                        cs = slice(c * 64, (c + 1) * 64)
                        k.op(k.pe, (lambda e, b=b, cs=cs: e.matmul(Bb[:, cs], lhsT=ones_f[:], rhs=dgb[b][:, cs], start=True, stop=True)),
                             reads=[ones_f, dgb[b]], writes=[Bb])
                    k.op(k.dve, (lambda e, b=b, n=n: e.tensor_tensor(out=gf[b][:, 0:n], in0=BG[0:64, 0:n], in1=fmS[b][:, 0:n], op=ALU.mult)),
                         reads=[BG, fmS[b]], writes=[gf[b]])
                    k.op(k.dve, (lambda e, b=b, n=n: e.tensor_tensor(out=gfu[b][:, 0:n], in0=BG[0:64, 0:n], in1=fmN[b][:, 0:n], op=ALU.mult)),
                         reads=[BG, fmN[b]], writes=[gfu[b]])
                    R0, P0, PT0 = Rb[0][b], Pb[0][b], PTb[0][b]
                    k.op(k.dve, (lambda e, b=b, n=n, ncn=ncn, PT0=PT0: e.tensor_tensor(
                        out=PT0[:, 0:n].rearrange("p (c i) -> p c i", i=64), in0=gf[b][:, 0:n].rearrange("p (c i) -> p c i", i=64),
                        in1=lab[b][:, 0, 0:ncn].unsqueeze(2).to_broadcast([64, ncn, 64]), op=ALU.mult)), reads=[gf[b], lab[b]], writes=[PT0])
                    k.op(k.dve, (lambda e, b=b, n=n, P0=P0: e.tensor_tensor(out=P0[:, 0:n], in0=Bb[0:64, 0:n], in1=gfu[b][:, 0:n], op=ALU.mult)),
                         reads=[Bb, gfu[b]], writes=[P0])
                    k.op(k.pool, (lambda e, n=n, R0=R0, P0=P0: e.tensor_tensor(out=R0[:, 0:n], in0=idr[:, 0:n], in1=P0[:, 0:n], op=ALU.subtract)),
                         reads=[P0, idr], writes=[R0])
                    if GDCUT < 3:
                        return
                    for lv in range(6):
                        cur, nxt = lv % 2, (lv + 1) % 2
                        Pc, PTc = Pb[cur][b], PTb[cur][b]
                        Pn, PTn = Pb[nxt][b], PTb[nxt][b]
                        Rc, Rn = Rb[(lv + 1) % 2][b], Rb[lv % 2][b]
                        for c in range(ncn):
                            cs = slice(c * 64, (c + 1) * 64)
                            if lv >= 1:
                                k.op(k.pe, (lambda e, cs=cs, PTc=PTc, Rc=Rc: e.matmul(BD[0:64, cs], lhsT=PTc[:, cs], rhs=Rc[:, cs], start=True, stop=True)),
                                     reads=[PTc, Rc], writes=[BD])
                            if lv <= 4:
                                k.op(k.pe, (lambda e, cs=cs, PTc=PTc, Pc=Pc: e.matmul(BP[0:64, cs], lhsT=PTc[:, cs], rhs=Pc[:, cs], start=True, stop=True)),
                                     reads=[PTc, Pc], writes=[BP])
                                k.op(k.pe, (lambda e, cs=cs, PTc=PTc, Pc=Pc: e.matmul(BT[0:64, cs], lhsT=Pc[:, cs], rhs=PTc[:, cs], start=True, stop=True)),
                                     reads=[PTc, Pc], writes=[BT])
                        if lv >= 1:
                            k.op(k.dve, (lambda e, n=n, Rn=Rn, Rc=Rc: e.tensor_tensor(out=Rn[:, 0:n], in0=BD[0:64, 0:n], in1=Rc[:, 0:n], op=ALU.add)),
                                 reads=[BD, Rc], writes=[Rn])
                        if lv <= 4:
                            k.op(k.act, (lambda e, n=n, Pn=Pn: e.copy(Pn[:, 0:n], BP[0:64, 0:n])), reads=[BP], writes=[Pn])
                            k.op(k.act, (lambda e, n=n, PTn=PTn: e.copy(PTn[:, 0:n], BT[0:64, 0:n])), reads=[BT], writes=[PTn])
                    if GDCUT < 4:
                        return
                    Rf = Rb[1][b]
                    k.op(k.dve, (lambda e, b=b, ncn=ncn: e.tensor_tensor(out=rhu[b][:, 0:ncn, :], in0=vtm[b][:, 0:ncn, :],
                                                                        in1=lab[b][:, 0, 0:ncn].unsqueeze(2).to_broadcast([64, ncn, 128]), op=ALU.mult)),
                         reads=[vtm[b], lab[b]], writes=[rhu[b]])
                    k.op(k.pool, (lambda e, b=b, ncn=ncn: e.tensor_tensor(out=rhw[b][:, 0:ncn, :], in0=ktm[b][:, 0:ncn, :],
                                                                         in1=bek[b][:, 0:ncn].unsqueeze(2).to_broadcast([64, ncn, 128]), op=ALU.mult)),
                         reads=[ktm[b], bek[b]], writes=[rhw[b]])
                    k.op(k.pool, (lambda e, b=b, ncn=ncn: e.tensor_tensor(out=kdec[b][:, 0:ncn, :], in0=ktm[b][:, 0:ncn, :],
                                                                         in1=dcol[b][:, 0:ncn].unsqueeze(2).to_broadcast([64, ncn, 128]), op=ALU.mult)),
                         reads=[ktm[b], dcol[b]], writes=[kdec[b]])
                    for c in range(ncn):
                        cs = slice(c * 64, (c + 1) * 64)
                        pu = BD if c < 4 else BP
                        us = slice((c % 4) * 128, (c % 4 + 1) * 128)
                        k.op(k.pe, (lambda e, b=b, c=c, cs=cs, pu=pu, us=us, Rf=Rf: e.matmul(pu[0:64, us], lhsT=Rf[:, cs], rhs=rhu[b][:, c, :], start=True, stop=True)),
                             reads=[Rf, rhu[b]], writes=[pu])
                        k.op(k.pe, (lambda e, b=b, c=c, cs=cs, Rf=Rf: e.matmul(BT[:, cs], lhsT=rhw[b][:, c, :], rhs=Rf[:, cs], start=True, stop=True)),
                             reads=[Rf, rhw[b]], writes=[BT])
                        k.op(k.pe, (lambda e, b=b, cs=cs: e.matmul(Bq[0:64, cs], lhsT=kT[b][:, cs], rhs=qT[b][:, cs], start=True, stop=True)),
                             reads=[kT[b], qT[b]], writes=[Bq])
                    n4 = min(ncn, 4)
                    k.op(k.act, (lambda e, b=b, n4=n4: e.copy(u_sb[b][:, 0:n4, :], BD[0:64, 0:n4 * 128].rearrange("p (c d) -> p c d", d=128))),
                         reads=[BD], writes=[u_sb[b]])
                    if ncn > 4:
                        k.op(k.act, (lambda e, b=b, ncn=ncn: e.copy(u_sb[b][:, 4:ncn, :], BP[0:64, 0:(ncn - 4) * 128].rearrange("p (c d) -> p c d", d=128))),
                             reads=[BP], writes=[u_sb[b]])
                    k.op(k.dve, (lambda e, b=b, n=n: e.tensor_scalar(nwT[b][:, 0:n], BT[:, 0:n], -1.0, None, op0=ALU.mult)), reads=[BT], writes=[nwT[b]])
                    k.op(k.dve, (lambda e, b=b, n=n: e.tensor_tensor(out=qkm[b][:, 0:n], in0=Bq[0:64, 0:n], in1=fmi[b][:, 0:n], op=ALU.mult)),
                         reads=[Bq, fmi[b]], writes=[qkm[b]])
                    if GDCUT < 5:
                        return
                    k.mark()
                    for c in corder:
                        cs = slice(c * 64, (c + 1) * 64)
                        v_ = vn[b][c % 2]
                        k.op(k.pe, (lambda e, b=b, cs=cs, h=h: e.matmul(Bv[0:64, 256:384], lhsT=nwT[b][:, cs], rhs=Sb[h][:], start=True, stop=True)),
                             reads=[nwT[b], Sb[h]], writes=[Bv])
                        k.op(k.dve, (lambda e, b=b, c=c, v_=v_: e.tensor_tensor(out=v_[:], in0=Bv[0:64, 256:384], in1=u_sb[b][:, c, :], op=ALU.add)),
                             reads=[Bv, u_sb[b]], writes=[v_])
                        k.op(k.pe, (lambda e, b=b, cs=cs, h=h: e.matmul(Bo[:, cs], lhsT=Sb[h][:], rhs=qd[b][:, cs], start=True, stop=False)),
                             reads=[Sb[h], qd[b]], writes=[Bo])
                        k.op(k.pe, (lambda e, b=b, cs=cs, v_=v_: e.matmul(Bo[:, cs], lhsT=v_[:], rhs=qkm[b][:, cs], start=False, stop=True)),
                             reads=[v_, qkm[b]], writes=[Bo])
                        k.op(k.pe, (lambda e, b=b, c=c, v_=v_: e.matmul(Bs[:, 128:256], lhsT=kdec[b][:, c, :], rhs=v_[:], start=True, stop=True)),
                             reads=[kdec[b], v_], writes=[Bs])
                        k.op(k.dve, (lambda e, b=b, h=h, c=c: e.scalar_tensor_tensor(
                            out=S[h][:], in0=S[h][:], scalar=alast[b][:, c:c + 1], in1=Bs[:, 128:256], op0=ALU.mult, op1=ALU.add)),
                            reads=[S[h], alast[b], Bs], writes=[S[h]])
                        k.op(k.act, (lambda e, h=h: e.copy(Sb[h][:], S[h][:])), reads=[S[h]], writes=[Sb[h]])
                    if GDCUT < 6:
                        return
                    os_ = ost[it % 2]
                    if d == 1:
                        k.op(k.act, (lambda e, os_=os_, n=n: e.copy(os_[:, 0:n], Bo[:, 0:n])), reads=[Bo], writes=[os_])
                        k.dma(k.sp, self.obT[rows, t0:t0 + n], os_[:, 0:n], reads=[os_])
                    else:
                        sqo, rs = sqo_l[b], rs_l[b]
                        k.op(k.dve, (lambda e, os_=os_, b=b, n=n: e.tensor_tensor(out=os_[:, 0:n], in0=Bo[:, 0:n], in1=obt[b][:, 0:n], op=ALU.add)),
                             reads=[Bo, obt[b]], writes=[os_])
                        k.op(k.act, (lambda e, os_=os_, n=n: e.activation(sqo[:, 0:n], os_[:, 0:n], AF.Square)), reads=[os_], writes=[sqo])
                        k.op(k.pe, (lambda e, n=n: e.matmul(Bst[:, 0:n], lhsT=self.ones_bf[:], rhs=sqo[:, 0:n], start=True, stop=True)),
                             reads=[self.ones_bf, sqo], writes=[Bst])
                        k.op(k.act, (lambda e, n=n: e.activation(rs[:, 0:n], Bst[:, 0:n], AF.Ln, bias=self.eps_col[:, 0:1], scale=1.0 / 128)),
                             reads=[Bst, self.eps_col], writes=[rs])
                        k.op(k.act, (lambda e, n=n: e.activation(rs[:, 0:n], rs[:, 0:n], AF.Exp, scale=-0.5)), reads=[rs], writes=[rs])
                        k.op(k.dve, (lambda e, os_=os_, n=n: e.scalar_tensor_tensor(out=os_[:, 0:n], in0=os_[:, 0:n], scalar=onorm_col,
                                                                                   in1=rs[:, 0:n], op0=ALU.mult, op1=ALU.mult)),
                             reads=[os_, rs, self.asm], writes=[os_])
                        fo = fin[it % 2]
                        k.op(k.pool, (lambda e, os_=os_, fo=fo, b=b, n=n: e.tensor_tensor(out=fo[:, 0:n], in0=os_[:, 0:n], in1=gt[b][:, 0:n], op=ALU.mult)),
                             reads=[os_, gt[b]], writes=[fo])
                        k.dma(k.sp, self.oT[rows, t0:t0 + n], fo[:, 0:n], reads=[fo])
            uid = 0
            prev_scan = []
            for ti in order:
                t0, n = tiles[ti]
                ncn = n // 64
                corder = list(range(ncn)) if d == 0 else list(range(ncn))[::-1]
                for h in range(8):
                    b = uid % NB
                    uid += 1
                    k.marks = []
                    k.set_lane("u")
                    unit(h, b, uid, t0, n, ncn, corder)
                    k.set_lane(None)
                    recs = k.lanes.pop("u", [])
                    cut = k.marks[0] if k.marks else len(recs)
                    k.emit_interleaved(prev_scan, recs[:cut])
                    prev_scan = recs[cut:]
            k.emit_recs(prev_scan)
            if d == 1 and os.environ.get("DEBUG_DUMP"):
                o = self.nc.dram_tensor("dbg_S0", [128, 128], F32, kind="ExternalOutput").ap()
                k.dma(k.sp, o, S[0][:], reads=[S[0]])
                o = self.nc.dram_tensor("dbg_u", [64, 8 * 128], F32, kind="ExternalOutput").ap()
                k.dma(k.sp, o, u_sb[0][:].rearrange("p c d -> p (c d)"), reads=[u_sb[0]])
                o = self.nc.dram_tensor("dbg_rhu", [64, 8 * 128], BF16, kind="ExternalOutput").ap()
                k.dma(k.sp, o, rhu[0][:].rearrange("p c d -> p (c d)"), reads=[rhu[0]])
                o = self.nc.dram_tensor("dbg_R", [64, 512], BF16, kind="ExternalOutput").ap()
                k.dma(k.sp, o, Rb[0][0][:], reads=[Rb[0][0]])
                o = self.nc.dram_tensor("dbg_alast", [128, 8], F32, kind="ExternalOutput").ap()
                k.dma(k.sp, o, alast[0][:], reads=[alast[0]])
            k.end()

    def mixer_gdn(self, j, nrow):
        k = self.k
        k.begin()
        k.dma(k.sp, self.asm[:], self.a_small[j], writes=[self.asm])
        k.end()
        import os
        st = int(os.environ.get("GD_STAGE", "9"))
        self.phase_gd_proj(j, nrow)
        if st >= 2:
            self.phase_gd_conv()
        if st >= 3:
            self.phase_gd_core()
        if st >= 4:
            self.phase_out_proj(self.a_wo[j])

    def build(self):
        k = self.k
        self.eps_col = k.gsb([128, 1], F32, "epscol")
        k.begin()
        k.op(k.dve, lambda e: e.memset(self.eps_col[:], EPS), writes=[self.eps_col])
        self.bsm = k.gsb([128, 260], F32, "bsm")
        k.dma(k.sp, self.bsm[:], self.b_small, writes=[self.bsm])
        self.csm = k.gsb([128, 33], F32, "csm")
        k.dma(k.sp, self.csm[:], self.c_small, writes=[self.csm])
        self.tri = k.gsb([64, 256], F32, "tric")
        k.dma(k.sp, self.tri[:], self.tri_in, writes=[self.tri])
        self.tmc = k.gsb([128, (self.T + 127) // 128], F32, "tmc")
        k.dma(k.sp, self.tmc[:], self.tmask_col, writes=[self.tmc])
        self.one_col = k.gsb([128, 1], F32, "onecol")
        k.op(k.dve, lambda e: e.memset(self.one_col[:], 1.0), writes=[self.one_col])
        self.lb_col = k.gsb([128, 16], F32, "lbcol")
        self.asm = k.gsb([128, 153], F32, "asm")
        self.identb = k.gsb([128, 128], BF16, "identbc")
        k.dma(k.sp, self.identb[:], self.identb_in, writes=[self.identb])
        self.oml_row = k.gsb([128, D_MODEL], F32, "omlrow")
        k.end()
        self.phase_in()
        if self.only == "gdn":
            self.mixer_gdn(0, 0 * 3 + 1)
            self.phase_out(self.depth * 3)
            import os
            if os.environ.get("DEBUG_DUMP"):
                k.begin()
                dummy = k.sb([128, 4], F32, "dummy")
                for nm, ap, shp, dt in (("fmA", self.fmA, [D_MODEL, self.T], BF16), ("fmB0", self.fmB[0], [D_MODEL, self.T], BF16),
                                        ("tmK0", self.tmK[0], [self.T, D_MODEL], BF16), ("v_tm", self.v_tm, [self.T, D_MODEL], BF16),
                                        ("ba_tm", self.ba_tm, [self.T, 32], F32), ("obT", self.obT, [D_MODEL, self.T], F32),
                                        ("oT", self.oT, [D_MODEL, self.T], BF16), ("fmG", self.fmG, [D_MODEL, self.T], BF16)):
                    o = self.nc.dram_tensor("dbg_" + nm, shp, dt, kind="ExternalOutput").ap()
                    k.dma(k.sp, o, ap, reads=[dummy])
                k.end()
            return
        if self.only == "hgrn":
            self.mixer_hgrn(2, 2 * 3 + 1)
            self.phase_out(self.depth * 3)
            return
        if self.only == "attn":
            self.mixer_attn(1 * 3 + 1)
            self.phase_out(self.depth * 3)
            return
        for li in range(self.depth):
            if self.ffn:
                self.phase_ffn(li, 0, li * 3 + 0)
            if self.mixers and li % 3 == 1:
                self.mixer_attn(li * 3 + 1)
            if self.mixers and li % 3 == 2:
                self.mixer_hgrn(li, li * 3 + 1)
            if self.mixers and li % 3 == 0:
                self.mixer_gdn(li // 3, li * 3 + 1)
            if self.ffn:
                self.phase_ffn(li, 1, li * 3 + 2)
        self.phase_out(self.depth * 3)


def col_layout(a):
    a = np.asarray(a, np.float32)
    R = a.shape[0]
    C = a.shape[1] // 128
    return np.ascontiguousarray(a.reshape(R, C, 128).transpose(2, 0, 1).reshape(128, R * C))


def attn_host_inputs(b_w_in, b_lambda, b_sub_norm, layer_idx, T, n_valid):
    f32 = np.float32
    w = np.asarray(b_w_in, f32)
    perm = np.arange(2048)
    d = perm % 64
    perm = np.where(d < 8, perm + 8, np.where(d < 16, perm - 8, perm))
    w_sw = np.ascontiguousarray(w[:, perm])
    sm = np.zeros((128, 260), f32)
    sm[:, 0:256] = np.asarray(b_lambda, f32).reshape(1, 256)
    sm[:, 256] = np.asarray(b_sub_norm, f32).reshape(128)
    dd = np.arange(128) % 64
    ii = np.where(dd < 8, dd, dd - 8).astype(np.float64)
    invf = np.exp(-np.log(500000.0) * ii / 8.0)
    sm[:, 257] = np.where(dd < 16, invf, 0.0)
    sm[:, 258] = np.where(dd < 8, -1.0, np.where(dd < 16, 1.0, 0.0))
    sm[:, 259] = 0.8 - 0.6 * np.exp(-0.3 * layer_idx)
    nkt = (T + 127) // 128
    tok = np.arange(nkt * 128)
    kb = np.where(tok < n_valid, 0.0, -30000.0).astype(f32).reshape(nkt, 128).T
    return w_sw, sm, np.ascontiguousarray(kb)


def common_host_inputs(T, n_valid):
    f32 = np.float32
    j = np.arange(64)[:, None]
    i = np.arange(64)[None, :]
    tri = np.concatenate([(j <= i), (j > i), (j >= i), (j < i)], axis=1).astype(f32)
    tok = np.arange(T)
    tm = (tok < n_valid).astype(f32)
    tmask_rep = np.ascontiguousarray(np.broadcast_to(tm[None, :], (128, T)))
    nb = (T + 127) // 128
    tmp = np.zeros(nb * 128, f32)
    tmp[:T] = tm
    tmask_col = np.ascontiguousarray(tmp.reshape(nb, 128).T)
    return {"tri": tri, "tmask_rep": tmask_rep, "tmask_col": tmask_col, "ident": np.eye(128, dtype=f32)}


def hgrn_host_inputs(c_lb_logits, c_o_norm):
    sm = np.zeros((128, 33), np.float32)
    sm[:, 0:32] = col_layout(np.asarray(c_lb_logits, np.float32))
    sm[:, 32] = np.asarray(c_o_norm, np.float32).reshape(128)
    return sm


def gdn_host_inputs(a_log, a_dt_bias, a_o_norm, a_conv_w):
    f32 = np.float32
    n_a = np.asarray(a_log).shape[0]
    out = np.zeros((n_a, 128, 153), f32)
    for j in range(n_a):
        out[j, :, 0:16] = np.asarray(a_log[j], f32).reshape(1, 16)
        out[j, :, 16:32] = np.asarray(a_dt_bias[j], f32).reshape(1, 16)
        out[j, :, 32] = np.asarray(a_o_norm[j], f32).reshape(128)
        cw = np.asarray(a_conv_w[j], f32)
        out[j, :, 33:153] = cw.reshape(5, 24, 128).transpose(2, 1, 0).reshape(128, 120)
    return out


_PROG_CACHE = {}


def get_prog(T, depth, n_groups):
    key = (T, depth, n_groups)
    if key not in _PROG_CACHE:
        _PROG_CACHE[key] = Prog(T, depth, n_groups)
    return _PROG_CACHE[key]


T_FULL = 8256
SEQ_P = 8192
SEQ_S = 4096


def kernel(x_prompt, x_sample, meta_tokens, norm_w, ffn_w_up, ffn_w_down,
           a_w_in, a_conv_w, a_log, a_dt_bias, a_o_norm, a_w_out,
           b_w_in, b_lambda, b_sub_norm, b_w_out,
           c_w_in, c_lb_logits, c_o_norm, c_w_out, final_norm):
    import ml_dtypes
    depth = norm_w.shape[0]
    T = T_FULL
    prog = get_prog(T, depth, 4)
    f32 = np.float32
    seqs = [x_prompt[0], x_prompt[1], x_sample[0], x_sample[1], x_sample[2], x_sample[3], x_sample[0], x_sample[1]]
    nw = np.concatenate([np.asarray(norm_w, f32).reshape(depth * 3, D_MODEL), np.asarray(final_norm, f32)[None]], 0)
    nw = col_layout(nw)
    shared = {
        "norm_w": nw,
        "ffn_w_up": np.asarray(ffn_w_up, f32), "ffn_w_down": np.asarray(ffn_w_down, f32),
        "b_w_in": np.asarray(b_w_in[0], f32), "b_w_out": np.asarray(b_w_out[0], f32),
        "c_w_in": np.asarray(c_w_in[0], f32), "c_w_out": np.asarray(c_w_out[0], f32),
        "c_small": hgrn_host_inputs(c_lb_logits, c_o_norm[0]),
        "a_w_in": np.asarray(a_w_in, f32), "a_w_out": np.asarray(a_w_out, f32),
        "a_small": gdn_host_inputs(a_log, a_dt_bias, a_o_norm, a_conv_w),
        "identb": np.eye(128, dtype=f32).astype(ml_dtypes.bfloat16),
    }
    in_maps = []
    per_len = {}
    for s in seqs:
        L = s.shape[0]
        nv = N_META + L
        if L not in per_len:
            w_sw, sm, kb = attn_host_inputs(b_w_in[0], b_lambda[0], b_sub_norm[0], 1, T, nv)
            d = {"b_w_sw": w_sw, "b_small": sm, "kbias": kb}
            d.update(common_host_inputs(T, nv))
            per_len[L] = d
        xin = np.zeros((T, D_MODEL), f32)
        xin[:N_META] = meta_tokens
        xin[N_META:nv] = s
        m = {"xin": xin}
        m.update(shared)
        m.update(per_len[L])
        in_maps.append(m)
    res = run_bass_kernel_spmd(prog.nc, in_maps, core_ids=list(range(8)))
    outs = [r["yout"] for r in res.results]
    y_prompt = np.stack([outs[0][N_META:N_META + SEQ_P], outs[1][N_META:N_META + SEQ_P]], 0).astype(f32)
    y_sample = np.stack([outs[i][N_META:N_META + SEQ_S] for i in range(2, 6)], 0).astype(f32)
    return (y_prompt, y_sample)
```

```python
import numpy as np
from contextlib import ExitStack
import concourse.bass as bass
import concourse.mybir as mybir
from concourse.bass_utils import run_bass_kernel_spmd

F32 = mybir.dt.float32
BF16 = mybir.dt.bfloat16
ALU = mybir.AluOpType
AF = mybir.ActivationFunctionType
AX = mybir.AxisListType

D_MODEL = 1024
D_FF = 2816
N_META = 16
EPS = 1e-6


class Eng:
    def __init__(self, name, sem):
        self.name = name
        self.sem = sem
        self.cnt = 0
        self.seen = {}
        self.ops = []


class Tl:
    def __init__(self, t, name):
        self.t = t
        self.name = name
        self.w = None
        self.r = {}
        self.dsem = None

    def __getitem__(self, idx):
        return self.t[idx]

    def view(self):
        return Tl(self.t, self.name)


class KB:
    SAME_ENGINE_SYNC = True

    def __init__(self, nc, n_dma_sems=84):
        self.nc = nc
        self.es = ExitStack()
        self.engs = {}
        for name in ("pe", "act", "dve", "pool", "sp"):
            sem = self.es.enter_context(nc.semaphore("s_" + name))
            self.engs[name] = Eng(name, sem)
        self.pe, self.act, self.dve, self.pool, self.sp = (self.engs[n] for n in ("pe", "act", "dve", "pool", "sp"))
        self.dma_pool = []
        for i in range(n_dma_sems):
            sem = self.es.enter_context(nc.semaphore("s_d%d" % i))
            self.dma_pool.append([sem, 0])
        self.dma_free = list(range(n_dma_sems))
        self.phase_tiles = []
        self.dma_tiles = []
        self.lanes = {}
        self.marks = []
        self.cur_lane = None
        self.pes = None
        self.uid = 0
        self.nops = 0

    def begin(self):
        self.pes = ExitStack()
        self.phase_tiles = []

    def sb(self, shape, dt, name=None):
        self.uid += 1
        name = "%s_%d" % (name or "t", self.uid)
        t = self.pes.enter_context(self.nc.sbuf_tensor(name, list(shape), dt))
        tl = Tl(t, name)
        self.phase_tiles.append(tl)
        return tl

    def views(self, tl, n):
        vs = [tl.view() for _ in range(n)]
        self.phase_tiles.extend(vs)
        return vs

    def ps(self, shape, dt=F32, name=None):
        self.uid += 1
        name = "%s_%d" % (name or "p", self.uid)
        t = self.pes.enter_context(self.nc.psum_tensor(name, list(shape), dt))
        tl = Tl(t, name)
        self.phase_tiles.append(tl)
        return tl

    def gsb(self, shape, dt, name):
        t = self.es.enter_context(self.nc.sbuf_tensor(name, list(shape), dt))
        return Tl(t, name)

    def _deps(self, eng, reads, writes):
        deps = []
        for tl in reads:
            if tl.w is not None:
                deps.append(tl.w)
        for tl in writes:
            if tl.w is not None and tl.w[0] != eng.name:
                deps.append(tl.w)
            deps.extend(v for v in tl.r.values() if v[0] != eng.name)
        waits = []
        for key, sem, cnt in deps:
            if key == eng.name and (eng.name in ("pe", "sp") or not self.SAME_ENGINE_SYNC):
                continue
            if eng.seen.get(key, 0) < cnt:
                eng.seen[key] = cnt
                waits.append((sem, cnt))
        return waits

    def set_lane(self, lane):
        self.cur_lane = lane

    def merge(self):
        lanes = [v for _, v in sorted(self.lanes.items()) if v]
        self.lanes = {}
        save, self.cur_lane = self.cur_lane, None
        idx = [0] * len(lanes)
        left = sum(len(l) for l in lanes)
        while left:
            for li, l in enumerate(lanes):
                if idx[li] < len(l):
                    rec = l[idx[li]]
                    idx[li] += 1
                    left -= 1
                    if rec[0] == "op":
                        self.op(*rec[1:])
                    else:
                        self.dma(rec[1], rec[2], rec[3], rec[4], rec[5], **rec[6])
        self.cur_lane = save

    def mark(self):
        self.marks.append(len(self.lanes.get(self.cur_lane, [])))

    def emit_recs(self, recs):
        for rec in recs:
            if rec[0] == "op":
                self.op(*rec[1:])
            else:
                self.dma(rec[1], rec[2], rec[3], rec[4], rec[5], **rec[6])

    def emit_interleaved(self, a, b):
        save, self.cur_lane = self.cur_lane, None
        na, nb = len(a), len(b)
        ia = ib = 0
        while ia < na or ib < nb:
            if ib >= nb or (ia < na and ia * nb <= ib * na):
                self.emit_recs([a[ia]])
                ia += 1
            else:
                self.emit_recs([b[ib]])
                ib += 1
        self.cur_lane = save

    def op(self, eng, fn, reads=(), writes=()):
        if self.cur_lane is not None:
            self.lanes.setdefault(self.cur_lane, []).append(("op", eng, fn, tuple(reads), tuple(writes)))
            return
        waits = self._deps(eng, reads, writes)
        eng.cnt += 1
        me = (eng.name, eng.sem, eng.cnt)
        eng.ops.append((waits, fn, (eng.sem, 1)))
        for tl in writes:
            tl.w = me
            tl.r = {}
        for tl in reads:
            if tl not in writes:
                tl.r[eng.name] = me
        self.nops += 1

    def dma(self, q, out, in_, reads=(), writes=(), **kw):
        if self.cur_lane is not None:
            self.lanes.setdefault(self.cur_lane, []).append(("dma", q, out, in_, tuple(reads), tuple(writes), kw))
            return
        tl = (list(writes) + list(reads))[0]
        if tl.dsem is None:
            tl.dsem = self.dma_free.pop()
            self.dma_tiles.append(tl)
        slot = self.dma_pool[tl.dsem]
        waits = self._deps(q, reads, writes)
        slot[1] += 16
        key = "d%d" % tl.dsem
        me = (key, slot[0], slot[1])
        q.ops.append((waits, (lambda e, o=out, i=in_, k=kw: e.dma_start(out=o, in_=i, **k)), (slot[0], 16)))
        for t in writes:
            t.w = me
            t.r = {}
        for t in reads:
            t.r[key] = me
        self.nops += 1

    def end(self):
        used = set()
        for tl in self.dma_tiles:
            if tl.dsem is not None:
                used.add(tl.dsem)
        for d in sorted(used):
            sem, cnt = self.dma_pool[d]
            key = "d%d" % d
            if self.sp.seen.get(key, 0) < cnt:
                self.sp.seen[key] = cnt
                self.sp.ops.append(([(sem, cnt)], None, None))
        for e in (self.pe, self.act, self.dve, self.pool):
            if self.sp.seen.get(e.name, 0) < e.cnt:
                self.sp.seen[e.name] = e.cnt
                self.sp.ops.append(([(e.sem, e.cnt)], None, None))
        self.sp.cnt += 1
        self.sp.ops.append(([], (lambda e: e.nop()), (self.sp.sem, 1)))
        for e in (self.pe, self.act, self.dve, self.pool):
            e.ops.append(([(self.sp.sem, self.sp.cnt)], None, None))
        with self.nc.Block() as block:
            for name, deco in (("pe", block.tensor), ("act", block.scalar), ("dve", block.vector),
                               ("pool", block.gpsimd), ("sp", block.sync)):
                eng = self.engs[name]
                ops = eng.ops
                eng.ops = []

                def body(e, ops=ops):
                    for waits, fn, inc in ops:
                        for sem, cnt in waits:
                            e.wait_ge(sem, cnt)
                        if fn is not None:
                            ins = fn(e)
                            if inc is not None:
                                ins.then_inc(inc[0], inc[1])

                deco(body)
        for d in used:
            self.dma_free.append(d)
        for tl in self.dma_tiles:
            tl.dsem = None
        self.dma_tiles = []
        for e in self.engs.values():
            for o in self.engs.values():
                e.seen[o.name] = o.cnt
            for d in range(len(self.dma_pool)):
                e.seen["d%d" % d] = self.dma_pool[d][1]
        self.pes.close()
        self.pes = None

    def close(self):
        self.es.close()


def tiles_of(n, step=512):
    out = []
    s = 0
    while s < n:
        out.append((s, min(step, n - s)))
        s += step
    return out


class Prog:
    def __init__(self, T, depth, n_groups, mixers=True, ffn=True, only=None):
        self.only = only
        self.mixers = mixers
        self.ffn = ffn
        self.T = T
        self.depth = depth
        self.n_groups = n_groups
        base = (T // n_groups) // 512 * 512 if n_groups > 1 else T
        self.groups = [(g * base, base) for g in range(n_groups - 1)]
        self.groups.append(((n_groups - 1) * base, T - (n_groups - 1) * base))
        nc = bass.Bass("TRN2", target_bir_lowering=False)
        self.nc = nc
        d = nc.dram_tensor
        self.xin = d("xin", [T, D_MODEL], F32, kind="ExternalInput").ap()
        self.yout = d("yout", [T, D_MODEL], F32, kind="ExternalOutput").ap()
        self.norm_w = d("norm_w", [128, (depth * 3 + 1) * 8], F32, kind="ExternalInput").ap()
        self.w_up = d("ffn_w_up", [depth, 2, D_MODEL, 2 * D_FF], F32, kind="ExternalInput").ap()
        self.w_dn = d("ffn_w_down", [depth, 2, D_FF, D_MODEL], F32, kind="ExternalInput").ap()
        self.ident_in = d("ident", [128, 128], F32, kind="ExternalInput").ap()
        self.hT = d("hT", [D_MODEL, T], F32).ap()
        self.b_w = d("b_w_in", [D_MODEL, 3072], F32, kind="ExternalInput").ap()
        self.b_wsw = d("b_w_sw", [D_MODEL, 2048], F32, kind="ExternalInput").ap()
        self.b_wo = d("b_w_out", [D_MODEL, D_MODEL], F32, kind="ExternalInput").ap()
        self.b_small = d("b_small", [128, 256 + 4], F32, kind="ExternalInput").ap()
        self.kbias_in = d("kbias", [128, (T + 127) // 128], F32, kind="ExternalInput").ap()
        self.qkT = d("qkT", [2048, T], BF16).ap()
        self.v_tm = d("v_tm", [T, 1024], BF16).ap()
        self.oT = d("oT", [D_MODEL, T], BF16).ap()
        self.tri_in = d("tri", [64, 4 * 64], F32, kind="ExternalInput").ap()
        self.tmask_rep = d("tmask_rep", [128, T], F32, kind="ExternalInput").ap()
        self.tmask_col = d("tmask_col", [128, (T + 127) // 128], F32, kind="ExternalInput").ap()
        self.c_w = d("c_w_in", [D_MODEL, 5120], F32, kind="ExternalInput").ap()
        self.c_wo = d("c_w_out", [D_MODEL, D_MODEL], F32, kind="ExternalInput").ap()
        self.c_small = d("c_small", [128, 32 + 1], F32, kind="ExternalInput").ap()
        self.fmA = d("fmA", [D_MODEL, T], BF16).ap()
        self.fmB = [d("fmB%d" % i, [D_MODEL, T], BF16).ap() for i in range(2)]
        self.fmG = d("fmG", [D_MODEL, T], BF16).ap()
        self.tmK = [d("tmK%d" % i, [T, D_MODEL], BF16).ap() for i in range(2)]
        self.tmL = [d("tmL%d" % i, [T, D_MODEL], F32).ap() for i in range(2)]
        self.obT = d("obT", [D_MODEL, T], F32).ap()
        self.lbrow_t = d("lbrow", [2, D_MODEL], F32)
        self.n_a = (depth + 2) // 3
        self.a_w = d("a_w_in", [self.n_a, D_MODEL, 4128], F32, kind="ExternalInput").ap()
        self.a_wo = d("a_w_out", [self.n_a, D_MODEL, D_MODEL], F32, kind="ExternalInput").ap()
        self.a_small = d("a_small", [self.n_a, 128, 32 + 1 + 120], F32, kind="ExternalInput").ap()
        self.pre = d("pre", [3 * D_MODEL, T], BF16).ap()
        self.ba_tm = d("ba_tm", [T, 32], F32).ap()
        self.identb_in = d("identb", [128, 128], BF16, kind="ExternalInput").ap()
        self.k = KB(nc)
        k = self.k
        self.ident = k.gsb([128, 128], F32, "identc")
        self.ones_bf = k.gsb([128, 128], BF16, "onesbf")
        self.nw = k.gsb([128, (depth * 3 + 1) * 8], F32, "nwcol")
        self.build()
        k.close()

    def phase_in(self):
        k, nc, T = self.k, self.nc, self.T
        k.begin()
        k.dma(k.sp, self.ident[:], self.ident_in, writes=[self.ident])
        k.op(k.dve, lambda e: e.memset(self.ones_bf[:], 1.0), writes=[self.ones_bf])
        nrows = self.depth * 3 + 1
        k.dma(k.sp, self.nw[:], self.norm_w, writes=[self.nw])
        xt = [k.sb([128, 4, D_MODEL], F32, "xt") for _ in range(2)]
        st = [k.sb([128, 8, 512], F32, "st") for _ in range(2)]
        pst = [k.ps([128, 512], F32, "pst") for _ in range(4)]
        pi = 0
        for gi, (t0, n) in enumerate(tiles_of(T, 512)):
            x = xt[gi % 2]
            s = st[gi % 2]
            nb = (n + 127) // 128
            blocks = [(b * 128, min(128, n - b * 128)) for b in range(nb)]
            if n % 128 == 0:
                k.dma(k.sp, x[:, 0:nb, :], self.xin[t0:t0 + n, :].rearrange("(j p) f -> p j f", p=128), writes=[x])
            else:
                for b, (o, r) in enumerate(blocks):
                    k.dma(k.sp, x[0:r, b, :], self.xin[t0 + o:t0 + o + r, :], writes=[x])
            for c in range(8):
                p = pst[pi % 4]
                pi += 1
                for b, (o, r) in enumerate(blocks):
                    k.op(k.pe, (lambda e, p=p, x=x, b=b, o=o, r=r, c=c: e.transpose(
                        p[:, o:o + r], x[0:r, b, c * 128:(c + 1) * 128], self.ident[0:r, 0:r])),
                        reads=[x, self.ident], writes=[p])
                eng = k.dve if c % 2 == 0 else k.act
                if eng is k.dve:
                    k.op(eng, (lambda e, p=p, s=s, c=c, n=n: e.tensor_copy(s[:, c, 0:n], p[:, 0:n])), reads=[p], writes=[s])
                else:
                    k.op(eng, (lambda e, p=p, s=s, c=c, n=n: e.copy(s[:, c, 0:n], p[:, 0:n])), reads=[p], writes=[s])
            k.dma(k.sp, self.hT.rearrange("(c p) t -> p c t", p=128)[:, :, t0:t0 + n], s[:, :, 0:n], reads=[s])
        k.end()


    def emit_norm(self, y, yv, hn, hnv, tl, nrow, pd, sq, rstd):
        k = self.k
        for ti, (o, n) in enumerate(tl):
            ss = pd[ti % 4]
            for c in range(8):
                s_ = sq[c % 2]
                k.op(k.act, (lambda e, s_=s_, c=c, o=o, n=n: e.activation(s_[:, 0:n], y[:, c, o:o + n], AF.Square)),
                     reads=[yv[c][ti]], writes=[s_])
                k.op(k.pe, (lambda e, ss=ss, s_=s_, c=c, n=n: e.matmul(ss[:, 0:n], lhsT=self.ones_bf[:], rhs=s_[:, 0:n],
                                                                        start=(c == 0), stop=(c == 7))),
                     reads=[s_, self.ones_bf], writes=[ss])
            r = rstd[ti % 2]
            k.op(k.act, (lambda e, r=r, ss=ss, n=n: e.activation(r[:, 0:n], ss[:, 0:n], AF.Ln, bias=self.eps_col[:, 0:1],
                                                                  scale=1.0 / D_MODEL)),
                 reads=[ss, self.eps_col], writes=[r])
            k.op(k.act, (lambda e, r=r, n=n: e.activation(r[:, 0:n], r[:, 0:n], AF.Exp, scale=-0.5)), reads=[r], writes=[r])
            for c in range(8):
                col = nrow * 8 + c
                k.op(k.dve, (lambda e, r=r, c=c, o=o, n=n, col=col: e.scalar_tensor_tensor(
                    out=hn[:, c, o:o + n], in0=y[:, c, o:o + n], scalar=self.nw[:, col:col + 1], in1=r[:, 0:n],
                    op0=ALU.mult, op1=ALU.mult)),
                    reads=[yv[c][ti], r, self.nw], writes=[hnv[c][ti]])

    def phase_ffn(self, li, fj, nrow):
        k, nc = self.k, self.nc
        HG = 256
        NG = D_FF // HG
        hTv = self.hT.rearrange("(c p) t -> p c t", p=128)
        for (T0, GS) in self.groups:
            tl = tiles_of(GS, 512)
            NT = len(tl)
            k.begin()
            y = k.sb([128, 8, GS], F32, "y")
            hn = k.sb([128, 8, GS], BF16, "hn")
            yv = [k.views(y, NT) for _ in range(8)]
            hnv = [k.views(hn, NT) for _ in range(8)]
            sq = [k.sb([128, 512], BF16, "sq") for _ in range(2)]
            rstd = [k.sb([128, 512], F32, "rstd") for _ in range(2)]
            wg = [k.sb([128, 8, HG], BF16, "wg") for _ in range(2)]
            wu = [k.sb([128, 8, HG], BF16, "wu") for _ in range(2)]
            wd = [k.sb([128, HG // 128, D_MODEL], BF16, "wd") for _ in range(2)]
            sg = [k.sb([128, 2, 512], F32, "sg") for _ in range(2)]
            act = [k.sb([128, 2, 512], BF16, "act") for _ in range(2)]
            pg = [k.ps([128, 512], F32, "pg") for _ in range(2)]
            pu = [k.ps([128, 512], F32, "pu") for _ in range(2)]
            pd = [k.ps([128, 512], F32, "pd") for _ in range(4)]
            allv = [v for c in range(8) for v in yv[c]]
            k.dma(k.sp, y[:], hTv[:, :, T0:T0 + GS], writes=allv)
            self.emit_norm(y, yv, hn, hnv, tl, nrow, pd, sq, rstd)
            wup = self.w_up[li, fj]
            wdn = self.w_dn[li, fj]
            steps = []
            for g in range(NG):
                for ti, (o, n) in enumerate(tl):
                    steps.append((g, g % 2, ti, o, n, len(steps) % 2))
            loaded = set()

            def load_w(g, b):
                if g in loaded:
                    return
                loaded.add(g)
                k.dma(k.pool, wg[b][:], wup[:, g * HG:(g + 1) * HG].rearrange("(kk p) c -> p kk c", p=128), writes=[wg[b]])
                k.dma(k.pool, wu[b][:], wup[:, D_FF + g * HG:D_FF + (g + 1) * HG].rearrange("(kk p) c -> p kk c", p=128),
                      writes=[wu[b]])
                k.dma(k.pool, wd[b][:], wdn[g * HG:(g + 1) * HG, :].rearrange("(kk p) c -> p kk c", p=128), writes=[wd[b]])

            def emit_gu(step, j):
                g, b, ti, o, n, ab = step
                load_w(g, b)
                for (pt, wt) in ((pg[j], wg[b]), (pu[j], wu[b])):
                    for c in range(8):
                        k.op(k.pe, (lambda e, pt=pt, wt=wt, j=j, c=c, o=o, n=n: e.matmul(
                            pt[:, 0:n], lhsT=wt[:, c, j * 128:(j + 1) * 128], rhs=hn[:, c, o:o + n],
                            start=(c == 0), stop=(c == 7))),
                            reads=[wt, hnv[c][ti]], writes=[pt])
                k.op(k.act, (lambda e, j=j, ab=ab, n=n: e.activation(sg[ab][:, j, 0:n], pg[j][:, 0:n], AF.Silu)),
                     reads=[pg[j]], writes=[sgv[ab][j]])
                k.op(k.dve, (lambda e, j=j, ab=ab, n=n: e.scalar_tensor_tensor(
                    out=act[ab][:, j, 0:n], in0=pu[j][:, 0:n], scalar=0.5, in1=sg[ab][:, j, 0:n],
                    op0=ALU.mult, op1=ALU.mult)),
                    reads=[pu[j], sgv[ab][j]], writes=[actv[ab][j]])

            def emit_down(step):
                g, b, ti, o, n, ab = step
                for m in range(8):
                    pdt = pd[m % 4]
                    for j in range(2):
                        k.op(k.pe, (lambda e, pdt=pdt, b=b, j=j, m=m, ab=ab, n=n: e.matmul(
                            pdt[:, 0:n], lhsT=wd[b][:, j, m * 128:(m + 1) * 128], rhs=act[ab][:, j, 0:n],
                            start=(j == 0), stop=(j == 1))),
                            reads=[wd[b], actv[ab][j]], writes=[pdt])
                    k.op(k.dve, (lambda e, pdt=pdt, m=m, o=o, n=n: e.tensor_tensor(
                        out=y[:, m, o:o + n], in0=y[:, m, o:o + n], in1=pdt[:, 0:n], op=ALU.add)),
                        reads=[pdt, yv[m][ti]], writes=[yv[m][ti]])

            sgv = [k.views(sg[i], 2) for i in range(2)]
            actv = [k.views(act[i], 2) for i in range(2)]
            emit_gu(steps[0], 0)
            emit_gu(steps[0], 1)
            for i, st in enumerate(steps):
                if i + 1 < len(steps):
                    emit_gu(steps[i + 1], 0)
                emit_down(st)
                if i + 1 < len(steps):
                    emit_gu(steps[i + 1], 1)
            k.dma(k.sp, hTv[:, :, T0:T0 + GS], y[:], reads=allv)
            k.end()

    def phase_out(self, nrow):
        k, nc, T = self.k, self.nc, self.T
        hTv = self.hT.rearrange("(c p) t -> p c t", p=128)
        k.begin()
        hb = [k.sb([128, 8, 512], F32, "hb") for _ in range(2)]
        sq = [k.sb([128, 512], BF16, "sq") for _ in range(2)]
        rstd = [k.sb([128, 512], F32, "rstd") for _ in range(2)]
        yn = [k.sb([128, 8, 512], F32, "yn") for _ in range(2)]
        ot = [k.sb([128, 4, D_MODEL], F32, "ot") for _ in range(2)]
        pss = k.ps([128, 512], F32, "pss")
        pt = [k.ps([128, 512], F32, "pt") for _ in range(4)]
        pi = 0
        for gi, (t0, n) in enumerate(tiles_of(T, 512)):
            h = hb[gi % 2]
            r = rstd[gi % 2]
            yy = yn[gi % 2]
            o_ = ot[gi % 2]
            k.dma(k.sp, h[:, :, 0:n], hTv[:, :, t0:t0 + n], writes=[h])
            for c in range(8):
                s_ = sq[c % 2]
                k.op(k.act, (lambda e, s_=s_, h=h, c=c, n=n: e.activation(s_[:, 0:n], h[:, c, 0:n], AF.Square)),
                     reads=[h], writes=[s_])
                k.op(k.pe, (lambda e, s_=s_, c=c, n=n: e.matmul(pss[:, 0:n], lhsT=self.ones_bf[:], rhs=s_[:, 0:n],
                                                                 start=(c == 0), stop=(c == 7))),
                     reads=[s_, self.ones_bf], writes=[pss])
            k.op(k.act, (lambda e, r=r, n=n: e.activation(r[:, 0:n], pss[:, 0:n], AF.Ln, bias=self.eps_col[:, 0:1],
                                                           scale=1.0 / D_MODEL)),
                 reads=[pss, self.eps_col], writes=[r])
            k.op(k.act, (lambda e, r=r, n=n: e.activation(r[:, 0:n], r[:, 0:n], AF.Exp, scale=-0.5)), reads=[r], writes=[r])
            for c in range(8):
                col = nrow * 8 + c
                k.op(k.dve, (lambda e, r=r, h=h, yy=yy, c=c, n=n, col=col: e.scalar_tensor_tensor(
                    out=yy[:, c, 0:n], in0=h[:, c, 0:n], scalar=self.nw[:, col:col + 1], in1=r[:, 0:n],
                    op0=ALU.mult, op1=ALU.mult)),
                    reads=[h, r, self.nw], writes=[yy])
            nb = (n + 127) // 128
            blocks = [(b * 128, min(128, n - b * 128)) for b in range(nb)]
            for b, (o, rr) in enumerate(blocks):
                for half in range(2):
                    p = pt[pi % 4]
                    pi += 1
                    for cc in range(4):
                        c = half * 4 + cc
                        k.op(k.pe, (lambda e, p=p, yy=yy, c=c, cc=cc, o=o, rr=rr: e.transpose(
                            p[0:rr, cc * 128:(cc + 1) * 128], yy[:, c, o:o + rr], self.ident[:])),
                            reads=[yy, self.ident], writes=[p])
                    if half == 0:
                        k.op(k.dve, (lambda e, p=p, o_=o_, b=b, rr=rr: e.tensor_copy(o_[0:rr, b, 0:512], p[0:rr, :])),
                             reads=[p], writes=[o_])
                    else:
                        k.op(k.act, (lambda e, p=p, o_=o_, b=b, rr=rr: e.copy(o_[0:rr, b, 512:1024], p[0:rr, :])),
                             reads=[p], writes=[o_])
            if n % 128 == 0:
                k.dma(k.sp, self.yout[t0:t0 + n, :].rearrange("(j p) f -> p j f", p=128), o_[:, 0:nb, :], reads=[o_])
            else:
                for b, (o, rr) in enumerate(blocks):
                    k.dma(k.sp, self.yout[t0 + o:t0 + o + rr, :], o_[0:rr, b, :], reads=[o_])
        k.end()


    def emit_rope_tables(self, T0, tl, cosF, sinF, tmp):
        k = self.k
        PI = float(np.pi)
        invf = self.bsm[:, 257:258]
        sign = self.bsm[:, 258:259]
        for ti, (o, n) in enumerate(tl):
            pos, ang, ki, kf, tf = tmp
            k.op(k.pool, (lambda e, pos=pos, n=n, b=T0 + o: e.iota(pos[:, 0:n], [[1, n]], base=b, channel_multiplier=0,
                                                                   allow_small_or_imprecise_dtypes=True)), writes=[pos])
            k.op(k.dve, (lambda e, n=n: e.tensor_scalar(ang[:, 0:n], pos[:, 0:n], invf, None, op0=ALU.mult)),
                 reads=[pos, self.bsm], writes=[ang])
            def reduce(src, dst, shift, n=n):
                k.op(k.dve, (lambda e: e.tensor_scalar(dst[:, 0:n], src[:, 0:n], shift, None, op0=ALU.add)),
                     reads=[src], writes=[dst])
                k.op(k.dve, (lambda e: e.tensor_scalar(ki[:, 0:n], dst[:, 0:n], 1.0 / (2 * PI), None, op0=ALU.mult)),
                     reads=[dst], writes=[ki])
                k.op(k.dve, (lambda e: e.tensor_copy(tf[:, 0:n], ki[:, 0:n])), reads=[ki], writes=[tf])
                k.op(k.dve, (lambda e: e.scalar_tensor_tensor(out=dst[:, 0:n], in0=tf[:, 0:n], scalar=-2 * PI, in1=dst[:, 0:n],
                                                              op0=ALU.mult, op1=ALU.add)), reads=[tf, dst], writes=[dst])
                k.op(k.dve, (lambda e: e.tensor_scalar(tf[:, 0:n], dst[:, 0:n], PI, -2 * PI, op0=ALU.is_gt, op1=ALU.mult)),
                     reads=[dst], writes=[tf])
                k.op(k.dve, (lambda e: e.tensor_tensor(out=dst[:, 0:n], in0=dst[:, 0:n], in1=tf[:, 0:n], op=ALU.add)),
                     reads=[dst, tf], writes=[dst])
                k.op(k.dve, (lambda e: e.tensor_scalar(tf[:, 0:n], dst[:, 0:n], -PI, 2 * PI, op0=ALU.is_lt, op1=ALU.mult)),
                     reads=[dst], writes=[tf])
                k.op(k.dve, (lambda e: e.tensor_tensor(out=dst[:, 0:n], in0=dst[:, 0:n], in1=tf[:, 0:n], op=ALU.add)),
                     reads=[dst, tf], writes=[dst])
                k.op(k.dve, (lambda e: e.tensor_scalar(dst[:, 0:n], dst[:, 0:n], -PI, PI, op0=ALU.max, op1=ALU.min)),
                     reads=[dst], writes=[dst])
            reduce(ang, kf, 0.0)
            reduce(ang, pos, PI / 2)
            s_, c_ = sinF[ti], cosF[ti]
            k.op(k.act, (lambda e, s_=s_, n=n: e.activation(s_[:, 0:n], kf[:, 0:n], AF.Sin)), reads=[kf], writes=[s_])
            k.op(k.act, (lambda e, c_=c_, n=n: e.activation(c_[:, 0:n], pos[:, 0:n], AF.Sin)), reads=[pos], writes=[c_])
            k.op(k.dve, (lambda e, s_=s_, n=n: e.tensor_scalar(s_[:, 0:n], s_[:, 0:n], sign, None, op0=ALU.mult)),
                 reads=[s_, self.bsm], writes=[s_])

    def phase_attn_proj(self, nrow):
        k, nc = self.k, self.nc
        hTv = self.hT.rearrange("(c p) t -> p c t", p=128)
        for (T0, GS) in self.groups:
            tl = tiles_of(GS, 512)
            NT = len(tl)
            k.begin()
            y = k.sb([128, 8, GS], F32, "y")
            hn = k.sb([128, 8, GS], BF16, "hn")
            yv = [k.views(y, NT) for _ in range(8)]
            hnv = [k.views(hn, NT) for _ in range(8)]
            sq = [k.sb([128, 512], BF16, "sq") for _ in range(2)]
            rstd = [k.sb([128, 512], F32, "rstd") for _ in range(2)]
            pd = [k.ps([128, 512], F32, "pd") for _ in range(4)]
            pa = [k.ps([128, 512], F32, "pa") for _ in range(2)]
            pb = [k.ps([128, 512], F32, "pb") for _ in range(2)]
            allv = [v for c in range(8) for v in yv[c]]
            k.dma(k.sp, y[:], hTv[:, :, T0:T0 + GS], writes=allv)
            self.emit_norm(y, yv, hn, hnv, tl, nrow, pd, sq, rstd)
            cosF = [k.sb([128, 512], F32, "cosF") for _ in range(NT)]
            sinF = [k.sb([128, 512], F32, "sinF") for _ in range(NT)]
            tmp = (k.sb([128, 512], F32, "pos"), k.sb([128, 512], F32, "ang"),
                   k.sb([128, 512], mybir.dt.int32, "ki"), k.sb([128, 512], F32, "kf"), k.sb([128, 512], F32, "tf"))
            self.emit_rope_tables(T0, tl, cosF, sinF, tmp)
            wa = [k.sb([128, 8, 128], BF16, "wa") for _ in range(2)]
            wb = [k.sb([128, 8, 128], BF16, "wb") for _ in range(2)]
            t1 = [k.sb([128, 512], F32, "t1") for _ in range(2)]
            t2 = [k.sb([128, 512], F32, "t2") for _ in range(2)]
            stg = [k.sb([128, 512], BF16, "stg") for _ in range(3)]
            it = 0
            for m in range(16):
                b = m % 2
                k.dma(k.pool, wa[b][:], self.b_w[:, m * 128:(m + 1) * 128].rearrange("(kk p) c -> p kk c", p=128), writes=[wa[b]])
                k.dma(k.pool, wb[b][:], self.b_wsw[:, m * 128:(m + 1) * 128].rearrange("(kk p) c -> p kk c", p=128), writes=[wb[b]])
                for ti, (o, n) in enumerate(tl):
                    ab = it % 2
                    sb_ = stg[it % 3]
                    it += 1
                    for (pt, wt) in ((pa[ab], wa[b]), (pb[ab], wb[b])):
                        for c in range(8):
                            k.op(k.pe, (lambda e, pt=pt, wt=wt, c=c, o=o, n=n: e.matmul(
                                pt[:, 0:n], lhsT=wt[:, c, :], rhs=hn[:, c, o:o + n], start=(c == 0), stop=(c == 7))),
                                reads=[wt, hnv[c][ti]], writes=[pt])
                    k.op(k.dve, (lambda e, ab=ab, ti=ti, n=n: e.tensor_tensor(out=t1[ab][:, 0:n], in0=pa[ab][:, 0:n],
                                                                              in1=cosF[ti][:, 0:n], op=ALU.mult)),
                         reads=[pa[ab], cosF[ti]], writes=[t1[ab]])
                    k.op(k.dve, (lambda e, ab=ab, ti=ti, n=n: e.tensor_tensor(out=t2[ab][:, 0:n], in0=pb[ab][:, 0:n],
                                                                              in1=sinF[ti][:, 0:n], op=ALU.mult)),
                         reads=[pb[ab], sinF[ti]], writes=[t2[ab]])
                    k.op(k.pool, (lambda e, ab=ab, sb_=sb_, n=n: e.tensor_tensor(out=sb_[:, 0:n], in0=t1[ab][:, 0:n],
                                                                                 in1=t2[ab][:, 0:n], op=ALU.add)),
                         reads=[t1[ab], t2[ab]], writes=[sb_])
                    k.dma(k.sp, self.qkT[m * 128:(m + 1) * 128, T0 + o:T0 + o + n], sb_[:, 0:n], reads=[sb_])
            wv = [k.sb([128, 8, 512], BF16, "wv") for _ in range(2)]
            vst = [k.sb([128, 512], BF16, "vst") for _ in range(3)]
            it = 0
            for vb in range(2):
                k.dma(k.pool, wv[vb][:], self.b_w[:, 2048 + vb * 512:2048 + (vb + 1) * 512].rearrange("(kk p) c -> p kk c", p=128),
                      writes=[wv[vb]])
                for ti, (o, n) in enumerate(tl):
                    for bo in range(0, n, 128):
                        r = min(128, n - bo)
                        pt = pd[it % 4]
                        vs = vst[it % 3]
                        it += 1
                        for c in range(8):
                            k.op(k.pe, (lambda e, pt=pt, vb=vb, c=c, o=o, bo=bo, r=r: e.matmul(
                                pt[0:r, :], lhsT=hn[:, c, o + bo:o + bo + r], rhs=wv[vb][:, c, :], start=(c == 0), stop=(c == 7))),
                                reads=[wv[vb], hnv[c][ti]], writes=[pt])
                        k.op(k.act, (lambda e, pt=pt, vs=vs, r=r: e.copy(vs[0:r, :], pt[0:r, :])), reads=[pt], writes=[vs])
                        k.dma(k.sp, self.v_tm[T0 + o + bo:T0 + o + bo + r, vb * 512:(vb + 1) * 512], vs[0:r, :], reads=[vs])
            k.end()

    def phase_attn_core(self):
        k, nc, T = self.k, self.nc, self.T
        NKT = (T + 127) // 128
        qtl = tiles_of(T, 512)
        k.begin()
        lt = k.sb([128, 256], F32, "lt")
        l2 = k.sb([128, 2], F32, "l2")
        neglam = k.sb([128, 1], F32, "neglam")
        subw = k.sb([128, 1], F32, "subw")
        kb = k.sb([128, NKT], F32, "kb")
        k.dma(k.sp, kb[:], self.kbias_in, writes=[kb])
        k.op(k.dve, (lambda e: e.tensor_tensor(out=lt[:, 0:64], in0=self.bsm[:, 0:64], in1=self.bsm[:, 64:128], op=ALU.mult)),
             reads=[self.bsm], writes=[lt])
        k.op(k.dve, (lambda e: e.tensor_tensor(out=lt[:, 64:128], in0=self.bsm[:, 128:192], in1=self.bsm[:, 192:256], op=ALU.mult)),
             reads=[self.bsm], writes=[lt])
        k.op(k.dve, (lambda e: e.reduce_sum(l2[:, 0:2], lt[:, 0:128].rearrange("p (a b) -> p a b", a=2), axis=AX.X)),
             reads=[lt], writes=[l2])
        k.op(k.act, (lambda e: e.activation(l2[:, 0:2], l2[:, 0:2], AF.Exp)), reads=[l2], writes=[l2])
        k.op(k.dve, (lambda e: e.tensor_tensor(out=neglam[:], in0=l2[:, 1:2], in1=l2[:, 0:1], op=ALU.subtract)),
             reads=[l2], writes=[neglam])
        k.op(k.dve, (lambda e: e.tensor_tensor(out=neglam[:], in0=neglam[:], in1=self.bsm[:, 259:260], op=ALU.subtract)),
             reads=[neglam, self.bsm], writes=[neglam])
        k.op(k.dve, (lambda e: e.tensor_scalar(subw[:], self.bsm[:, 259:260], -1.0, 1.0, op0=ALU.mult, op1=ALU.add)),
             reads=[self.bsm], writes=[subw])
        k.op(k.dve, (lambda e: e.tensor_tensor(out=subw[:], in0=subw[:], in1=self.bsm[:, 256:257], op=ALU.mult)),
             reads=[subw, self.bsm], writes=[subw])

        kk1 = [k.sb([64, T], BF16, "kk1") for _ in range(2)]
        kk2 = [k.sb([64, T], BF16, "kk2") for _ in range(2)]
        vh = [k.sb([128, NKT, 128], BF16, "vh") for _ in range(2)]
        q1 = [k.sb([64, 512], BF16, "q1") for _ in range(2)]
        q2 = [k.sb([64, 512], BF16, "q2") for _ in range(2)]
        p1 = [k.sb([128, 512], BF16, "p1") for _ in range(3)]
        p2 = [k.sb([128, 512], BF16, "p2") for _ in range(3)]
        ps1 = [k.ps([128, 512], F32, "ps1") for _ in range(2)]
        ps2 = [k.ps([128, 512], F32, "ps2") for _ in range(2)]
        num1, num2, z1, z2 = (k.ps([128, 512], F32, nm) for nm in ("num1", "num2", "z1", "z2"))
        za1 = k.sb([128, 512], F32, "za1")
        za2 = k.sb([128, 512], F32, "za2")
        ones_f = k.sb([128, 128], F32, "ones_f")
        k.op(k.dve, (lambda e: e.memset(ones_f[:], 1.0)), writes=[ones_f])
        r1 = k.sb([128, 512], F32, "r1")
        r2 = k.sb([128, 512], F32, "r2")
        o1 = k.sb([128, 512], F32, "o1")
        o2 = k.sb([128, 512], F32, "o2")
        oo = k.sb([128, 512], F32, "oo")
        sqo = k.sb([128, 512], BF16, "sqo")
        rs = k.sb([128, 512], F32, "rs")
        ost = [k.sb([128, 512], BF16, "ost") for _ in range(2)]
        nfull = T // 128
        rem = T - nfull * 128
        it = 0
        fi = 0
        for h in range(8):
            hb = h % 2
            k.dma(k.sp, kk1[hb][:], self.qkT[1024 + h * 64:1024 + (h + 1) * 64, :], writes=[kk1[hb]])
            k.dma(k.sp, kk2[hb][:], self.qkT[1536 + h * 64:1536 + (h + 1) * 64, :], writes=[kk2[hb]])
            k.dma(k.sp, vh[hb][:, 0:nfull, :],
                  self.v_tm[0:nfull * 128, h * 128:(h + 1) * 128].rearrange("(kt p) e -> p kt e", p=128), writes=[vh[hb]])
            if rem:
                k.dma(k.sp, vh[hb][0:rem, nfull, :], self.v_tm[nfull * 128:T, h * 128:(h + 1) * 128], writes=[vh[hb]])
            for qi, (t0, n) in enumerate(qtl):
                qb = fi % 2
                k.dma(k.sp, q1[qb][:, 0:n], self.qkT[h * 64:(h + 1) * 64, t0:t0 + n], writes=[q1[qb]])
                k.dma(k.sp, q2[qb][:, 0:n], self.qkT[512 + h * 64:512 + (h + 1) * 64, t0:t0 + n], writes=[q2[qb]])
                base_it = it
                it += NKT

                def emit_scores(kt):
                    kn = 128 if kt < nfull else rem
                    sb2 = (base_it + kt) % 2
                    for (ps, kk, qq) in ((ps1[sb2], kk1[hb], q1[qb]), (ps2[sb2], kk2[hb], q2[qb])):
                        k.op(k.pe, (lambda e, ps=ps, kk=kk, qq=qq, kt=kt, kn=kn, n=n: e.matmul(
                            ps[0:kn, 0:n], lhsT=kk[:, kt * 128:kt * 128 + kn], rhs=qq[:, 0:n], start=True, stop=True)),
                            reads=[kk, qq], writes=[ps])

                def emit_exp(kt):
                    kn = 128 if kt < nfull else rem
                    sb2 = (base_it + kt) % 2
                    pb3 = (base_it + kt) % 3
                    for (ps, pp) in ((ps1[sb2], p1[pb3]), (ps2[sb2], p2[pb3])):
                        k.op(k.act, (lambda e, ps=ps, pp=pp, kt=kt, kn=kn, n=n: e.activation(
                            pp[0:kn, 0:n], ps[0:kn, 0:n], AF.Exp, bias=kb[0:kn, kt:kt + 1], scale=0.125)),
                            reads=[ps, kb], writes=[pp])

                def emit_pv(kt):
                    kn = 128 if kt < nfull else rem
                    pb3 = (base_it + kt) % 3
                    first, last = (kt == 0), (kt == NKT - 1)
                    for (pp, nm, za, zeng) in ((p1[pb3], num1, za1, k.dve), (p2[pb3], num2, za2, k.pool)):
                        k.op(k.pe, (lambda e, nm=nm, pp=pp, kt=kt, kn=kn, n=n, first=first, last=last, vv=vh[hb]: e.matmul(
                            nm[:, 0:n], lhsT=vv[0:kn, kt, :], rhs=pp[0:kn, 0:n], start=first, stop=last)),
                            reads=[vh[hb], pp], writes=[nm])
                        if first:
                            k.op(zeng, (lambda e, za=za, pp=pp, n=n: e.tensor_copy(za[:, 0:n], pp[:, 0:n])), reads=[pp], writes=[za])
                        else:
                            k.op(zeng, (lambda e, za=za, pp=pp, kn=kn, n=n: e.tensor_tensor(out=za[0:kn, 0:n], in0=za[0:kn, 0:n],
                                                                                           in1=pp[0:kn, 0:n], op=ALU.add)),
                                 reads=[pp, za], writes=[za])

                emit_scores(0)
                for kt in range(NKT):
                    emit_exp(kt)
                    if kt + 1 < NKT:
                        emit_scores(kt + 1)
                    emit_pv(kt)
                for (zz, za) in ((z1, za1), (z2, za2)):
                    k.op(k.pe, (lambda e, zz=zz, za=za, n=n: e.matmul(zz[:, 0:n], lhsT=ones_f[:], rhs=za[:, 0:n], start=True, stop=True)),
                         reads=[ones_f, za], writes=[zz])
                k.op(k.dve, (lambda e, n=n: e.reciprocal(r1[:, 0:n], z1[:, 0:n])), reads=[z1], writes=[r1])
                k.op(k.dve, (lambda e, n=n: e.reciprocal(r2[:, 0:n], z2[:, 0:n])), reads=[z2], writes=[r2])
                k.op(k.dve, (lambda e, n=n: e.tensor_tensor(out=o1[:, 0:n], in0=num1[:, 0:n], in1=r1[:, 0:n], op=ALU.mult)),
                     reads=[num1, r1], writes=[o1])
                k.op(k.dve, (lambda e, n=n: e.scalar_tensor_tensor(out=o2[:, 0:n], in0=num2[:, 0:n], scalar=neglam[:, 0:1],
                                                                   in1=r2[:, 0:n], op0=ALU.mult, op1=ALU.mult)),
                     reads=[num2, r2, neglam], writes=[o2])
                k.op(k.pool, (lambda e, n=n: e.tensor_tensor(out=oo[:, 0:n], in0=o1[:, 0:n], in1=o2[:, 0:n], op=ALU.add)),
                     reads=[o1, o2], writes=[oo])
                k.op(k.act, (lambda e, n=n: e.activation(sqo[:, 0:n], oo[:, 0:n], AF.Square)), reads=[oo], writes=[sqo])
                pss = ps1[it % 2]
                k.op(k.pe, (lambda e, pss=pss, n=n: e.matmul(pss[:, 0:n], lhsT=self.ones_bf[:], rhs=sqo[:, 0:n], start=True, stop=True)),
                     reads=[self.ones_bf, sqo], writes=[pss])
                k.op(k.act, (lambda e, pss=pss, n=n: e.activation(rs[:, 0:n], pss[:, 0:n], AF.Ln, bias=self.eps_col[:, 0:1],
                                                                   scale=1.0 / 128)), reads=[pss, self.eps_col], writes=[rs])
                k.op(k.act, (lambda e, n=n: e.activation(rs[:, 0:n], rs[:, 0:n], AF.Exp, scale=-0.5)), reads=[rs], writes=[rs])
                os_ = ost[fi % 2]
                fi += 1
                k.op(k.dve, (lambda e, os_=os_, n=n: e.scalar_tensor_tensor(out=os_[:, 0:n], in0=oo[:, 0:n], scalar=subw[:, 0:1],
                                                                            in1=rs[:, 0:n], op0=ALU.mult, op1=ALU.mult)),
                     reads=[oo, rs, subw], writes=[os_])
                k.dma(k.sp, self.oT[h * 128:(h + 1) * 128, t0:t0 + n], os_[:, 0:n], reads=[os_])
        k.end()

    def phase_out_proj(self, wo_ap):
        k, nc, T = self.k, self.nc, self.T
        hTv = self.hT.rearrange("(c p) t -> p c t", p=128)
        oTv = self.oT.rearrange("(c p) t -> p c t", p=128)
        k.begin()
        wo = k.sb([128, 8, D_MODEL], BF16, "wo")
        for c in range(8):
            for hf in range(2):
                k.dma(k.pool, wo[:, c, hf * 512:(hf + 1) * 512], wo_ap[c * 128:(c + 1) * 128, hf * 512:(hf + 1) * 512], writes=[wo])
        ob = [k.sb([128, 8, 512], BF16, "ob") for _ in range(2)]
        hb = [k.sb([128, 8, 512], F32, "hb") for _ in range(2)]
        pp = [k.ps([128, 512], F32, "pp") for _ in range(4)]
        pi = 0
        for gi, (t0, n) in enumerate(tiles_of(T, 512)):
            o_ = ob[gi % 2]
            h_ = hb[gi % 2]
            k.dma(k.sp, o_[:, :, 0:n], oTv[:, :, t0:t0 + n], writes=[o_])
            k.dma(k.sp, h_[:, :, 0:n], hTv[:, :, t0:t0 + n], writes=[h_])
            for m in range(8):
                p = pp[pi % 4]
                pi += 1
                for c in range(8):
                    k.op(k.pe, (lambda e, p=p, o_=o_, c=c, m=m, n=n: e.matmul(
                        p[:, 0:n], lhsT=wo[:, c, m * 128:(m + 1) * 128], rhs=o_[:, c, 0:n], start=(c == 0), stop=(c == 7))),
                        reads=[wo, o_], writes=[p])
                k.op(k.dve, (lambda e, p=p, h_=h_, m=m, n=n: e.tensor_tensor(out=h_[:, m, 0:n], in0=h_[:, m, 0:n],
                                                                             in1=p[:, 0:n], op=ALU.add)),
                     reads=[p, h_], writes=[h_])
            k.dma(k.sp, hTv[:, :, t0:t0 + n], h_[:, :, 0:n], reads=[h_])
        k.end()

    def mixer_attn(self, nrow):
        import os
        st = int(os.environ.get("ATT_STAGE", "3"))
        self.phase_attn_proj(nrow)
        if st >= 2:
            self.phase_attn_core()
        if st >= 3:
            self.phase_out_proj(self.b_wo)


    def phase_proj(self, nrow, setup, fm_jobs, tm_jobs):
        k, nc = self.k, self.nc
        hTv = self.hT.rearrange("(c p) t -> p c t", p=128)
        for (T0, GS) in self.groups:
            tl = tiles_of(GS, 512)
            NT = len(tl)
            k.begin()
            y = k.sb([128, 8, GS], F32, "y")
            hn = k.sb([128, 8, GS], BF16, "hn")
            yv = [k.views(y, NT) for _ in range(8)]
            hnv = [k.views(hn, NT) for _ in range(8)]
            sq = [k.sb([128, 512], BF16, "sq") for _ in range(2)]
            rstd = [k.sb([128, 512], F32, "rstd") for _ in range(2)]
            pd = [k.ps([128, 512], F32, "pd") for _ in range(4)]
            allv = [v for c in range(8) for v in yv[c]]
            k.dma(k.sp, y[:], hTv[:, :, T0:T0 + GS], writes=allv)
            self.emit_norm(y, yv, hn, hnv, tl, nrow, pd, sq, rstd)
            ctx = setup(T0, tl)
            wa = [k.sb([128, 8, 128], BF16, "wa") for _ in range(2)]
            it = 0
            for ji, (wap, post) in enumerate(fm_jobs):
                b = ji % 2
                k.dma(k.pool, wa[b][:], wap.rearrange("(kk p) c -> p kk c", p=128), writes=[wa[b]])
                for ti, (o, n) in enumerate(tl):
                    pt = pd[it % 4]
                    it += 1
                    for c in range(8):
                        k.op(k.pe, (lambda e, pt=pt, b=b, c=c, o=o, n=n: e.matmul(
                            pt[:, 0:n], lhsT=wa[b][:, c, :], rhs=hn[:, c, o:o + n], start=(c == 0), stop=(c == 7))),
                            reads=[wa[b], hnv[c][ti]], writes=[pt])
                    post(ctx, pt, T0, ti, o, n)
            wv = [k.sb([128, 8, 512], BF16, "wv") for _ in range(2)]
            for ji, job in enumerate(tm_jobs):
                wap, post = job[0], job[1]
                ncl = job[2] if len(job) > 2 else 512
                b = ji % 2
                k.dma(k.pool, wv[b][:, :, 0:ncl], wap.rearrange("(kk p) c -> p kk c", p=128), writes=[wv[b]])
                for ti, (o, n) in enumerate(tl):
                    for bo in range(0, n, 128):
                        r = min(128, n - bo)
                        pt = pd[it % 4]
                        it += 1
                        for c in range(8):
                            k.op(k.pe, (lambda e, pt=pt, b=b, c=c, o=o, bo=bo, r=r, ncl=ncl: e.matmul(
                                pt[0:r, 0:ncl], lhsT=hn[:, c, o + bo:o + bo + r], rhs=wv[b][:, c, 0:ncl], start=(c == 0), stop=(c == 7))),
                                reads=[wv[b], hnv[c][ti]], writes=[pt])
                        post(ctx, pt, T0 + o + bo, r)
            k.end()

    def hg_prep(self, li):
        k, nc = self.k, self.nc
        k.begin()
        e = k.sb([128, 32], F32, "lbe")
        ssum = k.sb([128, 8], F32, "lbs")
        k.op(k.act, (lambda en: en.activation(e[:], self.csm[:, 0:32], AF.Exp)), reads=[self.csm], writes=[e])
        k.op(k.dve, (lambda en: en.tensor_tensor(out=ssum[:], in0=e[:, 0:8], in1=e[:, 8:16], op=ALU.add)), reads=[e], writes=[ssum])
        k.op(k.dve, (lambda en: en.tensor_tensor(out=ssum[:], in0=ssum[:], in1=e[:, 16:24], op=ALU.add)), reads=[e, ssum], writes=[ssum])
        k.op(k.dve, (lambda en: en.tensor_tensor(out=ssum[:], in0=ssum[:], in1=e[:, 24:32], op=ALU.add)), reads=[e, ssum], writes=[ssum])
        k.op(k.dve, (lambda en: en.reciprocal(ssum[:], ssum[:])), reads=[ssum], writes=[ssum])
        lbc = self.lb_col
        k.op(k.dve, (lambda en: en.memset(lbc[:, 0:8], 0.0)), writes=[lbc])
        for r in range(1, li + 1):
            k.op(k.dve, (lambda en, r=r: en.tensor_tensor(out=lbc[:, 0:8], in0=lbc[:, 0:8], in1=e[:, r * 8:(r + 1) * 8], op=ALU.add)),
                 reads=[e, lbc], writes=[lbc])
        k.op(k.dve, (lambda en: en.tensor_tensor(out=lbc[:, 0:8], in0=lbc[:, 0:8], in1=ssum[:], op=ALU.mult)), reads=[lbc, ssum], writes=[lbc])
        k.op(k.dve, (lambda en: en.tensor_scalar(lbc[:, 8:16], lbc[:, 0:8], -1.0, 1.0, op0=ALU.mult, op1=ALU.add)), reads=[lbc], writes=[lbc])
        lbr = self.lbrow_t.ap()
        k.dma(k.sp, lbr.rearrange("r (c p) -> p r c", p=128), lbc[:].rearrange("p (r c) -> p r c", c=8), reads=[lbc],
              allow_slow_non_contiguous=True)
        k.end()
        k.begin()
        k.dma(k.sp, self.oml_row[:], bass.AP(self.lbrow_t, D_MODEL, [[0, 128], [1, D_MODEL]]), writes=[self.oml_row])
        k.end()

    def phase_hg_proj(self, nrow):
        k = self.k
        QS = 128 ** -0.5

        def setup(T0, tl):
            ctx = {}
            ctx["tm"] = [k.sb([128, 512], F32, "tmk") for _ in tl]
            for ti, (o, n) in enumerate(tl):
                k.dma(k.sp, ctx["tm"][ti][:, 0:n], self.tmask_rep[:, T0 + o:T0 + o + n], writes=[ctx["tm"][ti]])
            ctx["sg"] = [k.sb([128, 512], F32, "sg") for _ in range(2)]
            ctx["st"] = [k.sb([128, 512], BF16, "st") for _ in range(3)]
            ctx["k32"] = [k.sb([128, 512], F32, "k32") for _ in range(2)]
            ctx["lf"] = [k.sb([128, 512], F32, "lf") for _ in range(2)]
            ctx["kb"] = [k.sb([128, 512], BF16, "kb") for _ in range(2)]
            ctx["i"] = 0
            return ctx

        def post_silu(dst, scale):
            def post(ctx, pt, T0, ti, o, n):
                i = ctx["i"]
                ctx["i"] += 1
                sg, st = ctx["sg"][i % 2], ctx["st"][i % 3]
                k.op(k.act, (lambda e: e.activation(sg[:, 0:n], pt[:, 0:n], AF.Silu)), reads=[pt], writes=[sg])
                k.op(k.pool, (lambda e: e.tensor_scalar(st[:, 0:n], sg[:, 0:n], scale, None, op0=ALU.mult)), reads=[sg], writes=[st])
                return st
            return post

        def fm_q(c):
            base = post_silu(None, QS)

            def post(ctx, pt, T0, ti, o, n):
                st = base(ctx, pt, T0, ti, o, n)
                k.dma(k.sp, self.fmA[c * 128:(c + 1) * 128, T0 + o:T0 + o + n], st[:, 0:n], reads=[st])
            return post

        def fm_g(c):
            base = post_silu(None, 1.0)

            def post(ctx, pt, T0, ti, o, n):
                st = base(ctx, pt, T0, ti, o, n)
                k.dma(k.sp, self.fmG[c * 128:(c + 1) * 128, T0 + o:T0 + o + n], st[:, 0:n], reads=[st])
            return post

        def fm_k(d, c):
            def post(ctx, pt, T0, ti, o, n):
                i = ctx["i"]
                ctx["i"] += 1
                sg, st = ctx["sg"][i % 2], ctx["st"][i % 3]
                k.op(k.act, (lambda e: e.activation(sg[:, 0:n], pt[:, 0:n], AF.Sigmoid, scale=-1.0)), reads=[pt], writes=[sg])
                k.op(k.dve, (lambda e: e.scalar_tensor_tensor(out=st[:, 0:n], in0=sg[:, 0:n], scalar=self.lb_col[:, 8 + c:9 + c],
                                                              in1=ctx["tm"][ti][:, 0:n], op0=ALU.mult, op1=ALU.mult)),
                     reads=[sg, self.lb_col, ctx["tm"][ti]], writes=[st])
                k.dma(k.sp, self.fmB[d][c * 128:(c + 1) * 128, T0 + o:T0 + o + n], st[:, 0:n], reads=[st])
            return post

        def tm_v(cb):
            def post(ctx, pt, tok0, r):
                i = ctx["i"]
                ctx["i"] += 1
                st = ctx["st"][i % 3]
                k.op(k.act, (lambda e: e.copy(st[0:r, :], pt[0:r, :])), reads=[pt], writes=[st])
                k.dma(k.sp, self.v_tm[tok0:tok0 + r, cb * 512:(cb + 1) * 512], st[0:r, :], reads=[st])
            return post

        def tm_f(d, cb):
            def post(ctx, pt, tok0, r):
                i = ctx["i"]
                ctx["i"] += 1
                sg, k32, lf, kb = ctx["sg"][i % 2], ctx["k32"][i % 2], ctx["lf"][i % 2], ctx["kb"][i % 2]
                blk = tok0 // 128
                assert tok0 % 128 == 0
                k.op(k.act, (lambda e: e.activation(sg[0:r, :], pt[0:r, :], AF.Sigmoid, scale=-1.0)), reads=[pt], writes=[sg])
                k.op(k.dve, (lambda e: e.scalar_tensor_tensor(out=k32[0:r, :], in0=sg[0:r, :], scalar=self.tmc[0:r, blk:blk + 1],
                                                              in1=self.oml_row[0:r, cb * 512:(cb + 1) * 512], op0=ALU.mult, op1=ALU.mult)),
                     reads=[sg, self.tmc, self.oml_row], writes=[k32])
                k.op(k.act, (lambda e: e.activation(lf[0:r, :], k32[0:r, :], AF.Ln, bias=self.one_col[0:r, 0:1], scale=-1.0)),
                     reads=[k32, self.one_col], writes=[lf])
                k.op(k.pool, (lambda e: e.tensor_copy(kb[0:r, :], k32[0:r, :])), reads=[k32], writes=[kb])
                k.dma(k.sp, self.tmK[d][tok0:tok0 + r, cb * 512:(cb + 1) * 512], kb[0:r, :], reads=[kb])
                k.dma(k.sp, self.tmL[d][tok0:tok0 + r, cb * 512:(cb + 1) * 512], lf[0:r, :], reads=[lf])
            return post

        W = self.c_w
        fm = []
        for c in range(8):
            fm.append((W[:, c * 128:(c + 1) * 128], fm_q(c)))
        for c in range(8):
            fm.append((W[:, 2048 + c * 128:2048 + (c + 1) * 128], fm_g(c)))
        for d in range(2):
            for c in range(8):
                fm.append((W[:, 3072 + d * 1024 + c * 128:3072 + d * 1024 + (c + 1) * 128], fm_k(d, c)))
        tm = []
        for cb in range(2):
            tm.append((W[:, 1024 + cb * 512:1024 + (cb + 1) * 512], tm_v(cb)))
        for d in range(2):
            for cb in range(2):
                tm.append((W[:, 3072 + d * 1024 + cb * 512:3072 + d * 1024 + (cb + 1) * 512], tm_f(d, cb)))
        self.phase_proj(nrow, setup, fm, tm)

    def phase_hg_core(self, onorm_col):
        k, nc, T = self.k, self.nc, self.T
        tiles = tiles_of(T, 512)
        for sweep in (1, 0):
            d = sweep
            k.begin()
            if d == 0:
                Uin, Uex, Mk, lastcol = self.tri[:, 0:64], self.tri[:, 64:128], 0, 63
            else:
                Uin, Uex, Mk, lastcol = self.tri[:, 128:192], self.tri[:, 192:256], 2, 0
            mrep = k.sb([64, 512], F32, "mrep")
            for c in range(8):
                k.op(k.dve, (lambda e, c=c, Mk=Mk: e.tensor_copy(mrep[:, c * 64:(c + 1) * 64], self.tri[:, Mk * 64:(Mk + 1) * 64])),
                     reads=[self.tri], writes=[mrep])
            S = [k.sb([128, 128], F32, "S") for _ in range(8)]
            Sb = [k.sb([128, 128], BF16, "Sb") for _ in range(8)]
            for h in range(8):
                k.op(k.dve, (lambda e, h=h: e.memset(S[h][:], 0.0)), writes=[S[h]])
                k.op(k.pool, (lambda e, h=h: e.memset(Sb[h][:], 0.0)), writes=[Sb[h]])
            NB = 2
            qT = [k.sb([128, 512], BF16, "qT") for _ in range(NB)]
            kT = [k.sb([128, 512], BF16, "kT") for _ in range(NB)]
            ktm = [k.sb([64, 8, 128], BF16, "ktm") for _ in range(NB)]
            lf = [k.sb([64, 8, 128], F32, "lf") for _ in range(NB)]
            vt = [k.sb([64, 8, 128], BF16, "vt") for _ in range(3)]
            eb = [k.sb([128, 512], F32, "eb") for _ in range(NB)]
            enb = [k.sb([128, 512], F32, "enb") for _ in range(NB)]
            qd = [k.sb([128, 512], BF16, "qd") for _ in range(NB)]
            kd = [k.sb([128, 512], BF16, "kd") for _ in range(NB)]
            ekd = [k.sb([64, 8, 128], F32, "ekd") for _ in range(NB)]
            kdec = [k.sb([64, 8, 128], BF16, "kdec") for _ in range(NB)]
            atm = [k.sb([64, 512], BF16, "atm") for _ in range(NB)]
            B = [k.ps([128, 512], F32, "Y%d" % i) for i in range(8)]
            ost = [k.sb([128, 512], F32, "ost") for _ in range(2)]
            if d == 0:
                obt = [k.sb([128, 512], F32, "obt") for _ in range(3)]
                gt = [k.sb([128, 512], BF16, "gt") for _ in range(3)]
                sqo_l = [k.sb([128, 512], BF16, "sqo") for _ in range(2)]
                rs_l = [k.sb([128, 512], F32, "rs") for _ in range(2)]
                fin = [k.sb([128, 512], BF16, "fin") for _ in range(2)]
            order = list(range(len(tiles)))
            if d == 1:
                order = order[::-1]

            def unit(h, b, it, t0, n, ncn, corder):
                if True:
                    rows = slice(h * 128, (h + 1) * 128)
                    Y = B[4 * b:4 * b + 4]
                    vt_ = vt[it % 3]
                    pbc, pat = Y[0], Y[3]
                    psuf = [Y[1], Y[2]]
                    pout = Y[1]
                    pst = [Y[2], Y[2]]
                    if d == 0:
                        sqo, rs = sqo_l[b], rs_l[b]
                    k.dma(k.sp, qT[b][:, 0:n], self.fmA[rows, t0:t0 + n], writes=[qT[b]])
                    k.dma(k.sp, kT[b][:, 0:n], self.fmB[d][rows, t0:t0 + n], writes=[kT[b]])
                    k.dma(k.sp, ktm[b][:, 0:ncn, :], self.tmK[d][t0:t0 + n, rows].rearrange("(c p) e -> p c e", p=64), writes=[ktm[b]])
                    k.dma(k.sp, lf[b][:, 0:ncn, :], self.tmL[d][t0:t0 + n, rows].rearrange("(c p) e -> p c e", p=64), writes=[lf[b]])
                    k.dma(k.sp, vt_[:, 0:ncn, :], self.v_tm[t0:t0 + n, rows].rearrange("(c p) e -> p c e", p=64), writes=[vt_])
                    if d == 0:
                        k.dma(k.sp, obt[it % 3][:, 0:n], self.obT[rows, t0:t0 + n], writes=[obt[it % 3]])
                        k.dma(k.sp, gt[it % 3][:, 0:n], self.fmG[rows, t0:t0 + n], writes=[gt[it % 3]])
                    k.mark()
                    for c in range(ncn):
                        k.op(k.pe, (lambda e, b=b, c=c: e.matmul(pbc[:, c * 64:(c + 1) * 64], lhsT=lf[b][:, c, :], rhs=Uin,
                                                                 start=True, stop=True)), reads=[lf[b], self.tri], writes=[pbc])
                    k.op(k.act, (lambda e, b=b, n=n: e.activation(eb[b][:, 0:n], pbc[:, 0:n], AF.Exp)), reads=[pbc], writes=[eb[b]])
                    k.op(k.act, (lambda e, b=b, n=n: e.activation(enb[b][:, 0:n], pbc[:, 0:n], AF.Exp, scale=-1.0)), reads=[pbc], writes=[enb[b]])
                    k.op(k.dve, (lambda e, b=b, n=n: e.tensor_tensor(out=qd[b][:, 0:n], in0=qT[b][:, 0:n], in1=eb[b][:, 0:n], op=ALU.mult)),
                         reads=[qT[b], eb[b]], writes=[qd[b]])
                    k.op(k.dve, (lambda e, b=b, n=n: e.tensor_tensor(out=kd[b][:, 0:n], in0=kT[b][:, 0:n], in1=enb[b][:, 0:n], op=ALU.mult)),
                         reads=[kT[b], enb[b]], writes=[kd[b]])
                    for c in range(ncn):
                        ps_ = psuf[c // 4]
                        k.op(k.pe, (lambda e, b=b, c=c, ps_=ps_: e.matmul(ps_[0:64, (c % 4) * 128:(c % 4 + 1) * 128], lhsT=Uex, rhs=lf[b][:, c, :],
                                                                          start=True, stop=True)), reads=[lf[b], self.tri], writes=[ps_])
                    for half in range((ncn + 3) // 4):
                        nn = min(4, ncn - half * 4)
                        k.op(k.act, (lambda e, b=b, half=half, nn=nn, pq=psuf[half]: e.activation(
                            ekd[b][:, half * 4:half * 4 + nn, :], pq[0:64, 0:nn * 128].rearrange("p (c d) -> p c d", d=128), AF.Exp)),
                             reads=[psuf[half]], writes=[ekd[b]])
                    k.op(k.dve, (lambda e, b=b, ncn=ncn: e.tensor_tensor(out=kdec[b][:, 0:ncn, :], in0=ktm[b][:, 0:ncn, :],
                                                                        in1=ekd[b][:, 0:ncn, :], op=ALU.mult)),
                         reads=[ktm[b], ekd[b]], writes=[kdec[b]])
                    for c in range(ncn):
                        cs = slice(c * 64, (c + 1) * 64)
                        k.op(k.pe, (lambda e, b=b, cs=cs: e.matmul(pat[0:64, cs], lhsT=kd[b][:, cs], rhs=qd[b][:, cs], start=True, stop=True)),
                             reads=[kd[b], qd[b]], writes=[pat])
                    k.op(k.dve, (lambda e, b=b, n=n: e.tensor_tensor(out=atm[b][:, 0:n], in0=pat[0:64, 0:n], in1=mrep[:, 0:n], op=ALU.mult)),
                         reads=[pat, mrep], writes=[atm[b]])
                    k.mark()
                    for c in corder:
                        cs = slice(c * 64, (c + 1) * 64)
                        k.op(k.pe, (lambda e, b=b, cs=cs, h=h: e.matmul(pout[:, cs], lhsT=Sb[h][:], rhs=qd[b][:, cs], start=True, stop=False)),
                             reads=[Sb[h], qd[b]], writes=[pout])
                        k.op(k.pe, (lambda e, b=b, cs=cs, c=c: e.matmul(pout[:, cs], lhsT=vt_[:, c, :], rhs=atm[b][:, cs], start=False, stop=True)),
                             reads=[vt_, atm[b]], writes=[pout])
                        pp = pst[c % 2]
                        pc = slice((c % 2) * 128, (c % 2 + 1) * 128)
                        k.op(k.pe, (lambda e, b=b, c=c, pp=pp, pc=pc: e.matmul(pp[:, pc], lhsT=kdec[b][:, c, :], rhs=vt_[:, c, :], start=True, stop=True)),
                             reads=[kdec[b], vt_], writes=[pp])
                        fc = c * 64 + lastcol
                        k.op(k.dve, (lambda e, b=b, h=h, pp=pp, fc=fc, pc=pc: e.scalar_tensor_tensor(
                            out=S[h][:], in0=S[h][:], scalar=eb[b][:, fc:fc + 1], in1=pp[:, pc], op0=ALU.mult, op1=ALU.add)),
                            reads=[S[h], eb[b], pp], writes=[S[h]])
                        k.op(k.act, (lambda e, h=h: e.copy(Sb[h][:], S[h][:])), reads=[S[h]], writes=[Sb[h]])
                    if d == 1:
                        os_ = ost[it % 2]
                        k.op(k.act, (lambda e, os_=os_, n=n: e.copy(os_[:, 0:n], pout[:, 0:n])), reads=[pout], writes=[os_])
                        k.dma(k.sp, self.obT[rows, t0:t0 + n], os_[:, 0:n], reads=[os_])
                    else:
                        os_ = ost[it % 2]
                        k.op(k.dve, (lambda e, os_=os_, ob_=obt[it % 3], n=n: e.tensor_tensor(out=os_[:, 0:n], in0=pout[:, 0:n], in1=ob_[:, 0:n], op=ALU.add)),
                             reads=[pout, obt[it % 3]], writes=[os_])
                        k.op(k.act, (lambda e, os_=os_, n=n: e.activation(sqo[:, 0:n], os_[:, 0:n], AF.Square)), reads=[os_], writes=[sqo])
                        k.op(k.pe, (lambda e, n=n: e.matmul(pbc[:, 0:n], lhsT=self.ones_bf[:], rhs=sqo[:, 0:n], start=True, stop=True)),
                             reads=[self.ones_bf, sqo], writes=[pbc])
                        k.op(k.act, (lambda e, n=n: e.activation(rs[:, 0:n], pbc[:, 0:n], AF.Ln, bias=self.eps_col[:, 0:1], scale=1.0 / 128)),
                             reads=[pbc, self.eps_col], writes=[rs])
                        k.op(k.act, (lambda e, n=n: e.activation(rs[:, 0:n], rs[:, 0:n], AF.Exp, scale=-0.5)), reads=[rs], writes=[rs])
                        k.op(k.dve, (lambda e, os_=os_, n=n: e.scalar_tensor_tensor(out=os_[:, 0:n], in0=os_[:, 0:n], scalar=onorm_col,
                                                                                   in1=rs[:, 0:n], op0=ALU.mult, op1=ALU.mult)),
                             reads=[os_, rs, self.csm], writes=[os_])
                        fo = fin[it % 2]
                        k.op(k.pool, (lambda e, os_=os_, fo=fo, g_=gt[it % 3], n=n: e.tensor_tensor(out=fo[:, 0:n], in0=os_[:, 0:n], in1=g_[:, 0:n], op=ALU.mult)),
                             reads=[os_, gt[it % 3]], writes=[fo])
                        k.dma(k.sp, self.oT[rows, t0:t0 + n], fo[:, 0:n], reads=[fo])
            units = []
            for ti in order:
                t0, n = tiles[ti]
                ncn = n // 64
                corder = list(range(ncn)) if d == 0 else list(range(ncn))[::-1]
                for h in range(8):
                    units.append((h, t0, n, ncn, corder))

            def record(u):
                h, t0, n, ncn, corder = units[u]
                k.marks = []
                k.set_lane("u")
                unit(h, u % NB, u + 1, t0, n, ncn, corder)
                k.set_lane(None)
                recs = k.lanes.pop("u", [])
                m0 = k.marks[0] if len(k.marks) > 0 else len(recs)
                m1 = k.marks[1] if len(k.marks) > 1 else len(recs)
                return recs[:m0], recs[m0:m1], recs[m1:]

            cur = record(0)
            k.emit_recs(cur[0])
            prev_scan = []
            for u in range(len(units)):
                nxt = record(u + 1) if u + 1 < len(units) else None
                if nxt is not None:
                    k.emit_recs(nxt[0])
                k.emit_interleaved(prev_scan, cur[1])
                prev_scan = cur[2]
                cur = nxt
            k.emit_recs(prev_scan)
            k.end()

    def mixer_hgrn(self, li, nrow):
        self.hg_prep(li)
        self.phase_hg_proj(nrow)
        self.phase_hg_core(self.csm[:, 32:33])
        self.phase_out_proj(self.c_wo)


    def phase_gd_proj(self, j, nrow):
        k = self.k
        W = self.a_w[j]

        def setup(T0, tl):
            ctx = {}
            ctx["tm"] = [k.sb([128, 512], F32, "tmk") for _ in tl]
            for ti, (o, n) in enumerate(tl):
                k.dma(k.sp, ctx["tm"][ti][:, 0:n], self.tmask_rep[:, T0 + o:T0 + o + n], writes=[ctx["tm"][ti]])
            ctx["sg"] = [k.sb([128, 512], F32, "sg") for _ in range(2)]
            ctx["st"] = [k.sb([128, 512], BF16, "st") for _ in range(3)]
            ctx["z"] = [k.sb([128, 16], F32, "z") for _ in range(2)]
            ctx["bat"] = [k.sb([128, 32], F32, "bat") for _ in range(2)]
            na = k.sb([128, 16], F32, "negA")
            k.op(k.act, (lambda e: e.activation(na[:], self.asm[:, 0:16], AF.Exp)), reads=[self.asm], writes=[na])
            k.op(k.dve, (lambda e: e.tensor_scalar(na[:], na[:], -1.0, None, op0=ALU.mult)), reads=[na], writes=[na])
            ctx["negA"] = na
            ctx["i"] = 0
            return ctx

        def fm_pre(c):
            def post(ctx, pt, T0, ti, o, n):
                i = ctx["i"]
                ctx["i"] += 1
                st = ctx["st"][i % 3]
                k.op(k.dve, (lambda e: e.tensor_tensor(out=st[:, 0:n], in0=pt[:, 0:n], in1=ctx["tm"][ti][:, 0:n], op=ALU.mult)),
                     reads=[pt, ctx["tm"][ti]], writes=[st])
                k.dma(k.sp, self.pre[c * 128:(c + 1) * 128, T0 + o:T0 + o + n], st[:, 0:n], reads=[st])
            return post

        def fm_g(c):
            def post(ctx, pt, T0, ti, o, n):
                i = ctx["i"]
                ctx["i"] += 1
                st = ctx["st"][i % 3]
                k.op(k.act, (lambda e: e.activation(st[:, 0:n], pt[:, 0:n], AF.Silu)), reads=[pt], writes=[st])
                k.dma(k.sp, self.fmG[c * 128:(c + 1) * 128, T0 + o:T0 + o + n], st[:, 0:n], reads=[st])
            return post

        def tm_ba(ctx, pt, tok0, r):
            i = ctx["i"]
            ctx["i"] += 1
            z, bat = ctx["z"][i % 2], ctx["bat"][i % 2]
            blk = tok0 // 128
            assert tok0 % 128 == 0
            k.op(k.dve, (lambda e: e.tensor_tensor(out=z[0:r, :], in0=pt[0:r, 16:32], in1=self.asm[0:r, 16:32], op=ALU.add)),
                 reads=[pt, self.asm], writes=[z])
            k.op(k.act, (lambda e: e.activation(z[0:r, :], z[0:r, :], AF.Exp)), reads=[z], writes=[z])
            k.op(k.act, (lambda e: e.activation(z[0:r, :], z[0:r, :], AF.Ln, bias=self.one_col[0:r, 0:1])), reads=[z, self.one_col], writes=[z])
            k.op(k.dve, (lambda e: e.tensor_tensor(out=bat[0:r, 16:32], in0=z[0:r, :], in1=ctx["negA"][0:r, :], op=ALU.mult)),
                 reads=[z, ctx["negA"]], writes=[bat])
            k.op(k.act, (lambda e: e.activation(bat[0:r, 0:16], pt[0:r, 0:16], AF.Sigmoid)), reads=[pt], writes=[bat])
            k.op(k.dve, (lambda e: e.tensor_scalar(bat[0:r, 0:16], bat[0:r, 0:16], self.tmc[0:r, blk:blk + 1], None, op0=ALU.mult)),
                 reads=[bat, self.tmc], writes=[bat])
            k.dma(k.sp, self.ba_tm[tok0:tok0 + r, :], bat[0:r, :], reads=[bat])

        fm = []
        for c in range(24):
            fm.append((W[:, c * 128:(c + 1) * 128], fm_pre(c)))
        for c in range(8):
            fm.append((W[:, 3072 + c * 128:3072 + (c + 1) * 128], fm_g(c)))
        tm = [(W[:, 4096:4128], tm_ba, 32)]
        self.phase_proj(nrow, setup, fm, tm)

    def phase_gd_conv(self):
        k, T = self.k, self.T
        tiles = tiles_of(T, 512)
        QS = 128 ** -0.5
        k.begin()
        xin = [k.sb([128, 516], BF16, "cx") for _ in range(3)]
        acc = [k.sb([128, 512], F32, "cacc") for _ in range(2)]
        sv = [k.sb([128, 512], F32, "csv") for _ in range(2)]
        sq = [k.sb([128, 512], BF16, "csq") for _ in range(2)]
        rs = [k.sb([128, 512], F32, "crs") for _ in range(2)]
        ob = [k.sb([128, 512], BF16, "cob") for _ in range(3)]
        tmt = [k.sb([128, 512], F32, "ctm") for _ in range(2)]
        tb = [k.sb([128, 4, 128], BF16, "ctb") for _ in range(2)]
        pss = [k.ps([128, 512], F32, "cps") for _ in range(2)]
        ptb = [k.ps([128, 512], BF16, "cpt") for _ in range(2)]
        it = 0
        for ti, (t0, n) in enumerate(tiles):
            tmk = tmt[ti % 2]
            k.dma(k.sp, tmk[:, 0:n], self.tmask_rep[:, t0:t0 + n], writes=[tmk])
            for cc in range(24):
                kind = cc // 8
                x = xin[it % 3]
                a_, s_, q_, r_, o_ = acc[it % 2], sv[it % 2], sq[it % 2], rs[it % 2], ob[it % 3]
                ps = pss[it % 2]
                if it % 2 == 0:
                    k.merge()
                k.set_lane(it % 2)
                it += 1
                lo = max(t0 - 2, 0)
                hi = min(t0 + n + 2, T)
                if lo > t0 - 2 or hi < t0 + n + 2:
                    k.op(k.pool, (lambda e, x=x: e.memset(x[:], 0.0)), writes=[x])
                k.dma(k.sp, x[:, lo - (t0 - 2):hi - (t0 - 2)], self.pre[cc * 128:(cc + 1) * 128, lo:hi], writes=[x])
                wcol = lambda jj, cc=cc: self.asm[:, 33 + cc * 5 + jj:34 + cc * 5 + jj]
                k.op(k.dve, (lambda e, a_=a_, x=x, n=n, w=wcol(0): e.tensor_scalar(a_[:, 0:n], x[:, 0:n], w, None, op0=ALU.mult)),
                     reads=[x, self.asm], writes=[a_])
                for jj in range(1, 5):
                    k.op(k.dve, (lambda e, a_=a_, x=x, n=n, jj=jj, w=wcol(jj): e.scalar_tensor_tensor(
                        out=a_[:, 0:n], in0=x[:, jj:jj + n], scalar=w, in1=a_[:, 0:n], op0=ALU.mult, op1=ALU.add)),
                        reads=[x, a_, self.asm], writes=[a_])
                k.op(k.act, (lambda e, a_=a_, s_=s_, n=n: e.activation(s_[:, 0:n], a_[:, 0:n], AF.Silu)), reads=[a_], writes=[s_])
                if kind < 2:
                    k.op(k.act, (lambda e, s_=s_, q_=q_, n=n: e.activation(q_[:, 0:n], s_[:, 0:n], AF.Square)), reads=[s_], writes=[q_])
                    k.op(k.pe, (lambda e, ps=ps, q_=q_, n=n: e.matmul(ps[:, 0:n], lhsT=self.ones_bf[:], rhs=q_[:, 0:n], start=True, stop=True)),
                         reads=[self.ones_bf, q_], writes=[ps])
                    k.op(k.act, (lambda e, ps=ps, r_=r_, n=n: e.activation(r_[:, 0:n], ps[:, 0:n], AF.Ln, bias=self.eps_col[:, 0:1])),
                         reads=[ps, self.eps_col], writes=[r_])
                    k.op(k.act, (lambda e, r_=r_, n=n: e.activation(r_[:, 0:n], r_[:, 0:n], AF.Exp, scale=-0.5)), reads=[r_], writes=[r_])
                if kind == 0:
                    k.op(k.dve, (lambda e, o_=o_, s_=s_, r_=r_, n=n: e.scalar_tensor_tensor(
                        out=o_[:, 0:n], in0=s_[:, 0:n], scalar=QS, in1=r_[:, 0:n], op0=ALU.mult, op1=ALU.mult)),
                        reads=[s_, r_], writes=[o_])
                    k.dma(k.sp, self.fmA[cc * 128:(cc + 1) * 128, t0:t0 + n], o_[:, 0:n], reads=[o_])
                    k.set_lane(None)
                    continue
                if kind == 1:
                    k.op(k.dve, (lambda e, s_=s_, r_=r_, n=n: e.tensor_tensor(out=s_[:, 0:n], in0=s_[:, 0:n], in1=r_[:, 0:n], op=ALU.mult)),
                         reads=[s_, r_], writes=[s_])
                    k.op(k.pool, (lambda e, o_=o_, s_=s_, tmk=tmk, n=n: e.tensor_tensor(out=o_[:, 0:n], in0=s_[:, 0:n], in1=tmk[:, 0:n], op=ALU.mult)),
                         reads=[s_, tmk], writes=[o_])
                    k.dma(k.sp, self.fmB[0][(cc - 8) * 128:(cc - 7) * 128, t0:t0 + n], o_[:, 0:n], reads=[o_])
                    dst = self.tmK[0]
                else:
                    k.op(k.pool, (lambda e, o_=o_, s_=s_, n=n: e.tensor_copy(o_[:, 0:n], s_[:, 0:n])), reads=[s_], writes=[o_])
                    dst = self.v_tm
                cl = (cc % 8)
                pt = ptb[it % 2]
                tt = tb[it % 2]
                nb = (n + 127) // 128
                for b in range(nb):
                    r = min(128, n - b * 128)
                    k.op(k.pe, (lambda e, pt=pt, o_=o_, b=b, r=r: e.transpose(pt[0:r, b * 128:(b + 1) * 128], o_[:, b * 128:b * 128 + r],
                                                                             self.identb[:])),
                         reads=[o_, self.identb], writes=[pt])
                if n % 128 == 0:
                    k.op(k.act, (lambda e, pt=pt, tt=tt, nb=nb: e.copy(tt[:, 0:nb, :], pt[:, 0:nb * 128].rearrange("p (b d) -> p b d", d=128))),
                         reads=[pt], writes=[tt])
                    k.dma(k.sp, dst[t0:t0 + n, cl * 128:(cl + 1) * 128].rearrange("(b p) d -> p b d", p=128), tt[:, 0:nb, :], reads=[tt])
                else:
                    for b in range(nb):
                        r = min(128, n - b * 128)
                        k.op(k.act, (lambda e, pt=pt, tt=tt, b=b, r=r: e.copy(tt[0:r, b, :], pt[0:r, b * 128:(b + 1) * 128])),
                             reads=[pt], writes=[tt])
                        k.dma(k.sp, dst[t0 + b * 128:t0 + b * 128 + r, cl * 128:(cl + 1) * 128], tt[0:r, b, :], reads=[tt])
                k.set_lane(None)
        k.merge()
        k.end()

    def phase_gd_core(self):
        import os
        GDCUT = int(os.environ.get("GD_CUT", "99"))
        GDSUB = int(os.environ.get("GD_SUB", "99"))
        k, nc, T = self.k, self.nc, self.T
        tiles = tiles_of(T, 512)
        onorm_col = self.asm[:, 32:33]
        for sweep in (1, 0):
            d = sweep
            k.begin()
            tri = self.tri
            if d == 0:
                Uin, MS, MI, MN = tri[:, 0:64], 1, 0, 3
            else:
                Uin, MS, MI, MN = tri[:, 128:192], 3, 2, 1
            mI = k.sb([64, 512], F32, "mI")
            mS = k.sb([64, 512], F32, "mS")
            mN = k.sb([64, 512], F32, "mN")
            idr = k.sb([64, 512], F32, "idr")
            for c in range(8):
                cs = slice(c * 64, (c + 1) * 64)
                k.op(k.dve, (lambda e, cs=cs: e.tensor_copy(mI[:, cs], tri[:, MI * 64:(MI + 1) * 64])), reads=[tri], writes=[mI])
                k.op(k.dve, (lambda e, cs=cs: e.tensor_copy(mS[:, cs], tri[:, MS * 64:(MS + 1) * 64])), reads=[tri], writes=[mS])
                k.op(k.dve, (lambda e, cs=cs: e.tensor_copy(mN[:, cs], tri[:, MN * 64:(MN + 1) * 64])), reads=[tri], writes=[mN])
                k.op(k.dve, (lambda e, cs=cs: e.tensor_copy(idr[:, cs], self.ident[0:64, 0:64])), reads=[self.ident], writes=[idr])
            ones_f = k.sb([64, 128], F32, "onesf")
            k.op(k.dve, (lambda e: e.memset(ones_f[:], 1.0)), writes=[ones_f])
            S = [k.sb([128, 128], F32, "S") for _ in range(8)]
            Sb = [k.sb([128, 128], BF16, "Sb") for _ in range(8)]
            for h in range(8):
                k.op(k.dve, (lambda e, h=h: e.memset(S[h][:], 0.0)), writes=[S[h]])
                k.op(k.pool, (lambda e, h=h: e.memset(Sb[h][:], 0.0)), writes=[Sb[h]])
            NB = 2
            mk = lambda shp, dt, nm: [k.sb(shp, dt, nm) for _ in range(NB)]
            qT, kT = mk([128, 512], BF16, "qT"), mk([128, 512], BF16, "kT")
            ktm, vtm = mk([64, 8, 128], BF16, "ktm"), mk([64, 8, 128], BF16, "vtm")
            bat = mk([64, 8, 32], F32, "bat")
            lab = mk([64, 2, 8], F32, "lab")
            gc, egc, bek, dcol = mk([64, 8], F32, "gc"), mk([64, 8], F32, "egc"), mk([64, 8], F32, "bek"), mk([64, 8], F32, "dcol")
            alast = mk([128, 8], F32, "alast")
            dg = mk([64, 512], F32, "dg")
            egr = mk([128, 512], F32, "egr")
            qd = mk([128, 512], BF16, "qd")
            fab = mk([64, 512], F32, "fab")
            fm_ = mk([64, 512], F32, "fm")
            fmi = mk([64, 512], F32, "fmi")
            gf = mk([64, 512], F32, "gf")
            a32 = mk([64, 512], F32, "a32")
            Rb = [mk([64, 512], F32, "Rb%d" % i) for i in range(2)]
            Pb = [mk([64, 512], F32, "Pb%d" % i) for i in range(2)]
            PTb = [mk([64, 512], F32, "PTb%d" % i) for i in range(2)]
            rhu, rhw, kdec = mk([64, 8, 128], F32, "rhu"), mk([64, 8, 128], F32, "rhw"), mk([64, 8, 128], BF16, "kdec")
            gfu = mk([64, 512], F32, "gfu")
            fmS = mk([64, 512], F32, "fmS")
            fmN = mk([64, 512], F32, "fmN")
            dgb = mk([64, 512], F32, "dgb")
            u_sb = mk([64, 8, 128], F32, "u")
            nwT = mk([128, 512], BF16, "nwT")
            qkm = mk([64, 512], BF16, "qkm")
            vn = [[k.sb([64, 128], BF16, "vn") for _ in range(2)] for _ in range(NB)]
            ost = [k.sb([128, 512], F32, "ost") for _ in range(2)]
            B = [k.ps([128, 512], F32, "B%d" % i) for i in range(8)]
            if d == 0:
                obt = [k.sb([128, 512], F32, "obt") for _ in range(3)]
                gt = [k.sb([128, 512], BF16, "gt") for _ in range(3)]
                sqo_l = mk([128, 512], BF16, "sqo")
                rs_l = mk([128, 512], F32, "rs")
                fin = [k.sb([128, 512], BF16, "fin") for _ in range(2)]
            order = list(range(len(tiles)))
            if d == 1:
                order = order[::-1]

            def unit(h, b, it, t0, n, ncn, corder):
                if True:
                    rows = slice(h * 128, (h + 1) * 128)
                    X = B[4 * b:4 * b + 4]
                    Bs, Brow, BG, Bb = X
                    BD, BP, BT = X[1], X[2], X[3]
                    Bq, Bo, Bst, Bv = X[0], X[1], X[2], X[0]
                    k.dma(k.sp, qT[b][:, 0:n], self.fmA[rows, t0:t0 + n], writes=[qT[b]])
                    k.dma(k.sp, kT[b][:, 0:n], self.fmB[0][rows, t0:t0 + n], writes=[kT[b]])
                    k.dma(k.sp, ktm[b][:, 0:ncn, :], self.tmK[0][t0:t0 + n, rows].rearrange("(c p) e -> p c e", p=64), writes=[ktm[b]])
                    k.dma(k.sp, vtm[b][:, 0:ncn, :], self.v_tm[t0:t0 + n, rows].rearrange("(c p) e -> p c e", p=64), writes=[vtm[b]])
                    k.dma(k.sp, bat[b][:, 0:ncn, :], self.ba_tm[t0:t0 + n, :].rearrange("(c p) e -> p c e", p=64), writes=[bat[b]])
                    if d == 0:
                        k.dma(k.sp, obt[it % 3][:, 0:n], self.obT[rows, t0:t0 + n], writes=[obt[it % 3]])
                        k.dma(k.sp, gt[it % 3][:, 0:n], self.fmG[rows, t0:t0 + n], writes=[gt[it % 3]])
                    k.mark()
                    col = d * 8 + h
                    k.op(k.dve, (lambda e, b=b, ncn=ncn, col=col: e.tensor_copy(lab[b][:, 0, 0:ncn], bat[b][:, 0:ncn, col])),
                         reads=[bat[b]], writes=[lab[b]])
                    k.op(k.dve, (lambda e, b=b, ncn=ncn, col=col: e.tensor_copy(lab[b][:, 1, 0:ncn], bat[b][:, 0:ncn, 16 + col])),
                         reads=[bat[b]], writes=[lab[b]])
                    beta = lambda c, b=b: lab[b][:, 0, c:c + 1]
                    k.op(k.pe, (lambda e, b=b, ncn=ncn: e.matmul(Bs[0:64, 0:ncn], lhsT=Uin, rhs=lab[b][:, 1, 0:ncn], start=True, stop=True)),
                         reads=[tri, lab[b]], writes=[Bs])
                    k.op(k.pe, (lambda e, b=b, ncn=ncn: e.matmul(Bs[:, 16:16 + ncn], lhsT=ones_f[:], rhs=lab[b][:, 1, 0:ncn], start=True, stop=True)),
                         reads=[ones_f, lab[b]], writes=[Bs])
                    k.op(k.dve, (lambda e, b=b, ncn=ncn: e.tensor_copy(gc[b][:, 0:ncn], Bs[0:64, 0:ncn])), reads=[Bs], writes=[gc[b]])
                    k.op(k.act, (lambda e, b=b, ncn=ncn: e.activation(alast[b][:, 0:ncn], Bs[:, 16:16 + ncn], AF.Exp)), reads=[Bs], writes=[alast[b]])
                    k.op(k.dve, (lambda e, b=b, ncn=ncn: e.tensor_tensor(out=dcol[b][:, 0:ncn], in0=Bs[0:64, 16:16 + ncn], in1=gc[b][:, 0:ncn],
                                                                        op=ALU.subtract)), reads=[Bs, gc[b]], writes=[dcol[b]])
                    k.op(k.act, (lambda e, b=b, ncn=ncn: e.activation(dcol[b][:, 0:ncn], dcol[b][:, 0:ncn], AF.Exp)), reads=[dcol[b]], writes=[dcol[b]])
                    k.op(k.act, (lambda e, b=b, ncn=ncn: e.activation(egc[b][:, 0:ncn], gc[b][:, 0:ncn], AF.Exp)), reads=[gc[b]], writes=[egc[b]])
                    k.op(k.dve, (lambda e, b=b, ncn=ncn: e.tensor_tensor(out=bek[b][:, 0:ncn], in0=egc[b][:, 0:ncn], in1=lab[b][:, 0, 0:ncn],
                                                                        op=ALU.mult)), reads=[egc[b], lab[b]], writes=[bek[b]])
                    if GDCUT < 1:
                        return
                    k.op(k.dve, (lambda e, b=b, n=n, ncn=ncn: e.tensor_tensor(
                        out=dg[b][:, 0:n].rearrange("p (c i) -> p c i", i=64), in0=idr[:, 0:n].rearrange("p (c i) -> p c i", i=64),
                        in1=gc[b][:, 0:ncn].unsqueeze(2).to_broadcast([64, ncn, 64]), op=ALU.mult)), reads=[idr, gc[b]], writes=[dg[b]])
                    for c in range(ncn):
                        cs = slice(c * 64, (c + 1) * 64)
                        k.op(k.pe, (lambda e, b=b, cs=cs: e.matmul(Brow[:, cs], lhsT=ones_f[:], rhs=dg[b][:, cs], start=True, stop=True)),
                             reads=[ones_f, dg[b]], writes=[Brow])
                    k.op(k.act, (lambda e, b=b, n=n: e.activation(egr[b][:, 0:n], Brow[:, 0:n], AF.Exp)), reads=[Brow], writes=[egr[b]])
                    k.op(k.dve, (lambda e, b=b, n=n: e.tensor_tensor(out=qd[b][:, 0:n], in0=qT[b][:, 0:n], in1=egr[b][:, 0:n], op=ALU.mult)),
                         reads=[qT[b], egr[b]], writes=[qd[b]])
                    k.op(k.dve, (lambda e, b=b, n=n, ncn=ncn: e.tensor_tensor(
                        out=fab[b][:, 0:n].rearrange("p (c i) -> p c i", i=64), in0=Brow[0:64, 0:n].rearrange("p (c i) -> p c i", i=64),
                        in1=gc[b][:, 0:ncn].unsqueeze(2).to_broadcast([64, ncn, 64]), op=ALU.subtract)), reads=[Brow, gc[b]], writes=[fab[b]])
                    k.op(k.act, (lambda e, b=b, n=n: e.activation(fab[b][:, 0:n], fab[b][:, 0:n], AF.Abs)), reads=[fab[b]], writes=[fab[b]])
                    k.op(k.act, (lambda e, b=b, n=n: e.activation(fm_[b][:, 0:n], fab[b][:, 0:n], AF.Exp, scale=-1.0)), reads=[fab[b]], writes=[fm_[b]])
                    k.op(k.pool, (lambda e, b=b, n=n: e.tensor_tensor(out=fmi[b][:, 0:n], in0=fm_[b][:, 0:n], in1=mI[:, 0:n], op=ALU.mult)),
                         reads=[fm_[b], mI], writes=[fmi[b]])
                    k.op(k.pool, (lambda e, b=b, n=n: e.tensor_tensor(out=fmS[b][:, 0:n], in0=fm_[b][:, 0:n], in1=mS[:, 0:n], op=ALU.mult)),
                         reads=[fm_[b], mS], writes=[fmS[b]])
                    k.op(k.pool, (lambda e, b=b, n=n: e.tensor_tensor(out=fmN[b][:, 0:n], in0=fm_[b][:, 0:n], in1=mN[:, 0:n], op=ALU.mult)),
                         reads=[fm_[b], mN], writes=[fmN[b]])
                    if GDCUT < 2:
                        return
                    k.op(k.dve, (lambda e, b=b, n=n, ncn=ncn: e.tensor_tensor(
                        out=dgb[b][:, 0:n].rearrange("p (c i) -> p c i", i=64), in0=idr[:, 0:n].rearrange("p (c i) -> p c i", i=64),
                        in1=lab[b][:, 0, 0:ncn].unsqueeze(2).to_broadcast([64, ncn, 64]), op=ALU.mult)), reads=[idr, lab[b]], writes=[dgb[b]])
                    for c in range(ncn):
                        cs = slice(c * 64, (c + 1) * 64)
                        k.op(k.pe, (lambda e, b=b, cs=cs: e.matmul(BG[0:64, cs], lhsT=kT[b][:, cs], rhs=kT[b][:, cs], start=True, stop=True)),
                             reads=[kT[b]], writes=[BG])
                    for c in range(ncn):
                        cs = slice(c * 64, (c + 1) * 64)
                        k.op(k.pe, (lambda e, b=b, cs=cs: e.matmul(Bb[:, cs], lhsT=ones_f[:], rhs=dgb[b][:, cs], start=True, stop=True)),
                             reads=[ones_f, dgb[b]], writes=[Bb])
                    k.op(k.dve, (lambda e, b=b, n=n: e.tensor_tensor(out=gf[b][:, 0:n], in0=BG[0:64, 0:n], in1=fmS[b][:, 0:n], op=ALU.mult)),
                         reads=[BG, fmS[b]], writes=[gf[b]])
                    k.op(k.dve, (lambda e, b=b, n=n: e.tensor_tensor(out=gfu[b][:, 0:n], in0=BG[0:64, 0:n], in1=fmN[b][:, 0:n], op=ALU.mult)),
                         reads=[BG, fmN[b]], writes=[gfu[b]])
                    R0, P0, PT0 = Rb[0][b], Pb[0][b], PTb[0][b]
                    k.op(k.dve, (lambda e, b=b, n=n, ncn=ncn, PT0=PT0: e.tensor_tensor(
                        out=PT0[:, 0:n].rearrange("p (c i) -> p c i", i=64), in0=gf[b][:, 0:n].rearrange("p (c i) -> p c i", i=64),
                        in1=lab[b][:, 0, 0:ncn].unsqueeze(2).to_broadcast([64, ncn, 64]), op=ALU.mult)), reads=[gf[b], lab[b]], writes=[PT0])
                    k.op(k.dve, (lambda e, b=b, n=n, P0=P0: e.tensor_tensor(out=P0[:, 0:n], in0=Bb[0:64, 0:n], in1=gfu[b][:, 0:n], op=ALU.mult)),
                         reads=[Bb, gfu[b]], writes=[P0])
                    k.op(k.pool, (lambda e, n=n, R0=R0, P0=P0: e.tensor_tensor(out=R0[:, 0:n], in0=idr[:, 0:n], in1=P0[:, 0:n], op=ALU.subtract)),
                         reads=[P0, idr], writes=[R0])
                    if GDCUT < 3:
                        return
                    for lv in range(6):
                        cur, nxt = lv % 2, (lv + 1) % 2
                        Pc, PTc = Pb[cur][b], PTb[cur][b]
                        Pn, PTn = Pb[nxt][b], PTb[nxt][b]
                        Rc, Rn = Rb[(lv + 1) % 2][b], Rb[lv % 2][b]
                        for c in range(ncn):
                            cs = slice(c * 64, (c + 1) * 64)
                            if lv >= 1:
                                k.op(k.pe, (lambda e, cs=cs, PTc=PTc, Rc=Rc: e.matmul(BD[0:64, cs], lhsT=PTc[:, cs], rhs=Rc[:, cs], start=True, stop=True)),
                                     reads=[PTc, Rc], writes=[BD])
                            if lv <= 4:
                                k.op(k.pe, (lambda e, cs=cs, PTc=PTc, Pc=Pc: e.matmul(BP[0:64, cs], lhsT=PTc[:, cs], rhs=Pc[:, cs], start=True, stop=True)),
                                     reads=[PTc, Pc], writes=[BP])
                                k.op(k.pe, (lambda e, cs=cs, PTc=PTc, Pc=Pc: e.matmul(BT[0:64, cs], lhsT=Pc[:, cs], rhs=PTc[:, cs], start=True, stop=True)),
                                     reads=[PTc, Pc], writes=[BT])
                        if lv >= 1:
                            k.op(k.dve, (lambda e, n=n, Rn=Rn, Rc=Rc: e.tensor_tensor(out=Rn[:, 0:n], in0=BD[0:64, 0:n], in1=Rc[:, 0:n], op=ALU.add)),
                                 reads=[BD, Rc], writes=[Rn])
                        if lv <= 4:
                            k.op(k.act, (lambda e, n=n, Pn=Pn: e.copy(Pn[:, 0:n], BP[0:64, 0:n])), reads=[BP], writes=[Pn])
                            k.op(k.act, (lambda e, n=n, PTn=PTn: e.copy(PTn[:, 0:n], BT[0:64, 0:n])), reads=[BT], writes=[PTn])
                    if GDCUT < 4:
                        return
                    Rf = Rb[1][b]
                    k.op(k.dve, (lambda e, b=b, ncn=ncn: e.tensor_tensor(out=rhu[b][:, 0:ncn, :], in0=vtm[b][:, 0:ncn, :],
                                                                        in1=lab[b][:, 0, 0:ncn].unsqueeze(2).to_broadcast([64, ncn, 128]), op=ALU.mult)),
                         reads=[vtm[b], lab[b]], writes=[rhu[b]])
                    k.op(k.pool, (lambda e, b=b, ncn=ncn: e.tensor_tensor(out=rhw[b][:, 0:ncn, :], in0=ktm[b][:, 0:ncn, :],
                                                                         in1=bek[b][:, 0:ncn].unsqueeze(2).to_broadcast([64, ncn, 128]), op=ALU.mult)),
                         reads=[ktm[b], bek[b]], writes=[rhw[b]])
                    k.op(k.pool, (lambda e, b=b, ncn=ncn: e.tensor_tensor(out=kdec[b][:, 0:ncn, :], in0=ktm[b][:, 0:ncn, :],
                                                                         in1=dcol[b][:, 0:ncn].unsqueeze(2).to_broadcast([64, ncn, 128]), op=ALU.mult)),
                         reads=[ktm[b], dcol[b]], writes=[kdec[b]])
                    for c in range(ncn):
                        cs = slice(c * 64, (c + 1) * 64)
                        pu = BD if c < 4 else BP
                        us = slice((c % 4) * 128, (c % 4 + 1) * 128)
                        k.op(k.pe, (lambda e, b=b, c=c, cs=cs, pu=pu, us=us, Rf=Rf: e.matmul(pu[0:64, us], lhsT=Rf[:, cs], rhs=rhu[b][:, c, :], start=True, stop=True)),
                             reads=[Rf, rhu[b]], writes=[pu])
                        k.op(k.pe, (lambda e, b=b, c=c, cs=cs, Rf=Rf: e.matmul(BT[:, cs], lhsT=rhw[b][:, c, :], rhs=Rf[:, cs], start=True, stop=True)),
                             reads=[Rf, rhw[b]], writes=[BT])
                        k.op(k.pe, (lambda e, b=b, cs=cs: e.matmul(Bq[0:64, cs], lhsT=kT[b][:, cs], rhs=qT[b][:, cs], start=True, stop=True)),
                             reads=[kT[b], qT[b]], writes=[Bq])
                    n4 = min(ncn, 4)
                    k.op(k.act, (lambda e, b=b, n4=n4: e.copy(u_sb[b][:, 0:n4, :], BD[0:64, 0:n4 * 128].rearrange("p (c d) -> p c d", d=128))),
                         reads=[BD], writes=[u_sb[b]])
                    if ncn > 4:
                        k.op(k.act, (lambda e, b=b, ncn=ncn: e.copy(u_sb[b][:, 4:ncn, :], BP[0:64, 0:(ncn - 4) * 128].rearrange("p (c d) -> p c d", d=128))),
                             reads=[BP], writes=[u_sb[b]])
                    k.op(k.dve, (lambda e, b=b, n=n: e.tensor_scalar(nwT[b][:, 0:n], BT[:, 0:n], -1.0, None, op0=ALU.mult)), reads=[BT], writes=[nwT[b]])
                    k.op(k.dve, (lambda e, b=b, n=n: e.tensor_tensor(out=qkm[b][:, 0:n], in0=Bq[0:64, 0:n], in1=fmi[b][:, 0:n], op=ALU.mult)),
                         reads=[Bq, fmi[b]], writes=[qkm[b]])
                    if GDCUT < 5:
                        return
                    k.mark()
                    for c in corder:
                        cs = slice(c * 64, (c + 1) * 64)
                        v_ = vn[b][c % 2]
                        k.op(k.pe, (lambda e, b=b, cs=cs, h=h: e.matmul(Bv[0:64, 256:384], lhsT=nwT[b][:, cs], rhs=Sb[h][:], start=True, stop=True)),
                             reads=[nwT[b], Sb[h]], writes=[Bv])
                        k.op(k.dve, (lambda e, b=b, c=c, v_=v_: e.tensor_tensor(out=v_[:], in0=Bv[0:64, 256:384], in1=u_sb[b][:, c, :], op=ALU.add)),
                             reads=[Bv, u_sb[b]], writes=[v_])
                        k.op(k.pe, (lambda e, b=b, cs=cs, h=h: e.matmul(Bo[:, cs], lhsT=Sb[h][:], rhs=qd[b][:, cs], start=True, stop=False)),
                             reads=[Sb[h], qd[b]], writes=[Bo])
                        k.op(k.pe, (lambda e, b=b, cs=cs, v_=v_: e.matmul(Bo[:, cs], lhsT=v_[:], rhs=qkm[b][:, cs], start=False, stop=True)),
                             reads=[v_, qkm[b]], writes=[Bo])
                        k.op(k.pe, (lambda e, b=b, c=c, v_=v_: e.matmul(Bs[:, 128:256], lhsT=kdec[b][:, c, :], rhs=v_[:], start=True, stop=True)),
                             reads=[kdec[b], v_], writes=[Bs])
                        k.op(k.dve, (lambda e, b=b, h=h, c=c: e.scalar_tensor_tensor(
                            out=S[h][:], in0=S[h][:], scalar=alast[b][:, c:c + 1], in1=Bs[:, 128:256], op0=ALU.mult, op1=ALU.add)),
                            reads=[S[h], alast[b], Bs], writes=[S[h]])
                        k.op(k.act, (lambda e, h=h: e.copy(Sb[h][:], S[h][:])), reads=[S[h]], writes=[Sb[h]])
                    if GDCUT < 6:
                        return
                    os_ = ost[it % 2]
                    if d == 1:
                        k.op(k.act, (lambda e, os_=os_, n=n: e.copy(os_[:, 0:n], Bo[:, 0:n])), reads=[Bo], writes=[os_])
                        k.dma(k.sp, self.obT[rows, t0:t0 + n], os_[:, 0:n], reads=[os_])
                    else:
                        sqo, rs = sqo_l[b], rs_l[b]
                        k.op(k.dve, (lambda e, os_=os_, ob_=obt[it % 3], n=n: e.tensor_tensor(out=os_[:, 0:n], in0=Bo[:, 0:n], in1=ob_[:, 0:n], op=ALU.add)),
                             reads=[Bo, obt[it % 3]], writes=[os_])
                        k.op(k.act, (lambda e, os_=os_, n=n: e.activation(sqo[:, 0:n], os_[:, 0:n], AF.Square)), reads=[os_], writes=[sqo])
                        k.op(k.pe, (lambda e, n=n: e.matmul(Bst[:, 0:n], lhsT=self.ones_bf[:], rhs=sqo[:, 0:n], start=True, stop=True)),
                             reads=[self.ones_bf, sqo], writes=[Bst])
                        k.op(k.act, (lambda e, n=n: e.activation(rs[:, 0:n], Bst[:, 0:n], AF.Ln, bias=self.eps_col[:, 0:1], scale=1.0 / 128)),
                             reads=[Bst, self.eps_col], writes=[rs])
                        k.op(k.act, (lambda e, n=n: e.activation(rs[:, 0:n], rs[:, 0:n], AF.Exp, scale=-0.5)), reads=[rs], writes=[rs])
                        k.op(k.dve, (lambda e, os_=os_, n=n: e.scalar_tensor_tensor(out=os_[:, 0:n], in0=os_[:, 0:n], scalar=onorm_col,
                                                                                   in1=rs[:, 0:n], op0=ALU.mult, op1=ALU.mult)),
                             reads=[os_, rs, self.asm], writes=[os_])
                        fo = fin[it % 2]
                        k.op(k.pool, (lambda e, os_=os_, fo=fo, g_=gt[it % 3], n=n: e.tensor_tensor(out=fo[:, 0:n], in0=os_[:, 0:n], in1=g_[:, 0:n], op=ALU.mult)),
                             reads=[os_, gt[it % 3]], writes=[fo])
                        k.dma(k.sp, self.oT[rows, t0:t0 + n], fo[:, 0:n], reads=[fo])
            units = []
            for ti in order:
                t0, n = tiles[ti]
                ncn = n // 64
                corder = list(range(ncn)) if d == 0 else list(range(ncn))[::-1]
                for h in range(8):
                    units.append((h, t0, n, ncn, corder))

            def record(u):
                h, t0, n, ncn, corder = units[u]
                k.marks = []
                k.set_lane("u")
                unit(h, u % NB, u + 1, t0, n, ncn, corder)
                k.set_lane(None)
                recs = k.lanes.pop("u", [])
                m0 = k.marks[0] if len(k.marks) > 0 else len(recs)
                m1 = k.marks[1] if len(k.marks) > 1 else len(recs)
                return recs[:m0], recs[m0:m1], recs[m1:]

            cur = record(0)
            k.emit_recs(cur[0])
            prev_scan = []
            for u in range(len(units)):
                nxt = record(u + 1) if u + 1 < len(units) else None
                if nxt is not None:
                    k.emit_recs(nxt[0])
                k.emit_interleaved(prev_scan, cur[1])
                prev_scan = cur[2]
                cur = nxt
            k.emit_recs(prev_scan)
            if d == 1 and os.environ.get("DEBUG_DUMP"):
                o = self.nc.dram_tensor("dbg_S0", [128, 128], F32, kind="ExternalOutput").ap()
                k.dma(k.sp, o, S[0][:], reads=[S[0]])
                o = self.nc.dram_tensor("dbg_u", [64, 8 * 128], F32, kind="ExternalOutput").ap()
                k.dma(k.sp, o, u_sb[0][:].rearrange("p c d -> p (c d)"), reads=[u_sb[0]])
                o = self.nc.dram_tensor("dbg_rhu", [64, 8 * 128], BF16, kind="ExternalOutput").ap()
                k.dma(k.sp, o, rhu[0][:].rearrange("p c d -> p (c d)"), reads=[rhu[0]])
                o = self.nc.dram_tensor("dbg_R", [64, 512], BF16, kind="ExternalOutput").ap()
                k.dma(k.sp, o, Rb[0][0][:], reads=[Rb[0][0]])
                o = self.nc.dram_tensor("dbg_alast", [128, 8], F32, kind="ExternalOutput").ap()
                k.dma(k.sp, o, alast[0][:], reads=[alast[0]])
            k.end()

    def mixer_gdn(self, j, nrow):
        k = self.k
        k.begin()
        k.dma(k.sp, self.asm[:], self.a_small[j], writes=[self.asm])
        k.end()
        import os
        st = int(os.environ.get("GD_STAGE", "9"))
        self.phase_gd_proj(j, nrow)
        if st >= 2:
            self.phase_gd_conv()
        if st >= 3:
            self.phase_gd_core()
        if st >= 4:
            self.phase_out_proj(self.a_wo[j])

    def build(self):
        k = self.k
        self.eps_col = k.gsb([128, 1], F32, "epscol")
        k.begin()
        k.op(k.dve, lambda e: e.memset(self.eps_col[:], EPS), writes=[self.eps_col])
        self.bsm = k.gsb([128, 260], F32, "bsm")
        k.dma(k.sp, self.bsm[:], self.b_small, writes=[self.bsm])
        self.csm = k.gsb([128, 33], F32, "csm")
        k.dma(k.sp, self.csm[:], self.c_small, writes=[self.csm])
        self.tri = k.gsb([64, 256], F32, "tric")
        k.dma(k.sp, self.tri[:], self.tri_in, writes=[self.tri])
        self.tmc = k.gsb([128, (self.T + 127) // 128], F32, "tmc")
        k.dma(k.sp, self.tmc[:], self.tmask_col, writes=[self.tmc])
        self.one_col = k.gsb([128, 1], F32, "onecol")
        k.op(k.dve, lambda e: e.memset(self.one_col[:], 1.0), writes=[self.one_col])
        self.lb_col = k.gsb([128, 16], F32, "lbcol")
        self.asm = k.gsb([128, 153], F32, "asm")
        self.identb = k.gsb([128, 128], BF16, "identbc")
        k.dma(k.sp, self.identb[:], self.identb_in, writes=[self.identb])
        self.oml_row = k.gsb([128, D_MODEL], F32, "omlrow")
        k.end()
        self.phase_in()
        if self.only == "gdn":
            self.mixer_gdn(0, 0 * 3 + 1)
            self.phase_out(self.depth * 3)
            import os
            if os.environ.get("DEBUG_DUMP"):
                k.begin()
                dummy = k.sb([128, 4], F32, "dummy")
                for nm, ap, shp, dt in (("fmA", self.fmA, [D_MODEL, self.T], BF16), ("fmB0", self.fmB[0], [D_MODEL, self.T], BF16),
                                        ("tmK0", self.tmK[0], [self.T, D_MODEL], BF16), ("v_tm", self.v_tm, [self.T, D_MODEL], BF16),
                                        ("ba_tm", self.ba_tm, [self.T, 32], F32), ("obT", self.obT, [D_MODEL, self.T], F32),
                                        ("oT", self.oT, [D_MODEL, self.T], BF16), ("fmG", self.fmG, [D_MODEL, self.T], BF16)):
                    o = self.nc.dram_tensor("dbg_" + nm, shp, dt, kind="ExternalOutput").ap()
                    k.dma(k.sp, o, ap, reads=[dummy])
                k.end()
            return
        if self.only == "hgrn":
            self.mixer_hgrn(2, 2 * 3 + 1)
            self.phase_out(self.depth * 3)
            return
        if self.only == "attn":
            self.mixer_attn(1 * 3 + 1)
            self.phase_out(self.depth * 3)
            return
        for li in range(self.depth):
            if self.ffn:
                self.phase_ffn(li, 0, li * 3 + 0)
            if self.mixers and li % 3 == 1:
                self.mixer_attn(li * 3 + 1)
            if self.mixers and li % 3 == 2:
                self.mixer_hgrn(li, li * 3 + 1)
            if self.mixers and li % 3 == 0:
                self.mixer_gdn(li // 3, li * 3 + 1)
            if self.ffn:
                self.phase_ffn(li, 1, li * 3 + 2)
        self.phase_out(self.depth * 3)


def col_layout(a):
    a = np.asarray(a, np.float32)
    R = a.shape[0]
    C = a.shape[1] // 128
    return np.ascontiguousarray(a.reshape(R, C, 128).transpose(2, 0, 1).reshape(128, R * C))


def attn_host_inputs(b_w_in, b_lambda, b_sub_norm, layer_idx, T, n_valid):
    f32 = np.float32
    w = np.asarray(b_w_in, f32)
    perm = np.arange(2048)
    d = perm % 64
    perm = np.where(d < 8, perm + 8, np.where(d < 16, perm - 8, perm))
    w_sw = np.ascontiguousarray(w[:, perm])
    sm = np.zeros((128, 260), f32)
    sm[:, 0:256] = np.asarray(b_lambda, f32).reshape(1, 256)
    sm[:, 256] = np.asarray(b_sub_norm, f32).reshape(128)
    dd = np.arange(128) % 64
    ii = np.where(dd < 8, dd, dd - 8).astype(np.float64)
    invf = np.exp(-np.log(500000.0) * ii / 8.0)
    sm[:, 257] = np.where(dd < 16, invf, 0.0)
    sm[:, 258] = np.where(dd < 8, -1.0, np.where(dd < 16, 1.0, 0.0))
    sm[:, 259] = 0.8 - 0.6 * np.exp(-0.3 * layer_idx)
    nkt = (T + 127) // 128
    tok = np.arange(nkt * 128)
    kb = np.where(tok < n_valid, 0.0, -30000.0).astype(f32).reshape(nkt, 128).T
    return w_sw, sm, np.ascontiguousarray(kb)


def common_host_inputs(T, n_valid):
    f32 = np.float32
    j = np.arange(64)[:, None]
    i = np.arange(64)[None, :]
    tri = np.concatenate([(j <= i), (j > i), (j >= i), (j < i)], axis=1).astype(f32)
    tok = np.arange(T)
    tm = (tok < n_valid).astype(f32)
    tmask_rep = np.ascontiguousarray(np.broadcast_to(tm[None, :], (128, T)))
    nb = (T + 127) // 128
    tmp = np.zeros(nb * 128, f32)
    tmp[:T] = tm
    tmask_col = np.ascontiguousarray(tmp.reshape(nb, 128).T)
    return {"tri": tri, "tmask_rep": tmask_rep, "tmask_col": tmask_col, "ident": np.eye(128, dtype=f32)}


def hgrn_host_inputs(c_lb_logits, c_o_norm):
    sm = np.zeros((128, 33), np.float32)
    sm[:, 0:32] = col_layout(np.asarray(c_lb_logits, np.float32))
    sm[:, 32] = np.asarray(c_o_norm, np.float32).reshape(128)
    return sm


def gdn_host_inputs(a_log, a_dt_bias, a_o_norm, a_conv_w):
    f32 = np.float32
    n_a = np.asarray(a_log).shape[0]
    out = np.zeros((n_a, 128, 153), f32)
    for j in range(n_a):
        out[j, :, 0:16] = np.asarray(a_log[j], f32).reshape(1, 16)
        out[j, :, 16:32] = np.asarray(a_dt_bias[j], f32).reshape(1, 16)
        out[j, :, 32] = np.asarray(a_o_norm[j], f32).reshape(128)
        cw = np.asarray(a_conv_w[j], f32)
        out[j, :, 33:153] = cw.reshape(5, 24, 128).transpose(2, 1, 0).reshape(128, 120)
    return out


_PROG_CACHE = {}


def get_prog(T, depth, n_groups):
    key = (T, depth, n_groups)
    if key not in _PROG_CACHE:
        _PROG_CACHE[key] = Prog(T, depth, n_groups)
    return _PROG_CACHE[key]


T_FULL = 8256
SEQ_P = 8192
SEQ_S = 4096


def kernel(x_prompt, x_sample, meta_tokens, norm_w, ffn_w_up, ffn_w_down,
           a_w_in, a_conv_w, a_log, a_dt_bias, a_o_norm, a_w_out,
           b_w_in, b_lambda, b_sub_norm, b_w_out,
           c_w_in, c_lb_logits, c_o_norm, c_w_out, final_norm):
    import ml_dtypes
    depth = norm_w.shape[0]
    T = T_FULL
    prog = get_prog(T, depth, 4)
    f32 = np.float32
    seqs = [x_prompt[0], x_prompt[1], x_sample[0], x_sample[1], x_sample[2], x_sample[3], x_sample[0], x_sample[1]]
    nw = np.concatenate([np.asarray(norm_w, f32).reshape(depth * 3, D_MODEL), np.asarray(final_norm, f32)[None]], 0)
    nw = col_layout(nw)
    shared = {
        "norm_w": nw,
        "ffn_w_up": np.asarray(ffn_w_up, f32), "ffn_w_down": np.asarray(ffn_w_down, f32),
        "b_w_in": np.asarray(b_w_in[0], f32), "b_w_out": np.asarray(b_w_out[0], f32),
        "c_w_in": np.asarray(c_w_in[0], f32), "c_w_out": np.asarray(c_w_out[0], f32),
        "c_small": hgrn_host_inputs(c_lb_logits, c_o_norm[0]),
        "a_w_in": np.asarray(a_w_in, f32), "a_w_out": np.asarray(a_w_out, f32),
        "a_small": gdn_host_inputs(a_log, a_dt_bias, a_o_norm, a_conv_w),
        "identb": np.eye(128, dtype=f32).astype(ml_dtypes.bfloat16),
    }
    in_maps = []
    per_len = {}
    for s in seqs:
        L = s.shape[0]
        nv = N_META + L
        if L not in per_len:
            w_sw, sm, kb = attn_host_inputs(b_w_in[0], b_lambda[0], b_sub_norm[0], 1, T, nv)
            d = {"b_w_sw": w_sw, "b_small": sm, "kbias": kb}
            d.update(common_host_inputs(T, nv))
            per_len[L] = d
        xin = np.zeros((T, D_MODEL), f32)
        xin[:N_META] = meta_tokens
        xin[N_META:nv] = s
        m = {"xin": xin}
        m.update(shared)
        m.update(per_len[L])
        in_maps.append(m)
    res = run_bass_kernel_spmd(prog.nc, in_maps, core_ids=list(range(8)))
    outs = [r["yout"] for r in res.results]
    y_prompt = np.stack([outs[0][N_META:N_META + SEQ_P], outs[1][N_META:N_META + SEQ_P]], 0).astype(f32)
    y_sample = np.stack([outs[i][N_META:N_META + SEQ_S] for i in range(2, 6)], 0).astype(f32)
    return (y_prompt, y_sample)
```

```python
import numpy as np
from contextlib import ExitStack
import concourse.bass as bass
import concourse.mybir as mybir
from concourse.bass_utils import run_bass_kernel_spmd

F32 = mybir.dt.float32
BF16 = mybir.dt.bfloat16
ALU = mybir.AluOpType
AF = mybir.ActivationFunctionType
AX = mybir.AxisListType

D_MODEL = 1024
D_FF = 2816
N_META = 16
EPS = 1e-6


class Eng:
    def __init__(self, name, sem):
        self.name = name
        self.sem = sem
        self.cnt = 0
        self.seen = {}
        self.ops = []


class Tl:
    def __init__(self, t, name):
        self.t = t
        self.name = name
        self.w = None
        self.r = {}
        self.dsem = None

    def __getitem__(self, idx):
        return self.t[idx]

    def view(self):
        return Tl(self.t, self.name)


class KB:
    SAME_ENGINE_SYNC = True

    def __init__(self, nc, n_dma_sems=84):
        self.nc = nc
        self.es = ExitStack()
        self.engs = {}
        for name in ("pe", "act", "dve", "pool", "sp"):
            sem = self.es.enter_context(nc.semaphore("s_" + name))
            self.engs[name] = Eng(name, sem)
        self.pe, self.act, self.dve, self.pool, self.sp = (self.engs[n] for n in ("pe", "act", "dve", "pool", "sp"))
        self.dma_pool = []
        for i in range(n_dma_sems):
            sem = self.es.enter_context(nc.semaphore("s_d%d" % i))
            self.dma_pool.append([sem, 0])
        self.dma_free = list(range(n_dma_sems))
        self.phase_tiles = []
        self.dma_tiles = []
        self.lanes = {}
        self.marks = []
        self.cur_lane = None
        self.pes = None
        self.uid = 0
        self.nops = 0

    def begin(self):
        self.pes = ExitStack()
        self.phase_tiles = []

    def sb(self, shape, dt, name=None):
        self.uid += 1
        name = "%s_%d" % (name or "t", self.uid)
        t = self.pes.enter_context(self.nc.sbuf_tensor(name, list(shape), dt))
        tl = Tl(t, name)
        self.phase_tiles.append(tl)
        return tl

    def views(self, tl, n):
        vs = [tl.view() for _ in range(n)]
        self.phase_tiles.extend(vs)
        return vs

    def ps(self, shape, dt=F32, name=None):
        self.uid += 1
        name = "%s_%d" % (name or "p", self.uid)
        t = self.pes.enter_context(self.nc.psum_tensor(name, list(shape), dt))
        tl = Tl(t, name)
        self.phase_tiles.append(tl)
        return tl

    def gsb(self, shape, dt, name):
        t = self.es.enter_context(self.nc.sbuf_tensor(name, list(shape), dt))
        return Tl(t, name)

    def _deps(self, eng, reads, writes):
        deps = []
        for tl in reads:
            if tl.w is not None:
                deps.append(tl.w)
        for tl in writes:
            if tl.w is not None and tl.w[0] != eng.name:
                deps.append(tl.w)
            deps.extend(v for v in tl.r.values() if v[0] != eng.name)
        waits = []
        for key, sem, cnt in deps:
            if key == eng.name and (eng.name in ("pe", "sp") or not self.SAME_ENGINE_SYNC):
                continue
            if eng.seen.get(key, 0) < cnt:
                eng.seen[key] = cnt
                waits.append((sem, cnt))
        return waits

    def set_lane(self, lane):
        self.cur_lane = lane

    def merge(self):
        lanes = [v for _, v in sorted(self.lanes.items()) if v]
        self.lanes = {}
        save, self.cur_lane = self.cur_lane, None
        idx = [0] * len(lanes)
        left = sum(len(l) for l in lanes)
        while left:
            for li, l in enumerate(lanes):
                if idx[li] < len(l):
                    rec = l[idx[li]]
                    idx[li] += 1
                    left -= 1
                    if rec[0] == "op":
                        self.op(*rec[1:])
                    else:
                        self.dma(rec[1], rec[2], rec[3], rec[4], rec[5], **rec[6])
        self.cur_lane = save

    def mark(self):
        self.marks.append(len(self.lanes.get(self.cur_lane, [])))

    def emit_recs(self, recs):
        for rec in recs:
            if rec[0] == "op":
                self.op(*rec[1:])
            else:
                self.dma(rec[1], rec[2], rec[3], rec[4], rec[5], **rec[6])

    def emit_interleaved(self, a, b):
        save, self.cur_lane = self.cur_lane, None
        na, nb = len(a), len(b)
        ia = ib = 0
        while ia < na or ib < nb:
            if ib >= nb or (ia < na and ia * nb <= ib * na):
                self.emit_recs([a[ia]])
                ia += 1
            else:
                self.emit_recs([b[ib]])
                ib += 1
        self.cur_lane = save

    def op(self, eng, fn, reads=(), writes=()):
        if self.cur_lane is not None:
            self.lanes.setdefault(self.cur_lane, []).append(("op", eng, fn, tuple(reads), tuple(writes)))
            return
        waits = self._deps(eng, reads, writes)
        eng.cnt += 1
        me = (eng.name, eng.sem, eng.cnt)
        eng.ops.append((waits, fn, (eng.sem, 1)))
        for tl in writes:
            tl.w = me
            tl.r = {}
        for tl in reads:
            if tl not in writes:
                tl.r[eng.name] = me
        self.nops += 1

    def dma(self, q, out, in_, reads=(), writes=(), **kw):
        if self.cur_lane is not None:
            self.lanes.setdefault(self.cur_lane, []).append(("dma", q, out, in_, tuple(reads), tuple(writes), kw))
            return
        tl = (list(writes) + list(reads))[0]
        if tl.dsem is None:
            tl.dsem = self.dma_free.pop()
            self.dma_tiles.append(tl)
        slot = self.dma_pool[tl.dsem]
        waits = self._deps(q, reads, writes)
        slot[1] += 16
        key = "d%d" % tl.dsem
        me = (key, slot[0], slot[1])
        q.ops.append((waits, (lambda e, o=out, i=in_, k=kw: e.dma_start(out=o, in_=i, **k)), (slot[0], 16)))
        for t in writes:
            t.w = me
            t.r = {}
        for t in reads:
            t.r[key] = me
        self.nops += 1

    def end(self):
        used = set()
        for tl in self.dma_tiles:
            if tl.dsem is not None:
                used.add(tl.dsem)
        for d in sorted(used):
            sem, cnt = self.dma_pool[d]
            key = "d%d" % d
            if self.sp.seen.get(key, 0) < cnt:
                self.sp.seen[key] = cnt
                self.sp.ops.append(([(sem, cnt)], None, None))
        for e in (self.pe, self.act, self.dve, self.pool):
            if self.sp.seen.get(e.name, 0) < e.cnt:
                self.sp.seen[e.name] = e.cnt
                self.sp.ops.append(([(e.sem, e.cnt)], None, None))
        self.sp.cnt += 1
        self.sp.ops.append(([], (lambda e: e.nop()), (self.sp.sem, 1)))
        for e in (self.pe, self.act, self.dve, self.pool):
            e.ops.append(([(self.sp.sem, self.sp.cnt)], None, None))
        with self.nc.Block() as block:
            for name, deco in (("pe", block.tensor), ("act", block.scalar), ("dve", block.vector),
                               ("pool", block.gpsimd), ("sp", block.sync)):
                eng = self.engs[name]
                ops = eng.ops
                eng.ops = []

                def body(e, ops=ops):
                    for waits, fn, inc in ops:
                        for sem, cnt in waits:
                            e.wait_ge(sem, cnt)
                        if fn is not None:
                            ins = fn(e)
                            if inc is not None:
                                ins.then_inc(inc[0], inc[1])

                deco(body)
        for d in used:
            self.dma_free.append(d)
        for tl in self.dma_tiles:
            tl.dsem = None
        self.dma_tiles = []
        for e in self.engs.values():
            for o in self.engs.values():
                e.seen[o.name] = o.cnt
            for d in range(len(self.dma_pool)):
                e.seen["d%d" % d] = self.dma_pool[d][1]
        self.pes.close()
        self.pes = None

    def close(self):
        self.es.close()


def tiles_of(n, step=512):
    out = []
    s = 0
    while s < n:
        out.append((s, min(step, n - s)))
        s += step
    return out


class Prog:
    def __init__(self, T, depth, n_groups, mixers=True, ffn=True, only=None):
        self.only = only
        self.mixers = mixers
        self.ffn = ffn
        self.T = T
        self.depth = depth
        self.n_groups = n_groups
        base = (T // n_groups) // 512 * 512 if n_groups > 1 else T
        self.groups = [(g * base, base) for g in range(n_groups - 1)]
        self.groups.append(((n_groups - 1) * base, T - (n_groups - 1) * base))
        nc = bass.Bass("TRN2", target_bir_lowering=False)
        self.nc = nc
        d = nc.dram_tensor
        self.xin = d("xin", [T, D_MODEL], F32, kind="ExternalInput").ap()
        self.yout = d("yout", [T, D_MODEL], F32, kind="ExternalOutput").ap()
        self.norm_w = d("norm_w", [128, (depth * 3 + 1) * 8], F32, kind="ExternalInput").ap()
        self.w_up = d("ffn_w_up", [depth, 2, D_MODEL, 2 * D_FF], F32, kind="ExternalInput").ap()
        self.w_dn = d("ffn_w_down", [depth, 2, D_FF, D_MODEL], F32, kind="ExternalInput").ap()
        self.ident_in = d("ident", [128, 128], F32, kind="ExternalInput").ap()
        self.hT = d("hT", [D_MODEL, T], F32).ap()
        self.b_w = d("b_w_in", [D_MODEL, 3072], F32, kind="ExternalInput").ap()
        self.b_wsw = d("b_w_sw", [D_MODEL, 2048], F32, kind="ExternalInput").ap()
        self.b_wo = d("b_w_out", [D_MODEL, D_MODEL], F32, kind="ExternalInput").ap()
        self.b_small = d("b_small", [128, 256 + 4], F32, kind="ExternalInput").ap()
        self.kbias_in = d("kbias", [128, (T + 127) // 128], F32, kind="ExternalInput").ap()
        self.qkT = d("qkT", [2048, T], BF16).ap()
        self.v_tm = d("v_tm", [T, 1024], BF16).ap()
        self.oT = d("oT", [D_MODEL, T], BF16).ap()
        self.tri_in = d("tri", [64, 4 * 64], F32, kind="ExternalInput").ap()
        self.tmask_rep = d("tmask_rep", [128, T], F32, kind="ExternalInput").ap()
        self.tmask_col = d("tmask_col", [128, (T + 127) // 128], F32, kind="ExternalInput").ap()
        self.c_w = d("c_w_in", [D_MODEL, 5120], F32, kind="ExternalInput").ap()
        self.c_wo = d("c_w_out", [D_MODEL, D_MODEL], F32, kind="ExternalInput").ap()
        self.c_small = d("c_small", [128, 32 + 1], F32, kind="ExternalInput").ap()
        self.fmA = d("fmA", [D_MODEL, T], BF16).ap()
        self.fmB = [d("fmB%d" % i, [D_MODEL, T], BF16).ap() for i in range(2)]
        self.fmG = d("fmG", [D_MODEL, T], BF16).ap()
        self.tmK = [d("tmK%d" % i, [T, D_MODEL], BF16).ap() for i in range(2)]
        self.tmL = [d("tmL%d" % i, [T, D_MODEL], F32).ap() for i in range(2)]
        self.obT = d("obT", [D_MODEL, T], F32).ap()
        self.lbrow_t = d("lbrow", [2, D_MODEL], F32)
        self.n_a = (depth + 2) // 3
        self.a_w = d("a_w_in", [self.n_a, D_MODEL, 4128], F32, kind="ExternalInput").ap()
        self.a_wo = d("a_w_out", [self.n_a, D_MODEL, D_MODEL], F32, kind="ExternalInput").ap()
        self.a_small = d("a_small", [self.n_a, 128, 32 + 1 + 120], F32, kind="ExternalInput").ap()
        self.pre = d("pre", [3 * D_MODEL, T], BF16).ap()
        self.ba_tm = d("ba_tm", [T, 32], F32).ap()
        self.identb_in = d("identb", [128, 128], BF16, kind="ExternalInput").ap()
        self.k = KB(nc)
        k = self.k
        self.ident = k.gsb([128, 128], F32, "identc")
        self.ones_bf = k.gsb([128, 128], BF16, "onesbf")
        self.nw = k.gsb([128, (depth * 3 + 1) * 8], F32, "nwcol")
        self.build()
        k.close()

    def phase_in(self):
        k, nc, T = self.k, self.nc, self.T
        k.begin()
        k.dma(k.sp, self.ident[:], self.ident_in, writes=[self.ident])
        k.op(k.dve, lambda e: e.memset(self.ones_bf[:], 1.0), writes=[self.ones_bf])
        nrows = self.depth * 3 + 1
        k.dma(k.sp, self.nw[:], self.norm_w, writes=[self.nw])
        xt = [k.sb([128, 4, D_MODEL], F32, "xt") for _ in range(2)]
        st = [k.sb([128, 8, 512], F32, "st") for _ in range(2)]
        pst = [k.ps([128, 512], F32, "pst") for _ in range(4)]
        pi = 0
        for gi, (t0, n) in enumerate(tiles_of(T, 512)):
            x = xt[gi % 2]
            s = st[gi % 2]
            nb = (n + 127) // 128
            blocks = [(b * 128, min(128, n - b * 128)) for b in range(nb)]
            if n % 128 == 0:
                k.dma(k.sp, x[:, 0:nb, :], self.xin[t0:t0 + n, :].rearrange("(j p) f -> p j f", p=128), writes=[x])
            else:
                for b, (o, r) in enumerate(blocks):
                    k.dma(k.sp, x[0:r, b, :], self.xin[t0 + o:t0 + o + r, :], writes=[x])
            for c in range(8):
                p = pst[pi % 4]
                pi += 1
                for b, (o, r) in enumerate(blocks):
                    k.op(k.pe, (lambda e, p=p, x=x, b=b, o=o, r=r, c=c: e.transpose(
                        p[:, o:o + r], x[0:r, b, c * 128:(c + 1) * 128], self.ident[0:r, 0:r])),
                        reads=[x, self.ident], writes=[p])
                eng = k.dve if c % 2 == 0 else k.act
                if eng is k.dve:
                    k.op(eng, (lambda e, p=p, s=s, c=c, n=n: e.tensor_copy(s[:, c, 0:n], p[:, 0:n])), reads=[p], writes=[s])
                else:
                    k.op(eng, (lambda e, p=p, s=s, c=c, n=n: e.copy(s[:, c, 0:n], p[:, 0:n])), reads=[p], writes=[s])
            k.dma(k.sp, self.hT.rearrange("(c p) t -> p c t", p=128)[:, :, t0:t0 + n], s[:, :, 0:n], reads=[s])
        k.end()


    def emit_norm(self, y, yv, hn, hnv, tl, nrow, pd, sq, rstd):
        k = self.k
        for ti, (o, n) in enumerate(tl):
            ss = pd[ti % 4]
            for c in range(8):
                s_ = sq[c % 2]
                k.op(k.act, (lambda e, s_=s_, c=c, o=o, n=n: e.activation(s_[:, 0:n], y[:, c, o:o + n], AF.Square)),
                     reads=[yv[c][ti]], writes=[s_])
                k.op(k.pe, (lambda e, ss=ss, s_=s_, c=c, n=n: e.matmul(ss[:, 0:n], lhsT=self.ones_bf[:], rhs=s_[:, 0:n],
                                                                        start=(c == 0), stop=(c == 7))),
                     reads=[s_, self.ones_bf], writes=[ss])
            r = rstd[ti % 2]
            k.op(k.act, (lambda e, r=r, ss=ss, n=n: e.activation(r[:, 0:n], ss[:, 0:n], AF.Ln, bias=self.eps_col[:, 0:1],
                                                                  scale=1.0 / D_MODEL)),
                 reads=[ss, self.eps_col], writes=[r])
            k.op(k.act, (lambda e, r=r, n=n: e.activation(r[:, 0:n], r[:, 0:n], AF.Exp, scale=-0.5)), reads=[r], writes=[r])
            for c in range(8):
                col = nrow * 8 + c
                k.op(k.dve, (lambda e, r=r, c=c, o=o, n=n, col=col: e.scalar_tensor_tensor(
                    out=hn[:, c, o:o + n], in0=y[:, c, o:o + n], scalar=self.nw[:, col:col + 1], in1=r[:, 0:n],
                    op0=ALU.mult, op1=ALU.mult)),
                    reads=[yv[c][ti], r, self.nw], writes=[hnv[c][ti]])

    def phase_ffn(self, li, fj, nrow):
        k, nc = self.k, self.nc
        HG = 256
        NG = D_FF // HG
        hTv = self.hT.rearrange("(c p) t -> p c t", p=128)
        for (T0, GS) in self.groups:
            tl = tiles_of(GS, 512)
            NT = len(tl)
            k.begin()
            y = k.sb([128, 8, GS], F32, "y")
            hn = k.sb([128, 8, GS], BF16, "hn")
            yv = [k.views(y, NT) for _ in range(8)]
            hnv = [k.views(hn, NT) for _ in range(8)]
            sq = [k.sb([128, 512], BF16, "sq") for _ in range(2)]
            rstd = [k.sb([128, 512], F32, "rstd") for _ in range(2)]
            wg = [k.sb([128, 8, HG], BF16, "wg") for _ in range(2)]
            wu = [k.sb([128, 8, HG], BF16, "wu") for _ in range(2)]
            wd = [k.sb([128, HG // 128, D_MODEL], BF16, "wd") for _ in range(2)]
            sg = [k.sb([128, 2, 512], F32, "sg") for _ in range(2)]
            act = [k.sb([128, 2, 512], BF16, "act") for _ in range(2)]
            pg = [k.ps([128, 512], F32, "pg") for _ in range(2)]
            pu = [k.ps([128, 512], F32, "pu") for _ in range(2)]
            pd = [k.ps([128, 512], F32, "pd") for _ in range(4)]
            allv = [v for c in range(8) for v in yv[c]]
            k.dma(k.sp, y[:], hTv[:, :, T0:T0 + GS], writes=allv)
            self.emit_norm(y, yv, hn, hnv, tl, nrow, pd, sq, rstd)
            wup = self.w_up[li, fj]
            wdn = self.w_dn[li, fj]
            steps = []
            for g in range(NG):
                for ti, (o, n) in enumerate(tl):
                    steps.append((g, g % 2, ti, o, n, len(steps) % 2))
            loaded = set()

            def load_w(g, b):
                if g in loaded:
                    return
                loaded.add(g)
                k.dma(k.pool, wg[b][:], wup[:, g * HG:(g + 1) * HG].rearrange("(kk p) c -> p kk c", p=128), writes=[wg[b]])
                k.dma(k.pool, wu[b][:], wup[:, D_FF + g * HG:D_FF + (g + 1) * HG].rearrange("(kk p) c -> p kk c", p=128),
                      writes=[wu[b]])
                k.dma(k.pool, wd[b][:], wdn[g * HG:(g + 1) * HG, :].rearrange("(kk p) c -> p kk c", p=128), writes=[wd[b]])

            def emit_gu(step, j):
                g, b, ti, o, n, ab = step
                load_w(g, b)
                for (pt, wt) in ((pg[j], wg[b]), (pu[j], wu[b])):
                    for c in range(8):
                        k.op(k.pe, (lambda e, pt=pt, wt=wt, j=j, c=c, o=o, n=n: e.matmul(
                            pt[:, 0:n], lhsT=wt[:, c, j * 128:(j + 1) * 128], rhs=hn[:, c, o:o + n],
                            start=(c == 0), stop=(c == 7))),
                            reads=[wt, hnv[c][ti]], writes=[pt])
                k.op(k.act, (lambda e, j=j, ab=ab, n=n: e.activation(sg[ab][:, j, 0:n], pg[j][:, 0:n], AF.Silu)),
                     reads=[pg[j]], writes=[sgv[ab][j]])
                k.op(k.dve, (lambda e, j=j, ab=ab, n=n: e.scalar_tensor_tensor(
                    out=act[ab][:, j, 0:n], in0=pu[j][:, 0:n], scalar=0.5, in1=sg[ab][:, j, 0:n],
                    op0=ALU.mult, op1=ALU.mult)),
                    reads=[pu[j], sgv[ab][j]], writes=[actv[ab][j]])

            def emit_down(step):
                g, b, ti, o, n, ab = step
                for m in range(8):
                    pdt = pd[m % 4]
                    for j in range(2):
                        k.op(k.pe, (lambda e, pdt=pdt, b=b, j=j, m=m, ab=ab, n=n: e.matmul(
                            pdt[:, 0:n], lhsT=wd[b][:, j, m * 128:(m + 1) * 128], rhs=act[ab][:, j, 0:n],
                            start=(j == 0), stop=(j == 1))),
                            reads=[wd[b], actv[ab][j]], writes=[pdt])
                    k.op(k.dve, (lambda e, pdt=pdt, m=m, o=o, n=n: e.tensor_tensor(
                        out=y[:, m, o:o + n], in0=y[:, m, o:o + n], in1=pdt[:, 0:n], op=ALU.add)),
                        reads=[pdt, yv[m][ti]], writes=[yv[m][ti]])

            sgv = [k.views(sg[i], 2) for i in range(2)]
            actv = [k.views(act[i], 2) for i in range(2)]
            emit_gu(steps[0], 0)
            emit_gu(steps[0], 1)
            for i, st in enumerate(steps):
                if i + 1 < len(steps):
                    emit_gu(steps[i + 1], 0)
                emit_down(st)
                if i + 1 < len(steps):
                    emit_gu(steps[i + 1], 1)
            k.dma(k.sp, hTv[:, :, T0:T0 + GS], y[:], reads=allv)
            k.end()

    def phase_out(self, nrow):
        k, nc, T = self.k, self.nc, self.T
        hTv = self.hT.rearrange("(c p) t -> p c t", p=128)
        k.begin()
        hb = [k.sb([128, 8, 512], F32, "hb") for _ in range(2)]
        sq = [k.sb([128, 512], BF16, "sq") for _ in range(2)]
        rstd = [k.sb([128, 512], F32, "rstd") for _ in range(2)]
        yn = [k.sb([128, 8, 512], F32, "yn") for _ in range(2)]
        ot = [k.sb([128, 4, D_MODEL], F32, "ot") for _ in range(2)]
        pss = k.ps([128, 512], F32, "pss")
        pt = [k.ps([128, 512], F32, "pt") for _ in range(4)]
        pi = 0
        for gi, (t0, n) in enumerate(tiles_of(T, 512)):
            h = hb[gi % 2]
            r = rstd[gi % 2]
            yy = yn[gi % 2]
            o_ = ot[gi % 2]
            k.dma(k.sp, h[:, :, 0:n], hTv[:, :, t0:t0 + n], writes=[h])
            for c in range(8):
                s_ = sq[c % 2]
                k.op(k.act, (lambda e, s_=s_, h=h, c=c, n=n: e.activation(s_[:, 0:n], h[:, c, 0:n], AF.Square)),
                     reads=[h], writes=[s_])
                k.op(k.pe, (lambda e, s_=s_, c=c, n=n: e.matmul(pss[:, 0:n], lhsT=self.ones_bf[:], rhs=s_[:, 0:n],
                                                                 start=(c == 0), stop=(c == 7))),
                     reads=[s_, self.ones_bf], writes=[pss])
            k.op(k.act, (lambda e, r=r, n=n: e.activation(r[:, 0:n], pss[:, 0:n], AF.Ln, bias=self.eps_col[:, 0:1],
                                                           scale=1.0 / D_MODEL)),
                 reads=[pss, self.eps_col], writes=[r])
            k.op(k.act, (lambda e, r=r, n=n: e.activation(r[:, 0:n], r[:, 0:n], AF.Exp, scale=-0.5)), reads=[r], writes=[r])
            for c in range(8):
                col = nrow * 8 + c
                k.op(k.dve, (lambda e, r=r, h=h, yy=yy, c=c, n=n, col=col: e.scalar_tensor_tensor(
                    out=yy[:, c, 0:n], in0=h[:, c, 0:n], scalar=self.nw[:, col:col + 1], in1=r[:, 0:n],
                    op0=ALU.mult, op1=ALU.mult)),
                    reads=[h, r, self.nw], writes=[yy])
            nb = (n + 127) // 128
            blocks = [(b * 128, min(128, n - b * 128)) for b in range(nb)]
            for b, (o, rr) in enumerate(blocks):
                for half in range(2):
                    p = pt[pi % 4]
                    pi += 1
                    for cc in range(4):
                        c = half * 4 + cc
                        k.op(k.pe, (lambda e, p=p, yy=yy, c=c, cc=cc, o=o, rr=rr: e.transpose(
                            p[0:rr, cc * 128:(cc + 1) * 128], yy[:, c, o:o + rr], self.ident[:])),
                            reads=[yy, self.ident], writes=[p])
                    if half == 0:
                        k.op(k.dve, (lambda e, p=p, o_=o_, b=b, rr=rr: e.tensor_copy(o_[0:rr, b, 0:512], p[0:rr, :])),
                             reads=[p], writes=[o_])
                    else:
                        k.op(k.act, (lambda e, p=p, o_=o_, b=b, rr=rr: e.copy(o_[0:rr, b, 512:1024], p[0:rr, :])),
                             reads=[p], writes=[o_])
            if n % 128 == 0:
                k.dma(k.sp, self.yout[t0:t0 + n, :].rearrange("(j p) f -> p j f", p=128), o_[:, 0:nb, :], reads=[o_])
            else:
                for b, (o, rr) in enumerate(blocks):
                    k.dma(k.sp, self.yout[t0 + o:t0 + o + rr, :], o_[0:rr, b, :], reads=[o_])
        k.end()


    def emit_rope_tables(self, T0, tl, cosF, sinF, tmp):
        k = self.k
        PI = float(np.pi)
        invf = self.bsm[:, 257:258]
        sign = self.bsm[:, 258:259]
        for ti, (o, n) in enumerate(tl):
            pos, ang, ki, kf, tf = tmp
            k.op(k.pool, (lambda e, pos=pos, n=n, b=T0 + o: e.iota(pos[:, 0:n], [[1, n]], base=b, channel_multiplier=0,
                                                                   allow_small_or_imprecise_dtypes=True)), writes=[pos])
            k.op(k.dve, (lambda e, n=n: e.tensor_scalar(ang[:, 0:n], pos[:, 0:n], invf, None, op0=ALU.mult)),
                 reads=[pos, self.bsm], writes=[ang])
            def reduce(src, dst, shift, n=n):
                k.op(k.dve, (lambda e: e.tensor_scalar(dst[:, 0:n], src[:, 0:n], shift, None, op0=ALU.add)),
                     reads=[src], writes=[dst])
                k.op(k.dve, (lambda e: e.tensor_scalar(ki[:, 0:n], dst[:, 0:n], 1.0 / (2 * PI), None, op0=ALU.mult)),
                     reads=[dst], writes=[ki])
                k.op(k.dve, (lambda e: e.tensor_copy(tf[:, 0:n], ki[:, 0:n])), reads=[ki], writes=[tf])
                k.op(k.dve, (lambda e: e.scalar_tensor_tensor(out=dst[:, 0:n], in0=tf[:, 0:n], scalar=-2 * PI, in1=dst[:, 0:n],
                                                              op0=ALU.mult, op1=ALU.add)), reads=[tf, dst], writes=[dst])
                k.op(k.dve, (lambda e: e.tensor_scalar(tf[:, 0:n], dst[:, 0:n], PI, -2 * PI, op0=ALU.is_gt, op1=ALU.mult)),
                     reads=[dst], writes=[tf])
                k.op(k.dve, (lambda e: e.tensor_tensor(out=dst[:, 0:n], in0=dst[:, 0:n], in1=tf[:, 0:n], op=ALU.add)),
                     reads=[dst, tf], writes=[dst])
                k.op(k.dve, (lambda e: e.tensor_scalar(tf[:, 0:n], dst[:, 0:n], -PI, 2 * PI, op0=ALU.is_lt, op1=ALU.mult)),
                     reads=[dst], writes=[tf])
                k.op(k.dve, (lambda e: e.tensor_tensor(out=dst[:, 0:n], in0=dst[:, 0:n], in1=tf[:, 0:n], op=ALU.add)),
                     reads=[dst, tf], writes=[dst])
                k.op(k.dve, (lambda e: e.tensor_scalar(dst[:, 0:n], dst[:, 0:n], -PI, PI, op0=ALU.max, op1=ALU.min)),
                     reads=[dst], writes=[dst])
            reduce(ang, kf, 0.0)
            reduce(ang, pos, PI / 2)
            s_, c_ = sinF[ti], cosF[ti]
            k.op(k.act, (lambda e, s_=s_, n=n: e.activation(s_[:, 0:n], kf[:, 0:n], AF.Sin)), reads=[kf], writes=[s_])
            k.op(k.act, (lambda e, c_=c_, n=n: e.activation(c_[:, 0:n], pos[:, 0:n], AF.Sin)), reads=[pos], writes=[c_])
            k.op(k.dve, (lambda e, s_=s_, n=n: e.tensor_scalar(s_[:, 0:n], s_[:, 0:n], sign, None, op0=ALU.mult)),
                 reads=[s_, self.bsm], writes=[s_])

    def phase_attn_proj(self, nrow):
        k, nc = self.k, self.nc
        hTv = self.hT.rearrange("(c p) t -> p c t", p=128)
        for (T0, GS) in self.groups:
            tl = tiles_of(GS, 512)
            NT = len(tl)
            k.begin()
            y = k.sb([128, 8, GS], F32, "y")
            hn = k.sb([128, 8, GS], BF16, "hn")
            yv = [k.views(y, NT) for _ in range(8)]
            hnv = [k.views(hn, NT) for _ in range(8)]
            sq = [k.sb([128, 512], BF16, "sq") for _ in range(2)]
            rstd = [k.sb([128, 512], F32, "rstd") for _ in range(2)]
            pd = [k.ps([128, 512], F32, "pd") for _ in range(4)]
            pa = [k.ps([128, 512], F32, "pa") for _ in range(2)]
            pb = [k.ps([128, 512], F32, "pb") for _ in range(2)]
            allv = [v for c in range(8) for v in yv[c]]
            k.dma(k.sp, y[:], hTv[:, :, T0:T0 + GS], writes=allv)
            self.emit_norm(y, yv, hn, hnv, tl, nrow, pd, sq, rstd)
            cosF = [k.sb([128, 512], F32, "cosF") for _ in range(NT)]
            sinF = [k.sb([128, 512], F32, "sinF") for _ in range(NT)]
            tmp = (k.sb([128, 512], F32, "pos"), k.sb([128, 512], F32, "ang"),
                   k.sb([128, 512], mybir.dt.int32, "ki"), k.sb([128, 512], F32, "kf"), k.sb([128, 512], F32, "tf"))
            self.emit_rope_tables(T0, tl, cosF, sinF, tmp)
            wa = [k.sb([128, 8, 128], BF16, "wa") for _ in range(2)]
            wb = [k.sb([128, 8, 128], BF16, "wb") for _ in range(2)]
            t1 = [k.sb([128, 512], F32, "t1") for _ in range(2)]
            t2 = [k.sb([128, 512], F32, "t2") for _ in range(2)]
            stg = [k.sb([128, 512], BF16, "stg") for _ in range(3)]
            it = 0
            for m in range(16):
                b = m % 2
                k.dma(k.pool, wa[b][:], self.b_w[:, m * 128:(m + 1) * 128].rearrange("(kk p) c -> p kk c", p=128), writes=[wa[b]])
                k.dma(k.pool, wb[b][:], self.b_wsw[:, m * 128:(m + 1) * 128].rearrange("(kk p) c -> p kk c", p=128), writes=[wb[b]])
                for ti, (o, n) in enumerate(tl):
                    ab = it % 2
                    sb_ = stg[it % 3]
                    it += 1
                    for (pt, wt) in ((pa[ab], wa[b]), (pb[ab], wb[b])):
                        for c in range(8):
                            k.op(k.pe, (lambda e, pt=pt, wt=wt, c=c, o=o, n=n: e.matmul(
                                pt[:, 0:n], lhsT=wt[:, c, :], rhs=hn[:, c, o:o + n], start=(c == 0), stop=(c == 7))),
                                reads=[wt, hnv[c][ti]], writes=[pt])
                    k.op(k.dve, (lambda e, ab=ab, ti=ti, n=n: e.tensor_tensor(out=t1[ab][:, 0:n], in0=pa[ab][:, 0:n],
                                                                              in1=cosF[ti][:, 0:n], op=ALU.mult)),
                         reads=[pa[ab], cosF[ti]], writes=[t1[ab]])
                    k.op(k.dve, (lambda e, ab=ab, ti=ti, n=n: e.tensor_tensor(out=t2[ab][:, 0:n], in0=pb[ab][:, 0:n],
                                                                              in1=sinF[ti][:, 0:n], op=ALU.mult)),
                         reads=[pb[ab], sinF[ti]], writes=[t2[ab]])
                    k.op(k.pool, (lambda e, ab=ab, sb_=sb_, n=n: e.tensor_tensor(out=sb_[:, 0:n], in0=t1[ab][:, 0:n],
                                                                                 in1=t2[ab][:, 0:n], op=ALU.add)),
                         reads=[t1[ab], t2[ab]], writes=[sb_])
                    k.dma(k.sp, self.qkT[m * 128:(m + 1) * 128, T0 + o:T0 + o + n], sb_[:, 0:n], reads=[sb_])
            wv = [k.sb([128, 8, 512], BF16, "wv") for _ in range(2)]
            vst = [k.sb([128, 512], BF16, "vst") for _ in range(3)]
            it = 0
            for vb in range(2):
                k.dma(k.pool, wv[vb][:], self.b_w[:, 2048 + vb * 512:2048 + (vb + 1) * 512].rearrange("(kk p) c -> p kk c", p=128),
                      writes=[wv[vb]])
                for ti, (o, n) in enumerate(tl):
                    for bo in range(0, n, 128):
                        r = min(128, n - bo)
                        pt = pd[it % 4]
                        vs = vst[it % 3]
                        it += 1
                        for c in range(8):
                            k.op(k.pe, (lambda e, pt=pt, vb=vb, c=c, o=o, bo=bo, r=r: e.matmul(
                                pt[0:r, :], lhsT=hn[:, c, o + bo:o + bo + r], rhs=wv[vb][:, c, :], start=(c == 0), stop=(c == 7))),
                                reads=[wv[vb], hnv[c][ti]], writes=[pt])
                        k.op(k.act, (lambda e, pt=pt, vs=vs, r=r: e.copy(vs[0:r, :], pt[0:r, :])), reads=[pt], writes=[vs])
                        k.dma(k.sp, self.v_tm[T0 + o + bo:T0 + o + bo + r, vb * 512:(vb + 1) * 512], vs[0:r, :], reads=[vs])
            k.end()

    def phase_attn_core(self):
        k, nc, T = self.k, self.nc, self.T
        NKT = (T + 127) // 128
        qtl = tiles_of(T, 512)
        k.begin()
        lt = k.sb([128, 256], F32, "lt")
        l2 = k.sb([128, 2], F32, "l2")
        neglam = k.sb([128, 1], F32, "neglam")
        subw = k.sb([128, 1], F32, "subw")
        kb = k.sb([128, NKT], F32, "kb")
        k.dma(k.sp, kb[:], self.kbias_in, writes=[kb])
        k.op(k.dve, (lambda e: e.tensor_tensor(out=lt[:, 0:64], in0=self.bsm[:, 0:64], in1=self.bsm[:, 64:128], op=ALU.mult)),
             reads=[self.bsm], writes=[lt])
        k.op(k.dve, (lambda e: e.tensor_tensor(out=lt[:, 64:128], in0=self.bsm[:, 128:192], in1=self.bsm[:, 192:256], op=ALU.mult)),
             reads=[self.bsm], writes=[lt])
        k.op(k.dve, (lambda e: e.reduce_sum(l2[:, 0:2], lt[:, 0:128].rearrange("p (a b) -> p a b", a=2), axis=AX.X)),
             reads=[lt], writes=[l2])
        k.op(k.act, (lambda e: e.activation(l2[:, 0:2], l2[:, 0:2], AF.Exp)), reads=[l2], writes=[l2])
        k.op(k.dve, (lambda e: e.tensor_tensor(out=neglam[:], in0=l2[:, 1:2], in1=l2[:, 0:1], op=ALU.subtract)),
             reads=[l2], writes=[neglam])
        k.op(k.dve, (lambda e: e.tensor_tensor(out=neglam[:], in0=neglam[:], in1=self.bsm[:, 259:260], op=ALU.subtract)),
             reads=[neglam, self.bsm], writes=[neglam])
        k.op(k.dve, (lambda e: e.tensor_scalar(subw[:], self.bsm[:, 259:260], -1.0, 1.0, op0=ALU.mult, op1=ALU.add)),
             reads=[self.bsm], writes=[subw])
        k.op(k.dve, (lambda e: e.tensor_tensor(out=subw[:], in0=subw[:], in1=self.bsm[:, 256:257], op=ALU.mult)),
             reads=[subw, self.bsm], writes=[subw])

        kk1 = [k.sb([64, T], BF16, "kk1") for _ in range(2)]
        kk2 = [k.sb([64, T], BF16, "kk2") for _ in range(2)]
        vh = [k.sb([128, NKT, 128], BF16, "vh") for _ in range(2)]
        q1 = [k.sb([64, 512], BF16, "q1") for _ in range(2)]
        q2 = [k.sb([64, 512], BF16, "q2") for _ in range(2)]
        p1 = [k.sb([128, 512], BF16, "p1") for _ in range(3)]
        p2 = [k.sb([128, 512], BF16, "p2") for _ in range(3)]
        ps1 = [k.ps([128, 512], F32, "ps1") for _ in range(2)]
        ps2 = [k.ps([128, 512], F32, "ps2") for _ in range(2)]
        num1, num2, z1, z2 = (k.ps([128, 512], F32, nm) for nm in ("num1", "num2", "z1", "z2"))
        za1 = k.sb([128, 512], F32, "za1")
        za2 = k.sb([128, 512], F32, "za2")
        ones_f = k.sb([128, 128], F32, "ones_f")
        k.op(k.dve, (lambda e: e.memset(ones_f[:], 1.0)), writes=[ones_f])
        r1 = k.sb([128, 512], F32, "r1")
        r2 = k.sb([128, 512], F32, "r2")
        o1 = k.sb([128, 512], F32, "o1")
        o2 = k.sb([128, 512], F32, "o2")
        oo = k.sb([128, 512], F32, "oo")
        sqo = k.sb([128, 512], BF16, "sqo")
        rs = k.sb([128, 512], F32, "rs")
        ost = [k.sb([128, 512], BF16, "ost") for _ in range(2)]
        nfull = T // 128
        rem = T - nfull * 128
        it = 0
        fi = 0
        for h in range(8):
            hb = h % 2
            k.dma(k.sp, kk1[hb][:], self.qkT[1024 + h * 64:1024 + (h + 1) * 64, :], writes=[kk1[hb]])
            k.dma(k.sp, kk2[hb][:], self.qkT[1536 + h * 64:1536 + (h + 1) * 64, :], writes=[kk2[hb]])
            k.dma(k.sp, vh[hb][:, 0:nfull, :],
                  self.v_tm[0:nfull * 128, h * 128:(h + 1) * 128].rearrange("(kt p) e -> p kt e", p=128), writes=[vh[hb]])
            if rem:
                k.dma(k.sp, vh[hb][0:rem, nfull, :], self.v_tm[nfull * 128:T, h * 128:(h + 1) * 128], writes=[vh[hb]])
            for qi, (t0, n) in enumerate(qtl):
                qb = fi % 2
                k.dma(k.sp, q1[qb][:, 0:n], self.qkT[h * 64:(h + 1) * 64, t0:t0 + n], writes=[q1[qb]])
                k.dma(k.sp, q2[qb][:, 0:n], self.qkT[512 + h * 64:512 + (h + 1) * 64, t0:t0 + n], writes=[q2[qb]])
                base_it = it
                it += NKT

                def emit_scores(kt):
                    kn = 128 if kt < nfull else rem
                    sb2 = (base_it + kt) % 2
                    for (ps, kk, qq) in ((ps1[sb2], kk1[hb], q1[qb]), (ps2[sb2], kk2[hb], q2[qb])):
                        k.op(k.pe, (lambda e, ps=ps, kk=kk, qq=qq, kt=kt, kn=kn, n=n: e.matmul(
                            ps[0:kn, 0:n], lhsT=kk[:, kt * 128:kt * 128 + kn], rhs=qq[:, 0:n], start=True, stop=True)),
                            reads=[kk, qq], writes=[ps])

                def emit_exp(kt):
                    kn = 128 if kt < nfull else rem
                    sb2 = (base_it + kt) % 2
                    pb3 = (base_it + kt) % 3
                    for (ps, pp) in ((ps1[sb2], p1[pb3]), (ps2[sb2], p2[pb3])):
                        k.op(k.act, (lambda e, ps=ps, pp=pp, kt=kt, kn=kn, n=n: e.activation(
                            pp[0:kn, 0:n], ps[0:kn, 0:n], AF.Exp, bias=kb[0:kn, kt:kt + 1], scale=0.125)),
                            reads=[ps, kb], writes=[pp])

                def emit_pv(kt):
                    kn = 128 if kt < nfull else rem
                    pb3 = (base_it + kt) % 3
                    first, last = (kt == 0), (kt == NKT - 1)
                    for (pp, nm, za, zeng) in ((p1[pb3], num1, za1, k.dve), (p2[pb3], num2, za2, k.pool)):
                        k.op(k.pe, (lambda e, nm=nm, pp=pp, kt=kt, kn=kn, n=n, first=first, last=last, vv=vh[hb]: e.matmul(
                            nm[:, 0:n], lhsT=vv[0:kn, kt, :], rhs=pp[0:kn, 0:n], start=first, stop=last)),
                            reads=[vh[hb], pp], writes=[nm])
                        if first:
                            k.op(zeng, (lambda e, za=za, pp=pp, n=n: e.tensor_copy(za[:, 0:n], pp[:, 0:n])), reads=[pp], writes=[za])
                        else:
                            k.op(zeng, (lambda e, za=za, pp=pp, kn=kn, n=n: e.tensor_tensor(out=za[0:kn, 0:n], in0=za[0:kn, 0:n],
                                                                                           in1=pp[0:kn, 0:n], op=ALU.add)),
                                 reads=[pp, za], writes=[za])

                emit_scores(0)
                for kt in range(NKT):
                    emit_exp(kt)
                    if kt + 1 < NKT:
                        emit_scores(kt + 1)
                    emit_pv(kt)
                for (zz, za) in ((z1, za1), (z2, za2)):
                    k.op(k.pe, (lambda e, zz=zz, za=za, n=n: e.matmul(zz[:, 0:n], lhsT=ones_f[:], rhs=za[:, 0:n], start=True, stop=True)),
                         reads=[ones_f, za], writes=[zz])
                k.op(k.dve, (lambda e, n=n: e.reciprocal(r1[:, 0:n], z1[:, 0:n])), reads=[z1], writes=[r1])
                k.op(k.dve, (lambda e, n=n: e.reciprocal(r2[:, 0:n], z2[:, 0:n])), reads=[z2], writes=[r2])
                k.op(k.dve, (lambda e, n=n: e.tensor_tensor(out=o1[:, 0:n], in0=num1[:, 0:n], in1=r1[:, 0:n], op=ALU.mult)),
                     reads=[num1, r1], writes=[o1])
                k.op(k.dve, (lambda e, n=n: e.scalar_tensor_tensor(out=o2[:, 0:n], in0=num2[:, 0:n], scalar=neglam[:, 0:1],
                                                                   in1=r2[:, 0:n], op0=ALU.mult, op1=ALU.mult)),
                     reads=[num2, r2, neglam], writes=[o2])
                k.op(k.pool, (lambda e, n=n: e.tensor_tensor(out=oo[:, 0:n], in0=o1[:, 0:n], in1=o2[:, 0:n], op=ALU.add)),
                     reads=[o1, o2], writes=[oo])
                k.op(k.act, (lambda e, n=n: e.activation(sqo[:, 0:n], oo[:, 0:n], AF.Square)), reads=[oo], writes=[sqo])
                pss = ps1[it % 2]
                k.op(k.pe, (lambda e, pss=pss, n=n: e.matmul(pss[:, 0:n], lhsT=self.ones_bf[:], rhs=sqo[:, 0:n], start=True, stop=True)),
                     reads=[self.ones_bf, sqo], writes=[pss])
                k.op(k.act, (lambda e, pss=pss, n=n: e.activation(rs[:, 0:n], pss[:, 0:n], AF.Ln, bias=self.eps_col[:, 0:1],
                                                                   scale=1.0 / 128)), reads=[pss, self.eps_col], writes=[rs])
                k.op(k.act, (lambda e, n=n: e.activation(rs[:, 0:n], rs[:, 0:n], AF.Exp, scale=-0.5)), reads=[rs], writes=[rs])
                os_ = ost[fi % 2]
                fi += 1
                k.op(k.dve, (lambda e, os_=os_, n=n: e.scalar_tensor_tensor(out=os_[:, 0:n], in0=oo[:, 0:n], scalar=subw[:, 0:1],
                                                                            in1=rs[:, 0:n], op0=ALU.mult, op1=ALU.mult)),
                     reads=[oo, rs, subw], writes=[os_])
                k.dma(k.sp, self.oT[h * 128:(h + 1) * 128, t0:t0 + n], os_[:, 0:n], reads=[os_])
        k.end()

    def phase_out_proj(self, wo_ap):
        k, nc, T = self.k, self.nc, self.T
        hTv = self.hT.rearrange("(c p) t -> p c t", p=128)
        oTv = self.oT.rearrange("(c p) t -> p c t", p=128)
        k.begin()
        wo = k.sb([128, 8, D_MODEL], BF16, "wo")
        for c in range(8):
            for hf in range(2):
                k.dma(k.pool, wo[:, c, hf * 512:(hf + 1) * 512], wo_ap[c * 128:(c + 1) * 128, hf * 512:(hf + 1) * 512], writes=[wo])
        ob = [k.sb([128, 8, 512], BF16, "ob") for _ in range(2)]
        hb = [k.sb([128, 8, 512], F32, "hb") for _ in range(2)]
        pp = [k.ps([128, 512], F32, "pp") for _ in range(4)]
        pi = 0
        for gi, (t0, n) in enumerate(tiles_of(T, 512)):
            o_ = ob[gi % 2]
            h_ = hb[gi % 2]
            k.dma(k.sp, o_[:, :, 0:n], oTv[:, :, t0:t0 + n], writes=[o_])
            k.dma(k.sp, h_[:, :, 0:n], hTv[:, :, t0:t0 + n], writes=[h_])
            for m in range(8):
                p = pp[pi % 4]
                pi += 1
                for c in range(8):
                    k.op(k.pe, (lambda e, p=p, o_=o_, c=c, m=m, n=n: e.matmul(
                        p[:, 0:n], lhsT=wo[:, c, m * 128:(m + 1) * 128], rhs=o_[:, c, 0:n], start=(c == 0), stop=(c == 7))),
                        reads=[wo, o_], writes=[p])
                k.op(k.dve, (lambda e, p=p, h_=h_, m=m, n=n: e.tensor_tensor(out=h_[:, m, 0:n], in0=h_[:, m, 0:n],
                                                                             in1=p[:, 0:n], op=ALU.add)),
                     reads=[p, h_], writes=[h_])
            k.dma(k.sp, hTv[:, :, t0:t0 + n], h_[:, :, 0:n], reads=[h_])
        k.end()

    def mixer_attn(self, nrow):
        import os
        st = int(os.environ.get("ATT_STAGE", "3"))
        self.phase_attn_proj(nrow)
        if st >= 2:
            self.phase_attn_core()
        if st >= 3:
            self.phase_out_proj(self.b_wo)


    def phase_proj(self, nrow, setup, fm_jobs, tm_jobs):
        k, nc = self.k, self.nc
        hTv = self.hT.rearrange("(c p) t -> p c t", p=128)
        for (T0, GS) in self.groups:
            tl = tiles_of(GS, 512)
            NT = len(tl)
            k.begin()
            y = k.sb([128, 8, GS], F32, "y")
            hn = k.sb([128, 8, GS], BF16, "hn")
            yv = [k.views(y, NT) for _ in range(8)]
            hnv = [k.views(hn, NT) for _ in range(8)]
            sq = [k.sb([128, 512], BF16, "sq") for _ in range(2)]
            rstd = [k.sb([128, 512], F32, "rstd") for _ in range(2)]
            pd = [k.ps([128, 512], F32, "pd") for _ in range(4)]
            allv = [v for c in range(8) for v in yv[c]]
            k.dma(k.sp, y[:], hTv[:, :, T0:T0 + GS], writes=allv)
            self.emit_norm(y, yv, hn, hnv, tl, nrow, pd, sq, rstd)
            ctx = setup(T0, tl)
            wa = [k.sb([128, 8, 128], BF16, "wa") for _ in range(2)]
            it = 0
            for ji, (wap, post) in enumerate(fm_jobs):
                b = ji % 2
                k.dma(k.pool, wa[b][:], wap.rearrange("(kk p) c -> p kk c", p=128), writes=[wa[b]])
                for ti, (o, n) in enumerate(tl):
                    pt = pd[it % 4]
                    it += 1
                    for c in range(8):
                        k.op(k.pe, (lambda e, pt=pt, b=b, c=c, o=o, n=n: e.matmul(
                            pt[:, 0:n], lhsT=wa[b][:, c, :], rhs=hn[:, c, o:o + n], start=(c == 0), stop=(c == 7))),
                            reads=[wa[b], hnv[c][ti]], writes=[pt])
                    post(ctx, pt, T0, ti, o, n)
            wv = [k.sb([128, 8, 512], BF16, "wv") for _ in range(2)]
            for ji, job in enumerate(tm_jobs):
                wap, post = job[0], job[1]
                ncl = job[2] if len(job) > 2 else 512
                b = ji % 2
                k.dma(k.pool, wv[b][:, :, 0:ncl], wap.rearrange("(kk p) c -> p kk c", p=128), writes=[wv[b]])
                for ti, (o, n) in enumerate(tl):
                    for bo in range(0, n, 128):
                        r = min(128, n - bo)
                        pt = pd[it % 4]
                        it += 1
                        for c in range(8):
                            k.op(k.pe, (lambda e, pt=pt, b=b, c=c, o=o, bo=bo, r=r, ncl=ncl: e.matmul(
                                pt[0:r, 0:ncl], lhsT=hn[:, c, o + bo:o + bo + r], rhs=wv[b][:, c, 0:ncl], start=(c == 0), stop=(c == 7))),
                                reads=[wv[b], hnv[c][ti]], writes=[pt])
                        post(ctx, pt, T0 + o + bo, r)
            k.end()

    def hg_prep(self, li):
        k, nc = self.k, self.nc
        k.begin()
        e = k.sb([128, 32], F32, "lbe")
        ssum = k.sb([128, 8], F32, "lbs")
        k.op(k.act, (lambda en: en.activation(e[:], self.csm[:, 0:32], AF.Exp)), reads=[self.csm], writes=[e])
        k.op(k.dve, (lambda en: en.tensor_tensor(out=ssum[:], in0=e[:, 0:8], in1=e[:, 8:16], op=ALU.add)), reads=[e], writes=[ssum])
        k.op(k.dve, (lambda en: en.tensor_tensor(out=ssum[:], in0=ssum[:], in1=e[:, 16:24], op=ALU.add)), reads=[e, ssum], writes=[ssum])
        k.op(k.dve, (lambda en: en.tensor_tensor(out=ssum[:], in0=ssum[:], in1=e[:, 24:32], op=ALU.add)), reads=[e, ssum], writes=[ssum])
        k.op(k.dve, (lambda en: en.reciprocal(ssum[:], ssum[:])), reads=[ssum], writes=[ssum])
        lbc = self.lb_col
        k.op(k.dve, (lambda en: en.memset(lbc[:, 0:8], 0.0)), writes=[lbc])
        for r in range(1, li + 1):
            k.op(k.dve, (lambda en, r=r: en.tensor_tensor(out=lbc[:, 0:8], in0=lbc[:, 0:8], in1=e[:, r * 8:(r + 1) * 8], op=ALU.add)),
                 reads=[e, lbc], writes=[lbc])
        k.op(k.dve, (lambda en: en.tensor_tensor(out=lbc[:, 0:8], in0=lbc[:, 0:8], in1=ssum[:], op=ALU.mult)), reads=[lbc, ssum], writes=[lbc])
        k.op(k.dve, (lambda en: en.tensor_scalar(lbc[:, 8:16], lbc[:, 0:8], -1.0, 1.0, op0=ALU.mult, op1=ALU.add)), reads=[lbc], writes=[lbc])
        lbr = self.lbrow_t.ap()
        k.dma(k.sp, lbr.rearrange("r (c p) -> p r c", p=128), lbc[:].rearrange("p (r c) -> p r c", c=8), reads=[lbc],
              allow_slow_non_contiguous=True)
        k.end()
        k.begin()
        k.dma(k.sp, self.oml_row[:], bass.AP(self.lbrow_t, D_MODEL, [[0, 128], [1, D_MODEL]]), writes=[self.oml_row])
        k.end()

    def phase_hg_proj(self, nrow):
        k = self.k
        QS = 128 ** -0.5

        def setup(T0, tl):
            ctx = {}
            ctx["tm"] = [k.sb([128, 512], F32, "tmk") for _ in tl]
            for ti, (o, n) in enumerate(tl):
                k.dma(k.sp, ctx["tm"][ti][:, 0:n], self.tmask_rep[:, T0 + o:T0 + o + n], writes=[ctx["tm"][ti]])
            ctx["sg"] = [k.sb([128, 512], F32, "sg") for _ in range(2)]
            ctx["st"] = [k.sb([128, 512], BF16, "st") for _ in range(3)]
            ctx["k32"] = [k.sb([128, 512], F32, "k32") for _ in range(2)]
            ctx["lf"] = [k.sb([128, 512], F32, "lf") for _ in range(2)]
            ctx["kb"] = [k.sb([128, 512], BF16, "kb") for _ in range(2)]
            ctx["i"] = 0
            return ctx

        def post_silu(dst, scale):
            def post(ctx, pt, T0, ti, o, n):
                i = ctx["i"]
                ctx["i"] += 1
                sg, st = ctx["sg"][i % 2], ctx["st"][i % 3]
                k.op(k.act, (lambda e: e.activation(sg[:, 0:n], pt[:, 0:n], AF.Silu)), reads=[pt], writes=[sg])
                k.op(k.pool, (lambda e: e.tensor_scalar(st[:, 0:n], sg[:, 0:n], scale, None, op0=ALU.mult)), reads=[sg], writes=[st])
                return st
            return post

        def fm_q(c):
            base = post_silu(None, QS)

            def post(ctx, pt, T0, ti, o, n):
                st = base(ctx, pt, T0, ti, o, n)
                k.dma(k.sp, self.fmA[c * 128:(c + 1) * 128, T0 + o:T0 + o + n], st[:, 0:n], reads=[st])
            return post

        def fm_g(c):
            base = post_silu(None, 1.0)

            def post(ctx, pt, T0, ti, o, n):
                st = base(ctx, pt, T0, ti, o, n)
                k.dma(k.sp, self.fmG[c * 128:(c + 1) * 128, T0 + o:T0 + o + n], st[:, 0:n], reads=[st])
            return post

        def fm_k(d, c):
            def post(ctx, pt, T0, ti, o, n):
                i = ctx["i"]
                ctx["i"] += 1
                sg, st = ctx["sg"][i % 2], ctx["st"][i % 3]
                k.op(k.act, (lambda e: e.activation(sg[:, 0:n], pt[:, 0:n], AF.Sigmoid, scale=-1.0)), reads=[pt], writes=[sg])
                k.op(k.dve, (lambda e: e.scalar_tensor_tensor(out=st[:, 0:n], in0=sg[:, 0:n], scalar=self.lb_col[:, 8 + c:9 + c],
                                                              in1=ctx["tm"][ti][:, 0:n], op0=ALU.mult, op1=ALU.mult)),
                     reads=[sg, self.lb_col, ctx["tm"][ti]], writes=[st])
                k.dma(k.sp, self.fmB[d][c * 128:(c + 1) * 128, T0 + o:T0 + o + n], st[:, 0:n], reads=[st])
            return post

        def tm_v(cb):
            def post(ctx, pt, tok0, r):
                i = ctx["i"]
                ctx["i"] += 1
                st = ctx["st"][i % 3]
                k.op(k.act, (lambda e: e.copy(st[0:r, :], pt[0:r, :])), reads=[pt], writes=[st])
                k.dma(k.sp, self.v_tm[tok0:tok0 + r, cb * 512:(cb + 1) * 512], st[0:r, :], reads=[st])
            return post

        def tm_f(d, cb):
            def post(ctx, pt, tok0, r):
                i = ctx["i"]
                ctx["i"] += 1
                sg, k32, lf, kb = ctx["sg"][i % 2], ctx["k32"][i % 2], ctx["lf"][i % 2], ctx["kb"][i % 2]
                blk = tok0 // 128
                assert tok0 % 128 == 0
                k.op(k.act, (lambda e: e.activation(sg[0:r, :], pt[0:r, :], AF.Sigmoid, scale=-1.0)), reads=[pt], writes=[sg])
                k.op(k.dve, (lambda e: e.scalar_tensor_tensor(out=k32[0:r, :], in0=sg[0:r, :], scalar=self.tmc[0:r, blk:blk + 1],
                                                              in1=self.oml_row[0:r, cb * 512:(cb + 1) * 512], op0=ALU.mult, op1=ALU.mult)),
                     reads=[sg, self.tmc, self.oml_row], writes=[k32])
                k.op(k.act, (lambda e: e.activation(lf[0:r, :], k32[0:r, :], AF.Ln, bias=self.one_col[0:r, 0:1], scale=-1.0)),
                     reads=[k32, self.one_col], writes=[lf])
                k.op(k.pool, (lambda e: e.tensor_copy(kb[0:r, :], k32[0:r, :])), reads=[k32], writes=[kb])
                k.dma(k.sp, self.tmK[d][tok0:tok0 + r, cb * 512:(cb + 1) * 512], kb[0:r, :], reads=[kb])
                k.dma(k.sp, self.tmL[d][tok0:tok0 + r, cb * 512:(cb + 1) * 512], lf[0:r, :], reads=[lf])
            return post

        W = self.c_w
        fm = []
        for c in range(8):
            fm.append((W[:, c * 128:(c + 1) * 128], fm_q(c)))
        for c in range(8):
            fm.append((W[:, 2048 + c * 128:2048 + (c + 1) * 128], fm_g(c)))
        for d in range(2):
            for c in range(8):
                fm.append((W[:, 3072 + d * 1024 + c * 128:3072 + d * 1024 + (c + 1) * 128], fm_k(d, c)))
        tm = []
        for cb in range(2):
            tm.append((W[:, 1024 + cb * 512:1024 + (cb + 1) * 512], tm_v(cb)))
        for d in range(2):
            for cb in range(2):
                tm.append((W[:, 3072 + d * 1024 + cb * 512:3072 + d * 1024 + (cb + 1) * 512], tm_f(d, cb)))
        self.phase_proj(nrow, setup, fm, tm)

    def phase_hg_core(self, onorm_col):
        k, nc, T = self.k, self.nc, self.T
        tiles = tiles_of(T, 512)
        for sweep in (1, 0):
            d = sweep
            k.begin()
            if d == 0:
                Uin, Uex, Mk, lastcol = self.tri[:, 0:64], self.tri[:, 64:128], 0, 63
            else:
                Uin, Uex, Mk, lastcol = self.tri[:, 128:192], self.tri[:, 192:256], 2, 0
            mrep = k.sb([64, 512], F32, "mrep")
            for c in range(8):
                k.op(k.dve, (lambda e, c=c, Mk=Mk: e.tensor_copy(mrep[:, c * 64:(c + 1) * 64], self.tri[:, Mk * 64:(Mk + 1) * 64])),
                     reads=[self.tri], writes=[mrep])
            S = [k.sb([128, 128], F32, "S") for _ in range(8)]
            Sb = [k.sb([128, 128], BF16, "Sb") for _ in range(8)]
            for h in range(8):
                k.op(k.dve, (lambda e, h=h: e.memset(S[h][:], 0.0)), writes=[S[h]])
                k.op(k.pool, (lambda e, h=h: e.memset(Sb[h][:], 0.0)), writes=[Sb[h]])
            NB = 2
            qT = [k.sb([128, 512], BF16, "qT") for _ in range(NB)]
            kT = [k.sb([128, 512], BF16, "kT") for _ in range(NB)]
            ktm = [k.sb([64, 8, 128], BF16, "ktm") for _ in range(NB)]
            lf = [k.sb([64, 8, 128], F32, "lf") for _ in range(NB)]
            vt = [k.sb([64, 8, 128], BF16, "vt") for _ in range(3)]
            eb = [k.sb([128, 512], F32, "eb") for _ in range(NB)]
            enb = [k.sb([128, 512], F32, "enb") for _ in range(NB)]
            qd = [k.sb([128, 512], BF16, "qd") for _ in range(NB)]
            kd = [k.sb([128, 512], BF16, "kd") for _ in range(NB)]
            ekd = [k.sb([64, 8, 128], F32, "ekd") for _ in range(NB)]
            kdec = [k.sb([64, 8, 128], BF16, "kdec") for _ in range(NB)]
            atm = [k.sb([64, 512], BF16, "atm") for _ in range(NB)]
            B = [k.ps([128, 512], F32, "Y%d" % i) for i in range(8)]
            ost = [k.sb([128, 512], F32, "ost") for _ in range(2)]
            if d == 0:
                obt = [k.sb([128, 512], F32, "obt") for _ in range(3)]
                gt = [k.sb([128, 512], BF16, "gt") for _ in range(3)]
                sqo_l = [k.sb([128, 512], BF16, "sqo") for _ in range(2)]
                rs_l = [k.sb([128, 512], F32, "rs") for _ in range(2)]
                fin = [k.sb([128, 512], BF16, "fin") for _ in range(2)]
            order = list(range(len(tiles)))
            if d == 1:
                order = order[::-1]

            def unit(h, b, it, t0, n, ncn, corder):
                if True:
                    rows = slice(h * 128, (h + 1) * 128)
                    Y = B[4 * b:4 * b + 4]
                    vt_ = vt[it % 3]
                    pbc, pat = Y[0], Y[3]
                    psuf = [Y[1], Y[2]]
                    pout = Y[1]
                    pst = [Y[2], Y[2]]
                    if d == 0:
                        sqo, rs = sqo_l[b], rs_l[b]
                    k.dma(k.sp, qT[b][:, 0:n], self.fmA[rows, t0:t0 + n], writes=[qT[b]])
                    k.dma(k.sp, kT[b][:, 0:n], self.fmB[d][rows, t0:t0 + n], writes=[kT[b]])
                    k.dma(k.sp, ktm[b][:, 0:ncn, :], self.tmK[d][t0:t0 + n, rows].rearrange("(c p) e -> p c e", p=64), writes=[ktm[b]])
                    k.dma(k.sp, lf[b][:, 0:ncn, :], self.tmL[d][t0:t0 + n, rows].rearrange("(c p) e -> p c e", p=64), writes=[lf[b]])
                    k.dma(k.sp, vt_[:, 0:ncn, :], self.v_tm[t0:t0 + n, rows].rearrange("(c p) e -> p c e", p=64), writes=[vt_])
                    if d == 0:
                        k.dma(k.sp, obt[it % 3][:, 0:n], self.obT[rows, t0:t0 + n], writes=[obt[it % 3]])
                        k.dma(k.sp, gt[it % 3][:, 0:n], self.fmG[rows, t0:t0 + n], writes=[gt[it % 3]])
                    k.mark()
                    for c in range(ncn):
                        k.op(k.pe, (lambda e, b=b, c=c: e.matmul(pbc[:, c * 64:(c + 1) * 64], lhsT=lf[b][:, c, :], rhs=Uin,
                                                                 start=True, stop=True)), reads=[lf[b], self.tri], writes=[pbc])
                    k.op(k.act, (lambda e, b=b, n=n: e.activation(eb[b][:, 0:n], pbc[:, 0:n], AF.Exp)), reads=[pbc], writes=[eb[b]])
                    k.op(k.act, (lambda e, b=b, n=n: e.activation(enb[b][:, 0:n], pbc[:, 0:n], AF.Exp, scale=-1.0)), reads=[pbc], writes=[enb[b]])
                    k.op(k.dve, (lambda e, b=b, n=n: e.tensor_tensor(out=qd[b][:, 0:n], in0=qT[b][:, 0:n], in1=eb[b][:, 0:n], op=ALU.mult)),
                         reads=[qT[b], eb[b]], writes=[qd[b]])
                    k.op(k.dve, (lambda e, b=b, n=n: e.tensor_tensor(out=kd[b][:, 0:n], in0=kT[b][:, 0:n], in1=enb[b][:, 0:n], op=ALU.mult)),
                         reads=[kT[b], enb[b]], writes=[kd[b]])
                    for c in range(ncn):
                        ps_ = psuf[c // 4]
                        k.op(k.pe, (lambda e, b=b, c=c, ps_=ps_: e.matmul(ps_[0:64, (c % 4) * 128:(c % 4 + 1) * 128], lhsT=Uex, rhs=lf[b][:, c, :],
                                                                          start=True, stop=True)), reads=[lf[b], self.tri], writes=[ps_])
                    for half in range((ncn + 3) // 4):
                        nn = min(4, ncn - half * 4)
                        k.op(k.act, (lambda e, b=b, half=half, nn=nn, pq=psuf[half]: e.activation(
                            ekd[b][:, half * 4:half * 4 + nn, :], pq[0:64, 0:nn * 128].rearrange("p (c d) -> p c d", d=128), AF.Exp)),
                             reads=[psuf[half]], writes=[ekd[b]])
                    k.op(k.dve, (lambda e, b=b, ncn=ncn: e.tensor_tensor(out=kdec[b][:, 0:ncn, :], in0=ktm[b][:, 0:ncn, :],
                                                                        in1=ekd[b][:, 0:ncn, :], op=ALU.mult)),
                         reads=[ktm[b], ekd[b]], writes=[kdec[b]])
                    for c in range(ncn):
                        cs = slice(c * 64, (c + 1) * 64)
                        k.op(k.pe, (lambda e, b=b, cs=cs: e.matmul(pat[0:64, cs], lhsT=kd[b][:, cs], rhs=qd[b][:, cs], start=True, stop=True)),
                             reads=[kd[b], qd[b]], writes=[pat])
                    k.op(k.dve, (lambda e, b=b, n=n: e.tensor_tensor(out=atm[b][:, 0:n], in0=pat[0:64, 0:n], in1=mrep[:, 0:n], op=ALU.mult)),
                         reads=[pat, mrep], writes=[atm[b]])
                    k.mark()
                    for c in corder:
                        cs = slice(c * 64, (c + 1) * 64)
                        k.op(k.pe, (lambda e, b=b, cs=cs, h=h: e.matmul(pout[:, cs], lhsT=Sb[h][:], rhs=qd[b][:, cs], start=True, stop=False)),
                             reads=[Sb[h], qd[b]], writes=[pout])
                        k.op(k.pe, (lambda e, b=b, cs=cs, c=c: e.matmul(pout[:, cs], lhsT=vt_[:, c, :], rhs=atm[b][:, cs], start=False, stop=True)),
                             reads=[vt_, atm[b]], writes=[pout])
                        pp = pst[c % 2]
                        pc = slice((c % 2) * 128, (c % 2 + 1) * 128)
                        k.op(k.pe, (lambda e, b=b, c=c, pp=pp, pc=pc: e.matmul(pp[:, pc], lhsT=kdec[b][:, c, :], rhs=vt_[:, c, :], start=True, stop=True)),
                             reads=[kdec[b], vt_], writes=[pp])
                        fc = c * 64 + lastcol
                        k.op(k.dve, (lambda e, b=b, h=h, pp=pp, fc=fc, pc=pc: e.scalar_tensor_tensor(
                            out=S[h][:], in0=S[h][:], scalar=eb[b][:, fc:fc + 1], in1=pp[:, pc], op0=ALU.mult, op1=ALU.add)),
                            reads=[S[h], eb[b], pp], writes=[S[h]])
                        k.op(k.act, (lambda e, h=h: e.copy(Sb[h][:], S[h][:])), reads=[S[h]], writes=[Sb[h]])
                    if d == 1:
                        os_ = ost[it % 2]
                        k.op(k.act, (lambda e, os_=os_, n=n: e.copy(os_[:, 0:n], pout[:, 0:n])), reads=[pout], writes=[os_])
                        k.dma(k.sp, self.obT[rows, t0:t0 + n], os_[:, 0:n], reads=[os_])
                    else:
                        os_ = ost[it % 2]
                        k.op(k.dve, (lambda e, os_=os_, ob_=obt[it % 3], n=n: e.tensor_tensor(out=os_[:, 0:n], in0=pout[:, 0:n], in1=ob_[:, 0:n], op=ALU.add)),
                             reads=[pout, obt[it % 3]], writes=[os_])
                        k.op(k.act, (lambda e, os_=os_, n=n: e.activation(sqo[:, 0:n], os_[:, 0:n], AF.Square)), reads=[os_], writes=[sqo])
                        k.op(k.pe, (lambda e, n=n: e.matmul(pbc[:, 0:n], lhsT=self.ones_bf[:], rhs=sqo[:, 0:n], start=True, stop=True)),
                             reads=[self.ones_bf, sqo], writes=[pbc])
                        k.op(k.act, (lambda e, n=n: e.activation(rs[:, 0:n], pbc[:, 0:n], AF.Ln, bias=self.eps_col[:, 0:1], scale=1.0 / 128)),
                             reads=[pbc, self.eps_col], writes=[rs])
                        k.op(k.act, (lambda e, n=n: e.activation(rs[:, 0:n], rs[:, 0:n], AF.Exp, scale=-0.5)), reads=[rs], writes=[rs])
                        k.op(k.dve, (lambda e, os_=os_, n=n: e.scalar_tensor_tensor(out=os_[:, 0:n], in0=os_[:, 0:n], scalar=onorm_col,
                                                                                   in1=rs[:, 0:n], op0=ALU.mult, op1=ALU.mult)),
                             reads=[os_, rs, self.csm], writes=[os_])
                        fo = fin[it % 2]
                        k.op(k.pool, (lambda e, os_=os_, fo=fo, g_=gt[it % 3], n=n: e.tensor_tensor(out=fo[:, 0:n], in0=os_[:, 0:n], in1=g_[:, 0:n], op=ALU.mult)),
                             reads=[os_, gt[it % 3]], writes=[fo])
                        k.dma(k.sp, self.oT[rows, t0:t0 + n], fo[:, 0:n], reads=[fo])
            units = []
            for ti in order:
                t0, n = tiles[ti]
                ncn = n // 64
                corder = list(range(ncn)) if d == 0 else list(range(ncn))[::-1]
                for h in range(8):
                    units.append((h, t0, n, ncn, corder))

            def record(u):
                h, t0, n, ncn, corder = units[u]
                k.marks = []
                k.set_lane("u")
                unit(h, u % NB, u + 1, t0, n, ncn, corder)
                k.set_lane(None)
                recs = k.lanes.pop("u", [])
                m0 = k.marks[0] if len(k.marks) > 0 else len(recs)
                m1 = k.marks[1] if len(k.marks) > 1 else len(recs)
                return recs[:m0], recs[m0:m1], recs[m1:]

            cur = record(0)
            k.emit_recs(cur[0])
            prev_scan = []
            for u in range(len(units)):
                nxt = record(u + 1) if u + 1 < len(units) else None
                if nxt is not None:
                    k.emit_recs(nxt[0])
                k.emit_interleaved(prev_scan, cur[1])
                prev_scan = cur[2]
                cur = nxt
            k.emit_recs(prev_scan)
            k.end()

    def mixer_hgrn(self, li, nrow):
        self.hg_prep(li)
        self.phase_hg_proj(nrow)
        self.phase_hg_core(self.csm[:, 32:33])
        self.phase_out_proj(self.c_wo)


    def phase_gd_proj(self, j, nrow):
        k = self.k
        W = self.a_w[j]

        def setup(T0, tl):
            ctx = {}
            ctx["tm"] = [k.sb([128, 512], F32, "tmk") for _ in tl]
            for ti, (o, n) in enumerate(tl):
                k.dma(k.sp, ctx["tm"][ti][:, 0:n], self.tmask_rep[:, T0 + o:T0 + o + n], writes=[ctx["tm"][ti]])
            ctx["sg"] = [k.sb([128, 512], F32, "sg") for _ in range(2)]
            ctx["st"] = [k.sb([128, 512], BF16, "st") for _ in range(3)]
            ctx["z"] = [k.sb([128, 16], F32, "z") for _ in range(2)]
            ctx["bat"] = [k.sb([128, 32], F32, "bat") for _ in range(2)]
            na = k.sb([128, 16], F32, "negA")
            k.op(k.act, (lambda e: e.activation(na[:], self.asm[:, 0:16], AF.Exp)), reads=[self.asm], writes=[na])
            k.op(k.dve, (lambda e: e.tensor_scalar(na[:], na[:], -1.0, None, op0=ALU.mult)), reads=[na], writes=[na])
            ctx["negA"] = na
            ctx["i"] = 0
            return ctx

        def fm_pre(c):
            def post(ctx, pt, T0, ti, o, n):
                i = ctx["i"]
                ctx["i"] += 1
                st = ctx["st"][i % 3]
                k.op(k.dve, (lambda e: e.tensor_tensor(out=st[:, 0:n], in0=pt[:, 0:n], in1=ctx["tm"][ti][:, 0:n], op=ALU.mult)),
                     reads=[pt, ctx["tm"][ti]], writes=[st])
                k.dma(k.sp, self.pre[c * 128:(c + 1) * 128, T0 + o:T0 + o + n], st[:, 0:n], reads=[st])
            return post

        def fm_g(c):
            def post(ctx, pt, T0, ti, o, n):
                i = ctx["i"]
                ctx["i"] += 1
                st = ctx["st"][i % 3]
                k.op(k.act, (lambda e: e.activation(st[:, 0:n], pt[:, 0:n], AF.Silu)), reads=[pt], writes=[st])
                k.dma(k.sp, self.fmG[c * 128:(c + 1) * 128, T0 + o:T0 + o + n], st[:, 0:n], reads=[st])
            return post

        def tm_ba(ctx, pt, tok0, r):
            i = ctx["i"]
            ctx["i"] += 1
            z, bat = ctx["z"][i % 2], ctx["bat"][i % 2]
            blk = tok0 // 128
            assert tok0 % 128 == 0
            k.op(k.dve, (lambda e: e.tensor_tensor(out=z[0:r, :], in0=pt[0:r, 16:32], in1=self.asm[0:r, 16:32], op=ALU.add)),
                 reads=[pt, self.asm], writes=[z])
            k.op(k.act, (lambda e: e.activation(z[0:r, :], z[0:r, :], AF.Exp)), reads=[z], writes=[z])
            k.op(k.act, (lambda e: e.activation(z[0:r, :], z[0:r, :], AF.Ln, bias=self.one_col[0:r, 0:1])), reads=[z, self.one_col], writes=[z])
            k.op(k.dve, (lambda e: e.tensor_tensor(out=bat[0:r, 16:32], in0=z[0:r, :], in1=ctx["negA"][0:r, :], op=ALU.mult)),
                 reads=[z, ctx["negA"]], writes=[bat])
            k.op(k.act, (lambda e: e.activation(bat[0:r, 0:16], pt[0:r, 0:16], AF.Sigmoid)), reads=[pt], writes=[bat])
            k.op(k.dve, (lambda e: e.tensor_scalar(bat[0:r, 0:16], bat[0:r, 0:16], self.tmc[0:r, blk:blk + 1], None, op0=ALU.mult)),
                 reads=[bat, self.tmc], writes=[bat])
            k.dma(k.sp, self.ba_tm[tok0:tok0 + r, :], bat[0:r, :], reads=[bat])

        fm = []
        for c in range(24):
            fm.append((W[:, c * 128:(c + 1) * 128], fm_pre(c)))
        for c in range(8):
            fm.append((W[:, 3072 + c * 128:3072 + (c + 1) * 128], fm_g(c)))
        tm = [(W[:, 4096:4128], tm_ba, 32)]
        self.phase_proj(nrow, setup, fm, tm)

    def phase_gd_conv(self):
        k, T = self.k, self.T
        tiles = tiles_of(T, 512)
        QS = 128 ** -0.5
        k.begin()
        xin = [k.sb([128, 516], BF16, "cx") for _ in range(3)]
        acc = [k.sb([128, 512], F32, "cacc") for _ in range(3)]
        sv = [k.sb([128, 512], F32, "csv") for _ in range(3)]
        sq = [k.sb([128, 512], BF16, "csq") for _ in range(3)]
        rs = [k.sb([128, 512], F32, "crs") for _ in range(3)]
        ob = [k.sb([128, 512], BF16, "cob") for _ in range(3)]
        tmt = [k.sb([128, 512], F32, "ctm") for _ in range(2)]
        tb = [k.sb([128, 4, 128], BF16, "ctb") for _ in range(3)]
        pss = [k.ps([128, 512], F32, "cps") for _ in range(3)]
        ptb = [k.ps([128, 512], BF16, "cpt") for _ in range(3)]
        it = 0
        for ti, (t0, n) in enumerate(tiles):
            tmk = tmt[ti % 2]
            k.dma(k.sp, tmk[:, 0:n], self.tmask_rep[:, t0:t0 + n], writes=[tmk])
            for cc in range(24):
                kind = cc // 8
                x = xin[it % 3]
                a_, s_, q_, r_, o_ = acc[it % 3], sv[it % 3], sq[it % 3], rs[it % 3], ob[it % 3]
                ps = pss[it % 3]
                if it % 3 == 0:
                    k.merge()
                k.set_lane(it % 3)
                lane_i = it % 3
                it += 1
                lo = max(t0 - 2, 0)
                hi = min(t0 + n + 2, T)
                if lo > t0 - 2 or hi < t0 + n + 2:
                    k.op(k.pool, (lambda e, x=x: e.memset(x[:], 0.0)), writes=[x])
                k.dma(k.sp, x[:, lo - (t0 - 2):hi - (t0 - 2)], self.pre[cc * 128:(cc + 1) * 128, lo:hi], writes=[x])
                wcol = lambda jj, cc=cc: self.asm[:, 33 + cc * 5 + jj:34 + cc * 5 + jj]
                k.op(k.dve, (lambda e, a_=a_, x=x, n=n, w=wcol(0): e.tensor_scalar(a_[:, 0:n], x[:, 0:n], w, None, op0=ALU.mult)),
                     reads=[x, self.asm], writes=[a_])
                for jj in range(1, 5):
                    k.op(k.dve, (lambda e, a_=a_, x=x, n=n, jj=jj, w=wcol(jj): e.scalar_tensor_tensor(
                        out=a_[:, 0:n], in0=x[:, jj:jj + n], scalar=w, in1=a_[:, 0:n], op0=ALU.mult, op1=ALU.add)),
                        reads=[x, a_, self.asm], writes=[a_])
                k.op(k.act, (lambda e, a_=a_, s_=s_, n=n: e.activation(s_[:, 0:n], a_[:, 0:n], AF.Silu)), reads=[a_], writes=[s_])
                if kind < 2:
                    k.op(k.act, (lambda e, s_=s_, q_=q_, n=n: e.activation(q_[:, 0:n], s_[:, 0:n], AF.Square)), reads=[s_], writes=[q_])
                    k.op(k.pe, (lambda e, ps=ps, q_=q_, n=n: e.matmul(ps[:, 0:n], lhsT=self.ones_bf[:], rhs=q_[:, 0:n], start=True, stop=True)),
                         reads=[self.ones_bf, q_], writes=[ps])
                    k.op(k.act, (lambda e, ps=ps, r_=r_, n=n: e.activation(r_[:, 0:n], ps[:, 0:n], AF.Ln, bias=self.eps_col[:, 0:1])),
                         reads=[ps, self.eps_col], writes=[r_])
                    k.op(k.act, (lambda e, r_=r_, n=n: e.activation(r_[:, 0:n], r_[:, 0:n], AF.Exp, scale=-0.5)), reads=[r_], writes=[r_])
                if kind == 0:
                    k.op(k.dve, (lambda e, o_=o_, s_=s_, r_=r_, n=n: e.scalar_tensor_tensor(
                        out=o_[:, 0:n], in0=s_[:, 0:n], scalar=QS, in1=r_[:, 0:n], op0=ALU.mult, op1=ALU.mult)),
                        reads=[s_, r_], writes=[o_])
                    k.dma(k.sp, self.fmA[cc * 128:(cc + 1) * 128, t0:t0 + n], o_[:, 0:n], reads=[o_])
                    k.set_lane(None)
                    continue
                if kind == 1:
                    k.op(k.dve, (lambda e, s_=s_, r_=r_, n=n: e.tensor_tensor(out=s_[:, 0:n], in0=s_[:, 0:n], in1=r_[:, 0:n], op=ALU.mult)),
                         reads=[s_, r_], writes=[s_])
                    k.op(k.pool, (lambda e, o_=o_, s_=s_, tmk=tmk, n=n: e.tensor_tensor(out=o_[:, 0:n], in0=s_[:, 0:n], in1=tmk[:, 0:n], op=ALU.mult)),
                         reads=[s_, tmk], writes=[o_])
                    k.dma(k.sp, self.fmB[0][(cc - 8) * 128:(cc - 7) * 128, t0:t0 + n], o_[:, 0:n], reads=[o_])
                    dst = self.tmK[0]
                else:
                    k.op(k.pool, (lambda e, o_=o_, s_=s_, n=n: e.tensor_copy(o_[:, 0:n], s_[:, 0:n])), reads=[s_], writes=[o_])
                    dst = self.v_tm
                cl = (cc % 8)
                pt = ptb[lane_i]
                tt = tb[lane_i]
                nb = (n + 127) // 128
                for b in range(nb):
                    r = min(128, n - b * 128)
                    k.op(k.pe, (lambda e, pt=pt, o_=o_, b=b, r=r: e.transpose(pt[0:r, b * 128:(b + 1) * 128], o_[:, b * 128:b * 128 + r],
                                                                             self.identb[:])),
                         reads=[o_, self.identb], writes=[pt])
                if n % 128 == 0:
                    k.op(k.act, (lambda e, pt=pt, tt=tt, nb=nb: e.copy(tt[:, 0:nb, :], pt[:, 0:nb * 128].rearrange("p (b d) -> p b d", d=128))),
                         reads=[pt], writes=[tt])
                    k.dma(k.sp, dst[t0:t0 + n, cl * 128:(cl + 1) * 128].rearrange("(b p) d -> p b d", p=128), tt[:, 0:nb, :], reads=[tt])
                else:
                    for b in range(nb):
                        r = min(128, n - b * 128)
                        k.op(k.act, (lambda e, pt=pt, tt=tt, b=b, r=r: e.copy(tt[0:r, b, :], pt[0:r, b * 128:(b + 1) * 128])),
                             reads=[pt], writes=[tt])
                        k.dma(k.sp, dst[t0 + b * 128:t0 + b * 128 + r, cl * 128:(cl + 1) * 128], tt[0:r, b, :], reads=[tt])
                k.set_lane(None)
        k.merge()
        k.end()

    def phase_gd_core(self):
        import os
        GDCUT = int(os.environ.get("GD_CUT", "99"))
        GDSUB = int(os.environ.get("GD_SUB", "99"))
        k, nc, T = self.k, self.nc, self.T
        tiles = tiles_of(T, 512)
        onorm_col = self.asm[:, 32:33]
        for sweep in (1, 0):
            d = sweep
            k.begin()
            tri = self.tri
            if d == 0:
                Uin, MS, MI, MN = tri[:, 0:64], 1, 0, 3
            else:
                Uin, MS, MI, MN = tri[:, 128:192], 3, 2, 1
            mI = k.sb([64, 512], F32, "mI")
            mS = k.sb([64, 512], F32, "mS")
            mN = k.sb([64, 512], F32, "mN")
            idr = k.sb([64, 512], F32, "idr")
            for c in range(8):
                cs = slice(c * 64, (c + 1) * 64)
                k.op(k.dve, (lambda e, cs=cs: e.tensor_copy(mI[:, cs], tri[:, MI * 64:(MI + 1) * 64])), reads=[tri], writes=[mI])
                k.op(k.dve, (lambda e, cs=cs: e.tensor_copy(mS[:, cs], tri[:, MS * 64:(MS + 1) * 64])), reads=[tri], writes=[mS])
                k.op(k.dve, (lambda e, cs=cs: e.tensor_copy(mN[:, cs], tri[:, MN * 64:(MN + 1) * 64])), reads=[tri], writes=[mN])
                k.op(k.dve, (lambda e, cs=cs: e.tensor_copy(idr[:, cs], self.ident[0:64, 0:64])), reads=[self.ident], writes=[idr])
            ones_f = k.sb([64, 128], F32, "onesf")
            k.op(k.dve, (lambda e: e.memset(ones_f[:], 1.0)), writes=[ones_f])
            S = [k.sb([128, 128], F32, "S") for _ in range(8)]
            Sb = [k.sb([128, 128], BF16, "Sb") for _ in range(8)]
            for h in range(8):
                k.op(k.dve, (lambda e, h=h: e.memset(S[h][:], 0.0)), writes=[S[h]])
                k.op(k.pool, (lambda e, h=h: e.memset(Sb[h][:], 0.0)), writes=[Sb[h]])
            NB = 2
            mk = lambda shp, dt, nm: [k.sb(shp, dt, nm) for _ in range(NB)]
            qT, kT = mk([128, 512], BF16, "qT"), mk([128, 512], BF16, "kT")
            ktm, vtm = mk([64, 8, 128], BF16, "ktm"), mk([64, 8, 128], BF16, "vtm")
            bat = mk([64, 8, 32], F32, "bat")
            lab = mk([64, 2, 8], F32, "lab")
            gc, egc, bek, dcol = mk([64, 8], F32, "gc"), mk([64, 8], F32, "egc"), mk([64, 8], F32, "bek"), mk([64, 8], F32, "dcol")
            alast = mk([128, 8], F32, "alast")
            dg = mk([64, 512], F32, "dg")
            egr = mk([128, 512], F32, "egr")
            qd = mk([128, 512], BF16, "qd")
            fab = mk([64, 512], F32, "fab")
            fm_ = mk([64, 512], F32, "fm")
            fmi = mk([64, 512], F32, "fmi")
            gf = mk([64, 512], F32, "gf")
            a32 = mk([64, 512], F32, "a32")
            Rb = [mk([64, 512], F32, "Rb%d" % i) for i in range(2)]
            Pb = [mk([64, 512], F32, "Pb%d" % i) for i in range(2)]
            PTb = [mk([64, 512], F32, "PTb%d" % i) for i in range(2)]
            rhu, rhw, kdec = mk([64, 8, 128], F32, "rhu"), mk([64, 8, 128], F32, "rhw"), mk([64, 8, 128], BF16, "kdec")
            gfu = mk([64, 512], F32, "gfu")
            fmS = mk([64, 512], F32, "fmS")
            fmN = mk([64, 512], F32, "fmN")
            dgb = mk([64, 512], F32, "dgb")
            u_sb = mk([64, 8, 128], F32, "u")
            nwT = mk([128, 512], BF16, "nwT")
            qkm = mk([64, 512], BF16, "qkm")
            vn = [[k.sb([64, 128], BF16, "vn") for _ in range(2)] for _ in range(NB)]
            ost = [k.sb([128, 512], F32, "ost") for _ in range(2)]
            B = [k.ps([128, 512], F32, "B%d" % i) for i in range(8)]
            if d == 0:
                obt = [k.sb([128, 512], F32, "obt") for _ in range(3)]
                gt = [k.sb([128, 512], BF16, "gt") for _ in range(3)]
                sqo_l = mk([128, 512], BF16, "sqo")
                rs_l = mk([128, 512], F32, "rs")
                fin = [k.sb([128, 512], BF16, "fin") for _ in range(2)]
            order = list(range(len(tiles)))
            if d == 1:
                order = order[::-1]

            def unit(h, b, it, t0, n, ncn, corder):
                if True:
                    rows = slice(h * 128, (h + 1) * 128)
                    X = B[4 * b:4 * b + 4]
                    Bs, Brow, BG, Bb = X
                    BD, BP, BT = X[1], X[2], X[3]
                    Bq, Bo, Bst, Bv = X[0], X[1], X[2], X[0]
                    k.dma(k.sp, qT[b][:, 0:n], self.fmA[rows, t0:t0 + n], writes=[qT[b]])
                    k.dma(k.sp, kT[b][:, 0:n], self.fmB[0][rows, t0:t0 + n], writes=[kT[b]])
                    k.dma(k.sp, ktm[b][:, 0:ncn, :], self.tmK[0][t0:t0 + n, rows].rearrange("(c p) e -> p c e", p=64), writes=[ktm[b]])
                    k.dma(k.sp, vtm[b][:, 0:ncn, :], self.v_tm[t0:t0 + n, rows].rearrange("(c p) e -> p c e", p=64), writes=[vtm[b]])
                    k.dma(k.sp, bat[b][:, 0:ncn, :], self.ba_tm[t0:t0 + n, :].rearrange("(c p) e -> p c e", p=64), writes=[bat[b]])
                    if d == 0:
                        k.dma(k.sp, obt[it % 3][:, 0:n], self.obT[rows, t0:t0 + n], writes=[obt[it % 3]])
                        k.dma(k.sp, gt[it % 3][:, 0:n], self.fmG[rows, t0:t0 + n], writes=[gt[it % 3]])
                    k.mark()
                    col = d * 8 + h
                    k.op(k.dve, (lambda e, b=b, ncn=ncn, col=col: e.tensor_copy(lab[b][:, 0, 0:ncn], bat[b][:, 0:ncn, col])),
                         reads=[bat[b]], writes=[lab[b]])
                    k.op(k.dve, (lambda e, b=b, ncn=ncn, col=col: e.tensor_copy(lab[b][:, 1, 0:ncn], bat[b][:, 0:ncn, 16 + col])),
                         reads=[bat[b]], writes=[lab[b]])
                    beta = lambda c, b=b: lab[b][:, 0, c:c + 1]
                    k.op(k.pe, (lambda e, b=b, ncn=ncn: e.matmul(Bs[0:64, 0:ncn], lhsT=Uin, rhs=lab[b][:, 1, 0:ncn], start=True, stop=True)),
                         reads=[tri, lab[b]], writes=[Bs])
                    k.op(k.pe, (lambda e, b=b, ncn=ncn: e.matmul(Bs[:, 16:16 + ncn], lhsT=ones_f[:], rhs=lab[b][:, 1, 0:ncn], start=True, stop=True)),
                         reads=[ones_f, lab[b]], writes=[Bs])
                    k.op(k.dve, (lambda e, b=b, ncn=ncn: e.tensor_copy(gc[b][:, 0:ncn], Bs[0:64, 0:ncn])), reads=[Bs], writes=[gc[b]])
                    k.op(k.act, (lambda e, b=b, ncn=ncn: e.activation(alast[b][:, 0:ncn], Bs[:, 16:16 + ncn], AF.Exp)), reads=[Bs], writes=[alast[b]])
                    k.op(k.dve, (lambda e, b=b, ncn=ncn: e.tensor_tensor(out=dcol[b][:, 0:ncn], in0=Bs[0:64, 16:16 + ncn], in1=gc[b][:, 0:ncn],
                                                                        op=ALU.subtract)), reads=[Bs, gc[b]], writes=[dcol[b]])
                    k.op(k.act, (lambda e, b=b, ncn=ncn: e.activation(dcol[b][:, 0:ncn], dcol[b][:, 0:ncn], AF.Exp)), reads=[dcol[b]], writes=[dcol[b]])
                    k.op(k.act, (lambda e, b=b, ncn=ncn: e.activation(egc[b][:, 0:ncn], gc[b][:, 0:ncn], AF.Exp)), reads=[gc[b]], writes=[egc[b]])
                    k.op(k.dve, (lambda e, b=b, ncn=ncn: e.tensor_tensor(out=bek[b][:, 0:ncn], in0=egc[b][:, 0:ncn], in1=lab[b][:, 0, 0:ncn],
                                                                        op=ALU.mult)), reads=[egc[b], lab[b]], writes=[bek[b]])
                    if GDCUT < 1:
                        return
                    k.op(k.dve, (lambda e, b=b, n=n, ncn=ncn: e.tensor_tensor(
                        out=dg[b][:, 0:n].rearrange("p (c i) -> p c i", i=64), in0=idr[:, 0:n].rearrange("p (c i) -> p c i", i=64),
                        in1=gc[b][:, 0:ncn].unsqueeze(2).to_broadcast([64, ncn, 64]), op=ALU.mult)), reads=[idr, gc[b]], writes=[dg[b]])
                    for c in range(ncn):
                        cs = slice(c * 64, (c + 1) * 64)
                        k.op(k.pe, (lambda e, b=b, cs=cs: e.matmul(Brow[:, cs], lhsT=ones_f[:], rhs=dg[b][:, cs], start=True, stop=True)),
                             reads=[ones_f, dg[b]], writes=[Brow])
                    k.op(k.act, (lambda e, b=b, n=n: e.activation(egr[b][:, 0:n], Brow[:, 0:n], AF.Exp)), reads=[Brow], writes=[egr[b]])
                    k.op(k.dve, (lambda e, b=b, n=n: e.tensor_tensor(out=qd[b][:, 0:n], in0=qT[b][:, 0:n], in1=egr[b][:, 0:n], op=ALU.mult)),
                         reads=[qT[b], egr[b]], writes=[qd[b]])
                    k.op(k.dve, (lambda e, b=b, n=n, ncn=ncn: e.tensor_tensor(
                        out=fab[b][:, 0:n].rearrange("p (c i) -> p c i", i=64), in0=Brow[0:64, 0:n].rearrange("p (c i) -> p c i", i=64),
                        in1=gc[b][:, 0:ncn].unsqueeze(2).to_broadcast([64, ncn, 64]), op=ALU.subtract)), reads=[Brow, gc[b]], writes=[fab[b]])
                    k.op(k.act, (lambda e, b=b, n=n: e.activation(fab[b][:, 0:n], fab[b][:, 0:n], AF.Abs)), reads=[fab[b]], writes=[fab[b]])
                    k.op(k.act, (lambda e, b=b, n=n: e.activation(fm_[b][:, 0:n], fab[b][:, 0:n], AF.Exp, scale=-1.0)), reads=[fab[b]], writes=[fm_[b]])
                    k.op(k.pool, (lambda e, b=b, n=n: e.tensor_tensor(out=fmi[b][:, 0:n], in0=fm_[b][:, 0:n], in1=mI[:, 0:n], op=ALU.mult)),
                         reads=[fm_[b], mI], writes=[fmi[b]])
                    k.op(k.pool, (lambda e, b=b, n=n: e.tensor_tensor(out=fmS[b][:, 0:n], in0=fm_[b][:, 0:n], in1=mS[:, 0:n], op=ALU.mult)),
                         reads=[fm_[b], mS], writes=[fmS[b]])
                    k.op(k.pool, (lambda e, b=b, n=n: e.tensor_tensor(out=fmN[b][:, 0:n], in0=fm_[b][:, 0:n], in1=mN[:, 0:n], op=ALU.mult)),
                         reads=[fm_[b], mN], writes=[fmN[b]])
                    if GDCUT < 2:
                        return
                    k.op(k.dve, (lambda e, b=b, n=n, ncn=ncn: e.tensor_tensor(
                        out=dgb[b][:, 0:n].rearrange("p (c i) -> p c i", i=64), in0=idr[:, 0:n].rearrange("p (c i) -> p c i", i=64),
                        in1=lab[b][:, 0, 0:ncn].unsqueeze(2).to_broadcast([64, ncn, 64]), op=ALU.mult)), reads=[idr, lab[b]], writes=[dgb[b]])
                    for c in range(ncn):
                        cs = slice(c * 64, (c + 1) * 64)
                        k.op(k.pe, (lambda e, b=b, cs=cs: e.matmul(BG[0:64, cs], lhsT=kT[b][:, cs], rhs=kT[b][:, cs], start=True, stop=True)),
                             reads=[kT[b]], writes=[BG])
                    for c in range(ncn):
                        cs = slice(c * 64, (c + 1) * 64)
                        k.op(k.pe, (lambda e, b=b, cs=cs: e.matmul(Bb[:, cs], lhsT=ones_f[:], rhs=dgb[b][:, cs], start=True, stop=True)),
                             reads=[ones_f, dgb[b]], writes=[Bb])
                    k.op(k.dve, (lambda e, b=b, n=n: e.tensor_tensor(out=gf[b][:, 0:n], in0=BG[0:64, 0:n], in1=fmS[b][:, 0:n], op=ALU.mult)),
                         reads=[BG, fmS[b]], writes=[gf[b]])
                    k.op(k.dve, (lambda e, b=b, n=n: e.tensor_tensor(out=gfu[b][:, 0:n], in0=BG[0:64, 0:n], in1=fmN[b][:, 0:n], op=ALU.mult)),
                         reads=[BG, fmN[b]], writes=[gfu[b]])
                    R0, P0, PT0 = Rb[0][b], Pb[0][b], PTb[0][b]
                    k.op(k.dve, (lambda e, b=b, n=n, ncn=ncn, PT0=PT0: e.tensor_tensor(
                        out=PT0[:, 0:n].rearrange("p (c i) -> p c i", i=64), in0=gf[b][:, 0:n].rearrange("p (c i) -> p c i", i=64),
                        in1=lab[b][:, 0, 0:ncn].unsqueeze(2).to_broadcast([64, ncn, 64]), op=ALU.mult)), reads=[gf[b], lab[b]], writes=[PT0])
                    k.op(k.dve, (lambda e, b=b, n=n, P0=P0: e.tensor_tensor(out=P0[:, 0:n], in0=Bb[0:64, 0:n], in1=gfu[b][:, 0:n], op=ALU.mult)),
                         reads=[Bb, gfu[b]], writes=[P0])
                    k.op(k.pool, (lambda e, n=n, R0=R0, P0=P0: e.tensor_tensor(out=R0[:, 0:n], in0=idr[:, 0:n], in1=P0[:, 0:n], op=ALU.subtract)),
                         reads=[P0, idr], writes=[R0])
                    if GDCUT < 3:
                        return
                    for lv in range(6):
                        cur, nxt = lv % 2, (lv + 1) % 2
                        Pc, PTc = Pb[cur][b], PTb[cur][b]
                        Pn, PTn = Pb[nxt][b], PTb[nxt][b]
                        Rc, Rn = Rb[(lv + 1) % 2][b], Rb[lv % 2][b]
                        for c in range(ncn):
                            cs = slice(c * 64, (c + 1) * 64)
                            if lv >= 1:
                                k.op(k.pe, (lambda e, cs=cs, PTc=PTc, Rc=Rc: e.matmul(BD[0:64, cs], lhsT=PTc[:, cs], rhs=Rc[:, cs], start=True, stop=True)),
                                     reads=[PTc, Rc], writes=[BD])
                            if lv <= 4:
                                k.op(k.pe, (lambda e, cs=cs, PTc=PTc, Pc=Pc: e.matmul(BP[0:64, cs], lhsT=PTc[:, cs], rhs=Pc[:, cs], start=True, stop=True)),
                                     reads=[PTc, Pc], writes=[BP])
                                k.op(k.pe, (lambda e, cs=cs, PTc=PTc, Pc=Pc: e.matmul(BT[0:64, cs], lhsT=Pc[:, cs], rhs=PTc[:, cs], start=True, stop=True)),
                                     reads=[PTc, Pc], writes=[BT])
                        if lv >= 1:
                            k.op(k.dve, (lambda e, n=n, Rn=Rn, Rc=Rc: e.tensor_tensor(out=Rn[:, 0:n], in0=BD[0:64, 0:n], in1=Rc[:, 0:n], op=ALU.add)),
                                 reads=[BD, Rc], writes=[Rn])
                        if lv <= 4:
                            k.op(k.act, (lambda e, n=n, Pn=Pn: e.copy(Pn[:, 0:n], BP[0:64, 0:n])), reads=[BP], writes=[Pn])
                            k.op(k.act, (lambda e, n=n, PTn=PTn: e.copy(PTn[:, 0:n], BT[0:64, 0:n])), reads=[BT], writes=[PTn])
                    if GDCUT < 4:
                        return
                    Rf = Rb[1][b]
                    k.op(k.dve, (lambda e, b=b, ncn=ncn: e.tensor_tensor(out=rhu[b][:, 0:ncn, :], in0=vtm[b][:, 0:ncn, :],
                                                                        in1=lab[b][:, 0, 0:ncn].unsqueeze(2).to_broadcast([64, ncn, 128]), op=ALU.mult)),
                         reads=[vtm[b], lab[b]], writes=[rhu[b]])
                    k.op(k.pool, (lambda e, b=b, ncn=ncn: e.tensor_tensor(out=rhw[b][:, 0:ncn, :], in0=ktm[b][:, 0:ncn, :],
                                                                         in1=bek[b][:, 0:ncn].unsqueeze(2).to_broadcast([64, ncn, 128]), op=ALU.mult)),
                         reads=[ktm[b], bek[b]], writes=[rhw[b]])
                    k.op(k.pool, (lambda e, b=b, ncn=ncn: e.tensor_tensor(out=kdec[b][:, 0:ncn, :], in0=ktm[b][:, 0:ncn, :],
                                                                         in1=dcol[b][:, 0:ncn].unsqueeze(2).to_broadcast([64, ncn, 128]), op=ALU.mult)),
                         reads=[ktm[b], dcol[b]], writes=[kdec[b]])
                    for c in range(ncn):
                        cs = slice(c * 64, (c + 1) * 64)
                        pu = BD if c < 4 else BP
                        us = slice((c % 4) * 128, (c % 4 + 1) * 128)
                        k.op(k.pe, (lambda e, b=b, c=c, cs=cs, pu=pu, us=us, Rf=Rf: e.matmul(pu[0:64, us], lhsT=Rf[:, cs], rhs=rhu[b][:, c, :], start=True, stop=True)),
                             reads=[Rf, rhu[b]], writes=[pu])
                        k.op(k.pe, (lambda e, b=b, c=c, cs=cs, Rf=Rf: e.matmul(BT[:, cs], lhsT=rhw[b][:, c, :], rhs=Rf[:, cs], start=True, stop=True)),
                             reads=[Rf, rhw[b]], writes=[BT])
                        k.op(k.pe, (lambda e, b=b, cs=cs: e.matmul(Bq[0:64, cs], lhsT=kT[b][:, cs], rhs=qT[b][:, cs], start=True, stop=True)),
                             reads=[kT[b], qT[b]], writes=[Bq])
                    n4 = min(ncn, 4)
                    k.op(k.act, (lambda e, b=b, n4=n4: e.copy(u_sb[b][:, 0:n4, :], BD[0:64, 0:n4 * 128].rearrange("p (c d) -> p c d", d=128))),
                         reads=[BD], writes=[u_sb[b]])
                    if ncn > 4:
                        k.op(k.act, (lambda e, b=b, ncn=ncn: e.copy(u_sb[b][:, 4:ncn, :], BP[0:64, 0:(ncn - 4) * 128].rearrange("p (c d) -> p c d", d=128))),
                             reads=[BP], writes=[u_sb[b]])
                    k.op(k.dve, (lambda e, b=b, n=n: e.tensor_scalar(nwT[b][:, 0:n], BT[:, 0:n], -1.0, None, op0=ALU.mult)), reads=[BT], writes=[nwT[b]])
                    k.op(k.dve, (lambda e, b=b, n=n: e.tensor_tensor(out=qkm[b][:, 0:n], in0=Bq[0:64, 0:n], in1=fmi[b][:, 0:n], op=ALU.mult)),
                         reads=[Bq, fmi[b]], writes=[qkm[b]])
                    if GDCUT < 5:
                        return
                    k.mark()
                    for c in corder:
                        cs = slice(c * 64, (c + 1) * 64)
                        v_ = vn[b][c % 2]
                        k.op(k.pe, (lambda e, b=b, cs=cs, h=h: e.matmul(Bv[0:64, 256:384], lhsT=nwT[b][:, cs], rhs=Sb[h][:], start=True, stop=True)),
                             reads=[nwT[b], Sb[h]], writes=[Bv])
                        k.op(k.dve, (lambda e, b=b, c=c, v_=v_: e.tensor_tensor(out=v_[:], in0=Bv[0:64, 256:384], in1=u_sb[b][:, c, :], op=ALU.add)),
                             reads=[Bv, u_sb[b]], writes=[v_])
                        k.op(k.pe, (lambda e, b=b, cs=cs, h=h: e.matmul(Bo[:, cs], lhsT=Sb[h][:], rhs=qd[b][:, cs], start=True, stop=False)),
                             reads=[Sb[h], qd[b]], writes=[Bo])
                        k.op(k.pe, (lambda e, b=b, cs=cs, v_=v_: e.matmul(Bo[:, cs], lhsT=v_[:], rhs=qkm[b][:, cs], start=False, stop=True)),
                             reads=[v_, qkm[b]], writes=[Bo])
                        k.op(k.pe, (lambda e, b=b, c=c, v_=v_: e.matmul(Bs[:, 128:256], lhsT=kdec[b][:, c, :], rhs=v_[:], start=True, stop=True)),
                             reads=[kdec[b], v_], writes=[Bs])
                        k.op(k.dve, (lambda e, b=b, h=h, c=c: e.scalar_tensor_tensor(
                            out=S[h][:], in0=S[h][:], scalar=alast[b][:, c:c + 1], in1=Bs[:, 128:256], op0=ALU.mult, op1=ALU.add)),
                            reads=[S[h], alast[b], Bs], writes=[S[h]])
                        k.op(k.act, (lambda e, h=h: e.copy(Sb[h][:], S[h][:])), reads=[S[h]], writes=[Sb[h]])
                    if GDCUT < 6:
                        return
                    os_ = ost[it % 2]
                    if d == 1:
                        k.op(k.act, (lambda e, os_=os_, n=n: e.copy(os_[:, 0:n], Bo[:, 0:n])), reads=[Bo], writes=[os_])
                        k.dma(k.sp, self.obT[rows, t0:t0 + n], os_[:, 0:n], reads=[os_])
                    else:
                        sqo, rs = sqo_l[b], rs_l[b]
                        k.op(k.dve, (lambda e, os_=os_, ob_=obt[it % 3], n=n: e.tensor_tensor(out=os_[:, 0:n], in0=Bo[:, 0:n], in1=ob_[:, 0:n], op=ALU.add)),
                             reads=[Bo, obt[it % 3]], writes=[os_])
                        k.op(k.act, (lambda e, os_=os_, n=n: e.activation(sqo[:, 0:n], os_[:, 0:n], AF.Square)), reads=[os_], writes=[sqo])
                        k.op(k.pe, (lambda e, n=n: e.matmul(Bst[:, 0:n], lhsT=self.ones_bf[:], rhs=sqo[:, 0:n], start=True, stop=True)),
                             reads=[self.ones_bf, sqo], writes=[Bst])
                        k.op(k.act, (lambda e, n=n: e.activation(rs[:, 0:n], Bst[:, 0:n], AF.Ln, bias=self.eps_col[:, 0:1], scale=1.0 / 128)),
                             reads=[Bst, self.eps_col], writes=[rs])
                        k.op(k.act, (lambda e, n=n: e.activation(rs[:, 0:n], rs[:, 0:n], AF.Exp, scale=-0.5)), reads=[rs], writes=[rs])
                        k.op(k.dve, (lambda e, os_=os_, n=n: e.scalar_tensor_tensor(out=os_[:, 0:n], in0=os_[:, 0:n], scalar=onorm_col,
                                                                                   in1=rs[:, 0:n], op0=ALU.mult, op1=ALU.mult)),
                             reads=[os_, rs, self.asm], writes=[os_])
                        fo = fin[it % 2]
                        k.op(k.pool, (lambda e, os_=os_, fo=fo, g_=gt[it % 3], n=n: e.tensor_tensor(out=fo[:, 0:n], in0=os_[:, 0:n], in1=g_[:, 0:n], op=ALU.mult)),
                             reads=[os_, gt[it % 3]], writes=[fo])
                        k.dma(k.sp, self.oT[rows, t0:t0 + n], fo[:, 0:n], reads=[fo])
            units = []
            for ti in order:
                t0, n = tiles[ti]
                ncn = n // 64
                corder = list(range(ncn)) if d == 0 else list(range(ncn))[::-1]
                for h in range(8):
                    units.append((h, t0, n, ncn, corder))

            def record(u):
                h, t0, n, ncn, corder = units[u]
                k.marks = []
                k.set_lane("u")
                unit(h, u % NB, u + 1, t0, n, ncn, corder)
                k.set_lane(None)
                recs = k.lanes.pop("u", [])
                m0 = k.marks[0] if len(k.marks) > 0 else len(recs)
                m1 = k.marks[1] if len(k.marks) > 1 else len(recs)
                return recs[:m0], recs[m0:m1], recs[m1:]

            cur = record(0)
            k.emit_recs(cur[0])
            prev_scan = []
            for u in range(len(units)):
                nxt = record(u + 1) if u + 1 < len(units) else None
                if nxt is not None:
                    k.emit_recs(nxt[0])
                k.emit_interleaved(prev_scan, cur[1])
                prev_scan = cur[2]
                cur = nxt
            k.emit_recs(prev_scan)
            if d == 1 and os.environ.get("DEBUG_DUMP"):
                o = self.nc.dram_tensor("dbg_S0", [128, 128], F32, kind="ExternalOutput").ap()
                k.dma(k.sp, o, S[0][:], reads=[S[0]])
                o = self.nc.dram_tensor("dbg_u", [64, 8 * 128], F32, kind="ExternalOutput").ap()
                k.dma(k.sp, o, u_sb[0][:].rearrange("p c d -> p (c d)"), reads=[u_sb[0]])
                o = self.nc.dram_tensor("dbg_rhu", [64, 8 * 128], BF16, kind="ExternalOutput").ap()
                k.dma(k.sp, o, rhu[0][:].rearrange("p c d -> p (c d)"), reads=[rhu[0]])
                o = self.nc.dram_tensor("dbg_R", [64, 512], BF16, kind="ExternalOutput").ap()
                k.dma(k.sp, o, Rb[0][0][:], reads=[Rb[0][0]])
                o = self.nc.dram_tensor("dbg_alast", [128, 8], F32, kind="ExternalOutput").ap()
                k.dma(k.sp, o, alast[0][:], reads=[alast[0]])
            k.end()

    def mixer_gdn(self, j, nrow):
        k = self.k
        k.begin()
        k.dma(k.sp, self.asm[:], self.a_small[j], writes=[self.asm])
        k.end()
        import os
        st = int(os.environ.get("GD_STAGE", "9"))
        self.phase_gd_proj(j, nrow)
        if st >= 2:
            self.phase_gd_conv()
        if st >= 3:
            self.phase_gd_core()
        if st >= 4:
            self.phase_out_proj(self.a_wo[j])

    def build(self):
        k = self.k
        self.eps_col = k.gsb([128, 1], F32, "epscol")
        k.begin()
        k.op(k.dve, lambda e: e.memset(self.eps_col[:], EPS), writes=[self.eps_col])
        self.bsm = k.gsb([128, 260], F32, "bsm")
        k.dma(k.sp, self.bsm[:], self.b_small, writes=[self.bsm])
        self.csm = k.gsb([128, 33], F32, "csm")
        k.dma(k.sp, self.csm[:], self.c_small, writes=[self.csm])
        self.tri = k.gsb([64, 256], F32, "tric")
        k.dma(k.sp, self.tri[:], self.tri_in, writes=[self.tri])
        self.tmc = k.gsb([128, (self.T + 127) // 128], F32, "tmc")
        k.dma(k.sp, self.tmc[:], self.tmask_col, writes=[self.tmc])
        self.one_col = k.gsb([128, 1], F32, "onecol")
        k.op(k.dve, lambda e: e.memset(self.one_col[:], 1.0), writes=[self.one_col])
        self.lb_col = k.gsb([128, 16], F32, "lbcol")
        self.asm = k.gsb([128, 153], F32, "asm")
        self.identb = k.gsb([128, 128], BF16, "identbc")
        k.dma(k.sp, self.identb[:], self.identb_in, writes=[self.identb])
        self.oml_row = k.gsb([128, D_MODEL], F32, "omlrow")
        k.end()
        self.phase_in()
        if self.only == "gdn":
            self.mixer_gdn(0, 0 * 3 + 1)
            self.phase_out(self.depth * 3)
            import os
            if os.environ.get("DEBUG_DUMP"):
                k.begin()
                dummy = k.sb([128, 4], F32, "dummy")
                for nm, ap, shp, dt in (("fmA", self.fmA, [D_MODEL, self.T], BF16), ("fmB0", self.fmB[0], [D_MODEL, self.T], BF16),
                                        ("tmK0", self.tmK[0], [self.T, D_MODEL], BF16), ("v_tm", self.v_tm, [self.T, D_MODEL], BF16),
                                        ("ba_tm", self.ba_tm, [self.T, 32], F32), ("obT", self.obT, [D_MODEL, self.T], F32),
                                        ("oT", self.oT, [D_MODEL, self.T], BF16), ("fmG", self.fmG, [D_MODEL, self.T], BF16)):
                    o = self.nc.dram_tensor("dbg_" + nm, shp, dt, kind="ExternalOutput").ap()
                    k.dma(k.sp, o, ap, reads=[dummy])
                k.end()
            return
        if self.only == "hgrn":
            self.mixer_hgrn(2, 2 * 3 + 1)
            self.phase_out(self.depth * 3)
            return
        if self.only == "attn":
            self.mixer_attn(1 * 3 + 1)
            self.phase_out(self.depth * 3)
            return
        for li in range(self.depth):
            if self.ffn:
                self.phase_ffn(li, 0, li * 3 + 0)
            if self.mixers and li % 3 == 1:
                self.mixer_attn(li * 3 + 1)
            if self.mixers and li % 3 == 2:
                self.mixer_hgrn(li, li * 3 + 1)
            if self.mixers and li % 3 == 0:
                self.mixer_gdn(li // 3, li * 3 + 1)
            if self.ffn:
                self.phase_ffn(li, 1, li * 3 + 2)
        self.phase_out(self.depth * 3)


def col_layout(a):
    a = np.asarray(a, np.float32)
    R = a.shape[0]
    C = a.shape[1] // 128
    return np.ascontiguousarray(a.reshape(R, C, 128).transpose(2, 0, 1).reshape(128, R * C))


def attn_host_inputs(b_w_in, b_lambda, b_sub_norm, layer_idx, T, n_valid):
    f32 = np.float32
    w = np.asarray(b_w_in, f32)
    perm = np.arange(2048)
    d = perm % 64
    perm = np.where(d < 8, perm + 8, np.where(d < 16, perm - 8, perm))
    w_sw = np.ascontiguousarray(w[:, perm])
    sm = np.zeros((128, 260), f32)
    sm[:, 0:256] = np.asarray(b_lambda, f32).reshape(1, 256)
    sm[:, 256] = np.asarray(b_sub_norm, f32).reshape(128)
    dd = np.arange(128) % 64
    ii = np.where(dd < 8, dd, dd - 8).astype(np.float64)
    invf = np.exp(-np.log(500000.0) * ii / 8.0)
    sm[:, 257] = np.where(dd < 16, invf, 0.0)
    sm[:, 258] = np.where(dd < 8, -1.0, np.where(dd < 16, 1.0, 0.0))
    sm[:, 259] = 0.8 - 0.6 * np.exp(-0.3 * layer_idx)
    nkt = (T + 127) // 128
    tok = np.arange(nkt * 128)
    kb = np.where(tok < n_valid, 0.0, -30000.0).astype(f32).reshape(nkt, 128).T
    return w_sw, sm, np.ascontiguousarray(kb)


def common_host_inputs(T, n_valid):
    f32 = np.float32
    j = np.arange(64)[:, None]
    i = np.arange(64)[None, :]
    tri = np.concatenate([(j <= i), (j > i), (j >= i), (j < i)], axis=1).astype(f32)
    tok = np.arange(T)
    tm = (tok < n_valid).astype(f32)
    tmask_rep = np.ascontiguousarray(np.broadcast_to(tm[None, :], (128, T)))
    nb = (T + 127) // 128
    tmp = np.zeros(nb * 128, f32)
    tmp[:T] = tm
    tmask_col = np.ascontiguousarray(tmp.reshape(nb, 128).T)
    return {"tri": tri, "tmask_rep": tmask_rep, "tmask_col": tmask_col, "ident": np.eye(128, dtype=f32)}


def hgrn_host_inputs(c_lb_logits, c_o_norm):
    sm = np.zeros((128, 33), np.float32)
    sm[:, 0:32] = col_layout(np.asarray(c_lb_logits, np.float32))
    sm[:, 32] = np.asarray(c_o_norm, np.float32).reshape(128)
    return sm


def gdn_host_inputs(a_log, a_dt_bias, a_o_norm, a_conv_w):
    f32 = np.float32
    n_a = np.asarray(a_log).shape[0]
    out = np.zeros((n_a, 128, 153), f32)
    for j in range(n_a):
        out[j, :, 0:16] = np.asarray(a_log[j], f32).reshape(1, 16)
        out[j, :, 16:32] = np.asarray(a_dt_bias[j], f32).reshape(1, 16)
        out[j, :, 32] = np.asarray(a_o_norm[j], f32).reshape(128)
        cw = np.asarray(a_conv_w[j], f32)
        out[j, :, 33:153] = cw.reshape(5, 24, 128).transpose(2, 1, 0).reshape(128, 120)
    return out


_PROG_CACHE = {}


def get_prog(T, depth, n_groups):
    key = (T, depth, n_groups)
    if key not in _PROG_CACHE:
        _PROG_CACHE[key] = Prog(T, depth, n_groups)
    return _PROG_CACHE[key]


T_FULL = 8256
SEQ_P = 8192
SEQ_S = 4096


def kernel(x_prompt, x_sample, meta_tokens, norm_w, ffn_w_up, ffn_w_down,
           a_w_in, a_conv_w, a_log, a_dt_bias, a_o_norm, a_w_out,
           b_w_in, b_lambda, b_sub_norm, b_w_out,
           c_w_in, c_lb_logits, c_o_norm, c_w_out, final_norm):
    import ml_dtypes
    depth = norm_w.shape[0]
    T = T_FULL
    prog = get_prog(T, depth, 4)
    f32 = np.float32
    seqs = [x_prompt[0], x_prompt[1], x_sample[0], x_sample[1], x_sample[2], x_sample[3], x_sample[0], x_sample[1]]
    nw = np.concatenate([np.asarray(norm_w, f32).reshape(depth * 3, D_MODEL), np.asarray(final_norm, f32)[None]], 0)
    nw = col_layout(nw)
    shared = {
        "norm_w": nw,
        "ffn_w_up": np.asarray(ffn_w_up, f32), "ffn_w_down": np.asarray(ffn_w_down, f32),
        "b_w_in": np.asarray(b_w_in[0], f32), "b_w_out": np.asarray(b_w_out[0], f32),
        "c_w_in": np.asarray(c_w_in[0], f32), "c_w_out": np.asarray(c_w_out[0], f32),
        "c_small": hgrn_host_inputs(c_lb_logits, c_o_norm[0]),
        "a_w_in": np.asarray(a_w_in, f32), "a_w_out": np.asarray(a_w_out, f32),
        "a_small": gdn_host_inputs(a_log, a_dt_bias, a_o_norm, a_conv_w),
        "identb": np.eye(128, dtype=f32).astype(ml_dtypes.bfloat16),
    }
    in_maps = []
    per_len = {}
    for s in seqs:
        L = s.shape[0]
        nv = N_META + L
        if L not in per_len:
            w_sw, sm, kb = attn_host_inputs(b_w_in[0], b_lambda[0], b_sub_norm[0], 1, T, nv)
            d = {"b_w_sw": w_sw, "b_small": sm, "kbias": kb}
            d.update(common_host_inputs(T, nv))
            per_len[L] = d
        xin = np.zeros((T, D_MODEL), f32)
        xin[:N_META] = meta_tokens
        xin[N_META:nv] = s
        m = {"xin": xin}
        m.update(shared)
        m.update(per_len[L])
        in_maps.append(m)
    res = run_bass_kernel_spmd(prog.nc, in_maps, core_ids=list(range(8)))
    outs = [r["yout"] for r in res.results]
    y_prompt = np.stack([outs[0][N_META:N_META + SEQ_P], outs[1][N_META:N_META + SEQ_P]], 0).astype(f32)
    y_sample = np.stack([outs[i][N_META:N_META + SEQ_S] for i in range(2, 6)], 0).astype(f32)
    return (y_prompt, y_sample)
```
